# Optimizing a Trainium2 kernel written in Bass

```python
import jax, jax.numpy as jnp
from jax import lax
import numpy as np

D_MODEL = 1024
BATCH = 8
SEQ = 4096
DEPTH = 2

CTX_LEN = 256
GRID_W = 64
D_GROUP = D_MODEL // 4
HG_HEADS = 4
HG_DK = D_GROUP // HG_HEADS
HG_DV = D_GROUP // HG_HEADS
HG_CHUNK = 16
RET_HEADS = 4
RET_DK = D_GROUP // RET_HEADS
RET_DV = D_GROUP // RET_HEADS
RET_CHUNK = 64
ROPE_BASE = 10000.0
GDN_HEADS = 4
GDN_DK = D_GROUP // GDN_HEADS
GDN_DV = D_GROUP // GDN_HEADS
GDN_CHUNK = 64
CONV_K = 3
S5_GROUP = 16
S5_GROUPS = D_GROUP // S5_GROUP
S5_STATE = 64
D_FF = 4 * D_MODEL
N_MOD = 6
EPS = 1e-6
LB_FLOOR = 1e-30
IN_SIZES = (D_GROUP, D_GROUP, D_GROUP, 2 * D_GROUP,
            D_GROUP, D_GROUP, D_GROUP, D_GROUP,
            3 * D_GROUP, D_GROUP, 2 * GDN_HEADS, 2 * GDN_HEADS,
            D_GROUP)
D_IN = sum(IN_SIZES)

kernel_name = 'hybrid_parallel_heads_dit_block'

F32 = jnp.float32


def rmsnorm(x, g):
    xf = x.astype(F32)
    return xf * lax.rsqrt(jnp.mean(xf * xf, axis=-1, keepdims=True) + EPS) * g.astype(F32)


def split_heads(a, h):
    return a.reshape(a.shape[:-1] + (h, a.shape[-1] // h))


def flip_t(a):
    return jnp.flip(a, axis=1)


def to_chunks(a, c):
    b, t = a.shape[:2]
    a = a.reshape((b, t // c, c) + a.shape[2:])
    return jnp.moveaxis(a, (1, 3), (0, 2))


def from_chunks(a):
    a = jnp.moveaxis(a, (0, 2), (1, 3))
    return a.reshape((a.shape[0], a.shape[1] * a.shape[2]) + a.shape[3:])


def gated_head_norm(o, gate, g):
    y = o * lax.rsqrt(jnp.mean(o * o, axis=-1, keepdims=True) + EPS)
    if g is not None:
        y = y * g.astype(F32)
    return y.reshape(gate.shape) * jax.nn.silu(gate)


def l2norm(a):
    return a * lax.rsqrt(jnp.sum(a * a, axis=-1, keepdims=True) + EPS)


def bidirectional_prefix_scan(core, ctx_seq, lat_seq, par, s0):
    outs_c, outs_l = [], []
    for d in range(2):
        tf = (lambda a: a) if d == 0 else flip_t
        oc, s_ctx = core(tuple(tf(a) for a in ctx_seq[d]), par[d], s0)
        ol, _ = core(tuple(tf(a) for a in lat_seq[d]), par[d], s_ctx)
        outs_c.append(tf(oc))
        outs_l.append(tf(ol))
    return outs_c[0] + outs_c[1], outs_l[0] + outs_l[1]


def gla_chunked(seq, par, s0):
    q, v, log_f = seq
    k = -jnp.expm1(log_f)
    bsz, t, h, _ = q.shape
    c = HG_CHUNK
    q, k, v, b = (to_chunks(a, c) for a in (q, k, v, log_f))
    b = jnp.cumsum(b, axis=3)
    tri = jnp.tril(jnp.ones((c, c), bool))[:, :, None]
    diff = b[..., :, None, :] - b[..., None, :, :]
    decay = jnp.where(tri, jnp.exp(jnp.minimum(diff, 0.0)), 0.0)
    attn = jnp.einsum('nbhik,nbhjk,nbhijk->nbhij', q, k, decay)
    o_intra = jnp.einsum('nbhij,nbhjv->nbhiv', attn, v)
    q_dec = q * jnp.exp(b)
    k_dec = k * jnp.exp(b[..., -1:, :] - b)
    c_dec = jnp.exp(b[..., -1, :])
    kv = jnp.einsum('nbhjk,nbhjv->nbhkv', k_dec, v)

    def step(s, xs):
        kv_n, cd_n = xs
        return s * cd_n[..., None] + kv_n, s

    s_fin, s_prev = lax.scan(step, s0, (kv, c_dec))
    o = o_intra + jnp.einsum('nbhik,nbhkv->nbhiv', q_dec, s_prev)
    return from_chunks(o), s_fin


def forget_log(f_raw, lb):
    return jnp.logaddexp(jnp.log(jnp.maximum(lb, LB_FLOOR)) + jax.nn.log_sigmoid(-f_raw),
                         jax.nn.log_sigmoid(f_raw))


def hgrn2_mixer(pc, pl, lb, norm_g):
    def prep(p):
        q, i, g, f = p
        q = split_heads(jax.nn.silu(q), HG_HEADS) * HG_DK ** -0.5
        v = split_heads(i, HG_HEADS)
        f_dir = jnp.split(f, 2, axis=-1)
        seqs = tuple((q, v, split_heads(forget_log(f_dir[d], lb[d]), HG_HEADS)) for d in range(2))
        return seqs, g
    sc, gc = prep(pc)
    sl, gl = prep(pl)
    s0 = jnp.zeros((pc[0].shape[0], HG_HEADS, HG_DK, HG_DV), F32)
    oc, ol = bidirectional_prefix_scan(gla_chunked, sc, sl, ((), ()), s0)
    return gated_head_norm(oc, gc, norm_g), gated_head_norm(ol, gl, norm_g)


def rope(x, pos):
    half = x.shape[-1] // 2
    inv = ROPE_BASE ** (-jnp.arange(half, dtype=F32) / half)
    ang = pos.astype(F32)[:, None] * inv[None, :]
    cos = jnp.cos(ang)[None, :, None, :]
    sin = jnp.sin(ang)[None, :, None, :]
    x1, x2 = x[..., :half], x[..., half:]
    return jnp.concatenate([x1 * cos - x2 * sin, x1 * sin + x2 * cos], axis=-1)


def retention_chunked(seq, par, s0):
    q, k, v = seq
    (log_g,) = par
    c = RET_CHUNK
    q, k, v = (to_chunks(a, c) for a in (q, k, v))
    idx = jnp.arange(c, dtype=F32)
    lg = log_g[:, None]
    rel = idx[:, None] - idx[None, :]
    dmat = jnp.where(rel >= 0, jnp.exp(jnp.maximum(rel, 0.0)[None] * lg[..., None]), 0.0)
    attn = jnp.einsum('nbhik,nbhjk->nbhij', q, k) * dmat
    o_intra = jnp.einsum('nbhij,nbhjv->nbhiv', attn, v)
    q_dec = jnp.exp((idx + 1.0)[None, :] * lg)
    k_dec = jnp.exp((c - 1.0 - idx)[None, :] * lg)
    c_dec = jnp.exp(c * log_g)
    kv = jnp.einsum('nbhjk,hj,nbhjv->nbhkv', k, k_dec, v)

    def step(s, kv_n):
        return s * c_dec[:, None, None] + kv_n, s

    s_fin, s_prev = lax.scan(step, s0, kv)
    o = o_intra + jnp.einsum('nbhik,hi,nbhkv->nbhiv', q, q_dec, s_prev)
    return from_chunks(o), s_fin


def retention_mixer(pc, pl, log_gamma, pos_c, pos_l):
    def prep(p, pos):
        q, k, v, g = p
        q = rope(split_heads(q, RET_HEADS), pos)
        k = rope(split_heads(k, RET_HEADS), pos) * RET_DK ** -0.5
        s = (q, k, split_heads(v, RET_HEADS))
        return (s, s), g
    sc, gc = prep(pc, pos_c)
    sl, gl = prep(pl, pos_l)
    s0 = jnp.zeros((pc[0].shape[0], RET_HEADS, RET_DK, RET_DV), F32)
    par = ((log_gamma[0],), (log_gamma[1],))
    oc, ol = bidirectional_prefix_scan(retention_chunked, sc, sl, par, s0)
    return gated_head_norm(oc, gc, None), gated_head_norm(ol, gl, None)


def gated_delta_chunked(seq, par, s0):
    c = GDN_CHUNK
    q, k, v, log_a, beta = (to_chunks(a, c) for a in seq)
    dv = v.shape[-1]
    g = jnp.cumsum(log_a, axis=-1)
    tri = jnp.tril(jnp.ones((c, c), bool))
    strict = jnp.tril(jnp.ones((c, c), bool), -1)
    diff = g[..., :, None] - g[..., None, :]
    lmat = jnp.where(tri, jnp.exp(jnp.minimum(diff, 0.0)), 0.0)
    kb = k * beta[..., None]
    a_mat = jnp.where(strict, jnp.einsum('nbhik,nbhjk->nbhij', kb, k) * lmat, 0.0)
    rhs = jnp.concatenate([v * beta[..., None], kb * jnp.exp(g)[..., None]], axis=-1)
    uw = lax.linalg.triangular_solve(a_mat + jnp.eye(c, dtype=F32), rhs, left_side=True, lower=True)
    u, w = uw[..., :dv], uw[..., dv:]
    qk = jnp.where(tri, jnp.einsum('nbhik,nbhjk->nbhij', q, k) * lmat, 0.0)
    q_dec = q * jnp.exp(g)[..., None]
    k_dec = k * jnp.exp(g[..., -1:] - g)[..., None]
    c_dec = jnp.exp(g[..., -1])

    def step(s, xs):
        u_n, w_n, qk_n, qd_n, kd_n, cd_n = xs
        v_new = u_n - jnp.einsum('bhck,bhkv->bhcv', w_n, s)
        o_n = jnp.einsum('bhck,bhkv->bhcv', qd_n, s) + jnp.einsum('bhij,bhjv->bhiv', qk_n, v_new)
        s = s * cd_n[..., None, None] + jnp.einsum('bhck,bhcv->bhkv', kd_n, v_new)
        return s, o_n

    s_fin, o = lax.scan(step, s0, (u, w, qk, q_dec, k_dec, c_dec))
    return from_chunks(o), s_fin


def depthwise_conv2d(x, w):
    ch = x.shape[-1]
    return lax.conv_general_dilated(x, w[:, :, None, :], window_strides=(1, 1), padding='SAME',
                                    dimension_numbers=('NHWC', 'HWIO', 'NHWC'),
                                    feature_group_count=ch)


def gdn_mixer(pc, pl, rows, conv_w, a_log, dt_bias, norm_g):
    conv_w = conv_w.astype(F32)

    def prep(p, n_rows):
        qkv, g, a, b = p
        bsz, t, ch = qkv.shape
        grid = qkv.reshape(bsz, n_rows, t // n_rows, ch)
        qkv = jax.nn.silu(depthwise_conv2d(grid, conv_w)).reshape(bsz, t, ch)
        q, k, v = jnp.split(qkv, 3, axis=-1)
        q = l2norm(split_heads(q, GDN_HEADS)) * GDN_DK ** -0.5
        k = l2norm(split_heads(k, GDN_HEADS))
        v = split_heads(v, GDN_HEADS)
        a_dir = jnp.split(a, 2, axis=-1)
        b_dir = jnp.split(b, 2, axis=-1)
        seqs = tuple((q, k, v,
                      -jnp.exp(a_log[d]) * jax.nn.softplus(a_dir[d] + dt_bias[d]),
                      jax.nn.sigmoid(b_dir[d])) for d in range(2))
        return seqs, g
    sc, gc = prep(pc, 1)
    sl, gl = prep(pl, rows)
    s0 = jnp.zeros((pc[0].shape[0], GDN_HEADS, GDN_DK, GDN_DV), F32)
    oc, ol = bidirectional_prefix_scan(gated_delta_chunked, sc, sl, ((), ()), s0)
    return gated_head_norm(oc, gc, norm_g), gated_head_norm(ol, gl, norm_g)


def s5_core(seq, par, s0):
    (u,) = seq
    lam_re, lam_im, log_dt, b_re, b_im, c_re, c_im = par
    bsz, t, _ = u.shape
    ug = u.reshape(bsz, t, S5_GROUPS, S5_GROUP)
    dt = jnp.exp(log_dt)[:, None]
    mag = jnp.exp(lam_re * dt)
    ar, ai = mag * jnp.cos(lam_im * dt), mag * jnp.sin(lam_im * dt)
    den = lam_re * lam_re + lam_im * lam_im
    nr, ni = ar - 1.0, ai
    fr = (nr * lam_re + ni * lam_im) / den
    fi = (ni * lam_re - nr * lam_im) / den
    bbr = fr[..., None] * b_re - fi[..., None] * b_im
    bbi = fr[..., None] * b_im + fi[..., None] * b_re
    xr = jnp.einsum('gpc,btgc->tbgp', bbr, ug)
    xi = jnp.einsum('gpc,btgc->tbgp', bbi, ug)
    h0r, h0i = s0
    xr = xr.at[0].add(ar * h0r - ai * h0i)
    xi = xi.at[0].add(ar * h0i + ai * h0r)
    a_r = jnp.broadcast_to(ar, (t, 1) + ar.shape)
    a_i = jnp.broadcast_to(ai, (t, 1) + ai.shape)

    def combine(e1, e2):
        a1r, a1i, b1r, b1i = e1
        a2r, a2i, b2r, b2i = e2
        return (a2r * a1r - a2i * a1i, a2r * a1i + a2i * a1r,
                a2r * b1r - a2i * b1i + b2r, a2r * b1i + a2i * b1r + b2i)

    _, _, hr, hi = lax.associative_scan(combine, (a_r, a_i, xr, xi), axis=0)
    y = jnp.einsum('gcp,tbgp->btgc', c_re, hr) - jnp.einsum('gcp,tbgp->btgc', c_im, hi)
    return y.reshape(bsz, t, D_GROUP), (hr[-1], hi[-1])


def s5_mixer(uc, ul, lam_re, lam_im, log_dt, b_re, b_im, c_re, c_im, d_skip, glu_w, glu_b):
    f = lambda a: a.astype(F32)
    par = tuple((f(lam_re[d]), f(lam_im[d]), f(log_dt[d]), f(b_re), f(b_im), f(c_re), f(c_im))
                for d in range(2))
    zero = jnp.zeros((uc.shape[0], S5_GROUPS, S5_STATE), F32)
    oc, ol = bidirectional_prefix_scan(s5_core, ((uc,), (uc,)), ((ul,), (ul,)), par, (zero, zero))

    def finish(y, u):
        y = jax.nn.gelu(y + u * f(d_skip))
        return y * jax.nn.sigmoid(y @ f(glu_w) + f(glu_b))
    return finish(oc, uc), finish(ol, ul)


def split_cols(z):
    parts = jnp.split(z.astype(F32), np.cumsum(IN_SIZES)[:-1].tolist(), axis=-1)
    hq, hi, hg, hf, rq, rk, rv, rg, gqkv, gg, ga, gb, su = parts
    return (hq, hi, hg, hf), (rq, rk, rv, rg), (gqkv, gg, ga, gb), su


def hybrid_mixer(hc, hl, rows, pos_c, pos_l, need_ctx, w_in, lb, hg_norm_g, ret_log_gamma,
                 conv_w, a_log, dt_bias, gdn_norm_g, lam_re, lam_im, log_dt, b_re, b_im,
                 c_re, c_im, d_skip, glu_w, glu_b, w_out):
    pc = split_cols(hc @ w_in)
    pl = split_cols(hl @ w_in)
    a_c, a_l = hgrn2_mixer(pc[0], pl[0], lb, hg_norm_g)
    r_c, r_l = retention_mixer(pc[1], pl[1], ret_log_gamma, pos_c, pos_l)
    g_c, g_l = gdn_mixer(pc[2], pl[2], rows, conv_w, a_log.astype(F32), dt_bias.astype(F32), gdn_norm_g)
    s_c, s_l = s5_mixer(pc[3], pl[3], lam_re, lam_im, log_dt, b_re, b_im, c_re, c_im, d_skip, glu_w, glu_b)
    y_l = jnp.concatenate([a_l, r_l, g_l, s_l], axis=-1) @ w_out.astype(F32)
    y_c = jnp.concatenate([a_c, r_c, g_c, s_c], axis=-1) @ w_out.astype(F32) if need_ctx else None
    return y_c, y_l


def modulation(cvec, w, b):
    m = jax.nn.silu(cvec.reshape(-1, cvec.shape[-1]).astype(F32)) @ w.astype(F32) + b.astype(F32)
    return jnp.split(m[:, None, :], N_MOD, axis=-1)


def modulate(h, shift, scale):
    return h * (1.0 + scale) + shift


def sqrelu_mlp(h, w1, w2):
    return jnp.square(jax.nn.relu(h @ w1.astype(F32))) @ w2.astype(F32)


def setup_inputs(seed: int = 0) -> dict:
    key = jax.random.key(seed)
    ks = jax.random.split(key, 32)
    L, D, G, P = DEPTH, D_MODEL, S5_GROUPS, S5_STATE

    def nrm(k, shape, s=1.0):
        return s * jax.random.normal(k, shape, F32)

    ret_logit = jnp.log(2.0 ** (5.0 + jnp.arange(RET_HEADS, dtype=F32)) - 1.0)
    lo, hi = float(np.log(1e-3)), float(np.log(1e-1))
    gdn_dt = jnp.exp(jax.random.uniform(ks[16], (L, 2, GDN_HEADS), F32, lo, hi))
    return {
        'x': nrm(ks[0], (BATCH, SEQ, D)),
        'c': nrm(ks[1], (BATCH, D)),
        'ctx': nrm(ks[2], (BATCH, CTX_LEN, D)),
        'c_ctx': nrm(ks[3], (D,)),
        'mod_w': nrm(ks[4], (L, D, N_MOD * D), 0.5 * D ** -0.5),
        'mod_b': nrm(ks[5], (L, N_MOD * D), 0.02),
        'norm1_g': 1.0 + nrm(ks[6], (L, D), 0.02),
        'norm2_g': 1.0 + nrm(ks[7], (L, D), 0.02),
        'w_in': nrm(ks[8], (L, D, D_IN), D ** -0.5),
        'hgrn_lb_logits': nrm(ks[9], (L, 2, D_GROUP), 0.1),
        'hgrn_norm_g': 1.0 + nrm(ks[10], (L, HG_DV), 0.02),
        'ret_decay_logit': ret_logit + nrm(ks[11], (L, 2, RET_HEADS), 0.05),
        'gdn_conv_w': nrm(ks[12], (L, CONV_K, CONV_K, 3 * D_GROUP), 1.0 / CONV_K),
        'gdn_a_log': jnp.log(jax.random.uniform(ks[13], (L, 2, GDN_HEADS), F32, 1.0, 16.0)),
        'gdn_dt_bias': gdn_dt + jnp.log(-jnp.expm1(-gdn_dt)),
        'gdn_norm_g': 1.0 + nrm(ks[14], (L, GDN_DV), 0.02),
        's5_lam_re': -0.5 + nrm(ks[15], (L, 2, G, P), 0.01),
        's5_lam_im': jnp.pi * jnp.arange(P, dtype=F32) + nrm(ks[17], (L, 2, G, P), 0.01),
        's5_log_dt': jax.random.uniform(ks[18], (L, 2, G), F32, lo, hi),
        's5_b_re': nrm(ks[19], (L, G, P, S5_GROUP), (2 * S5_GROUP) ** -0.5),
        's5_b_im': nrm(ks[20], (L, G, P, S5_GROUP), (2 * S5_GROUP) ** -0.5),
        's5_c_re': nrm(ks[21], (L, G, S5_GROUP, P), P ** -0.5),
        's5_c_im': nrm(ks[22], (L, G, S5_GROUP, P), P ** -0.5),
        's5_d': nrm(ks[23], (L, D_GROUP)),
        's5_glu_w': nrm(ks[24], (L, D_GROUP, D_GROUP), D_GROUP ** -0.5),
        's5_glu_b': nrm(ks[25], (L, D_GROUP), 0.02),
        'w_out': nrm(ks[26], (L, D, D), D ** -0.5),
        'mlp_w1': nrm(ks[27], (L, D, D_FF), D ** -0.5),
        'mlp_w2': nrm(ks[28], (L, D_FF, D), D_FF ** -0.5),
        'final_norm_g': 1.0 + nrm(ks[29], (D,), 0.02),
    }


def reference(x, c, ctx, c_ctx, mod_w, mod_b, norm1_g, norm2_g, w_in, hgrn_lb_logits, hgrn_norm_g,
              ret_decay_logit, gdn_conv_w, gdn_a_log, gdn_dt_bias, gdn_norm_g, s5_lam_re, s5_lam_im,
              s5_log_dt, s5_b_re, s5_b_im, s5_c_re, s5_c_im, s5_d, s5_glu_w, s5_glu_b, w_out,
              mlp_w1, mlp_w2, final_norm_g):
    t_lat, t_ctx = x.shape[1], ctx.shape[1]
    rows = t_lat // GRID_W
    pos_c = jnp.arange(t_ctx)
    pos_l = t_ctx + jnp.arange(t_lat)
    sm = jax.nn.softmax(hgrn_lb_logits.astype(F32), axis=0)
    lbs = jnp.cumsum(sm, axis=0) - sm[:1]
    for l in range(DEPTH):
        need_ctx = l < DEPTH - 1
        ml = modulation(c, mod_w[l], mod_b[l])
        mc = modulation(c_ctx, mod_w[l], mod_b[l])
        hl = modulate(rmsnorm(x, norm1_g[l]), ml[0], ml[1])
        hc = modulate(rmsnorm(ctx, norm1_g[l]), mc[0], mc[1])
        y_c, y_l = hybrid_mixer(
            hc, hl, rows, pos_c, pos_l, need_ctx, w_in[l].astype(F32), lbs[l], hgrn_norm_g[l],
            jax.nn.log_sigmoid(ret_decay_logit[l].astype(F32)), gdn_conv_w[l], gdn_a_log[l],
            gdn_dt_bias[l], gdn_norm_g[l], s5_lam_re[l], s5_lam_im[l], s5_log_dt[l], s5_b_re[l],
            s5_b_im[l], s5_c_re[l], s5_c_im[l], s5_d[l], s5_glu_w[l], s5_glu_b[l], w_out[l])
        x = x + (ml[2] * y_l).astype(x.dtype)
        h2 = modulate(rmsnorm(x, norm2_g[l]), ml[3], ml[4])
        x = x + (ml[5] * sqrelu_mlp(h2, mlp_w1[l], mlp_w2[l])).astype(x.dtype)
        if need_ctx:
            ctx = ctx + (mc[2] * y_c).astype(ctx.dtype)
            h2c = modulate(rmsnorm(ctx, norm2_g[l]), mc[3], mc[4])
            ctx = ctx + (mc[5] * sqrelu_mlp(h2c, mlp_w1[l], mlp_w2[l])).astype(ctx.dtype)
    return rmsnorm(x, final_norm_g).astype(x.dtype)
```

```python
import numpy as np
import concourse.bass as bass
import concourse.mybir as mybir
from concourse.bass_utils import run_bass_kernel_spmd
from contextlib import ExitStack

F32 = mybir.dt.float32
BF16 = mybir.dt.bfloat16
AF = mybir.ActivationFunctionType
ALU = mybir.AluOpType

ENGS = ("pe", "act", "dve", "pool", "sp")
EPOCH = 16000
N_DMA_SEM = 32


class Buf:
    __slots__ = ("name", "w", "r", "excl")

    def __init__(self, name="", excl=False):
        self.name = name
        self.w = None
        self.r = []
        self.excl = excl


class T:
    def __init__(self, h, name, excl=False):
        self.h = h
        self.name = name
        self.b = Buf(name, excl)
        self.excl = excl
        self.subs = {}

    def __getitem__(self, k):
        return self.h[k]

    def s(self, key):
        if self.excl:
            return self.b
        if key not in self.subs:
            self.subs[key] = Buf("%s.%s" % (self.name, key))
        return self.subs[key]


class Prog:
    def __init__(self, nc):
        self.nc = nc
        self.es = ExitStack()
        self.stack = [self.es]
        self.ops = {e: [] for e in ENGS}
        self.cnt = {e: 0 for e in ENGS}
        self.seen = {e: {} for e in ENGS}
        self.last = {}
        self.dma_k = 0
        self.dma_use = [0] * N_DMA_SEM
        self.dma_sems = [self.es.enter_context(nc.semaphore("dq%d" % i)) for i in range(N_DMA_SEM)]
        self.eng_sems = {}
        self.out_tokens = []
        self.n_ops = 0
        self.uid = 0

    def _nm(self, name):
        self.uid += 1
        return "%s_%d" % (name, self.uid)

    def sbuf(self, name, shape, dt=F32):
        h = self.stack[-1].enter_context(self.nc.sbuf_tensor(self._nm(name), list(shape), dt))
        return T(h, name)

    def psum(self, name, shape, dt=F32):
        h = self.stack[-1].enter_context(self.nc.psum_tensor(self._nm(name), list(shape), dt))
        return T(h, name, excl=True)

    def dram(self, name, shape, dt=F32, kind="Internal"):
        h = self.nc.dram_tensor(name, list(shape), dt, kind=kind)
        return T(h.ap(), name)

    class _Scope:
        def __init__(self, p):
            self.p = p

        def __enter__(self):
            st = ExitStack()
            self.p.stack.append(st)
            return st

        def __exit__(self, *a):
            self.p.barrier()
            st = self.p.stack.pop()
            st.close()
            return False

    def scope(self):
        return Prog._Scope(self)

    def _eng_sem(self, e, epoch):
        k = (e, epoch)
        if k not in self.eng_sems:
            self.eng_sems[k] = self.es.enter_context(self.nc.semaphore("s_%s_%d" % (e, epoch)))
        return self.eng_sems[k]

    def _waits(self, eng, reads, writes, extra=()):
        need = {}

        def add(tok):
            if tok is None:
                return
            key, val = tok
            if need.get(key, 0) < val:
                need[key] = val
        for b in reads:
            add(b.w)
        for b in writes:
            add(b.w)
            for t in b.r:
                add(t)
        for t in extra:
            add(t)
        out = []
        seen = self.seen[eng]
        for key, val in need.items():
            if seen.get(key, 0) < val:
                seen[key] = val
                out.append((key, val))
        return out

    @staticmethod
    def _bufs(xs):
        out = []
        for x in xs:
            if x is None:
                continue
            out.append(x.b if isinstance(x, T) else x)
        return out

    def _commit(self, tok, reads, writes):
        self.last[tok[0]] = tok[1]
        for b in reads:
            b.r.append(tok)
            if len(b.r) > 64:
                mx = {}
                for k, v in b.r:
                    if mx.get(k, 0) < v:
                        mx[k] = v
                b.r = list(mx.items())
        for b in writes:
            b.w = tok
            b.r = []
        self.n_ops += 1

    def op(self, eng, fn, reads=(), writes=()):
        reads = self._bufs(reads)
        writes = self._bufs(writes)
        ex = [b for b in reads if b.excl]
        if ex:
            reads = [b for b in reads if not b.excl]
            writes = writes + [b for b in ex if b not in writes]
        waits = self._waits(eng, reads, writes)
        self.cnt[eng] += 1
        epoch, val = divmod(self.cnt[eng] - 1, EPOCH)
        tok = (("e", eng, epoch), val + 1)
        self.ops[eng].append((waits, fn, tok))
        self._commit(tok, reads, writes)
        return tok

    def dma(self, out_ap, in_ap, reads=(), writes=(), q="sp", is_output=False, **kw):
        reads = self._bufs(reads)
        writes = self._bufs(writes)
        i = self.dma_k % N_DMA_SEM
        self.dma_k += 1
        prev = self.dma_use[i]
        extra = [(("d", i), 16 * prev)] if prev else []
        waits = self._waits(q, reads, writes, extra)
        self.dma_use[i] = prev + 1
        tok = (("d", i), 16 * (prev + 1))

        def fn(e):
            return e.dma_start(out=out_ap, in_=in_ap, **kw)
        self.ops[q].append((waits, fn, tok))
        self._commit(tok, reads, writes)
        if is_output:
            self.out_tokens.append(tok)
        return tok

    def barrier(self):
        toks = list(self.last.items())
        for e in ENGS:
            waits = self._waits(e, [], [], toks)
            if waits:
                self.ops[e].append((waits, None, None))

    def _sem_of(self, key):
        if key[0] == "d":
            return self.dma_sems[key[1]]
        return self._eng_sem(key[1], key[2])

    def emit(self):
        nc = self.nc
        self.barrier()
        for e in ENGS:
            for waits, fn, tok in self.ops[e]:
                if tok is not None:
                    self._sem_of(tok[0])
                for key, val in waits:
                    self._sem_of(key)
        with nc.Block() as block:
            def run(e, handle):
                for waits, fn, tok in self.ops[e]:
                    for key, val in waits:
                        handle.wait_ge(self._sem_of(key), val)
                    if fn is None:
                        continue
                    ins = fn(handle)
                    key, val = tok
                    ins.then_inc(self._sem_of(key), 16 if key[0] == "d" else 1)

            @block.sync
            def _(h):
                run("sp", h)

            @block.tensor
            def _(h):
                run("pe", h)

            @block.scalar
            def _(h):
                run("act", h)

            @block.vector
            def _(h):
                run("dve", h)

            @block.gpsimd
            def _(h):
                run("pool", h)

    def close(self):
        self.es.close()


D = 1024
S = 4352
NT = 34
LAT0 = 256
DEPTH = 2
EPS = 1e-6
NEG = -30000.0
ORDER = [list(range(NT)), [1, 0] + list(range(NT - 1, 1, -1))]

C_HQ, C_HI, C_HG, C_HFF, C_HFB = 0, 256, 512, 768, 1024
C_RQ, C_RK, C_RV, C_RG = 1280, 1536, 1792, 2048
C_GQKV, C_GG, C_GA, C_GB, C_SU = 2304, 3072, 3328, 3336, 3344
Z_HQ, Z_HG, Z_HFF, Z_HFB, Z_RQ, Z_RK, Z_RG, Z_GQKV, Z_GG, Z_SU = 0, 256, 512, 768, 1024, 1280, 1536, 1792, 2560, 2816
NZF = 3072
FM_MAP = [(Z_HQ, C_HQ, 256), (Z_HG, C_HG, 256), (Z_HFF, C_HFF, 256), (Z_HFB, C_HFB, 256), (Z_RQ, C_RQ, 256),
          (Z_RK, C_RK, 256), (Z_RG, C_RG, 256), (Z_GQKV, C_GQKV, 768), (Z_GG, C_GG, 256), (Z_SU, C_SU, 256)]
FM_BLOCKS = [(zr + i, wc + i) for zr, wc, n in FM_MAP for i in range(0, n, 128)]
NZT = 528

CN = {}


def _const_pack():
    mats = []

    def add(name, m):
        CN[name] = len(mats)
        mats.append(np.asarray(m, np.float32))
    p = np.arange(128)[:, None]
    f = np.arange(128)[None, :]
    add("IDENT", (p == f))
    add("ONES", np.ones((128, 128)))
    add("TRIF", (p <= f))
    add("TRIB", (p >= f))
    add("SUFF", (p > f))
    add("PREB", (p < f))
    add("NLE", np.where(p <= f, 0.0, NEG))
    add("NLT", np.where(p < f, 0.0, NEG))
    add("NGE", np.where(p >= f, 0.0, NEG))
    add("NGT", np.where(p > f, 0.0, NEG))
    for s in (1, 2, 4, 8, 16, 32, 64):
        m = (((p // s) % 2) == 1) & ((f // s) == (p // s) - 1)
        add("MOFF%d" % s, m)
        add("MOFFT%d" % s, m.T)
    add("BLK64", (p // 64) == (f // 64))
    rot = np.zeros((128, 128))
    for m in range(128):
        if (m % 64) < 32:
            rot[m + 32, m] = -1.0
        else:
            rot[m - 32, m] = 1.0
    add("ROT", rot)
    add("IOTAF", np.broadcast_to(f, (128, 128)))
    add("IOTAF1", np.broadcast_to(f + 1, (128, 128)))
    add("RIOTAF", np.broadcast_to(128 - f, (128, 128)))
    add("R127F", np.broadcast_to(127 - f, (128, 128)))
    add("DIFF", f - p)
    add("NDIFF", p - f)
    for h in range(4):
        m = np.zeros((128, 128)); m[h, :] = 1.0
        add("SELH%d" % h, m)
    for hp in range(2):
        m = np.zeros((128, 128)); m[2 * hp, 0:64] = 1.0; m[2 * hp + 1, 64:128] = 1.0
        add("SELP%d" % hp, m)
    cc = np.zeros((128, 128))
    cc[:, 0] = EPS; cc[:, 1] = 1.0; cc[:, 2] = np.arange(128); cc[:, 3] = 127 - np.arange(128)
    cc[:, 5] = -np.pi; cc[:, 6] = -np.arange(128); cc[:, 7] = -(127 - np.arange(128))
    add("CCOL", cc)
    gm = np.zeros((128, 128))
    for g in range(16):
        gm[(g % 8) * 16:(g % 8) * 16 + 16, g] = 1.0
    add("GMASK", gm)
    return np.concatenate(mats, axis=1)


CONST_NP = _const_pack()
NCONST = CONST_NP.shape[1] // 128


def _rope_tables():
    half = 32
    inv = 10000.0 ** (-np.arange(half, dtype=np.float64) / half)
    pos = np.arange(S, dtype=np.float64)
    ang = pos[None, :] * inv[:, None]
    cos = np.cos(ang); sin = np.sin(ang)
    cos128 = np.tile(cos, (4, 1)); sin128 = np.tile(sin, (4, 1))
    return cos128.astype(np.float32), sin128.astype(np.float32)


def _conv_masks():
    m = np.ones((2, 512), np.float32)
    w = np.arange(512) % 64
    m[0, w == 0] = 0.0
    m[1, w == 63] = 0.0
    lat = np.broadcast_to(m[None], (128, 2, 512)).copy()
    c = np.ones((2, 256), np.float32)
    c[0, 0] = 0.0
    c[1, 255] = 0.0
    ctx = np.broadcast_to(c[None], (128, 2, 256)).copy()
    return lat, ctx


class KB:
    def __init__(self, cfg):
        self.cfg = cfg
        nc = bass.Bass("TRN2", target_bir_lowering=False)
        self.nc = nc
        self.P = Prog(nc)
        self.rr = 0

    def MM(self, ps, lhsT, rhs, st, sp, R, W):
        self.P.op("pe", lambda e: e.matmul(ps, lhsT, rhs, start=st, stop=sp), R, W)

    def TR(self, ps, in_, ident, R, W):
        self.P.op("pe", lambda e: e.transpose(ps, in_, ident), R, W)

    def ACT(self, out, in_, func, R, W, **kw):
        self.P.op("act", lambda e: e.activation(out=out, in_=in_, func=func, **kw), R, W)

    def TS(self, out, in0, s1, s2, op0, op1, R, W, eng="dve"):
        if s2 is None:
            self.P.op(eng, lambda e: e.tensor_scalar(out=out, in0=in0, scalar1=s1, scalar2=None, op0=op0), R, W)
        else:
            self.P.op(eng, lambda e: e.tensor_scalar(out=out, in0=in0, scalar1=s1, scalar2=s2, op0=op0, op1=op1), R, W)

    def TT(self, out, in0, in1, op, R, W, eng="dve"):
        self.P.op(eng, lambda e: e.tensor_tensor(out=out, in0=in0, in1=in1, op=op), R, W)

    def STT(self, out, in0, sc, in1, op0, op1, R, W, eng="dve"):
        eng = "dve"
        self.P.op(eng, lambda e: e.scalar_tensor_tensor(out=out, in0=in0, scalar=sc, in1=in1, op0=op0, op1=op1), R, W)

    def CP(self, out, in_, R, W, eng="dve"):
        if eng == "act":
            self.ACT(out, in_, AF.Copy, R, W)
        else:
            self.P.op(eng, lambda e: e.tensor_copy(out=out, in_=in_), R, W)

    def MS(self, ap, val, W, eng="dve"):
        self.P.op(eng, lambda e: e.memset(ap, val), (), W)

    def RECIP(self, out, in_, R, W):
        self.P.op("dve", lambda e: e.reciprocal(out=out, in_=in_), R, W)

    def SCAN(self, out, d0, d1, R, W):
        self.P.op("dve", lambda e: e.tensor_tensor_scan(out=out, data0=d0, data1=d1, initial=0.0,
                                                        op0=ALU.mult, op1=ALU.add), R, W)

    def LD(self, out, in_, W, R=(), q="sp", **kw):
        self.P.dma(out, in_, reads=R, writes=W, q=q, **kw)

    def ST(self, out, in_, R, W=(), q="pool", **kw):
        self.P.dma(out, in_, reads=R, writes=W, q=q, **kw)

    def evac_eng(self):
        self.rr += 1
        return "act" if self.rr % 2 else "dve"

    def C(self, name):
        i = CN[name]
        return self.const[:, i * 128:(i + 1) * 128]


PARAM_SHAPES = {
    "mod_w": [2, 1024, 6144], "mod_b": [2, 6144], "norm1_g": [2, 1024], "norm2_g": [2, 1024],
    "w_in": [2, 1024, 3600], "hgrn_lb_logits": [2, 2, 256], "hgrn_norm_g": [2, 64],
    "ret_decay_logit": [2, 2, 4], "gdn_conv_w": [2, 3, 3, 768], "gdn_a_log": [2, 2, 4],
    "gdn_dt_bias": [2, 2, 4], "gdn_norm_g": [2, 64], "s5_lam_re": [2, 2, 16, 64],
    "s5_lam_im": [2, 2, 16, 64], "s5_log_dt": [2, 2, 16], "s5_b_re": [2, 16, 64, 16],
    "s5_b_im": [2, 16, 64, 16], "s5_c_re": [2, 16, 16, 64], "s5_c_im": [2, 16, 16, 64],
    "s5_d": [2, 256], "s5_glu_w": [2, 256, 256], "s5_glu_b": [2, 256], "w_out": [2, 1024, 1024],
    "mlp_w1": [2, 1024, 4096], "mlp_w2": [2, 4096, 1024], "final_norm_g": [1024],
}


def declare(kb):
    P = kb.P
    cfg = kb.cfg
    kinds = cfg.get("kinds", {})
    kb.xin = P.dram("xin", [S, D], F32, kind="ExternalInput")
    kb.cvecT = P.dram("cvecT", [1024, 2], F32, kind="ExternalInput")
    kb.prm = {k: P.dram(k, shp, F32, kind="ExternalInput") for k, shp in PARAM_SHAPES.items()}
    kb.constd = P.dram("constp", [128, NCONST * 128], F32, kind="ExternalInput")
    kb.ropec = P.dram("ropec", [128, S], F32, kind="ExternalInput")
    kb.ropes = P.dram("ropes", [128, S], F32, kind="ExternalInput")
    kb.cmlat = P.dram("cmlat", [128, 2, 512], F32, kind="ExternalInput")
    kb.cmctx = P.dram("cmctx", [128, 2, 256], F32, kind="ExternalInput")
    kb.y = P.dram("y", [4096, D], F32, kind="ExternalOutput")
    kb.XS = P.dram("XS", [S, D], F32, kind=kinds.get("XS", "Internal"))
    kb.ZF = P.dram("ZF", [NZF, S], F32, kind=kinds.get("ZF", "Internal"))
    kb.ZT = P.dram("ZT", [S, NZT], F32, kind=kinds.get("ZT", "Internal"))
    kb.QKVF = P.dram("QKVF", [768, S], F32, kind=kinds.get("QKVF", "Internal"))
    kb.YC = P.dram("YC", [1024, S], F32, kind=kinds.get("YC", "Internal"))
    kb.H2T = P.dram("H2T", [1024, S], BF16, kind=kinds.get("H2T", "Internal"))
    kb.const = P.sbuf("const", [128, NCONST * 128])
    nchunk = 4
    w = NCONST * 128 // nchunk
    for i in range(nchunk):
        a, b = i * w, (i + 1) * w if i < nchunk - 1 else NCONST * 128
        kb.LD(kb.const[:, a:b], kb.constd[:, a:b], [kb.const.s(i)])
    kb.const_bufs = [kb.const.s(i) for i in range(nchunk)]
    kb.CB = kb.const_bufs
    kb.GS1 = P.sbuf("GS1", [128, 8, 2]); kb.SH1 = P.sbuf("SH1", [128, 8, 2])
    kb.GS2 = P.sbuf("GS2", [128, 8, 2]); kb.SH2 = P.sbuf("SH2", [128, 8, 2])
    kb.GATE1 = P.sbuf("GATE1", [128, 2, 1024]); kb.GATE2 = P.sbuf("GATE2", [128, 2, 1024])


def phase_mod(kb, l):
    P = kb.P
    prm = kb.prm
    with P.scope():
        cT = P.sbuf("cT", [128, 8, 2])
        kb.LD(cT[:], kb.cvecT[:].rearrange("(et e) c -> e et c", e=128), [cT])
        sc = P.sbuf("sc", [128, 8, 2])
        kb.ACT(sc[:], cT[:], AF.Silu, [cT], [sc])
        screp = P.sbuf("screp", [128, 8, 2, 128])
        kb.CP(screp[:], sc[:].unsqueeze(3).to_broadcast([128, 8, 2, 128]), [sc], [screp])
        mbf = P.sbuf("mbf", [128, 48])
        kb.LD(mbf[:], prm["mod_b"][l].rearrange("(j p) -> p j", p=128), [mbf], allow_slow_non_contiguous=True)
        ngf = P.sbuf("ngf", [128, 2, 8])
        kb.LD(ngf[:, 0, :], prm["norm1_g"][l].rearrange("(j p) -> p j", p=128), [ngf], allow_slow_non_contiguous=True)
        kb.LD(ngf[:, 1, :], prm["norm2_g"][l].rearrange("(j p) -> p j", p=128), [ngf], allow_slow_non_contiguous=True)
        mbrow = P.sbuf("mbrow", [128, 2, 1024])
        for gi, v in enumerate((2, 5)):
            kb.LD(mbrow[:, gi, :], prm["mod_b"][l][v * 1024:(v + 1) * 1024].partition_broadcast(128), [mbrow])
        wch = [P.sbuf("wch%d" % i, [128, 8, 1024]) for i in range(2)]
        ps_fm = P.psum("ps_fm", [128, 96])
        ps_g = [P.psum("ps_g%d" % i, [128, 512]) for i in range(2)]
        MF = P.sbuf("MF", [128, 48, 2])
        k = 0
        for v in range(6):
            wc = wch[v % 2]
            for et in range(8):
                kb.LD(wc[:, et, :], prm["mod_w"][l][et * 128:(et + 1) * 128, v * 1024:(v + 1) * 1024], [wc])
            for db in range(8):
                col = (v * 8 + db) * 2
                for et in range(8):
                    kb.MM(ps_fm[:, col:col + 2], wc[:, et, db * 128:(db + 1) * 128], sc[:, et, :],
                          et == 0, et == 7, [wc, sc], [ps_fm])
            if v in (2, 5):
                gt = kb.GATE1 if v == 2 else kb.GATE2
                gi = 0 if v == 2 else 1
                for which in range(2):
                    for half in range(2):
                        pg = ps_g[k % 2]; k += 1
                        for et in range(8):
                            kb.MM(pg[:], screp[:, et, which, :], wc[:, et, half * 512:(half + 1) * 512],
                                  et == 0, et == 7, [screp, wc], [pg])
                        kb.TT(gt[:, which, half * 512:(half + 1) * 512], pg[:], mbrow[:, gi, half * 512:(half + 1) * 512],
                              ALU.add, [pg, mbrow], [gt])
        kb.TT(MF[:], ps_fm[:].rearrange("p (j c) -> p j c", c=2), mbf[:].unsqueeze(2).to_broadcast([128, 48, 2]),
              ALU.add, [ps_fm, mbf], [MF])
        tmp = P.sbuf("mtmp", [128, 8, 2])
        kb.TS(tmp[:], MF[:, 8:16, :], 1.0, None, ALU.add, None, [MF], [tmp])
        kb.TT(kb.GS1[:], tmp[:], ngf[:, 0, :].unsqueeze(2).to_broadcast([128, 8, 2]), ALU.mult, [tmp, ngf], [kb.GS1])
        kb.CP(kb.SH1[:], MF[:, 0:8, :], [MF], [kb.SH1])
        tmp2 = P.sbuf("mtmp2", [128, 8, 2])
        kb.TS(tmp2[:], MF[:, 32:40, :], 1.0, None, ALU.add, None, [MF], [tmp2])
        kb.TT(kb.GS2[:], tmp2[:], ngf[:, 1, :].unsqueeze(2).to_broadcast([128, 8, 2]), ALU.mult, [tmp2, ngf], [kb.GS2])
        kb.CP(kb.SH2[:], MF[:, 24:32, :], [MF], [kb.SH2])


def norm_to_fm(kb, xt, hT, col0, GS, SH, which, bufs, R_x):
    P = kb.P
    junk, st, xn, ps_ts = bufs["junk"], bufs["st"], bufs["xn"], bufs["ps_t"]
    kb.MS(st[:, 0:1], 0.0, [st])
    kb.ACT(junk[:], xt[:], AF.Square, [xt], [junk, st], accum_out=st[:, 0:1])
    kb.ACT(st[:, 1:2], st[:, 0:1], AF.Sqrt, [st] + kb.CB, [st], scale=1.0 / D, bias=kb.C("CCOL")[:, 0:1])
    kb.RECIP(st[:, 2:3], st[:, 1:2], [st], [st])
    kb.ACT(xn[:], xt[:], AF.Copy, [xt, st], [xn], scale=st[:, 2:3])
    for half in range(2):
        ps_t = ps_ts[half]
        for q in range(4):
            dt = half * 4 + q
            kb.TR(ps_t[:, q * 128:(q + 1) * 128], xn[:, dt * 128:(dt + 1) * 128], kb.C("IDENT"), [xn] + kb.CB, [ps_t])
        for q in range(4):
            dt = half * 4 + q
            if q % 2 == 0:
                kb.TS(hT[:, dt, col0:col0 + 128], ps_t[:, q * 128:(q + 1) * 128], GS[:, dt, which:which + 1],
                      SH[:, dt, which:which + 1], ALU.mult, ALU.add, [ps_t, GS, SH], [hT])
            else:
                kb.ACT(hT[:, dt, col0:col0 + 128], ps_t[:, q * 128:(q + 1) * 128], AF.Identity, [ps_t, GS, SH], [hT],
                       scale=GS[:, dt, which:which + 1], bias=SH[:, dt, which:which + 1])


def phase_a(kb, l, src):
    P = kb.P
    with P.scope():
        win = P.sbuf("win", [128, 8, 3600], BF16)
        for kt in range(8):
            kb.LD(win[:, kt, :], kb.prm["w_in"][l][kt * 128:(kt + 1) * 128, :], [win.s(kt)], q="pool")
        winb = [win.s(kt) for kt in range(8)]
        xbuf = [P.sbuf("xa%d" % i, [128, 1024]) for i in range(2)]
        hTb = [P.sbuf("hTa%d" % i, [128, 8, 512], BF16) for i in range(2)]
        nb = {"junk": P.sbuf("junk", [128, 1024]), "st": P.sbuf("st", [128, 4]), "xn": P.sbuf("xn", [128, 1024]),
              "ps_t": [P.psum("ps_t%d" % i, [128, 512]) for i in range(2)]}
        ps_f = [P.psum("ps_f%d" % i, [128, 512]) for i in range(3)]
        ps_a = [P.psum("ps_a%d" % i, [128, 512]) for i in range(2)]
        ps_b = P.psum("ps_b", [128, 16])
        stg = [P.sbuf("stg%d" % i, [128, 512]) for i in range(4)]
        stt = [P.sbuf("stt%d" % i, [128, NZT]) for i in range(2)]
        kx = kf = ks = ka = 0
        for gi, t0 in enumerate(range(0, S, 512)):
            n = min(512, S - t0)
            hT = hTb[gi % 2]
            for ti in range(n // 128):
                tt = t0 // 128 + ti
                which = 1 if tt < 2 else 0
                xt = xbuf[kx % 2]; kx += 1
                kb.LD(xt[:], src[tt * 128:(tt + 1) * 128, :], [xt])
                norm_to_fm(kb, xt, hT, ti * 128, kb.GS1, kb.SH1, which, nb, None)
            for (zr, wc) in FM_BLOCKS:
                ps = ps_f[kf % 3]; kf += 1
                for kt in range(8):
                    kb.MM(ps[:, :n], win[:, kt, wc:wc + 128], hT[:, kt, :n], kt == 0, kt == 7, [winb[kt], hT], [ps])
                sg = stg[ks % 4]; ks += 1
                kb.CP(sg[:, :n], ps[:, :n], [ps], [sg], eng=kb.evac_eng())
                kb.ST(kb.ZF[zr:zr + 128, t0:t0 + n], sg[:, :n], [sg])
            for ti in range(n // 128):
                tt = t0 // 128 + ti
                pa = ps_a[ka % 2]
                so = stt[ka % 2]; ka += 1
                for (c0, w0, wn) in ((0, C_HI, 256), (256, C_RV, 256)):
                    for kt in range(8):
                        kb.MM(pa[:, c0:c0 + wn], hT[:, kt, ti * 128:(ti + 1) * 128], win[:, kt, w0:w0 + wn],
                              kt == 0, kt == 7, [winb[kt], hT], [pa])
                for kt in range(8):
                    kb.MM(ps_b[:], hT[:, kt, ti * 128:(ti + 1) * 128], win[:, kt, C_GA:C_GA + 16],
                          kt == 0, kt == 7, [winb[kt], hT], [ps_b])
                kb.CP(so[:, 0:512], pa[:], [pa], [so], eng="act")
                kb.CP(so[:, 512:528], ps_b[:], [ps_b], [so], eng="dve")
                kb.ST(kb.ZT[tt * 128:(tt + 1) * 128, :], so[:], [so])


def build(cfg):
    kb = KB(cfg)
    P = kb.P
    declare(kb)
    P.barrier()
    stages = cfg.get("stages", "all")
    for l in cfg.get("layers", range(DEPTH)):
        src = kb.xin if l == 0 else kb.XS
        if stages == "all" or "M" in stages:
            phase_mod(kb, l)
        if stages == "all" or "A" in stages:
            phase_a(kb, l, src)
        if stages == "all" or "R" in stages:
            mixer_ret(kb, l)
        if stages == "all" or "H" in stages:
            mixer_hgrn(kb, l)
        if stages == "all" or "G" in stages:
            mixer_gdn(kb, l)
        if stages == "all" or "S" in stages:
            mixer_s5(kb, l)
        if stages == "all" or "C" in stages:
            phase_c(kb, l, src)
    P.emit()
    P.close()
    return kb


_CONSTS = None


def host_inputs(inputs, cores=range(8)):
    global _CONSTS
    if _CONSTS is None:
        rc, rs = _rope_tables()
        cl, cc = _conv_masks()
        _CONSTS = {"constp": CONST_NP, "ropec": rc, "ropes": rs, "cmlat": cl, "cmctx": cc}
    maps = []
    for b in cores:
        m = {"xin": np.ascontiguousarray(np.concatenate([inputs["ctx"][b], inputs["x"][b]], axis=0), dtype=np.float32),
             "cvecT": np.ascontiguousarray(np.stack([inputs["c"][b], inputs["c_ctx"]], axis=1), dtype=np.float32)}
        for k in PARAM_SHAPES:
            m[k] = np.ascontiguousarray(inputs[k], dtype=np.float32)
        m.update(_CONSTS)
        maps.append(m)
    return maps


def kernel(**inputs):
    inputs = {k: np.asarray(v) for k, v in inputs.items()}
    kb = build({})
    maps = host_inputs(inputs)
    res = run_bass_kernel_spmd(kb.nc, maps, core_ids=list(range(8)))
    out = np.stack([np.asarray(r["y"]).reshape(4096, D) for r in res.results], axis=0)
    return out.astype(np.float32)


def phase_c(kb, l, src):
    P = kb.P
    last = (l == DEPTH - 1)
    t_start = 2 if last else 0
    with P.scope():
        wout = P.sbuf("wout", [128, 8, 1024], BF16)
        for ft in range(8):
            kb.LD(wout[:, ft, :], kb.prm["w_out"][l][ft * 128:(ft + 1) * 128, :], [wout.s(ft)], q="pool")
        wb = [wout.s(ft) for ft in range(8)]
        ycb = [P.sbuf("yc%d" % i, [128, 8, 128], BF16) for i in range(2)]
        xb = [P.sbuf("xc%d" % i, [128, 1024]) for i in range(2)]
        x1b = [P.sbuf("x1c%d" % i, [128, 1024]) for i in range(2)]
        tmpb = [P.sbuf("tc%d" % i, [128, 512]) for i in range(2)]
        h2b = [P.sbuf("h2c%d" % i, [128, 8, 128], BF16) for i in range(2)]
        nb = {"junk": P.sbuf("junkc", [128, 1024]), "st": P.sbuf("stc", [128, 4]), "xn": P.sbuf("xnc", [128, 1024]),
              "ps_t": [P.psum("ps_tc%d" % i, [128, 512]) for i in range(2)]}
        ps_y = [P.psum("ps_y%d" % i, [128, 512]) for i in range(4)]
        k = 0
        for tt in range(t_start, NT):
            which = 1 if tt < 2 else 0
            yc = ycb[k % 2]; xt = xb[k % 2]; x1 = x1b[k % 2]; h2 = h2b[k % 2]
            cols = slice(tt * 128, (tt + 1) * 128)
            kb.LD(yc[:], kb.YC[:, cols].rearrange("(ft p) t -> p ft t", p=128), [yc], q="pool")
            kb.LD(xt[:], src[cols, :], [xt])
            for half in range(2):
                ps = ps_y[(2 * k + half) % 4]
                for ft in range(8):
                    kb.MM(ps[:], yc[:, ft, :], wout[:, ft, half * 512:(half + 1) * 512], ft == 0, ft == 7,
                          [yc, wb[ft]], [ps])
                tm = tmpb[half]
                kb.TT(tm[:], ps[:], kb.GATE1[:, which, half * 512:(half + 1) * 512], ALU.mult, [ps, kb.GATE1], [tm])
                kb.TT(x1[:, half * 512:(half + 1) * 512], xt[:, half * 512:(half + 1) * 512], tm[:], ALU.add,
                      [xt, tm], [x1], eng="pool")
            kb.ST(kb.XS[cols, :], x1[:], [x1])
            norm_to_fm(kb, x1, h2, 0, kb.GS2, kb.SH2, which, nb, None)
            kb.ST(kb.H2T[:, cols].rearrange("(dt p) t -> p dt t", p=128), h2[:], [h2])
            k += 1
    with P.scope():
        w1 = P.sbuf("w1", [128, 8, 4096], BF16)
        w2 = P.sbuf("w2", [128, 32, 1024], BF16)
        for kt in range(8):
            kb.LD(w1[:, kt, :], kb.prm["mlp_w1"][l][kt * 128:(kt + 1) * 128, :], [w1.s(kt)], q="pool")
        for fb in range(32):
            kb.LD(w2[:, fb, :], kb.prm["mlp_w2"][l][fb * 128:(fb + 1) * 128, :], [w2.s(fb)], q="pool")
        h2b = [P.sbuf("h2d%d" % i, [128, 8, 256], BF16) for i in range(2)]
        uTb = [P.sbuf("uT%d" % i, [128, 16, 256], BF16) for i in range(1)]
        rb = [P.sbuf("relu%d" % i, [128, 256]) for i in range(3)]
        xb = [P.sbuf("xd%d" % i, [128, 1024]) for i in range(2)]
        tmpb = [P.sbuf("td%d" % i, [128, 512]) for i in range(2)]
        ps_u = [P.psum("ps_u%d" % i, [128, 256]) for i in range(3)]
        ps_y = [P.psum("ps_y2%d" % i, [128, 512]) for i in range(4)]
        if last:
            fg = P.sbuf("fg", [128, 1024])
            kb.LD(fg[:], kb.prm["final_norm_g"][:].partition_broadcast(128), [fg])
            stf = P.sbuf("stf", [128, 4])
            xnf = P.sbuf("xnf", [128, 1024])
        k = 0; ku = 0
        for g0 in range(t_start, NT, 2):
            h2 = h2b[k % 2]; uT = uTb[0]
            cols = slice(g0 * 128, (g0 + 2) * 128)
            kb.LD(h2[:], kb.H2T[:, cols].rearrange("(dt p) t -> p dt t", p=128), [h2])
            for hh in range(2):
                for fl in range(16):
                    fb = hh * 16 + fl
                    ps = ps_u[ku % 3]; r = rb[ku % 3]; ku += 1
                    for kt in range(8):
                        kb.MM(ps[:], w1[:, kt, fb * 128:(fb + 1) * 128], h2[:, kt, :], kt == 0, kt == 7, [w1.s(kt), h2], [ps])
                    kb.ACT(r[:], ps[:], AF.Relu, [ps], [r])
                    kb.TT(uT[:, fl, :], r[:], r[:], ALU.mult, [r], [uT.s(fl)], eng=("dve" if fb % 2 else "pool"))
                for ti in range(2):
                    for half in range(2):
                        ps = ps_y[2 * ti + half]
                        for fl in range(16):
                            fb = hh * 16 + fl
                            kb.MM(ps[:], uT[:, fl, ti * 128:(ti + 1) * 128], w2[:, fb, half * 512:(half + 1) * 512],
                                  fb == 0, fb == 31, [uT.s(fl), w2.s(fb)], [ps])
            for ti in range(2):
                tt = g0 + ti
                which = 1 if tt < 2 else 0
                xt = xb[ti]
                rows = slice(tt * 128, (tt + 1) * 128)
                kb.LD(xt[:], kb.XS[rows, :], [xt])
                for half in range(2):
                    ps = ps_y[2 * ti + half]
                    tm = tmpb[half]
                    kb.TT(tm[:], ps[:], kb.GATE2[:, which, half * 512:(half + 1) * 512], ALU.mult, [ps, kb.GATE2], [tm])
                    kb.TT(xt[:, half * 512:(half + 1) * 512], xt[:, half * 512:(half + 1) * 512], tm[:], ALU.add,
                          [xt, tm], [xt], eng="pool")
                if not last:
                    kb.ST(kb.XS[rows, :], xt[:], [xt])
                else:
                    kb.MS(stf[:, 0:1], 0.0, [stf])
                    kb.ACT(xnf[:], xt[:], AF.Square, [xt], [xnf, stf], accum_out=stf[:, 0:1])
                    kb.ACT(stf[:, 1:2], stf[:, 0:1], AF.Sqrt, [stf], [stf], scale=1.0 / D, bias=kb.C("CCOL")[:, 0:1])
                    kb.RECIP(stf[:, 2:3], stf[:, 1:2], [stf], [stf])
                    kb.ACT(xnf[:], xt[:], AF.Copy, [xt, stf], [xnf], scale=stf[:, 2:3])
                    kb.TT(xnf[:], xnf[:], fg[:], ALU.mult, [xnf, fg], [xnf])
                    kb.P.dma(kb.y[(tt - 2) * 128:(tt - 1) * 128, :], xnf[:], reads=[xnf.b], q="pool", is_output=True)
            k += 1


def finalize_gated(kb, OACC, gate_row0, gain, yc_row0, pfx):
    P = kb.P
    gb = [P.sbuf(pfx + "fg%d" % i, [128, 2, 128]) for i in range(2)]
    sq = [P.sbuf(pfx + "fsq%d" % i, [128, 128]) for i in range(2)]
    rt = [P.sbuf(pfx + "frt%d" % i, [128, 128]) for i in range(2)]
    sg = [P.sbuf(pfx + "fsg%d" % i, [128, 128]) for i in range(2)]
    ob = [P.sbuf(pfx + "fo%d" % i, [128, 128]) for i in range(2)]
    ps_m = [P.psum(pfx + "fps%d" % i, [128, 128]) for i in range(2)]
    k = 0
    for n in range(NT):
        cols = slice(n * 128, (n + 1) * 128)
        g = gb[n % 2]
        kb.LD(g[:], kb.ZF[gate_row0:gate_row0 + 256, cols].rearrange("(hp p) t -> p hp t", p=128), [g])
        for hp in range(2):
            i = k % 2; k += 1
            o = OACC[:, hp, cols]
            kb.TT(sq[i][:], o, o, ALU.mult, [OACC.s(n)], [sq[i]], eng="pool")
            kb.MM(ps_m[i][:], kb.C("BLK64"), sq[i][:], True, True, [sq[i]], [ps_m[i]])
            kb.ACT(rt[i][:], ps_m[i][:], AF.Sqrt, [ps_m[i]], [rt[i]], scale=1.0 / 64, bias=kb.C("CCOL")[:, 0:1])
            kb.RECIP(rt[i][:], rt[i][:], [rt[i]], [rt[i]])
            kb.ACT(sg[i][:], g[:, hp, :], AF.Silu, [g], [sg[i]])
            kb.TT(ob[i][:], o, rt[i][:], ALU.mult, [OACC.s(n), rt[i]], [ob[i]])
            if gain is not None:
                kb.STT(ob[i][:], ob[i][:], gain[:, 0:1], sg[i][:], ALU.mult, ALU.mult, [ob[i], gain, sg[i]], [ob[i]])
            else:
                kb.TT(ob[i][:], ob[i][:], sg[i][:], ALU.mult, [ob[i], sg[i]], [ob[i]])
            kb.ST(kb.YC[yc_row0 + hp * 128:yc_row0 + (hp + 1) * 128, cols], ob[i][:], [ob[i]])


def oacc_write(kb, OACC, hp, n, ps, d):
    cols = slice(n * 128, (n + 1) * 128)
    if d == 0:
        kb.CP(OACC[:, hp, cols], ps[:], [ps], [OACC.s(n)], eng="act")
    else:
        kb.TT(OACC[:, hp, cols], OACC[:, hp, cols], ps[:], ALU.add, [ps], [OACC.s(n)])


def mixer_ret(kb, l):
    P = kb.P
    with P.scope():
        OACC = P.sbuf("r_oacc", [128, 2, S])
        with P.scope():
            lgt = P.sbuf("r_lgt", [128, 8])
            kb.LD(lgt[:], kb.prm["ret_decay_logit"][l].rearrange("d h -> (d h)").partition_broadcast(128), [lgt])
            LG = P.sbuf("r_LG", [128, 8])
            kb.ACT(LG[:], lgt[:], AF.Sigmoid, [lgt], [LG])
            kb.ACT(LG[:], LG[:], AF.Ln, [LG], [LG])
            LGP = P.sbuf("r_LGP", [128, 4])
            for d in range(2):
                for hp in range(2):
                    c = 2 * d + hp
                    kb.CP(LGP[0:64, c:c + 1], LG[0:64, 4 * d + 2 * hp:4 * d + 2 * hp + 1], [LG], [LGP])
                    kb.CP(LGP[64:128, c:c + 1], LG[64:128, 4 * d + 2 * hp + 1:4 * d + 2 * hp + 2], [LG], [LGP])
            MK = [P.sbuf("r_MK%d" % d, [128, 4, 128]) for d in range(2)]
            QDEC = [[P.sbuf("r_QD%d%d" % (d, hp), [128, 128]) for hp in range(2)] for d in range(2)]
            etmp = P.sbuf("r_etmp", [128, 128])
            for d in range(2):
                for h in range(4):
                    kb.ACT(etmp[:], kb.C("DIFF" if d == 0 else "NDIFF"), AF.Exp, [LG], [etmp],
                           scale=LG[:, 4 * d + h:4 * d + h + 1])
                    kb.STT(MK[d][:, h, :], etmp[:], 0.125, kb.C("TRIF" if d == 0 else "TRIB"), ALU.mult, ALU.mult,
                           [etmp], [MK[d]])
                for hp in range(2):
                    kb.ACT(QDEC[d][hp][:], kb.C("IOTAF1" if d == 0 else "RIOTAF"), AF.Exp, [LGP], [QDEC[d][hp]],
                           scale=LGP[:, 2 * d + hp:2 * d + hp + 1])
            KD = P.sbuf("r_KD", [128, 8])
            kb.ACT(KD[:, 0:4], LG[:, 0:4], AF.Exp, [LG], [KD], scale=kb.C("CCOL")[:, 3:4])
            kb.ACT(KD[:, 4:8], LG[:, 4:8], AF.Exp, [LG], [KD], scale=kb.C("CCOL")[:, 2:3])
            kb.TS(KD[:], KD[:], 0.125, None, ALU.mult, None, [KD], [KD])
            CV = P.sbuf("r_CV", [128, 4])
            kb.ACT(CV[:], LGP[:], AF.Exp, [LGP], [CV], scale=128.0)
            qTb = [P.sbuf("r_q%d" % i, [128, 2, 128]) for i in range(2)]
            kTb = [P.sbuf("r_k%d" % i, [128, 2, 128]) for i in range(2)]
            csb = [P.sbuf("r_cs%d" % i, [128, 2, 128]) for i in range(2)]
            Vp = [[P.sbuf("r_vp%d%d" % (i, h), [128, 128]) for h in range(4)] for i in range(2)]
            khp = [[P.sbuf("r_kh%d%d" % (i, h), [128, 128]) for h in range(4)] for i in range(2)]
            for i in range(2):
                for h in range(4):
                    kb.MS(Vp[i][h][:], 0.0, [Vp[i][h]], eng="pool")
                    kb.MS(khp[i][h][:], 0.0, [khp[i][h]], eng="pool")
            t1 = [P.sbuf("r_t1%d" % i, [128, 128]) for i in range(2)]
            t2 = [P.sbuf("r_t2%d" % i, [128, 128]) for i in range(2)]
            qr = [P.sbuf("r_qr%d" % i, [128, 2, 128]) for i in range(2)]
            kr = [P.sbuf("r_kr%d" % i, [128, 2, 128]) for i in range(2)]
            AT = [P.sbuf("r_AT%d" % i, [128, 2, 128]) for i in range(2)]
            qd = [P.sbuf("r_qd%d" % i, [128, 128]) for i in range(2)]
            Sb = [P.sbuf("r_S%d" % hp, [128, 128]) for hp in range(2)]
            ps_r = [P.psum("r_psr%d" % i, [128, 256]) for i in range(2)]
            ps_s = [P.psum("r_pss%d" % i, [128, 2, 128]) for i in range(2)]
            ps_o = [P.psum("r_pso%d" % i, [128, 128]) for i in range(2)]
            ps_k = P.psum("r_psk", [128, 128])
            ps_kv = P.psum("r_pskv", [128, 128])
            it = 0
            for d in range(2):
                for hp in range(2):
                    kb.MS(Sb[hp][:], 0.0, [Sb[hp]])
                for n in ORDER[d]:
                    cols = slice(n * 128, (n + 1) * 128)
                    b = it % 2; it += 1
                    qT, kT, cs = qTb[b], kTb[b], csb[b]
                    kb.LD(qT[:], kb.ZF[Z_RQ:Z_RQ + 256, cols].rearrange("(hp p) t -> p hp t", p=128), [qT])
                    kb.LD(kT[:], kb.ZF[Z_RK:Z_RK + 256, cols].rearrange("(hp p) t -> p hp t", p=128), [kT])
                    kb.LD(cs[:, 0, :], kb.ropec[:, cols], [cs])
                    kb.LD(cs[:, 1, :], kb.ropes[:, cols], [cs])
                    for h in range(4):
                        kb.LD(Vp[b][h][:, 64 * (h % 2):64 * (h % 2) + 64], kb.ZT[cols, 256 + 64 * h:256 + 64 * h + 64],
                              [Vp[b][h]])
                    for hp in range(2):
                        j = (it * 2 + hp) % 2
                        pr = ps_r[j]
                        kb.MM(pr[:, 0:128], kb.C("ROT"), qT[:, hp, :], True, True, [qT], [pr])
                        kb.MM(pr[:, 128:256], kb.C("ROT"), kT[:, hp, :], True, True, [kT], [pr])
                        for (src_, dst, off) in ((qT, qr[b], 0), (kT, kr[b], 128)):
                            kb.TT(t1[j][:], src_[:, hp, :], cs[:, 0, :], ALU.mult, [src_, cs], [t1[j]], eng="pool")
                            kb.TT(t2[j][:], pr[:, off:off + 128], cs[:, 1, :], ALU.mult, [pr, cs], [t2[j]])
                            kb.TT(dst[:, hp, :], t1[j][:], t2[j][:], ALU.add, [t1[j], t2[j]], [dst.s(hp)], eng="pool")
                        pss = ps_s[j]
                        for h2 in range(2):
                            kb.MM(pss[:, h2, :], kr[b][64 * h2:64 * h2 + 64, hp, :], qr[b][64 * h2:64 * h2 + 64, hp, :],
                                  True, True, [kr[b].s(hp), qr[b].s(hp)], [pss])
                        kb.TT(AT[j][:], pss[:], MK[d][:, 2 * hp:2 * hp + 2, :], ALU.mult, [pss, MK[d]], [AT[j]])
                        kb.TT(qd[j][:], qr[b][:, hp, :], QDEC[d][hp][:], ALU.mult, [qr[b].s(hp), QDEC[d][hp]], [qd[j]],
                              eng="pool")
                        po = ps_o[j]
                        kb.MM(po[:], Vp[b][2 * hp][:], AT[j][:, 0, :], True, False, [Vp[b][2 * hp], AT[j]], [po])
                        kb.MM(po[:], Vp[b][2 * hp + 1][:], AT[j][:, 1, :], False, False, [Vp[b][2 * hp + 1], AT[j]], [po])
                        kb.MM(po[:], Sb[hp][:], qd[j][:], False, True, [Sb[hp], qd[j]], [po])
                        oacc_write(kb, OACC, hp, n, po, d)
                        kb.TR(ps_k[:], kr[b][:, hp, :], kb.C("IDENT"), [kr[b].s(hp)], [ps_k])
                        for h2 in range(2):
                            h = 2 * hp + h2
                            kb.ACT(khp[b][h][:, 64 * h2:64 * h2 + 64], ps_k[:, 64 * h2:64 * h2 + 64], AF.Copy,
                                   [ps_k, KD], [khp[b][h]], scale=KD[:, 4 * d + h:4 * d + h + 1])
                        kb.MM(ps_kv[:], khp[b][2 * hp][:], Vp[b][2 * hp][:], True, False,
                              [khp[b][2 * hp], Vp[b][2 * hp]], [ps_kv])
                        kb.MM(ps_kv[:], khp[b][2 * hp + 1][:], Vp[b][2 * hp + 1][:], False, True,
                              [khp[b][2 * hp + 1], Vp[b][2 * hp + 1]], [ps_kv])
                        kb.STT(Sb[hp][:], Sb[hp][:], CV[:, 2 * d + hp:2 * d + hp + 1], ps_kv[:], ALU.mult, ALU.add,
                               [Sb[hp], CV, ps_kv], [Sb[hp]])
        with P.scope():
            finalize_gated(kb, OACC, Z_RG, None, 256, "r_")


def mixer_hgrn(kb, l):
    P = kb.P
    with P.scope():
        OACC = P.sbuf("h_oacc", [128, 2, S])
        with P.scope():
            LB = P.sbuf("h_LB", [128, 4]); OML = P.sbuf("h_OML", [128, 4])
            if l == 0:
                kb.MS(LB[:], 0.0, [LB]); kb.MS(OML[:], 1.0, [OML])
            else:
                lgt = P.sbuf("h_lgt", [128, 8])
                kb.LD(lgt[:], kb.prm["hgrn_lb_logits"][:].rearrange("l d (hp p) -> p (l d hp)", p=128), [lgt],
                      allow_slow_non_contiguous=True)
                kb.TT(LB[:], lgt[:, 4:8], lgt[:, 0:4], ALU.subtract, [lgt], [LB])
                kb.ACT(LB[:], LB[:], AF.Sigmoid, [LB], [LB])
                kb.TS(OML[:], LB[:], -1.0, 1.0, ALU.mult, ALU.add, [LB], [OML])
            G = P.sbuf("h_G", [128, 1])
            for hh in range(2):
                kb.LD(G[64 * hh:64 * hh + 64, :], kb.prm["hgrn_norm_g"][l].rearrange("(p o) -> p o", o=1), [G])
            kb.hgrn_gain = G
            hqb = [P.sbuf("h_q%d" % i, [128, 2, 128]) for i in range(2)]
            hfb = [P.sbuf("h_f%d" % i, [128, 2, 128]) for i in range(2)]
            Vp = [[P.sbuf("h_vp%d%d" % (i, h), [128, 128]) for h in range(4)] for i in range(2)]
            khp = [[P.sbuf("h_kh%d%d" % (i, h), [128, 128]) for h in range(4)] for i in range(2)]
            for i in range(2):
                for h in range(4):
                    kb.MS(Vp[i][h][:], 0.0, [Vp[i][h]], eng="pool")
                    kb.MS(khp[i][h][:], 0.0, [khp[i][h]], eng="pool")
            MREF = [[P.sbuf("h_mr%d%d" % (d, i), [128, 4]) for i in range(2)] for d in range(2)]
            for d in range(2):
                for i in range(2):
                    kb.MS(MREF[d][i][:], 0.0, [MREF[d][i]])

            def two(name, shape=(128, 128)):
                return [P.sbuf("h_%s%d" % (name, i), list(shape)) for i in range(2)]
            qs, sgm, ff, logf, kk, bb, pre = two("qs"), two("sg"), two("ff"), two("lf"), two("kk"), two("bb"), two("pre")
            e1, Ql, e2, Qd = two("e1"), two("Ql"), two("e2"), two("Qd")
            Kt = [two("Kt%d" % r) for r in range(4)]
            ex = two("ex")
            AT = two("AT", (128, 2, 128))
            KhT = two("KhT")
            bend = two("bend", (128, 2))
            Sb = [P.sbuf("h_S%d" % hp, [128, 128]) for hp in range(2)]
            ps_s = [P.psum("h_pss%d" % i, [128, 2, 128]) for i in range(2)]
            ps_o = [P.psum("h_pso%d" % i, [128, 128]) for i in range(2)]
            ps_k = [P.psum("h_psk%d" % i, [128, 128]) for i in range(2)]
            ps_kv = [P.psum("h_pskv%d" % i, [128, 128]) for i in range(2)]
            it = 0
            jj = 0
            for d in range(2):
                zf = Z_HFF if d == 0 else Z_HFB
                for hp in range(2):
                    kb.MS(Sb[hp][:], 0.0, [Sb[hp]])
                for n in ORDER[d]:
                    cols = slice(n * 128, (n + 1) * 128)
                    b = it % 2; it += 1
                    hq, hf = hqb[b], hfb[b]
                    kb.LD(hq[:], kb.ZF[Z_HQ:Z_HQ + 256, cols].rearrange("(hp p) t -> p hp t", p=128), [hq])
                    kb.LD(hf[:], kb.ZF[zf:zf + 256, cols].rearrange("(hp p) t -> p hp t", p=128), [hf])
                    for h in range(4):
                        kb.LD(Vp[b][h][:, 64 * (h % 2):64 * (h % 2) + 64], kb.ZT[cols, 64 * h:64 * h + 64], [Vp[b][h]])
                    for hp in range(2):
                        j = jj % 2; jj += 1
                        c = 2 * d + hp
                        mref = MREF[d][j]
                        kb.ACT(qs[j][:], hq[:, hp, :], AF.Silu, [hq], [qs[j]])
                        kb.ACT(sgm[j][:], hf[:, hp, :], AF.Sigmoid, [hf], [sgm[j]])
                        kb.TS(ff[j][:], sgm[j][:], OML[:, c:c + 1], LB[:, c:c + 1], ALU.mult, ALU.add, [sgm[j], OML, LB], [ff[j]])
                        kb.ACT(logf[j][:], ff[j][:], AF.Ln, [ff[j]], [logf[j]])
                        kb.TS(kk[j][:], ff[j][:], -1.0, 1.0, ALU.mult, ALU.add, [ff[j]], [kk[j]], eng="pool")
                        B = bb[j]
                        if d == 0:
                            kb.SCAN(B[:], kb.C("ONES"), logf[j][:], [logf[j]], [B])
                            kb.CP(mref[:, 1:4], B[:].rearrange("p (r c) -> p r c", c=32)[:, 0:3, 31], [B], [mref])
                            be = B[:, 127:128]
                        else:
                            kb.SCAN(pre[j][:], kb.C("ONES"), logf[j][:], [logf[j]], [pre[j]])
                            kb.STT(B[:], pre[j][:], -1.0, logf[j][:], ALU.mult, ALU.add, [pre[j], logf[j]], [B])
                            kb.TS(B[:], B[:], pre[j][:, 127:128], None, ALU.add, None, [B, pre[j]], [B])
                            kb.CP(mref[:, 0:3], B[:].rearrange("p (r c) -> p r c", c=32)[:, 1:4, 0], [B], [mref])
                            be = B[:, 0:1]
                        kb.TT(e1[j][:].rearrange("p (r c) -> p r c", c=32), B[:].rearrange("p (r c) -> p r c", c=32),
                              mref[:].unsqueeze(2).to_broadcast([128, 4, 32]), ALU.subtract, [B, mref], [e1[j]])
                        kb.ACT(e1[j][:], e1[j][:], AF.Exp, [e1[j]], [e1[j]])
                        kb.STT(Ql[j][:], qs[j][:], 0.125, e1[j][:], ALU.mult, ALU.mult, [qs[j], e1[j]], [Ql[j]], eng="pool")
                        kb.ACT(e2[j][:], B[:], AF.Exp, [B], [e2[j]])
                        kb.STT(Qd[j][:], qs[j][:], 0.125, e2[j][:], ALU.mult, ALU.mult, [qs[j], e2[j]], [Qd[j]], eng="pool")
                        pss = ps_s[j]
                        for r in range(4):
                            kb.ACT(ex[j][:], B[:], AF.Exp, [B, mref], [ex[j]], scale=-1.0, bias=mref[:, r:r + 1])
                            kb.STT(Kt[r][j][:], ex[j][:], 1e26, kk[j][:], ALU.min, ALU.mult, [ex[j], kk[j]], [Kt[r][j]])
                            for h2 in range(2):
                                kb.MM(pss[:, h2, 32 * r:32 * r + 32], Kt[r][j][64 * h2:64 * h2 + 64, :],
                                      Ql[j][64 * h2:64 * h2 + 64, 32 * r:32 * r + 32], True, True,
                                      [Kt[r][j], Ql[j]], [pss])
                        kb.TT(AT[j][:], pss[:], kb.C("TRIF" if d == 0 else "TRIB").unsqueeze(1).to_broadcast([128, 2, 128]),
                              ALU.mult, [pss], [AT[j]])
                        po = ps_o[j]
                        kb.MM(po[:], Vp[b][2 * hp][:], AT[j][:, 0, :], True, False, [Vp[b][2 * hp], AT[j]], [po])
                        kb.MM(po[:], Vp[b][2 * hp + 1][:], AT[j][:, 1, :], False, False, [Vp[b][2 * hp + 1], AT[j]], [po])
                        kb.MM(po[:], Sb[hp][:], Qd[j][:], False, True, [Sb[hp], Qd[j]], [po])
                        oacc_write(kb, OACC, hp, n, po, d)
                        kb.CP(bend[j][:, 0:1], be, [B], [bend[j]])
                        kb.ACT(KhT[j][:], B[:], AF.Exp, [B, bend[j]], [KhT[j]], scale=-1.0, bias=bend[j][:, 0:1])
                        kb.TT(KhT[j][:], KhT[j][:], kk[j][:], ALU.mult, [KhT[j], kk[j]], [KhT[j]], eng="pool")
                        kb.ACT(bend[j][:, 1:2], bend[j][:, 0:1], AF.Exp, [bend[j]], [bend[j]])
                        pk = ps_k[j]
                        kb.TR(pk[:], KhT[j][:], kb.C("IDENT"), [KhT[j]], [pk])
                        for h2 in range(2):
                            h = 2 * hp + h2
                            kb.CP(khp[b][h][:, 64 * h2:64 * h2 + 64], pk[:, 64 * h2:64 * h2 + 64], [pk], [khp[b][h]],
                                  eng=("act" if h2 else "dve"))
                        pkv = ps_kv[j]
                        kb.MM(pkv[:], khp[b][2 * hp][:], Vp[b][2 * hp][:], True, False, [khp[b][2 * hp], Vp[b][2 * hp]], [pkv])
                        kb.MM(pkv[:], khp[b][2 * hp + 1][:], Vp[b][2 * hp + 1][:], False, True,
                              [khp[b][2 * hp + 1], Vp[b][2 * hp + 1]], [pkv])
                        kb.STT(Sb[hp][:], Sb[hp][:], bend[j][:, 1:2], pkv[:], ALU.mult, ALU.add,
                               [Sb[hp], bend[j], pkv], [Sb[hp]])
        with P.scope():
            G = P.sbuf("h_G2", [128, 1])
            for hh in range(2):
                kb.LD(G[64 * hh:64 * hh + 64, :], kb.prm["hgrn_norm_g"][l].rearrange("(p o) -> p o", o=1), [G])
            finalize_gated(kb, OACC, Z_HG, G, 0, "h_")


PI = float(np.pi)


def _sincos(kb, ang, sin_out, cos_out, R, tmp, shape=None):
    P = kb.P
    shp = list(ang.shape)
    with P.scope():
        ki = P.sbuf("sc_ki", shp, mybir.dt.int32)
        kf = P.sbuf("sc_kf", shp)
        r = P.sbuf("sc_r", shp)
        m = P.sbuf("sc_m", shp)
        C1 = 6.28125
        C2 = 2 * PI - C1
        for (shift, out) in ((0.0, sin_out), (PI / 2, cos_out)):
            kb.TS(r[:], ang, shift, None, ALU.add, None, R, [r])
            kb.TS(kf[:], r[:], 1.0 / (2 * PI), None, ALU.mult, None, [r], [kf])
            kb.CP(ki[:], kf[:], [kf], [ki])
            kb.CP(kf[:], ki[:], [ki], [kf])
            kb.STT(r[:], kf[:], -C1, r[:], ALU.mult, ALU.add, [kf, r], [r])
            kb.STT(r[:], kf[:], -C2, r[:], ALU.mult, ALU.add, [kf, r], [r])
            kb.TS(m[:], r[:], PI, 2 * PI, ALU.is_gt, ALU.mult, [r], [m])
            kb.TT(r[:], r[:], m[:], ALU.subtract, [r, m], [r])
            kb.TS(m[:], r[:], -PI, 2 * PI, ALU.is_lt, ALU.mult, [r], [m])
            kb.TT(r[:], r[:], m[:], ALU.add, [r, m], [r])
            kb.ACT(out, r[:], AF.Sin, [r], R)


def mixer_s5(kb, l):
    P = kb.P
    prm = kb.prm
    with P.scope():
        OACC = P.sbuf("s_oacc", [128, 2, S])
        with P.scope():
            WX = P.sbuf("s_WX", [128, 2, 8, 2, 64])
            Cblk = P.sbuf("s_Cblk", [128, 16, 128])
            kb.MS(WX[:], 0.0, [WX], eng="pool")
            kb.MS(Cblk[:], 0.0, [Cblk], eng="pool")
            for g8 in range(8):
                for ri, nm in enumerate(("s5_b_re", "s5_b_im")):
                    for gg in range(2):
                        src = prm[nm][l][8 * gg + g8].rearrange("p c -> c p")
                        kb.LD(WX[16 * g8:16 * g8 + 16, gg, g8, ri, :], src, [WX], allow_slow_non_contiguous=True)
            for g in range(16):
                g8 = g % 8
                kb.LD(Cblk[0:64, g, 16 * g8:16 * g8 + 16], prm["s5_c_re"][l][g].rearrange("c p -> p c"), [Cblk],
                      allow_slow_non_contiguous=True)
                kb.LD(Cblk[64:128, g, 16 * g8:16 * g8 + 16], prm["s5_c_im"][l][g].rearrange("c p -> p c"), [Cblk],
                      allow_slow_non_contiguous=True)
            kb.TS(Cblk[64:128, :, :], Cblk[64:128, :, :], -1.0, None, ALU.mult, None, [Cblk], [Cblk])
            VFr = P.sbuf("s_VFr", [128, 16, 64]); VFi = P.sbuf("s_VFi", [128, 16, 64])
            T1 = P.sbuf("s_T1", [128, 16, 128]); T2 = P.sbuf("s_T2", [128, 16, 128])
            AR = P.sbuf("s_AR", [128, 16]); NAI = P.sbuf("s_NAI", [128, 16])
            for d in range(2):
                with P.scope():
                    lr = P.sbuf("s_lr", [128, 16, 64]); li = P.sbuf("s_li", [128, 16, 64]); dtb = P.sbuf("s_dt", [128, 16])
                    kb.LD(lr[:], prm["s5_lam_re"][l][d].rearrange("g p -> (g p)").partition_broadcast(128), [lr])
                    kb.LD(li[:], prm["s5_lam_im"][l][d].rearrange("g p -> (g p)").partition_broadcast(128), [li])
                    kb.LD(dtb[:], prm["s5_log_dt"][l][d].partition_broadcast(128), [dtb])
                    kb.ACT(dtb[:], dtb[:], AF.Exp, [dtb], [dtb])
                    dt_bc = dtb[:].unsqueeze(2).to_broadcast([128, 16, 64])
                    lrdt = P.sbuf("s_lrdt", [128, 16, 64]); lidt = P.sbuf("s_lidt", [128, 16, 64])
                    kb.TT(lrdt[:], lr[:], dt_bc, ALU.mult, [lr, dtb], [lrdt])
                    kb.TT(lidt[:], li[:], dt_bc, ALU.mult, [li, dtb], [lidt])
                    a = [P.sbuf("s_a%d" % i, [128, 16, 64]) for i in range(8)]
                    mag, ang, sn, cs, tmp, ar, ai, t2 = a
                    kb.ACT(mag[:], lrdt[:], AF.Exp, [lrdt], [mag])
                    _sincos(kb, lidt[:], sn[:], cs[:], [lidt, sn, cs, tmp], tmp[:])
                    kb.TT(ar[:], mag[:], cs[:], ALU.mult, [mag, cs], [ar])
                    kb.TT(ai[:], mag[:], sn[:], ALU.mult, [mag, sn], [ai])
                    den = P.sbuf("s_den", [128, 16, 64]); fr = P.sbuf("s_fr", [128, 16, 64]); fi = P.sbuf("s_fi", [128, 16, 64])
                    kb.TT(den[:], lr[:], lr[:], ALU.mult, [lr], [den])
                    kb.TT(t2[:], li[:], li[:], ALU.mult, [li], [t2])
                    kb.TT(den[:], den[:], t2[:], ALU.add, [den, t2], [den])
                    kb.RECIP(den[:], den[:], [den], [den])
                    kb.TS(ar[:], ar[:], -1.0, None, ALU.add, None, [ar], [ar])
                    kb.TT(fr[:], ar[:], lr[:], ALU.mult, [ar, lr], [fr])
                    kb.TT(t2[:], ai[:], li[:], ALU.mult, [ai, li], [t2])
                    kb.TT(fr[:], fr[:], t2[:], ALU.add, [fr, t2], [fr])
                    kb.TT(fr[:], fr[:], den[:], ALU.mult, [fr, den], [fr])
                    kb.TT(fi[:], ai[:], lr[:], ALU.mult, [ai, lr], [fi])
                    kb.TT(t2[:], ar[:], li[:], ALU.mult, [ar, li], [t2])
                    kb.TT(fi[:], fi[:], t2[:], ALU.subtract, [fi, t2], [fi])
                    kb.TT(fi[:], fi[:], den[:], ALU.mult, [fi, den], [fi])
                    jcol = kb.C("CCOL")[:, 2:3] if d == 0 else kb.C("CCOL")[:, 3:4]
                    njcol = kb.C("CCOL")[:, 6:7] if d == 0 else kb.C("CCOL")[:, 7:8]
                    kb.ACT(mag[:], lrdt[:], AF.Exp, [lrdt], [mag], scale=njcol)
                    kb.TS(ang[:], lidt[:], jcol, None, ALU.mult, None, [lidt], [ang])
                    _sincos(kb, ang[:], sn[:], cs[:], [ang, sn, cs, tmp], tmp[:])
                    vr, vi = ar, ai
                    kb.TT(vr[:], mag[:], cs[:], ALU.mult, [mag, cs], [vr])
                    kb.TT(vi[:], mag[:], sn[:], ALU.mult, [mag, sn], [vi])
                    kb.TS(vi[:], vi[:], -1.0, None, ALU.mult, None, [vi], [vi])
                    kb.TT(VFr[:], vr[:], fr[:], ALU.mult, [vr, fr], [VFr])
                    kb.TT(t2[:], vi[:], fi[:], ALU.mult, [vi, fi], [t2])
                    kb.TT(VFr[:], VFr[:], t2[:], ALU.subtract, [VFr, t2], [VFr])
                    kb.TT(VFi[:], vr[:], fi[:], ALU.mult, [vr, fi], [VFi])
                    kb.TT(t2[:], vi[:], fr[:], ALU.mult, [vi, fr], [t2])
                    kb.TT(VFi[:], VFi[:], t2[:], ALU.add, [VFi, t2], [VFi])
                with P.scope():
                    dtb = P.sbuf("s_dt2", [128, 16])
                    kb.LD(dtb[:], prm["s5_log_dt"][l][d].partition_broadcast(128), [dtb])
                    kb.ACT(dtb[:], dtb[:], AF.Exp, [dtb], [dtb])
                    lrp = P.sbuf("s_lrp", [128, 16]); lip = P.sbuf("s_lip", [128, 16])
                    for hh in range(2):
                        kb.LD(lrp[64 * hh:64 * hh + 64, :], prm["s5_lam_re"][l][d].rearrange("g p -> p g"), [lrp],
                              allow_slow_non_contiguous=True)
                        kb.LD(lip[64 * hh:64 * hh + 64, :], prm["s5_lam_im"][l][d].rearrange("g p -> p g"), [lip],
                              allow_slow_non_contiguous=True)
                    kb.TT(lrp[:], lrp[:], dtb[:], ALU.mult, [lrp, dtb], [lrp])
                    kb.TT(lip[:], lip[:], dtb[:], ALU.mult, [lip, dtb], [lip])
                    b4 = [P.sbuf("s_b%d" % i, [128, 16, 128]) for i in range(4)]
                    arg, sn2, cs2, tmp2 = b4
                    mt = kb.C("IOTAF" if d == 0 else "R127F")
                    mt_bc = mt.unsqueeze(1).to_broadcast([128, 16, 128])
                    kb.TT(arg[:], lrp[:].unsqueeze(2).to_broadcast([128, 16, 128]), mt_bc, ALU.mult, [lrp], [arg])
                    kb.ACT(T1[:], arg[:], AF.Exp, [arg], [T1])
                    kb.TT(arg[:], lip[:].unsqueeze(2).to_broadcast([128, 16, 128]), mt_bc, ALU.mult, [lip, T1], [arg])
                    _sincos(kb, arg[:], sn2[:], cs2[:], [arg, sn2, cs2, tmp2], tmp2[:])
                    kb.TT(T2[:], T1[:], sn2[:], ALU.mult, [T1, sn2], [T2])
                    kb.TS(T2[:], T2[:], -1.0, None, ALU.mult, None, [T2], [T2])
                    kb.TT(T1[:], T1[:], cs2[:], ALU.mult, [T1, cs2], [T1])
                    c4 = [P.sbuf("s_c%d" % i, [128, 16]) for i in range(4)]
                    kb.ACT(c4[0][:], lrp[:], AF.Exp, [lrp], [c4[0]])
                    _sincos(kb, lip[:], c4[1][:], c4[2][:], [lip, c4[1], c4[2], c4[3]], c4[3][:])
                    kb.TT(AR[:], c4[0][:], c4[2][:], ALU.mult, [c4[0], c4[2]], [AR])
                    kb.TT(NAI[:], c4[0][:], c4[1][:], ALU.mult, [c4[0], c4[1]], [NAI])
                    kb.TS(NAI[:], NAI[:], -1.0, None, ALU.mult, None, [NAI], [NAI])
                sweep_scope = P.scope(); sweep_scope.__enter__()
                uTb = [P.sbuf("s_u%d" % i, [128, 2, 128]) for i in range(2)]
                mm_ = [P.sbuf("s_m%d" % i, [128, 8, 64]) for i in range(4)]
                W3 = [P.sbuf("s_W3%d" % i, [128, 8, 3, 64]) for i in range(2)]
                tP = [P.sbuf("s_tP%d" % i, [128, 8, 128]) for i in range(2)]
                tPs = [P.sbuf("s_tPs%d" % i, [128, 8, 128]) for i in range(2)]
                H1 = [P.sbuf("s_H1%d" % i, [128, 8, 128]) for i in range(2)]
                H2 = [P.sbuf("s_H2%d" % i, [128, 8, 128]) for i in range(2)]
                hend = P.sbuf("s_hend", [128, 16]); hsend = P.sbuf("s_hsend", [128, 16])
                hp_ = P.sbuf("s_hp", [128, 16]); hps_ = P.sbuf("s_hps", [128, 16])
                sm = [P.sbuf("s_sm%d" % i, [128, 16]) for i in range(4)]
                xps = P.psum("s_xps", [128, 1024])
                pps = P.psum("s_pps", [128, 8, 128])
                ppss = P.psum("s_ppss", [128, 8, 128])
                yps = [P.psum("s_yps%d" % i, [128, 128]) for i in range(2)]
                kb.MS(hp_[:], 0.0, [hp_]); kb.MS(hps_[:], 0.0, [hps_])
                te = 127 if d == 0 else 0
                tri = kb.C("TRIF" if d == 0 else "TRIB")
                it = 0
                for n in ORDER[d]:
                    cols = slice(n * 128, (n + 1) * 128)
                    uT = uTb[it % 2]; it += 1
                    kb.LD(uT[:], kb.ZF[Z_SU:Z_SU + 256, cols].rearrange("(gg p) t -> p gg t", p=128), [uT])
                    for gg in range(2):
                        j = gg
                        for half in range(2):
                            kb.MM(xps[:, half * 512:(half + 1) * 512], uT[:, gg, :],
                                  WX[:, gg, half * 4:(half + 1) * 4, :, :].rearrange("q a r p -> q (a r p)"),
                                  True, True, [uT, WX], [xps])
                        xv = xps[:].rearrange("t (g r p) -> t g r p", r=2, p=64)
                        gs = slice(gg * 8, gg * 8 + 8)
                        kb.TT(mm_[0][:], xv[:, :, 0, :], VFr[:, gs, :], ALU.mult, [xps, VFr], [mm_[0]])
                        kb.TT(mm_[1][:], xv[:, :, 1, :], VFi[:, gs, :], ALU.mult, [xps, VFi], [mm_[1]])
                        kb.TT(mm_[2][:], xv[:, :, 0, :], VFi[:, gs, :], ALU.mult, [xps, VFi], [mm_[2]])
                        kb.TT(mm_[3][:], xv[:, :, 1, :], VFr[:, gs, :], ALU.mult, [xps, VFr], [mm_[3]])
                        w3 = W3[j]
                        kb.TT(w3[:, :, 0, :], mm_[0][:], mm_[1][:], ALU.subtract, [mm_[0], mm_[1]], [w3], eng="pool")
                        kb.TT(w3[:, :, 1, :], mm_[2][:], mm_[3][:], ALU.add, [mm_[2], mm_[3]], [w3], eng="pool")
                        kb.TT(w3[:, :, 2, :], mm_[1][:], mm_[0][:], ALU.subtract, [mm_[0], mm_[1]], [w3], eng="pool")
                        for g8 in range(8):
                            kb.MM(pps[:, g8, :], w3[:, g8, 0:2, :].rearrange("q r p -> q (r p)"), tri, True, True, [w3], [pps])
                            kb.MM(ppss[:, g8, :], w3[:, g8, 1:3, :].rearrange("q r p -> q (r p)"), tri, True, True, [w3], [ppss])
                        kb.TT(tP[j][:], pps[:], hp_[:, gs].unsqueeze(2).to_broadcast([128, 8, 128]), ALU.add, [pps, hp_], [tP[j]])
                        kb.TT(tPs[j][:], ppss[:], hps_[:, gs].unsqueeze(2).to_broadcast([128, 8, 128]), ALU.add,
                              [ppss, hps_], [tPs[j]])
                        kb.TT(H1[j][:], tP[j][:], T1[:, gs, :], ALU.mult, [tP[j], T1], [H1[j]], eng="pool")
                        kb.TT(H2[j][:], tPs[j][:], T2[:, gs, :], ALU.mult, [tPs[j], T2], [H2[j]])
                        kb.TT(H1[j][:], H1[j][:], H2[j][:], ALU.add, [H1[j], H2[j]], [H1[j]], eng="pool")
                        yp = yps[gg]
                        for g8 in range(8):
                            kb.MM(yp[:], Cblk[:, gg * 8 + g8, :], H1[j][:, g8, :], g8 == 0, g8 == 7, [Cblk, H1[j]], [yp])
                        oacc_write(kb, OACC, gg, n, yp, d)
                        kb.CP(hend[:, gs], H1[j][:, :, te], [H1[j]], [hend])
                        kb.TT(sm[0][:, 0:8], tPs[j][:, :, te], T1[:, gs, te], ALU.mult, [tPs[j], T1], [sm[0]])
                        kb.TT(sm[1][:, 0:8], tP[j][:, :, te], T2[:, gs, te], ALU.mult, [tP[j], T2], [sm[1]])
                        kb.TT(hsend[:, gs], sm[0][:, 0:8], sm[1][:, 0:8], ALU.subtract, [sm[0], sm[1]], [hsend])
                    kb.TT(sm[0][:], hend[:], AR[:], ALU.mult, [hend, AR], [sm[0]])
                    kb.TT(sm[1][:], hsend[:], NAI[:], ALU.mult, [hsend, NAI], [sm[1]])
                    kb.TT(sm[2][:], hsend[:], AR[:], ALU.mult, [hsend, AR], [sm[2]])
                    kb.TT(sm[3][:], hend[:], NAI[:], ALU.mult, [hend, NAI], [sm[3]])
                    kb.TT(hp_[:], sm[0][:], sm[1][:], ALU.add, [sm[0], sm[1]], [hp_])
                    kb.TT(hps_[:], sm[2][:], sm[3][:], ALU.subtract, [sm[2], sm[3]], [hps_])
                sweep_scope.__exit__(None, None, None)
        with P.scope():
            dsk = P.sbuf("s_dsk", [128, 2]); glb = P.sbuf("s_glb", [128, 2])
            kb.LD(dsk[:], prm["s5_d"][l].rearrange("(gg p) -> p gg", p=128), [dsk], allow_slow_non_contiguous=True)
            kb.LD(glb[:], prm["s5_glu_b"][l].rearrange("(gg p) -> p gg", p=128), [glb], allow_slow_non_contiguous=True)
            gw = P.sbuf("s_gw", [128, 2, 256])
            kb.LD(gw[:], prm["s5_glu_w"][l].rearrange("(ct p) o -> p ct o", p=128), [gw])
            uTb = [P.sbuf("s_fu%d" % i, [128, 2, 128]) for i in range(2)]
            yy = [P.sbuf("s_yy%d" % i, [128, 2, 128]) for i in range(2)]
            x2 = [P.sbuf("s_x2%d" % i, [128, 2, 128]) for i in range(2)]
            th = [P.sbuf("s_th%d" % i, [128, 2, 128]) for i in range(2)]
            sgb = [P.sbuf("s_sg%d" % i, [128, 128]) for i in range(2)]
            ob = [P.sbuf("s_ob%d" % i, [128, 128]) for i in range(2)]
            psz = [P.psum("s_psz%d" % i, [128, 128]) for i in range(2)]
            k = 0
            for n in range(NT):
                cols = slice(n * 128, (n + 1) * 128)
                i = n % 2
                kb.LD(uTb[i][:], kb.ZF[Z_SU:Z_SU + 256, cols].rearrange("(gg p) t -> p gg t", p=128), [uTb[i]])
                for gg in range(2):
                    kb.STT(yy[i][:, gg, :], uTb[i][:, gg, :], dsk[:, gg:gg + 1], OACC[:, gg, cols], ALU.mult, ALU.add,
                           [uTb[i], dsk, OACC.s(n)], [yy[i]])
                kb.TT(x2[i][:], yy[i][:], yy[i][:], ALU.mult, [yy[i]], [x2[i]], eng="pool")
                kb.TS(x2[i][:], x2[i][:], 0.044715, 1.0, ALU.mult, ALU.add, [x2[i]], [x2[i]])
                kb.TT(x2[i][:], x2[i][:], yy[i][:], ALU.mult, [x2[i], yy[i]], [x2[i]], eng="pool")
                kb.ACT(th[i][:], x2[i][:], AF.Tanh, [x2[i]], [th[i]], scale=0.7978845608028654)
                kb.TS(th[i][:], th[i][:], 1.0, 0.5, ALU.add, ALU.mult, [th[i]], [th[i]])
                kb.TT(yy[i][:], yy[i][:], th[i][:], ALU.mult, [yy[i], th[i]], [yy[i]], eng="pool")
                for ot in range(2):
                    q = k % 2; k += 1
                    for ct in range(2):
                        kb.MM(psz[q][:], gw[:, ct, ot * 128:(ot + 1) * 128], yy[i][:, ct, :], ct == 0, ct == 1, [gw, yy[i]], [psz[q]])
                    kb.ACT(sgb[q][:], psz[q][:], AF.Sigmoid, [psz[q], glb], [sgb[q]], bias=glb[:, ot:ot + 1])
                    kb.TT(ob[q][:], yy[i][:, ot, :], sgb[q][:], ALU.mult, [yy[i], sgb[q]], [ob[q]])
                    kb.ST(kb.YC[768 + ot * 128:768 + (ot + 1) * 128, cols], ob[q][:], [ob[q]])


def gdn_conv(kb, l):
    P = kb.P
    with P.scope():
        CW = P.sbuf("g_cw", [128, 6, 9])
        for kh in range(3):
            for kw in range(3):
                kb.LD(CW[:, :, kh * 3 + kw], kb.prm["gdn_conv_w"][l][kh, kw].rearrange("(ct p) -> p ct", p=128), [CW],
                      allow_slow_non_contiguous=True)
        mlat = P.sbuf("g_mlat", [128, 2, 512]); mctx = P.sbuf("g_mctx", [128, 2, 256])
        kb.LD(mlat[:], kb.cmlat[:], [mlat]); kb.LD(mctx[:], kb.cmctx[:], [mctx])
        Wb = [P.sbuf("g_w%d" % i, [128, 642]) for i in range(2)]
        acc = [[P.sbuf("g_acc%d%d" % (i, j), [128, 512]) for j in range(3)] for i in range(2)]
        sl = [P.sbuf("g_sl%d" % i, [128, 512]) for i in range(2)]
        sq = [P.sbuf("g_sq%d" % i, [128, 512]) for i in range(2)]
        rt = [P.sbuf("g_rt%d" % i, [128, 512]) for i in range(2)]
        ps = [P.psum("g_psn%d" % i, [128, 512]) for i in range(2)]
        spans = [(0, 256, True)] + [(256 + 512 * k, 512, False) for k in range(8)]
        it = 0
        for (t0, L, is_ctx) in spans:
            lo = 0 if is_ctx else 256
            hi = 256 if is_ctx else S
            a = max(lo, t0 - 65); b = min(hi, t0 + L + 65)
            for ct in range(6):
                i = it % 2; it += 1
                W = Wb[i]
                kb.MS(W[:], 0.0, [W], eng="pool")
                kb.LD(W[:, 65 + (a - t0):65 + (b - t0)], kb.ZF[Z_GQKV + ct * 128:Z_GQKV + (ct + 1) * 128, a:b], [W])
                rows = (1,) if is_ctx else (0, 1, 2)
                masks = mctx if is_ctx else mlat
                for dwi, shift in enumerate((-1, 0, 1)):
                    A = acc[i][dwi]
                    eng = "dve"
                    for q, dh in enumerate(rows):
                        o0 = 65 + 64 * (dh - 1) + shift
                        src = W[:, o0:o0 + L]
                        wcol = CW[:, ct, dh * 3 + dwi:dh * 3 + dwi + 1]
                        if q == 0:
                            kb.TS(A[:, :L], src, wcol, None, ALU.mult, None, [W, CW], [A], eng=("pool" if dwi != 1 else "dve"))
                        else:
                            kb.STT(A[:, :L], src, wcol, A[:, :L], ALU.mult, ALU.add, [W, CW, A], [A])
                    if dwi != 1:
                        mi = 0 if dwi == 0 else 1
                        kb.TT(A[:, :L], A[:, :L], masks[:, mi, :L], ALU.mult, [A, masks], [A], eng="pool")
                A0, A1, A2 = acc[i]
                kb.TT(A1[:, :L], A1[:, :L], A0[:, :L], ALU.add, [A0, A1], [A1], eng="pool")
                kb.TT(A1[:, :L], A1[:, :L], A2[:, :L], ALU.add, [A1, A2], [A1], eng="pool")
                kb.ACT(sl[i][:, :L], A1[:, :L], AF.Silu, [A1], [sl[i]])
                if ct < 4:
                    kb.TT(sq[i][:, :L], sl[i][:, :L], sl[i][:, :L], ALU.mult, [sl[i]], [sq[i]], eng="pool")
                    kb.MM(ps[i][:, :L], kb.C("BLK64"), sq[i][:, :L], True, True, [sq[i]], [ps[i]])
                    kb.ACT(rt[i][:, :L], ps[i][:, :L], AF.Sqrt, [ps[i]], [rt[i]], bias=kb.C("CCOL")[:, 0:1])
                    kb.RECIP(rt[i][:, :L], rt[i][:, :L], [rt[i]], [rt[i]])
                    if ct < 2:
                        kb.STT(sl[i][:, :L], sl[i][:, :L], 0.125, rt[i][:, :L], ALU.mult, ALU.mult, [sl[i], rt[i]], [sl[i]])
                    else:
                        kb.TT(sl[i][:, :L], sl[i][:, :L], rt[i][:, :L], ALU.mult, [sl[i], rt[i]], [sl[i]])
                kb.ST(kb.QKVF[ct * 128:(ct + 1) * 128, t0:t0 + L], sl[i][:, :L], [sl[i]])


def mixer_gdn(kb, l):
    P = kb.P
    gdn_conv(kb, l)
    upto = kb.cfg.get("gdn_upto", 99)
    if upto < 1:
        return
    with P.scope():
        OACC = P.sbuf("g_oacc", [128, 2, S])
        with P.scope():
            DTB = P.sbuf("g_dtb", [128, 8]); NEGA = P.sbuf("g_nega", [128, 8])
            kb.LD(DTB[:], kb.prm["gdn_dt_bias"][l].rearrange("d h -> (d h)").partition_broadcast(128), [DTB])
            kb.LD(NEGA[:], kb.prm["gdn_a_log"][l].rearrange("d h -> (d h)").partition_broadcast(128), [NEGA])
            kb.ACT(NEGA[:], NEGA[:], AF.Exp, [NEGA], [NEGA])
            kb.TS(NEGA[:], NEGA[:], -1.0, None, ALU.mult, None, [NEGA], [NEGA])
            qnb = [P.sbuf("g_q%d" % i, [128, 2, 128]) for i in range(2)]
            knb = [P.sbuf("g_k%d" % i, [128, 2, 128]) for i in range(2)]
            vvb = [P.sbuf("g_v%d" % i, [128, 2, 128]) for i in range(2)]
            gabb = [P.sbuf("g_gab%d" % i, [128, 16]) for i in range(2)]

            def sm4(name, w=4):
                return P.sbuf("g_" + name, [128, w])
            xa, ea, loga, beta, lnb = sm4("xa"), sm4("ea"), sm4("loga"), sm4("beta"), sm4("lnb")
            gtm, ngt, ekr, cdec, eg, beg, gpl = sm4("gtm"), sm4("ngt"), sm4("ekr"), sm4("cdec"), sm4("eg"), sm4("beg"), sm4("gpl")
            ROWS = P.sbuf("g_rows", [4, 384])
            LI = P.sbuf("g_LI", [128, 4, 128]); LBT = P.sbuf("g_LBT", [128, 4, 128]); LBm = P.sbuf("g_LB", [128, 4, 128])
            NAT = P.sbuf("g_NAT", [128, 4, 128]); NA = P.sbuf("g_NA", [128, 4, 128]); QKm = P.sbuf("g_QKm", [128, 4, 128])
            Tm = P.sbuf("g_Tm", [128, 4, 128]); Wm = P.sbuf("g_Wm", [128, 4, 128])
            x1 = P.sbuf("g_x1", [128, 4, 128]); y1 = P.sbuf("g_y1", [128, 4, 128])
            tmx = P.sbuf("g_tmx", [128, 4, 128]); tmy = P.sbuf("g_tmy", [128, 4, 128])
            Rm = [P.sbuf("g_R%d" % h, [128, 128]) for h in range(4)]
            khp = [P.sbuf("g_kh%d" % h, [128, 128]) for h in range(4)]
            vnp = [P.sbuf("g_vn%d" % h, [128, 128]) for h in range(4)]
            for h in range(4):
                kb.MS(khp[h][:], 0.0, [khp[h]], eng="pool")
                kb.MS(vnp[h][:], 0.0, [vnp[h]], eng="pool")
            upair = [P.sbuf("g_up%d" % hp, [128, 128]) for hp in range(2)]
            wTp = [P.sbuf("g_wT%d" % hp, [128, 128]) for hp in range(2)]
            EG = [P.sbuf("g_EG%d" % hp, [128, 128]) for hp in range(2)]
            qd = [P.sbuf("g_qd%d" % hp, [128, 128]) for hp in range(2)]
            cdp = [P.sbuf("g_cdp%d" % hp, [128, 1]) for hp in range(2)]
            Sb = [P.sbuf("g_S%d" % hp, [128, 128]) for hp in range(2)]
            B = [P.psum("g_B%d" % i, [128, 512]) for i in range(8)]
            ident = kb.C("IDENT")
            it = 0
            for d in range(2):
                tri = kb.C("TRIF" if d == 0 else "TRIB")
                rem = kb.C("SUFF" if d == 0 else "PREB")
                n_incl = kb.C("NLE" if d == 0 else "NGE")
                n_strT = kb.C("NLT" if d == 0 else "NGT")
                n_str = kb.C("NGT" if d == 0 else "NLT")
                for hp in range(2):
                    kb.MS(Sb[hp][:], 0.0, [Sb[hp]])
                for n in ORDER[d][:kb.cfg.get("ntiles", NT)]:
                    cols = slice(n * 128, (n + 1) * 128)
                    b = it % 2; it += 1
                    qn, kn, vv, gab = qnb[b], knb[b], vvb[b], gabb[b]
                    kb.LD(qn[:], kb.QKVF[0:256, cols].rearrange("(hp p) t -> p hp t", p=128), [qn])
                    kb.LD(kn[:], kb.QKVF[256:512, cols].rearrange("(hp p) t -> p hp t", p=128), [kn])
                    kb.LD(vv[:], kb.QKVF[512:768, cols].rearrange("(hp p) t -> p hp t", p=128), [vv])
                    kb.LD(gab[:], kb.ZT[cols, 512:528], [gab])
                    kb.TT(xa[:], gab[:, 4 * d:4 * d + 4], DTB[:, 4 * d:4 * d + 4], ALU.add, [gab, DTB], [xa])
                    kb.ACT(ea[:], xa[:], AF.Exp, [xa], [ea])
                    kb.ACT(ea[:], ea[:], AF.Ln, [ea], [ea], bias=kb.C("CCOL")[:, 1:2])
                    kb.TT(loga[:], ea[:], NEGA[:, 4 * d:4 * d + 4], ALU.mult, [ea, NEGA], [loga])
                    kb.ACT(beta[:], gab[:, 8 + 4 * d:12 + 4 * d], AF.Sigmoid, [gab], [beta])
                    kb.ACT(lnb[:], beta[:], AF.Ln, [beta], [lnb])
                    kb.MM(B[0][:, 0:4], tri, loga[:], True, True, [loga], [B[0]])
                    kb.MM(B[0][:, 4:8], rem, loga[:], True, True, [loga], [B[0]])
                    kb.MM(B[0][:, 8:12], kb.C("ONES"), loga[:], True, True, [loga], [B[0]])
                    kb.CP(gtm[:], B[0][:, 0:4], [B[0]], [gtm])
                    kb.TS(ngt[:], B[0][:, 0:4], -1.0, None, ALU.mult, None, [B[0]], [ngt])
                    kb.ACT(ekr[:], B[0][:, 4:8], AF.Exp, [B[0]], [ekr])
                    kb.ACT(cdec[:], B[0][:, 8:12], AF.Exp, [B[0]], [cdec])
                    kb.ACT(eg[:], gtm[:], AF.Exp, [gtm], [eg])
                    kb.TT(beg[:], beta[:], eg[:], ALU.mult, [beta, eg], [beg])
                    kb.TT(gpl[:], gtm[:], lnb[:], ALU.add, [gtm, lnb], [gpl])
                    kb.MM(B[1][0:4, 0:128], loga[:], tri, True, True, [loga], [B[1]])
                    kb.MM(B[1][0:4, 128:256], loga[:], tri, True, False, [loga], [B[1]])
                    kb.MM(B[1][0:4, 128:256], lnb[:], ident, False, True, [lnb], [B[1]])
                    kb.CP(ROWS[:, 0:256], B[1][0:4, 0:256], [B[1]], [ROWS])
                    kb.TS(ROWS[:, 256:384], B[1][0:4, 0:128], -1.0, None, ALU.mult, None, [B[1]], [ROWS])
                    if upto < 2:
                        continue
                    for (dst, rsl, negm, bias_t, bank) in ((LI, slice(0, 128), n_incl, ngt, B[2]),
                                                           (LBT, slice(128, 256), n_strT, ngt, B[3]),
                                                           (LBm, slice(256, 384), n_str, gpl, B[2])):
                        for h in range(4):
                            kb.MM(bank[:, h * 128:(h + 1) * 128], kb.C("SELH%d" % h)[0:4, :], ROWS[:, rsl], True, False,
                                  [ROWS], [bank])
                            kb.MM(bank[:, h * 128:(h + 1) * 128], ident, negm, False, True, [], [bank])
                        for h in range(4):
                            kb.ACT(dst[:, h, :], bank[:, h * 128:(h + 1) * 128], AF.Exp, [bank, bias_t], [dst],
                                   bias=bias_t[:, h:h + 1])
                    if upto < 3:
                        continue
                    for h in range(4):
                        hp, h2 = divmod(h, 2)
                        ksl = kn[64 * h2:64 * h2 + 64, hp, :]
                        kb.MM(B[4][:, h * 128:(h + 1) * 128], ksl, ksl, True, True, [kn], [B[4]])
                        kb.MM(B[5][:, h * 128:(h + 1) * 128], ksl, qn[64 * h2:64 * h2 + 64, hp, :], True, True, [kn, qn], [B[5]])
                    b4v = B[4][:].rearrange("p (h t) -> p h t", h=4)
                    b5v = B[5][:].rearrange("p (h t) -> p h t", h=4)
                    kb.STT(NAT[:], b4v, -1.0, LBT[:], ALU.mult, ALU.mult, [B[4], LBT], [NAT])
                    kb.STT(NA[:], b4v, -1.0, LBm[:], ALU.mult, ALU.mult, [B[4], LBm], [NA])
                    kb.TT(QKm[:], b5v, LI[:], ALU.mult, [B[5], LI], [QKm])
                    if upto < 4:
                        continue
                    idb = ident.unsqueeze(1).to_broadcast([128, 4, 128])
                    kb.CP(Tm[:], idb, [], [Tm])
                    kb.CP(Wm[:], idb, [], [Wm], eng="pool")
                    for s_ in (1, 2, 4, 8, 16, 32, 64):
                        mT = kb.C(("MOFF%d" if d == 0 else "MOFFT%d") % s_).unsqueeze(1).to_broadcast([128, 4, 128])
                        mW = kb.C(("MOFFT%d" if d == 0 else "MOFF%d") % s_).unsqueeze(1).to_broadcast([128, 4, 128])
                        for h in range(4):
                            kb.MM(B[2][:, h * 128:(h + 1) * 128], NAT[:, h, :], Tm[:, h, :], True, True, [NAT, Tm], [B[2]])
                        for h in range(4):
                            kb.MM(B[3][:, h * 128:(h + 1) * 128], NA[:, h, :], Wm[:, h, :], True, True, [NA, Wm], [B[3]])
                        kb.CP(x1[:], B[2][:].rearrange("p (h t) -> p h t", h=4), [B[2]], [x1], eng="act")
                        kb.CP(y1[:], B[3][:].rearrange("p (h t) -> p h t", h=4), [B[3]], [y1], eng="dve")
                        for h in range(4):
                            kb.MM(B[4][:, h * 128:(h + 1) * 128], Wm[:, h, :], x1[:, h, :], True, True, [Wm, x1], [B[4]])
                        for h in range(4):
                            kb.MM(B[5][:, h * 128:(h + 1) * 128], Tm[:, h, :], y1[:, h, :], True, True, [Tm, y1], [B[5]])
                        kb.TT(tmx[:], B[4][:].rearrange("p (h t) -> p h t", h=4), mT, ALU.mult, [B[4]], [tmx])
                        kb.TT(tmy[:], B[5][:].rearrange("p (h t) -> p h t", h=4), mW, ALU.mult, [B[5]], [tmy])
                        kb.TT(Tm[:], Tm[:], tmx[:], ALU.add, [Tm, tmx], [Tm], eng="pool")
                        kb.TT(Wm[:], Wm[:], tmy[:], ALU.add, [Wm, tmy], [Wm], eng="pool")
                    if upto < 5:
                        continue
                    for hp in range(2):
                        kb.TR(B[0][:, 128:256], kn[:, hp, :], ident, [kn], [B[0]])
                        kb.TR(B[0][:, 256:384], vv[:, hp, :], ident, [vv], [B[0]])
                        for h2 in range(2):
                            h = 2 * hp + h2
                            kc = slice(64 * h2, 64 * h2 + 64)
                            vc = slice(64 * (1 - h2), 64 * (1 - h2) + 64)
                            kb.TS(Rm[h][:, kc], B[0][:, 128 + 64 * h2:128 + 64 * h2 + 64], beg[:, h:h + 1], None, ALU.mult, None,
                                  [B[0], beg], [Rm[h]])
                            kb.ACT(Rm[h][:, vc], B[0][:, 256 + 64 * h2:256 + 64 * h2 + 64], AF.Copy, [B[0], beta], [Rm[h]],
                                   scale=beta[:, h:h + 1])
                            kb.ACT(khp[h][:, kc], B[0][:, 128 + 64 * h2:128 + 64 * h2 + 64], AF.Copy, [B[0], ekr], [khp[h]],
                                   scale=ekr[:, h:h + 1])
                    if upto < 6:
                        continue
                    for h in range(4):
                        kb.MM(B[2][:, h * 128:(h + 1) * 128], Wm[:, h, :], Rm[h][:], True, True, [Wm, Rm[h]], [B[2]])
                        kb.MM(B[3][:, h * 128:(h + 1) * 128], Rm[h][:], Wm[:, h, :], True, True, [Wm, Rm[h]], [B[3]])
                    for h in range(4):
                        hp, h2 = divmod(h, 2)
                        vc0 = 64 * (1 - h2)
                        kb.CP(upair[hp][:, 64 * h2:64 * h2 + 64], B[2][:, h * 128 + vc0:h * 128 + vc0 + 64], [B[2]], [upair[hp]],
                              eng=("act" if h2 else "dve"))
                        kb.CP(wTp[hp][64 * h2:64 * h2 + 64, :], B[3][64 * h2:64 * h2 + 64, h * 128:(h + 1) * 128], [B[3]], [wTp[hp]],
                              eng=("dve" if h2 else "act"))
                    if upto < 7:
                        continue
                    for hp in range(2):
                        kb.MM(B[1][:, 256:384], kb.C("SELP%d" % hp)[0:4, :], ROWS[:, 0:128], True, True, [ROWS], [B[1]])
                        kb.ACT(EG[hp][:], B[1][:, 256:384], AF.Exp, [B[1]], [EG[hp]])
                        kb.TT(qd[hp][:], qn[:, hp, :], EG[hp][:], ALU.mult, [qn, EG[hp]], [qd[hp]], eng="pool")
                        pws = B[7][:, hp * 128:(hp + 1) * 128]
                        kb.MM(pws, wTp[hp][:], Sb[hp][:], True, True, [wTp[hp], Sb[hp]], [B[7]])
                        for h2 in range(2):
                            h = 2 * hp + h2
                            cs_ = slice(64 * h2, 64 * h2 + 64)
                            kb.TT(vnp[h][:, cs_], upair[hp][:, cs_], B[7][:, hp * 128 + 64 * h2:hp * 128 + 64 * h2 + 64],
                                  ALU.subtract, [upair[hp], B[7]], [vnp[h]])
                        po = B[6][:, hp * 256:hp * 256 + 128]
                        kb.MM(po, Sb[hp][:], qd[hp][:], True, False, [Sb[hp], qd[hp]], [B[6].s(hp)])
                        kb.MM(po, vnp[2 * hp][:], QKm[:, 2 * hp, :], False, False, [vnp[2 * hp], QKm], [B[6].s(hp)])
                        kb.MM(po, vnp[2 * hp + 1][:], QKm[:, 2 * hp + 1, :], False, True, [vnp[2 * hp + 1], QKm], [B[6].s(hp)])
                        cols_ = slice(n * 128, (n + 1) * 128)
                        if d == 0:
                            kb.CP(OACC[:, hp, cols_], po, [B[6].s(hp)], [OACC.s(n)], eng="act")
                        else:
                            kb.TT(OACC[:, hp, cols_], OACC[:, hp, cols_], po, ALU.add, [B[6].s(hp)], [OACC.s(n)])
                        pkv = B[6][:, hp * 256 + 128:hp * 256 + 256]
                        kb.MM(pkv, khp[2 * hp][:], vnp[2 * hp][:], True, False, [khp[2 * hp], vnp[2 * hp]], [B[6].s(2 + hp)])
                        kb.MM(pkv, khp[2 * hp + 1][:], vnp[2 * hp + 1][:], False, True, [khp[2 * hp + 1], vnp[2 * hp + 1]],
                              [B[6].s(2 + hp)])
                        kb.CP(cdp[hp][0:64, :], cdec[0:64, 2 * hp:2 * hp + 1], [cdec], [cdp[hp]])
                        kb.CP(cdp[hp][64:128, :], cdec[64:128, 2 * hp + 1:2 * hp + 2], [cdec], [cdp[hp]])
                        kb.STT(Sb[hp][:], Sb[hp][:], cdp[hp][:, 0:1], pkv, ALU.mult, ALU.add,
                               [Sb[hp], cdp[hp], B[6].s(2 + hp)], [Sb[hp]])
        with P.scope():
            G = P.sbuf("g_G2", [128, 1])
            for hh in range(2):
                kb.LD(G[64 * hh:64 * hh + 64, :], kb.prm["gdn_norm_g"][l].rearrange("(p o) -> p o", o=1), [G])
            finalize_gated(kb, OACC, Z_GG, G, 512, "g_")
```

```python
import numpy as np
import concourse.bass as bass
import concourse.mybir as mybir
from concourse.bass_utils import run_bass_kernel_spmd
from contextlib import ExitStack

F32 = mybir.dt.float32
BF16 = mybir.dt.bfloat16
AF = mybir.ActivationFunctionType
ALU = mybir.AluOpType

ENGS = ("pe", "act", "dve", "pool", "sp")
EPOCH = 16000
N_DMA_SEM = 32


class Buf:
    __slots__ = ("name", "w", "r", "excl", "pe_partial")

    def __init__(self, name="", excl=False):
        self.name = name
        self.w = None
        self.r = []
        self.excl = excl
        self.pe_partial = False


class T:
    def __init__(self, h, name, excl=False):
        self.h = h
        self.name = name
        self.b = Buf(name, excl)
        self.excl = excl
        self.subs = {}

    def __getitem__(self, k):
        return self.h[k]

    def s(self, key):
        if self.excl:
            return self.b
        if key not in self.subs:
            self.subs[key] = Buf("%s.%s" % (self.name, key))
        return self.subs[key]


class Prog:
    def __init__(self, nc):
        self.nc = nc
        self.es = ExitStack()
        self.stack = [self.es]
        self.ops = {e: [] for e in ENGS}
        self.cnt = {e: 0 for e in ENGS}
        self.seen = {e: {} for e in ENGS}
        self.last = {}
        self.dma_k = 0
        self.dma_use = [0] * N_DMA_SEM
        self.dma_sems = [self.es.enter_context(nc.semaphore("dq%d" % i)) for i in range(N_DMA_SEM)]
        self.eng_sems = {}
        self.out_tokens = []
        self.n_ops = 0
        self.uid = 0

    def _nm(self, name):
        self.uid += 1
        return "%s_%d" % (name, self.uid)

    def sbuf(self, name, shape, dt=F32):
        h = self.stack[-1].enter_context(self.nc.sbuf_tensor(self._nm(name), list(shape), dt))
        return T(h, name)

    def psum(self, name, shape, dt=F32):
        n = 1
        for d_ in shape[1:]:
            n *= d_
        nb = (n * 4 + 2047) // 2048
        h = self.stack[-1].enter_context(self.nc.psum_tensor(self._nm(name), [128, nb * 512], F32))
        v = h[0:shape[0], 0:n]
        if len(shape) == 3:
            v = v.rearrange("p (a b) -> p a b", a=shape[1])
        elif len(shape) == 4:
            v = v.rearrange("p (a b c) -> p a b c", a=shape[1], b=shape[2])
        return T(v, name, excl=True)

    def dram(self, name, shape, dt=F32, kind="Internal"):
        h = self.nc.dram_tensor(name, list(shape), dt, kind=kind)
        return T(h.ap(), name)

    class _Scope:
        def __init__(self, p):
            self.p = p

        def __enter__(self):
            st = ExitStack()
            self.p.stack.append(st)
            return st

        def __exit__(self, *a):
            self.p.barrier()
            st = self.p.stack.pop()
            st.close()
            return False

    def scope(self):
        return Prog._Scope(self)

    def _eng_sem(self, e, epoch):
        k = (e, epoch)
        if k not in self.eng_sems:
            self.eng_sems[k] = self.es.enter_context(self.nc.semaphore("s_%s_%d" % (e, epoch)))
        return self.eng_sems[k]

    def _waits(self, eng, reads, writes, extra=(), skip_pe=False):
        need = {}

        def add(tok):
            if tok is None:
                return
            key, val = tok
            if need.get(key, 0) < val:
                need[key] = val
        for b in reads:
            add(b.w)
        for b in writes:
            add(b.w)
            for t in b.r:
                add(t)
        for t in extra:
            add(t)
        out = []
        seen = self.seen[eng]
        for key, val in need.items():
            if skip_pe and key[0] == "e" and key[1] == "pe":
                continue
            if seen.get(key, 0) < val:
                seen[key] = val
                out.append((key, val))
        return out

    @staticmethod
    def _bufs(xs):
        out = []
        for x in xs:
            if x is None:
                continue
            out.append(x.b if isinstance(x, T) else x)
        return out

    def _commit(self, tok, reads, writes):
        self.last[tok[0]] = tok[1]
        for b in reads:
            b.r.append(tok)
            if len(b.r) > 64:
                mx = {}
                for k, v in b.r:
                    if mx.get(k, 0) < v:
                        mx[k] = v
                b.r = list(mx.items())
        for b in writes:
            b.w = tok
            b.r = []
        self.n_ops += 1

    def op(self, eng, fn, reads=(), writes=(), partial=False):
        reads = self._bufs(reads)
        writes = self._bufs(writes)
        ex = [b for b in reads if b.excl]
        if ex:
            reads = [b for b in reads if not b.excl]
            writes = writes + [b for b in ex if b not in writes]
        skip_pe = False
        if eng == "pe":
            skip_pe = (not partial) and all(not b.pe_partial for b in writes)
            for b in writes:
                b.pe_partial = partial
        waits = self._waits(eng, reads, writes, skip_pe=skip_pe)
        self.cnt[eng] += 1
        epoch, val = divmod(self.cnt[eng] - 1, EPOCH)
        tok = (("e", eng, epoch), val + 1)
        self.ops[eng].append((waits, fn, tok))
        self._commit(tok, reads, writes)
        return tok

    def dma(self, out_ap, in_ap, reads=(), writes=(), q="sp", is_output=False, **kw):
        reads = self._bufs(reads)
        writes = self._bufs(writes)
        i = self.dma_k % N_DMA_SEM
        self.dma_k += 1
        prev = self.dma_use[i]
        extra = [(("d", i), 16 * prev)] if prev else []
        waits = self._waits(q, reads, writes, extra)
        self.dma_use[i] = prev + 1
        tok = (("d", i), 16 * (prev + 1))

        def fn(e):
            return e.dma_start(out=out_ap, in_=in_ap, **kw)
        self.ops[q].append((waits, fn, tok))
        self._commit(tok, reads, writes)
        if is_output:
            self.out_tokens.append(tok)
        return tok

    def barrier(self):
        toks = list(self.last.items())
        for e in ENGS:
            waits = self._waits(e, [], [], toks)
            if waits:
                self.ops[e].append((waits, None, None))

    def _sem_of(self, key):
        if key[0] == "d":
            return self.dma_sems[key[1]]
        return self._eng_sem(key[1], key[2])

    def emit(self):
        nc = self.nc
        self.barrier()
        for e in ENGS:
            for waits, fn, tok in self.ops[e]:
                if tok is not None:
                    self._sem_of(tok[0])
                for key, val in waits:
                    self._sem_of(key)
        with nc.Block() as block:
            def run(e, handle):
                for waits, fn, tok in self.ops[e]:
                    for key, val in waits:
                        handle.wait_ge(self._sem_of(key), val)
                    if fn is None:
                        continue
                    ins = fn(handle)
                    key, val = tok
                    ins.then_inc(self._sem_of(key), 16 if key[0] == "d" else 1)

            @block.sync
            def _(h):
                run("sp", h)

            @block.tensor
            def _(h):
                run("pe", h)

            @block.scalar
            def _(h):
                run("act", h)

            @block.vector
            def _(h):
                run("dve", h)

            @block.gpsimd
            def _(h):
                run("pool", h)

    def close(self):
        self.es.close()


D = 1024
S = 4352
NT = 34
LAT0 = 256
DEPTH = 2
EPS = 1e-6
NEG = -30000.0
ORDER = [list(range(NT)), [1, 0] + list(range(NT - 1, 1, -1))]

C_HQ, C_HI, C_HG, C_HFF, C_HFB = 0, 256, 512, 768, 1024
C_RQ, C_RK, C_RV, C_RG = 1280, 1536, 1792, 2048
C_GQKV, C_GG, C_GA, C_GB, C_SU = 2304, 3072, 3328, 3336, 3344
Z_HQ, Z_HG, Z_HFF, Z_HFB, Z_RQ, Z_RK, Z_RG, Z_GQKV, Z_GG, Z_SU = 0, 256, 512, 768, 1024, 1280, 1536, 1792, 2560, 2816
NZF = 3072
FM_MAP = [(Z_HQ, C_HQ, 256), (Z_HG, C_HG, 256), (Z_HFF, C_HFF, 256), (Z_HFB, C_HFB, 256), (Z_RQ, C_RQ, 256),
          (Z_RK, C_RK, 256), (Z_RG, C_RG, 256), (Z_GQKV, C_GQKV, 768), (Z_GG, C_GG, 256), (Z_SU, C_SU, 256)]
FM_BLOCKS = [(zr + i, wc + i) for zr, wc, n in FM_MAP for i in range(0, n, 128)]
NZT = 528

CN = {}


def _const_pack():
    mats = []

    def add(name, m):
        CN[name] = len(mats)
        mats.append(np.asarray(m, np.float32))
    p = np.arange(128)[:, None]
    f = np.arange(128)[None, :]
    add("IDENT", (p == f))
    add("ONES", np.ones((128, 128)))
    add("TRIF", (p <= f))
    add("TRIB", (p >= f))
    add("SUFF", (p > f))
    add("PREB", (p < f))
    add("NLE", np.where(p <= f, 0.0, NEG))
    add("NLT", np.where(p < f, 0.0, NEG))
    add("NGE", np.where(p >= f, 0.0, NEG))
    add("NGT", np.where(p > f, 0.0, NEG))
    for s in (1, 2, 4, 8, 16, 32, 64):
        m = (((p // s) % 2) == 1) & ((f // s) == (p // s) - 1)
        add("MOFF%d" % s, m)
        add("MOFFT%d" % s, m.T)
    add("BLK64", (p // 64) == (f // 64))
    rot = np.zeros((128, 128))
    for m in range(128):
        if (m % 64) < 32:
            rot[m + 32, m] = -1.0
        else:
            rot[m - 32, m] = 1.0
    add("ROT", rot)
    add("IOTAF", np.broadcast_to(f, (128, 128)))
    add("IOTAF1", np.broadcast_to(f + 1, (128, 128)))
    add("RIOTAF", np.broadcast_to(128 - f, (128, 128)))
    add("R127F", np.broadcast_to(127 - f, (128, 128)))
    add("DIFF", f - p)
    add("NDIFF", p - f)
    for h in range(4):
        m = np.zeros((128, 128)); m[h, :] = 1.0
        add("SELH%d" % h, m)
    for hp in range(2):
        m = np.zeros((128, 128)); m[2 * hp, 0:64] = 1.0; m[2 * hp + 1, 64:128] = 1.0
        add("SELP%d" % hp, m)
    cc = np.zeros((128, 128))
    cc[:, 0] = EPS; cc[:, 1] = 1.0; cc[:, 2] = np.arange(128); cc[:, 3] = 127 - np.arange(128)
    cc[:, 5] = -np.pi; cc[:, 6] = -np.arange(128); cc[:, 7] = -(127 - np.arange(128))
    add("CCOL", cc)
    gm = np.zeros((128, 128))
    for g in range(16):
        gm[(g % 8) * 16:(g % 8) * 16 + 16, g] = 1.0
    add("GMASK", gm)
    return np.concatenate(mats, axis=1)


CONST_NP = _const_pack()
NCONST = CONST_NP.shape[1] // 128


def _rope_tables():
    half = 32
    inv = 10000.0 ** (-np.arange(half, dtype=np.float64) / half)
    pos = np.arange(S, dtype=np.float64)
    ang = pos[None, :] * inv[:, None]
    cos = np.cos(ang); sin = np.sin(ang)
    cos128 = np.tile(cos, (4, 1)); sin128 = np.tile(sin, (4, 1))
    return cos128.astype(np.float32), sin128.astype(np.float32)


def _conv_masks():
    m = np.ones((2, 512), np.float32)
    w = np.arange(512) % 64
    m[0, w == 0] = 0.0
    m[1, w == 63] = 0.0
    lat = np.broadcast_to(m[None], (128, 2, 512)).copy()
    c = np.ones((2, 256), np.float32)
    c[0, 0] = 0.0
    c[1, 255] = 0.0
    ctx = np.broadcast_to(c[None], (128, 2, 256)).copy()
    return lat, ctx


class KB:
    def __init__(self, cfg):
        self.cfg = cfg
        nc = bass.Bass("TRN2", target_bir_lowering=False)
        self.nc = nc
        self.P = Prog(nc)
        self.rr = 0

    def MM(self, ps, lhsT, rhs, st, sp, R, W):
        partial = lhsT.partition_size() < 128
        self.P.op("pe", lambda e: e.matmul(ps, lhsT, rhs, start=st, stop=sp), R, W, partial=partial)

    def TR(self, ps, in_, ident, R, W):
        self.P.op("pe", lambda e: e.transpose(ps, in_, ident), R, W)

    def ACT(self, out, in_, func, R, W, **kw):
        self.P.op("act", lambda e: e.activation(out=out, in_=in_, func=func, **kw), R, W)

    def TS(self, out, in0, s1, s2, op0, op1, R, W, eng="dve"):
        if s2 is None:
            self.P.op(eng, lambda e: e.tensor_scalar(out=out, in0=in0, scalar1=s1, scalar2=None, op0=op0), R, W)
        else:
            self.P.op(eng, lambda e: e.tensor_scalar(out=out, in0=in0, scalar1=s1, scalar2=s2, op0=op0, op1=op1), R, W)

    def TT(self, out, in0, in1, op, R, W, eng="dve"):
        self.P.op(eng, lambda e: e.tensor_tensor(out=out, in0=in0, in1=in1, op=op), R, W)

    def STT(self, out, in0, sc, in1, op0, op1, R, W, eng="dve"):
        eng = "dve"
        self.P.op(eng, lambda e: e.scalar_tensor_tensor(out=out, in0=in0, scalar=sc, in1=in1, op0=op0, op1=op1), R, W)

    def CP(self, out, in_, R, W, eng="dve"):
        if eng == "act":
            self.ACT(out, in_, AF.Copy, R, W)
        else:
            self.P.op(eng, lambda e: e.tensor_copy(out=out, in_=in_), R, W)

    def MS(self, ap, val, W, eng="dve"):
        self.P.op(eng, lambda e: e.memset(ap, val), (), W)

    def RECIP(self, out, in_, R, W):
        self.P.op("dve", lambda e: e.reciprocal(out=out, in_=in_), R, W)

    def SCAN(self, out, d0, d1, R, W):
        self.P.op("dve", lambda e: e.tensor_tensor_scan(out=out, data0=d0, data1=d1, initial=0.0,
                                                        op0=ALU.mult, op1=ALU.add), R, W)

    def LD(self, out, in_, W, R=(), q="sp", **kw):
        self.P.dma(out, in_, reads=R, writes=W, q=q, **kw)

    def ST(self, out, in_, R, W=(), q="pool", **kw):
        self.P.dma(out, in_, reads=R, writes=W, q=q, **kw)

    def evac_eng(self):
        self.rr += 1
        return "act" if self.rr % 2 else "dve"

    def C(self, name):
        i = CN[name]
        return self.const[:, i * 128:(i + 1) * 128]


PARAM_SHAPES = {
    "mod_w": [2, 1024, 6144], "mod_b": [2, 6144], "norm1_g": [2, 1024], "norm2_g": [2, 1024],
    "w_in": [2, 1024, 3600], "hgrn_lb_logits": [2, 2, 256], "hgrn_norm_g": [2, 64],
    "ret_decay_logit": [2, 2, 4], "gdn_conv_w": [2, 3, 3, 768], "gdn_a_log": [2, 2, 4],
    "gdn_dt_bias": [2, 2, 4], "gdn_norm_g": [2, 64], "s5_lam_re": [2, 2, 16, 64],
    "s5_lam_im": [2, 2, 16, 64], "s5_log_dt": [2, 2, 16], "s5_b_re": [2, 16, 64, 16],
    "s5_b_im": [2, 16, 64, 16], "s5_c_re": [2, 16, 16, 64], "s5_c_im": [2, 16, 16, 64],
    "s5_d": [2, 256], "s5_glu_w": [2, 256, 256], "s5_glu_b": [2, 256], "w_out": [2, 1024, 1024],
    "mlp_w1": [2, 1024, 4096], "mlp_w2": [2, 4096, 1024], "final_norm_g": [1024],
}


def declare(kb):
    P = kb.P
    cfg = kb.cfg
    kinds = cfg.get("kinds", {})
    kb.xin = P.dram("xin", [S, D], F32, kind="ExternalInput")
    kb.cvecT = P.dram("cvecT", [1024, 2], F32, kind="ExternalInput")
    kb.prm = {k: P.dram(k, shp, F32, kind="ExternalInput") for k, shp in PARAM_SHAPES.items()}
    kb.constd = P.dram("constp", [128, NCONST * 128], F32, kind="ExternalInput")
    kb.ropec = P.dram("ropec", [128, S], F32, kind="ExternalInput")
    kb.ropes = P.dram("ropes", [128, S], F32, kind="ExternalInput")
    kb.cmlat = P.dram("cmlat", [128, 2, 512], F32, kind="ExternalInput")
    kb.cmctx = P.dram("cmctx", [128, 2, 256], F32, kind="ExternalInput")
    kb.y = P.dram("y", [4096, D], F32, kind="ExternalOutput")
    kb.XS = P.dram("XS", [S, D], F32, kind=kinds.get("XS", "Internal"))
    kb.ZF = P.dram("ZF", [NZF, S], F32, kind=kinds.get("ZF", "Internal"))
    kb.ZT = P.dram("ZT", [S, NZT], F32, kind=kinds.get("ZT", "Internal"))
    kb.QKVF = P.dram("QKVF", [768, S], F32, kind=kinds.get("QKVF", "Internal"))
    kb.YC = P.dram("YC", [1024, S], F32, kind=kinds.get("YC", "Internal"))
    kb.H2T = P.dram("H2T", [1024, S], BF16, kind=kinds.get("H2T", "Internal"))
    kb.const = P.sbuf("const", [128, NCONST * 128])
    nchunk = 4
    w = NCONST * 128 // nchunk
    for i in range(nchunk):
        a, b = i * w, (i + 1) * w if i < nchunk - 1 else NCONST * 128
        kb.LD(kb.const[:, a:b], kb.constd[:, a:b], [kb.const.s(i)])
    kb.const_bufs = [kb.const.s(i) for i in range(nchunk)]
    kb.CB = kb.const_bufs
    kb.GS1 = P.sbuf("GS1", [128, 8, 2]); kb.SH1 = P.sbuf("SH1", [128, 8, 2])
    kb.GS2 = P.sbuf("GS2", [128, 8, 2]); kb.SH2 = P.sbuf("SH2", [128, 8, 2])
    kb.GATE1 = P.sbuf("GATE1", [128, 2, 1024]); kb.GATE2 = P.sbuf("GATE2", [128, 2, 1024])


def phase_mod(kb, l):
    P = kb.P
    prm = kb.prm
    with P.scope():
        cT = P.sbuf("cT", [128, 8, 2])
        kb.LD(cT[:], kb.cvecT[:].rearrange("(et e) c -> e et c", e=128), [cT])
        sc = P.sbuf("sc", [128, 8, 2])
        kb.ACT(sc[:], cT[:], AF.Silu, [cT], [sc])
        screp = P.sbuf("screp", [128, 8, 2, 128])
        kb.CP(screp[:], sc[:].unsqueeze(3).to_broadcast([128, 8, 2, 128]), [sc], [screp])
        mbf = P.sbuf("mbf", [128, 48])
        kb.LD(mbf[:], prm["mod_b"][l].rearrange("(j p) -> p j", p=128), [mbf], allow_slow_non_contiguous=True)
        ngf = P.sbuf("ngf", [128, 2, 8])
        kb.LD(ngf[:, 0, :], prm["norm1_g"][l].rearrange("(j p) -> p j", p=128), [ngf], allow_slow_non_contiguous=True)
        kb.LD(ngf[:, 1, :], prm["norm2_g"][l].rearrange("(j p) -> p j", p=128), [ngf], allow_slow_non_contiguous=True)
        mbrow = P.sbuf("mbrow", [128, 2, 1024])
        for gi, v in enumerate((2, 5)):
            kb.LD(mbrow[:, gi, :], prm["mod_b"][l][v * 1024:(v + 1) * 1024].partition_broadcast(128), [mbrow])
        wch = [P.sbuf("wch%d" % i, [128, 8, 1024]) for i in range(2)]
        ps_fm = P.psum("ps_fm", [128, 96])
        ps_g = [P.psum("ps_g%d" % i, [128, 512]) for i in range(2)]
        MF = P.sbuf("MF", [128, 48, 2])
        k = 0
        for v in range(6):
            wc = wch[v % 2]
            for et in range(8):
                kb.LD(wc[:, et, :], prm["mod_w"][l][et * 128:(et + 1) * 128, v * 1024:(v + 1) * 1024], [wc])
            for db in range(8):
                col = (v * 8 + db) * 2
                for et in range(8):
                    kb.MM(ps_fm[:, col:col + 2], wc[:, et, db * 128:(db + 1) * 128], sc[:, et, :],
                          et == 0, et == 7, [wc, sc], [ps_fm])
            if v in (2, 5):
                gt = kb.GATE1 if v == 2 else kb.GATE2
                gi = 0 if v == 2 else 1
                for which in range(2):
                    for half in range(2):
                        pg = ps_g[k % 2]; k += 1
                        for et in range(8):
                            kb.MM(pg[:], screp[:, et, which, :], wc[:, et, half * 512:(half + 1) * 512],
                                  et == 0, et == 7, [screp, wc], [pg])
                        kb.TT(gt[:, which, half * 512:(half + 1) * 512], pg[:], mbrow[:, gi, half * 512:(half + 1) * 512],
                              ALU.add, [pg, mbrow], [gt])
        kb.TT(MF[:], ps_fm[:].rearrange("p (j c) -> p j c", c=2), mbf[:].unsqueeze(2).to_broadcast([128, 48, 2]),
              ALU.add, [ps_fm, mbf], [MF])
        tmp = P.sbuf("mtmp", [128, 8, 2])
        kb.TS(tmp[:], MF[:, 8:16, :], 1.0, None, ALU.add, None, [MF], [tmp])
        kb.TT(kb.GS1[:], tmp[:], ngf[:, 0, :].unsqueeze(2).to_broadcast([128, 8, 2]), ALU.mult, [tmp, ngf], [kb.GS1])
        kb.CP(kb.SH1[:], MF[:, 0:8, :], [MF], [kb.SH1])
        tmp2 = P.sbuf("mtmp2", [128, 8, 2])
        kb.TS(tmp2[:], MF[:, 32:40, :], 1.0, None, ALU.add, None, [MF], [tmp2])
        kb.TT(kb.GS2[:], tmp2[:], ngf[:, 1, :].unsqueeze(2).to_broadcast([128, 8, 2]), ALU.mult, [tmp2, ngf], [kb.GS2])
        kb.CP(kb.SH2[:], MF[:, 24:32, :], [MF], [kb.SH2])


def norm_to_fm(kb, xt, hT, col0, GS, SH, which, bufs, R_x):
    P = kb.P
    junk, st, xn, ps_ts = bufs["junk"], bufs["st"], bufs["xn"], bufs["ps_t"]
    kb.MS(st[:, 0:1], 0.0, [st])
    kb.ACT(junk[:], xt[:], AF.Square, [xt], [junk, st], accum_out=st[:, 0:1])
    kb.ACT(st[:, 1:2], st[:, 0:1], AF.Sqrt, [st] + kb.CB, [st], scale=1.0 / D, bias=kb.C("CCOL")[:, 0:1])
    kb.RECIP(st[:, 2:3], st[:, 1:2], [st], [st])
    kb.ACT(xn[:], xt[:], AF.Copy, [xt, st], [xn], scale=st[:, 2:3])
    for half in range(2):
        ps_t = ps_ts[half]
        for q in range(4):
            dt = half * 4 + q
            kb.TR(ps_t[:, q * 128:(q + 1) * 128], xn[:, dt * 128:(dt + 1) * 128], kb.C("IDENT"), [xn] + kb.CB, [ps_t])
        for q in range(4):
            dt = half * 4 + q
            if q % 2 == 0:
                kb.TS(hT[:, dt, col0:col0 + 128], ps_t[:, q * 128:(q + 1) * 128], GS[:, dt, which:which + 1],
                      SH[:, dt, which:which + 1], ALU.mult, ALU.add, [ps_t, GS, SH], [hT])
            else:
                kb.ACT(hT[:, dt, col0:col0 + 128], ps_t[:, q * 128:(q + 1) * 128], AF.Identity, [ps_t, GS, SH], [hT],
                       scale=GS[:, dt, which:which + 1], bias=SH[:, dt, which:which + 1])


def phase_a(kb, l, src):
    P = kb.P
    with P.scope():
        win = P.sbuf("win", [128, 8, 3600], BF16)
        for kt in range(8):
            kb.LD(win[:, kt, :], kb.prm["w_in"][l][kt * 128:(kt + 1) * 128, :], [win.s(kt)], q="pool")
        winb = [win.s(kt) for kt in range(8)]
        xbuf = [P.sbuf("xa%d" % i, [128, 1024]) for i in range(2)]
        hTb = [P.sbuf("hTa%d" % i, [128, 8, 512], BF16) for i in range(2)]
        nb = {"junk": P.sbuf("junk", [128, 1024]), "st": P.sbuf("st", [128, 4]), "xn": P.sbuf("xn", [128, 1024]),
              "ps_t": [P.psum("ps_t%d" % i, [128, 512]) for i in range(2)]}
        ps_f = [P.psum("ps_f%d" % i, [128, 512]) for i in range(3)]
        ps_a = [P.psum("ps_a%d" % i, [128, 512]) for i in range(2)]
        ps_b = P.psum("ps_b", [128, 16])
        stg = [P.sbuf("stg%d" % i, [128, 512]) for i in range(4)]
        stt = [P.sbuf("stt%d" % i, [128, NZT]) for i in range(2)]
        kx = kf = ks = ka = 0
        for gi, t0 in enumerate(range(0, S, 512)):
            n = min(512, S - t0)
            hT = hTb[gi % 2]
            for ti in range(n // 128):
                tt = t0 // 128 + ti
                which = 1 if tt < 2 else 0
                xt = xbuf[kx % 2]; kx += 1
                kb.LD(xt[:], src[tt * 128:(tt + 1) * 128, :], [xt])
                norm_to_fm(kb, xt, hT, ti * 128, kb.GS1, kb.SH1, which, nb, None)
            for (zr, wc) in FM_BLOCKS:
                ps = ps_f[kf % 3]; kf += 1
                for kt in range(8):
                    kb.MM(ps[:, :n], win[:, kt, wc:wc + 128], hT[:, kt, :n], kt == 0, kt == 7, [winb[kt], hT], [ps])
                sg = stg[ks % 4]; ks += 1
                kb.CP(sg[:, :n], ps[:, :n], [ps], [sg], eng=kb.evac_eng())
                kb.ST(kb.ZF[zr:zr + 128, t0:t0 + n], sg[:, :n], [sg])
            for ti in range(n // 128):
                tt = t0 // 128 + ti
                pa = ps_a[ka % 2]
                so = stt[ka % 2]; ka += 1
                for (c0, w0, wn) in ((0, C_HI, 256), (256, C_RV, 256)):
                    for kt in range(8):
                        kb.MM(pa[:, c0:c0 + wn], hT[:, kt, ti * 128:(ti + 1) * 128], win[:, kt, w0:w0 + wn],
                              kt == 0, kt == 7, [winb[kt], hT], [pa])
                for kt in range(8):
                    kb.MM(ps_b[:], hT[:, kt, ti * 128:(ti + 1) * 128], win[:, kt, C_GA:C_GA + 16],
                          kt == 0, kt == 7, [winb[kt], hT], [ps_b])
                kb.CP(so[:, 0:512], pa[:], [pa], [so], eng="act")
                kb.CP(so[:, 512:528], ps_b[:], [ps_b], [so], eng="dve")
                kb.ST(kb.ZT[tt * 128:(tt + 1) * 128, :], so[:], [so])


def build(cfg):
    kb = KB(cfg)
    P = kb.P
    declare(kb)
    P.barrier()
    stages = cfg.get("stages", "all")
    for l in cfg.get("layers", range(DEPTH)):
        src = kb.xin if l == 0 else kb.XS
        if stages == "all" or "M" in stages:
            phase_mod(kb, l)
        if stages == "all" or "A" in stages:
            phase_a(kb, l, src)
        if stages == "all" or "R" in stages:
            mixer_ret(kb, l)
        if stages == "all" or "H" in stages:
            mixer_hgrn(kb, l)
        if stages == "all" or "G" in stages:
            mixer_gdn(kb, l)
        if stages == "all" or "S" in stages:
            mixer_s5(kb, l)
        if stages == "all" or "C" in stages:
            phase_c(kb, l, src)
    P.emit()
    P.close()
    return kb


_CONSTS = None


def host_inputs(inputs, cores=range(8)):
    global _CONSTS
    if _CONSTS is None:
        rc, rs = _rope_tables()
        cl, cc = _conv_masks()
        _CONSTS = {"constp": CONST_NP, "ropec": rc, "ropes": rs, "cmlat": cl, "cmctx": cc}
    maps = []
    for b in cores:
        m = {"xin": np.ascontiguousarray(np.concatenate([inputs["ctx"][b], inputs["x"][b]], axis=0), dtype=np.float32),
             "cvecT": np.ascontiguousarray(np.stack([inputs["c"][b], inputs["c_ctx"]], axis=1), dtype=np.float32)}
        for k in PARAM_SHAPES:
            m[k] = np.ascontiguousarray(inputs[k], dtype=np.float32)
        m.update(_CONSTS)
        maps.append(m)
    return maps


def kernel(**inputs):
    inputs = {k: np.asarray(v) for k, v in inputs.items()}
    kb = build({})
    maps = host_inputs(inputs)
    res = run_bass_kernel_spmd(kb.nc, maps, core_ids=list(range(8)))
    out = np.stack([np.asarray(r["y"]).reshape(4096, D) for r in res.results], axis=0)
    return out.astype(np.float32)


def phase_c(kb, l, src):
    P = kb.P
    last = (l == DEPTH - 1)
    t_start = 2 if last else 0
    with P.scope():
        wout = P.sbuf("wout", [128, 8, 1024], BF16)
        for ft in range(8):
            kb.LD(wout[:, ft, :], kb.prm["w_out"][l][ft * 128:(ft + 1) * 128, :], [wout.s(ft)], q="pool")
        wb = [wout.s(ft) for ft in range(8)]
        ycb = [P.sbuf("yc%d" % i, [128, 8, 128], BF16) for i in range(2)]
        xb = [P.sbuf("xc%d" % i, [128, 1024]) for i in range(2)]
        x1b = [P.sbuf("x1c%d" % i, [128, 1024]) for i in range(2)]
        tmpb = [P.sbuf("tc%d" % i, [128, 512]) for i in range(2)]
        h2b = [P.sbuf("h2c%d" % i, [128, 8, 128], BF16) for i in range(2)]
        nb = {"junk": P.sbuf("junkc", [128, 1024]), "st": P.sbuf("stc", [128, 4]), "xn": P.sbuf("xnc", [128, 1024]),
              "ps_t": [P.psum("ps_tc%d" % i, [128, 512]) for i in range(2)]}
        ps_y = [P.psum("ps_y%d" % i, [128, 512]) for i in range(4)]
        k = 0
        for tt in range(t_start, NT):
            which = 1 if tt < 2 else 0
            yc = ycb[k % 2]; xt = xb[k % 2]; x1 = x1b[k % 2]; h2 = h2b[k % 2]
            cols = slice(tt * 128, (tt + 1) * 128)
            kb.LD(yc[:], kb.YC[:, cols].rearrange("(ft p) t -> p ft t", p=128), [yc], q="pool")
            kb.LD(xt[:], src[cols, :], [xt])
            for half in range(2):
                ps = ps_y[(2 * k + half) % 4]
                for ft in range(8):
                    kb.MM(ps[:], yc[:, ft, :], wout[:, ft, half * 512:(half + 1) * 512], ft == 0, ft == 7,
                          [yc, wb[ft]], [ps])
                tm = tmpb[half]
                kb.TT(tm[:], ps[:], kb.GATE1[:, which, half * 512:(half + 1) * 512], ALU.mult, [ps, kb.GATE1], [tm])
                kb.TT(x1[:, half * 512:(half + 1) * 512], xt[:, half * 512:(half + 1) * 512], tm[:], ALU.add,
                      [xt, tm], [x1], eng="pool")
            kb.ST(kb.XS[cols, :], x1[:], [x1])
            norm_to_fm(kb, x1, h2, 0, kb.GS2, kb.SH2, which, nb, None)
            kb.ST(kb.H2T[:, cols].rearrange("(dt p) t -> p dt t", p=128), h2[:], [h2])
            k += 1
    with P.scope():
        w1 = P.sbuf("w1", [128, 8, 4096], BF16)
        w2 = P.sbuf("w2", [128, 32, 1024], BF16)
        for kt in range(8):
            kb.LD(w1[:, kt, :], kb.prm["mlp_w1"][l][kt * 128:(kt + 1) * 128, :], [w1.s(kt)], q="pool")
        for fb in range(32):
            kb.LD(w2[:, fb, :], kb.prm["mlp_w2"][l][fb * 128:(fb + 1) * 128, :], [w2.s(fb)], q="pool")
        h2b = [P.sbuf("h2d%d" % i, [128, 8, 256], BF16) for i in range(2)]
        uTb = [P.sbuf("uT%d" % i, [128, 16, 256], BF16) for i in range(1)]
        rb = [P.sbuf("relu%d" % i, [128, 256]) for i in range(3)]
        xb = [P.sbuf("xd%d" % i, [128, 1024]) for i in range(2)]
        tmpb = [P.sbuf("td%d" % i, [128, 512]) for i in range(2)]
        ps_u = [P.psum("ps_u%d" % i, [128, 256]) for i in range(3)]
        ps_y = [P.psum("ps_y2%d" % i, [128, 512]) for i in range(4)]
        if last:
            fg = P.sbuf("fg", [128, 1024])
            kb.LD(fg[:], kb.prm["final_norm_g"][:].partition_broadcast(128), [fg])
            stf = P.sbuf("stf", [128, 4])
            xnf = P.sbuf("xnf", [128, 1024])
        k = 0; ku = 0
        for g0 in range(t_start, NT, 2):
            h2 = h2b[k % 2]; uT = uTb[0]
            cols = slice(g0 * 128, (g0 + 2) * 128)
            kb.LD(h2[:], kb.H2T[:, cols].rearrange("(dt p) t -> p dt t", p=128), [h2])
            for hh in range(2):
                for fl in range(16):
                    fb = hh * 16 + fl
                    ps = ps_u[ku % 3]; r = rb[ku % 3]; ku += 1
                    for kt in range(8):
                        kb.MM(ps[:], w1[:, kt, fb * 128:(fb + 1) * 128], h2[:, kt, :], kt == 0, kt == 7, [w1.s(kt), h2], [ps])
                    kb.ACT(r[:], ps[:], AF.Relu, [ps], [r])
                    kb.TT(uT[:, fl, :], r[:], r[:], ALU.mult, [r], [uT.s(fl)], eng=("dve" if fb % 2 else "pool"))
                for ti in range(2):
                    for half in range(2):
                        ps = ps_y[2 * ti + half]
                        for fl in range(16):
                            fb = hh * 16 + fl
                            kb.MM(ps[:], uT[:, fl, ti * 128:(ti + 1) * 128], w2[:, fb, half * 512:(half + 1) * 512],
                                  fb == 0, fb == 31, [uT.s(fl), w2.s(fb)], [ps])
            for ti in range(2):
                tt = g0 + ti
                which = 1 if tt < 2 else 0
                xt = xb[ti]
                rows = slice(tt * 128, (tt + 1) * 128)
                kb.LD(xt[:], kb.XS[rows, :], [xt])
                for half in range(2):
                    ps = ps_y[2 * ti + half]
                    tm = tmpb[half]
                    kb.TT(tm[:], ps[:], kb.GATE2[:, which, half * 512:(half + 1) * 512], ALU.mult, [ps, kb.GATE2], [tm])
                    kb.TT(xt[:, half * 512:(half + 1) * 512], xt[:, half * 512:(half + 1) * 512], tm[:], ALU.add,
                          [xt, tm], [xt], eng="pool")
                if not last:
                    kb.ST(kb.XS[rows, :], xt[:], [xt])
                else:
                    kb.MS(stf[:, 0:1], 0.0, [stf])
                    kb.ACT(xnf[:], xt[:], AF.Square, [xt], [xnf, stf], accum_out=stf[:, 0:1])
                    kb.ACT(stf[:, 1:2], stf[:, 0:1], AF.Sqrt, [stf], [stf], scale=1.0 / D, bias=kb.C("CCOL")[:, 0:1])
                    kb.RECIP(stf[:, 2:3], stf[:, 1:2], [stf], [stf])
                    kb.ACT(xnf[:], xt[:], AF.Copy, [xt, stf], [xnf], scale=stf[:, 2:3])
                    kb.TT(xnf[:], xnf[:], fg[:], ALU.mult, [xnf, fg], [xnf])
                    kb.P.dma(kb.y[(tt - 2) * 128:(tt - 1) * 128, :], xnf[:], reads=[xnf.b], q="pool", is_output=True)
            k += 1


def finalize_gated(kb, OACC, gate_row0, gain, yc_row0, pfx):
    P = kb.P
    gb = [P.sbuf(pfx + "fg%d" % i, [128, 2, 128]) for i in range(2)]
    sq = [P.sbuf(pfx + "fsq%d" % i, [128, 128]) for i in range(2)]
    rt = [P.sbuf(pfx + "frt%d" % i, [128, 128]) for i in range(2)]
    sg = [P.sbuf(pfx + "fsg%d" % i, [128, 128]) for i in range(2)]
    ob = [P.sbuf(pfx + "fo%d" % i, [128, 128]) for i in range(2)]
    ps_m = [P.psum(pfx + "fps%d" % i, [128, 128]) for i in range(2)]
    k = 0
    for n in range(NT):
        cols = slice(n * 128, (n + 1) * 128)
        g = gb[n % 2]
        kb.LD(g[:], kb.ZF[gate_row0:gate_row0 + 256, cols].rearrange("(hp p) t -> p hp t", p=128), [g])
        for hp in range(2):
            i = k % 2; k += 1
            o = OACC[:, hp, cols]
            kb.TT(sq[i][:], o, o, ALU.mult, [OACC.s(n)], [sq[i]], eng="pool")
            kb.MM(ps_m[i][:], kb.C("BLK64"), sq[i][:], True, True, [sq[i]], [ps_m[i]])
            kb.ACT(rt[i][:], ps_m[i][:], AF.Sqrt, [ps_m[i]], [rt[i]], scale=1.0 / 64, bias=kb.C("CCOL")[:, 0:1])
            kb.RECIP(rt[i][:], rt[i][:], [rt[i]], [rt[i]])
            kb.ACT(sg[i][:], g[:, hp, :], AF.Silu, [g], [sg[i]])
            kb.TT(ob[i][:], o, rt[i][:], ALU.mult, [OACC.s(n), rt[i]], [ob[i]])
            if gain is not None:
                kb.STT(ob[i][:], ob[i][:], gain[:, 0:1], sg[i][:], ALU.mult, ALU.mult, [ob[i], gain, sg[i]], [ob[i]])
            else:
                kb.TT(ob[i][:], ob[i][:], sg[i][:], ALU.mult, [ob[i], sg[i]], [ob[i]])
            kb.ST(kb.YC[yc_row0 + hp * 128:yc_row0 + (hp + 1) * 128, cols], ob[i][:], [ob[i]])


def oacc_write(kb, OACC, hp, n, ps, d):
    cols = slice(n * 128, (n + 1) * 128)
    if d == 0:
        kb.CP(OACC[:, hp, cols], ps[:], [ps], [OACC.s(n)], eng="act")
    else:
        kb.TT(OACC[:, hp, cols], OACC[:, hp, cols], ps[:], ALU.add, [ps], [OACC.s(n)])


def mixer_ret(kb, l):
    P = kb.P
    with P.scope():
        OACC = P.sbuf("r_oacc", [128, 2, S])
        with P.scope():
            lgt = P.sbuf("r_lgt", [128, 8])
            kb.LD(lgt[:], kb.prm["ret_decay_logit"][l].rearrange("d h -> (d h)").partition_broadcast(128), [lgt])
            LG = P.sbuf("r_LG", [128, 8])
            kb.ACT(LG[:], lgt[:], AF.Sigmoid, [lgt], [LG])
            kb.ACT(LG[:], LG[:], AF.Ln, [LG], [LG])
            LGP = P.sbuf("r_LGP", [128, 4])
            for d in range(2):
                for hp in range(2):
                    c = 2 * d + hp
                    kb.CP(LGP[0:64, c:c + 1], LG[0:64, 4 * d + 2 * hp:4 * d + 2 * hp + 1], [LG], [LGP])
                    kb.CP(LGP[64:128, c:c + 1], LG[64:128, 4 * d + 2 * hp + 1:4 * d + 2 * hp + 2], [LG], [LGP])
            MK = [P.sbuf("r_MK%d" % d, [128, 4, 128]) for d in range(2)]
            QDEC = [[P.sbuf("r_QD%d%d" % (d, hp), [128, 128]) for hp in range(2)] for d in range(2)]
            etmp = P.sbuf("r_etmp", [128, 128])
            for d in range(2):
                for h in range(4):
                    kb.ACT(etmp[:], kb.C("DIFF" if d == 0 else "NDIFF"), AF.Exp, [LG], [etmp],
                           scale=LG[:, 4 * d + h:4 * d + h + 1])
                    kb.STT(MK[d][:, h, :], etmp[:], 0.125, kb.C("TRIF" if d == 0 else "TRIB"), ALU.mult, ALU.mult,
                           [etmp], [MK[d]])
                for hp in range(2):
                    kb.ACT(QDEC[d][hp][:], kb.C("IOTAF1" if d == 0 else "RIOTAF"), AF.Exp, [LGP], [QDEC[d][hp]],
                           scale=LGP[:, 2 * d + hp:2 * d + hp + 1])
            KD = P.sbuf("r_KD", [128, 8])
            kb.ACT(KD[:, 0:4], LG[:, 0:4], AF.Exp, [LG], [KD], scale=kb.C("CCOL")[:, 3:4])
            kb.ACT(KD[:, 4:8], LG[:, 4:8], AF.Exp, [LG], [KD], scale=kb.C("CCOL")[:, 2:3])
            kb.TS(KD[:], KD[:], 0.125, None, ALU.mult, None, [KD], [KD])
            CV = P.sbuf("r_CV", [128, 4])
            kb.ACT(CV[:], LGP[:], AF.Exp, [LGP], [CV], scale=128.0)
            qTb = [P.sbuf("r_q%d" % i, [128, 2, 128]) for i in range(2)]
            kTb = [P.sbuf("r_k%d" % i, [128, 2, 128]) for i in range(2)]
            csb = [P.sbuf("r_cs%d" % i, [128, 2, 128]) for i in range(2)]
            Vp = [[P.sbuf("r_vp%d%d" % (i, h), [128, 128]) for h in range(4)] for i in range(2)]
            khp = [[P.sbuf("r_kh%d%d" % (i, h), [128, 128]) for h in range(4)] for i in range(2)]
            for i in range(2):
                for h in range(4):
                    kb.MS(Vp[i][h][:], 0.0, [Vp[i][h]], eng="pool")
                    kb.MS(khp[i][h][:], 0.0, [khp[i][h]], eng="pool")
            t1 = [P.sbuf("r_t1%d" % i, [128, 128]) for i in range(2)]
            t2 = [P.sbuf("r_t2%d" % i, [128, 128]) for i in range(2)]
            qr = [P.sbuf("r_qr%d" % i, [128, 2, 128]) for i in range(2)]
            kr = [P.sbuf("r_kr%d" % i, [128, 2, 128]) for i in range(2)]
            AT = [P.sbuf("r_AT%d" % i, [128, 2, 128]) for i in range(2)]
            qd = [P.sbuf("r_qd%d" % i, [128, 128]) for i in range(2)]
            Sb = [P.sbuf("r_S%d" % hp, [128, 128]) for hp in range(2)]
            ps_r = [P.psum("r_psr%d" % i, [128, 256]) for i in range(2)]
            ps_s = [P.psum("r_pss%d" % i, [128, 2, 128]) for i in range(2)]
            ps_o = [P.psum("r_pso%d" % i, [128, 128]) for i in range(2)]
            ps_k = P.psum("r_psk", [128, 128])
            ps_kv = P.psum("r_pskv", [128, 128])
            it = 0
            for d in range(2):
                for hp in range(2):
                    kb.MS(Sb[hp][:], 0.0, [Sb[hp]])
                for n in ORDER[d]:
                    cols = slice(n * 128, (n + 1) * 128)
                    b = it % 2; it += 1
                    qT, kT, cs = qTb[b], kTb[b], csb[b]
                    kb.LD(qT[:], kb.ZF[Z_RQ:Z_RQ + 256, cols].rearrange("(hp p) t -> p hp t", p=128), [qT])
                    kb.LD(kT[:], kb.ZF[Z_RK:Z_RK + 256, cols].rearrange("(hp p) t -> p hp t", p=128), [kT])
                    kb.LD(cs[:, 0, :], kb.ropec[:, cols], [cs])
                    kb.LD(cs[:, 1, :], kb.ropes[:, cols], [cs])
                    for h in range(4):
                        kb.LD(Vp[b][h][:, 64 * (h % 2):64 * (h % 2) + 64], kb.ZT[cols, 256 + 64 * h:256 + 64 * h + 64],
                              [Vp[b][h]])
                    for hp in range(2):
                        j = (it * 2 + hp) % 2
                        pr = ps_r[j]
                        kb.MM(pr[:, 0:128], kb.C("ROT"), qT[:, hp, :], True, True, [qT], [pr])
                        kb.MM(pr[:, 128:256], kb.C("ROT"), kT[:, hp, :], True, True, [kT], [pr])
                        for (src_, dst, off) in ((qT, qr[b], 0), (kT, kr[b], 128)):
                            kb.TT(t1[j][:], src_[:, hp, :], cs[:, 0, :], ALU.mult, [src_, cs], [t1[j]], eng="pool")
                            kb.TT(t2[j][:], pr[:, off:off + 128], cs[:, 1, :], ALU.mult, [pr, cs], [t2[j]])
                            kb.TT(dst[:, hp, :], t1[j][:], t2[j][:], ALU.add, [t1[j], t2[j]], [dst.s(hp)], eng="pool")
                        pss = ps_s[j]
                        for h2 in range(2):
                            kb.MM(pss[:, h2, :], kr[b][64 * h2:64 * h2 + 64, hp, :], qr[b][64 * h2:64 * h2 + 64, hp, :],
                                  True, True, [kr[b].s(hp), qr[b].s(hp)], [pss])
                        kb.TT(AT[j][:], pss[:], MK[d][:, 2 * hp:2 * hp + 2, :], ALU.mult, [pss, MK[d]], [AT[j]])
                        kb.TT(qd[j][:], qr[b][:, hp, :], QDEC[d][hp][:], ALU.mult, [qr[b].s(hp), QDEC[d][hp]], [qd[j]],
                              eng="pool")
                        po = ps_o[j]
                        kb.MM(po[:], Vp[b][2 * hp][:], AT[j][:, 0, :], True, False, [Vp[b][2 * hp], AT[j]], [po])
                        kb.MM(po[:], Vp[b][2 * hp + 1][:], AT[j][:, 1, :], False, False, [Vp[b][2 * hp + 1], AT[j]], [po])
                        kb.MM(po[:], Sb[hp][:], qd[j][:], False, True, [Sb[hp], qd[j]], [po])
                        oacc_write(kb, OACC, hp, n, po, d)
                        kb.TR(ps_k[:], kr[b][:, hp, :], kb.C("IDENT"), [kr[b].s(hp)], [ps_k])
                        for h2 in range(2):
                            h = 2 * hp + h2
                            kb.ACT(khp[b][h][:, 64 * h2:64 * h2 + 64], ps_k[:, 64 * h2:64 * h2 + 64], AF.Copy,
                                   [ps_k, KD], [khp[b][h]], scale=KD[:, 4 * d + h:4 * d + h + 1])
                        kb.MM(ps_kv[:], khp[b][2 * hp][:], Vp[b][2 * hp][:], True, False,
                              [khp[b][2 * hp], Vp[b][2 * hp]], [ps_kv])
                        kb.MM(ps_kv[:], khp[b][2 * hp + 1][:], Vp[b][2 * hp + 1][:], False, True,
                              [khp[b][2 * hp + 1], Vp[b][2 * hp + 1]], [ps_kv])
                        kb.STT(Sb[hp][:], Sb[hp][:], CV[:, 2 * d + hp:2 * d + hp + 1], ps_kv[:], ALU.mult, ALU.add,
                               [Sb[hp], CV, ps_kv], [Sb[hp]])
        with P.scope():
            finalize_gated(kb, OACC, Z_RG, None, 256, "r_")


def mixer_hgrn(kb, l):
    P = kb.P
    with P.scope():
        OACC = P.sbuf("h_oacc", [128, 2, S])
        with P.scope():
            LB = P.sbuf("h_LB", [128, 4]); OML = P.sbuf("h_OML", [128, 4])
            if l == 0:
                kb.MS(LB[:], 0.0, [LB]); kb.MS(OML[:], 1.0, [OML])
            else:
                lgt = P.sbuf("h_lgt", [128, 8])
                kb.LD(lgt[:], kb.prm["hgrn_lb_logits"][:].rearrange("l d (hp p) -> p (l d hp)", p=128), [lgt],
                      allow_slow_non_contiguous=True)
                kb.TT(LB[:], lgt[:, 4:8], lgt[:, 0:4], ALU.subtract, [lgt], [LB])
                kb.ACT(LB[:], LB[:], AF.Sigmoid, [LB], [LB])
                kb.TS(OML[:], LB[:], -1.0, 1.0, ALU.mult, ALU.add, [LB], [OML])
            G = P.sbuf("h_G", [128, 1])
            for hh in range(2):
                kb.LD(G[64 * hh:64 * hh + 64, :], kb.prm["hgrn_norm_g"][l].rearrange("(p o) -> p o", o=1), [G])
            kb.hgrn_gain = G
            hqb = [P.sbuf("h_q%d" % i, [128, 2, 128]) for i in range(2)]
            hfb = [P.sbuf("h_f%d" % i, [128, 2, 128]) for i in range(2)]
            Vp = [[P.sbuf("h_vp%d%d" % (i, h), [128, 128]) for h in range(4)] for i in range(2)]
            khp = [[P.sbuf("h_kh%d%d" % (i, h), [128, 128]) for h in range(4)] for i in range(2)]
            for i in range(2):
                for h in range(4):
                    kb.MS(Vp[i][h][:], 0.0, [Vp[i][h]], eng="pool")
                    kb.MS(khp[i][h][:], 0.0, [khp[i][h]], eng="pool")
            MREF = [[P.sbuf("h_mr%d%d" % (d, i), [128, 4]) for i in range(2)] for d in range(2)]
            for d in range(2):
                for i in range(2):
                    kb.MS(MREF[d][i][:], 0.0, [MREF[d][i]])

            def two(name, shape=(128, 128)):
                return [P.sbuf("h_%s%d" % (name, i), list(shape)) for i in range(2)]
            qs, sgm, ff, logf, kk, bb, pre = two("qs"), two("sg"), two("ff"), two("lf"), two("kk"), two("bb"), two("pre")
            e1, Ql, e2, Qd = two("e1"), two("Ql"), two("e2"), two("Qd")
            Kt = [two("Kt%d" % r) for r in range(4)]
            ex = two("ex")
            AT = two("AT", (128, 2, 128))
            KhT = two("KhT")
            bend = two("bend", (128, 2))
            Sb = [P.sbuf("h_S%d" % hp, [128, 128]) for hp in range(2)]
            ps_s = [P.psum("h_pss%d" % i, [128, 2, 128]) for i in range(2)]
            ps_o = [P.psum("h_pso%d" % i, [128, 128]) for i in range(2)]
            ps_k = [P.psum("h_psk%d" % i, [128, 128]) for i in range(2)]
            ps_kv = [P.psum("h_pskv%d" % i, [128, 128]) for i in range(2)]
            it = 0
            jj = 0
            for d in range(2):
                zf = Z_HFF if d == 0 else Z_HFB
                for hp in range(2):
                    kb.MS(Sb[hp][:], 0.0, [Sb[hp]])
                for n in ORDER[d]:
                    cols = slice(n * 128, (n + 1) * 128)
                    b = it % 2; it += 1
                    hq, hf = hqb[b], hfb[b]
                    kb.LD(hq[:], kb.ZF[Z_HQ:Z_HQ + 256, cols].rearrange("(hp p) t -> p hp t", p=128), [hq])
                    kb.LD(hf[:], kb.ZF[zf:zf + 256, cols].rearrange("(hp p) t -> p hp t", p=128), [hf])
                    for h in range(4):
                        kb.LD(Vp[b][h][:, 64 * (h % 2):64 * (h % 2) + 64], kb.ZT[cols, 64 * h:64 * h + 64], [Vp[b][h]])
                    for hp in range(2):
                        j = jj % 2; jj += 1
                        c = 2 * d + hp
                        mref = MREF[d][j]
                        kb.ACT(qs[j][:], hq[:, hp, :], AF.Silu, [hq], [qs[j]])
                        kb.ACT(sgm[j][:], hf[:, hp, :], AF.Sigmoid, [hf], [sgm[j]])
                        kb.TS(ff[j][:], sgm[j][:], OML[:, c:c + 1], LB[:, c:c + 1], ALU.mult, ALU.add, [sgm[j], OML, LB], [ff[j]])
                        kb.ACT(logf[j][:], ff[j][:], AF.Ln, [ff[j]], [logf[j]])
                        kb.TS(kk[j][:], ff[j][:], -1.0, 1.0, ALU.mult, ALU.add, [ff[j]], [kk[j]], eng="pool")
                        B = bb[j]
                        if d == 0:
                            kb.SCAN(B[:], kb.C("ONES"), logf[j][:], [logf[j]], [B])
                            kb.CP(mref[:, 1:4], B[:].rearrange("p (r c) -> p r c", c=32)[:, 0:3, 31], [B], [mref])
                            be = B[:, 127:128]
                        else:
                            kb.SCAN(pre[j][:], kb.C("ONES"), logf[j][:], [logf[j]], [pre[j]])
                            kb.STT(B[:], pre[j][:], -1.0, logf[j][:], ALU.mult, ALU.add, [pre[j], logf[j]], [B])
                            kb.TS(B[:], B[:], pre[j][:, 127:128], None, ALU.add, None, [B, pre[j]], [B])
                            kb.CP(mref[:, 0:3], B[:].rearrange("p (r c) -> p r c", c=32)[:, 1:4, 0], [B], [mref])
                            be = B[:, 0:1]
                        kb.TT(e1[j][:].rearrange("p (r c) -> p r c", c=32), B[:].rearrange("p (r c) -> p r c", c=32),
                              mref[:].unsqueeze(2).to_broadcast([128, 4, 32]), ALU.subtract, [B, mref], [e1[j]])
                        kb.ACT(e1[j][:], e1[j][:], AF.Exp, [e1[j]], [e1[j]])
                        kb.STT(Ql[j][:], qs[j][:], 0.125, e1[j][:], ALU.mult, ALU.mult, [qs[j], e1[j]], [Ql[j]], eng="pool")
                        kb.ACT(e2[j][:], B[:], AF.Exp, [B], [e2[j]])
                        kb.STT(Qd[j][:], qs[j][:], 0.125, e2[j][:], ALU.mult, ALU.mult, [qs[j], e2[j]], [Qd[j]], eng="pool")
                        pss = ps_s[j]
                        for r in range(4):
                            kb.ACT(ex[j][:], B[:], AF.Exp, [B, mref], [ex[j]], scale=-1.0, bias=mref[:, r:r + 1])
                            kb.STT(Kt[r][j][:], ex[j][:], 1e26, kk[j][:], ALU.min, ALU.mult, [ex[j], kk[j]], [Kt[r][j]])
                            for h2 in range(2):
                                kb.MM(pss[:, h2, 32 * r:32 * r + 32], Kt[r][j][64 * h2:64 * h2 + 64, :],
                                      Ql[j][64 * h2:64 * h2 + 64, 32 * r:32 * r + 32], True, True,
                                      [Kt[r][j], Ql[j]], [pss])
                        kb.TT(AT[j][:], pss[:], kb.C("TRIF" if d == 0 else "TRIB").unsqueeze(1).to_broadcast([128, 2, 128]),
                              ALU.mult, [pss], [AT[j]])
                        po = ps_o[j]
                        kb.MM(po[:], Vp[b][2 * hp][:], AT[j][:, 0, :], True, False, [Vp[b][2 * hp], AT[j]], [po])
                        kb.MM(po[:], Vp[b][2 * hp + 1][:], AT[j][:, 1, :], False, False, [Vp[b][2 * hp + 1], AT[j]], [po])
                        kb.MM(po[:], Sb[hp][:], Qd[j][:], False, True, [Sb[hp], Qd[j]], [po])
                        oacc_write(kb, OACC, hp, n, po, d)
                        kb.CP(bend[j][:, 0:1], be, [B], [bend[j]])
                        kb.ACT(KhT[j][:], B[:], AF.Exp, [B, bend[j]], [KhT[j]], scale=-1.0, bias=bend[j][:, 0:1])
                        kb.TT(KhT[j][:], KhT[j][:], kk[j][:], ALU.mult, [KhT[j], kk[j]], [KhT[j]], eng="pool")
                        kb.ACT(bend[j][:, 1:2], bend[j][:, 0:1], AF.Exp, [bend[j]], [bend[j]])
                        pk = ps_k[j]
                        kb.TR(pk[:], KhT[j][:], kb.C("IDENT"), [KhT[j]], [pk])
                        for h2 in range(2):
                            h = 2 * hp + h2
                            kb.CP(khp[b][h][:, 64 * h2:64 * h2 + 64], pk[:, 64 * h2:64 * h2 + 64], [pk], [khp[b][h]],
                                  eng=("act" if h2 else "dve"))
                        pkv = ps_kv[j]
                        kb.MM(pkv[:], khp[b][2 * hp][:], Vp[b][2 * hp][:], True, False, [khp[b][2 * hp], Vp[b][2 * hp]], [pkv])
                        kb.MM(pkv[:], khp[b][2 * hp + 1][:], Vp[b][2 * hp + 1][:], False, True,
                              [khp[b][2 * hp + 1], Vp[b][2 * hp + 1]], [pkv])
                        kb.STT(Sb[hp][:], Sb[hp][:], bend[j][:, 1:2], pkv[:], ALU.mult, ALU.add,
                               [Sb[hp], bend[j], pkv], [Sb[hp]])
        with P.scope():
            G = P.sbuf("h_G2", [128, 1])
            for hh in range(2):
                kb.LD(G[64 * hh:64 * hh + 64, :], kb.prm["hgrn_norm_g"][l].rearrange("(p o) -> p o", o=1), [G])
            finalize_gated(kb, OACC, Z_HG, G, 0, "h_")


PI = float(np.pi)


def _sincos(kb, ang, sin_out, cos_out, R, tmp, shape=None):
    P = kb.P
    shp = list(ang.shape)
    with P.scope():
        ki = P.sbuf("sc_ki", shp, mybir.dt.int32)
        kf = P.sbuf("sc_kf", shp)
        r = P.sbuf("sc_r", shp)
        m = P.sbuf("sc_m", shp)
        C1 = 6.28125
        C2 = 2 * PI - C1
        for (shift, out) in ((0.0, sin_out), (PI / 2, cos_out)):
            kb.TS(r[:], ang, shift, None, ALU.add, None, R, [r])
            kb.TS(kf[:], r[:], 1.0 / (2 * PI), None, ALU.mult, None, [r], [kf])
            kb.CP(ki[:], kf[:], [kf], [ki])
            kb.CP(kf[:], ki[:], [ki], [kf])
            kb.STT(r[:], kf[:], -C1, r[:], ALU.mult, ALU.add, [kf, r], [r])
            kb.STT(r[:], kf[:], -C2, r[:], ALU.mult, ALU.add, [kf, r], [r])
            kb.TS(m[:], r[:], PI, 2 * PI, ALU.is_gt, ALU.mult, [r], [m])
            kb.TT(r[:], r[:], m[:], ALU.subtract, [r, m], [r])
            kb.TS(m[:], r[:], -PI, 2 * PI, ALU.is_lt, ALU.mult, [r], [m])
            kb.TT(r[:], r[:], m[:], ALU.add, [r, m], [r])
            kb.ACT(out, r[:], AF.Sin, [r], R)


def mixer_s5(kb, l):
    P = kb.P
    prm = kb.prm
    with P.scope():
        OACC = P.sbuf("s_oacc", [128, 2, S])
        with P.scope():
            WX = P.sbuf("s_WX", [128, 2, 8, 2, 64])
            Cblk = P.sbuf("s_Cblk", [128, 16, 128])
            kb.MS(WX[:], 0.0, [WX], eng="pool")
            kb.MS(Cblk[:], 0.0, [Cblk], eng="pool")
            for g8 in range(8):
                for ri, nm in enumerate(("s5_b_re", "s5_b_im")):
                    for gg in range(2):
                        src = prm[nm][l][8 * gg + g8].rearrange("p c -> c p")
                        kb.LD(WX[16 * g8:16 * g8 + 16, gg, g8, ri, :], src, [WX], allow_slow_non_contiguous=True)
            for g in range(16):
                g8 = g % 8
                kb.LD(Cblk[0:64, g, 16 * g8:16 * g8 + 16], prm["s5_c_re"][l][g].rearrange("c p -> p c"), [Cblk],
                      allow_slow_non_contiguous=True)
                kb.LD(Cblk[64:128, g, 16 * g8:16 * g8 + 16], prm["s5_c_im"][l][g].rearrange("c p -> p c"), [Cblk],
                      allow_slow_non_contiguous=True)
            kb.TS(Cblk[64:128, :, :], Cblk[64:128, :, :], -1.0, None, ALU.mult, None, [Cblk], [Cblk])
            VFr = P.sbuf("s_VFr", [128, 16, 64]); VFi = P.sbuf("s_VFi", [128, 16, 64])
            T1 = P.sbuf("s_T1", [128, 16, 128]); T2 = P.sbuf("s_T2", [128, 16, 128])
            AR = P.sbuf("s_AR", [128, 16]); NAI = P.sbuf("s_NAI", [128, 16])
            for d in range(2):
                with P.scope():
                    lr = P.sbuf("s_lr", [128, 16, 64]); li = P.sbuf("s_li", [128, 16, 64]); dtb = P.sbuf("s_dt", [128, 16])
                    kb.LD(lr[:], prm["s5_lam_re"][l][d].rearrange("g p -> (g p)").partition_broadcast(128), [lr])
                    kb.LD(li[:], prm["s5_lam_im"][l][d].rearrange("g p -> (g p)").partition_broadcast(128), [li])
                    kb.LD(dtb[:], prm["s5_log_dt"][l][d].partition_broadcast(128), [dtb])
                    kb.ACT(dtb[:], dtb[:], AF.Exp, [dtb], [dtb])
                    dt_bc = dtb[:].unsqueeze(2).to_broadcast([128, 16, 64])
                    lrdt = P.sbuf("s_lrdt", [128, 16, 64]); lidt = P.sbuf("s_lidt", [128, 16, 64])
                    kb.TT(lrdt[:], lr[:], dt_bc, ALU.mult, [lr, dtb], [lrdt])
                    kb.TT(lidt[:], li[:], dt_bc, ALU.mult, [li, dtb], [lidt])
                    a = [P.sbuf("s_a%d" % i, [128, 16, 64]) for i in range(8)]
                    mag, ang, sn, cs, tmp, ar, ai, t2 = a
                    kb.ACT(mag[:], lrdt[:], AF.Exp, [lrdt], [mag])
                    _sincos(kb, lidt[:], sn[:], cs[:], [lidt, sn, cs, tmp], tmp[:])
                    kb.TT(ar[:], mag[:], cs[:], ALU.mult, [mag, cs], [ar])
                    kb.TT(ai[:], mag[:], sn[:], ALU.mult, [mag, sn], [ai])
                    den = P.sbuf("s_den", [128, 16, 64]); fr = P.sbuf("s_fr", [128, 16, 64]); fi = P.sbuf("s_fi", [128, 16, 64])
                    kb.TT(den[:], lr[:], lr[:], ALU.mult, [lr], [den])
                    kb.TT(t2[:], li[:], li[:], ALU.mult, [li], [t2])
                    kb.TT(den[:], den[:], t2[:], ALU.add, [den, t2], [den])
                    kb.RECIP(den[:], den[:], [den], [den])
                    kb.TS(ar[:], ar[:], -1.0, None, ALU.add, None, [ar], [ar])
                    kb.TT(fr[:], ar[:], lr[:], ALU.mult, [ar, lr], [fr])
                    kb.TT(t2[:], ai[:], li[:], ALU.mult, [ai, li], [t2])
                    kb.TT(fr[:], fr[:], t2[:], ALU.add, [fr, t2], [fr])
                    kb.TT(fr[:], fr[:], den[:], ALU.mult, [fr, den], [fr])
                    kb.TT(fi[:], ai[:], lr[:], ALU.mult, [ai, lr], [fi])
                    kb.TT(t2[:], ar[:], li[:], ALU.mult, [ar, li], [t2])
                    kb.TT(fi[:], fi[:], t2[:], ALU.subtract, [fi, t2], [fi])
                    kb.TT(fi[:], fi[:], den[:], ALU.mult, [fi, den], [fi])
                    jcol = kb.C("CCOL")[:, 2:3] if d == 0 else kb.C("CCOL")[:, 3:4]
                    njcol = kb.C("CCOL")[:, 6:7] if d == 0 else kb.C("CCOL")[:, 7:8]
                    kb.ACT(mag[:], lrdt[:], AF.Exp, [lrdt], [mag], scale=njcol)
                    kb.TS(ang[:], lidt[:], jcol, None, ALU.mult, None, [lidt], [ang])
                    _sincos(kb, ang[:], sn[:], cs[:], [ang, sn, cs, tmp], tmp[:])
                    vr, vi = ar, ai
                    kb.TT(vr[:], mag[:], cs[:], ALU.mult, [mag, cs], [vr])
                    kb.TT(vi[:], mag[:], sn[:], ALU.mult, [mag, sn], [vi])
                    kb.TS(vi[:], vi[:], -1.0, None, ALU.mult, None, [vi], [vi])
                    kb.TT(VFr[:], vr[:], fr[:], ALU.mult, [vr, fr], [VFr])
                    kb.TT(t2[:], vi[:], fi[:], ALU.mult, [vi, fi], [t2])
                    kb.TT(VFr[:], VFr[:], t2[:], ALU.subtract, [VFr, t2], [VFr])
                    kb.TT(VFi[:], vr[:], fi[:], ALU.mult, [vr, fi], [VFi])
                    kb.TT(t2[:], vi[:], fr[:], ALU.mult, [vi, fr], [t2])
                    kb.TT(VFi[:], VFi[:], t2[:], ALU.add, [VFi, t2], [VFi])
                with P.scope():
                    dtb = P.sbuf("s_dt2", [128, 16])
                    kb.LD(dtb[:], prm["s5_log_dt"][l][d].partition_broadcast(128), [dtb])
                    kb.ACT(dtb[:], dtb[:], AF.Exp, [dtb], [dtb])
                    lrp = P.sbuf("s_lrp", [128, 16]); lip = P.sbuf("s_lip", [128, 16])
                    for hh in range(2):
                        kb.LD(lrp[64 * hh:64 * hh + 64, :], prm["s5_lam_re"][l][d].rearrange("g p -> p g"), [lrp],
                              allow_slow_non_contiguous=True)
                        kb.LD(lip[64 * hh:64 * hh + 64, :], prm["s5_lam_im"][l][d].rearrange("g p -> p g"), [lip],
                              allow_slow_non_contiguous=True)
                    kb.TT(lrp[:], lrp[:], dtb[:], ALU.mult, [lrp, dtb], [lrp])
                    kb.TT(lip[:], lip[:], dtb[:], ALU.mult, [lip, dtb], [lip])
                    b4 = [P.sbuf("s_b%d" % i, [128, 16, 128]) for i in range(4)]
                    arg, sn2, cs2, tmp2 = b4
                    mt = kb.C("IOTAF" if d == 0 else "R127F")
                    mt_bc = mt.unsqueeze(1).to_broadcast([128, 16, 128])
                    kb.TT(arg[:], lrp[:].unsqueeze(2).to_broadcast([128, 16, 128]), mt_bc, ALU.mult, [lrp], [arg])
                    kb.ACT(T1[:], arg[:], AF.Exp, [arg], [T1])
                    kb.TT(arg[:], lip[:].unsqueeze(2).to_broadcast([128, 16, 128]), mt_bc, ALU.mult, [lip, T1], [arg])
                    _sincos(kb, arg[:], sn2[:], cs2[:], [arg, sn2, cs2, tmp2], tmp2[:])
                    kb.TT(T2[:], T1[:], sn2[:], ALU.mult, [T1, sn2], [T2])
                    kb.TS(T2[:], T2[:], -1.0, None, ALU.mult, None, [T2], [T2])
                    kb.TT(T1[:], T1[:], cs2[:], ALU.mult, [T1, cs2], [T1])
                    c4 = [P.sbuf("s_c%d" % i, [128, 16]) for i in range(4)]
                    kb.ACT(c4[0][:], lrp[:], AF.Exp, [lrp], [c4[0]])
                    _sincos(kb, lip[:], c4[1][:], c4[2][:], [lip, c4[1], c4[2], c4[3]], c4[3][:])
                    kb.TT(AR[:], c4[0][:], c4[2][:], ALU.mult, [c4[0], c4[2]], [AR])
                    kb.TT(NAI[:], c4[0][:], c4[1][:], ALU.mult, [c4[0], c4[1]], [NAI])
                    kb.TS(NAI[:], NAI[:], -1.0, None, ALU.mult, None, [NAI], [NAI])
                sweep_scope = P.scope(); sweep_scope.__enter__()
                uTb = [P.sbuf("s_u%d" % i, [128, 2, 128]) for i in range(2)]
                mm_ = [P.sbuf("s_m%d" % i, [128, 8, 64]) for i in range(4)]
                W3 = [P.sbuf("s_W3%d" % i, [128, 8, 3, 64]) for i in range(2)]
                tP = [P.sbuf("s_tP%d" % i, [128, 8, 128]) for i in range(2)]
                tPs = [P.sbuf("s_tPs%d" % i, [128, 8, 128]) for i in range(2)]
                H1 = [P.sbuf("s_H1%d" % i, [128, 8, 128]) for i in range(2)]
                H2 = [P.sbuf("s_H2%d" % i, [128, 8, 128]) for i in range(2)]
                hend = P.sbuf("s_hend", [128, 16]); hsend = P.sbuf("s_hsend", [128, 16])
                hp_ = P.sbuf("s_hp", [128, 16]); hps_ = P.sbuf("s_hps", [128, 16])
                sm = [P.sbuf("s_sm%d" % i, [128, 16]) for i in range(4)]
                xps = P.psum("s_xps", [128, 1024])
                pps = P.psum("s_pps", [128, 8, 128])
                ppss = P.psum("s_ppss", [128, 8, 128])
                yps = [P.psum("s_yps%d" % i, [128, 128]) for i in range(2)]
                kb.MS(hp_[:], 0.0, [hp_]); kb.MS(hps_[:], 0.0, [hps_])
                te = 127 if d == 0 else 0
                tri = kb.C("TRIF" if d == 0 else "TRIB")
                it = 0
                for n in ORDER[d]:
                    cols = slice(n * 128, (n + 1) * 128)
                    uT = uTb[it % 2]; it += 1
                    kb.LD(uT[:], kb.ZF[Z_SU:Z_SU + 256, cols].rearrange("(gg p) t -> p gg t", p=128), [uT])
                    for gg in range(2):
                        j = gg
                        for half in range(2):
                            kb.MM(xps[:, half * 512:(half + 1) * 512], uT[:, gg, :],
                                  WX[:, gg, half * 4:(half + 1) * 4, :, :].rearrange("q a r p -> q (a r p)"),
                                  True, True, [uT, WX], [xps])
                        xv = xps[:].rearrange("t (g r p) -> t g r p", r=2, p=64)
                        gs = slice(gg * 8, gg * 8 + 8)
                        kb.TT(mm_[0][:], xv[:, :, 0, :], VFr[:, gs, :], ALU.mult, [xps, VFr], [mm_[0]])
                        kb.TT(mm_[1][:], xv[:, :, 1, :], VFi[:, gs, :], ALU.mult, [xps, VFi], [mm_[1]])
                        kb.TT(mm_[2][:], xv[:, :, 0, :], VFi[:, gs, :], ALU.mult, [xps, VFi], [mm_[2]])
                        kb.TT(mm_[3][:], xv[:, :, 1, :], VFr[:, gs, :], ALU.mult, [xps, VFr], [mm_[3]])
                        w3 = W3[j]
                        kb.TT(w3[:, :, 0, :], mm_[0][:], mm_[1][:], ALU.subtract, [mm_[0], mm_[1]], [w3], eng="pool")
                        kb.TT(w3[:, :, 1, :], mm_[2][:], mm_[3][:], ALU.add, [mm_[2], mm_[3]], [w3], eng="pool")
                        kb.TT(w3[:, :, 2, :], mm_[1][:], mm_[0][:], ALU.subtract, [mm_[0], mm_[1]], [w3], eng="pool")
                        for g8 in range(8):
                            kb.MM(pps[:, g8, :], w3[:, g8, 0:2, :].rearrange("q r p -> q (r p)"), tri, True, True, [w3], [pps])
                            kb.MM(ppss[:, g8, :], w3[:, g8, 1:3, :].rearrange("q r p -> q (r p)"), tri, True, True, [w3], [ppss])
                        kb.TT(tP[j][:], pps[:], hp_[:, gs].unsqueeze(2).to_broadcast([128, 8, 128]), ALU.add, [pps, hp_], [tP[j]])
                        kb.TT(tPs[j][:], ppss[:], hps_[:, gs].unsqueeze(2).to_broadcast([128, 8, 128]), ALU.add,
                              [ppss, hps_], [tPs[j]])
                        kb.TT(H1[j][:], tP[j][:], T1[:, gs, :], ALU.mult, [tP[j], T1], [H1[j]], eng="pool")
                        kb.TT(H2[j][:], tPs[j][:], T2[:, gs, :], ALU.mult, [tPs[j], T2], [H2[j]])
                        kb.TT(H1[j][:], H1[j][:], H2[j][:], ALU.add, [H1[j], H2[j]], [H1[j]], eng="pool")
                        yp = yps[gg]
                        for g8 in range(8):
                            kb.MM(yp[:], Cblk[:, gg * 8 + g8, :], H1[j][:, g8, :], g8 == 0, g8 == 7, [Cblk, H1[j]], [yp])
                        oacc_write(kb, OACC, gg, n, yp, d)
                        kb.CP(hend[:, gs], H1[j][:, :, te], [H1[j]], [hend])
                        kb.TT(sm[0][:, 0:8], tPs[j][:, :, te], T1[:, gs, te], ALU.mult, [tPs[j], T1], [sm[0]])
                        kb.TT(sm[1][:, 0:8], tP[j][:, :, te], T2[:, gs, te], ALU.mult, [tP[j], T2], [sm[1]])
                        kb.TT(hsend[:, gs], sm[0][:, 0:8], sm[1][:, 0:8], ALU.subtract, [sm[0], sm[1]], [hsend])
                    kb.TT(sm[0][:], hend[:], AR[:], ALU.mult, [hend, AR], [sm[0]])
                    kb.TT(sm[1][:], hsend[:], NAI[:], ALU.mult, [hsend, NAI], [sm[1]])
                    kb.TT(sm[2][:], hsend[:], AR[:], ALU.mult, [hsend, AR], [sm[2]])
                    kb.TT(sm[3][:], hend[:], NAI[:], ALU.mult, [hend, NAI], [sm[3]])
                    kb.TT(hp_[:], sm[0][:], sm[1][:], ALU.add, [sm[0], sm[1]], [hp_])
                    kb.TT(hps_[:], sm[2][:], sm[3][:], ALU.subtract, [sm[2], sm[3]], [hps_])
                sweep_scope.__exit__(None, None, None)
        with P.scope():
            dsk = P.sbuf("s_dsk", [128, 2]); glb = P.sbuf("s_glb", [128, 2])
            kb.LD(dsk[:], prm["s5_d"][l].rearrange("(gg p) -> p gg", p=128), [dsk], allow_slow_non_contiguous=True)
            kb.LD(glb[:], prm["s5_glu_b"][l].rearrange("(gg p) -> p gg", p=128), [glb], allow_slow_non_contiguous=True)
            gw = P.sbuf("s_gw", [128, 2, 256])
            kb.LD(gw[:], prm["s5_glu_w"][l].rearrange("(ct p) o -> p ct o", p=128), [gw])
            uTb = [P.sbuf("s_fu%d" % i, [128, 2, 128]) for i in range(2)]
            yy = [P.sbuf("s_yy%d" % i, [128, 2, 128]) for i in range(2)]
            x2 = [P.sbuf("s_x2%d" % i, [128, 2, 128]) for i in range(2)]
            th = [P.sbuf("s_th%d" % i, [128, 2, 128]) for i in range(2)]
            sgb = [P.sbuf("s_sg%d" % i, [128, 128]) for i in range(2)]
            ob = [P.sbuf("s_ob%d" % i, [128, 128]) for i in range(2)]
            psz = [P.psum("s_psz%d" % i, [128, 128]) for i in range(2)]
            k = 0
            for n in range(NT):
                cols = slice(n * 128, (n + 1) * 128)
                i = n % 2
                kb.LD(uTb[i][:], kb.ZF[Z_SU:Z_SU + 256, cols].rearrange("(gg p) t -> p gg t", p=128), [uTb[i]])
                for gg in range(2):
                    kb.STT(yy[i][:, gg, :], uTb[i][:, gg, :], dsk[:, gg:gg + 1], OACC[:, gg, cols], ALU.mult, ALU.add,
                           [uTb[i], dsk, OACC.s(n)], [yy[i]])
                kb.TT(x2[i][:], yy[i][:], yy[i][:], ALU.mult, [yy[i]], [x2[i]], eng="pool")
                kb.TS(x2[i][:], x2[i][:], 0.044715, 1.0, ALU.mult, ALU.add, [x2[i]], [x2[i]])
                kb.TT(x2[i][:], x2[i][:], yy[i][:], ALU.mult, [x2[i], yy[i]], [x2[i]], eng="pool")
                kb.ACT(th[i][:], x2[i][:], AF.Tanh, [x2[i]], [th[i]], scale=0.7978845608028654)
                kb.TS(th[i][:], th[i][:], 1.0, 0.5, ALU.add, ALU.mult, [th[i]], [th[i]])
                kb.TT(yy[i][:], yy[i][:], th[i][:], ALU.mult, [yy[i], th[i]], [yy[i]], eng="pool")
                for ot in range(2):
                    q = k % 2; k += 1
                    for ct in range(2):
                        kb.MM(psz[q][:], gw[:, ct, ot * 128:(ot + 1) * 128], yy[i][:, ct, :], ct == 0, ct == 1, [gw, yy[i]], [psz[q]])
                    kb.ACT(sgb[q][:], psz[q][:], AF.Sigmoid, [psz[q], glb], [sgb[q]], bias=glb[:, ot:ot + 1])
                    kb.TT(ob[q][:], yy[i][:, ot, :], sgb[q][:], ALU.mult, [yy[i], sgb[q]], [ob[q]])
                    kb.ST(kb.YC[768 + ot * 128:768 + (ot + 1) * 128, cols], ob[q][:], [ob[q]])


def gdn_conv(kb, l):
    P = kb.P
    with P.scope():
        CW = P.sbuf("g_cw", [128, 6, 9])
        for kh in range(3):
            for kw in range(3):
                kb.LD(CW[:, :, kh * 3 + kw], kb.prm["gdn_conv_w"][l][kh, kw].rearrange("(ct p) -> p ct", p=128), [CW],
                      allow_slow_non_contiguous=True)
        mlat = P.sbuf("g_mlat", [128, 2, 512]); mctx = P.sbuf("g_mctx", [128, 2, 256])
        kb.LD(mlat[:], kb.cmlat[:], [mlat]); kb.LD(mctx[:], kb.cmctx[:], [mctx])
        Wb = [P.sbuf("g_w%d" % i, [128, 642]) for i in range(2)]
        acc = [[P.sbuf("g_acc%d%d" % (i, j), [128, 512]) for j in range(3)] for i in range(2)]
        sl = [P.sbuf("g_sl%d" % i, [128, 512]) for i in range(2)]
        sq = [P.sbuf("g_sq%d" % i, [128, 512]) for i in range(2)]
        rt = [P.sbuf("g_rt%d" % i, [128, 512]) for i in range(2)]
        ps = [P.psum("g_psn%d" % i, [128, 512]) for i in range(2)]
        spans = [(0, 256, True)] + [(256 + 512 * k, 512, False) for k in range(8)]
        it = 0
        for (t0, L, is_ctx) in spans:
            lo = 0 if is_ctx else 256
            hi = 256 if is_ctx else S
            a = max(lo, t0 - 65); b = min(hi, t0 + L + 65)
            for ct in range(6):
                i = it % 2; it += 1
                W = Wb[i]
                kb.MS(W[:], 0.0, [W], eng="pool")
                kb.LD(W[:, 65 + (a - t0):65 + (b - t0)], kb.ZF[Z_GQKV + ct * 128:Z_GQKV + (ct + 1) * 128, a:b], [W])
                rows = (1,) if is_ctx else (0, 1, 2)
                masks = mctx if is_ctx else mlat
                for dwi, shift in enumerate((-1, 0, 1)):
                    A = acc[i][dwi]
                    eng = "dve"
                    for q, dh in enumerate(rows):
                        o0 = 65 + 64 * (dh - 1) + shift
                        src = W[:, o0:o0 + L]
                        wcol = CW[:, ct, dh * 3 + dwi:dh * 3 + dwi + 1]
                        if q == 0:
                            kb.TS(A[:, :L], src, wcol, None, ALU.mult, None, [W, CW], [A], eng=("pool" if dwi != 1 else "dve"))
                        else:
                            kb.STT(A[:, :L], src, wcol, A[:, :L], ALU.mult, ALU.add, [W, CW, A], [A])
                    if dwi != 1:
                        mi = 0 if dwi == 0 else 1
                        kb.TT(A[:, :L], A[:, :L], masks[:, mi, :L], ALU.mult, [A, masks], [A], eng="pool")
                A0, A1, A2 = acc[i]
                kb.TT(A1[:, :L], A1[:, :L], A0[:, :L], ALU.add, [A0, A1], [A1], eng="pool")
                kb.TT(A1[:, :L], A1[:, :L], A2[:, :L], ALU.add, [A1, A2], [A1], eng="pool")
                kb.ACT(sl[i][:, :L], A1[:, :L], AF.Silu, [A1], [sl[i]])
                if ct < 4:
                    kb.TT(sq[i][:, :L], sl[i][:, :L], sl[i][:, :L], ALU.mult, [sl[i]], [sq[i]], eng="pool")
                    kb.MM(ps[i][:, :L], kb.C("BLK64"), sq[i][:, :L], True, True, [sq[i]], [ps[i]])
                    kb.ACT(rt[i][:, :L], ps[i][:, :L], AF.Sqrt, [ps[i]], [rt[i]], bias=kb.C("CCOL")[:, 0:1])
                    kb.RECIP(rt[i][:, :L], rt[i][:, :L], [rt[i]], [rt[i]])
                    if ct < 2:
                        kb.STT(sl[i][:, :L], sl[i][:, :L], 0.125, rt[i][:, :L], ALU.mult, ALU.mult, [sl[i], rt[i]], [sl[i]])
                    else:
                        kb.TT(sl[i][:, :L], sl[i][:, :L], rt[i][:, :L], ALU.mult, [sl[i], rt[i]], [sl[i]])
                kb.ST(kb.QKVF[ct * 128:(ct + 1) * 128, t0:t0 + L], sl[i][:, :L], [sl[i]])


def mixer_gdn(kb, l):
    P = kb.P
    gdn_conv(kb, l)
    upto = kb.cfg.get("gdn_upto", 99)
    if upto < 1:
        return
    with P.scope():
        OACC = P.sbuf("g_oacc", [128, 2, S])
        with P.scope():
            DTB = P.sbuf("g_dtb", [128, 8]); NEGA = P.sbuf("g_nega", [128, 8])
            kb.LD(DTB[:], kb.prm["gdn_dt_bias"][l].rearrange("d h -> (d h)").partition_broadcast(128), [DTB])
            kb.LD(NEGA[:], kb.prm["gdn_a_log"][l].rearrange("d h -> (d h)").partition_broadcast(128), [NEGA])
            kb.ACT(NEGA[:], NEGA[:], AF.Exp, [NEGA], [NEGA])
            kb.TS(NEGA[:], NEGA[:], -1.0, None, ALU.mult, None, [NEGA], [NEGA])
            qnb = [P.sbuf("g_q%d" % i, [128, 2, 128]) for i in range(2)]
            knb = [P.sbuf("g_k%d" % i, [128, 2, 128]) for i in range(2)]
            vvb = [P.sbuf("g_v%d" % i, [128, 2, 128]) for i in range(2)]
            gabb = [P.sbuf("g_gab%d" % i, [128, 16]) for i in range(2)]

            def sm4(name, w=4):
                return P.sbuf("g_" + name, [128, w])
            xa, ea, loga, beta, lnb = sm4("xa"), sm4("ea"), sm4("loga"), sm4("beta"), sm4("lnb")
            gtm, ngt, ekr, cdec, eg, beg, gpl = sm4("gtm"), sm4("ngt"), sm4("ekr"), sm4("cdec"), sm4("eg"), sm4("beg"), sm4("gpl")
            ROWS = P.sbuf("g_rows", [4, 384])
            LI = P.sbuf("g_LI", [128, 4, 128]); LBT = P.sbuf("g_LBT", [128, 4, 128]); LBm = P.sbuf("g_LB", [128, 4, 128])
            NAT = P.sbuf("g_NAT", [128, 4, 128]); NA = P.sbuf("g_NA", [128, 4, 128]); QKm = P.sbuf("g_QKm", [128, 4, 128])
            Tm = P.sbuf("g_Tm", [128, 4, 128]); Wm = P.sbuf("g_Wm", [128, 4, 128])
            x1 = P.sbuf("g_x1", [128, 4, 128]); y1 = P.sbuf("g_y1", [128, 4, 128])
            tmx = P.sbuf("g_tmx", [128, 4, 128]); tmy = P.sbuf("g_tmy", [128, 4, 128])
            Rm = [P.sbuf("g_R%d" % h, [128, 128]) for h in range(4)]
            khp = [P.sbuf("g_kh%d" % h, [128, 128]) for h in range(4)]
            vnp = [P.sbuf("g_vn%d" % h, [128, 128]) for h in range(4)]
            for h in range(4):
                kb.MS(khp[h][:], 0.0, [khp[h]], eng="pool")
                kb.MS(vnp[h][:], 0.0, [vnp[h]], eng="pool")
            upair = [P.sbuf("g_up%d" % hp, [128, 128]) for hp in range(2)]
            wTp = [P.sbuf("g_wT%d" % hp, [128, 128]) for hp in range(2)]
            EG = [P.sbuf("g_EG%d" % hp, [128, 128]) for hp in range(2)]
            qd = [P.sbuf("g_qd%d" % hp, [128, 128]) for hp in range(2)]
            cdp = [P.sbuf("g_cdp%d" % hp, [128, 1]) for hp in range(2)]
            Sb = [P.sbuf("g_S%d" % hp, [128, 128]) for hp in range(2)]
            B = [P.psum("g_B%d" % i, [128, 512]) for i in range(8)]
            ident = kb.C("IDENT")
            it = 0
            for d in range(2):
                tri = kb.C("TRIF" if d == 0 else "TRIB")
                rem = kb.C("SUFF" if d == 0 else "PREB")
                n_incl = kb.C("NLE" if d == 0 else "NGE")
                n_strT = kb.C("NLT" if d == 0 else "NGT")
                n_str = kb.C("NGT" if d == 0 else "NLT")
                for hp in range(2):
                    kb.MS(Sb[hp][:], 0.0, [Sb[hp]])
                for n in ORDER[d][:kb.cfg.get("ntiles", NT)]:
                    cols = slice(n * 128, (n + 1) * 128)
                    b = it % 2; it += 1
                    qn, kn, vv, gab = qnb[b], knb[b], vvb[b], gabb[b]
                    kb.LD(qn[:], kb.QKVF[0:256, cols].rearrange("(hp p) t -> p hp t", p=128), [qn])
                    kb.LD(kn[:], kb.QKVF[256:512, cols].rearrange("(hp p) t -> p hp t", p=128), [kn])
                    kb.LD(vv[:], kb.QKVF[512:768, cols].rearrange("(hp p) t -> p hp t", p=128), [vv])
                    kb.LD(gab[:], kb.ZT[cols, 512:528], [gab])
                    kb.TT(xa[:], gab[:, 4 * d:4 * d + 4], DTB[:, 4 * d:4 * d + 4], ALU.add, [gab, DTB], [xa])
                    kb.ACT(ea[:], xa[:], AF.Exp, [xa], [ea])
                    kb.ACT(ea[:], ea[:], AF.Ln, [ea], [ea], bias=kb.C("CCOL")[:, 1:2])
                    kb.TT(loga[:], ea[:], NEGA[:, 4 * d:4 * d + 4], ALU.mult, [ea, NEGA], [loga])
                    kb.ACT(beta[:], gab[:, 8 + 4 * d:12 + 4 * d], AF.Sigmoid, [gab], [beta])
                    kb.ACT(lnb[:], beta[:], AF.Ln, [beta], [lnb])
                    kb.MM(B[0][:, 0:4], tri, loga[:], True, True, [loga], [B[0]])
                    kb.MM(B[0][:, 4:8], rem, loga[:], True, True, [loga], [B[0]])
                    kb.MM(B[0][:, 8:12], kb.C("ONES"), loga[:], True, True, [loga], [B[0]])
                    kb.CP(gtm[:], B[0][:, 0:4], [B[0]], [gtm])
                    kb.TS(ngt[:], B[0][:, 0:4], -1.0, None, ALU.mult, None, [B[0]], [ngt])
                    kb.ACT(ekr[:], B[0][:, 4:8], AF.Exp, [B[0]], [ekr])
                    kb.ACT(cdec[:], B[0][:, 8:12], AF.Exp, [B[0]], [cdec])
                    kb.ACT(eg[:], gtm[:], AF.Exp, [gtm], [eg])
                    kb.TT(beg[:], beta[:], eg[:], ALU.mult, [beta, eg], [beg])
                    kb.TT(gpl[:], gtm[:], lnb[:], ALU.add, [gtm, lnb], [gpl])
                    kb.MM(B[1][0:4, 0:128], loga[:], tri, True, True, [loga], [B[1]])
                    kb.MM(B[1][0:4, 128:256], loga[:], tri, True, False, [loga], [B[1]])
                    kb.MM(B[1][0:4, 128:256], lnb[:], ident, False, True, [lnb], [B[1]])
                    kb.CP(ROWS[:, 0:256], B[1][0:4, 0:256], [B[1]], [ROWS])
                    kb.TS(ROWS[:, 256:384], B[1][0:4, 0:128], -1.0, None, ALU.mult, None, [B[1]], [ROWS])
                    if upto < 2:
                        continue
                    for (dst, rsl, negm, bias_t, bank) in ((LI, slice(0, 128), n_incl, ngt, B[2]),
                                                           (LBT, slice(128, 256), n_strT, ngt, B[3]),
                                                           (LBm, slice(256, 384), n_str, gpl, B[2])):
                        for h in range(4):
                            kb.MM(bank[:, h * 128:(h + 1) * 128], kb.C("SELH%d" % h)[0:4, :], ROWS[:, rsl], True, False,
                                  [ROWS], [bank])
                            kb.MM(bank[:, h * 128:(h + 1) * 128], ident, negm, False, True, [], [bank])
                        for h in range(4):
                            kb.ACT(dst[:, h, :], bank[:, h * 128:(h + 1) * 128], AF.Exp, [bank, bias_t], [dst],
                                   bias=bias_t[:, h:h + 1])
                    if upto < 3:
                        continue
                    for h in range(4):
                        hp, h2 = divmod(h, 2)
                        ksl = kn[64 * h2:64 * h2 + 64, hp, :]
                        kb.MM(B[4][:, h * 128:(h + 1) * 128], ksl, ksl, True, True, [kn], [B[4]])
                        kb.MM(B[5][:, h * 128:(h + 1) * 128], ksl, qn[64 * h2:64 * h2 + 64, hp, :], True, True, [kn, qn], [B[5]])
                    b4v = B[4][:].rearrange("p (h t) -> p h t", h=4)
                    b5v = B[5][:].rearrange("p (h t) -> p h t", h=4)
                    kb.STT(NAT[:], b4v, -1.0, LBT[:], ALU.mult, ALU.mult, [B[4], LBT], [NAT])
                    kb.STT(NA[:], b4v, -1.0, LBm[:], ALU.mult, ALU.mult, [B[4], LBm], [NA])
                    kb.TT(QKm[:], b5v, LI[:], ALU.mult, [B[5], LI], [QKm])
                    if upto < 4:
                        continue
                    idb = ident.unsqueeze(1).to_broadcast([128, 4, 128])
                    kb.CP(Tm[:], idb, [], [Tm])
                    kb.CP(Wm[:], idb, [], [Wm], eng="pool")
                    for s_ in (1, 2, 4, 8, 16, 32, 64):
                        mT = kb.C(("MOFF%d" if d == 0 else "MOFFT%d") % s_).unsqueeze(1).to_broadcast([128, 4, 128])
                        mW = kb.C(("MOFFT%d" if d == 0 else "MOFF%d") % s_).unsqueeze(1).to_broadcast([128, 4, 128])
                        for h in range(4):
                            kb.MM(B[2][:, h * 128:(h + 1) * 128], NAT[:, h, :], Tm[:, h, :], True, True, [NAT, Tm], [B[2]])
                        for h in range(4):
                            kb.MM(B[3][:, h * 128:(h + 1) * 128], NA[:, h, :], Wm[:, h, :], True, True, [NA, Wm], [B[3]])
                        kb.CP(x1[:], B[2][:].rearrange("p (h t) -> p h t", h=4), [B[2]], [x1], eng="act")
                        kb.CP(y1[:], B[3][:].rearrange("p (h t) -> p h t", h=4), [B[3]], [y1], eng="dve")
                        for h in range(4):
                            kb.MM(B[4][:, h * 128:(h + 1) * 128], Wm[:, h, :], x1[:, h, :], True, True, [Wm, x1], [B[4]])
                        for h in range(4):
                            kb.MM(B[5][:, h * 128:(h + 1) * 128], Tm[:, h, :], y1[:, h, :], True, True, [Tm, y1], [B[5]])
                        kb.TT(tmx[:], B[4][:].rearrange("p (h t) -> p h t", h=4), mT, ALU.mult, [B[4]], [tmx])
                        kb.TT(tmy[:], B[5][:].rearrange("p (h t) -> p h t", h=4), mW, ALU.mult, [B[5]], [tmy])
                        kb.TT(Tm[:], Tm[:], tmx[:], ALU.add, [Tm, tmx], [Tm], eng="pool")
                        kb.TT(Wm[:], Wm[:], tmy[:], ALU.add, [Wm, tmy], [Wm], eng="pool")
                    if upto < 5:
                        continue
                    for hp in range(2):
                        kb.TR(B[0][:, 128:256], kn[:, hp, :], ident, [kn], [B[0]])
                        kb.TR(B[0][:, 256:384], vv[:, hp, :], ident, [vv], [B[0]])
                        for h2 in range(2):
                            h = 2 * hp + h2
                            kc = slice(64 * h2, 64 * h2 + 64)
                            vc = slice(64 * (1 - h2), 64 * (1 - h2) + 64)
                            kb.TS(Rm[h][:, kc], B[0][:, 128 + 64 * h2:128 + 64 * h2 + 64], beg[:, h:h + 1], None, ALU.mult, None,
                                  [B[0], beg], [Rm[h]])
                            kb.ACT(Rm[h][:, vc], B[0][:, 256 + 64 * h2:256 + 64 * h2 + 64], AF.Copy, [B[0], beta], [Rm[h]],
                                   scale=beta[:, h:h + 1])
                            kb.ACT(khp[h][:, kc], B[0][:, 128 + 64 * h2:128 + 64 * h2 + 64], AF.Copy, [B[0], ekr], [khp[h]],
                                   scale=ekr[:, h:h + 1])
                    if upto < 6:
                        continue
                    for h in range(4):
                        kb.MM(B[2][:, h * 128:(h + 1) * 128], Wm[:, h, :], Rm[h][:], True, True, [Wm, Rm[h]], [B[2]])
                        kb.MM(B[3][:, h * 128:(h + 1) * 128], Rm[h][:], Wm[:, h, :], True, True, [Wm, Rm[h]], [B[3]])
                    for h in range(4):
                        hp, h2 = divmod(h, 2)
                        vc0 = 64 * (1 - h2)
                        kb.CP(upair[hp][:, 64 * h2:64 * h2 + 64], B[2][:, h * 128 + vc0:h * 128 + vc0 + 64], [B[2]], [upair[hp]],
                              eng=("act" if h2 else "dve"))
                        kb.CP(wTp[hp][64 * h2:64 * h2 + 64, :], B[3][64 * h2:64 * h2 + 64, h * 128:(h + 1) * 128], [B[3]], [wTp[hp]],
                              eng=("dve" if h2 else "act"))
                    if upto < 7:
                        continue
                    for hp in range(2):
                        kb.MM(B[1][:, 256:384], kb.C("SELP%d" % hp)[0:4, :], ROWS[:, 0:128], True, True, [ROWS], [B[1]])
                        kb.ACT(EG[hp][:], B[1][:, 256:384], AF.Exp, [B[1]], [EG[hp]])
                        kb.TT(qd[hp][:], qn[:, hp, :], EG[hp][:], ALU.mult, [qn, EG[hp]], [qd[hp]], eng="pool")
                        pws = B[7][:, hp * 128:(hp + 1) * 128]
                        kb.MM(pws, wTp[hp][:], Sb[hp][:], True, True, [wTp[hp], Sb[hp]], [B[7]])
                        for h2 in range(2):
                            h = 2 * hp + h2
                            cs_ = slice(64 * h2, 64 * h2 + 64)
                            kb.TT(vnp[h][:, cs_], upair[hp][:, cs_], B[7][:, hp * 128 + 64 * h2:hp * 128 + 64 * h2 + 64],
                                  ALU.subtract, [upair[hp], B[7]], [vnp[h]])
                        po = B[6][:, hp * 256:hp * 256 + 128]
                        kb.MM(po, Sb[hp][:], qd[hp][:], True, False, [Sb[hp], qd[hp]], [B[6].s(hp)])
                        kb.MM(po, vnp[2 * hp][:], QKm[:, 2 * hp, :], False, False, [vnp[2 * hp], QKm], [B[6].s(hp)])
                        kb.MM(po, vnp[2 * hp + 1][:], QKm[:, 2 * hp + 1, :], False, True, [vnp[2 * hp + 1], QKm], [B[6].s(hp)])
                        cols_ = slice(n * 128, (n + 1) * 128)
                        if d == 0:
                            kb.CP(OACC[:, hp, cols_], po, [B[6].s(hp)], [OACC.s(n)], eng="act")
                        else:
                            kb.TT(OACC[:, hp, cols_], OACC[:, hp, cols_], po, ALU.add, [B[6].s(hp)], [OACC.s(n)])
                        pkv = B[6][:, hp * 256 + 128:hp * 256 + 256]
                        kb.MM(pkv, khp[2 * hp][:], vnp[2 * hp][:], True, False, [khp[2 * hp], vnp[2 * hp]], [B[6].s(2 + hp)])
                        kb.MM(pkv, khp[2 * hp + 1][:], vnp[2 * hp + 1][:], False, True, [khp[2 * hp + 1], vnp[2 * hp + 1]],
                              [B[6].s(2 + hp)])
                        kb.CP(cdp[hp][0:64, :], cdec[0:64, 2 * hp:2 * hp + 1], [cdec], [cdp[hp]])
                        kb.CP(cdp[hp][64:128, :], cdec[64:128, 2 * hp + 1:2 * hp + 2], [cdec], [cdp[hp]])
                        kb.STT(Sb[hp][:], Sb[hp][:], cdp[hp][:, 0:1], pkv, ALU.mult, ALU.add,
                               [Sb[hp], cdp[hp], B[6].s(2 + hp)], [Sb[hp]])
        with P.scope():
            G = P.sbuf("g_G2", [128, 1])
            for hh in range(2):
                kb.LD(G[64 * hh:64 * hh + 64, :], kb.prm["gdn_norm_g"][l].rearrange("(p o) -> p o", o=1), [G])
            finalize_gated(kb, OACC, Z_GG, G, 512, "g_")
```

```python
import numpy as np
import concourse.bass as bass
import concourse.mybir as mybir
from concourse.bass_utils import run_bass_kernel_spmd
from contextlib import ExitStack

F32 = mybir.dt.float32
BF16 = mybir.dt.bfloat16
AF = mybir.ActivationFunctionType
ALU = mybir.AluOpType

ENGS = ("pe", "act", "dve", "pool", "sp")
EPOCH = 16000
N_DMA_SEM = 32


class Buf:
    __slots__ = ("name", "w", "r", "excl", "pe_partial")

    def __init__(self, name="", excl=False):
        self.name = name
        self.w = None
        self.r = []
        self.excl = excl
        self.pe_partial = False


class T:
    def __init__(self, h, name, excl=False):
        self.h = h
        self.name = name
        self.b = Buf(name, excl)
        self.excl = excl
        self.subs = {}

    def __getitem__(self, k):
        return self.h[k]

    def s(self, key):
        if self.excl:
            return self.b
        if key not in self.subs:
            self.subs[key] = Buf("%s.%s" % (self.name, key))
        return self.subs[key]


class Prog:
    def __init__(self, nc):
        self.nc = nc
        self.es = ExitStack()
        self.stack = [self.es]
        self.ops = {e: [] for e in ENGS}
        self.cnt = {e: 0 for e in ENGS}
        self.seen = {e: {} for e in ENGS}
        self.last = {}
        self.dma_k = 0
        self.dma_use = [0] * N_DMA_SEM
        self.dma_sems = [self.es.enter_context(nc.semaphore("dq%d" % i)) for i in range(N_DMA_SEM)]
        self.eng_sems = {}
        self.out_tokens = []
        self.n_ops = 0
        self.uid = 0

    def _nm(self, name):
        self.uid += 1
        return "%s_%d" % (name, self.uid)

    def sbuf(self, name, shape, dt=F32):
        h = self.stack[-1].enter_context(self.nc.sbuf_tensor(self._nm(name), list(shape), dt))
        return T(h, name)

    def psum(self, name, shape, dt=F32):
        n = 1
        for d_ in shape[1:]:
            n *= d_
        nb = (n * 4 + 2047) // 2048
        h = self.stack[-1].enter_context(self.nc.psum_tensor(self._nm(name), [128, nb * 512], F32))
        v = h[0:shape[0], 0:n]
        if len(shape) == 3:
            v = v.rearrange("p (a b) -> p a b", a=shape[1])
        elif len(shape) == 4:
            v = v.rearrange("p (a b c) -> p a b c", a=shape[1], b=shape[2])
        return T(v, name, excl=True)

    def dram(self, name, shape, dt=F32, kind="Internal"):
        h = self.nc.dram_tensor(name, list(shape), dt, kind=kind)
        return T(h.ap(), name)

    class _Scope:
        def __init__(self, p):
            self.p = p

        def __enter__(self):
            st = ExitStack()
            self.p.stack.append(st)
            return st

        def __exit__(self, *a):
            self.p.barrier()
            st = self.p.stack.pop()
            st.close()
            return False

    def scope(self):
        return Prog._Scope(self)

    def _eng_sem(self, e, epoch):
        k = (e, epoch)
        if k not in self.eng_sems:
            self.eng_sems[k] = self.es.enter_context(self.nc.semaphore("s_%s_%d" % (e, epoch)))
        return self.eng_sems[k]

    def _waits(self, eng, reads, writes, extra=(), skip_pe=False):
        need = {}

        def add(tok):
            if tok is None:
                return
            key, val = tok
            if need.get(key, 0) < val:
                need[key] = val
        for b in reads:
            add(b.w)
        for b in writes:
            add(b.w)
            for t in b.r:
                add(t)
        for t in extra:
            add(t)
        out = []
        seen = self.seen[eng]
        for key, val in need.items():
            if skip_pe and key[0] == "e" and key[1] == "pe":
                continue
            if seen.get(key, 0) < val:
                seen[key] = val
                out.append((key, val))
        return out

    @staticmethod
    def _bufs(xs):
        out = []
        for x in xs:
            if x is None:
                continue
            out.append(x.b if isinstance(x, T) else x)
        return out

    def _commit(self, tok, reads, writes):
        self.last[tok[0]] = tok[1]
        for b in reads:
            b.r.append(tok)
            if len(b.r) > 64:
                mx = {}
                for k, v in b.r:
                    if mx.get(k, 0) < v:
                        mx[k] = v
                b.r = list(mx.items())
        for b in writes:
            b.w = tok
            b.r = []
        self.n_ops += 1

    def op(self, eng, fn, reads=(), writes=(), partial=False):
        reads = self._bufs(reads)
        writes = self._bufs(writes)
        ex = [b for b in reads if b.excl]
        if ex:
            reads = [b for b in reads if not b.excl]
            writes = writes + [b for b in ex if b not in writes]
        skip_pe = False
        if eng == "pe":
            skip_pe = (not partial) and all(not b.pe_partial for b in writes)
            for b in writes:
                b.pe_partial = partial
        waits = self._waits(eng, reads, writes, skip_pe=skip_pe)
        self.cnt[eng] += 1
        epoch, val = divmod(self.cnt[eng] - 1, EPOCH)
        tok = (("e", eng, epoch), val + 1)
        self.ops[eng].append((waits, fn, tok))
        self._commit(tok, reads, writes)
        return tok

    def dma(self, out_ap, in_ap, reads=(), writes=(), q="sp", is_output=False, **kw):
        reads = self._bufs(reads)
        writes = self._bufs(writes)
        i = self.dma_k % N_DMA_SEM
        self.dma_k += 1
        prev = self.dma_use[i]
        extra = [(("d", i), 16 * prev)] if prev else []
        waits = self._waits(q, reads, writes, extra)
        self.dma_use[i] = prev + 1
        tok = (("d", i), 16 * (prev + 1))

        def fn(e):
            return e.dma_start(out=out_ap, in_=in_ap, **kw)
        self.ops[q].append((waits, fn, tok))
        self._commit(tok, reads, writes)
        if is_output:
            self.out_tokens.append(tok)
        return tok

    def barrier(self):
        toks = list(self.last.items())
        for e in ENGS:
            waits = self._waits(e, [], [], toks)
            if waits:
                self.ops[e].append((waits, None, None))

    def _sem_of(self, key):
        if key[0] == "d":
            return self.dma_sems[key[1]]
        return self._eng_sem(key[1], key[2])

    def emit(self):
        nc = self.nc
        self.barrier()
        for e in ENGS:
            for waits, fn, tok in self.ops[e]:
                if tok is not None:
                    self._sem_of(tok[0])
                for key, val in waits:
                    self._sem_of(key)
        with nc.Block() as block:
            def run(e, handle):
                for waits, fn, tok in self.ops[e]:
                    for key, val in waits:
                        handle.wait_ge(self._sem_of(key), val)
                    if fn is None:
                        continue
                    ins = fn(handle)
                    key, val = tok
                    ins.then_inc(self._sem_of(key), 16 if key[0] == "d" else 1)

            @block.sync
            def _(h):
                run("sp", h)

            @block.tensor
            def _(h):
                run("pe", h)

            @block.scalar
            def _(h):
                run("act", h)

            @block.vector
            def _(h):
                run("dve", h)

            @block.gpsimd
            def _(h):
                run("pool", h)

    def close(self):
        self.es.close()


D = 1024
S = 4352
NT = 34
LAT0 = 256
DEPTH = 2
EPS = 1e-6
NEG = -30000.0
ORDER = [list(range(NT)), [1, 0] + list(range(NT - 1, 1, -1))]

C_HQ, C_HI, C_HG, C_HFF, C_HFB = 0, 256, 512, 768, 1024
C_RQ, C_RK, C_RV, C_RG = 1280, 1536, 1792, 2048
C_GQKV, C_GG, C_GA, C_GB, C_SU = 2304, 3072, 3328, 3336, 3344
Z_HQ, Z_HG, Z_HFF, Z_HFB, Z_RQ, Z_RK, Z_RG, Z_GQKV, Z_GG, Z_SU = 0, 256, 512, 768, 1024, 1280, 1536, 1792, 2560, 2816
NZF = 3072
FM_MAP = [(Z_HQ, C_HQ, 256), (Z_HG, C_HG, 256), (Z_HFF, C_HFF, 256), (Z_HFB, C_HFB, 256), (Z_RQ, C_RQ, 256),
          (Z_RK, C_RK, 256), (Z_RG, C_RG, 256), (Z_GQKV, C_GQKV, 768), (Z_GG, C_GG, 256), (Z_SU, C_SU, 256)]
FM_BLOCKS = [(zr + i, wc + i) for zr, wc, n in FM_MAP for i in range(0, n, 128)]
NZT = 528

CN = {}


def _const_pack():
    mats = []

    def add(name, m):
        CN[name] = len(mats)
        mats.append(np.asarray(m, np.float32))
    p = np.arange(128)[:, None]
    f = np.arange(128)[None, :]
    add("IDENT", (p == f))
    add("ONES", np.ones((128, 128)))
    add("TRIF", (p <= f))
    add("TRIB", (p >= f))
    add("SUFF", (p > f))
    add("PREB", (p < f))
    add("NLE", np.where(p <= f, 0.0, NEG))
    add("NLT", np.where(p < f, 0.0, NEG))
    add("NGE", np.where(p >= f, 0.0, NEG))
    add("NGT", np.where(p > f, 0.0, NEG))
    for s in (1, 2, 4, 8, 16, 32, 64):
        m = (((p // s) % 2) == 1) & ((f // s) == (p // s) - 1)
        add("MOFF%d" % s, m)
        add("MOFFT%d" % s, m.T)
    add("BLK64", (p // 64) == (f // 64))
    rot = np.zeros((128, 128))
    for m in range(128):
        if (m % 64) < 32:
            rot[m + 32, m] = -1.0
        else:
            rot[m - 32, m] = 1.0
    add("ROT", rot)
    add("IOTAF", np.broadcast_to(f, (128, 128)))
    add("IOTAF1", np.broadcast_to(f + 1, (128, 128)))
    add("RIOTAF", np.broadcast_to(128 - f, (128, 128)))
    add("R127F", np.broadcast_to(127 - f, (128, 128)))
    add("DIFF", f - p)
    add("NDIFF", p - f)
    for h in range(4):
        m = np.zeros((128, 128)); m[h, :] = 1.0
        add("SELH%d" % h, m)
    for hp in range(2):
        m = np.zeros((128, 128)); m[2 * hp, 0:64] = 1.0; m[2 * hp + 1, 64:128] = 1.0
        add("SELP%d" % hp, m)
    cc = np.zeros((128, 128))
    cc[:, 0] = EPS; cc[:, 1] = 1.0; cc[:, 2] = np.arange(128); cc[:, 3] = 127 - np.arange(128)
    cc[:, 5] = -np.pi; cc[:, 6] = -np.arange(128); cc[:, 7] = -(127 - np.arange(128))
    add("CCOL", cc)
    gm = np.zeros((128, 128))
    for g in range(16):
        gm[(g % 8) * 16:(g % 8) * 16 + 16, g] = 1.0
    add("GMASK", gm)
    return np.concatenate(mats, axis=1)


CONST_NP = _const_pack()
NCONST = CONST_NP.shape[1] // 128


def _rope_tables():
    half = 32
    inv = 10000.0 ** (-np.arange(half, dtype=np.float64) / half)
    pos = np.arange(S, dtype=np.float64)
    ang = pos[None, :] * inv[:, None]
    cos = np.cos(ang); sin = np.sin(ang)
    cos128 = np.tile(cos, (4, 1)); sin128 = np.tile(sin, (4, 1))
    return cos128.astype(np.float32), sin128.astype(np.float32)


def _conv_masks():
    m = np.ones((2, 512), np.float32)
    w = np.arange(512) % 64
    m[0, w == 0] = 0.0
    m[1, w == 63] = 0.0
    lat = np.broadcast_to(m[None], (128, 2, 512)).copy()
    c = np.ones((2, 256), np.float32)
    c[0, 0] = 0.0
    c[1, 255] = 0.0
    ctx = np.broadcast_to(c[None], (128, 2, 256)).copy()
    return lat, ctx


class KB:
    def __init__(self, cfg):
        self.cfg = cfg
        nc = bass.Bass("TRN2", target_bir_lowering=False)
        self.nc = nc
        self.P = Prog(nc)
        self.rr = 0

    def MM(self, ps, lhsT, rhs, st, sp, R, W):
        partial = lhsT.partition_size() < 128
        self.P.op("pe", lambda e: e.matmul(ps, lhsT, rhs, start=st, stop=sp), R, W, partial=partial)

    def TR(self, ps, in_, ident, R, W):
        self.P.op("pe", lambda e: e.transpose(ps, in_, ident), R, W)

    def ACT(self, out, in_, func, R, W, **kw):
        self.P.op("act", lambda e: e.activation(out=out, in_=in_, func=func, **kw), R, W)

    def TS(self, out, in0, s1, s2, op0, op1, R, W, eng="dve"):
        if s2 is None:
            self.P.op(eng, lambda e: e.tensor_scalar(out=out, in0=in0, scalar1=s1, scalar2=None, op0=op0), R, W)
        else:
            self.P.op(eng, lambda e: e.tensor_scalar(out=out, in0=in0, scalar1=s1, scalar2=s2, op0=op0, op1=op1), R, W)

    def TT(self, out, in0, in1, op, R, W, eng="dve"):
        self.P.op(eng, lambda e: e.tensor_tensor(out=out, in0=in0, in1=in1, op=op), R, W)

    def STT(self, out, in0, sc, in1, op0, op1, R, W, eng="dve"):
        eng = "dve"
        self.P.op(eng, lambda e: e.scalar_tensor_tensor(out=out, in0=in0, scalar=sc, in1=in1, op0=op0, op1=op1), R, W)

    def CP(self, out, in_, R, W, eng="dve"):
        if eng == "act":
            self.ACT(out, in_, AF.Copy, R, W)
        else:
            self.P.op(eng, lambda e: e.tensor_copy(out=out, in_=in_), R, W)

    def CPRED(self, out, mask, data, R, W):
        self.P.op("dve", lambda e: e.copy_predicated(out=out, mask=mask, data=data), R, W)

    def MS(self, ap, val, W, eng="dve"):
        self.P.op(eng, lambda e: e.memset(ap, val), (), W)

    def RECIP(self, out, in_, R, W):
        self.P.op("dve", lambda e: e.reciprocal(out=out, in_=in_), R, W)

    def SCAN(self, out, d0, d1, R, W):
        self.P.op("dve", lambda e: e.tensor_tensor_scan(out=out, data0=d0, data1=d1, initial=0.0,
                                                        op0=ALU.mult, op1=ALU.add), R, W)

    def LD(self, out, in_, W, R=(), q="sp", **kw):
        self.P.dma(out, in_, reads=R, writes=W, q=q, **kw)

    def ST(self, out, in_, R, W=(), q="pool", **kw):
        self.P.dma(out, in_, reads=R, writes=W, q=q, **kw)

    def evac_eng(self):
        self.rr += 1
        return "act" if self.rr % 2 else "dve"

    def C(self, name):
        i = CN[name]
        return self.const[:, i * 128:(i + 1) * 128]


PARAM_SHAPES = {
    "mod_w": [2, 1024, 6144], "mod_b": [2, 6144], "norm1_g": [2, 1024], "norm2_g": [2, 1024],
    "w_in": [2, 1024, 3600], "hgrn_lb_logits": [2, 2, 256], "hgrn_norm_g": [2, 64],
    "ret_decay_logit": [2, 2, 4], "gdn_conv_w": [2, 3, 3, 768], "gdn_a_log": [2, 2, 4],
    "gdn_dt_bias": [2, 2, 4], "gdn_norm_g": [2, 64], "s5_lam_re": [2, 2, 16, 64],
    "s5_lam_im": [2, 2, 16, 64], "s5_log_dt": [2, 2, 16], "s5_b_re": [2, 16, 64, 16],
    "s5_b_im": [2, 16, 64, 16], "s5_c_re": [2, 16, 16, 64], "s5_c_im": [2, 16, 16, 64],
    "s5_d": [2, 256], "s5_glu_w": [2, 256, 256], "s5_glu_b": [2, 256], "w_out": [2, 1024, 1024],
    "mlp_w1": [2, 1024, 4096], "mlp_w2": [2, 4096, 1024], "final_norm_g": [1024],
}


def declare(kb):
    P = kb.P
    cfg = kb.cfg
    kinds = cfg.get("kinds", {})
    kb.xin = P.dram("xin", [S, D], F32, kind="ExternalInput")
    kb.cvecT = P.dram("cvecT", [1024, 2], F32, kind="ExternalInput")
    kb.prm = {k: P.dram(k, shp, F32, kind="ExternalInput") for k, shp in PARAM_SHAPES.items()}
    kb.constd = P.dram("constp", [128, NCONST * 128], F32, kind="ExternalInput")
    kb.ropec = P.dram("ropec", [128, S], F32, kind="ExternalInput")
    kb.ropes = P.dram("ropes", [128, S], F32, kind="ExternalInput")
    kb.cmlat = P.dram("cmlat", [128, 2, 512], F32, kind="ExternalInput")
    kb.cmctx = P.dram("cmctx", [128, 2, 256], F32, kind="ExternalInput")
    kb.y = P.dram("y", [4096, D], F32, kind="ExternalOutput")
    kb.XS = P.dram("XS", [S, D], F32, kind=kinds.get("XS", "Internal"))
    kb.ZF = P.dram("ZF", [NZF, S], F32, kind=kinds.get("ZF", "Internal"))
    kb.ZT = P.dram("ZT", [S, NZT], F32, kind=kinds.get("ZT", "Internal"))
    kb.QKVF = P.dram("QKVF", [768, S], F32, kind=kinds.get("QKVF", "Internal"))
    kb.YC = P.dram("YC", [1024, S], F32, kind=kinds.get("YC", "Internal"))
    kb.H2T = P.dram("H2T", [1024, S], BF16, kind=kinds.get("H2T", "Internal"))
    kb.const = P.sbuf("const", [128, NCONST * 128])
    nchunk = 4
    w = NCONST * 128 // nchunk
    for i in range(nchunk):
        a, b = i * w, (i + 1) * w if i < nchunk - 1 else NCONST * 128
        kb.LD(kb.const[:, a:b], kb.constd[:, a:b], [kb.const.s(i)])
    kb.const_bufs = [kb.const.s(i) for i in range(nchunk)]
    kb.CB = kb.const_bufs
    kb.GS1 = P.sbuf("GS1", [128, 8, 2]); kb.SH1 = P.sbuf("SH1", [128, 8, 2])
    kb.GS2 = P.sbuf("GS2", [128, 8, 2]); kb.SH2 = P.sbuf("SH2", [128, 8, 2])
    kb.GATE1 = P.sbuf("GATE1", [128, 2, 1024]); kb.GATE2 = P.sbuf("GATE2", [128, 2, 1024])


def phase_mod(kb, l):
    P = kb.P
    prm = kb.prm
    with P.scope():
        cT = P.sbuf("cT", [128, 8, 2])
        kb.LD(cT[:], kb.cvecT[:].rearrange("(et e) c -> e et c", e=128), [cT])
        sc = P.sbuf("sc", [128, 8, 2])
        kb.ACT(sc[:], cT[:], AF.Silu, [cT], [sc])
        screp = P.sbuf("screp", [128, 8, 2, 128])
        kb.CP(screp[:], sc[:].unsqueeze(3).to_broadcast([128, 8, 2, 128]), [sc], [screp])
        mbf = P.sbuf("mbf", [128, 48])
        kb.LD(mbf[:], prm["mod_b"][l].rearrange("(j p) -> p j", p=128), [mbf], allow_slow_non_contiguous=True)
        ngf = P.sbuf("ngf", [128, 2, 8])
        kb.LD(ngf[:, 0, :], prm["norm1_g"][l].rearrange("(j p) -> p j", p=128), [ngf], allow_slow_non_contiguous=True)
        kb.LD(ngf[:, 1, :], prm["norm2_g"][l].rearrange("(j p) -> p j", p=128), [ngf], allow_slow_non_contiguous=True)
        mbrow = P.sbuf("mbrow", [128, 2, 1024])
        for gi, v in enumerate((2, 5)):
            kb.LD(mbrow[:, gi, :], prm["mod_b"][l][v * 1024:(v + 1) * 1024].partition_broadcast(128), [mbrow])
        wch = [P.sbuf("wch%d" % i, [128, 8, 1024]) for i in range(2)]
        ps_fm = P.psum("ps_fm", [128, 96])
        ps_g = [P.psum("ps_g%d" % i, [128, 512]) for i in range(2)]
        MF = P.sbuf("MF", [128, 48, 2])
        k = 0
        for v in range(6):
            wc = wch[v % 2]
            for et in range(8):
                kb.LD(wc[:, et, :], prm["mod_w"][l][et * 128:(et + 1) * 128, v * 1024:(v + 1) * 1024], [wc])
            for db in range(8):
                col = (v * 8 + db) * 2
                for et in range(8):
                    kb.MM(ps_fm[:, col:col + 2], wc[:, et, db * 128:(db + 1) * 128], sc[:, et, :],
                          et == 0, et == 7, [wc, sc], [ps_fm])
            if v in (2, 5):
                gt = kb.GATE1 if v == 2 else kb.GATE2
                gi = 0 if v == 2 else 1
                for which in range(2):
                    for half in range(2):
                        pg = ps_g[k % 2]; k += 1
                        for et in range(8):
                            kb.MM(pg[:], screp[:, et, which, :], wc[:, et, half * 512:(half + 1) * 512],
                                  et == 0, et == 7, [screp, wc], [pg])
                        kb.TT(gt[:, which, half * 512:(half + 1) * 512], pg[:], mbrow[:, gi, half * 512:(half + 1) * 512],
                              ALU.add, [pg, mbrow], [gt])
        kb.TT(MF[:], ps_fm[:].rearrange("p (j c) -> p j c", c=2), mbf[:].unsqueeze(2).to_broadcast([128, 48, 2]),
              ALU.add, [ps_fm, mbf], [MF])
        tmp = P.sbuf("mtmp", [128, 8, 2])
        kb.TS(tmp[:], MF[:, 8:16, :], 1.0, None, ALU.add, None, [MF], [tmp])
        kb.TT(kb.GS1[:], tmp[:], ngf[:, 0, :].unsqueeze(2).to_broadcast([128, 8, 2]), ALU.mult, [tmp, ngf], [kb.GS1])
        kb.CP(kb.SH1[:], MF[:, 0:8, :], [MF], [kb.SH1])
        tmp2 = P.sbuf("mtmp2", [128, 8, 2])
        kb.TS(tmp2[:], MF[:, 32:40, :], 1.0, None, ALU.add, None, [MF], [tmp2])
        kb.TT(kb.GS2[:], tmp2[:], ngf[:, 1, :].unsqueeze(2).to_broadcast([128, 8, 2]), ALU.mult, [tmp2, ngf], [kb.GS2])
        kb.CP(kb.SH2[:], MF[:, 24:32, :], [MF], [kb.SH2])


def norm_to_fm(kb, xt, hT, col0, GS, SH, which, bufs, R_x):
    P = kb.P
    junk, st, xn, ps_ts = bufs["junk"], bufs["st"], bufs["xn"], bufs["ps_t"]
    kb.MS(st[:, 0:1], 0.0, [st])
    kb.ACT(junk[:], xt[:], AF.Square, [xt], [junk, st], accum_out=st[:, 0:1])
    kb.ACT(st[:, 1:2], st[:, 0:1], AF.Sqrt, [st] + kb.CB, [st], scale=1.0 / D, bias=kb.C("CCOL")[:, 0:1])
    kb.RECIP(st[:, 2:3], st[:, 1:2], [st], [st])
    kb.ACT(xn[:], xt[:], AF.Copy, [xt, st], [xn], scale=st[:, 2:3])
    for half in range(2):
        ps_t = ps_ts[half]
        for q in range(4):
            dt = half * 4 + q
            kb.TR(ps_t[:, q * 128:(q + 1) * 128], xn[:, dt * 128:(dt + 1) * 128], kb.C("IDENT"), [xn] + kb.CB, [ps_t])
        for q in range(4):
            dt = half * 4 + q
            if q % 2 == 0:
                kb.TS(hT[:, dt, col0:col0 + 128], ps_t[:, q * 128:(q + 1) * 128], GS[:, dt, which:which + 1],
                      SH[:, dt, which:which + 1], ALU.mult, ALU.add, [ps_t, GS, SH], [hT])
            else:
                kb.ACT(hT[:, dt, col0:col0 + 128], ps_t[:, q * 128:(q + 1) * 128], AF.Identity, [ps_t, GS, SH], [hT],
                       scale=GS[:, dt, which:which + 1], bias=SH[:, dt, which:which + 1])


def phase_a(kb, l, src):
    P = kb.P
    with P.scope():
        win = P.sbuf("win", [128, 8, 3600], BF16)
        for kt in range(8):
            kb.LD(win[:, kt, :], kb.prm["w_in"][l][kt * 128:(kt + 1) * 128, :], [win.s(kt)], q="pool")
        winb = [win.s(kt) for kt in range(8)]
        xbuf = [P.sbuf("xa%d" % i, [128, 1024]) for i in range(2)]
        hTb = [P.sbuf("hTa%d" % i, [128, 8, 512], BF16) for i in range(2)]
        nb = {"junk": P.sbuf("junk", [128, 1024]), "st": P.sbuf("st", [128, 4]), "xn": P.sbuf("xn", [128, 1024]),
              "ps_t": [P.psum("ps_t%d" % i, [128, 512]) for i in range(2)]}
        ps_f = [P.psum("ps_f%d" % i, [128, 512]) for i in range(3)]
        ps_a = [P.psum("ps_a%d" % i, [128, 512]) for i in range(2)]
        ps_b = P.psum("ps_b", [128, 16])
        stg = [P.sbuf("stg%d" % i, [128, 512]) for i in range(4)]
        stt = [P.sbuf("stt%d" % i, [128, NZT]) for i in range(2)]
        kx = kf = ks = ka = 0
        for gi, t0 in enumerate(range(0, S, 512)):
            n = min(512, S - t0)
            hT = hTb[gi % 2]
            for ti in range(n // 128):
                tt = t0 // 128 + ti
                which = 1 if tt < 2 else 0
                xt = xbuf[kx % 2]; kx += 1
                kb.LD(xt[:], src[tt * 128:(tt + 1) * 128, :], [xt])
                norm_to_fm(kb, xt, hT, ti * 128, kb.GS1, kb.SH1, which, nb, None)
            for (zr, wc) in FM_BLOCKS:
                ps = ps_f[kf % 3]; kf += 1
                for kt in range(8):
                    kb.MM(ps[:, :n], win[:, kt, wc:wc + 128], hT[:, kt, :n], kt == 0, kt == 7, [winb[kt], hT], [ps])
                sg = stg[ks % 4]; ks += 1
                kb.CP(sg[:, :n], ps[:, :n], [ps], [sg], eng=kb.evac_eng())
                kb.ST(kb.ZF[zr:zr + 128, t0:t0 + n], sg[:, :n], [sg])
            for ti in range(n // 128):
                tt = t0 // 128 + ti
                pa = ps_a[ka % 2]
                so = stt[ka % 2]; ka += 1
                for (c0, w0, wn) in ((0, C_HI, 256), (256, C_RV, 256)):
                    for kt in range(8):
                        kb.MM(pa[:, c0:c0 + wn], hT[:, kt, ti * 128:(ti + 1) * 128], win[:, kt, w0:w0 + wn],
                              kt == 0, kt == 7, [winb[kt], hT], [pa])
                for kt in range(8):
                    kb.MM(ps_b[:], hT[:, kt, ti * 128:(ti + 1) * 128], win[:, kt, C_GA:C_GA + 16],
                          kt == 0, kt == 7, [winb[kt], hT], [ps_b])
                kb.CP(so[:, 0:512], pa[:], [pa], [so], eng="act")
                kb.CP(so[:, 512:528], ps_b[:], [ps_b], [so], eng="dve")
                kb.ST(kb.ZT[tt * 128:(tt + 1) * 128, :], so[:], [so])


def build(cfg):
    kb = KB(cfg)
    P = kb.P
    declare(kb)
    P.barrier()
    stages = cfg.get("stages", "all")
    for l in cfg.get("layers", range(DEPTH)):
        src = kb.xin if l == 0 else kb.XS
        if stages == "all" or "M" in stages:
            phase_mod(kb, l)
        if stages == "all" or "A" in stages:
            phase_a(kb, l, src)
        if stages == "all" or "R" in stages:
            mixer_ret(kb, l)
        if stages == "all" or "H" in stages:
            mixer_hgrn(kb, l)
        if stages == "all" or "G" in stages:
            (mixer_gdn if cfg.get("gdn_old") else mixer_gdn2)(kb, l)
        if stages == "all" or "S" in stages:
            mixer_s5(kb, l)
        if stages == "all" or "C" in stages:
            phase_c(kb, l, src)
    P.emit()
    P.close()
    return kb


_CONSTS = None


def host_inputs(inputs, cores=range(8)):
    global _CONSTS
    if _CONSTS is None:
        rc, rs = _rope_tables()
        cl, cc = _conv_masks()
        _CONSTS = {"constp": CONST_NP, "ropec": rc, "ropes": rs, "cmlat": cl, "cmctx": cc}
    maps = []
    for b in cores:
        m = {"xin": np.ascontiguousarray(np.concatenate([inputs["ctx"][b], inputs["x"][b]], axis=0), dtype=np.float32),
             "cvecT": np.ascontiguousarray(np.stack([inputs["c"][b], inputs["c_ctx"]], axis=1), dtype=np.float32)}
        for k in PARAM_SHAPES:
            m[k] = np.ascontiguousarray(inputs[k], dtype=np.float32)
        m.update(_CONSTS)
        maps.append(m)
    return maps


def kernel(**inputs):
    inputs = {k: np.asarray(v) for k, v in inputs.items()}
    kb = build({})
    maps = host_inputs(inputs)
    res = run_bass_kernel_spmd(kb.nc, maps, core_ids=list(range(8)))
    out = np.stack([np.asarray(r["y"]).reshape(4096, D) for r in res.results], axis=0)
    return out.astype(np.float32)


def phase_c(kb, l, src):
    P = kb.P
    last = (l == DEPTH - 1)
    t_start = 2 if last else 0
    with P.scope():
        wout = P.sbuf("wout", [128, 8, 1024], BF16)
        for ft in range(8):
            kb.LD(wout[:, ft, :], kb.prm["w_out"][l][ft * 128:(ft + 1) * 128, :], [wout.s(ft)], q="pool")
        wb = [wout.s(ft) for ft in range(8)]
        ycb = [P.sbuf("yc%d" % i, [128, 8, 128], BF16) for i in range(2)]
        xb = [P.sbuf("xc%d" % i, [128, 1024]) for i in range(2)]
        x1b = [P.sbuf("x1c%d" % i, [128, 1024]) for i in range(2)]
        tmpb = [P.sbuf("tc%d" % i, [128, 512]) for i in range(2)]
        h2b = [P.sbuf("h2c%d" % i, [128, 8, 128], BF16) for i in range(2)]
        nb = {"junk": P.sbuf("junkc", [128, 1024]), "st": P.sbuf("stc", [128, 4]), "xn": P.sbuf("xnc", [128, 1024]),
              "ps_t": [P.psum("ps_tc%d" % i, [128, 512]) for i in range(2)]}
        ps_y = [P.psum("ps_y%d" % i, [128, 512]) for i in range(4)]
        k = 0
        for tt in range(t_start, NT):
            which = 1 if tt < 2 else 0
            yc = ycb[k % 2]; xt = xb[k % 2]; x1 = x1b[k % 2]; h2 = h2b[k % 2]
            cols = slice(tt * 128, (tt + 1) * 128)
            kb.LD(yc[:], kb.YC[:, cols].rearrange("(ft p) t -> p ft t", p=128), [yc], q="pool")
            kb.LD(xt[:], src[cols, :], [xt])
            for half in range(2):
                ps = ps_y[(2 * k + half) % 4]
                for ft in range(8):
                    kb.MM(ps[:], yc[:, ft, :], wout[:, ft, half * 512:(half + 1) * 512], ft == 0, ft == 7,
                          [yc, wb[ft]], [ps])
                tm = tmpb[half]
                kb.TT(tm[:], ps[:], kb.GATE1[:, which, half * 512:(half + 1) * 512], ALU.mult, [ps, kb.GATE1], [tm])
                kb.TT(x1[:, half * 512:(half + 1) * 512], xt[:, half * 512:(half + 1) * 512], tm[:], ALU.add,
                      [xt, tm], [x1], eng="pool")
            kb.ST(kb.XS[cols, :], x1[:], [x1])
            norm_to_fm(kb, x1, h2, 0, kb.GS2, kb.SH2, which, nb, None)
            kb.ST(kb.H2T[:, cols].rearrange("(dt p) t -> p dt t", p=128), h2[:], [h2])
            k += 1
    with P.scope():
        w1 = P.sbuf("w1", [128, 8, 4096], BF16)
        w2 = P.sbuf("w2", [128, 32, 1024], BF16)
        for kt in range(8):
            kb.LD(w1[:, kt, :], kb.prm["mlp_w1"][l][kt * 128:(kt + 1) * 128, :], [w1.s(kt)], q="pool")
        for fb in range(32):
            kb.LD(w2[:, fb, :], kb.prm["mlp_w2"][l][fb * 128:(fb + 1) * 128, :], [w2.s(fb)], q="pool")
        h2b = [P.sbuf("h2d%d" % i, [128, 8, 256], BF16) for i in range(2)]
        uTb = [P.sbuf("uT%d" % i, [128, 16, 256], BF16) for i in range(1)]
        rb = [P.sbuf("relu%d" % i, [128, 256]) for i in range(3)]
        xb = [P.sbuf("xd%d" % i, [128, 1024]) for i in range(2)]
        tmpb = [P.sbuf("td%d" % i, [128, 512]) for i in range(2)]
        ps_u = [P.psum("ps_u%d" % i, [128, 256]) for i in range(3)]
        ps_y = [P.psum("ps_y2%d" % i, [128, 512]) for i in range(4)]
        if last:
            fg = P.sbuf("fg", [128, 1024])
            kb.LD(fg[:], kb.prm["final_norm_g"][:].partition_broadcast(128), [fg])
            stf = P.sbuf("stf", [128, 4])
            xnf = P.sbuf("xnf", [128, 1024])
        k = 0; ku = 0
        for g0 in range(t_start, NT, 2):
            h2 = h2b[k % 2]; uT = uTb[0]
            cols = slice(g0 * 128, (g0 + 2) * 128)
            kb.LD(h2[:], kb.H2T[:, cols].rearrange("(dt p) t -> p dt t", p=128), [h2])
            for hh in range(2):
                for fl in range(16):
                    fb = hh * 16 + fl
                    ps = ps_u[ku % 3]; r = rb[ku % 3]; ku += 1
                    for kt in range(8):
                        kb.MM(ps[:], w1[:, kt, fb * 128:(fb + 1) * 128], h2[:, kt, :], kt == 0, kt == 7, [w1.s(kt), h2], [ps])
                    kb.ACT(r[:], ps[:], AF.Relu, [ps], [r])
                    kb.TT(uT[:, fl, :], r[:], r[:], ALU.mult, [r], [uT.s(fl)], eng=("dve" if fb % 2 else "pool"))
                for ti in range(2):
                    for half in range(2):
                        ps = ps_y[2 * ti + half]
                        for fl in range(16):
                            fb = hh * 16 + fl
                            kb.MM(ps[:], uT[:, fl, ti * 128:(ti + 1) * 128], w2[:, fb, half * 512:(half + 1) * 512],
                                  fb == 0, fb == 31, [uT.s(fl), w2.s(fb)], [ps])
            for ti in range(2):
                tt = g0 + ti
                which = 1 if tt < 2 else 0
                xt = xb[ti]
                rows = slice(tt * 128, (tt + 1) * 128)
                kb.LD(xt[:], kb.XS[rows, :], [xt])
                for half in range(2):
                    ps = ps_y[2 * ti + half]
                    tm = tmpb[half]
                    kb.TT(tm[:], ps[:], kb.GATE2[:, which, half * 512:(half + 1) * 512], ALU.mult, [ps, kb.GATE2], [tm])
                    kb.TT(xt[:, half * 512:(half + 1) * 512], xt[:, half * 512:(half + 1) * 512], tm[:], ALU.add,
                          [xt, tm], [xt], eng="pool")
                if not last:
                    kb.ST(kb.XS[rows, :], xt[:], [xt])
                else:
                    kb.MS(stf[:, 0:1], 0.0, [stf])
                    kb.ACT(xnf[:], xt[:], AF.Square, [xt], [xnf, stf], accum_out=stf[:, 0:1])
                    kb.ACT(stf[:, 1:2], stf[:, 0:1], AF.Sqrt, [stf], [stf], scale=1.0 / D, bias=kb.C("CCOL")[:, 0:1])
                    kb.RECIP(stf[:, 2:3], stf[:, 1:2], [stf], [stf])
                    kb.ACT(xnf[:], xt[:], AF.Copy, [xt, stf], [xnf], scale=stf[:, 2:3])
                    kb.TT(xnf[:], xnf[:], fg[:], ALU.mult, [xnf, fg], [xnf])
                    kb.P.dma(kb.y[(tt - 2) * 128:(tt - 1) * 128, :], xnf[:], reads=[xnf.b], q="pool", is_output=True)
            k += 1


def finalize_gated(kb, OACC, gate_row0, gain, yc_row0, pfx):
    P = kb.P
    gb = [P.sbuf(pfx + "fg%d" % i, [128, 2, 128]) for i in range(2)]
    sq = [P.sbuf(pfx + "fsq%d" % i, [128, 128]) for i in range(2)]
    rt = [P.sbuf(pfx + "frt%d" % i, [128, 128]) for i in range(2)]
    sg = [P.sbuf(pfx + "fsg%d" % i, [128, 128]) for i in range(2)]
    ob = [P.sbuf(pfx + "fo%d" % i, [128, 128]) for i in range(2)]
    ps_m = [P.psum(pfx + "fps%d" % i, [128, 128]) for i in range(2)]
    k = 0
    for n in range(NT):
        cols = slice(n * 128, (n + 1) * 128)
        g = gb[n % 2]
        kb.LD(g[:], kb.ZF[gate_row0:gate_row0 + 256, cols].rearrange("(hp p) t -> p hp t", p=128), [g])
        for hp in range(2):
            i = k % 2; k += 1
            o = OACC[:, hp, cols]
            kb.TT(sq[i][:], o, o, ALU.mult, [OACC.s(n)], [sq[i]], eng="pool")
            kb.MM(ps_m[i][:], kb.C("BLK64"), sq[i][:], True, True, [sq[i]], [ps_m[i]])
            kb.ACT(rt[i][:], ps_m[i][:], AF.Sqrt, [ps_m[i]], [rt[i]], scale=1.0 / 64, bias=kb.C("CCOL")[:, 0:1])
            kb.RECIP(rt[i][:], rt[i][:], [rt[i]], [rt[i]])
            kb.ACT(sg[i][:], g[:, hp, :], AF.Silu, [g], [sg[i]])
            kb.TT(ob[i][:], o, rt[i][:], ALU.mult, [OACC.s(n), rt[i]], [ob[i]])
            if gain is not None:
                kb.STT(ob[i][:], ob[i][:], gain[:, 0:1], sg[i][:], ALU.mult, ALU.mult, [ob[i], gain, sg[i]], [ob[i]])
            else:
                kb.TT(ob[i][:], ob[i][:], sg[i][:], ALU.mult, [ob[i], sg[i]], [ob[i]])
            kb.ST(kb.YC[yc_row0 + hp * 128:yc_row0 + (hp + 1) * 128, cols], ob[i][:], [ob[i]])


def oacc_write(kb, OACC, hp, n, ps, d):
    cols = slice(n * 128, (n + 1) * 128)
    if d == 0:
        kb.CP(OACC[:, hp, cols], ps[:], [ps], [OACC.s(n)], eng="act")
    else:
        kb.TT(OACC[:, hp, cols], OACC[:, hp, cols], ps[:], ALU.add, [ps], [OACC.s(n)])


def mixer_ret(kb, l):
    P = kb.P
    with P.scope():
        OACC = P.sbuf("r_oacc", [128, 2, S])
        with P.scope():
            lgt = P.sbuf("r_lgt", [128, 8])
            kb.LD(lgt[:], kb.prm["ret_decay_logit"][l].rearrange("d h -> (d h)").partition_broadcast(128), [lgt])
            LG = P.sbuf("r_LG", [128, 8])
            kb.ACT(LG[:], lgt[:], AF.Sigmoid, [lgt], [LG])
            kb.ACT(LG[:], LG[:], AF.Ln, [LG], [LG])
            LGP = P.sbuf("r_LGP", [128, 4])
            for d in range(2):
                for hp in range(2):
                    c = 2 * d + hp
                    kb.CP(LGP[0:64, c:c + 1], LG[0:64, 4 * d + 2 * hp:4 * d + 2 * hp + 1], [LG], [LGP])
                    kb.CP(LGP[64:128, c:c + 1], LG[64:128, 4 * d + 2 * hp + 1:4 * d + 2 * hp + 2], [LG], [LGP])
            MK = [P.sbuf("r_MK%d" % d, [128, 4, 128]) for d in range(2)]
            QDEC = [[P.sbuf("r_QD%d%d" % (d, hp), [128, 128]) for hp in range(2)] for d in range(2)]
            etmp = P.sbuf("r_etmp", [128, 128])
            for d in range(2):
                for h in range(4):
                    kb.ACT(etmp[:], kb.C("DIFF" if d == 0 else "NDIFF"), AF.Exp, [LG], [etmp],
                           scale=LG[:, 4 * d + h:4 * d + h + 1])
                    kb.STT(MK[d][:, h, :], etmp[:], 0.125, kb.C("TRIF" if d == 0 else "TRIB"), ALU.mult, ALU.mult,
                           [etmp], [MK[d]])
                for hp in range(2):
                    kb.ACT(QDEC[d][hp][:], kb.C("IOTAF1" if d == 0 else "RIOTAF"), AF.Exp, [LGP], [QDEC[d][hp]],
                           scale=LGP[:, 2 * d + hp:2 * d + hp + 1])
            KD = P.sbuf("r_KD", [128, 8])
            kb.ACT(KD[:, 0:4], LG[:, 0:4], AF.Exp, [LG], [KD], scale=kb.C("CCOL")[:, 3:4])
            kb.ACT(KD[:, 4:8], LG[:, 4:8], AF.Exp, [LG], [KD], scale=kb.C("CCOL")[:, 2:3])
            kb.TS(KD[:], KD[:], 0.125, None, ALU.mult, None, [KD], [KD])
            CV = P.sbuf("r_CV", [128, 4])
            kb.ACT(CV[:], LGP[:], AF.Exp, [LGP], [CV], scale=128.0)
            qTb = [P.sbuf("r_q%d" % i, [128, 2, 128]) for i in range(2)]
            kTb = [P.sbuf("r_k%d" % i, [128, 2, 128]) for i in range(2)]
            csb = [P.sbuf("r_cs%d" % i, [128, 2, 128]) for i in range(2)]
            Vp = [[P.sbuf("r_vp%d%d" % (i, h), [128, 128]) for h in range(4)] for i in range(2)]
            khp = [[P.sbuf("r_kh%d%d" % (i, h), [128, 128]) for h in range(4)] for i in range(2)]
            for i in range(2):
                for h in range(4):
                    kb.MS(Vp[i][h][:], 0.0, [Vp[i][h]], eng="pool")
                    kb.MS(khp[i][h][:], 0.0, [khp[i][h]], eng="pool")
            t1 = [P.sbuf("r_t1%d" % i, [128, 128]) for i in range(2)]
            t2 = [P.sbuf("r_t2%d" % i, [128, 128]) for i in range(2)]
            qr = [P.sbuf("r_qr%d" % i, [128, 2, 128]) for i in range(2)]
            kr = [P.sbuf("r_kr%d" % i, [128, 2, 128]) for i in range(2)]
            AT = [P.sbuf("r_AT%d" % i, [128, 2, 128]) for i in range(2)]
            qd = [P.sbuf("r_qd%d" % i, [128, 128]) for i in range(2)]
            Sb = [P.sbuf("r_S%d" % hp, [128, 128]) for hp in range(2)]
            ps_r = [P.psum("r_psr%d" % i, [128, 256]) for i in range(2)]
            ps_s = [P.psum("r_pss%d" % i, [128, 2, 128]) for i in range(2)]
            ps_o = [P.psum("r_pso%d" % i, [128, 128]) for i in range(2)]
            ps_k = P.psum("r_psk", [128, 128])
            ps_kv = P.psum("r_pskv", [128, 128])
            it = 0
            for d in range(2):
                for hp in range(2):
                    kb.MS(Sb[hp][:], 0.0, [Sb[hp]])
                for n in ORDER[d]:
                    cols = slice(n * 128, (n + 1) * 128)
                    b = it % 2; it += 1
                    qT, kT, cs = qTb[b], kTb[b], csb[b]
                    kb.LD(qT[:], kb.ZF[Z_RQ:Z_RQ + 256, cols].rearrange("(hp p) t -> p hp t", p=128), [qT])
                    kb.LD(kT[:], kb.ZF[Z_RK:Z_RK + 256, cols].rearrange("(hp p) t -> p hp t", p=128), [kT])
                    kb.LD(cs[:, 0, :], kb.ropec[:, cols], [cs])
                    kb.LD(cs[:, 1, :], kb.ropes[:, cols], [cs])
                    for h in range(4):
                        kb.LD(Vp[b][h][:, 64 * (h % 2):64 * (h % 2) + 64], kb.ZT[cols, 256 + 64 * h:256 + 64 * h + 64],
                              [Vp[b][h]])
                    for hp in range(2):
                        j = (it * 2 + hp) % 2
                        pr = ps_r[j]
                        kb.MM(pr[:, 0:128], kb.C("ROT"), qT[:, hp, :], True, True, [qT], [pr])
                        kb.MM(pr[:, 128:256], kb.C("ROT"), kT[:, hp, :], True, True, [kT], [pr])
                        for (src_, dst, off) in ((qT, qr[b], 0), (kT, kr[b], 128)):
                            kb.TT(t1[j][:], src_[:, hp, :], cs[:, 0, :], ALU.mult, [src_, cs], [t1[j]], eng="pool")
                            kb.TT(t2[j][:], pr[:, off:off + 128], cs[:, 1, :], ALU.mult, [pr, cs], [t2[j]])
                            kb.TT(dst[:, hp, :], t1[j][:], t2[j][:], ALU.add, [t1[j], t2[j]], [dst.s(hp)], eng="pool")
                        pss = ps_s[j]
                        for h2 in range(2):
                            kb.MM(pss[:, h2, :], kr[b][64 * h2:64 * h2 + 64, hp, :], qr[b][64 * h2:64 * h2 + 64, hp, :],
                                  True, True, [kr[b].s(hp), qr[b].s(hp)], [pss])
                        kb.TT(AT[j][:], pss[:], MK[d][:, 2 * hp:2 * hp + 2, :], ALU.mult, [pss, MK[d]], [AT[j]])
                        kb.TT(qd[j][:], qr[b][:, hp, :], QDEC[d][hp][:], ALU.mult, [qr[b].s(hp), QDEC[d][hp]], [qd[j]],
                              eng="pool")
                        po = ps_o[j]
                        kb.MM(po[:], Vp[b][2 * hp][:], AT[j][:, 0, :], True, False, [Vp[b][2 * hp], AT[j]], [po])
                        kb.MM(po[:], Vp[b][2 * hp + 1][:], AT[j][:, 1, :], False, False, [Vp[b][2 * hp + 1], AT[j]], [po])
                        kb.MM(po[:], Sb[hp][:], qd[j][:], False, True, [Sb[hp], qd[j]], [po])
                        oacc_write(kb, OACC, hp, n, po, d)
                        kb.TR(ps_k[:], kr[b][:, hp, :], kb.C("IDENT"), [kr[b].s(hp)], [ps_k])
                        for h2 in range(2):
                            h = 2 * hp + h2
                            kb.ACT(khp[b][h][:, 64 * h2:64 * h2 + 64], ps_k[:, 64 * h2:64 * h2 + 64], AF.Copy,
                                   [ps_k, KD], [khp[b][h]], scale=KD[:, 4 * d + h:4 * d + h + 1])
                        kb.MM(ps_kv[:], khp[b][2 * hp][:], Vp[b][2 * hp][:], True, False,
                              [khp[b][2 * hp], Vp[b][2 * hp]], [ps_kv])
                        kb.MM(ps_kv[:], khp[b][2 * hp + 1][:], Vp[b][2 * hp + 1][:], False, True,
                              [khp[b][2 * hp + 1], Vp[b][2 * hp + 1]], [ps_kv])
                        kb.STT(Sb[hp][:], Sb[hp][:], CV[:, 2 * d + hp:2 * d + hp + 1], ps_kv[:], ALU.mult, ALU.add,
                               [Sb[hp], CV, ps_kv], [Sb[hp]])
        with P.scope():
            finalize_gated(kb, OACC, Z_RG, None, 256, "r_")


def mixer_hgrn(kb, l):
    P = kb.P
    with P.scope():
        OACC = P.sbuf("h_oacc", [128, 2, S])
        with P.scope():
            LB = P.sbuf("h_LB", [128, 4]); OML = P.sbuf("h_OML", [128, 4])
            if l == 0:
                kb.MS(LB[:], 0.0, [LB]); kb.MS(OML[:], 1.0, [OML])
            else:
                lgt = P.sbuf("h_lgt", [128, 8])
                kb.LD(lgt[:], kb.prm["hgrn_lb_logits"][:].rearrange("l d (hp p) -> p (l d hp)", p=128), [lgt],
                      allow_slow_non_contiguous=True)
                kb.TT(LB[:], lgt[:, 4:8], lgt[:, 0:4], ALU.subtract, [lgt], [LB])
                kb.ACT(LB[:], LB[:], AF.Sigmoid, [LB], [LB])
                kb.TS(OML[:], LB[:], -1.0, 1.0, ALU.mult, ALU.add, [LB], [OML])
            G = P.sbuf("h_G", [128, 1])
            for hh in range(2):
                kb.LD(G[64 * hh:64 * hh + 64, :], kb.prm["hgrn_norm_g"][l].rearrange("(p o) -> p o", o=1), [G])
            kb.hgrn_gain = G
            hqb = [P.sbuf("h_q%d" % i, [128, 2, 128]) for i in range(2)]
            hfb = [P.sbuf("h_f%d" % i, [128, 2, 128]) for i in range(2)]
            Vp = [[P.sbuf("h_vp%d%d" % (i, h), [128, 128]) for h in range(4)] for i in range(2)]
            khp = [[P.sbuf("h_kh%d%d" % (i, h), [128, 128]) for h in range(4)] for i in range(2)]
            for i in range(2):
                for h in range(4):
                    kb.MS(Vp[i][h][:], 0.0, [Vp[i][h]], eng="pool")
                    kb.MS(khp[i][h][:], 0.0, [khp[i][h]], eng="pool")
            MREF = [[P.sbuf("h_mr%d%d" % (d, i), [128, 4]) for i in range(2)] for d in range(2)]
            for d in range(2):
                for i in range(2):
                    kb.MS(MREF[d][i][:], 0.0, [MREF[d][i]])

            def two(name, shape=(128, 128)):
                return [P.sbuf("h_%s%d" % (name, i), list(shape)) for i in range(2)]
            qs, sgm, ff, logf, kk, bb, pre = two("qs"), two("sg"), two("ff"), two("lf"), two("kk"), two("bb"), two("pre")
            e1, Ql, e2, Qd = two("e1"), two("Ql"), two("e2"), two("Qd")
            Kt = [two("Kt%d" % r) for r in range(4)]
            ex = two("ex")
            AT = two("AT", (128, 2, 128))
            KhT = two("KhT")
            bend = two("bend", (128, 2))
            Sb = [P.sbuf("h_S%d" % hp, [128, 128]) for hp in range(2)]
            ps_s = [P.psum("h_pss%d" % i, [128, 2, 128]) for i in range(2)]
            ps_o = [P.psum("h_pso%d" % i, [128, 128]) for i in range(2)]
            ps_k = [P.psum("h_psk%d" % i, [128, 128]) for i in range(2)]
            ps_kv = [P.psum("h_pskv%d" % i, [128, 128]) for i in range(2)]
            it = 0
            jj = 0
            for d in range(2):
                zf = Z_HFF if d == 0 else Z_HFB
                for hp in range(2):
                    kb.MS(Sb[hp][:], 0.0, [Sb[hp]])
                for n in ORDER[d]:
                    cols = slice(n * 128, (n + 1) * 128)
                    b = it % 2; it += 1
                    hq, hf = hqb[b], hfb[b]
                    kb.LD(hq[:], kb.ZF[Z_HQ:Z_HQ + 256, cols].rearrange("(hp p) t -> p hp t", p=128), [hq])
                    kb.LD(hf[:], kb.ZF[zf:zf + 256, cols].rearrange("(hp p) t -> p hp t", p=128), [hf])
                    for h in range(4):
                        kb.LD(Vp[b][h][:, 64 * (h % 2):64 * (h % 2) + 64], kb.ZT[cols, 64 * h:64 * h + 64], [Vp[b][h]])
                    for hp in range(2):
                        j = jj % 2; jj += 1
                        c = 2 * d + hp
                        mref = MREF[d][j]
                        kb.ACT(qs[j][:], hq[:, hp, :], AF.Silu, [hq], [qs[j]])
                        kb.ACT(sgm[j][:], hf[:, hp, :], AF.Sigmoid, [hf], [sgm[j]])
                        kb.TS(ff[j][:], sgm[j][:], OML[:, c:c + 1], LB[:, c:c + 1], ALU.mult, ALU.add, [sgm[j], OML, LB], [ff[j]])
                        kb.ACT(logf[j][:], ff[j][:], AF.Ln, [ff[j]], [logf[j]])
                        kb.TS(kk[j][:], ff[j][:], -1.0, 1.0, ALU.mult, ALU.add, [ff[j]], [kk[j]], eng="pool")
                        B = bb[j]
                        if d == 0:
                            kb.SCAN(B[:], kb.C("ONES"), logf[j][:], [logf[j]], [B])
                            kb.CP(mref[:, 1:4], B[:].rearrange("p (r c) -> p r c", c=32)[:, 0:3, 31], [B], [mref])
                            be = B[:, 127:128]
                        else:
                            kb.SCAN(pre[j][:], kb.C("ONES"), logf[j][:], [logf[j]], [pre[j]])
                            kb.STT(B[:], pre[j][:], -1.0, logf[j][:], ALU.mult, ALU.add, [pre[j], logf[j]], [B])
                            kb.TS(B[:], B[:], pre[j][:, 127:128], None, ALU.add, None, [B, pre[j]], [B])
                            kb.CP(mref[:, 0:3], B[:].rearrange("p (r c) -> p r c", c=32)[:, 1:4, 0], [B], [mref])
                            be = B[:, 0:1]
                        kb.TT(e1[j][:].rearrange("p (r c) -> p r c", c=32), B[:].rearrange("p (r c) -> p r c", c=32),
                              mref[:].unsqueeze(2).to_broadcast([128, 4, 32]), ALU.subtract, [B, mref], [e1[j]])
                        kb.ACT(e1[j][:], e1[j][:], AF.Exp, [e1[j]], [e1[j]])
                        kb.STT(Ql[j][:], qs[j][:], 0.125, e1[j][:], ALU.mult, ALU.mult, [qs[j], e1[j]], [Ql[j]], eng="pool")
                        kb.ACT(e2[j][:], B[:], AF.Exp, [B], [e2[j]])
                        kb.STT(Qd[j][:], qs[j][:], 0.125, e2[j][:], ALU.mult, ALU.mult, [qs[j], e2[j]], [Qd[j]], eng="pool")
                        pss = ps_s[j]
                        for r in range(4):
                            kb.ACT(ex[j][:], B[:], AF.Exp, [B, mref], [ex[j]], scale=-1.0, bias=mref[:, r:r + 1])
                            kb.STT(Kt[r][j][:], ex[j][:], 1e26, kk[j][:], ALU.min, ALU.mult, [ex[j], kk[j]], [Kt[r][j]])
                            for h2 in range(2):
                                kb.MM(pss[:, h2, 32 * r:32 * r + 32], Kt[r][j][64 * h2:64 * h2 + 64, :],
                                      Ql[j][64 * h2:64 * h2 + 64, 32 * r:32 * r + 32], True, True,
                                      [Kt[r][j], Ql[j]], [pss])
                        kb.TT(AT[j][:], pss[:], kb.C("TRIF" if d == 0 else "TRIB").unsqueeze(1).to_broadcast([128, 2, 128]),
                              ALU.mult, [pss], [AT[j]])
                        po = ps_o[j]
                        kb.MM(po[:], Vp[b][2 * hp][:], AT[j][:, 0, :], True, False, [Vp[b][2 * hp], AT[j]], [po])
                        kb.MM(po[:], Vp[b][2 * hp + 1][:], AT[j][:, 1, :], False, False, [Vp[b][2 * hp + 1], AT[j]], [po])
                        kb.MM(po[:], Sb[hp][:], Qd[j][:], False, True, [Sb[hp], Qd[j]], [po])
                        oacc_write(kb, OACC, hp, n, po, d)
                        kb.CP(bend[j][:, 0:1], be, [B], [bend[j]])
                        kb.ACT(KhT[j][:], B[:], AF.Exp, [B, bend[j]], [KhT[j]], scale=-1.0, bias=bend[j][:, 0:1])
                        kb.TT(KhT[j][:], KhT[j][:], kk[j][:], ALU.mult, [KhT[j], kk[j]], [KhT[j]], eng="pool")
                        kb.ACT(bend[j][:, 1:2], bend[j][:, 0:1], AF.Exp, [bend[j]], [bend[j]])
                        pk = ps_k[j]
                        kb.TR(pk[:], KhT[j][:], kb.C("IDENT"), [KhT[j]], [pk])
                        for h2 in range(2):
                            h = 2 * hp + h2
                            kb.CP(khp[b][h][:, 64 * h2:64 * h2 + 64], pk[:, 64 * h2:64 * h2 + 64], [pk], [khp[b][h]],
                                  eng=("act" if h2 else "dve"))
                        pkv = ps_kv[j]
                        kb.MM(pkv[:], khp[b][2 * hp][:], Vp[b][2 * hp][:], True, False, [khp[b][2 * hp], Vp[b][2 * hp]], [pkv])
                        kb.MM(pkv[:], khp[b][2 * hp + 1][:], Vp[b][2 * hp + 1][:], False, True,
                              [khp[b][2 * hp + 1], Vp[b][2 * hp + 1]], [pkv])
                        kb.STT(Sb[hp][:], Sb[hp][:], bend[j][:, 1:2], pkv[:], ALU.mult, ALU.add,
                               [Sb[hp], bend[j], pkv], [Sb[hp]])
        with P.scope():
            G = P.sbuf("h_G2", [128, 1])
            for hh in range(2):
                kb.LD(G[64 * hh:64 * hh + 64, :], kb.prm["hgrn_norm_g"][l].rearrange("(p o) -> p o", o=1), [G])
            finalize_gated(kb, OACC, Z_HG, G, 0, "h_")


PI = float(np.pi)


def _sincos(kb, ang, sin_out, cos_out, R, tmp, shape=None):
    P = kb.P
    shp = list(ang.shape)
    with P.scope():
        ki = P.sbuf("sc_ki", shp, mybir.dt.int32)
        kf = P.sbuf("sc_kf", shp)
        r = P.sbuf("sc_r", shp)
        m = P.sbuf("sc_m", shp)
        C1 = 6.28125
        C2 = 2 * PI - C1
        for (shift, out) in ((0.0, sin_out), (PI / 2, cos_out)):
            kb.TS(r[:], ang, shift, None, ALU.add, None, R, [r])
            kb.TS(kf[:], r[:], 1.0 / (2 * PI), None, ALU.mult, None, [r], [kf])
            kb.CP(ki[:], kf[:], [kf], [ki])
            kb.CP(kf[:], ki[:], [ki], [kf])
            kb.STT(r[:], kf[:], -C1, r[:], ALU.mult, ALU.add, [kf, r], [r])
            kb.STT(r[:], kf[:], -C2, r[:], ALU.mult, ALU.add, [kf, r], [r])
            kb.TS(m[:], r[:], PI, 2 * PI, ALU.is_gt, ALU.mult, [r], [m])
            kb.TT(r[:], r[:], m[:], ALU.subtract, [r, m], [r])
            kb.TS(m[:], r[:], -PI, 2 * PI, ALU.is_lt, ALU.mult, [r], [m])
            kb.TT(r[:], r[:], m[:], ALU.add, [r, m], [r])
            kb.ACT(out, r[:], AF.Sin, [r], R)


def mixer_s5(kb, l):
    P = kb.P
    prm = kb.prm
    with P.scope():
        OACC = P.sbuf("s_oacc", [128, 2, S])
        with P.scope():
            WX = P.sbuf("s_WX", [128, 2, 8, 2, 64])
            Cblk = P.sbuf("s_Cblk", [128, 16, 128])
            kb.MS(WX[:], 0.0, [WX], eng="pool")
            kb.MS(Cblk[:], 0.0, [Cblk], eng="pool")
            for g8 in range(8):
                for ri, nm in enumerate(("s5_b_re", "s5_b_im")):
                    for gg in range(2):
                        src = prm[nm][l][8 * gg + g8].rearrange("p c -> c p")
                        kb.LD(WX[16 * g8:16 * g8 + 16, gg, g8, ri, :], src, [WX], allow_slow_non_contiguous=True)
            for g in range(16):
                g8 = g % 8
                kb.LD(Cblk[0:64, g, 16 * g8:16 * g8 + 16], prm["s5_c_re"][l][g].rearrange("c p -> p c"), [Cblk],
                      allow_slow_non_contiguous=True)
                kb.LD(Cblk[64:128, g, 16 * g8:16 * g8 + 16], prm["s5_c_im"][l][g].rearrange("c p -> p c"), [Cblk],
                      allow_slow_non_contiguous=True)
            kb.TS(Cblk[64:128, :, :], Cblk[64:128, :, :], -1.0, None, ALU.mult, None, [Cblk], [Cblk])
            Cb16 = P.sbuf("s_Cb16", [128, 16, 128], BF16)
            kb.CP(Cb16[:], Cblk[:], [Cblk], [Cb16])
            VFr = P.sbuf("s_VFr", [128, 16, 64]); VFi = P.sbuf("s_VFi", [128, 16, 64])
            T1 = P.sbuf("s_T1", [128, 16, 128]); T2 = P.sbuf("s_T2", [128, 16, 128])
            AR = P.sbuf("s_AR", [128, 16]); NAI = P.sbuf("s_NAI", [128, 16])
            for d in range(2):
                with P.scope():
                    lr = P.sbuf("s_lr", [128, 16, 64]); li = P.sbuf("s_li", [128, 16, 64]); dtb = P.sbuf("s_dt", [128, 16])
                    kb.LD(lr[:], prm["s5_lam_re"][l][d].rearrange("g p -> (g p)").partition_broadcast(128), [lr])
                    kb.LD(li[:], prm["s5_lam_im"][l][d].rearrange("g p -> (g p)").partition_broadcast(128), [li])
                    kb.LD(dtb[:], prm["s5_log_dt"][l][d].partition_broadcast(128), [dtb])
                    kb.ACT(dtb[:], dtb[:], AF.Exp, [dtb], [dtb])
                    dt_bc = dtb[:].unsqueeze(2).to_broadcast([128, 16, 64])
                    lrdt = P.sbuf("s_lrdt", [128, 16, 64]); lidt = P.sbuf("s_lidt", [128, 16, 64])
                    kb.TT(lrdt[:], lr[:], dt_bc, ALU.mult, [lr, dtb], [lrdt])
                    kb.TT(lidt[:], li[:], dt_bc, ALU.mult, [li, dtb], [lidt])
                    a = [P.sbuf("s_a%d" % i, [128, 16, 64]) for i in range(8)]
                    mag, ang, sn, cs, tmp, ar, ai, t2 = a
                    kb.ACT(mag[:], lrdt[:], AF.Exp, [lrdt], [mag])
                    _sincos(kb, lidt[:], sn[:], cs[:], [lidt, sn, cs, tmp], tmp[:])
                    kb.TT(ar[:], mag[:], cs[:], ALU.mult, [mag, cs], [ar])
                    kb.TT(ai[:], mag[:], sn[:], ALU.mult, [mag, sn], [ai])
                    den = P.sbuf("s_den", [128, 16, 64]); fr = P.sbuf("s_fr", [128, 16, 64]); fi = P.sbuf("s_fi", [128, 16, 64])
                    kb.TT(den[:], lr[:], lr[:], ALU.mult, [lr], [den])
                    kb.TT(t2[:], li[:], li[:], ALU.mult, [li], [t2])
                    kb.TT(den[:], den[:], t2[:], ALU.add, [den, t2], [den])
                    kb.RECIP(den[:], den[:], [den], [den])
                    kb.TS(ar[:], ar[:], -1.0, None, ALU.add, None, [ar], [ar])
                    kb.TT(fr[:], ar[:], lr[:], ALU.mult, [ar, lr], [fr])
                    kb.TT(t2[:], ai[:], li[:], ALU.mult, [ai, li], [t2])
                    kb.TT(fr[:], fr[:], t2[:], ALU.add, [fr, t2], [fr])
                    kb.TT(fr[:], fr[:], den[:], ALU.mult, [fr, den], [fr])
                    kb.TT(fi[:], ai[:], lr[:], ALU.mult, [ai, lr], [fi])
                    kb.TT(t2[:], ar[:], li[:], ALU.mult, [ar, li], [t2])
                    kb.TT(fi[:], fi[:], t2[:], ALU.subtract, [fi, t2], [fi])
                    kb.TT(fi[:], fi[:], den[:], ALU.mult, [fi, den], [fi])
                    jcol = kb.C("CCOL")[:, 2:3] if d == 0 else kb.C("CCOL")[:, 3:4]
                    njcol = kb.C("CCOL")[:, 6:7] if d == 0 else kb.C("CCOL")[:, 7:8]
                    kb.ACT(mag[:], lrdt[:], AF.Exp, [lrdt], [mag], scale=njcol)
                    kb.TS(ang[:], lidt[:], jcol, None, ALU.mult, None, [lidt], [ang])
                    _sincos(kb, ang[:], sn[:], cs[:], [ang, sn, cs, tmp], tmp[:])
                    vr, vi = ar, ai
                    kb.TT(vr[:], mag[:], cs[:], ALU.mult, [mag, cs], [vr])
                    kb.TT(vi[:], mag[:], sn[:], ALU.mult, [mag, sn], [vi])
                    kb.TS(vi[:], vi[:], -1.0, None, ALU.mult, None, [vi], [vi])
                    kb.TT(VFr[:], vr[:], fr[:], ALU.mult, [vr, fr], [VFr])
                    kb.TT(t2[:], vi[:], fi[:], ALU.mult, [vi, fi], [t2])
                    kb.TT(VFr[:], VFr[:], t2[:], ALU.subtract, [VFr, t2], [VFr])
                    kb.TT(VFi[:], vr[:], fi[:], ALU.mult, [vr, fi], [VFi])
                    kb.TT(t2[:], vi[:], fr[:], ALU.mult, [vi, fr], [t2])
                    kb.TT(VFi[:], VFi[:], t2[:], ALU.add, [VFi, t2], [VFi])
                with P.scope():
                    dtb = P.sbuf("s_dt2", [128, 16])
                    kb.LD(dtb[:], prm["s5_log_dt"][l][d].partition_broadcast(128), [dtb])
                    kb.ACT(dtb[:], dtb[:], AF.Exp, [dtb], [dtb])
                    lrp = P.sbuf("s_lrp", [128, 16]); lip = P.sbuf("s_lip", [128, 16])
                    for hh in range(2):
                        kb.LD(lrp[64 * hh:64 * hh + 64, :], prm["s5_lam_re"][l][d].rearrange("g p -> p g"), [lrp],
                              allow_slow_non_contiguous=True)
                        kb.LD(lip[64 * hh:64 * hh + 64, :], prm["s5_lam_im"][l][d].rearrange("g p -> p g"), [lip],
                              allow_slow_non_contiguous=True)
                    kb.TT(lrp[:], lrp[:], dtb[:], ALU.mult, [lrp, dtb], [lrp])
                    kb.TT(lip[:], lip[:], dtb[:], ALU.mult, [lip, dtb], [lip])
                    b4 = [P.sbuf("s_b%d" % i, [128, 16, 128]) for i in range(4)]
                    arg, sn2, cs2, tmp2 = b4
                    mt = kb.C("IOTAF" if d == 0 else "R127F")
                    mt_bc = mt.unsqueeze(1).to_broadcast([128, 16, 128])
                    kb.TT(arg[:], lrp[:].unsqueeze(2).to_broadcast([128, 16, 128]), mt_bc, ALU.mult, [lrp], [arg])
                    kb.ACT(T1[:], arg[:], AF.Exp, [arg], [T1])
                    kb.TT(arg[:], lip[:].unsqueeze(2).to_broadcast([128, 16, 128]), mt_bc, ALU.mult, [lip, T1], [arg])
                    _sincos(kb, arg[:], sn2[:], cs2[:], [arg, sn2, cs2, tmp2], tmp2[:])
                    kb.TT(T2[:], T1[:], sn2[:], ALU.mult, [T1, sn2], [T2])
                    kb.TS(T2[:], T2[:], -1.0, None, ALU.mult, None, [T2], [T2])
                    kb.TT(T1[:], T1[:], cs2[:], ALU.mult, [T1, cs2], [T1])
                    c4 = [P.sbuf("s_c%d" % i, [128, 16]) for i in range(4)]
                    kb.ACT(c4[0][:], lrp[:], AF.Exp, [lrp], [c4[0]])
                    _sincos(kb, lip[:], c4[1][:], c4[2][:], [lip, c4[1], c4[2], c4[3]], c4[3][:])
                    kb.TT(AR[:], c4[0][:], c4[2][:], ALU.mult, [c4[0], c4[2]], [AR])
                    kb.TT(NAI[:], c4[0][:], c4[1][:], ALU.mult, [c4[0], c4[1]], [NAI])
                    kb.TS(NAI[:], NAI[:], -1.0, None, ALU.mult, None, [NAI], [NAI])
                sweep_scope = P.scope(); sweep_scope.__enter__()
                uTb = [P.sbuf("s_u%d" % i, [128, 2, 128]) for i in range(2)]
                mm_ = [P.sbuf("s_m%d" % i, [128, 8, 64]) for i in range(4)]
                W3 = [P.sbuf("s_W3%d" % i, [128, 8, 3, 64], BF16) for i in range(2)]
                Hb = [P.sbuf("s_Hb%d" % i, [128, 8, 128], BF16) for i in range(2)]
                tri16 = P.sbuf("s_tri16", [128, 128], BF16)
                kb.CP(tri16[:], kb.C("TRIF" if d == 0 else "TRIB"), [], [tri16])
                tP = [P.sbuf("s_tP%d" % i, [128, 8, 128]) for i in range(2)]
                tPs = [P.sbuf("s_tPs%d" % i, [128, 8, 128]) for i in range(2)]
                H1 = [P.sbuf("s_H1%d" % i, [128, 8, 128]) for i in range(2)]
                H2 = [P.sbuf("s_H2%d" % i, [128, 8, 128]) for i in range(2)]
                hend = P.sbuf("s_hend", [128, 16]); hsend = P.sbuf("s_hsend", [128, 16])
                hp_ = P.sbuf("s_hp", [128, 16]); hps_ = P.sbuf("s_hps", [128, 16])
                sm = [P.sbuf("s_sm%d" % i, [128, 16]) for i in range(4)]
                xps = P.psum("s_xps", [128, 1024])
                pps = P.psum("s_pps", [128, 8, 128])
                ppss = P.psum("s_ppss", [128, 8, 128])
                yps = [P.psum("s_yps%d" % i, [128, 128]) for i in range(2)]
                kb.MS(hp_[:], 0.0, [hp_]); kb.MS(hps_[:], 0.0, [hps_])
                te = 127 if d == 0 else 0
                tri = kb.C("TRIF" if d == 0 else "TRIB")
                it = 0
                for n in ORDER[d]:
                    cols = slice(n * 128, (n + 1) * 128)
                    uT = uTb[it % 2]; it += 1
                    kb.LD(uT[:], kb.ZF[Z_SU:Z_SU + 256, cols].rearrange("(gg p) t -> p gg t", p=128), [uT])
                    for gg in range(2):
                        j = gg
                        for half in range(2):
                            kb.MM(xps[:, half * 512:(half + 1) * 512], uT[:, gg, :],
                                  WX[:, gg, half * 4:(half + 1) * 4, :, :].rearrange("q a r p -> q (a r p)"),
                                  True, True, [uT, WX], [xps])
                        xv = xps[:].rearrange("t (g r p) -> t g r p", r=2, p=64)
                        gs = slice(gg * 8, gg * 8 + 8)
                        kb.TT(mm_[0][:], xv[:, :, 0, :], VFr[:, gs, :], ALU.mult, [xps, VFr], [mm_[0]])
                        kb.TT(mm_[1][:], xv[:, :, 1, :], VFi[:, gs, :], ALU.mult, [xps, VFi], [mm_[1]])
                        kb.TT(mm_[2][:], xv[:, :, 0, :], VFi[:, gs, :], ALU.mult, [xps, VFi], [mm_[2]])
                        kb.TT(mm_[3][:], xv[:, :, 1, :], VFr[:, gs, :], ALU.mult, [xps, VFr], [mm_[3]])
                        w3 = W3[j]
                        kb.TT(w3[:, :, 0, :], mm_[0][:], mm_[1][:], ALU.subtract, [mm_[0], mm_[1]], [w3], eng="pool")
                        kb.TT(w3[:, :, 1, :], mm_[2][:], mm_[3][:], ALU.add, [mm_[2], mm_[3]], [w3], eng="pool")
                        kb.TT(w3[:, :, 2, :], mm_[1][:], mm_[0][:], ALU.subtract, [mm_[0], mm_[1]], [w3], eng="pool")
                        for g8 in range(8):
                            kb.MM(pps[:, g8, :], w3[:, g8, 0:2, :].rearrange("q r p -> q (r p)"), tri16[:], True, True, [w3, tri16], [pps])
                            kb.MM(ppss[:, g8, :], w3[:, g8, 1:3, :].rearrange("q r p -> q (r p)"), tri16[:], True, True, [w3, tri16], [ppss])
                        kb.TT(tP[j][:], pps[:], hp_[:, gs].unsqueeze(2).to_broadcast([128, 8, 128]), ALU.add, [pps, hp_], [tP[j]])
                        kb.TT(tPs[j][:], ppss[:], hps_[:, gs].unsqueeze(2).to_broadcast([128, 8, 128]), ALU.add,
                              [ppss, hps_], [tPs[j]])
                        kb.TT(H1[j][:], tP[j][:], T1[:, gs, :], ALU.mult, [tP[j], T1], [H1[j]], eng="pool")
                        kb.TT(H2[j][:], tPs[j][:], T2[:, gs, :], ALU.mult, [tPs[j], T2], [H2[j]])
                        kb.TT(Hb[j][:], H1[j][:], H2[j][:], ALU.add, [H1[j], H2[j]], [Hb[j]], eng="pool")
                        yp = yps[gg]
                        for g8 in range(8):
                            kb.MM(yp[:], Cb16[:, gg * 8 + g8, :], Hb[j][:, g8, :], g8 == 0, g8 == 7, [Cb16, Hb[j]], [yp])
                        oacc_write(kb, OACC, gg, n, yp, d)
                        kb.TT(hend[:, gs], H1[j][:, :, te], H2[j][:, :, te], ALU.add, [H1[j], H2[j]], [hend])
                        kb.TT(sm[0][:, 0:8], tPs[j][:, :, te], T1[:, gs, te], ALU.mult, [tPs[j], T1], [sm[0]])
                        kb.TT(sm[1][:, 0:8], tP[j][:, :, te], T2[:, gs, te], ALU.mult, [tP[j], T2], [sm[1]])
                        kb.TT(hsend[:, gs], sm[0][:, 0:8], sm[1][:, 0:8], ALU.subtract, [sm[0], sm[1]], [hsend])
                    kb.TT(sm[0][:], hend[:], AR[:], ALU.mult, [hend, AR], [sm[0]])
                    kb.TT(sm[1][:], hsend[:], NAI[:], ALU.mult, [hsend, NAI], [sm[1]])
                    kb.TT(sm[2][:], hsend[:], AR[:], ALU.mult, [hsend, AR], [sm[2]])
                    kb.TT(sm[3][:], hend[:], NAI[:], ALU.mult, [hend, NAI], [sm[3]])
                    kb.TT(hp_[:], sm[0][:], sm[1][:], ALU.add, [sm[0], sm[1]], [hp_])
                    kb.TT(hps_[:], sm[2][:], sm[3][:], ALU.subtract, [sm[2], sm[3]], [hps_])
                sweep_scope.__exit__(None, None, None)
        with P.scope():
            dsk = P.sbuf("s_dsk", [128, 2]); glb = P.sbuf("s_glb", [128, 2])
            kb.LD(dsk[:], prm["s5_d"][l].rearrange("(gg p) -> p gg", p=128), [dsk], allow_slow_non_contiguous=True)
            kb.LD(glb[:], prm["s5_glu_b"][l].rearrange("(gg p) -> p gg", p=128), [glb], allow_slow_non_contiguous=True)
            gw = P.sbuf("s_gw", [128, 2, 256])
            kb.LD(gw[:], prm["s5_glu_w"][l].rearrange("(ct p) o -> p ct o", p=128), [gw])
            uTb = [P.sbuf("s_fu%d" % i, [128, 2, 128]) for i in range(2)]
            yy = [P.sbuf("s_yy%d" % i, [128, 2, 128]) for i in range(2)]
            x2 = [P.sbuf("s_x2%d" % i, [128, 2, 128]) for i in range(2)]
            th = [P.sbuf("s_th%d" % i, [128, 2, 128]) for i in range(2)]
            sgb = [P.sbuf("s_sg%d" % i, [128, 128]) for i in range(2)]
            ob = [P.sbuf("s_ob%d" % i, [128, 128]) for i in range(2)]
            psz = [P.psum("s_psz%d" % i, [128, 128]) for i in range(2)]
            k = 0
            for n in range(NT):
                cols = slice(n * 128, (n + 1) * 128)
                i = n % 2
                kb.LD(uTb[i][:], kb.ZF[Z_SU:Z_SU + 256, cols].rearrange("(gg p) t -> p gg t", p=128), [uTb[i]])
                for gg in range(2):
                    kb.STT(yy[i][:, gg, :], uTb[i][:, gg, :], dsk[:, gg:gg + 1], OACC[:, gg, cols], ALU.mult, ALU.add,
                           [uTb[i], dsk, OACC.s(n)], [yy[i]])
                kb.TT(x2[i][:], yy[i][:], yy[i][:], ALU.mult, [yy[i]], [x2[i]], eng="pool")
                kb.TS(x2[i][:], x2[i][:], 0.044715, 1.0, ALU.mult, ALU.add, [x2[i]], [x2[i]])
                kb.TT(x2[i][:], x2[i][:], yy[i][:], ALU.mult, [x2[i], yy[i]], [x2[i]], eng="pool")
                kb.ACT(th[i][:], x2[i][:], AF.Tanh, [x2[i]], [th[i]], scale=0.7978845608028654)
                kb.TS(th[i][:], th[i][:], 1.0, 0.5, ALU.add, ALU.mult, [th[i]], [th[i]])
                kb.TT(yy[i][:], yy[i][:], th[i][:], ALU.mult, [yy[i], th[i]], [yy[i]], eng="pool")
                for ot in range(2):
                    q = k % 2; k += 1
                    for ct in range(2):
                        kb.MM(psz[q][:], gw[:, ct, ot * 128:(ot + 1) * 128], yy[i][:, ct, :], ct == 0, ct == 1, [gw, yy[i]], [psz[q]])
                    kb.ACT(sgb[q][:], psz[q][:], AF.Sigmoid, [psz[q], glb], [sgb[q]], bias=glb[:, ot:ot + 1])
                    kb.TT(ob[q][:], yy[i][:, ot, :], sgb[q][:], ALU.mult, [yy[i], sgb[q]], [ob[q]])
                    kb.ST(kb.YC[768 + ot * 128:768 + (ot + 1) * 128, cols], ob[q][:], [ob[q]])


def gdn_conv(kb, l):
    P = kb.P
    with P.scope():
        CW = P.sbuf("g_cw", [128, 6, 9])
        for kh in range(3):
            for kw in range(3):
                kb.LD(CW[:, :, kh * 3 + kw], kb.prm["gdn_conv_w"][l][kh, kw].rearrange("(ct p) -> p ct", p=128), [CW],
                      allow_slow_non_contiguous=True)
        mlat = P.sbuf("g_mlat", [128, 2, 512]); mctx = P.sbuf("g_mctx", [128, 2, 256])
        kb.LD(mlat[:], kb.cmlat[:], [mlat]); kb.LD(mctx[:], kb.cmctx[:], [mctx])
        Wb = [P.sbuf("g_w%d" % i, [128, 642]) for i in range(2)]
        acc = [[P.sbuf("g_acc%d%d" % (i, j), [128, 512]) for j in range(3)] for i in range(2)]
        sl = [P.sbuf("g_sl%d" % i, [128, 512]) for i in range(2)]
        sq = [P.sbuf("g_sq%d" % i, [128, 512]) for i in range(2)]
        rt = [P.sbuf("g_rt%d" % i, [128, 512]) for i in range(2)]
        ps = [P.psum("g_psn%d" % i, [128, 512]) for i in range(2)]
        spans = [(0, 256, True)] + [(256 + 512 * k, 512, False) for k in range(8)]
        it = 0
        for (t0, L, is_ctx) in spans:
            lo = 0 if is_ctx else 256
            hi = 256 if is_ctx else S
            a = max(lo, t0 - 65); b = min(hi, t0 + L + 65)
            for ct in range(6):
                i = it % 2; it += 1
                W = Wb[i]
                kb.MS(W[:], 0.0, [W], eng="pool")
                kb.LD(W[:, 65 + (a - t0):65 + (b - t0)], kb.ZF[Z_GQKV + ct * 128:Z_GQKV + (ct + 1) * 128, a:b], [W])
                rows = (1,) if is_ctx else (0, 1, 2)
                masks = mctx if is_ctx else mlat
                for dwi, shift in enumerate((-1, 0, 1)):
                    A = acc[i][dwi]
                    eng = "dve"
                    for q, dh in enumerate(rows):
                        o0 = 65 + 64 * (dh - 1) + shift
                        src = W[:, o0:o0 + L]
                        wcol = CW[:, ct, dh * 3 + dwi:dh * 3 + dwi + 1]
                        if q == 0:
                            kb.TS(A[:, :L], src, wcol, None, ALU.mult, None, [W, CW], [A], eng=("pool" if dwi != 1 else "dve"))
                        else:
                            kb.STT(A[:, :L], src, wcol, A[:, :L], ALU.mult, ALU.add, [W, CW, A], [A])
                    if dwi != 1:
                        mi = 0 if dwi == 0 else 1
                        kb.TT(A[:, :L], A[:, :L], masks[:, mi, :L], ALU.mult, [A, masks], [A], eng="pool")
                A0, A1, A2 = acc[i]
                kb.TT(A1[:, :L], A1[:, :L], A0[:, :L], ALU.add, [A0, A1], [A1], eng="pool")
                kb.TT(A1[:, :L], A1[:, :L], A2[:, :L], ALU.add, [A1, A2], [A1], eng="pool")
                kb.ACT(sl[i][:, :L], A1[:, :L], AF.Silu, [A1], [sl[i]])
                if ct < 4:
                    kb.TT(sq[i][:, :L], sl[i][:, :L], sl[i][:, :L], ALU.mult, [sl[i]], [sq[i]], eng="pool")
                    kb.MM(ps[i][:, :L], kb.C("BLK64"), sq[i][:, :L], True, True, [sq[i]], [ps[i]])
                    kb.ACT(rt[i][:, :L], ps[i][:, :L], AF.Sqrt, [ps[i]], [rt[i]], bias=kb.C("CCOL")[:, 0:1])
                    kb.RECIP(rt[i][:, :L], rt[i][:, :L], [rt[i]], [rt[i]])
                    if ct < 2:
                        kb.STT(sl[i][:, :L], sl[i][:, :L], 0.125, rt[i][:, :L], ALU.mult, ALU.mult, [sl[i], rt[i]], [sl[i]])
                    else:
                        kb.TT(sl[i][:, :L], sl[i][:, :L], rt[i][:, :L], ALU.mult, [sl[i], rt[i]], [sl[i]])
                kb.ST(kb.QKVF[ct * 128:(ct + 1) * 128, t0:t0 + L], sl[i][:, :L], [sl[i]])


def mixer_gdn(kb, l):
    P = kb.P
    gdn_conv(kb, l)
    upto = kb.cfg.get("gdn_upto", 99)
    if upto < 1:
        return
    with P.scope():
        OACC = P.sbuf("g_oacc", [128, 2, S])
        with P.scope():
            DTB = P.sbuf("g_dtb", [128, 8]); NEGA = P.sbuf("g_nega", [128, 8])
            kb.LD(DTB[:], kb.prm["gdn_dt_bias"][l].rearrange("d h -> (d h)").partition_broadcast(128), [DTB])
            kb.LD(NEGA[:], kb.prm["gdn_a_log"][l].rearrange("d h -> (d h)").partition_broadcast(128), [NEGA])
            kb.ACT(NEGA[:], NEGA[:], AF.Exp, [NEGA], [NEGA])
            kb.TS(NEGA[:], NEGA[:], -1.0, None, ALU.mult, None, [NEGA], [NEGA])
            qnb = [P.sbuf("g_q%d" % i, [128, 2, 128]) for i in range(2)]
            knb = [P.sbuf("g_k%d" % i, [128, 2, 128]) for i in range(2)]
            vvb = [P.sbuf("g_v%d" % i, [128, 2, 128]) for i in range(2)]
            gabb = [P.sbuf("g_gab%d" % i, [128, 16]) for i in range(2)]

            def sm4(name, w=4):
                return P.sbuf("g_" + name, [128, w])
            xa, ea, loga, beta, lnb = sm4("xa"), sm4("ea"), sm4("loga"), sm4("beta"), sm4("lnb")
            gtm, ngt, ekr, cdec, eg, beg, gpl = sm4("gtm"), sm4("ngt"), sm4("ekr"), sm4("cdec"), sm4("eg"), sm4("beg"), sm4("gpl")
            ROWS = P.sbuf("g_rows", [4, 384])
            LI = P.sbuf("g_LI", [128, 4, 128]); LBT = P.sbuf("g_LBT", [128, 4, 128]); LBm = P.sbuf("g_LB", [128, 4, 128])
            NAT = P.sbuf("g_NAT", [128, 4, 128]); NA = P.sbuf("g_NA", [128, 4, 128]); QKm = P.sbuf("g_QKm", [128, 4, 128])
            Tm = P.sbuf("g_Tm", [128, 4, 128]); Wm = P.sbuf("g_Wm", [128, 4, 128])
            x1 = P.sbuf("g_x1", [128, 4, 128]); y1 = P.sbuf("g_y1", [128, 4, 128])
            tmx = P.sbuf("g_tmx", [128, 4, 128]); tmy = P.sbuf("g_tmy", [128, 4, 128])
            Rm = [P.sbuf("g_R%d" % h, [128, 128]) for h in range(4)]
            khp = [P.sbuf("g_kh%d" % h, [128, 128]) for h in range(4)]
            vnp = [P.sbuf("g_vn%d" % h, [128, 128]) for h in range(4)]
            for h in range(4):
                kb.MS(khp[h][:], 0.0, [khp[h]], eng="pool")
                kb.MS(vnp[h][:], 0.0, [vnp[h]], eng="pool")
            upair = [P.sbuf("g_up%d" % hp, [128, 128]) for hp in range(2)]
            wTp = [P.sbuf("g_wT%d" % hp, [128, 128]) for hp in range(2)]
            EG = [P.sbuf("g_EG%d" % hp, [128, 128]) for hp in range(2)]
            qd = [P.sbuf("g_qd%d" % hp, [128, 128]) for hp in range(2)]
            cdp = [P.sbuf("g_cdp%d" % hp, [128, 1]) for hp in range(2)]
            Sb = [P.sbuf("g_S%d" % hp, [128, 128]) for hp in range(2)]
            B = [P.psum("g_B%d" % i, [128, 512]) for i in range(8)]
            ident = kb.C("IDENT")
            it = 0
            for d in range(2):
                tri = kb.C("TRIF" if d == 0 else "TRIB")
                rem = kb.C("SUFF" if d == 0 else "PREB")
                n_incl = kb.C("NLE" if d == 0 else "NGE")
                n_strT = kb.C("NLT" if d == 0 else "NGT")
                n_str = kb.C("NGT" if d == 0 else "NLT")
                for hp in range(2):
                    kb.MS(Sb[hp][:], 0.0, [Sb[hp]])
                for n in ORDER[d][:kb.cfg.get("ntiles", NT)]:
                    cols = slice(n * 128, (n + 1) * 128)
                    b = it % 2; it += 1
                    qn, kn, vv, gab = qnb[b], knb[b], vvb[b], gabb[b]
                    kb.LD(qn[:], kb.QKVF[0:256, cols].rearrange("(hp p) t -> p hp t", p=128), [qn])
                    kb.LD(kn[:], kb.QKVF[256:512, cols].rearrange("(hp p) t -> p hp t", p=128), [kn])
                    kb.LD(vv[:], kb.QKVF[512:768, cols].rearrange("(hp p) t -> p hp t", p=128), [vv])
                    kb.LD(gab[:], kb.ZT[cols, 512:528], [gab])
                    kb.TT(xa[:], gab[:, 4 * d:4 * d + 4], DTB[:, 4 * d:4 * d + 4], ALU.add, [gab, DTB], [xa])
                    kb.ACT(ea[:], xa[:], AF.Exp, [xa], [ea])
                    kb.ACT(ea[:], ea[:], AF.Ln, [ea], [ea], bias=kb.C("CCOL")[:, 1:2])
                    kb.TT(loga[:], ea[:], NEGA[:, 4 * d:4 * d + 4], ALU.mult, [ea, NEGA], [loga])
                    kb.ACT(beta[:], gab[:, 8 + 4 * d:12 + 4 * d], AF.Sigmoid, [gab], [beta])
                    kb.ACT(lnb[:], beta[:], AF.Ln, [beta], [lnb])
                    kb.MM(B[0][:, 0:4], tri, loga[:], True, True, [loga], [B[0]])
                    kb.MM(B[0][:, 4:8], rem, loga[:], True, True, [loga], [B[0]])
                    kb.MM(B[0][:, 8:12], kb.C("ONES"), loga[:], True, True, [loga], [B[0]])
                    kb.CP(gtm[:], B[0][:, 0:4], [B[0]], [gtm])
                    kb.TS(ngt[:], B[0][:, 0:4], -1.0, None, ALU.mult, None, [B[0]], [ngt])
                    kb.ACT(ekr[:], B[0][:, 4:8], AF.Exp, [B[0]], [ekr])
                    kb.ACT(cdec[:], B[0][:, 8:12], AF.Exp, [B[0]], [cdec])
                    kb.ACT(eg[:], gtm[:], AF.Exp, [gtm], [eg])
                    kb.TT(beg[:], beta[:], eg[:], ALU.mult, [beta, eg], [beg])
                    kb.TT(gpl[:], gtm[:], lnb[:], ALU.add, [gtm, lnb], [gpl])
                    kb.MM(B[1][0:4, 0:128], loga[:], tri, True, True, [loga], [B[1]])
                    kb.MM(B[1][0:4, 128:256], loga[:], tri, True, False, [loga], [B[1]])
                    kb.MM(B[1][0:4, 128:256], lnb[:], ident, False, True, [lnb], [B[1]])
                    kb.CP(ROWS[:, 0:256], B[1][0:4, 0:256], [B[1]], [ROWS])
                    kb.TS(ROWS[:, 256:384], B[1][0:4, 0:128], -1.0, None, ALU.mult, None, [B[1]], [ROWS])
                    if upto < 2:
                        continue
                    for (dst, rsl, negm, bias_t, bank) in ((LI, slice(0, 128), n_incl, ngt, B[2]),
                                                           (LBT, slice(128, 256), n_strT, ngt, B[3]),
                                                           (LBm, slice(256, 384), n_str, gpl, B[2])):
                        for h in range(4):
                            kb.MM(bank[:, h * 128:(h + 1) * 128], kb.C("SELH%d" % h)[0:4, :], ROWS[:, rsl], True, False,
                                  [ROWS], [bank])
                            kb.MM(bank[:, h * 128:(h + 1) * 128], ident, negm, False, True, [], [bank])
                        for h in range(4):
                            kb.ACT(dst[:, h, :], bank[:, h * 128:(h + 1) * 128], AF.Exp, [bank, bias_t], [dst],
                                   bias=bias_t[:, h:h + 1])
                    if upto < 3:
                        continue
                    for h in range(4):
                        hp, h2 = divmod(h, 2)
                        ksl = kn[64 * h2:64 * h2 + 64, hp, :]
                        kb.MM(B[4][:, h * 128:(h + 1) * 128], ksl, ksl, True, True, [kn], [B[4]])
                        kb.MM(B[5][:, h * 128:(h + 1) * 128], ksl, qn[64 * h2:64 * h2 + 64, hp, :], True, True, [kn, qn], [B[5]])
                    b4v = B[4][:].rearrange("p (h t) -> p h t", h=4)
                    b5v = B[5][:].rearrange("p (h t) -> p h t", h=4)
                    kb.STT(NAT[:], b4v, -1.0, LBT[:], ALU.mult, ALU.mult, [B[4], LBT], [NAT])
                    kb.STT(NA[:], b4v, -1.0, LBm[:], ALU.mult, ALU.mult, [B[4], LBm], [NA])
                    kb.TT(QKm[:], b5v, LI[:], ALU.mult, [B[5], LI], [QKm])
                    if upto < 4:
                        continue
                    idb = ident.unsqueeze(1).to_broadcast([128, 4, 128])
                    kb.CP(Tm[:], idb, [], [Tm])
                    kb.CP(Wm[:], idb, [], [Wm], eng="pool")
                    for s_ in (1, 2, 4, 8, 16, 32, 64):
                        mT = kb.C(("MOFF%d" if d == 0 else "MOFFT%d") % s_).unsqueeze(1).to_broadcast([128, 4, 128])
                        mW = kb.C(("MOFFT%d" if d == 0 else "MOFF%d") % s_).unsqueeze(1).to_broadcast([128, 4, 128])
                        for h in range(4):
                            kb.MM(B[2][:, h * 128:(h + 1) * 128], NAT[:, h, :], Tm[:, h, :], True, True, [NAT, Tm], [B[2]])
                        for h in range(4):
                            kb.MM(B[3][:, h * 128:(h + 1) * 128], NA[:, h, :], Wm[:, h, :], True, True, [NA, Wm], [B[3]])
                        kb.CP(x1[:], B[2][:].rearrange("p (h t) -> p h t", h=4), [B[2]], [x1], eng="act")
                        kb.CP(y1[:], B[3][:].rearrange("p (h t) -> p h t", h=4), [B[3]], [y1], eng="dve")
                        for h in range(4):
                            kb.MM(B[4][:, h * 128:(h + 1) * 128], Wm[:, h, :], x1[:, h, :], True, True, [Wm, x1], [B[4]])
                        for h in range(4):
                            kb.MM(B[5][:, h * 128:(h + 1) * 128], Tm[:, h, :], y1[:, h, :], True, True, [Tm, y1], [B[5]])
                        kb.TT(tmx[:], B[4][:].rearrange("p (h t) -> p h t", h=4), mT, ALU.mult, [B[4]], [tmx])
                        kb.TT(tmy[:], B[5][:].rearrange("p (h t) -> p h t", h=4), mW, ALU.mult, [B[5]], [tmy])
                        kb.TT(Tm[:], Tm[:], tmx[:], ALU.add, [Tm, tmx], [Tm], eng="pool")
                        kb.TT(Wm[:], Wm[:], tmy[:], ALU.add, [Wm, tmy], [Wm], eng="pool")
                    if upto < 5:
                        continue
                    for hp in range(2):
                        kb.TR(B[0][:, 128:256], kn[:, hp, :], ident, [kn], [B[0]])
                        kb.TR(B[0][:, 256:384], vv[:, hp, :], ident, [vv], [B[0]])
                        for h2 in range(2):
                            h = 2 * hp + h2
                            kc = slice(64 * h2, 64 * h2 + 64)
                            vc = slice(64 * (1 - h2), 64 * (1 - h2) + 64)
                            kb.TS(Rm[h][:, kc], B[0][:, 128 + 64 * h2:128 + 64 * h2 + 64], beg[:, h:h + 1], None, ALU.mult, None,
                                  [B[0], beg], [Rm[h]])
                            kb.ACT(Rm[h][:, vc], B[0][:, 256 + 64 * h2:256 + 64 * h2 + 64], AF.Copy, [B[0], beta], [Rm[h]],
                                   scale=beta[:, h:h + 1])
                            kb.ACT(khp[h][:, kc], B[0][:, 128 + 64 * h2:128 + 64 * h2 + 64], AF.Copy, [B[0], ekr], [khp[h]],
                                   scale=ekr[:, h:h + 1])
                    if upto < 6:
                        continue
                    for h in range(4):
                        kb.MM(B[2][:, h * 128:(h + 1) * 128], Wm[:, h, :], Rm[h][:], True, True, [Wm, Rm[h]], [B[2]])
                        kb.MM(B[3][:, h * 128:(h + 1) * 128], Rm[h][:], Wm[:, h, :], True, True, [Wm, Rm[h]], [B[3]])
                    for h in range(4):
                        hp, h2 = divmod(h, 2)
                        vc0 = 64 * (1 - h2)
                        kb.CP(upair[hp][:, 64 * h2:64 * h2 + 64], B[2][:, h * 128 + vc0:h * 128 + vc0 + 64], [B[2]], [upair[hp]],
                              eng=("act" if h2 else "dve"))
                        kb.CP(wTp[hp][64 * h2:64 * h2 + 64, :], B[3][64 * h2:64 * h2 + 64, h * 128:(h + 1) * 128], [B[3]], [wTp[hp]],
                              eng=("dve" if h2 else "act"))
                    if upto < 7:
                        continue
                    for hp in range(2):
                        kb.MM(B[1][:, 256:384], kb.C("SELP%d" % hp)[0:4, :], ROWS[:, 0:128], True, True, [ROWS], [B[1]])
                        kb.ACT(EG[hp][:], B[1][:, 256:384], AF.Exp, [B[1]], [EG[hp]])
                        kb.TT(qd[hp][:], qn[:, hp, :], EG[hp][:], ALU.mult, [qn, EG[hp]], [qd[hp]], eng="pool")
                        pws = B[7][:, hp * 128:(hp + 1) * 128]
                        kb.MM(pws, wTp[hp][:], Sb[hp][:], True, True, [wTp[hp], Sb[hp]], [B[7]])
                        for h2 in range(2):
                            h = 2 * hp + h2
                            cs_ = slice(64 * h2, 64 * h2 + 64)
                            kb.TT(vnp[h][:, cs_], upair[hp][:, cs_], B[7][:, hp * 128 + 64 * h2:hp * 128 + 64 * h2 + 64],
                                  ALU.subtract, [upair[hp], B[7]], [vnp[h]])
                        po = B[6][:, hp * 256:hp * 256 + 128]
                        kb.MM(po, Sb[hp][:], qd[hp][:], True, False, [Sb[hp], qd[hp]], [B[6].s(hp)])
                        kb.MM(po, vnp[2 * hp][:], QKm[:, 2 * hp, :], False, False, [vnp[2 * hp], QKm], [B[6].s(hp)])
                        kb.MM(po, vnp[2 * hp + 1][:], QKm[:, 2 * hp + 1, :], False, True, [vnp[2 * hp + 1], QKm], [B[6].s(hp)])
                        cols_ = slice(n * 128, (n + 1) * 128)
                        if d == 0:
                            kb.CP(OACC[:, hp, cols_], po, [B[6].s(hp)], [OACC.s(n)], eng="act")
                        else:
                            kb.TT(OACC[:, hp, cols_], OACC[:, hp, cols_], po, ALU.add, [B[6].s(hp)], [OACC.s(n)])
                        pkv = B[6][:, hp * 256 + 128:hp * 256 + 256]
                        kb.MM(pkv, khp[2 * hp][:], vnp[2 * hp][:], True, False, [khp[2 * hp], vnp[2 * hp]], [B[6].s(2 + hp)])
                        kb.MM(pkv, khp[2 * hp + 1][:], vnp[2 * hp + 1][:], False, True, [khp[2 * hp + 1], vnp[2 * hp + 1]],
                              [B[6].s(2 + hp)])
                        kb.CP(cdp[hp][0:64, :], cdec[0:64, 2 * hp:2 * hp + 1], [cdec], [cdp[hp]])
                        kb.CP(cdp[hp][64:128, :], cdec[64:128, 2 * hp + 1:2 * hp + 2], [cdec], [cdp[hp]])
                        kb.STT(Sb[hp][:], Sb[hp][:], cdp[hp][:, 0:1], pkv, ALU.mult, ALU.add,
                               [Sb[hp], cdp[hp], B[6].s(2 + hp)], [Sb[hp]])
        with P.scope():
            G = P.sbuf("g_G2", [128, 1])
            for hh in range(2):
                kb.LD(G[64 * hh:64 * hh + 64, :], kb.prm["gdn_norm_g"][l].rearrange("(p o) -> p o", o=1), [G])
            finalize_gated(kb, OACC, Z_GG, G, 512, "g_")


class _Ctx:
    pass


def mixer_gdn2(kb, l):
    P = kb.P
    gdn_conv(kb, l)
    with P.scope():
        OACC = P.sbuf("g_oacc", [128, 2, S])
        kb.MS(OACC[:, 0, :], 0.0, [OACC.s(n) for n in range(NT)], eng="pool")
        kb.MS(OACC[:, 1, :], 0.0, [OACC.s(n) for n in range(NT)], eng="pool")
        with P.scope():
            DTB = P.sbuf("g_dtb", [128, 8]); NEGA = P.sbuf("g_nega", [128, 8])
            kb.LD(DTB[:], kb.prm["gdn_dt_bias"][l].rearrange("d h -> (d h)").partition_broadcast(128), [DTB])
            kb.LD(NEGA[:], kb.prm["gdn_a_log"][l].rearrange("d h -> (d h)").partition_broadcast(128), [NEGA])
            kb.ACT(NEGA[:], NEGA[:], AF.Exp, [NEGA], [NEGA])
            kb.TS(NEGA[:], NEGA[:], -1.0, None, ALU.mult, None, [NEGA], [NEGA])
            ident = kb.C("IDENT")
            idb = ident.unsqueeze(1).to_broadcast([128, 4, 128])
            cxs = []
            for d in range(2):
                cx = _Ctx()
                cx.d = d
                pf = "g%d_" % d
                cx.qnb = [P.sbuf(pf + "q%d" % i, [128, 2, 128]) for i in range(2)]
                cx.knb = [P.sbuf(pf + "k%d" % i, [128, 2, 128]) for i in range(2)]
                cx.vvb = [P.sbuf(pf + "v%d" % i, [128, 2, 128]) for i in range(2)]
                cx.gabb = [P.sbuf(pf + "gab%d" % i, [128, 16]) for i in range(2)]
                for nm in ("xa", "ea", "loga", "beta", "lnb", "gtm", "ngt", "ekr", "cdec", "eg", "beg", "gpl"):
                    setattr(cx, nm, P.sbuf(pf + nm, [128, 4]))
                cx.ROWS = P.sbuf(pf + "rows", [4, 384])
                for nm in ("LI", "LBT", "LBm", "QKm"):
                    setattr(cx, nm, P.sbuf(pf + nm, [128, 4, 128]))
                for nm in ("NAT", "NA", "Tm", "Wm", "x1", "y1", "tmx", "tmy"):
                    setattr(cx, nm, P.sbuf(pf + nm, [128, 4, 128], BF16))
                cx.Rm = [P.sbuf(pf + "R%d" % h, [128, 128], BF16) for h in range(4)]
                cx.khp = [P.sbuf(pf + "kh%d" % h, [128, 128]) for h in range(4)]
                cx.vnp = [P.sbuf(pf + "vn%d" % h, [128, 128]) for h in range(4)]
                for h in range(4):
                    kb.MS(cx.khp[h][:], 0.0, [cx.khp[h]], eng="pool")
                    kb.MS(cx.vnp[h][:], 0.0, [cx.vnp[h]], eng="pool")
                cx.upair = [P.sbuf(pf + "up%d" % hp, [128, 128]) for hp in range(2)]
                cx.wTp = [P.sbuf(pf + "wT%d" % hp, [128, 128]) for hp in range(2)]
                cx.EG = [P.sbuf(pf + "EG%d" % hp, [128, 128]) for hp in range(2)]
                cx.qd = [P.sbuf(pf + "qd%d" % hp, [128, 128]) for hp in range(2)]
                cx.cdp = [P.sbuf(pf + "cdp%d" % hp, [128, 1]) for hp in range(2)]
                cx.Sb = [P.sbuf(pf + "S%d" % hp, [128, 128]) for hp in range(2)]
                for hp in range(2):
                    kb.MS(cx.Sb[hp][:], 0.0, [cx.Sb[hp]])
                cx.B = [P.psum(pf + "B%d" % i, [128, 512]) for i in range(4)]
                cx.tri = kb.C("TRIF" if d == 0 else "TRIB")
                cx.rem = kb.C("SUFF" if d == 0 else "PREB")
                cx.n_incl = kb.C("NLE" if d == 0 else "NGE")
                cx.n_strT = kb.C("NLT" if d == 0 else "NGT")
                cx.n_str = kb.C("NGT" if d == 0 else "NLT")
                cx.it = 0
                cx.mT = {}; cx.mW = {}
                for s_ in (2, 4, 8, 16, 32, 64):
                    for nm_, dct, cn in (("mT", cx.mT, ("MOFF%d" if d == 0 else "MOFFT%d") % s_),
                                         ("mW", cx.mW, ("MOFFT%d" if d == 0 else "MOFF%d") % s_)):
                        mt_ = P.sbuf(pf + nm_ + str(s_), [128, 4, 128], mybir.dt.uint8)
                        kb.CP(mt_[:], kb.C(cn).unsqueeze(1).to_broadcast([128, 4, 128]), [], [mt_])
                        dct[s_] = mt_
                cxs.append(cx)

            def step(cx, n):
                d = cx.d
                Pa, Pb, Pc, Pd = cx.B
                cols = slice(n * 128, (n + 1) * 128)
                b = cx.it % 2; cx.it += 1
                qn, kn, vv, gab = cx.qnb[b], cx.knb[b], cx.vvb[b], cx.gabb[b]
                xa, ea, loga, beta, lnb = cx.xa, cx.ea, cx.loga, cx.beta, cx.lnb
                gtm, ngt, ekr, cdec, eg, beg, gpl = cx.gtm, cx.ngt, cx.ekr, cx.cdec, cx.eg, cx.beg, cx.gpl
                ROWS, LI, LBT, LBm, NAT, NA, QKm = cx.ROWS, cx.LI, cx.LBT, cx.LBm, cx.NAT, cx.NA, cx.QKm
                Tm, Wm, x1, y1, tmx, tmy = cx.Tm, cx.Wm, cx.x1, cx.y1, cx.tmx, cx.tmy
                Rm, khp, vnp, upair, wTp, EG, qd, cdp, Sb = cx.Rm, cx.khp, cx.vnp, cx.upair, cx.wTp, cx.EG, cx.qd, cx.cdp, cx.Sb
                tri = cx.tri
                kb.LD(qn[:], kb.QKVF[0:256, cols].rearrange("(hp p) t -> p hp t", p=128), [qn])
                kb.LD(kn[:], kb.QKVF[256:512, cols].rearrange("(hp p) t -> p hp t", p=128), [kn])
                kb.LD(vv[:], kb.QKVF[512:768, cols].rearrange("(hp p) t -> p hp t", p=128), [vv])
                kb.LD(gab[:], kb.ZT[cols, 512:528], [gab])
                kb.TT(xa[:], gab[:, 4 * d:4 * d + 4], DTB[:, 4 * d:4 * d + 4], ALU.add, [gab, DTB], [xa])
                kb.ACT(ea[:], xa[:], AF.Exp, [xa], [ea])
                kb.ACT(ea[:], ea[:], AF.Ln, [ea], [ea], bias=kb.C("CCOL")[:, 1:2])
                kb.TT(loga[:], ea[:], NEGA[:, 4 * d:4 * d + 4], ALU.mult, [ea, NEGA], [loga])
                kb.ACT(beta[:], gab[:, 8 + 4 * d:12 + 4 * d], AF.Sigmoid, [gab], [beta])
                kb.ACT(lnb[:], beta[:], AF.Ln, [beta], [lnb])
                kb.MM(Pc[:, 0:4], tri, loga[:], True, True, [loga], [Pc])
                kb.MM(Pc[:, 4:8], cx.rem, loga[:], True, True, [loga], [Pc])
                kb.MM(Pc[:, 8:12], kb.C("ONES"), loga[:], True, True, [loga], [Pc])
                kb.CP(gtm[:], Pc[:, 0:4], [Pc], [gtm])
                kb.TS(ngt[:], Pc[:, 0:4], -1.0, None, ALU.mult, None, [Pc], [ngt])
                kb.ACT(ekr[:], Pc[:, 4:8], AF.Exp, [Pc], [ekr])
                kb.ACT(cdec[:], Pc[:, 8:12], AF.Exp, [Pc], [cdec])
                kb.ACT(eg[:], gtm[:], AF.Exp, [gtm], [eg])
                kb.TT(beg[:], beta[:], eg[:], ALU.mult, [beta, eg], [beg])
                kb.TT(gpl[:], gtm[:], lnb[:], ALU.add, [gtm, lnb], [gpl])
                kb.MM(Pd[0:4, 0:128], loga[:], tri, True, True, [loga], [Pd])
                kb.MM(Pd[0:4, 128:256], loga[:], tri, True, False, [loga], [Pd])
                kb.MM(Pd[0:4, 128:256], lnb[:], ident, False, True, [lnb], [Pd])
                kb.CP(ROWS[:, 0:256], Pd[0:4, 0:256], [Pd], [ROWS])
                kb.TS(ROWS[:, 256:384], Pd[0:4, 0:128], -1.0, None, ALU.mult, None, [Pd], [ROWS])
                yield
                for (dst, rsl, negm, bias_t, bank) in ((LI, slice(0, 128), cx.n_incl, ngt, Pa),
                                                       (LBT, slice(128, 256), cx.n_strT, ngt, Pb),
                                                       (LBm, slice(256, 384), cx.n_str, gpl, Pa)):
                    for h in range(4):
                        kb.MM(bank[:, h * 128:(h + 1) * 128], kb.C("SELH%d" % h)[0:4, :], ROWS[:, rsl], True, False, [ROWS], [bank])
                        kb.MM(bank[:, h * 128:(h + 1) * 128], ident, negm, False, True, [], [bank])
                    yield
                    for h in range(4):
                        kb.ACT(dst[:, h, :], bank[:, h * 128:(h + 1) * 128], AF.Exp, [bank, bias_t], [dst], bias=bias_t[:, h:h + 1])
                    yield
                for h in range(4):
                    hp, h2 = divmod(h, 2)
                    ksl = kn[64 * h2:64 * h2 + 64, hp, :]
                    kb.MM(Pa[:, h * 128:(h + 1) * 128], ksl, ksl, True, True, [kn], [Pa])
                    kb.MM(Pb[:, h * 128:(h + 1) * 128], ksl, qn[64 * h2:64 * h2 + 64, hp, :], True, True, [kn, qn], [Pb])
                pav = Pa[:].rearrange("p (h t) -> p h t", h=4)
                pbv = Pb[:].rearrange("p (h t) -> p h t", h=4)
                kb.STT(NAT[:], pav, -1.0, LBT[:], ALU.mult, ALU.mult, [Pa, LBT], [NAT])
                kb.STT(NA[:], pav, -1.0, LBm[:], ALU.mult, ALU.mult, [Pa, LBm], [NA])
                kb.TT(QKm[:], pbv, LI[:], ALU.mult, [Pb, LI], [QKm])
                yield
                mT = kb.C("MOFF1" if d == 0 else "MOFFT1").unsqueeze(1).to_broadcast([128, 4, 128])
                mW = kb.C("MOFFT1" if d == 0 else "MOFF1").unsqueeze(1).to_broadcast([128, 4, 128])
                kb.TT(tmx[:], NA[:], mT, ALU.mult, [NA], [tmx], eng="pool")
                kb.TT(tmy[:], NAT[:], mW, ALU.mult, [NAT], [tmy], eng="pool")
                kb.TT(Tm[:], tmx[:], idb, ALU.add, [tmx], [Tm], eng="pool")
                kb.TT(Wm[:], tmy[:], idb, ALU.add, [tmy], [Wm], eng="pool")
                yield
                for s_ in (2, 4, 8, 16, 32, 64):
                    for h in range(4):
                        kb.MM(Pa[:, h * 128:(h + 1) * 128], NAT[:, h, :], Tm[:, h, :], True, True, [NAT, Tm], [Pa])
                    for h in range(4):
                        kb.MM(Pb[:, h * 128:(h + 1) * 128], NA[:, h, :], Wm[:, h, :], True, True, [NA, Wm], [Pb])
                    yield
                    kb.CP(x1[:], pav, [Pa], [x1], eng="act")
                    kb.CP(y1[:], pbv, [Pb], [y1], eng="act")
                    yield
                    for h in range(4):
                        kb.MM(Pa[:, h * 128:(h + 1) * 128], Wm[:, h, :], x1[:, h, :], True, True, [Wm, x1], [Pa])
                    for h in range(4):
                        kb.MM(Pb[:, h * 128:(h + 1) * 128], Tm[:, h, :], y1[:, h, :], True, True, [Tm, y1], [Pb])
                    yield
                    kb.CPRED(Tm[:], cx.mT[s_][:], pav, [Pa, cx.mT[s_]], [Tm])
                    kb.CPRED(Wm[:], cx.mW[s_][:], pbv, [Pb, cx.mW[s_]], [Wm])
                    yield
                for hp in range(2):
                    kb.TR(Pc[:, 128:256], kn[:, hp, :], ident, [kn], [Pc])
                    kb.TR(Pc[:, 256:384], vv[:, hp, :], ident, [vv], [Pc])
                    for h2 in range(2):
                        h = 2 * hp + h2
                        kc = slice(64 * h2, 64 * h2 + 64)
                        vc = slice(64 * (1 - h2), 64 * (1 - h2) + 64)
                        kb.TS(Rm[h][:, kc], Pc[:, 128 + 64 * h2:128 + 64 * h2 + 64], beg[:, h:h + 1], None, ALU.mult, None,
                              [Pc, beg], [Rm[h]])
                        kb.ACT(Rm[h][:, vc], Pc[:, 256 + 64 * h2:256 + 64 * h2 + 64], AF.Copy, [Pc, beta], [Rm[h]],
                               scale=beta[:, h:h + 1])
                        kb.ACT(khp[h][:, kc], Pc[:, 128 + 64 * h2:128 + 64 * h2 + 64], AF.Copy, [Pc, ekr], [khp[h]],
                               scale=ekr[:, h:h + 1])
                    yield
                for h in range(4):
                    kb.MM(Pa[:, h * 128:(h + 1) * 128], Wm[:, h, :], Rm[h][:], True, True, [Wm, Rm[h]], [Pa])
                    kb.MM(Pb[:, h * 128:(h + 1) * 128], Rm[h][:], Wm[:, h, :], True, True, [Wm, Rm[h]], [Pb])
                for h in range(4):
                    hp, h2 = divmod(h, 2)
                    vc0 = 64 * (1 - h2)
                    kb.CP(upair[hp][:, 64 * h2:64 * h2 + 64], Pa[:, h * 128 + vc0:h * 128 + vc0 + 64], [Pa], [upair[hp]], eng="dve")
                    kb.CP(wTp[hp][64 * h2:64 * h2 + 64, :], Pb[64 * h2:64 * h2 + 64, h * 128:(h + 1) * 128], [Pb], [wTp[hp]], eng="act")
                yield
                for hp in range(2):
                    kb.MM(Pc[:, 384:512], kb.C("SELP%d" % hp)[0:4, :], ROWS[:, 0:128], True, True, [ROWS], [Pc])
                    kb.ACT(EG[hp][:], Pc[:, 384:512], AF.Exp, [Pc], [EG[hp]])
                    kb.TT(qd[hp][:], qn[:, hp, :], EG[hp][:], ALU.mult, [qn, EG[hp]], [qd[hp]], eng="pool")
                    pws = Pc[:, hp * 128:(hp + 1) * 128]
                    kb.MM(pws, wTp[hp][:], Sb[hp][:], True, True, [wTp[hp], Sb[hp]], [Pc])
                    for h2 in range(2):
                        h = 2 * hp + h2
                        cs_ = slice(64 * h2, 64 * h2 + 64)
                        kb.TT(vnp[h][:, cs_], upair[hp][:, cs_], Pc[:, hp * 128 + 64 * h2:hp * 128 + 64 * h2 + 64],
                              ALU.subtract, [upair[hp], Pc], [vnp[h]])
                    po = Pd[:, hp * 256:hp * 256 + 128]
                    kb.MM(po, Sb[hp][:], qd[hp][:], True, False, [Sb[hp], qd[hp]], [Pd])
                    kb.MM(po, vnp[2 * hp][:], QKm[:, 2 * hp, :], False, False, [vnp[2 * hp], QKm], [Pd])
                    kb.MM(po, vnp[2 * hp + 1][:], QKm[:, 2 * hp + 1, :], False, True, [vnp[2 * hp + 1], QKm], [Pd])
                    kb.TT(OACC[:, hp, cols], OACC[:, hp, cols], po, ALU.add, [Pd], [OACC.s(n)])
                    pkv = Pd[:, hp * 256 + 128:hp * 256 + 256]
                    kb.MM(pkv, khp[2 * hp][:], vnp[2 * hp][:], True, False, [khp[2 * hp], vnp[2 * hp]], [Pd])
                    kb.MM(pkv, khp[2 * hp + 1][:], vnp[2 * hp + 1][:], False, True, [khp[2 * hp + 1], vnp[2 * hp + 1]], [Pd])
                    kb.CP(cdp[hp][0:64, :], cdec[0:64, 2 * hp:2 * hp + 1], [cdec], [cdp[hp]])
                    kb.CP(cdp[hp][64:128, :], cdec[64:128, 2 * hp + 1:2 * hp + 2], [cdec], [cdp[hp]])
                    kb.STT(Sb[hp][:], Sb[hp][:], cdp[hp][:, 0:1], pkv, ALU.mult, ALU.add, [Sb[hp], cdp[hp], Pd], [Sb[hp]])
                    yield

            def stream(cx):
                for n in ORDER[cx.d][:kb.cfg.get("ntiles", NT)]:
                    yield from step(cx, n)
            active = [stream(cxs[0]), stream(cxs[1])]
            while active:
                for g_ in list(active):
                    try:
                        next(g_)
                    except StopIteration:
                        active.remove(g_)
        with P.scope():
            G = P.sbuf("g_G2", [128, 1])
            for hh in range(2):
                kb.LD(G[64 * hh:64 * hh + 64, :], kb.prm["gdn_norm_g"][l].rearrange("(p o) -> p o", o=1), [G])
            finalize_gated(kb, OACC, Z_GG, G, 512, "g_")
```

```python
import numpy as np
import concourse.bass as bass
import concourse.mybir as mybir
from concourse.bass_utils import run_bass_kernel_spmd
from contextlib import ExitStack

F32 = mybir.dt.float32
BF16 = mybir.dt.bfloat16
AF = mybir.ActivationFunctionType
ALU = mybir.AluOpType

ENGS = ("pe", "act", "dve", "pool", "sp")
EPOCH = 16000
N_DMA_SEM = 32


class Buf:
    __slots__ = ("name", "w", "r", "excl", "pe_partial")

    def __init__(self, name="", excl=False):
        self.name = name
        self.w = None
        self.r = []
        self.excl = excl
        self.pe_partial = False


class T:
    def __init__(self, h, name, excl=False):
        self.h = h
        self.name = name
        self.b = Buf(name, excl)
        self.excl = excl
        self.subs = {}

    def __getitem__(self, k):
        return self.h[k]

    def s(self, key):
        if self.excl:
            return self.b
        if key not in self.subs:
            self.subs[key] = Buf("%s.%s" % (self.name, key))
        return self.subs[key]


class Prog:
    def __init__(self, nc):
        self.nc = nc
        self.es = ExitStack()
        self.stack = [self.es]
        self.ops = {e: [] for e in ENGS}
        self.cnt = {e: 0 for e in ENGS}
        self.seen = {e: {} for e in ENGS}
        self.last = {}
        self.dma_k = 0
        self.dma_use = [0] * N_DMA_SEM
        self.dma_sems = [self.es.enter_context(nc.semaphore("dq%d" % i)) for i in range(N_DMA_SEM)]
        self.eng_sems = {}
        self.out_tokens = []
        self.n_ops = 0
        self.uid = 0

    def _nm(self, name):
        self.uid += 1
        return "%s_%d" % (name, self.uid)

    def sbuf(self, name, shape, dt=F32):
        h = self.stack[-1].enter_context(self.nc.sbuf_tensor(self._nm(name), list(shape), dt))
        return T(h, name)

    def psum(self, name, shape, dt=F32):
        n = 1
        for d_ in shape[1:]:
            n *= d_
        nb = (n * 4 + 2047) // 2048
        h = self.stack[-1].enter_context(self.nc.psum_tensor(self._nm(name), [128, nb * 512], F32))
        v = h[0:shape[0], 0:n]
        if len(shape) == 3:
            v = v.rearrange("p (a b) -> p a b", a=shape[1])
        elif len(shape) == 4:
            v = v.rearrange("p (a b c) -> p a b c", a=shape[1], b=shape[2])
        return T(v, name, excl=True)

    def dram(self, name, shape, dt=F32, kind="Internal"):
        h = self.nc.dram_tensor(name, list(shape), dt, kind=kind)
        return T(h.ap(), name)

    class _Scope:
        def __init__(self, p):
            self.p = p

        def __enter__(self):
            st = ExitStack()
            self.p.stack.append(st)
            return st

        def __exit__(self, *a):
            self.p.barrier()
            st = self.p.stack.pop()
            st.close()
            return False

    def scope(self):
        return Prog._Scope(self)

    def _eng_sem(self, e, epoch):
        k = (e, epoch)
        if k not in self.eng_sems:
            self.eng_sems[k] = self.es.enter_context(self.nc.semaphore("s_%s_%d" % (e, epoch)))
        return self.eng_sems[k]

    def _waits(self, eng, reads, writes, extra=(), skip_pe=False):
        need = {}

        def add(tok):
            if tok is None:
                return
            key, val = tok
            if need.get(key, 0) < val:
                need[key] = val
        for b in reads:
            add(b.w)
        for b in writes:
            add(b.w)
            for t in b.r:
                add(t)
        for t in extra:
            add(t)
        out = []
        seen = self.seen[eng]
        for key, val in need.items():
            if skip_pe and key[0] == "e" and key[1] == "pe":
                continue
            if seen.get(key, 0) < val:
                seen[key] = val
                out.append((key, val))
        return out

    @staticmethod
    def _bufs(xs):
        out = []
        for x in xs:
            if x is None:
                continue
            out.append(x.b if isinstance(x, T) else x)
        return out

    def _commit(self, tok, reads, writes):
        self.last[tok[0]] = tok[1]
        for b in reads:
            b.r.append(tok)
            if len(b.r) > 64:
                mx = {}
                for k, v in b.r:
                    if mx.get(k, 0) < v:
                        mx[k] = v
                b.r = list(mx.items())
        for b in writes:
            b.w = tok
            b.r = []
        self.n_ops += 1

    def op(self, eng, fn, reads=(), writes=(), partial=False):
        reads = self._bufs(reads)
        writes = self._bufs(writes)
        ex = [b for b in reads if b.excl]
        if ex:
            reads = [b for b in reads if not b.excl]
            writes = writes + [b for b in ex if b not in writes]
        skip_pe = False
        if eng == "pe":
            skip_pe = (not partial) and all(not b.pe_partial for b in writes)
            for b in writes:
                b.pe_partial = partial
        waits = self._waits(eng, reads, writes, skip_pe=skip_pe)
        self.cnt[eng] += 1
        epoch, val = divmod(self.cnt[eng] - 1, EPOCH)
        tok = (("e", eng, epoch), val + 1)
        self.ops[eng].append((waits, fn, tok))
        self._commit(tok, reads, writes)
        return tok

    def dma(self, out_ap, in_ap, reads=(), writes=(), q="sp", is_output=False, **kw):
        reads = self._bufs(reads)
        writes = self._bufs(writes)
        i = self.dma_k % N_DMA_SEM
        self.dma_k += 1
        prev = self.dma_use[i]
        extra = [(("d", i), 16 * prev)] if prev else []
        waits = self._waits(q, reads, writes, extra)
        self.dma_use[i] = prev + 1
        tok = (("d", i), 16 * (prev + 1))

        def fn(e):
            return e.dma_start(out=out_ap, in_=in_ap, **kw)
        self.ops[q].append((waits, fn, tok))
        self._commit(tok, reads, writes)
        if is_output:
            self.out_tokens.append(tok)
        return tok

    def barrier(self):
        toks = list(self.last.items())
        for e in ENGS:
            waits = self._waits(e, [], [], toks)
            if waits:
                self.ops[e].append((waits, None, None))

    def _sem_of(self, key):
        if key[0] == "d":
            return self.dma_sems[key[1]]
        return self._eng_sem(key[1], key[2])

    def emit(self):
        nc = self.nc
        self.barrier()
        for e in ENGS:
            for waits, fn, tok in self.ops[e]:
                if tok is not None:
                    self._sem_of(tok[0])
                for key, val in waits:
                    self._sem_of(key)
        with nc.Block() as block:
            def run(e, handle):
                for waits, fn, tok in self.ops[e]:
                    for key, val in waits:
                        handle.wait_ge(self._sem_of(key), val)
                    if fn is None:
                        continue
                    ins = fn(handle)
                    key, val = tok
                    ins.then_inc(self._sem_of(key), 16 if key[0] == "d" else 1)

            @block.sync
            def _(h):
                run("sp", h)

            @block.tensor
            def _(h):
                run("pe", h)

            @block.scalar
            def _(h):
                run("act", h)

            @block.vector
            def _(h):
                run("dve", h)

            @block.gpsimd
            def _(h):
                run("pool", h)

    def close(self):
        self.es.close()


D = 1024
S = 4352
NT = 34
LAT0 = 256
DEPTH = 2
EPS = 1e-6
NEG = -30000.0
ORDER = [list(range(NT)), [1, 0] + list(range(NT - 1, 1, -1))]

C_HQ, C_HI, C_HG, C_HFF, C_HFB = 0, 256, 512, 768, 1024
C_RQ, C_RK, C_RV, C_RG = 1280, 1536, 1792, 2048
C_GQKV, C_GG, C_GA, C_GB, C_SU = 2304, 3072, 3328, 3336, 3344
Z_HQ, Z_HG, Z_HFF, Z_HFB, Z_RQ, Z_RK, Z_RG, Z_GQKV, Z_GG, Z_SU = 0, 256, 512, 768, 1024, 1280, 1536, 1792, 2560, 2816
NZF = 3072
FM_MAP = [(Z_HQ, C_HQ, 256), (Z_HG, C_HG, 256), (Z_HFF, C_HFF, 256), (Z_HFB, C_HFB, 256), (Z_RQ, C_RQ, 256),
          (Z_RK, C_RK, 256), (Z_RG, C_RG, 256), (Z_GQKV, C_GQKV, 768), (Z_GG, C_GG, 256), (Z_SU, C_SU, 256)]
FM_BLOCKS = [(zr + i, wc + i) for zr, wc, n in FM_MAP for i in range(0, n, 128)]
NZT = 528

CN = {}


def _const_pack():
    mats = []

    def add(name, m):
        CN[name] = len(mats)
        mats.append(np.asarray(m, np.float32))
    p = np.arange(128)[:, None]
    f = np.arange(128)[None, :]
    add("IDENT", (p == f))
    add("ONES", np.ones((128, 128)))
    add("TRIF", (p <= f))
    add("TRIB", (p >= f))
    add("SUFF", (p > f))
    add("PREB", (p < f))
    add("NLE", np.where(p <= f, 0.0, NEG))
    add("NLT", np.where(p < f, 0.0, NEG))
    add("NGE", np.where(p >= f, 0.0, NEG))
    add("NGT", np.where(p > f, 0.0, NEG))
    for s in (1, 2, 4, 8, 16, 32, 64):
        m = (((p // s) % 2) == 1) & ((f // s) == (p // s) - 1)
        add("MOFF%d" % s, m)
        add("MOFFT%d" % s, m.T)
    add("BLK64", (p // 64) == (f // 64))
    rot = np.zeros((128, 128))
    for m in range(128):
        if (m % 64) < 32:
            rot[m + 32, m] = -1.0
        else:
            rot[m - 32, m] = 1.0
    add("ROT", rot)
    add("IOTAF", np.broadcast_to(f, (128, 128)))
    add("IOTAF1", np.broadcast_to(f + 1, (128, 128)))
    add("RIOTAF", np.broadcast_to(128 - f, (128, 128)))
    add("R127F", np.broadcast_to(127 - f, (128, 128)))
    add("DIFF", f - p)
    add("NDIFF", p - f)
    for h in range(4):
        m = np.zeros((128, 128)); m[h, :] = 1.0
        add("SELH%d" % h, m)
    for hp in range(2):
        m = np.zeros((128, 128)); m[2 * hp, 0:64] = 1.0; m[2 * hp + 1, 64:128] = 1.0
        add("SELP%d" % hp, m)
    cc = np.zeros((128, 128))
    cc[:, 0] = EPS; cc[:, 1] = 1.0; cc[:, 2] = np.arange(128); cc[:, 3] = 127 - np.arange(128)
    cc[:, 5] = -np.pi; cc[:, 6] = -np.arange(128); cc[:, 7] = -(127 - np.arange(128))
    add("CCOL", cc)
    gm = np.zeros((128, 128))
    for g in range(16):
        gm[(g % 8) * 16:(g % 8) * 16 + 16, g] = 1.0
    add("GMASK", gm)
    return np.concatenate(mats, axis=1)


CONST_NP = _const_pack()
NCONST = CONST_NP.shape[1] // 128


def _rope_tables():
    half = 32
    inv = 10000.0 ** (-np.arange(half, dtype=np.float64) / half)
    pos = np.arange(S, dtype=np.float64)
    ang = pos[None, :] * inv[:, None]
    cos = np.cos(ang); sin = np.sin(ang)
    cos128 = np.tile(cos, (4, 1)); sin128 = np.tile(sin, (4, 1))
    return cos128.astype(np.float32), sin128.astype(np.float32)


def _conv_masks():
    m = np.ones((2, 512), np.float32)
    w = np.arange(512) % 64
    m[0, w == 0] = 0.0
    m[1, w == 63] = 0.0
    lat = np.broadcast_to(m[None], (128, 2, 512)).copy()
    c = np.ones((2, 256), np.float32)
    c[0, 0] = 0.0
    c[1, 255] = 0.0
    ctx = np.broadcast_to(c[None], (128, 2, 256)).copy()
    return lat, ctx


class KB:
    def __init__(self, cfg):
        self.cfg = cfg
        nc = bass.Bass("TRN2", target_bir_lowering=False)
        self.nc = nc
        self.P = Prog(nc)
        self.rr = 0

    def MM(self, ps, lhsT, rhs, st, sp, R, W):
        partial = lhsT.partition_size() < 128
        self.P.op("pe", lambda e: e.matmul(ps, lhsT, rhs, start=st, stop=sp), R, W, partial=partial)

    def TR(self, ps, in_, ident, R, W):
        self.P.op("pe", lambda e: e.transpose(ps, in_, ident), R, W)

    def ACT(self, out, in_, func, R, W, **kw):
        self.P.op("act", lambda e: e.activation(out=out, in_=in_, func=func, **kw), R, W)

    def TS(self, out, in0, s1, s2, op0, op1, R, W, eng="dve"):
        if s2 is None:
            self.P.op(eng, lambda e: e.tensor_scalar(out=out, in0=in0, scalar1=s1, scalar2=None, op0=op0), R, W)
        else:
            self.P.op(eng, lambda e: e.tensor_scalar(out=out, in0=in0, scalar1=s1, scalar2=s2, op0=op0, op1=op1), R, W)

    def TT(self, out, in0, in1, op, R, W, eng="dve"):
        self.P.op(eng, lambda e: e.tensor_tensor(out=out, in0=in0, in1=in1, op=op), R, W)

    def STT(self, out, in0, sc, in1, op0, op1, R, W, eng="dve"):
        eng = "dve"
        self.P.op(eng, lambda e: e.scalar_tensor_tensor(out=out, in0=in0, scalar=sc, in1=in1, op0=op0, op1=op1), R, W)

    def CP(self, out, in_, R, W, eng="dve"):
        if eng == "act":
            self.ACT(out, in_, AF.Copy, R, W)
        else:
            self.P.op(eng, lambda e: e.tensor_copy(out=out, in_=in_), R, W)

    def CPRED(self, out, mask, data, R, W):
        self.P.op("dve", lambda e: e.copy_predicated(out=out, mask=mask, data=data), R, W)

    def MS(self, ap, val, W, eng="dve"):
        self.P.op(eng, lambda e: e.memset(ap, val), (), W)

    def RECIP(self, out, in_, R, W):
        self.P.op("dve", lambda e: e.reciprocal(out=out, in_=in_), R, W)

    def SCAN(self, out, d0, d1, R, W):
        self.P.op("dve", lambda e: e.tensor_tensor_scan(out=out, data0=d0, data1=d1, initial=0.0,
                                                        op0=ALU.mult, op1=ALU.add), R, W)

    def LD(self, out, in_, W, R=(), q="sp", **kw):
        self.P.dma(out, in_, reads=R, writes=W, q=q, **kw)

    def ST(self, out, in_, R, W=(), q="pool", **kw):
        self.P.dma(out, in_, reads=R, writes=W, q=q, **kw)

    def evac_eng(self):
        self.rr += 1
        return "act" if self.rr % 2 else "dve"

    def C(self, name):
        i = CN[name]
        return self.const[:, i * 128:(i + 1) * 128]


PARAM_SHAPES = {
    "mod_w": [2, 1024, 6144], "mod_b": [2, 6144], "norm1_g": [2, 1024], "norm2_g": [2, 1024],
    "w_in": [2, 1024, 3600], "hgrn_lb_logits": [2, 2, 256], "hgrn_norm_g": [2, 64],
    "ret_decay_logit": [2, 2, 4], "gdn_conv_w": [2, 3, 3, 768], "gdn_a_log": [2, 2, 4],
    "gdn_dt_bias": [2, 2, 4], "gdn_norm_g": [2, 64], "s5_lam_re": [2, 2, 16, 64],
    "s5_lam_im": [2, 2, 16, 64], "s5_log_dt": [2, 2, 16], "s5_b_re": [2, 16, 64, 16],
    "s5_b_im": [2, 16, 64, 16], "s5_c_re": [2, 16, 16, 64], "s5_c_im": [2, 16, 16, 64],
    "s5_d": [2, 256], "s5_glu_w": [2, 256, 256], "s5_glu_b": [2, 256], "w_out": [2, 1024, 1024],
    "mlp_w1": [2, 1024, 4096], "mlp_w2": [2, 4096, 1024], "final_norm_g": [1024],
}


def declare(kb):
    P = kb.P
    cfg = kb.cfg
    kinds = cfg.get("kinds", {})
    kb.xin = P.dram("xin", [S, D], F32, kind="ExternalInput")
    kb.cvecT = P.dram("cvecT", [1024, 2], F32, kind="ExternalInput")
    kb.prm = {k: P.dram(k, shp, F32, kind="ExternalInput") for k, shp in PARAM_SHAPES.items()}
    kb.constd = P.dram("constp", [128, NCONST * 128], F32, kind="ExternalInput")
    kb.ropec = P.dram("ropec", [128, S], F32, kind="ExternalInput")
    kb.ropes = P.dram("ropes", [128, S], F32, kind="ExternalInput")
    kb.cmlat = P.dram("cmlat", [128, 2, 512], F32, kind="ExternalInput")
    kb.cmctx = P.dram("cmctx", [128, 2, 256], F32, kind="ExternalInput")
    kb.y = P.dram("y", [4096, D], F32, kind="ExternalOutput")
    kb.XS = P.dram("XS", [S, D], F32, kind=kinds.get("XS", "Internal"))
    kb.ZF = P.dram("ZF", [NZF, S], F32, kind=kinds.get("ZF", "Internal"))
    kb.ZT = P.dram("ZT", [S, NZT], F32, kind=kinds.get("ZT", "Internal"))
    kb.QKVF = P.dram("QKVF", [768, S], F32, kind=kinds.get("QKVF", "Internal"))
    kb.YC = P.dram("YC", [1024, S], F32, kind=kinds.get("YC", "Internal"))
    kb.H2T = P.dram("H2T", [1024, S], BF16, kind=kinds.get("H2T", "Internal"))
    kb.const = P.sbuf("const", [128, NCONST * 128])
    nchunk = 4
    w = NCONST * 128 // nchunk
    for i in range(nchunk):
        a, b = i * w, (i + 1) * w if i < nchunk - 1 else NCONST * 128
        kb.LD(kb.const[:, a:b], kb.constd[:, a:b], [kb.const.s(i)])
    kb.const_bufs = [kb.const.s(i) for i in range(nchunk)]
    kb.CB = kb.const_bufs
    kb.GS1 = P.sbuf("GS1", [128, 8, 2]); kb.SH1 = P.sbuf("SH1", [128, 8, 2])
    kb.GS2 = P.sbuf("GS2", [128, 8, 2]); kb.SH2 = P.sbuf("SH2", [128, 8, 2])
    kb.GATE1 = P.sbuf("GATE1", [128, 2, 1024]); kb.GATE2 = P.sbuf("GATE2", [128, 2, 1024])


def phase_mod(kb, l):
    P = kb.P
    prm = kb.prm
    with P.scope():
        cT = P.sbuf("cT", [128, 8, 2])
        kb.LD(cT[:], kb.cvecT[:].rearrange("(et e) c -> e et c", e=128), [cT])
        sc = P.sbuf("sc", [128, 8, 2])
        kb.ACT(sc[:], cT[:], AF.Silu, [cT], [sc])
        screp = P.sbuf("screp", [128, 8, 2, 128])
        kb.CP(screp[:], sc[:].unsqueeze(3).to_broadcast([128, 8, 2, 128]), [sc], [screp])
        mbf = P.sbuf("mbf", [128, 48])
        kb.LD(mbf[:], prm["mod_b"][l].rearrange("(j p) -> p j", p=128), [mbf], allow_slow_non_contiguous=True)
        ngf = P.sbuf("ngf", [128, 2, 8])
        kb.LD(ngf[:, 0, :], prm["norm1_g"][l].rearrange("(j p) -> p j", p=128), [ngf], allow_slow_non_contiguous=True)
        kb.LD(ngf[:, 1, :], prm["norm2_g"][l].rearrange("(j p) -> p j", p=128), [ngf], allow_slow_non_contiguous=True)
        mbrow = P.sbuf("mbrow", [128, 2, 1024])
        for gi, v in enumerate((2, 5)):
            kb.LD(mbrow[:, gi, :], prm["mod_b"][l][v * 1024:(v + 1) * 1024].partition_broadcast(128), [mbrow])
        wch = [P.sbuf("wch%d" % i, [128, 8, 1024]) for i in range(2)]
        ps_fm = P.psum("ps_fm", [128, 96])
        ps_g = [P.psum("ps_g%d" % i, [128, 512]) for i in range(2)]
        MF = P.sbuf("MF", [128, 48, 2])
        k = 0
        for v in range(6):
            wc = wch[v % 2]
            for et in range(8):
                kb.LD(wc[:, et, :], prm["mod_w"][l][et * 128:(et + 1) * 128, v * 1024:(v + 1) * 1024], [wc])
            for db in range(8):
                col = (v * 8 + db) * 2
                for et in range(8):
                    kb.MM(ps_fm[:, col:col + 2], wc[:, et, db * 128:(db + 1) * 128], sc[:, et, :],
                          et == 0, et == 7, [wc, sc], [ps_fm])
            if v in (2, 5):
                gt = kb.GATE1 if v == 2 else kb.GATE2
                gi = 0 if v == 2 else 1
                for which in range(2):
                    for half in range(2):
                        pg = ps_g[k % 2]; k += 1
                        for et in range(8):
                            kb.MM(pg[:], screp[:, et, which, :], wc[:, et, half * 512:(half + 1) * 512],
                                  et == 0, et == 7, [screp, wc], [pg])
                        kb.TT(gt[:, which, half * 512:(half + 1) * 512], pg[:], mbrow[:, gi, half * 512:(half + 1) * 512],
                              ALU.add, [pg, mbrow], [gt])
        kb.TT(MF[:], ps_fm[:].rearrange("p (j c) -> p j c", c=2), mbf[:].unsqueeze(2).to_broadcast([128, 48, 2]),
              ALU.add, [ps_fm, mbf], [MF])
        tmp = P.sbuf("mtmp", [128, 8, 2])
        kb.TS(tmp[:], MF[:, 8:16, :], 1.0, None, ALU.add, None, [MF], [tmp])
        kb.TT(kb.GS1[:], tmp[:], ngf[:, 0, :].unsqueeze(2).to_broadcast([128, 8, 2]), ALU.mult, [tmp, ngf], [kb.GS1])
        kb.CP(kb.SH1[:], MF[:, 0:8, :], [MF], [kb.SH1])
        tmp2 = P.sbuf("mtmp2", [128, 8, 2])
        kb.TS(tmp2[:], MF[:, 32:40, :], 1.0, None, ALU.add, None, [MF], [tmp2])
        kb.TT(kb.GS2[:], tmp2[:], ngf[:, 1, :].unsqueeze(2).to_broadcast([128, 8, 2]), ALU.mult, [tmp2, ngf], [kb.GS2])
        kb.CP(kb.SH2[:], MF[:, 24:32, :], [MF], [kb.SH2])


def norm_to_fm(kb, xt, hT, col0, GS, SH, which, bufs, R_x):
    P = kb.P
    junk, st, xn, ps_ts = bufs["junk"], bufs["st"], bufs["xn"], bufs["ps_t"]
    kb.MS(st[:, 0:1], 0.0, [st])
    kb.ACT(junk[:], xt[:], AF.Square, [xt], [junk, st], accum_out=st[:, 0:1])
    kb.ACT(st[:, 1:2], st[:, 0:1], AF.Sqrt, [st] + kb.CB, [st], scale=1.0 / D, bias=kb.C("CCOL")[:, 0:1])
    kb.RECIP(st[:, 2:3], st[:, 1:2], [st], [st])
    kb.ACT(xn[:], xt[:], AF.Copy, [xt, st], [xn], scale=st[:, 2:3])
    for half in range(2):
        ps_t = ps_ts[half]
        for q in range(4):
            dt = half * 4 + q
            kb.TR(ps_t[:, q * 128:(q + 1) * 128], xn[:, dt * 128:(dt + 1) * 128], kb.C("IDENT"), [xn] + kb.CB, [ps_t])
        for q in range(4):
            dt = half * 4 + q
            if q % 2 == 0:
                kb.TS(hT[:, dt, col0:col0 + 128], ps_t[:, q * 128:(q + 1) * 128], GS[:, dt, which:which + 1],
                      SH[:, dt, which:which + 1], ALU.mult, ALU.add, [ps_t, GS, SH], [hT])
            else:
                kb.ACT(hT[:, dt, col0:col0 + 128], ps_t[:, q * 128:(q + 1) * 128], AF.Identity, [ps_t, GS, SH], [hT],
                       scale=GS[:, dt, which:which + 1], bias=SH[:, dt, which:which + 1])


def phase_a(kb, l, src):
    P = kb.P
    with P.scope():
        win = P.sbuf("win", [128, 8, 3600], BF16)
        for kt in range(8):
            kb.LD(win[:, kt, :], kb.prm["w_in"][l][kt * 128:(kt + 1) * 128, :], [win.s(kt)], q="pool")
        winb = [win.s(kt) for kt in range(8)]
        xbuf = [P.sbuf("xa%d" % i, [128, 1024]) for i in range(2)]
        hTb = [P.sbuf("hTa%d" % i, [128, 8, 512], BF16) for i in range(2)]
        nb = {"junk": P.sbuf("junk", [128, 1024]), "st": P.sbuf("st", [128, 4]), "xn": P.sbuf("xn", [128, 1024]),
              "ps_t": [P.psum("ps_t%d" % i, [128, 512]) for i in range(2)]}
        ps_f = [P.psum("ps_f%d" % i, [128, 512]) for i in range(3)]
        ps_a = [P.psum("ps_a%d" % i, [128, 512]) for i in range(2)]
        ps_b = P.psum("ps_b", [128, 16])
        stg = [P.sbuf("stg%d" % i, [128, 512]) for i in range(4)]
        stt = [P.sbuf("stt%d" % i, [128, NZT]) for i in range(2)]
        kx = kf = ks = ka = 0
        for gi, t0 in enumerate(range(0, S, 512)):
            n = min(512, S - t0)
            hT = hTb[gi % 2]
            for ti in range(n // 128):
                tt = t0 // 128 + ti
                which = 1 if tt < 2 else 0
                xt = xbuf[kx % 2]; kx += 1
                kb.LD(xt[:], src[tt * 128:(tt + 1) * 128, :], [xt])
                norm_to_fm(kb, xt, hT, ti * 128, kb.GS1, kb.SH1, which, nb, None)
            for (zr, wc) in FM_BLOCKS:
                ps = ps_f[kf % 3]; kf += 1
                for kt in range(8):
                    kb.MM(ps[:, :n], win[:, kt, wc:wc + 128], hT[:, kt, :n], kt == 0, kt == 7, [winb[kt], hT], [ps])
                sg = stg[ks % 4]; ks += 1
                kb.CP(sg[:, :n], ps[:, :n], [ps], [sg], eng=kb.evac_eng())
                kb.ST(kb.ZF[zr:zr + 128, t0:t0 + n], sg[:, :n], [sg])
            for ti in range(n // 128):
                tt = t0 // 128 + ti
                pa = ps_a[ka % 2]
                so = stt[ka % 2]; ka += 1
                for (c0, w0, wn) in ((0, C_HI, 256), (256, C_RV, 256)):
                    for kt in range(8):
                        kb.MM(pa[:, c0:c0 + wn], hT[:, kt, ti * 128:(ti + 1) * 128], win[:, kt, w0:w0 + wn],
                              kt == 0, kt == 7, [winb[kt], hT], [pa])
                for kt in range(8):
                    kb.MM(ps_b[:], hT[:, kt, ti * 128:(ti + 1) * 128], win[:, kt, C_GA:C_GA + 16],
                          kt == 0, kt == 7, [winb[kt], hT], [ps_b])
                kb.CP(so[:, 0:512], pa[:], [pa], [so], eng="act")
                kb.CP(so[:, 512:528], ps_b[:], [ps_b], [so], eng="dve")
                kb.ST(kb.ZT[tt * 128:(tt + 1) * 128, :], so[:], [so])


def build(cfg):
    kb = KB(cfg)
    P = kb.P
    declare(kb)
    P.barrier()
    stages = cfg.get("stages", "all")
    for l in cfg.get("layers", range(DEPTH)):
        src = kb.xin if l == 0 else kb.XS
        if stages == "all" or "M" in stages:
            phase_mod(kb, l)
        if stages == "all" or "A" in stages:
            phase_a(kb, l, src)
        if stages == "all" or "R" in stages:
            (mixer_ret if cfg.get("ret_old") else mixer_ret2)(kb, l)
        if stages == "all" or "H" in stages:
            (mixer_hgrn if cfg.get("hgrn_old") else mixer_hgrn2)(kb, l)
        if stages == "all" or "G" in stages:
            (mixer_gdn if cfg.get("gdn_old") else mixer_gdn2)(kb, l)
        if stages == "all" or "S" in stages:
            (mixer_s5 if cfg.get("s5_old") else mixer_s5_2)(kb, l)
        if stages == "all" or "C" in stages:
            phase_c(kb, l, src)
    P.emit()
    P.close()
    return kb


_CONSTS = None


def host_inputs(inputs, cores=range(8)):
    global _CONSTS
    if _CONSTS is None:
        rc, rs = _rope_tables()
        cl, cc = _conv_masks()
        _CONSTS = {"constp": CONST_NP, "ropec": rc, "ropes": rs, "cmlat": cl, "cmctx": cc}
    maps = []
    for b in cores:
        m = {"xin": np.ascontiguousarray(np.concatenate([inputs["ctx"][b], inputs["x"][b]], axis=0), dtype=np.float32),
             "cvecT": np.ascontiguousarray(np.stack([inputs["c"][b], inputs["c_ctx"]], axis=1), dtype=np.float32)}
        for k in PARAM_SHAPES:
            m[k] = np.ascontiguousarray(inputs[k], dtype=np.float32)
        m.update(_CONSTS)
        maps.append(m)
    return maps


def kernel(**inputs):
    inputs = {k: np.asarray(v) for k, v in inputs.items()}
    kb = build({})
    maps = host_inputs(inputs)
    res = run_bass_kernel_spmd(kb.nc, maps, core_ids=list(range(8)))
    out = np.stack([np.asarray(r["y"]).reshape(4096, D) for r in res.results], axis=0)
    return out.astype(np.float32)


def phase_c(kb, l, src):
    P = kb.P
    last = (l == DEPTH - 1)
    t_start = 2 if last else 0
    with P.scope():
        wout = P.sbuf("wout", [128, 8, 1024], BF16)
        for ft in range(8):
            kb.LD(wout[:, ft, :], kb.prm["w_out"][l][ft * 128:(ft + 1) * 128, :], [wout.s(ft)], q="pool")
        wb = [wout.s(ft) for ft in range(8)]
        ycb = [P.sbuf("yc%d" % i, [128, 8, 128], BF16) for i in range(2)]
        xb = [P.sbuf("xc%d" % i, [128, 1024]) for i in range(2)]
        x1b = [P.sbuf("x1c%d" % i, [128, 1024]) for i in range(2)]
        tmpb = [P.sbuf("tc%d" % i, [128, 512]) for i in range(2)]
        h2b = [P.sbuf("h2c%d" % i, [128, 8, 128], BF16) for i in range(2)]
        nb = {"junk": P.sbuf("junkc", [128, 1024]), "st": P.sbuf("stc", [128, 4]), "xn": P.sbuf("xnc", [128, 1024]),
              "ps_t": [P.psum("ps_tc%d" % i, [128, 512]) for i in range(2)]}
        ps_y = [P.psum("ps_y%d" % i, [128, 512]) for i in range(4)]
        k = 0
        for tt in range(t_start, NT):
            which = 1 if tt < 2 else 0
            yc = ycb[k % 2]; xt = xb[k % 2]; x1 = x1b[k % 2]; h2 = h2b[k % 2]
            cols = slice(tt * 128, (tt + 1) * 128)
            kb.LD(yc[:], kb.YC[:, cols].rearrange("(ft p) t -> p ft t", p=128), [yc], q="pool")
            kb.LD(xt[:], src[cols, :], [xt])
            for half in range(2):
                ps = ps_y[(2 * k + half) % 4]
                for ft in range(8):
                    kb.MM(ps[:], yc[:, ft, :], wout[:, ft, half * 512:(half + 1) * 512], ft == 0, ft == 7,
                          [yc, wb[ft]], [ps])
                tm = tmpb[half]
                kb.TT(tm[:], ps[:], kb.GATE1[:, which, half * 512:(half + 1) * 512], ALU.mult, [ps, kb.GATE1], [tm])
                kb.TT(x1[:, half * 512:(half + 1) * 512], xt[:, half * 512:(half + 1) * 512], tm[:], ALU.add,
                      [xt, tm], [x1], eng="pool")
            kb.ST(kb.XS[cols, :], x1[:], [x1])
            norm_to_fm(kb, x1, h2, 0, kb.GS2, kb.SH2, which, nb, None)
            kb.ST(kb.H2T[:, cols].rearrange("(dt p) t -> p dt t", p=128), h2[:], [h2])
            k += 1
    with P.scope():
        w1 = P.sbuf("w1", [128, 8, 4096], BF16)
        w2 = P.sbuf("w2", [128, 32, 1024], BF16)
        for kt in range(8):
            kb.LD(w1[:, kt, :], kb.prm["mlp_w1"][l][kt * 128:(kt + 1) * 128, :], [w1.s(kt)], q="pool")
        for fb in range(32):
            kb.LD(w2[:, fb, :], kb.prm["mlp_w2"][l][fb * 128:(fb + 1) * 128, :], [w2.s(fb)], q="pool")
        h2b = [P.sbuf("h2d%d" % i, [128, 8, 256], BF16) for i in range(2)]
        uTb = [P.sbuf("uT%d" % i, [128, 16, 256], BF16) for i in range(1)]
        rb = [P.sbuf("relu%d" % i, [128, 256]) for i in range(3)]
        xb = [P.sbuf("xd%d" % i, [128, 1024]) for i in range(2)]
        tmpb = [P.sbuf("td%d" % i, [128, 512]) for i in range(2)]
        ps_u = [P.psum("ps_u%d" % i, [128, 256]) for i in range(3)]
        ps_y = [P.psum("ps_y2%d" % i, [128, 512]) for i in range(4)]
        if last:
            fg = P.sbuf("fg", [128, 1024])
            kb.LD(fg[:], kb.prm["final_norm_g"][:].partition_broadcast(128), [fg])
            stf = P.sbuf("stf", [128, 4])
            xnf = P.sbuf("xnf", [128, 1024])
        k = 0; ku = 0
        for g0 in range(t_start, NT, 2):
            h2 = h2b[k % 2]; uT = uTb[0]
            cols = slice(g0 * 128, (g0 + 2) * 128)
            kb.LD(h2[:], kb.H2T[:, cols].rearrange("(dt p) t -> p dt t", p=128), [h2])
            for hh in range(2):
                for fl in range(16):
                    fb = hh * 16 + fl
                    ps = ps_u[ku % 3]; r = rb[ku % 3]; ku += 1
                    for kt in range(8):
                        kb.MM(ps[:], w1[:, kt, fb * 128:(fb + 1) * 128], h2[:, kt, :], kt == 0, kt == 7, [w1.s(kt), h2], [ps])
                    kb.ACT(r[:], ps[:], AF.Relu, [ps], [r])
                    kb.TT(uT[:, fl, :], r[:], r[:], ALU.mult, [r], [uT.s(fl)], eng=("dve" if fb % 2 else "pool"))
                for ti in range(2):
                    for half in range(2):
                        ps = ps_y[2 * ti + half]
                        for fl in range(16):
                            fb = hh * 16 + fl
                            kb.MM(ps[:], uT[:, fl, ti * 128:(ti + 1) * 128], w2[:, fb, half * 512:(half + 1) * 512],
                                  fb == 0, fb == 31, [uT.s(fl), w2.s(fb)], [ps])
            for ti in range(2):
                tt = g0 + ti
                which = 1 if tt < 2 else 0
                xt = xb[ti]
                rows = slice(tt * 128, (tt + 1) * 128)
                kb.LD(xt[:], kb.XS[rows, :], [xt])
                for half in range(2):
                    ps = ps_y[2 * ti + half]
                    tm = tmpb[half]
                    kb.TT(tm[:], ps[:], kb.GATE2[:, which, half * 512:(half + 1) * 512], ALU.mult, [ps, kb.GATE2], [tm])
                    kb.TT(xt[:, half * 512:(half + 1) * 512], xt[:, half * 512:(half + 1) * 512], tm[:], ALU.add,
                          [xt, tm], [xt], eng="pool")
                if not last:
                    kb.ST(kb.XS[rows, :], xt[:], [xt])
                else:
                    kb.MS(stf[:, 0:1], 0.0, [stf])
                    kb.ACT(xnf[:], xt[:], AF.Square, [xt], [xnf, stf], accum_out=stf[:, 0:1])
                    kb.ACT(stf[:, 1:2], stf[:, 0:1], AF.Sqrt, [stf], [stf], scale=1.0 / D, bias=kb.C("CCOL")[:, 0:1])
                    kb.RECIP(stf[:, 2:3], stf[:, 1:2], [stf], [stf])
                    kb.ACT(xnf[:], xt[:], AF.Copy, [xt, stf], [xnf], scale=stf[:, 2:3])
                    kb.TT(xnf[:], xnf[:], fg[:], ALU.mult, [xnf, fg], [xnf])
                    kb.P.dma(kb.y[(tt - 2) * 128:(tt - 1) * 128, :], xnf[:], reads=[xnf.b], q="pool", is_output=True)
            k += 1


def finalize_gated(kb, OACC, gate_row0, gain, yc_row0, pfx):
    P = kb.P
    gb = [P.sbuf(pfx + "fg%d" % i, [128, 2, 128]) for i in range(2)]
    sq = [P.sbuf(pfx + "fsq%d" % i, [128, 128]) for i in range(2)]
    rt = [P.sbuf(pfx + "frt%d" % i, [128, 128]) for i in range(2)]
    sg = [P.sbuf(pfx + "fsg%d" % i, [128, 128]) for i in range(2)]
    ob = [P.sbuf(pfx + "fo%d" % i, [128, 128]) for i in range(2)]
    ps_m = [P.psum(pfx + "fps%d" % i, [128, 128]) for i in range(2)]
    k = 0
    for n in range(NT):
        cols = slice(n * 128, (n + 1) * 128)
        g = gb[n % 2]
        kb.LD(g[:], kb.ZF[gate_row0:gate_row0 + 256, cols].rearrange("(hp p) t -> p hp t", p=128), [g])
        for hp in range(2):
            i = k % 2; k += 1
            o = OACC[:, hp, cols]
            kb.TT(sq[i][:], o, o, ALU.mult, [OACC.s(n)], [sq[i]], eng="pool")
            kb.MM(ps_m[i][:], kb.C("BLK64"), sq[i][:], True, True, [sq[i]], [ps_m[i]])
            kb.ACT(rt[i][:], ps_m[i][:], AF.Sqrt, [ps_m[i]], [rt[i]], scale=1.0 / 64, bias=kb.C("CCOL")[:, 0:1])
            kb.RECIP(rt[i][:], rt[i][:], [rt[i]], [rt[i]])
            kb.ACT(sg[i][:], g[:, hp, :], AF.Silu, [g], [sg[i]])
            kb.TT(ob[i][:], o, rt[i][:], ALU.mult, [OACC.s(n), rt[i]], [ob[i]])
            if gain is not None:
                kb.STT(ob[i][:], ob[i][:], gain[:, 0:1], sg[i][:], ALU.mult, ALU.mult, [ob[i], gain, sg[i]], [ob[i]])
            else:
                kb.TT(ob[i][:], ob[i][:], sg[i][:], ALU.mult, [ob[i], sg[i]], [ob[i]])
            kb.ST(kb.YC[yc_row0 + hp * 128:yc_row0 + (hp + 1) * 128, cols], ob[i][:], [ob[i]])


def oacc_write(kb, OACC, hp, n, ps, d):
    cols = slice(n * 128, (n + 1) * 128)
    if d == 0:
        kb.CP(OACC[:, hp, cols], ps[:], [ps], [OACC.s(n)], eng="act")
    else:
        kb.TT(OACC[:, hp, cols], OACC[:, hp, cols], ps[:], ALU.add, [ps], [OACC.s(n)])


def mixer_ret(kb, l):
    P = kb.P
    with P.scope():
        OACC = P.sbuf("r_oacc", [128, 2, S])
        with P.scope():
            lgt = P.sbuf("r_lgt", [128, 8])
            kb.LD(lgt[:], kb.prm["ret_decay_logit"][l].rearrange("d h -> (d h)").partition_broadcast(128), [lgt])
            LG = P.sbuf("r_LG", [128, 8])
            kb.ACT(LG[:], lgt[:], AF.Sigmoid, [lgt], [LG])
            kb.ACT(LG[:], LG[:], AF.Ln, [LG], [LG])
            LGP = P.sbuf("r_LGP", [128, 4])
            for d in range(2):
                for hp in range(2):
                    c = 2 * d + hp
                    kb.CP(LGP[0:64, c:c + 1], LG[0:64, 4 * d + 2 * hp:4 * d + 2 * hp + 1], [LG], [LGP])
                    kb.CP(LGP[64:128, c:c + 1], LG[64:128, 4 * d + 2 * hp + 1:4 * d + 2 * hp + 2], [LG], [LGP])
            MK = [P.sbuf("r_MK%d" % d, [128, 4, 128]) for d in range(2)]
            QDEC = [[P.sbuf("r_QD%d%d" % (d, hp), [128, 128]) for hp in range(2)] for d in range(2)]
            etmp = P.sbuf("r_etmp", [128, 128])
            for d in range(2):
                for h in range(4):
                    kb.ACT(etmp[:], kb.C("DIFF" if d == 0 else "NDIFF"), AF.Exp, [LG], [etmp],
                           scale=LG[:, 4 * d + h:4 * d + h + 1])
                    kb.STT(MK[d][:, h, :], etmp[:], 0.125, kb.C("TRIF" if d == 0 else "TRIB"), ALU.mult, ALU.mult,
                           [etmp], [MK[d]])
                for hp in range(2):
                    kb.ACT(QDEC[d][hp][:], kb.C("IOTAF1" if d == 0 else "RIOTAF"), AF.Exp, [LGP], [QDEC[d][hp]],
                           scale=LGP[:, 2 * d + hp:2 * d + hp + 1])
            KD = P.sbuf("r_KD", [128, 8])
            kb.ACT(KD[:, 0:4], LG[:, 0:4], AF.Exp, [LG], [KD], scale=kb.C("CCOL")[:, 3:4])
            kb.ACT(KD[:, 4:8], LG[:, 4:8], AF.Exp, [LG], [KD], scale=kb.C("CCOL")[:, 2:3])
            kb.TS(KD[:], KD[:], 0.125, None, ALU.mult, None, [KD], [KD])
            CV = P.sbuf("r_CV", [128, 4])
            kb.ACT(CV[:], LGP[:], AF.Exp, [LGP], [CV], scale=128.0)
            qTb = [P.sbuf("r_q%d" % i, [128, 2, 128]) for i in range(2)]
            kTb = [P.sbuf("r_k%d" % i, [128, 2, 128]) for i in range(2)]
            csb = [P.sbuf("r_cs%d" % i, [128, 2, 128]) for i in range(2)]
            Vp = [[P.sbuf("r_vp%d%d" % (i, h), [128, 128]) for h in range(4)] for i in range(2)]
            khp = [[P.sbuf("r_kh%d%d" % (i, h), [128, 128]) for h in range(4)] for i in range(2)]
            for i in range(2):
                for h in range(4):
                    kb.MS(Vp[i][h][:], 0.0, [Vp[i][h]], eng="pool")
                    kb.MS(khp[i][h][:], 0.0, [khp[i][h]], eng="pool")
            t1 = [P.sbuf("r_t1%d" % i, [128, 128]) for i in range(2)]
            t2 = [P.sbuf("r_t2%d" % i, [128, 128]) for i in range(2)]
            qr = [P.sbuf("r_qr%d" % i, [128, 2, 128]) for i in range(2)]
            kr = [P.sbuf("r_kr%d" % i, [128, 2, 128]) for i in range(2)]
            AT = [P.sbuf("r_AT%d" % i, [128, 2, 128]) for i in range(2)]
            qd = [P.sbuf("r_qd%d" % i, [128, 128]) for i in range(2)]
            Sb = [P.sbuf("r_S%d" % hp, [128, 128]) for hp in range(2)]
            ps_r = [P.psum("r_psr%d" % i, [128, 256]) for i in range(2)]
            ps_s = [P.psum("r_pss%d" % i, [128, 2, 128]) for i in range(2)]
            ps_o = [P.psum("r_pso%d" % i, [128, 128]) for i in range(2)]
            ps_k = P.psum("r_psk", [128, 128])
            ps_kv = P.psum("r_pskv", [128, 128])
            it = 0
            for d in range(2):
                for hp in range(2):
                    kb.MS(Sb[hp][:], 0.0, [Sb[hp]])
                for n in ORDER[d]:
                    cols = slice(n * 128, (n + 1) * 128)
                    b = it % 2; it += 1
                    qT, kT, cs = qTb[b], kTb[b], csb[b]
                    kb.LD(qT[:], kb.ZF[Z_RQ:Z_RQ + 256, cols].rearrange("(hp p) t -> p hp t", p=128), [qT])
                    kb.LD(kT[:], kb.ZF[Z_RK:Z_RK + 256, cols].rearrange("(hp p) t -> p hp t", p=128), [kT])
                    kb.LD(cs[:, 0, :], kb.ropec[:, cols], [cs])
                    kb.LD(cs[:, 1, :], kb.ropes[:, cols], [cs])
                    for h in range(4):
                        kb.LD(Vp[b][h][:, 64 * (h % 2):64 * (h % 2) + 64], kb.ZT[cols, 256 + 64 * h:256 + 64 * h + 64],
                              [Vp[b][h]])
                    for hp in range(2):
                        j = (it * 2 + hp) % 2
                        pr = ps_r[j]
                        kb.MM(pr[:, 0:128], kb.C("ROT"), qT[:, hp, :], True, True, [qT], [pr])
                        kb.MM(pr[:, 128:256], kb.C("ROT"), kT[:, hp, :], True, True, [kT], [pr])
                        for (src_, dst, off) in ((qT, qr[b], 0), (kT, kr[b], 128)):
                            kb.TT(t1[j][:], src_[:, hp, :], cs[:, 0, :], ALU.mult, [src_, cs], [t1[j]], eng="pool")
                            kb.TT(t2[j][:], pr[:, off:off + 128], cs[:, 1, :], ALU.mult, [pr, cs], [t2[j]])
                            kb.TT(dst[:, hp, :], t1[j][:], t2[j][:], ALU.add, [t1[j], t2[j]], [dst.s(hp)], eng="pool")
                        pss = ps_s[j]
                        for h2 in range(2):
                            kb.MM(pss[:, h2, :], kr[b][64 * h2:64 * h2 + 64, hp, :], qr[b][64 * h2:64 * h2 + 64, hp, :],
                                  True, True, [kr[b].s(hp), qr[b].s(hp)], [pss])
                        kb.TT(AT[j][:], pss[:], MK[d][:, 2 * hp:2 * hp + 2, :], ALU.mult, [pss, MK[d]], [AT[j]])
                        kb.TT(qd[j][:], qr[b][:, hp, :], QDEC[d][hp][:], ALU.mult, [qr[b].s(hp), QDEC[d][hp]], [qd[j]],
                              eng="pool")
                        po = ps_o[j]
                        kb.MM(po[:], Vp[b][2 * hp][:], AT[j][:, 0, :], True, False, [Vp[b][2 * hp], AT[j]], [po])
                        kb.MM(po[:], Vp[b][2 * hp + 1][:], AT[j][:, 1, :], False, False, [Vp[b][2 * hp + 1], AT[j]], [po])
                        kb.MM(po[:], Sb[hp][:], qd[j][:], False, True, [Sb[hp], qd[j]], [po])
                        oacc_write(kb, OACC, hp, n, po, d)
                        kb.TR(ps_k[:], kr[b][:, hp, :], kb.C("IDENT"), [kr[b].s(hp)], [ps_k])
                        for h2 in range(2):
                            h = 2 * hp + h2
                            kb.ACT(khp[b][h][:, 64 * h2:64 * h2 + 64], ps_k[:, 64 * h2:64 * h2 + 64], AF.Copy,
                                   [ps_k, KD], [khp[b][h]], scale=KD[:, 4 * d + h:4 * d + h + 1])
                        kb.MM(ps_kv[:], khp[b][2 * hp][:], Vp[b][2 * hp][:], True, False,
                              [khp[b][2 * hp], Vp[b][2 * hp]], [ps_kv])
                        kb.MM(ps_kv[:], khp[b][2 * hp + 1][:], Vp[b][2 * hp + 1][:], False, True,
                              [khp[b][2 * hp + 1], Vp[b][2 * hp + 1]], [ps_kv])
                        kb.STT(Sb[hp][:], Sb[hp][:], CV[:, 2 * d + hp:2 * d + hp + 1], ps_kv[:], ALU.mult, ALU.add,
                               [Sb[hp], CV, ps_kv], [Sb[hp]])
        with P.scope():
            finalize_gated(kb, OACC, Z_RG, None, 256, "r_")


def mixer_hgrn(kb, l):
    P = kb.P
    with P.scope():
        OACC = P.sbuf("h_oacc", [128, 2, S])
        with P.scope():
            LB = P.sbuf("h_LB", [128, 4]); OML = P.sbuf("h_OML", [128, 4])
            if l == 0:
                kb.MS(LB[:], 0.0, [LB]); kb.MS(OML[:], 1.0, [OML])
            else:
                lgt = P.sbuf("h_lgt", [128, 8])
                kb.LD(lgt[:], kb.prm["hgrn_lb_logits"][:].rearrange("l d (hp p) -> p (l d hp)", p=128), [lgt],
                      allow_slow_non_contiguous=True)
                kb.TT(LB[:], lgt[:, 4:8], lgt[:, 0:4], ALU.subtract, [lgt], [LB])
                kb.ACT(LB[:], LB[:], AF.Sigmoid, [LB], [LB])
                kb.TS(OML[:], LB[:], -1.0, 1.0, ALU.mult, ALU.add, [LB], [OML])
            G = P.sbuf("h_G", [128, 1])
            for hh in range(2):
                kb.LD(G[64 * hh:64 * hh + 64, :], kb.prm["hgrn_norm_g"][l].rearrange("(p o) -> p o", o=1), [G])
            kb.hgrn_gain = G
            hqb = [P.sbuf("h_q%d" % i, [128, 2, 128]) for i in range(2)]
            hfb = [P.sbuf("h_f%d" % i, [128, 2, 128]) for i in range(2)]
            Vp = [[P.sbuf("h_vp%d%d" % (i, h), [128, 128]) for h in range(4)] for i in range(2)]
            khp = [[P.sbuf("h_kh%d%d" % (i, h), [128, 128]) for h in range(4)] for i in range(2)]
            for i in range(2):
                for h in range(4):
                    kb.MS(Vp[i][h][:], 0.0, [Vp[i][h]], eng="pool")
                    kb.MS(khp[i][h][:], 0.0, [khp[i][h]], eng="pool")
            MREF = [[P.sbuf("h_mr%d%d" % (d, i), [128, 4]) for i in range(2)] for d in range(2)]
            for d in range(2):
                for i in range(2):
                    kb.MS(MREF[d][i][:], 0.0, [MREF[d][i]])

            def two(name, shape=(128, 128)):
                return [P.sbuf("h_%s%d" % (name, i), list(shape)) for i in range(2)]
            qs, sgm, ff, logf, kk, bb, pre = two("qs"), two("sg"), two("ff"), two("lf"), two("kk"), two("bb"), two("pre")
            e1, Ql, e2, Qd = two("e1"), two("Ql"), two("e2"), two("Qd")
            Kt = [two("Kt%d" % r) for r in range(4)]
            ex = two("ex")
            AT = two("AT", (128, 2, 128))
            KhT = two("KhT")
            bend = two("bend", (128, 2))
            Sb = [P.sbuf("h_S%d" % hp, [128, 128]) for hp in range(2)]
            ps_s = [P.psum("h_pss%d" % i, [128, 2, 128]) for i in range(2)]
            ps_o = [P.psum("h_pso%d" % i, [128, 128]) for i in range(2)]
            ps_k = [P.psum("h_psk%d" % i, [128, 128]) for i in range(2)]
            ps_kv = [P.psum("h_pskv%d" % i, [128, 128]) for i in range(2)]
            it = 0
            jj = 0
            for d in range(2):
                zf = Z_HFF if d == 0 else Z_HFB
                for hp in range(2):
                    kb.MS(Sb[hp][:], 0.0, [Sb[hp]])
                for n in ORDER[d]:
                    cols = slice(n * 128, (n + 1) * 128)
                    b = it % 2; it += 1
                    hq, hf = hqb[b], hfb[b]
                    kb.LD(hq[:], kb.ZF[Z_HQ:Z_HQ + 256, cols].rearrange("(hp p) t -> p hp t", p=128), [hq])
                    kb.LD(hf[:], kb.ZF[zf:zf + 256, cols].rearrange("(hp p) t -> p hp t", p=128), [hf])
                    for h in range(4):
                        kb.LD(Vp[b][h][:, 64 * (h % 2):64 * (h % 2) + 64], kb.ZT[cols, 64 * h:64 * h + 64], [Vp[b][h]])
                    for hp in range(2):
                        j = jj % 2; jj += 1
                        c = 2 * d + hp
                        mref = MREF[d][j]
                        kb.ACT(qs[j][:], hq[:, hp, :], AF.Silu, [hq], [qs[j]])
                        kb.ACT(sgm[j][:], hf[:, hp, :], AF.Sigmoid, [hf], [sgm[j]])
                        kb.TS(ff[j][:], sgm[j][:], OML[:, c:c + 1], LB[:, c:c + 1], ALU.mult, ALU.add, [sgm[j], OML, LB], [ff[j]])
                        kb.ACT(logf[j][:], ff[j][:], AF.Ln, [ff[j]], [logf[j]])
                        kb.TS(kk[j][:], ff[j][:], -1.0, 1.0, ALU.mult, ALU.add, [ff[j]], [kk[j]], eng="pool")
                        B = bb[j]
                        if d == 0:
                            kb.SCAN(B[:], kb.C("ONES"), logf[j][:], [logf[j]], [B])
                            kb.CP(mref[:, 1:4], B[:].rearrange("p (r c) -> p r c", c=32)[:, 0:3, 31], [B], [mref])
                            be = B[:, 127:128]
                        else:
                            kb.SCAN(pre[j][:], kb.C("ONES"), logf[j][:], [logf[j]], [pre[j]])
                            kb.STT(B[:], pre[j][:], -1.0, logf[j][:], ALU.mult, ALU.add, [pre[j], logf[j]], [B])
                            kb.TS(B[:], B[:], pre[j][:, 127:128], None, ALU.add, None, [B, pre[j]], [B])
                            kb.CP(mref[:, 0:3], B[:].rearrange("p (r c) -> p r c", c=32)[:, 1:4, 0], [B], [mref])
                            be = B[:, 0:1]
                        kb.TT(e1[j][:].rearrange("p (r c) -> p r c", c=32), B[:].rearrange("p (r c) -> p r c", c=32),
                              mref[:].unsqueeze(2).to_broadcast([128, 4, 32]), ALU.subtract, [B, mref], [e1[j]])
                        kb.ACT(e1[j][:], e1[j][:], AF.Exp, [e1[j]], [e1[j]])
                        kb.STT(Ql[j][:], qs[j][:], 0.125, e1[j][:], ALU.mult, ALU.mult, [qs[j], e1[j]], [Ql[j]], eng="pool")
                        kb.ACT(e2[j][:], B[:], AF.Exp, [B], [e2[j]])
                        kb.STT(Qd[j][:], qs[j][:], 0.125, e2[j][:], ALU.mult, ALU.mult, [qs[j], e2[j]], [Qd[j]], eng="pool")
                        pss = ps_s[j]
                        for r in range(4):
                            kb.ACT(ex[j][:], B[:], AF.Exp, [B, mref], [ex[j]], scale=-1.0, bias=mref[:, r:r + 1])
                            kb.STT(Kt[r][j][:], ex[j][:], 1e26, kk[j][:], ALU.min, ALU.mult, [ex[j], kk[j]], [Kt[r][j]])
                            for h2 in range(2):
                                kb.MM(pss[:, h2, 32 * r:32 * r + 32], Kt[r][j][64 * h2:64 * h2 + 64, :],
                                      Ql[j][64 * h2:64 * h2 + 64, 32 * r:32 * r + 32], True, True,
                                      [Kt[r][j], Ql[j]], [pss])
                        kb.TT(AT[j][:], pss[:], kb.C("TRIF" if d == 0 else "TRIB").unsqueeze(1).to_broadcast([128, 2, 128]),
                              ALU.mult, [pss], [AT[j]])
                        po = ps_o[j]
                        kb.MM(po[:], Vp[b][2 * hp][:], AT[j][:, 0, :], True, False, [Vp[b][2 * hp], AT[j]], [po])
                        kb.MM(po[:], Vp[b][2 * hp + 1][:], AT[j][:, 1, :], False, False, [Vp[b][2 * hp + 1], AT[j]], [po])
                        kb.MM(po[:], Sb[hp][:], Qd[j][:], False, True, [Sb[hp], Qd[j]], [po])
                        oacc_write(kb, OACC, hp, n, po, d)
                        kb.CP(bend[j][:, 0:1], be, [B], [bend[j]])
                        kb.ACT(KhT[j][:], B[:], AF.Exp, [B, bend[j]], [KhT[j]], scale=-1.0, bias=bend[j][:, 0:1])
                        kb.TT(KhT[j][:], KhT[j][:], kk[j][:], ALU.mult, [KhT[j], kk[j]], [KhT[j]], eng="pool")
                        kb.ACT(bend[j][:, 1:2], bend[j][:, 0:1], AF.Exp, [bend[j]], [bend[j]])
                        pk = ps_k[j]
                        kb.TR(pk[:], KhT[j][:], kb.C("IDENT"), [KhT[j]], [pk])
                        for h2 in range(2):
                            h = 2 * hp + h2
                            kb.CP(khp[b][h][:, 64 * h2:64 * h2 + 64], pk[:, 64 * h2:64 * h2 + 64], [pk], [khp[b][h]],
                                  eng=("act" if h2 else "dve"))
                        pkv = ps_kv[j]
                        kb.MM(pkv[:], khp[b][2 * hp][:], Vp[b][2 * hp][:], True, False, [khp[b][2 * hp], Vp[b][2 * hp]], [pkv])
                        kb.MM(pkv[:], khp[b][2 * hp + 1][:], Vp[b][2 * hp + 1][:], False, True,
                              [khp[b][2 * hp + 1], Vp[b][2 * hp + 1]], [pkv])
                        kb.STT(Sb[hp][:], Sb[hp][:], bend[j][:, 1:2], pkv[:], ALU.mult, ALU.add,
                               [Sb[hp], bend[j], pkv], [Sb[hp]])
        with P.scope():
            G = P.sbuf("h_G2", [128, 1])
            for hh in range(2):
                kb.LD(G[64 * hh:64 * hh + 64, :], kb.prm["hgrn_norm_g"][l].rearrange("(p o) -> p o", o=1), [G])
            finalize_gated(kb, OACC, Z_HG, G, 0, "h_")


PI = float(np.pi)


def _sincos(kb, ang, sin_out, cos_out, R, tmp, shape=None):
    P = kb.P
    shp = list(ang.shape)
    with P.scope():
        ki = P.sbuf("sc_ki", shp, mybir.dt.int32)
        kf = P.sbuf("sc_kf", shp)
        r = P.sbuf("sc_r", shp)
        m = P.sbuf("sc_m", shp)
        C1 = 6.28125
        C2 = 2 * PI - C1
        for (shift, out) in ((0.0, sin_out), (PI / 2, cos_out)):
            kb.TS(r[:], ang, shift, None, ALU.add, None, R, [r])
            kb.TS(kf[:], r[:], 1.0 / (2 * PI), None, ALU.mult, None, [r], [kf])
            kb.CP(ki[:], kf[:], [kf], [ki])
            kb.CP(kf[:], ki[:], [ki], [kf])
            kb.STT(r[:], kf[:], -C1, r[:], ALU.mult, ALU.add, [kf, r], [r])
            kb.STT(r[:], kf[:], -C2, r[:], ALU.mult, ALU.add, [kf, r], [r])
            kb.TS(m[:], r[:], PI, 2 * PI, ALU.is_gt, ALU.mult, [r], [m])
            kb.TT(r[:], r[:], m[:], ALU.subtract, [r, m], [r])
            kb.TS(m[:], r[:], -PI, 2 * PI, ALU.is_lt, ALU.mult, [r], [m])
            kb.TT(r[:], r[:], m[:], ALU.add, [r, m], [r])
            kb.ACT(out, r[:], AF.Sin, [r], R)


def mixer_s5(kb, l):
    P = kb.P
    prm = kb.prm
    with P.scope():
        OACC = P.sbuf("s_oacc", [128, 2, S])
        with P.scope():
            WX = P.sbuf("s_WX", [128, 2, 8, 2, 64])
            Cblk = P.sbuf("s_Cblk", [128, 16, 128])
            kb.MS(WX[:], 0.0, [WX], eng="pool")
            kb.MS(Cblk[:], 0.0, [Cblk], eng="pool")
            for g8 in range(8):
                for ri, nm in enumerate(("s5_b_re", "s5_b_im")):
                    for gg in range(2):
                        src = prm[nm][l][8 * gg + g8].rearrange("p c -> c p")
                        kb.LD(WX[16 * g8:16 * g8 + 16, gg, g8, ri, :], src, [WX], allow_slow_non_contiguous=True)
            for g in range(16):
                g8 = g % 8
                kb.LD(Cblk[0:64, g, 16 * g8:16 * g8 + 16], prm["s5_c_re"][l][g].rearrange("c p -> p c"), [Cblk],
                      allow_slow_non_contiguous=True)
                kb.LD(Cblk[64:128, g, 16 * g8:16 * g8 + 16], prm["s5_c_im"][l][g].rearrange("c p -> p c"), [Cblk],
                      allow_slow_non_contiguous=True)
            kb.TS(Cblk[64:128, :, :], Cblk[64:128, :, :], -1.0, None, ALU.mult, None, [Cblk], [Cblk])
            Cb16 = P.sbuf("s_Cb16", [128, 16, 128], BF16)
            kb.CP(Cb16[:], Cblk[:], [Cblk], [Cb16])
            VFr = P.sbuf("s_VFr", [128, 16, 64]); VFi = P.sbuf("s_VFi", [128, 16, 64])
            T1 = P.sbuf("s_T1", [128, 16, 128]); T2 = P.sbuf("s_T2", [128, 16, 128])
            AR = P.sbuf("s_AR", [128, 16]); NAI = P.sbuf("s_NAI", [128, 16])
            for d in range(2):
                with P.scope():
                    lr = P.sbuf("s_lr", [128, 16, 64]); li = P.sbuf("s_li", [128, 16, 64]); dtb = P.sbuf("s_dt", [128, 16])
                    kb.LD(lr[:], prm["s5_lam_re"][l][d].rearrange("g p -> (g p)").partition_broadcast(128), [lr])
                    kb.LD(li[:], prm["s5_lam_im"][l][d].rearrange("g p -> (g p)").partition_broadcast(128), [li])
                    kb.LD(dtb[:], prm["s5_log_dt"][l][d].partition_broadcast(128), [dtb])
                    kb.ACT(dtb[:], dtb[:], AF.Exp, [dtb], [dtb])
                    dt_bc = dtb[:].unsqueeze(2).to_broadcast([128, 16, 64])
                    lrdt = P.sbuf("s_lrdt", [128, 16, 64]); lidt = P.sbuf("s_lidt", [128, 16, 64])
                    kb.TT(lrdt[:], lr[:], dt_bc, ALU.mult, [lr, dtb], [lrdt])
                    kb.TT(lidt[:], li[:], dt_bc, ALU.mult, [li, dtb], [lidt])
                    a = [P.sbuf("s_a%d" % i, [128, 16, 64]) for i in range(8)]
                    mag, ang, sn, cs, tmp, ar, ai, t2 = a
                    kb.ACT(mag[:], lrdt[:], AF.Exp, [lrdt], [mag])
                    _sincos(kb, lidt[:], sn[:], cs[:], [lidt, sn, cs, tmp], tmp[:])
                    kb.TT(ar[:], mag[:], cs[:], ALU.mult, [mag, cs], [ar])
                    kb.TT(ai[:], mag[:], sn[:], ALU.mult, [mag, sn], [ai])
                    den = P.sbuf("s_den", [128, 16, 64]); fr = P.sbuf("s_fr", [128, 16, 64]); fi = P.sbuf("s_fi", [128, 16, 64])
                    kb.TT(den[:], lr[:], lr[:], ALU.mult, [lr], [den])
                    kb.TT(t2[:], li[:], li[:], ALU.mult, [li], [t2])
                    kb.TT(den[:], den[:], t2[:], ALU.add, [den, t2], [den])
                    kb.RECIP(den[:], den[:], [den], [den])
                    kb.TS(ar[:], ar[:], -1.0, None, ALU.add, None, [ar], [ar])
                    kb.TT(fr[:], ar[:], lr[:], ALU.mult, [ar, lr], [fr])
                    kb.TT(t2[:], ai[:], li[:], ALU.mult, [ai, li], [t2])
                    kb.TT(fr[:], fr[:], t2[:], ALU.add, [fr, t2], [fr])
                    kb.TT(fr[:], fr[:], den[:], ALU.mult, [fr, den], [fr])
                    kb.TT(fi[:], ai[:], lr[:], ALU.mult, [ai, lr], [fi])
                    kb.TT(t2[:], ar[:], li[:], ALU.mult, [ar, li], [t2])
                    kb.TT(fi[:], fi[:], t2[:], ALU.subtract, [fi, t2], [fi])
                    kb.TT(fi[:], fi[:], den[:], ALU.mult, [fi, den], [fi])
                    jcol = kb.C("CCOL")[:, 2:3] if d == 0 else kb.C("CCOL")[:, 3:4]
                    njcol = kb.C("CCOL")[:, 6:7] if d == 0 else kb.C("CCOL")[:, 7:8]
                    kb.ACT(mag[:], lrdt[:], AF.Exp, [lrdt], [mag], scale=njcol)
                    kb.TS(ang[:], lidt[:], jcol, None, ALU.mult, None, [lidt], [ang])
                    _sincos(kb, ang[:], sn[:], cs[:], [ang, sn, cs, tmp], tmp[:])
                    vr, vi = ar, ai
                    kb.TT(vr[:], mag[:], cs[:], ALU.mult, [mag, cs], [vr])
                    kb.TT(vi[:], mag[:], sn[:], ALU.mult, [mag, sn], [vi])
                    kb.TS(vi[:], vi[:], -1.0, None, ALU.mult, None, [vi], [vi])
                    kb.TT(VFr[:], vr[:], fr[:], ALU.mult, [vr, fr], [VFr])
                    kb.TT(t2[:], vi[:], fi[:], ALU.mult, [vi, fi], [t2])
                    kb.TT(VFr[:], VFr[:], t2[:], ALU.subtract, [VFr, t2], [VFr])
                    kb.TT(VFi[:], vr[:], fi[:], ALU.mult, [vr, fi], [VFi])
                    kb.TT(t2[:], vi[:], fr[:], ALU.mult, [vi, fr], [t2])
                    kb.TT(VFi[:], VFi[:], t2[:], ALU.add, [VFi, t2], [VFi])
                with P.scope():
                    dtb = P.sbuf("s_dt2", [128, 16])
                    kb.LD(dtb[:], prm["s5_log_dt"][l][d].partition_broadcast(128), [dtb])
                    kb.ACT(dtb[:], dtb[:], AF.Exp, [dtb], [dtb])
                    lrp = P.sbuf("s_lrp", [128, 16]); lip = P.sbuf("s_lip", [128, 16])
                    for hh in range(2):
                        kb.LD(lrp[64 * hh:64 * hh + 64, :], prm["s5_lam_re"][l][d].rearrange("g p -> p g"), [lrp],
                              allow_slow_non_contiguous=True)
                        kb.LD(lip[64 * hh:64 * hh + 64, :], prm["s5_lam_im"][l][d].rearrange("g p -> p g"), [lip],
                              allow_slow_non_contiguous=True)
                    kb.TT(lrp[:], lrp[:], dtb[:], ALU.mult, [lrp, dtb], [lrp])
                    kb.TT(lip[:], lip[:], dtb[:], ALU.mult, [lip, dtb], [lip])
                    b4 = [P.sbuf("s_b%d" % i, [128, 16, 128]) for i in range(4)]
                    arg, sn2, cs2, tmp2 = b4
                    mt = kb.C("IOTAF" if d == 0 else "R127F")
                    mt_bc = mt.unsqueeze(1).to_broadcast([128, 16, 128])
                    kb.TT(arg[:], lrp[:].unsqueeze(2).to_broadcast([128, 16, 128]), mt_bc, ALU.mult, [lrp], [arg])
                    kb.ACT(T1[:], arg[:], AF.Exp, [arg], [T1])
                    kb.TT(arg[:], lip[:].unsqueeze(2).to_broadcast([128, 16, 128]), mt_bc, ALU.mult, [lip, T1], [arg])
                    _sincos(kb, arg[:], sn2[:], cs2[:], [arg, sn2, cs2, tmp2], tmp2[:])
                    kb.TT(T2[:], T1[:], sn2[:], ALU.mult, [T1, sn2], [T2])
                    kb.TS(T2[:], T2[:], -1.0, None, ALU.mult, None, [T2], [T2])
                    kb.TT(T1[:], T1[:], cs2[:], ALU.mult, [T1, cs2], [T1])
                    c4 = [P.sbuf("s_c%d" % i, [128, 16]) for i in range(4)]
                    kb.ACT(c4[0][:], lrp[:], AF.Exp, [lrp], [c4[0]])
                    _sincos(kb, lip[:], c4[1][:], c4[2][:], [lip, c4[1], c4[2], c4[3]], c4[3][:])
                    kb.TT(AR[:], c4[0][:], c4[2][:], ALU.mult, [c4[0], c4[2]], [AR])
                    kb.TT(NAI[:], c4[0][:], c4[1][:], ALU.mult, [c4[0], c4[1]], [NAI])
                    kb.TS(NAI[:], NAI[:], -1.0, None, ALU.mult, None, [NAI], [NAI])
                sweep_scope = P.scope(); sweep_scope.__enter__()
                uTb = [P.sbuf("s_u%d" % i, [128, 2, 128]) for i in range(2)]
                mm_ = [P.sbuf("s_m%d" % i, [128, 8, 64]) for i in range(4)]
                W3 = [P.sbuf("s_W3%d" % i, [128, 8, 3, 64], BF16) for i in range(2)]
                Hb = [P.sbuf("s_Hb%d" % i, [128, 8, 128], BF16) for i in range(2)]
                tri16 = P.sbuf("s_tri16", [128, 128], BF16)
                kb.CP(tri16[:], kb.C("TRIF" if d == 0 else "TRIB"), [], [tri16])
                tP = [P.sbuf("s_tP%d" % i, [128, 8, 128]) for i in range(2)]
                tPs = [P.sbuf("s_tPs%d" % i, [128, 8, 128]) for i in range(2)]
                H1 = [P.sbuf("s_H1%d" % i, [128, 8, 128]) for i in range(2)]
                H2 = [P.sbuf("s_H2%d" % i, [128, 8, 128]) for i in range(2)]
                hend = P.sbuf("s_hend", [128, 16]); hsend = P.sbuf("s_hsend", [128, 16])
                hp_ = P.sbuf("s_hp", [128, 16]); hps_ = P.sbuf("s_hps", [128, 16])
                sm = [P.sbuf("s_sm%d" % i, [128, 16]) for i in range(4)]
                xps = P.psum("s_xps", [128, 1024])
                pps = P.psum("s_pps", [128, 8, 128])
                ppss = P.psum("s_ppss", [128, 8, 128])
                yps = [P.psum("s_yps%d" % i, [128, 128]) for i in range(2)]
                kb.MS(hp_[:], 0.0, [hp_]); kb.MS(hps_[:], 0.0, [hps_])
                te = 127 if d == 0 else 0
                tri = kb.C("TRIF" if d == 0 else "TRIB")
                it = 0
                for n in ORDER[d]:
                    cols = slice(n * 128, (n + 1) * 128)
                    uT = uTb[it % 2]; it += 1
                    kb.LD(uT[:], kb.ZF[Z_SU:Z_SU + 256, cols].rearrange("(gg p) t -> p gg t", p=128), [uT])
                    for gg in range(2):
                        j = gg
                        for half in range(2):
                            kb.MM(xps[:, half * 512:(half + 1) * 512], uT[:, gg, :],
                                  WX[:, gg, half * 4:(half + 1) * 4, :, :].rearrange("q a r p -> q (a r p)"),
                                  True, True, [uT, WX], [xps])
                        xv = xps[:].rearrange("t (g r p) -> t g r p", r=2, p=64)
                        gs = slice(gg * 8, gg * 8 + 8)
                        kb.TT(mm_[0][:], xv[:, :, 0, :], VFr[:, gs, :], ALU.mult, [xps, VFr], [mm_[0]])
                        kb.TT(mm_[1][:], xv[:, :, 1, :], VFi[:, gs, :], ALU.mult, [xps, VFi], [mm_[1]])
                        kb.TT(mm_[2][:], xv[:, :, 0, :], VFi[:, gs, :], ALU.mult, [xps, VFi], [mm_[2]])
                        kb.TT(mm_[3][:], xv[:, :, 1, :], VFr[:, gs, :], ALU.mult, [xps, VFr], [mm_[3]])
                        w3 = W3[j]
                        kb.TT(w3[:, :, 0, :], mm_[0][:], mm_[1][:], ALU.subtract, [mm_[0], mm_[1]], [w3], eng="pool")
                        kb.TT(w3[:, :, 1, :], mm_[2][:], mm_[3][:], ALU.add, [mm_[2], mm_[3]], [w3], eng="pool")
                        kb.TT(w3[:, :, 2, :], mm_[1][:], mm_[0][:], ALU.subtract, [mm_[0], mm_[1]], [w3], eng="pool")
                        for g8 in range(8):
                            kb.MM(pps[:, g8, :], w3[:, g8, 0:2, :].rearrange("q r p -> q (r p)"), tri16[:], True, True, [w3, tri16], [pps])
                            kb.MM(ppss[:, g8, :], w3[:, g8, 1:3, :].rearrange("q r p -> q (r p)"), tri16[:], True, True, [w3, tri16], [ppss])
                        kb.TT(tP[j][:], pps[:], hp_[:, gs].unsqueeze(2).to_broadcast([128, 8, 128]), ALU.add, [pps, hp_], [tP[j]])
                        kb.TT(tPs[j][:], ppss[:], hps_[:, gs].unsqueeze(2).to_broadcast([128, 8, 128]), ALU.add,
                              [ppss, hps_], [tPs[j]])
                        kb.TT(H1[j][:], tP[j][:], T1[:, gs, :], ALU.mult, [tP[j], T1], [H1[j]], eng="pool")
                        kb.TT(H2[j][:], tPs[j][:], T2[:, gs, :], ALU.mult, [tPs[j], T2], [H2[j]])
                        kb.TT(Hb[j][:], H1[j][:], H2[j][:], ALU.add, [H1[j], H2[j]], [Hb[j]], eng="pool")
                        yp = yps[gg]
                        for g8 in range(8):
                            kb.MM(yp[:], Cb16[:, gg * 8 + g8, :], Hb[j][:, g8, :], g8 == 0, g8 == 7, [Cb16, Hb[j]], [yp])
                        oacc_write(kb, OACC, gg, n, yp, d)
                        kb.TT(hend[:, gs], H1[j][:, :, te], H2[j][:, :, te], ALU.add, [H1[j], H2[j]], [hend])
                        kb.TT(sm[0][:, 0:8], tPs[j][:, :, te], T1[:, gs, te], ALU.mult, [tPs[j], T1], [sm[0]])
                        kb.TT(sm[1][:, 0:8], tP[j][:, :, te], T2[:, gs, te], ALU.mult, [tP[j], T2], [sm[1]])
                        kb.TT(hsend[:, gs], sm[0][:, 0:8], sm[1][:, 0:8], ALU.subtract, [sm[0], sm[1]], [hsend])
                    kb.TT(sm[0][:], hend[:], AR[:], ALU.mult, [hend, AR], [sm[0]])
                    kb.TT(sm[1][:], hsend[:], NAI[:], ALU.mult, [hsend, NAI], [sm[1]])
                    kb.TT(sm[2][:], hsend[:], AR[:], ALU.mult, [hsend, AR], [sm[2]])
                    kb.TT(sm[3][:], hend[:], NAI[:], ALU.mult, [hend, NAI], [sm[3]])
                    kb.TT(hp_[:], sm[0][:], sm[1][:], ALU.add, [sm[0], sm[1]], [hp_])
                    kb.TT(hps_[:], sm[2][:], sm[3][:], ALU.subtract, [sm[2], sm[3]], [hps_])
                sweep_scope.__exit__(None, None, None)
        with P.scope():
            dsk = P.sbuf("s_dsk", [128, 2]); glb = P.sbuf("s_glb", [128, 2])
            kb.LD(dsk[:], prm["s5_d"][l].rearrange("(gg p) -> p gg", p=128), [dsk], allow_slow_non_contiguous=True)
            kb.LD(glb[:], prm["s5_glu_b"][l].rearrange("(gg p) -> p gg", p=128), [glb], allow_slow_non_contiguous=True)
            gw = P.sbuf("s_gw", [128, 2, 256])
            kb.LD(gw[:], prm["s5_glu_w"][l].rearrange("(ct p) o -> p ct o", p=128), [gw])
            uTb = [P.sbuf("s_fu%d" % i, [128, 2, 128]) for i in range(2)]
            yy = [P.sbuf("s_yy%d" % i, [128, 2, 128]) for i in range(2)]
            x2 = [P.sbuf("s_x2%d" % i, [128, 2, 128]) for i in range(2)]
            th = [P.sbuf("s_th%d" % i, [128, 2, 128]) for i in range(2)]
            sgb = [P.sbuf("s_sg%d" % i, [128, 128]) for i in range(2)]
            ob = [P.sbuf("s_ob%d" % i, [128, 128]) for i in range(2)]
            psz = [P.psum("s_psz%d" % i, [128, 128]) for i in range(2)]
            k = 0
            for n in range(NT):
                cols = slice(n * 128, (n + 1) * 128)
                i = n % 2
                kb.LD(uTb[i][:], kb.ZF[Z_SU:Z_SU + 256, cols].rearrange("(gg p) t -> p gg t", p=128), [uTb[i]])
                for gg in range(2):
                    kb.STT(yy[i][:, gg, :], uTb[i][:, gg, :], dsk[:, gg:gg + 1], OACC[:, gg, cols], ALU.mult, ALU.add,
                           [uTb[i], dsk, OACC.s(n)], [yy[i]])
                kb.TT(x2[i][:], yy[i][:], yy[i][:], ALU.mult, [yy[i]], [x2[i]], eng="pool")
                kb.TS(x2[i][:], x2[i][:], 0.044715, 1.0, ALU.mult, ALU.add, [x2[i]], [x2[i]])
                kb.TT(x2[i][:], x2[i][:], yy[i][:], ALU.mult, [x2[i], yy[i]], [x2[i]], eng="pool")
                kb.ACT(th[i][:], x2[i][:], AF.Tanh, [x2[i]], [th[i]], scale=0.7978845608028654)
                kb.TS(th[i][:], th[i][:], 1.0, 0.5, ALU.add, ALU.mult, [th[i]], [th[i]])
                kb.TT(yy[i][:], yy[i][:], th[i][:], ALU.mult, [yy[i], th[i]], [yy[i]], eng="pool")
                for ot in range(2):
                    q = k % 2; k += 1
                    for ct in range(2):
                        kb.MM(psz[q][:], gw[:, ct, ot * 128:(ot + 1) * 128], yy[i][:, ct, :], ct == 0, ct == 1, [gw, yy[i]], [psz[q]])
                    kb.ACT(sgb[q][:], psz[q][:], AF.Sigmoid, [psz[q], glb], [sgb[q]], bias=glb[:, ot:ot + 1])
                    kb.TT(ob[q][:], yy[i][:, ot, :], sgb[q][:], ALU.mult, [yy[i], sgb[q]], [ob[q]])
                    kb.ST(kb.YC[768 + ot * 128:768 + (ot + 1) * 128, cols], ob[q][:], [ob[q]])


def gdn_conv(kb, l):
    P = kb.P
    with P.scope():
        CW = P.sbuf("g_cw", [128, 6, 9])
        for kh in range(3):
            for kw in range(3):
                kb.LD(CW[:, :, kh * 3 + kw], kb.prm["gdn_conv_w"][l][kh, kw].rearrange("(ct p) -> p ct", p=128), [CW],
                      allow_slow_non_contiguous=True)
        mlat = P.sbuf("g_mlat", [128, 2, 512]); mctx = P.sbuf("g_mctx", [128, 2, 256])
        kb.LD(mlat[:], kb.cmlat[:], [mlat]); kb.LD(mctx[:], kb.cmctx[:], [mctx])
        Wb = [P.sbuf("g_w%d" % i, [128, 642]) for i in range(2)]
        acc = [[P.sbuf("g_acc%d%d" % (i, j), [128, 512]) for j in range(3)] for i in range(2)]
        sl = [P.sbuf("g_sl%d" % i, [128, 512]) for i in range(2)]
        sq = [P.sbuf("g_sq%d" % i, [128, 512]) for i in range(2)]
        rt = [P.sbuf("g_rt%d" % i, [128, 512]) for i in range(2)]
        ps = [P.psum("g_psn%d" % i, [128, 512]) for i in range(2)]
        spans = [(0, 256, True)] + [(256 + 512 * k, 512, False) for k in range(8)]
        it = 0
        for (t0, L, is_ctx) in spans:
            lo = 0 if is_ctx else 256
            hi = 256 if is_ctx else S
            a = max(lo, t0 - 65); b = min(hi, t0 + L + 65)
            for ct in range(6):
                i = it % 2; it += 1
                W = Wb[i]
                kb.MS(W[:], 0.0, [W], eng="pool")
                kb.LD(W[:, 65 + (a - t0):65 + (b - t0)], kb.ZF[Z_GQKV + ct * 128:Z_GQKV + (ct + 1) * 128, a:b], [W])
                rows = (1,) if is_ctx else (0, 1, 2)
                masks = mctx if is_ctx else mlat
                for dwi, shift in enumerate((-1, 0, 1)):
                    A = acc[i][dwi]
                    eng = "dve"
                    for q, dh in enumerate(rows):
                        o0 = 65 + 64 * (dh - 1) + shift
                        src = W[:, o0:o0 + L]
                        wcol = CW[:, ct, dh * 3 + dwi:dh * 3 + dwi + 1]
                        if q == 0:
                            kb.TS(A[:, :L], src, wcol, None, ALU.mult, None, [W, CW], [A], eng=("pool" if dwi != 1 else "dve"))
                        else:
                            kb.STT(A[:, :L], src, wcol, A[:, :L], ALU.mult, ALU.add, [W, CW, A], [A])
                    if dwi != 1:
                        mi = 0 if dwi == 0 else 1
                        kb.TT(A[:, :L], A[:, :L], masks[:, mi, :L], ALU.mult, [A, masks], [A], eng="pool")
                A0, A1, A2 = acc[i]
                kb.TT(A1[:, :L], A1[:, :L], A0[:, :L], ALU.add, [A0, A1], [A1], eng="pool")
                kb.TT(A1[:, :L], A1[:, :L], A2[:, :L], ALU.add, [A1, A2], [A1], eng="pool")
                kb.ACT(sl[i][:, :L], A1[:, :L], AF.Silu, [A1], [sl[i]])
                if ct < 4:
                    kb.TT(sq[i][:, :L], sl[i][:, :L], sl[i][:, :L], ALU.mult, [sl[i]], [sq[i]], eng="pool")
                    kb.MM(ps[i][:, :L], kb.C("BLK64"), sq[i][:, :L], True, True, [sq[i]], [ps[i]])
                    kb.ACT(rt[i][:, :L], ps[i][:, :L], AF.Sqrt, [ps[i]], [rt[i]], bias=kb.C("CCOL")[:, 0:1])
                    kb.RECIP(rt[i][:, :L], rt[i][:, :L], [rt[i]], [rt[i]])
                    if ct < 2:
                        kb.STT(sl[i][:, :L], sl[i][:, :L], 0.125, rt[i][:, :L], ALU.mult, ALU.mult, [sl[i], rt[i]], [sl[i]])
                    else:
                        kb.TT(sl[i][:, :L], sl[i][:, :L], rt[i][:, :L], ALU.mult, [sl[i], rt[i]], [sl[i]])
                kb.ST(kb.QKVF[ct * 128:(ct + 1) * 128, t0:t0 + L], sl[i][:, :L], [sl[i]])


def mixer_gdn(kb, l):
    P = kb.P
    gdn_conv(kb, l)
    upto = kb.cfg.get("gdn_upto", 99)
    if upto < 1:
        return
    with P.scope():
        OACC = P.sbuf("g_oacc", [128, 2, S])
        with P.scope():
            DTB = P.sbuf("g_dtb", [128, 8]); NEGA = P.sbuf("g_nega", [128, 8])
            kb.LD(DTB[:], kb.prm["gdn_dt_bias"][l].rearrange("d h -> (d h)").partition_broadcast(128), [DTB])
            kb.LD(NEGA[:], kb.prm["gdn_a_log"][l].rearrange("d h -> (d h)").partition_broadcast(128), [NEGA])
            kb.ACT(NEGA[:], NEGA[:], AF.Exp, [NEGA], [NEGA])
            kb.TS(NEGA[:], NEGA[:], -1.0, None, ALU.mult, None, [NEGA], [NEGA])
            qnb = [P.sbuf("g_q%d" % i, [128, 2, 128]) for i in range(2)]
            knb = [P.sbuf("g_k%d" % i, [128, 2, 128]) for i in range(2)]
            vvb = [P.sbuf("g_v%d" % i, [128, 2, 128]) for i in range(2)]
            gabb = [P.sbuf("g_gab%d" % i, [128, 16]) for i in range(2)]

            def sm4(name, w=4):
                return P.sbuf("g_" + name, [128, w])
            xa, ea, loga, beta, lnb = sm4("xa"), sm4("ea"), sm4("loga"), sm4("beta"), sm4("lnb")
            gtm, ngt, ekr, cdec, eg, beg, gpl = sm4("gtm"), sm4("ngt"), sm4("ekr"), sm4("cdec"), sm4("eg"), sm4("beg"), sm4("gpl")
            ROWS = P.sbuf("g_rows", [4, 384])
            LI = P.sbuf("g_LI", [128, 4, 128]); LBT = P.sbuf("g_LBT", [128, 4, 128]); LBm = P.sbuf("g_LB", [128, 4, 128])
            NAT = P.sbuf("g_NAT", [128, 4, 128]); NA = P.sbuf("g_NA", [128, 4, 128]); QKm = P.sbuf("g_QKm", [128, 4, 128])
            Tm = P.sbuf("g_Tm", [128, 4, 128]); Wm = P.sbuf("g_Wm", [128, 4, 128])
            x1 = P.sbuf("g_x1", [128, 4, 128]); y1 = P.sbuf("g_y1", [128, 4, 128])
            tmx = P.sbuf("g_tmx", [128, 4, 128]); tmy = P.sbuf("g_tmy", [128, 4, 128])
            Rm = [P.sbuf("g_R%d" % h, [128, 128]) for h in range(4)]
            khp = [P.sbuf("g_kh%d" % h, [128, 128]) for h in range(4)]
            vnp = [P.sbuf("g_vn%d" % h, [128, 128]) for h in range(4)]
            for h in range(4):
                kb.MS(khp[h][:], 0.0, [khp[h]], eng="pool")
                kb.MS(vnp[h][:], 0.0, [vnp[h]], eng="pool")
            upair = [P.sbuf("g_up%d" % hp, [128, 128]) for hp in range(2)]
            wTp = [P.sbuf("g_wT%d" % hp, [128, 128]) for hp in range(2)]
            EG = [P.sbuf("g_EG%d" % hp, [128, 128]) for hp in range(2)]
            qd = [P.sbuf("g_qd%d" % hp, [128, 128]) for hp in range(2)]
            cdp = [P.sbuf("g_cdp%d" % hp, [128, 1]) for hp in range(2)]
            Sb = [P.sbuf("g_S%d" % hp, [128, 128]) for hp in range(2)]
            B = [P.psum("g_B%d" % i, [128, 512]) for i in range(8)]
            ident = kb.C("IDENT")
            it = 0
            for d in range(2):
                tri = kb.C("TRIF" if d == 0 else "TRIB")
                rem = kb.C("SUFF" if d == 0 else "PREB")
                n_incl = kb.C("NLE" if d == 0 else "NGE")
                n_strT = kb.C("NLT" if d == 0 else "NGT")
                n_str = kb.C("NGT" if d == 0 else "NLT")
                for hp in range(2):
                    kb.MS(Sb[hp][:], 0.0, [Sb[hp]])
                for n in ORDER[d][:kb.cfg.get("ntiles", NT)]:
                    cols = slice(n * 128, (n + 1) * 128)
                    b = it % 2; it += 1
                    qn, kn, vv, gab = qnb[b], knb[b], vvb[b], gabb[b]
                    kb.LD(qn[:], kb.QKVF[0:256, cols].rearrange("(hp p) t -> p hp t", p=128), [qn])
                    kb.LD(kn[:], kb.QKVF[256:512, cols].rearrange("(hp p) t -> p hp t", p=128), [kn])
                    kb.LD(vv[:], kb.QKVF[512:768, cols].rearrange("(hp p) t -> p hp t", p=128), [vv])
                    kb.LD(gab[:], kb.ZT[cols, 512:528], [gab])
                    kb.TT(xa[:], gab[:, 4 * d:4 * d + 4], DTB[:, 4 * d:4 * d + 4], ALU.add, [gab, DTB], [xa])
                    kb.ACT(ea[:], xa[:], AF.Exp, [xa], [ea])
                    kb.ACT(ea[:], ea[:], AF.Ln, [ea], [ea], bias=kb.C("CCOL")[:, 1:2])
                    kb.TT(loga[:], ea[:], NEGA[:, 4 * d:4 * d + 4], ALU.mult, [ea, NEGA], [loga])
                    kb.ACT(beta[:], gab[:, 8 + 4 * d:12 + 4 * d], AF.Sigmoid, [gab], [beta])
                    kb.ACT(lnb[:], beta[:], AF.Ln, [beta], [lnb])
                    kb.MM(B[0][:, 0:4], tri, loga[:], True, True, [loga], [B[0]])
                    kb.MM(B[0][:, 4:8], rem, loga[:], True, True, [loga], [B[0]])
                    kb.MM(B[0][:, 8:12], kb.C("ONES"), loga[:], True, True, [loga], [B[0]])
                    kb.CP(gtm[:], B[0][:, 0:4], [B[0]], [gtm])
                    kb.TS(ngt[:], B[0][:, 0:4], -1.0, None, ALU.mult, None, [B[0]], [ngt])
                    kb.ACT(ekr[:], B[0][:, 4:8], AF.Exp, [B[0]], [ekr])
                    kb.ACT(cdec[:], B[0][:, 8:12], AF.Exp, [B[0]], [cdec])
                    kb.ACT(eg[:], gtm[:], AF.Exp, [gtm], [eg])
                    kb.TT(beg[:], beta[:], eg[:], ALU.mult, [beta, eg], [beg])
                    kb.TT(gpl[:], gtm[:], lnb[:], ALU.add, [gtm, lnb], [gpl])
                    kb.MM(B[1][0:4, 0:128], loga[:], tri, True, True, [loga], [B[1]])
                    kb.MM(B[1][0:4, 128:256], loga[:], tri, True, False, [loga], [B[1]])
                    kb.MM(B[1][0:4, 128:256], lnb[:], ident, False, True, [lnb], [B[1]])
                    kb.CP(ROWS[:, 0:256], B[1][0:4, 0:256], [B[1]], [ROWS])
                    kb.TS(ROWS[:, 256:384], B[1][0:4, 0:128], -1.0, None, ALU.mult, None, [B[1]], [ROWS])
                    if upto < 2:
                        continue
                    for (dst, rsl, negm, bias_t, bank) in ((LI, slice(0, 128), n_incl, ngt, B[2]),
                                                           (LBT, slice(128, 256), n_strT, ngt, B[3]),
                                                           (LBm, slice(256, 384), n_str, gpl, B[2])):
                        for h in range(4):
                            kb.MM(bank[:, h * 128:(h + 1) * 128], kb.C("SELH%d" % h)[0:4, :], ROWS[:, rsl], True, False,
                                  [ROWS], [bank])
                            kb.MM(bank[:, h * 128:(h + 1) * 128], ident, negm, False, True, [], [bank])
                        for h in range(4):
                            kb.ACT(dst[:, h, :], bank[:, h * 128:(h + 1) * 128], AF.Exp, [bank, bias_t], [dst],
                                   bias=bias_t[:, h:h + 1])
                    if upto < 3:
                        continue
                    for h in range(4):
                        hp, h2 = divmod(h, 2)
                        ksl = kn[64 * h2:64 * h2 + 64, hp, :]
                        kb.MM(B[4][:, h * 128:(h + 1) * 128], ksl, ksl, True, True, [kn], [B[4]])
                        kb.MM(B[5][:, h * 128:(h + 1) * 128], ksl, qn[64 * h2:64 * h2 + 64, hp, :], True, True, [kn, qn], [B[5]])
                    b4v = B[4][:].rearrange("p (h t) -> p h t", h=4)
                    b5v = B[5][:].rearrange("p (h t) -> p h t", h=4)
                    kb.STT(NAT[:], b4v, -1.0, LBT[:], ALU.mult, ALU.mult, [B[4], LBT], [NAT])
                    kb.STT(NA[:], b4v, -1.0, LBm[:], ALU.mult, ALU.mult, [B[4], LBm], [NA])
                    kb.TT(QKm[:], b5v, LI[:], ALU.mult, [B[5], LI], [QKm])
                    if upto < 4:
                        continue
                    idb = ident.unsqueeze(1).to_broadcast([128, 4, 128])
                    kb.CP(Tm[:], idb, [], [Tm])
                    kb.CP(Wm[:], idb, [], [Wm], eng="pool")
                    for s_ in (1, 2, 4, 8, 16, 32, 64):
                        mT = kb.C(("MOFF%d" if d == 0 else "MOFFT%d") % s_).unsqueeze(1).to_broadcast([128, 4, 128])
                        mW = kb.C(("MOFFT%d" if d == 0 else "MOFF%d") % s_).unsqueeze(1).to_broadcast([128, 4, 128])
                        for h in range(4):
                            kb.MM(B[2][:, h * 128:(h + 1) * 128], NAT[:, h, :], Tm[:, h, :], True, True, [NAT, Tm], [B[2]])
                        for h in range(4):
                            kb.MM(B[3][:, h * 128:(h + 1) * 128], NA[:, h, :], Wm[:, h, :], True, True, [NA, Wm], [B[3]])
                        kb.CP(x1[:], B[2][:].rearrange("p (h t) -> p h t", h=4), [B[2]], [x1], eng="act")
                        kb.CP(y1[:], B[3][:].rearrange("p (h t) -> p h t", h=4), [B[3]], [y1], eng="dve")
                        for h in range(4):
                            kb.MM(B[4][:, h * 128:(h + 1) * 128], Wm[:, h, :], x1[:, h, :], True, True, [Wm, x1], [B[4]])
                        for h in range(4):
                            kb.MM(B[5][:, h * 128:(h + 1) * 128], Tm[:, h, :], y1[:, h, :], True, True, [Tm, y1], [B[5]])
                        kb.TT(tmx[:], B[4][:].rearrange("p (h t) -> p h t", h=4), mT, ALU.mult, [B[4]], [tmx])
                        kb.TT(tmy[:], B[5][:].rearrange("p (h t) -> p h t", h=4), mW, ALU.mult, [B[5]], [tmy])
                        kb.TT(Tm[:], Tm[:], tmx[:], ALU.add, [Tm, tmx], [Tm], eng="pool")
                        kb.TT(Wm[:], Wm[:], tmy[:], ALU.add, [Wm, tmy], [Wm], eng="pool")
                    if upto < 5:
                        continue
                    for hp in range(2):
                        kb.TR(B[0][:, 128:256], kn[:, hp, :], ident, [kn], [B[0]])
                        kb.TR(B[0][:, 256:384], vv[:, hp, :], ident, [vv], [B[0]])
                        for h2 in range(2):
                            h = 2 * hp + h2
                            kc = slice(64 * h2, 64 * h2 + 64)
                            vc = slice(64 * (1 - h2), 64 * (1 - h2) + 64)
                            kb.TS(Rm[h][:, kc], B[0][:, 128 + 64 * h2:128 + 64 * h2 + 64], beg[:, h:h + 1], None, ALU.mult, None,
                                  [B[0], beg], [Rm[h]])
                            kb.ACT(Rm[h][:, vc], B[0][:, 256 + 64 * h2:256 + 64 * h2 + 64], AF.Copy, [B[0], beta], [Rm[h]],
                                   scale=beta[:, h:h + 1])
                            kb.ACT(khp[h][:, kc], B[0][:, 128 + 64 * h2:128 + 64 * h2 + 64], AF.Copy, [B[0], ekr], [khp[h]],
                                   scale=ekr[:, h:h + 1])
                    if upto < 6:
                        continue
                    for h in range(4):
                        kb.MM(B[2][:, h * 128:(h + 1) * 128], Wm[:, h, :], Rm[h][:], True, True, [Wm, Rm[h]], [B[2]])
                        kb.MM(B[3][:, h * 128:(h + 1) * 128], Rm[h][:], Wm[:, h, :], True, True, [Wm, Rm[h]], [B[3]])
                    for h in range(4):
                        hp, h2 = divmod(h, 2)
                        vc0 = 64 * (1 - h2)
                        kb.CP(upair[hp][:, 64 * h2:64 * h2 + 64], B[2][:, h * 128 + vc0:h * 128 + vc0 + 64], [B[2]], [upair[hp]],
                              eng=("act" if h2 else "dve"))
                        kb.CP(wTp[hp][64 * h2:64 * h2 + 64, :], B[3][64 * h2:64 * h2 + 64, h * 128:(h + 1) * 128], [B[3]], [wTp[hp]],
                              eng=("dve" if h2 else "act"))
                    if upto < 7:
                        continue
                    for hp in range(2):
                        kb.MM(B[1][:, 256:384], kb.C("SELP%d" % hp)[0:4, :], ROWS[:, 0:128], True, True, [ROWS], [B[1]])
                        kb.ACT(EG[hp][:], B[1][:, 256:384], AF.Exp, [B[1]], [EG[hp]])
                        kb.TT(qd[hp][:], qn[:, hp, :], EG[hp][:], ALU.mult, [qn, EG[hp]], [qd[hp]], eng="pool")
                        pws = B[7][:, hp * 128:(hp + 1) * 128]
                        kb.MM(pws, wTp[hp][:], Sb[hp][:], True, True, [wTp[hp], Sb[hp]], [B[7]])
                        for h2 in range(2):
                            h = 2 * hp + h2
                            cs_ = slice(64 * h2, 64 * h2 + 64)
                            kb.TT(vnp[h][:, cs_], upair[hp][:, cs_], B[7][:, hp * 128 + 64 * h2:hp * 128 + 64 * h2 + 64],
                                  ALU.subtract, [upair[hp], B[7]], [vnp[h]])
                        po = B[6][:, hp * 256:hp * 256 + 128]
                        kb.MM(po, Sb[hp][:], qd[hp][:], True, False, [Sb[hp], qd[hp]], [B[6].s(hp)])
                        kb.MM(po, vnp[2 * hp][:], QKm[:, 2 * hp, :], False, False, [vnp[2 * hp], QKm], [B[6].s(hp)])
                        kb.MM(po, vnp[2 * hp + 1][:], QKm[:, 2 * hp + 1, :], False, True, [vnp[2 * hp + 1], QKm], [B[6].s(hp)])
                        cols_ = slice(n * 128, (n + 1) * 128)
                        if d == 0:
                            kb.CP(OACC[:, hp, cols_], po, [B[6].s(hp)], [OACC.s(n)], eng="act")
                        else:
                            kb.TT(OACC[:, hp, cols_], OACC[:, hp, cols_], po, ALU.add, [B[6].s(hp)], [OACC.s(n)])
                        pkv = B[6][:, hp * 256 + 128:hp * 256 + 256]
                        kb.MM(pkv, khp[2 * hp][:], vnp[2 * hp][:], True, False, [khp[2 * hp], vnp[2 * hp]], [B[6].s(2 + hp)])
                        kb.MM(pkv, khp[2 * hp + 1][:], vnp[2 * hp + 1][:], False, True, [khp[2 * hp + 1], vnp[2 * hp + 1]],
                              [B[6].s(2 + hp)])
                        kb.CP(cdp[hp][0:64, :], cdec[0:64, 2 * hp:2 * hp + 1], [cdec], [cdp[hp]])
                        kb.CP(cdp[hp][64:128, :], cdec[64:128, 2 * hp + 1:2 * hp + 2], [cdec], [cdp[hp]])
                        kb.STT(Sb[hp][:], Sb[hp][:], cdp[hp][:, 0:1], pkv, ALU.mult, ALU.add,
                               [Sb[hp], cdp[hp], B[6].s(2 + hp)], [Sb[hp]])
        with P.scope():
            G = P.sbuf("g_G2", [128, 1])
            for hh in range(2):
                kb.LD(G[64 * hh:64 * hh + 64, :], kb.prm["gdn_norm_g"][l].rearrange("(p o) -> p o", o=1), [G])
            finalize_gated(kb, OACC, Z_GG, G, 512, "g_")


class _Ctx:
    pass


def mixer_gdn2(kb, l):
    P = kb.P
    gdn_conv(kb, l)
    with P.scope():
        OACC = P.sbuf("g_oacc", [128, 2, S])
        kb.MS(OACC[:, 0, :], 0.0, [OACC.s(n) for n in range(NT)], eng="pool")
        kb.MS(OACC[:, 1, :], 0.0, [OACC.s(n) for n in range(NT)], eng="pool")
        with P.scope():
            DTB = P.sbuf("g_dtb", [128, 8]); NEGA = P.sbuf("g_nega", [128, 8])
            kb.LD(DTB[:], kb.prm["gdn_dt_bias"][l].rearrange("d h -> (d h)").partition_broadcast(128), [DTB])
            kb.LD(NEGA[:], kb.prm["gdn_a_log"][l].rearrange("d h -> (d h)").partition_broadcast(128), [NEGA])
            kb.ACT(NEGA[:], NEGA[:], AF.Exp, [NEGA], [NEGA])
            kb.TS(NEGA[:], NEGA[:], -1.0, None, ALU.mult, None, [NEGA], [NEGA])
            ident = kb.C("IDENT")
            idb = ident.unsqueeze(1).to_broadcast([128, 4, 128])
            cxs = []
            for d in range(2):
                cx = _Ctx()
                cx.d = d
                pf = "g%d_" % d
                cx.qnb = [P.sbuf(pf + "q%d" % i, [128, 2, 128]) for i in range(2)]
                cx.knb = [P.sbuf(pf + "k%d" % i, [128, 2, 128]) for i in range(2)]
                cx.vvb = [P.sbuf(pf + "v%d" % i, [128, 2, 128]) for i in range(2)]
                cx.gabb = [P.sbuf(pf + "gab%d" % i, [128, 16]) for i in range(2)]
                for nm in ("xa", "ea", "loga", "beta", "lnb", "gtm", "ngt", "ekr", "cdec", "eg", "beg", "gpl"):
                    setattr(cx, nm, P.sbuf(pf + nm, [128, 4]))
                cx.ROWS = P.sbuf(pf + "rows", [4, 384])
                for nm in ("LI", "LBT", "LBm", "QKm"):
                    setattr(cx, nm, P.sbuf(pf + nm, [128, 4, 128]))
                for nm in ("NAT", "NA", "Tm", "Wm", "x1", "y1", "tmx", "tmy"):
                    setattr(cx, nm, P.sbuf(pf + nm, [128, 4, 128], BF16))
                cx.Rm = [P.sbuf(pf + "R%d" % h, [128, 128], BF16) for h in range(4)]
                cx.khp = [P.sbuf(pf + "kh%d" % h, [128, 128]) for h in range(4)]
                cx.vnp = [P.sbuf(pf + "vn%d" % h, [128, 128]) for h in range(4)]
                for h in range(4):
                    kb.MS(cx.khp[h][:], 0.0, [cx.khp[h]], eng="pool")
                    kb.MS(cx.vnp[h][:], 0.0, [cx.vnp[h]], eng="pool")
                cx.upair = [P.sbuf(pf + "up%d" % hp, [128, 128]) for hp in range(2)]
                cx.wTp = [P.sbuf(pf + "wT%d" % hp, [128, 128]) for hp in range(2)]
                cx.EG = [P.sbuf(pf + "EG%d" % hp, [128, 128]) for hp in range(2)]
                cx.qd = [P.sbuf(pf + "qd%d" % hp, [128, 128]) for hp in range(2)]
                cx.cdp = [P.sbuf(pf + "cdp%d" % hp, [128, 1]) for hp in range(2)]
                cx.Sb = [P.sbuf(pf + "S%d" % hp, [128, 128]) for hp in range(2)]
                for hp in range(2):
                    kb.MS(cx.Sb[hp][:], 0.0, [cx.Sb[hp]])
                cx.B = [P.psum(pf + "B%d" % i, [128, 512]) for i in range(4)]
                cx.tri = kb.C("TRIF" if d == 0 else "TRIB")
                cx.rem = kb.C("SUFF" if d == 0 else "PREB")
                cx.n_incl = kb.C("NLE" if d == 0 else "NGE")
                cx.n_strT = kb.C("NLT" if d == 0 else "NGT")
                cx.n_str = kb.C("NGT" if d == 0 else "NLT")
                cx.it = 0
                cx.mT = {}; cx.mW = {}
                for s_ in (2, 4, 8, 16, 32, 64):
                    for nm_, dct, cn in (("mT", cx.mT, ("MOFF%d" if d == 0 else "MOFFT%d") % s_),
                                         ("mW", cx.mW, ("MOFFT%d" if d == 0 else "MOFF%d") % s_)):
                        mt_ = P.sbuf(pf + nm_ + str(s_), [128, 4, 128], mybir.dt.uint8)
                        kb.CP(mt_[:], kb.C(cn).unsqueeze(1).to_broadcast([128, 4, 128]), [], [mt_])
                        dct[s_] = mt_
                cxs.append(cx)

            def step(cx, n):
                d = cx.d
                Pa, Pb, Pc, Pd = cx.B
                cols = slice(n * 128, (n + 1) * 128)
                b = cx.it % 2; cx.it += 1
                qn, kn, vv, gab = cx.qnb[b], cx.knb[b], cx.vvb[b], cx.gabb[b]
                xa, ea, loga, beta, lnb = cx.xa, cx.ea, cx.loga, cx.beta, cx.lnb
                gtm, ngt, ekr, cdec, eg, beg, gpl = cx.gtm, cx.ngt, cx.ekr, cx.cdec, cx.eg, cx.beg, cx.gpl
                ROWS, LI, LBT, LBm, NAT, NA, QKm = cx.ROWS, cx.LI, cx.LBT, cx.LBm, cx.NAT, cx.NA, cx.QKm
                Tm, Wm, x1, y1, tmx, tmy = cx.Tm, cx.Wm, cx.x1, cx.y1, cx.tmx, cx.tmy
                Rm, khp, vnp, upair, wTp, EG, qd, cdp, Sb = cx.Rm, cx.khp, cx.vnp, cx.upair, cx.wTp, cx.EG, cx.qd, cx.cdp, cx.Sb
                tri = cx.tri
                kb.LD(qn[:], kb.QKVF[0:256, cols].rearrange("(hp p) t -> p hp t", p=128), [qn])
                kb.LD(kn[:], kb.QKVF[256:512, cols].rearrange("(hp p) t -> p hp t", p=128), [kn])
                kb.LD(vv[:], kb.QKVF[512:768, cols].rearrange("(hp p) t -> p hp t", p=128), [vv])
                kb.LD(gab[:], kb.ZT[cols, 512:528], [gab])
                kb.TT(xa[:], gab[:, 4 * d:4 * d + 4], DTB[:, 4 * d:4 * d + 4], ALU.add, [gab, DTB], [xa])
                kb.ACT(ea[:], xa[:], AF.Exp, [xa], [ea])
                kb.ACT(ea[:], ea[:], AF.Ln, [ea], [ea], bias=kb.C("CCOL")[:, 1:2])
                kb.TT(loga[:], ea[:], NEGA[:, 4 * d:4 * d + 4], ALU.mult, [ea, NEGA], [loga])
                kb.ACT(beta[:], gab[:, 8 + 4 * d:12 + 4 * d], AF.Sigmoid, [gab], [beta])
                kb.ACT(lnb[:], beta[:], AF.Ln, [beta], [lnb])
                kb.MM(Pc[:, 0:4], tri, loga[:], True, True, [loga], [Pc])
                kb.MM(Pc[:, 4:8], cx.rem, loga[:], True, True, [loga], [Pc])
                kb.MM(Pc[:, 8:12], kb.C("ONES"), loga[:], True, True, [loga], [Pc])
                kb.CP(gtm[:], Pc[:, 0:4], [Pc], [gtm])
                kb.TS(ngt[:], Pc[:, 0:4], -1.0, None, ALU.mult, None, [Pc], [ngt])
                kb.ACT(ekr[:], Pc[:, 4:8], AF.Exp, [Pc], [ekr])
                kb.ACT(cdec[:], Pc[:, 8:12], AF.Exp, [Pc], [cdec])
                kb.ACT(eg[:], gtm[:], AF.Exp, [gtm], [eg])
                kb.TT(beg[:], beta[:], eg[:], ALU.mult, [beta, eg], [beg])
                kb.TT(gpl[:], gtm[:], lnb[:], ALU.add, [gtm, lnb], [gpl])
                kb.MM(Pd[0:4, 0:128], loga[:], tri, True, True, [loga], [Pd])
                kb.MM(Pd[0:4, 128:256], loga[:], tri, True, False, [loga], [Pd])
                kb.MM(Pd[0:4, 128:256], lnb[:], ident, False, True, [lnb], [Pd])
                kb.CP(ROWS[:, 0:256], Pd[0:4, 0:256], [Pd], [ROWS])
                kb.TS(ROWS[:, 256:384], Pd[0:4, 0:128], -1.0, None, ALU.mult, None, [Pd], [ROWS])
                yield
                for (dst, rsl, negm, bias_t, bank) in ((LI, slice(0, 128), cx.n_incl, ngt, Pa),
                                                       (LBT, slice(128, 256), cx.n_strT, ngt, Pb),
                                                       (LBm, slice(256, 384), cx.n_str, gpl, Pa)):
                    for h in range(4):
                        kb.MM(bank[:, h * 128:(h + 1) * 128], kb.C("SELH%d" % h)[0:4, :], ROWS[:, rsl], True, False, [ROWS], [bank])
                        kb.MM(bank[:, h * 128:(h + 1) * 128], ident, negm, False, True, [], [bank])
                    yield
                    for h in range(4):
                        kb.ACT(dst[:, h, :], bank[:, h * 128:(h + 1) * 128], AF.Exp, [bank, bias_t], [dst], bias=bias_t[:, h:h + 1])
                    yield
                for h in range(4):
                    hp, h2 = divmod(h, 2)
                    ksl = kn[64 * h2:64 * h2 + 64, hp, :]
                    kb.MM(Pa[:, h * 128:(h + 1) * 128], ksl, ksl, True, True, [kn], [Pa])
                    kb.MM(Pb[:, h * 128:(h + 1) * 128], ksl, qn[64 * h2:64 * h2 + 64, hp, :], True, True, [kn, qn], [Pb])
                pav = Pa[:].rearrange("p (h t) -> p h t", h=4)
                pbv = Pb[:].rearrange("p (h t) -> p h t", h=4)
                kb.STT(NAT[:], pav, -1.0, LBT[:], ALU.mult, ALU.mult, [Pa, LBT], [NAT])
                kb.STT(NA[:], pav, -1.0, LBm[:], ALU.mult, ALU.mult, [Pa, LBm], [NA])
                kb.TT(QKm[:], pbv, LI[:], ALU.mult, [Pb, LI], [QKm])
                yield
                mT = kb.C("MOFF1" if d == 0 else "MOFFT1").unsqueeze(1).to_broadcast([128, 4, 128])
                mW = kb.C("MOFFT1" if d == 0 else "MOFF1").unsqueeze(1).to_broadcast([128, 4, 128])
                kb.TT(tmx[:], NA[:], mT, ALU.mult, [NA], [tmx], eng="pool")
                kb.TT(tmy[:], NAT[:], mW, ALU.mult, [NAT], [tmy], eng="pool")
                kb.TT(Tm[:], tmx[:], idb, ALU.add, [tmx], [Tm], eng="pool")
                kb.TT(Wm[:], tmy[:], idb, ALU.add, [tmy], [Wm], eng="pool")
                yield
                for s_ in (2, 4, 8, 16, 32, 64):
                    for h in range(4):
                        kb.MM(Pa[:, h * 128:(h + 1) * 128], NAT[:, h, :], Tm[:, h, :], True, True, [NAT, Tm], [Pa])
                    for h in range(4):
                        kb.MM(Pb[:, h * 128:(h + 1) * 128], NA[:, h, :], Wm[:, h, :], True, True, [NA, Wm], [Pb])
                    yield
                    kb.CP(x1[:], pav, [Pa], [x1], eng="act")
                    kb.CP(y1[:], pbv, [Pb], [y1], eng="act")
                    yield
                    for h in range(4):
                        kb.MM(Pa[:, h * 128:(h + 1) * 128], Wm[:, h, :], x1[:, h, :], True, True, [Wm, x1], [Pa])
                    for h in range(4):
                        kb.MM(Pb[:, h * 128:(h + 1) * 128], Tm[:, h, :], y1[:, h, :], True, True, [Tm, y1], [Pb])
                    yield
                    kb.CPRED(Tm[:], cx.mT[s_][:], pav, [Pa, cx.mT[s_]], [Tm])
                    kb.CPRED(Wm[:], cx.mW[s_][:], pbv, [Pb, cx.mW[s_]], [Wm])
                    yield
                for hp in range(2):
                    kb.TR(Pc[:, 128:256], kn[:, hp, :], ident, [kn], [Pc])
                    kb.TR(Pc[:, 256:384], vv[:, hp, :], ident, [vv], [Pc])
                    for h2 in range(2):
                        h = 2 * hp + h2
                        kc = slice(64 * h2, 64 * h2 + 64)
                        vc = slice(64 * (1 - h2), 64 * (1 - h2) + 64)
                        kb.TS(Rm[h][:, kc], Pc[:, 128 + 64 * h2:128 + 64 * h2 + 64], beg[:, h:h + 1], None, ALU.mult, None,
                              [Pc, beg], [Rm[h]])
                        kb.ACT(Rm[h][:, vc], Pc[:, 256 + 64 * h2:256 + 64 * h2 + 64], AF.Copy, [Pc, beta], [Rm[h]],
                               scale=beta[:, h:h + 1])
                        kb.ACT(khp[h][:, kc], Pc[:, 128 + 64 * h2:128 + 64 * h2 + 64], AF.Copy, [Pc, ekr], [khp[h]],
                               scale=ekr[:, h:h + 1])
                    yield
                for h in range(4):
                    kb.MM(Pa[:, h * 128:(h + 1) * 128], Wm[:, h, :], Rm[h][:], True, True, [Wm, Rm[h]], [Pa])
                    kb.MM(Pb[:, h * 128:(h + 1) * 128], Rm[h][:], Wm[:, h, :], True, True, [Wm, Rm[h]], [Pb])
                for h in range(4):
                    hp, h2 = divmod(h, 2)
                    vc0 = 64 * (1 - h2)
                    kb.CP(upair[hp][:, 64 * h2:64 * h2 + 64], Pa[:, h * 128 + vc0:h * 128 + vc0 + 64], [Pa], [upair[hp]], eng="dve")
                    kb.CP(wTp[hp][64 * h2:64 * h2 + 64, :], Pb[64 * h2:64 * h2 + 64, h * 128:(h + 1) * 128], [Pb], [wTp[hp]], eng="act")
                yield
                for hp in range(2):
                    kb.MM(Pc[:, 384:512], kb.C("SELP%d" % hp)[0:4, :], ROWS[:, 0:128], True, True, [ROWS], [Pc])
                    kb.ACT(EG[hp][:], Pc[:, 384:512], AF.Exp, [Pc], [EG[hp]])
                    kb.TT(qd[hp][:], qn[:, hp, :], EG[hp][:], ALU.mult, [qn, EG[hp]], [qd[hp]], eng="pool")
                    pws = Pc[:, hp * 128:(hp + 1) * 128]
                    kb.MM(pws, wTp[hp][:], Sb[hp][:], True, True, [wTp[hp], Sb[hp]], [Pc])
                    for h2 in range(2):
                        h = 2 * hp + h2
                        cs_ = slice(64 * h2, 64 * h2 + 64)
                        kb.TT(vnp[h][:, cs_], upair[hp][:, cs_], Pc[:, hp * 128 + 64 * h2:hp * 128 + 64 * h2 + 64],
                              ALU.subtract, [upair[hp], Pc], [vnp[h]])
                    po = Pd[:, hp * 256:hp * 256 + 128]
                    kb.MM(po, Sb[hp][:], qd[hp][:], True, False, [Sb[hp], qd[hp]], [Pd])
                    kb.MM(po, vnp[2 * hp][:], QKm[:, 2 * hp, :], False, False, [vnp[2 * hp], QKm], [Pd])
                    kb.MM(po, vnp[2 * hp + 1][:], QKm[:, 2 * hp + 1, :], False, True, [vnp[2 * hp + 1], QKm], [Pd])
                    kb.TT(OACC[:, hp, cols], OACC[:, hp, cols], po, ALU.add, [Pd], [OACC.s(n)])
                    pkv = Pd[:, hp * 256 + 128:hp * 256 + 256]
                    kb.MM(pkv, khp[2 * hp][:], vnp[2 * hp][:], True, False, [khp[2 * hp], vnp[2 * hp]], [Pd])
                    kb.MM(pkv, khp[2 * hp + 1][:], vnp[2 * hp + 1][:], False, True, [khp[2 * hp + 1], vnp[2 * hp + 1]], [Pd])
                    kb.CP(cdp[hp][0:64, :], cdec[0:64, 2 * hp:2 * hp + 1], [cdec], [cdp[hp]])
                    kb.CP(cdp[hp][64:128, :], cdec[64:128, 2 * hp + 1:2 * hp + 2], [cdec], [cdp[hp]])
                    kb.STT(Sb[hp][:], Sb[hp][:], cdp[hp][:, 0:1], pkv, ALU.mult, ALU.add, [Sb[hp], cdp[hp], Pd], [Sb[hp]])
                    yield

            def stream(cx):
                for n in ORDER[cx.d][:kb.cfg.get("ntiles", NT)]:
                    yield from step(cx, n)
            active = [stream(cxs[0]), stream(cxs[1])]
            while active:
                for g_ in list(active):
                    try:
                        next(g_)
                    except StopIteration:
                        active.remove(g_)
        with P.scope():
            G = P.sbuf("g_G2", [128, 1])
            for hh in range(2):
                kb.LD(G[64 * hh:64 * hh + 64, :], kb.prm["gdn_norm_g"][l].rearrange("(p o) -> p o", o=1), [G])
            finalize_gated(kb, OACC, Z_GG, G, 512, "g_")


def run_interleaved(gens):
    active = list(gens)
    while active:
        for g_ in list(active):
            try:
                next(g_)
            except StopIteration:
                active.remove(g_)


def oacc_add(kb, OACC, hp, n, ps):
    cols = slice(n * 128, (n + 1) * 128)
    kb.TT(OACC[:, hp, cols], OACC[:, hp, cols], ps[:], ALU.add, [ps], [OACC.s(n)])


def oacc_zero(kb, OACC):
    for hp in range(2):
        kb.MS(OACC[:, hp, :], 0.0, [OACC.s(n) for n in range(NT)], eng="pool")


def mixer_hgrn2(kb, l):
    P = kb.P
    with P.scope():
        OACC = P.sbuf("h_oacc", [128, 2, S])
        oacc_zero(kb, OACC)
        with P.scope():
            LB = P.sbuf("h_LB", [128, 4]); OML = P.sbuf("h_OML", [128, 4])
            if l == 0:
                kb.MS(LB[:], 0.0, [LB]); kb.MS(OML[:], 1.0, [OML])
            else:
                lgt = P.sbuf("h_lgt", [128, 8])
                kb.LD(lgt[:], kb.prm["hgrn_lb_logits"][:].rearrange("l d (hp p) -> p (l d hp)", p=128), [lgt],
                      allow_slow_non_contiguous=True)
                kb.TT(LB[:], lgt[:, 4:8], lgt[:, 0:4], ALU.subtract, [lgt], [LB])
                kb.ACT(LB[:], LB[:], AF.Sigmoid, [LB], [LB])
                kb.TS(OML[:], LB[:], -1.0, 1.0, ALU.mult, ALU.add, [LB], [OML])

            def make(d):
                pf = "h%d_" % d
                hqb = [P.sbuf(pf + "q%d" % i, [128, 2, 128]) for i in range(2)]
                hfb = [P.sbuf(pf + "f%d" % i, [128, 2, 128]) for i in range(2)]
                Vp = [[P.sbuf(pf + "vp%d%d" % (i, h), [128, 128]) for h in range(4)] for i in range(2)]
                khp = [[P.sbuf(pf + "kh%d%d" % (i, h), [128, 128]) for h in range(4)] for i in range(2)]
                for i in range(2):
                    for h in range(4):
                        kb.MS(Vp[i][h][:], 0.0, [Vp[i][h]], eng="pool")
                        kb.MS(khp[i][h][:], 0.0, [khp[i][h]], eng="pool")
                MREF = [P.sbuf(pf + "mr%d" % i, [128, 4]) for i in range(2)]
                for i in range(2):
                    kb.MS(MREF[i][:], 0.0, [MREF[i]])

                def two(name, shape=(128, 128)):
                    return [P.sbuf(pf + "%s%d" % (name, i), list(shape)) for i in range(2)]
                qs, sgm, ff, logf, kk, bb, pre = two("qs"), two("sg"), two("ff"), two("lf"), two("kk"), two("bb"), two("pre")
                e1, Ql, e2, Qd = two("e1"), two("Ql"), two("e2"), two("Qd")
                Kt = [two("Kt%d" % r) for r in range(4)]
                ex = two("ex")
                AT = two("AT", (128, 2, 128))
                KhT = two("KhT")
                bend = two("bend", (128, 2))
                Sb = [P.sbuf(pf + "S%d" % hp, [128, 128]) for hp in range(2)]
                for hp in range(2):
                    kb.MS(Sb[hp][:], 0.0, [Sb[hp]])
                pss = P.psum(pf + "pss", [128, 2, 128])
                po = P.psum(pf + "pso", [128, 128])
                pk = P.psum(pf + "psk", [128, 128])
                pkv = P.psum(pf + "pskv", [128, 128])
                zf = Z_HFF if d == 0 else Z_HFB
                tri = kb.C("TRIF" if d == 0 else "TRIB").unsqueeze(1).to_broadcast([128, 2, 128])

                def gen():
                    it = 0
                    jj = 0
                    for n in ORDER[d]:
                        cols = slice(n * 128, (n + 1) * 128)
                        b = it % 2; it += 1
                        hq, hf = hqb[b], hfb[b]
                        kb.LD(hq[:], kb.ZF[Z_HQ:Z_HQ + 256, cols].rearrange("(hp p) t -> p hp t", p=128), [hq])
                        kb.LD(hf[:], kb.ZF[zf:zf + 256, cols].rearrange("(hp p) t -> p hp t", p=128), [hf])
                        for h in range(4):
                            kb.LD(Vp[b][h][:, 64 * (h % 2):64 * (h % 2) + 64], kb.ZT[cols, 64 * h:64 * h + 64], [Vp[b][h]])
                        yield
                        for hp in range(2):
                            j = jj % 2; jj += 1
                            c = 2 * d + hp
                            mref = MREF[j]
                            kb.ACT(qs[j][:], hq[:, hp, :], AF.Silu, [hq], [qs[j]])
                            kb.ACT(sgm[j][:], hf[:, hp, :], AF.Sigmoid, [hf], [sgm[j]])
                            kb.TS(ff[j][:], sgm[j][:], OML[:, c:c + 1], LB[:, c:c + 1], ALU.mult, ALU.add, [sgm[j], OML, LB], [ff[j]])
                            kb.ACT(logf[j][:], ff[j][:], AF.Ln, [ff[j]], [logf[j]])
                            kb.TS(kk[j][:], ff[j][:], -1.0, 1.0, ALU.mult, ALU.add, [ff[j]], [kk[j]], eng="pool")
                            yield
                            B = bb[j]
                            if d == 0:
                                kb.SCAN(B[:], kb.C("ONES"), logf[j][:], [logf[j]], [B])
                                kb.CP(mref[:, 1:4], B[:].rearrange("p (r c) -> p r c", c=32)[:, 0:3, 31], [B], [mref])
                                be = B[:, 127:128]
                            else:
                                kb.SCAN(pre[j][:], kb.C("ONES"), logf[j][:], [logf[j]], [pre[j]])
                                kb.STT(B[:], pre[j][:], -1.0, logf[j][:], ALU.mult, ALU.add, [pre[j], logf[j]], [B])
                                kb.TS(B[:], B[:], pre[j][:, 127:128], None, ALU.add, None, [B, pre[j]], [B])
                                kb.CP(mref[:, 0:3], B[:].rearrange("p (r c) -> p r c", c=32)[:, 1:4, 0], [B], [mref])
                                be = B[:, 0:1]
                            yield
                            kb.TT(e1[j][:].rearrange("p (r c) -> p r c", c=32), B[:].rearrange("p (r c) -> p r c", c=32),
                                  mref[:].unsqueeze(2).to_broadcast([128, 4, 32]), ALU.subtract, [B, mref], [e1[j]])
                            kb.ACT(e1[j][:], e1[j][:], AF.Exp, [e1[j]], [e1[j]])
                            kb.STT(Ql[j][:], qs[j][:], 0.125, e1[j][:], ALU.mult, ALU.mult, [qs[j], e1[j]], [Ql[j]])
                            kb.ACT(e2[j][:], B[:], AF.Exp, [B], [e2[j]])
                            kb.STT(Qd[j][:], qs[j][:], 0.125, e2[j][:], ALU.mult, ALU.mult, [qs[j], e2[j]], [Qd[j]])
                            yield
                            for r in range(4):
                                kb.ACT(ex[j][:], B[:], AF.Exp, [B, mref], [ex[j]], scale=-1.0, bias=mref[:, r:r + 1])
                                kb.STT(Kt[r][j][:], ex[j][:], 1e26, kk[j][:], ALU.min, ALU.mult, [ex[j], kk[j]], [Kt[r][j]])
                                for h2 in range(2):
                                    kb.MM(pss[:, h2, 32 * r:32 * r + 32], Kt[r][j][64 * h2:64 * h2 + 64, :],
                                          Ql[j][64 * h2:64 * h2 + 64, 32 * r:32 * r + 32], True, True,
                                          [Kt[r][j], Ql[j]], [pss])
                                yield
                            kb.TT(AT[j][:], pss[:], tri, ALU.mult, [pss], [AT[j]])
                            yield
                            kb.MM(po[:], Vp[b][2 * hp][:], AT[j][:, 0, :], True, False, [Vp[b][2 * hp], AT[j]], [po])
                            kb.MM(po[:], Vp[b][2 * hp + 1][:], AT[j][:, 1, :], False, False, [Vp[b][2 * hp + 1], AT[j]], [po])
                            kb.MM(po[:], Sb[hp][:], Qd[j][:], False, True, [Sb[hp], Qd[j]], [po])
                            oacc_add(kb, OACC, hp, n, po)
                            kb.CP(bend[j][:, 0:1], be, [B], [bend[j]])
                            kb.ACT(KhT[j][:], B[:], AF.Exp, [B, bend[j]], [KhT[j]], scale=-1.0, bias=bend[j][:, 0:1])
                            kb.TT(KhT[j][:], KhT[j][:], kk[j][:], ALU.mult, [KhT[j], kk[j]], [KhT[j]], eng="pool")
                            kb.ACT(bend[j][:, 1:2], bend[j][:, 0:1], AF.Exp, [bend[j]], [bend[j]])
                            yield
                            kb.TR(pk[:], KhT[j][:], kb.C("IDENT"), [KhT[j]], [pk])
                            for h2 in range(2):
                                h = 2 * hp + h2
                                kb.CP(khp[b][h][:, 64 * h2:64 * h2 + 64], pk[:, 64 * h2:64 * h2 + 64], [pk], [khp[b][h]],
                                      eng=("act" if h2 else "dve"))
                            yield
                            kb.MM(pkv[:], khp[b][2 * hp][:], Vp[b][2 * hp][:], True, False, [khp[b][2 * hp], Vp[b][2 * hp]], [pkv])
                            kb.MM(pkv[:], khp[b][2 * hp + 1][:], Vp[b][2 * hp + 1][:], False, True,
                                  [khp[b][2 * hp + 1], Vp[b][2 * hp + 1]], [pkv])
                            kb.STT(Sb[hp][:], Sb[hp][:], bend[j][:, 1:2], pkv[:], ALU.mult, ALU.add,
                                   [Sb[hp], bend[j], pkv], [Sb[hp]])
                            yield
                return gen()
            run_interleaved([make(0), make(1)])
        with P.scope():
            G = P.sbuf("h_G2", [128, 1])
            for hh in range(2):
                kb.LD(G[64 * hh:64 * hh + 64, :], kb.prm["hgrn_norm_g"][l].rearrange("(p o) -> p o", o=1), [G])
            finalize_gated(kb, OACC, Z_HG, G, 0, "h_")


def mixer_ret2(kb, l):
    P = kb.P
    with P.scope():
        OACC = P.sbuf("r_oacc", [128, 2, S])
        oacc_zero(kb, OACC)
        with P.scope():
            lgt = P.sbuf("r_lgt", [128, 8])
            kb.LD(lgt[:], kb.prm["ret_decay_logit"][l].rearrange("d h -> (d h)").partition_broadcast(128), [lgt])
            LG = P.sbuf("r_LG", [128, 8])
            kb.ACT(LG[:], lgt[:], AF.Sigmoid, [lgt], [LG])
            kb.ACT(LG[:], LG[:], AF.Ln, [LG], [LG])
            LGP = P.sbuf("r_LGP", [128, 4])
            for d in range(2):
                for hp in range(2):
                    c = 2 * d + hp
                    kb.CP(LGP[0:64, c:c + 1], LG[0:64, 4 * d + 2 * hp:4 * d + 2 * hp + 1], [LG], [LGP])
                    kb.CP(LGP[64:128, c:c + 1], LG[64:128, 4 * d + 2 * hp + 1:4 * d + 2 * hp + 2], [LG], [LGP])
            MK = [P.sbuf("r_MK%d" % d, [128, 4, 128]) for d in range(2)]
            QDEC = [[P.sbuf("r_QD%d%d" % (d, hp), [128, 128]) for hp in range(2)] for d in range(2)]
            etmp = P.sbuf("r_etmp", [128, 128])
            for d in range(2):
                for h in range(4):
                    kb.ACT(etmp[:], kb.C("DIFF" if d == 0 else "NDIFF"), AF.Exp, [LG], [etmp],
                           scale=LG[:, 4 * d + h:4 * d + h + 1])
                    kb.STT(MK[d][:, h, :], etmp[:], 0.125, kb.C("TRIF" if d == 0 else "TRIB"), ALU.mult, ALU.mult,
                           [etmp], [MK[d]])
                for hp in range(2):
                    kb.ACT(QDEC[d][hp][:], kb.C("IOTAF1" if d == 0 else "RIOTAF"), AF.Exp, [LGP], [QDEC[d][hp]],
                           scale=LGP[:, 2 * d + hp:2 * d + hp + 1])
            KD = P.sbuf("r_KD", [128, 8])
            kb.ACT(KD[:, 0:4], LG[:, 0:4], AF.Exp, [LG], [KD], scale=kb.C("CCOL")[:, 3:4])
            kb.ACT(KD[:, 4:8], LG[:, 4:8], AF.Exp, [LG], [KD], scale=kb.C("CCOL")[:, 2:3])
            kb.TS(KD[:], KD[:], 0.125, None, ALU.mult, None, [KD], [KD])
            CV = P.sbuf("r_CV", [128, 4])
            kb.ACT(CV[:], LGP[:], AF.Exp, [LGP], [CV], scale=128.0)

            def make(d):
                pf = "r%d_" % d
                qTb = [P.sbuf(pf + "q%d" % i, [128, 2, 128]) for i in range(2)]
                kTb = [P.sbuf(pf + "k%d" % i, [128, 2, 128]) for i in range(2)]
                csb = [P.sbuf(pf + "cs%d" % i, [128, 2, 128]) for i in range(2)]
                Vp = [[P.sbuf(pf + "vp%d%d" % (i, h), [128, 128]) for h in range(4)] for i in range(2)]
                khp = [[P.sbuf(pf + "kh%d%d" % (i, h), [128, 128]) for h in range(4)] for i in range(2)]
                for i in range(2):
                    for h in range(4):
                        kb.MS(Vp[i][h][:], 0.0, [Vp[i][h]], eng="pool")
                        kb.MS(khp[i][h][:], 0.0, [khp[i][h]], eng="pool")
                t1 = [P.sbuf(pf + "t1%d" % i, [128, 128]) for i in range(2)]
                t2 = [P.sbuf(pf + "t2%d" % i, [128, 128]) for i in range(2)]
                qr = [P.sbuf(pf + "qr%d" % i, [128, 2, 128]) for i in range(2)]
                kr = [P.sbuf(pf + "kr%d" % i, [128, 2, 128]) for i in range(2)]
                AT = [P.sbuf(pf + "AT%d" % i, [128, 2, 128]) for i in range(2)]
                qd = [P.sbuf(pf + "qd%d" % i, [128, 128]) for i in range(2)]
                Sb = [P.sbuf(pf + "S%d" % hp, [128, 128]) for hp in range(2)]
                for hp in range(2):
                    kb.MS(Sb[hp][:], 0.0, [Sb[hp]])
                pr = P.psum(pf + "psr", [128, 256])
                pss = P.psum(pf + "pss", [128, 2, 128])
                po = P.psum(pf + "pso", [128, 128])
                pkk = P.psum(pf + "pskk", [128, 256])

                def gen():
                    it = 0
                    jj = 0
                    for n in ORDER[d]:
                        cols = slice(n * 128, (n + 1) * 128)
                        b = it % 2; it += 1
                        qT, kT, cs = qTb[b], kTb[b], csb[b]
                        kb.LD(qT[:], kb.ZF[Z_RQ:Z_RQ + 256, cols].rearrange("(hp p) t -> p hp t", p=128), [qT])
                        kb.LD(kT[:], kb.ZF[Z_RK:Z_RK + 256, cols].rearrange("(hp p) t -> p hp t", p=128), [kT])
                        kb.LD(cs[:, 0, :], kb.ropec[:, cols], [cs])
                        kb.LD(cs[:, 1, :], kb.ropes[:, cols], [cs])
                        for h in range(4):
                            kb.LD(Vp[b][h][:, 64 * (h % 2):64 * (h % 2) + 64], kb.ZT[cols, 256 + 64 * h:256 + 64 * h + 64],
                                  [Vp[b][h]])
                        yield
                        for hp in range(2):
                            j = jj % 2; jj += 1
                            kb.MM(pr[:, 0:128], kb.C("ROT"), qT[:, hp, :], True, True, [qT], [pr])
                            kb.MM(pr[:, 128:256], kb.C("ROT"), kT[:, hp, :], True, True, [kT], [pr])
                            yield
                            for (src_, dst, off) in ((qT, qr[b], 0), (kT, kr[b], 128)):
                                kb.TT(t1[j][:], src_[:, hp, :], cs[:, 0, :], ALU.mult, [src_, cs], [t1[j]], eng="pool")
                                kb.TT(t2[j][:], pr[:, off:off + 128], cs[:, 1, :], ALU.mult, [pr, cs], [t2[j]])
                                kb.TT(dst[:, hp, :], t1[j][:], t2[j][:], ALU.add, [t1[j], t2[j]], [dst.s(hp)], eng="pool")
                                yield
                            for h2 in range(2):
                                kb.MM(pss[:, h2, :], kr[b][64 * h2:64 * h2 + 64, hp, :], qr[b][64 * h2:64 * h2 + 64, hp, :],
                                      True, True, [kr[b].s(hp), qr[b].s(hp)], [pss])
                            yield
                            kb.TT(AT[j][:], pss[:], MK[d][:, 2 * hp:2 * hp + 2, :], ALU.mult, [pss, MK[d]], [AT[j]])
                            kb.TT(qd[j][:], qr[b][:, hp, :], QDEC[d][hp][:], ALU.mult, [qr[b].s(hp), QDEC[d][hp]], [qd[j]],
                                  eng="pool")
                            yield
                            kb.MM(po[:], Vp[b][2 * hp][:], AT[j][:, 0, :], True, False, [Vp[b][2 * hp], AT[j]], [po])
                            kb.MM(po[:], Vp[b][2 * hp + 1][:], AT[j][:, 1, :], False, False, [Vp[b][2 * hp + 1], AT[j]], [po])
                            kb.MM(po[:], Sb[hp][:], qd[j][:], False, True, [Sb[hp], qd[j]], [po])
                            kb.TR(pkk[:, 0:128], kr[b][:, hp, :], kb.C("IDENT"), [kr[b].s(hp)], [pkk])
                            yield
                            oacc_add(kb, OACC, hp, n, po)
                            for h2 in range(2):
                                h = 2 * hp + h2
                                kb.ACT(khp[b][h][:, 64 * h2:64 * h2 + 64], pkk[:, 64 * h2:64 * h2 + 64], AF.Copy,
                                       [pkk, KD], [khp[b][h]], scale=KD[:, 4 * d + h:4 * d + h + 1])
                            yield
                            kb.MM(pkk[:, 128:256], khp[b][2 * hp][:], Vp[b][2 * hp][:], True, False,
                                  [khp[b][2 * hp], Vp[b][2 * hp]], [pkk])
                            kb.MM(pkk[:, 128:256], khp[b][2 * hp + 1][:], Vp[b][2 * hp + 1][:], False, True,
                                  [khp[b][2 * hp + 1], Vp[b][2 * hp + 1]], [pkk])
                            kb.STT(Sb[hp][:], Sb[hp][:], CV[:, 2 * d + hp:2 * d + hp + 1], pkk[:, 128:256], ALU.mult, ALU.add,
                                   [Sb[hp], CV, pkk], [Sb[hp]])
                            yield
                return gen()
            run_interleaved([make(0), make(1)])
        with P.scope():
            finalize_gated(kb, OACC, Z_RG, None, 256, "r_")


def s5_tables(kb, l, d, VFr, VFi, T1, T2, AR, NAI):
    P = kb.P
    prm = kb.prm
    with P.scope():
        lr = P.sbuf("s_lr", [128, 16, 64]); li = P.sbuf("s_li", [128, 16, 64]); dtb = P.sbuf("s_dt", [128, 16])
        kb.LD(lr[:], prm["s5_lam_re"][l][d].rearrange("g p -> (g p)").partition_broadcast(128), [lr])
        kb.LD(li[:], prm["s5_lam_im"][l][d].rearrange("g p -> (g p)").partition_broadcast(128), [li])
        kb.LD(dtb[:], prm["s5_log_dt"][l][d].partition_broadcast(128), [dtb])
        kb.ACT(dtb[:], dtb[:], AF.Exp, [dtb], [dtb])
        dt_bc = dtb[:].unsqueeze(2).to_broadcast([128, 16, 64])
        lrdt = P.sbuf("s_lrdt", [128, 16, 64]); lidt = P.sbuf("s_lidt", [128, 16, 64])
        kb.TT(lrdt[:], lr[:], dt_bc, ALU.mult, [lr, dtb], [lrdt])
        kb.TT(lidt[:], li[:], dt_bc, ALU.mult, [li, dtb], [lidt])
        a = [P.sbuf("s_a%d" % i, [128, 16, 64]) for i in range(8)]
        mag, ang, sn, cs, tmp, ar, ai, t2 = a
        kb.ACT(mag[:], lrdt[:], AF.Exp, [lrdt], [mag])
        _sincos(kb, lidt[:], sn[:], cs[:], [lidt, sn, cs, tmp], tmp[:])
        kb.TT(ar[:], mag[:], cs[:], ALU.mult, [mag, cs], [ar])
        kb.TT(ai[:], mag[:], sn[:], ALU.mult, [mag, sn], [ai])
        den = P.sbuf("s_den", [128, 16, 64]); fr = P.sbuf("s_fr", [128, 16, 64]); fi = P.sbuf("s_fi", [128, 16, 64])
        kb.TT(den[:], lr[:], lr[:], ALU.mult, [lr], [den])
        kb.TT(t2[:], li[:], li[:], ALU.mult, [li], [t2])
        kb.TT(den[:], den[:], t2[:], ALU.add, [den, t2], [den])
        kb.RECIP(den[:], den[:], [den], [den])
        kb.TS(ar[:], ar[:], -1.0, None, ALU.add, None, [ar], [ar])
        kb.TT(fr[:], ar[:], lr[:], ALU.mult, [ar, lr], [fr])
        kb.TT(t2[:], ai[:], li[:], ALU.mult, [ai, li], [t2])
        kb.TT(fr[:], fr[:], t2[:], ALU.add, [fr, t2], [fr])
        kb.TT(fr[:], fr[:], den[:], ALU.mult, [fr, den], [fr])
        kb.TT(fi[:], ai[:], lr[:], ALU.mult, [ai, lr], [fi])
        kb.TT(t2[:], ar[:], li[:], ALU.mult, [ar, li], [t2])
        kb.TT(fi[:], fi[:], t2[:], ALU.subtract, [fi, t2], [fi])
        kb.TT(fi[:], fi[:], den[:], ALU.mult, [fi, den], [fi])
        jcol = kb.C("CCOL")[:, 2:3] if d == 0 else kb.C("CCOL")[:, 3:4]
        njcol = kb.C("CCOL")[:, 6:7] if d == 0 else kb.C("CCOL")[:, 7:8]
        kb.ACT(mag[:], lrdt[:], AF.Exp, [lrdt], [mag], scale=njcol)
        kb.TS(ang[:], lidt[:], jcol, None, ALU.mult, None, [lidt], [ang])
        _sincos(kb, ang[:], sn[:], cs[:], [ang, sn, cs, tmp], tmp[:])
        vr, vi = ar, ai
        kb.TT(vr[:], mag[:], cs[:], ALU.mult, [mag, cs], [vr])
        kb.TT(vi[:], mag[:], sn[:], ALU.mult, [mag, sn], [vi])
        kb.TS(vi[:], vi[:], -1.0, None, ALU.mult, None, [vi], [vi])
        kb.TT(VFr[:], vr[:], fr[:], ALU.mult, [vr, fr], [VFr])
        kb.TT(t2[:], vi[:], fi[:], ALU.mult, [vi, fi], [t2])
        kb.TT(VFr[:], VFr[:], t2[:], ALU.subtract, [VFr, t2], [VFr])
        kb.TT(VFi[:], vr[:], fi[:], ALU.mult, [vr, fi], [VFi])
        kb.TT(t2[:], vi[:], fr[:], ALU.mult, [vi, fr], [t2])
        kb.TT(VFi[:], VFi[:], t2[:], ALU.add, [VFi, t2], [VFi])
    with P.scope():
        dtb = P.sbuf("s_dt2", [128, 16])
        kb.LD(dtb[:], prm["s5_log_dt"][l][d].partition_broadcast(128), [dtb])
        kb.ACT(dtb[:], dtb[:], AF.Exp, [dtb], [dtb])
        lrp = P.sbuf("s_lrp", [128, 16]); lip = P.sbuf("s_lip", [128, 16])
        for hh in range(2):
            kb.LD(lrp[64 * hh:64 * hh + 64, :], prm["s5_lam_re"][l][d].rearrange("g p -> p g"), [lrp],
                  allow_slow_non_contiguous=True)
            kb.LD(lip[64 * hh:64 * hh + 64, :], prm["s5_lam_im"][l][d].rearrange("g p -> p g"), [lip],
                  allow_slow_non_contiguous=True)
        kb.TT(lrp[:], lrp[:], dtb[:], ALU.mult, [lrp, dtb], [lrp])
        kb.TT(lip[:], lip[:], dtb[:], ALU.mult, [lip, dtb], [lip])
        b4 = [P.sbuf("s_b%d" % i, [128, 16, 128]) for i in range(4)]
        arg, sn2, cs2, tmp2 = b4
        mt = kb.C("IOTAF" if d == 0 else "R127F")
        mt_bc = mt.unsqueeze(1).to_broadcast([128, 16, 128])
        kb.TT(arg[:], lrp[:].unsqueeze(2).to_broadcast([128, 16, 128]), mt_bc, ALU.mult, [lrp], [arg])
        kb.ACT(T1[:], arg[:], AF.Exp, [arg], [T1])
        kb.TT(arg[:], lip[:].unsqueeze(2).to_broadcast([128, 16, 128]), mt_bc, ALU.mult, [lip, T1], [arg])
        _sincos(kb, arg[:], sn2[:], cs2[:], [arg, sn2, cs2, tmp2], tmp2[:])
        kb.TT(T2[:], T1[:], sn2[:], ALU.mult, [T1, sn2], [T2])
        kb.TS(T2[:], T2[:], -1.0, None, ALU.mult, None, [T2], [T2])
        kb.TT(T1[:], T1[:], cs2[:], ALU.mult, [T1, cs2], [T1])
        c4 = [P.sbuf("s_c%d" % i, [128, 16]) for i in range(4)]
        kb.ACT(c4[0][:], lrp[:], AF.Exp, [lrp], [c4[0]])
        _sincos(kb, lip[:], c4[1][:], c4[2][:], [lip, c4[1], c4[2], c4[3]], c4[3][:])
        kb.TT(AR[:], c4[0][:], c4[2][:], ALU.mult, [c4[0], c4[2]], [AR])
        kb.TT(NAI[:], c4[0][:], c4[1][:], ALU.mult, [c4[0], c4[1]], [NAI])
        kb.TS(NAI[:], NAI[:], -1.0, None, ALU.mult, None, [NAI], [NAI])


def mixer_s5_2(kb, l):
    P = kb.P
    prm = kb.prm
    with P.scope():
        WX = P.sbuf("s_WX", [128, 2, 8, 2, 64])
        Cblk = P.sbuf("s_Cblk", [128, 16, 128])
        kb.MS(WX[:], 0.0, [WX], eng="pool")
        kb.MS(Cblk[:], 0.0, [Cblk], eng="pool")
        for g8 in range(8):
            for ri, nm in enumerate(("s5_b_re", "s5_b_im")):
                for gg in range(2):
                    src = prm[nm][l][8 * gg + g8].rearrange("p c -> c p")
                    kb.LD(WX[16 * g8:16 * g8 + 16, gg, g8, ri, :], src, [WX], allow_slow_non_contiguous=True)
        for g in range(16):
            g8 = g % 8
            kb.LD(Cblk[0:64, g, 16 * g8:16 * g8 + 16], prm["s5_c_re"][l][g].rearrange("c p -> p c"), [Cblk],
                  allow_slow_non_contiguous=True)
            kb.LD(Cblk[64:128, g, 16 * g8:16 * g8 + 16], prm["s5_c_im"][l][g].rearrange("c p -> p c"), [Cblk],
                  allow_slow_non_contiguous=True)
        kb.TS(Cblk[64:128, :, :], Cblk[64:128, :, :], -1.0, None, ALU.mult, None, [Cblk], [Cblk])
        Cb16 = P.sbuf("s_Cb16", [128, 16, 128], BF16)
        kb.CP(Cb16[:], Cblk[:], [Cblk], [Cb16])

        tabs = []
        for d in range(2):
            VFr = P.sbuf("s_VFr%d" % d, [128, 16, 64]); VFi = P.sbuf("s_VFi%d" % d, [128, 16, 64])
            T1 = P.sbuf("s_T1%d" % d, [128, 16, 128]); T2 = P.sbuf("s_T2%d" % d, [128, 16, 128])
            AR = P.sbuf("s_AR%d" % d, [128, 16]); NAI = P.sbuf("s_NAI%d" % d, [128, 16])
            s5_tables(kb, l, d, VFr, VFi, T1, T2, AR, NAI)
            tabs.append((VFr, VFi, T1, T2, AR, NAI))
        OACC = P.sbuf("s_oacc", [128, 2, S])
        oacc_zero(kb, OACC)
        with P.scope():
            def make(d):
                pf = "s%d_" % d
                VFr, VFi, T1, T2, AR, NAI = tabs[d]
                uTb = [P.sbuf(pf + "u%d" % i, [128, 2, 128]) for i in range(2)]
                mm_ = [P.sbuf(pf + "m%d" % i, [128, 4, 64]) for i in range(4)]
                W3 = [P.sbuf(pf + "W3%d" % i, [128, 4, 3, 64], BF16) for i in range(2)]
                tP = P.sbuf(pf + "tP", [128, 4, 128]); tPs = P.sbuf(pf + "tPs", [128, 4, 128])
                H1 = P.sbuf(pf + "H1", [128, 4, 128]); H2 = P.sbuf(pf + "H2", [128, 4, 128])
                Hb = [P.sbuf(pf + "Hb%d" % i, [128, 4, 128], BF16) for i in range(2)]
                tri16 = P.sbuf(pf + "tri16", [128, 128], BF16)
                kb.CP(tri16[:], kb.C("TRIF" if d == 0 else "TRIB"), [], [tri16])
                hend = P.sbuf(pf + "hend", [128, 16]); hsend = P.sbuf(pf + "hsend", [128, 16])
                hp_ = P.sbuf(pf + "hp", [128, 16]); hps_ = P.sbuf(pf + "hps", [128, 16])
                sm = [P.sbuf(pf + "sm%d" % i, [128, 16]) for i in range(4)]
                kb.MS(hp_[:], 0.0, [hp_]); kb.MS(hps_[:], 0.0, [hps_])
                xps = P.psum(pf + "xps", [128, 512])
                pps = P.psum(pf + "pps", [128, 4, 128])
                ppss = P.psum(pf + "ppss", [128, 4, 128])
                yps = P.psum(pf + "yps", [128, 128])
                te = 127 if d == 0 else 0

                def gen():
                    it = 0
                    kq = 0
                    for n in ORDER[d]:
                        uT = uTb[it % 2]; it += 1
                        cols = slice(n * 128, (n + 1) * 128)
                        kb.LD(uT[:], kb.ZF[Z_SU:Z_SU + 256, cols].rearrange("(gg p) t -> p gg t", p=128), [uT])
                        yield
                        for q in range(4):
                            gg, qq = divmod(q, 2)
                            gs = slice(4 * q, 4 * q + 4)
                            w3 = W3[kq % 2]; hb = Hb[kq % 2]; kq += 1
                            kb.MM(xps[:], uT[:, gg, :], WX[:, gg, 4 * qq:4 * qq + 4, :, :].rearrange("q a r p -> q (a r p)"),
                                  True, True, [uT, WX], [xps])
                            xv = xps[:].rearrange("t (g r p) -> t g r p", r=2, p=64)
                            kb.TT(mm_[0][:], xv[:, :, 0, :], VFr[:, gs, :], ALU.mult, [xps, VFr], [mm_[0]])
                            kb.TT(mm_[1][:], xv[:, :, 1, :], VFi[:, gs, :], ALU.mult, [xps, VFi], [mm_[1]])
                            kb.TT(mm_[2][:], xv[:, :, 0, :], VFi[:, gs, :], ALU.mult, [xps, VFi], [mm_[2]])
                            kb.TT(mm_[3][:], xv[:, :, 1, :], VFr[:, gs, :], ALU.mult, [xps, VFr], [mm_[3]])
                            yield
                            kb.TT(w3[:, :, 0, :], mm_[0][:], mm_[1][:], ALU.subtract, [mm_[0], mm_[1]], [w3], eng="pool")
                            kb.TT(w3[:, :, 1, :], mm_[2][:], mm_[3][:], ALU.add, [mm_[2], mm_[3]], [w3], eng="pool")
                            kb.TT(w3[:, :, 2, :], mm_[1][:], mm_[0][:], ALU.subtract, [mm_[0], mm_[1]], [w3], eng="pool")
                            yield
                            for i in range(4):
                                kb.MM(pps[:, i, :], w3[:, i, 0:2, :].rearrange("q r p -> q (r p)"), tri16[:], True, True, [w3, tri16], [pps])
                            for i in range(4):
                                kb.MM(ppss[:, i, :], w3[:, i, 1:3, :].rearrange("q r p -> q (r p)"), tri16[:], True, True, [w3, tri16], [ppss])
                            yield
                            for i in range(4):
                                g = 4 * q + i
                                kb.ACT(tP[:, i, :], pps[:, i, :], AF.Identity, [pps, hp_], [tP], bias=hp_[:, g:g + 1])
                            for i in range(4):
                                g = 4 * q + i
                                kb.ACT(tPs[:, i, :], ppss[:, i, :], AF.Identity, [ppss, hps_], [tPs], bias=hps_[:, g:g + 1])
                            yield
                            kb.TT(sm[0][:, 0:4], tPs[:, :, te], T1[:, gs, te], ALU.mult, [tPs, T1], [sm[0]])
                            kb.TT(sm[1][:, 0:4], tP[:, :, te], T2[:, gs, te], ALU.mult, [tP, T2], [sm[1]])
                            kb.TT(hsend[:, gs], sm[0][:, 0:4], sm[1][:, 0:4], ALU.subtract, [sm[0], sm[1]], [hsend])
                            kb.TT(sm[2][:, 0:4], tP[:, :, te], T1[:, gs, te], ALU.mult, [tP, T1], [sm[2]])
                            kb.TT(sm[3][:, 0:4], tPs[:, :, te], T2[:, gs, te], ALU.mult, [tPs, T2], [sm[3]])
                            kb.TT(hend[:, gs], sm[2][:, 0:4], sm[3][:, 0:4], ALU.add, [sm[2], sm[3]], [hend])
                            yield
                            kb.TT(H1[:], tP[:], T1[:, gs, :], ALU.mult, [tP, T1], [H1], eng="pool")
                            kb.TT(H2[:], tPs[:], T2[:, gs, :], ALU.mult, [tPs, T2], [H2])
                            yield
                            kb.TT(hb[:], H1[:], H2[:], ALU.add, [H1, H2], [hb], eng="pool")
                            yield
                            for i in range(4):
                                g = 4 * q + i
                                kb.MM(yps[:], Cb16[:, g, :], hb[:, i, :], (g % 8) == 0, (g % 8) == 7, [Cb16, hb], [yps])
                            if qq == 1:
                                oacc_add(kb, OACC, gg, n, yps)
                            yield
                        kb.TT(sm[0][:], hend[:], AR[:], ALU.mult, [hend, AR], [sm[0]])
                        kb.TT(sm[1][:], hsend[:], NAI[:], ALU.mult, [hsend, NAI], [sm[1]])
                        kb.TT(sm[2][:], hsend[:], AR[:], ALU.mult, [hsend, AR], [sm[2]])
                        kb.TT(sm[3][:], hend[:], NAI[:], ALU.mult, [hend, NAI], [sm[3]])
                        kb.TT(hp_[:], sm[0][:], sm[1][:], ALU.add, [sm[0], sm[1]], [hp_])
                        kb.TT(hps_[:], sm[2][:], sm[3][:], ALU.subtract, [sm[2], sm[3]], [hps_])
                        yield
                return gen()
            run_interleaved([make(0), make(1)])
        with P.scope():
            dsk = P.sbuf("s_dsk", [128, 2]); glb = P.sbuf("s_glb", [128, 2])
            kb.LD(dsk[:], prm["s5_d"][l].rearrange("(gg p) -> p gg", p=128), [dsk], allow_slow_non_contiguous=True)
            kb.LD(glb[:], prm["s5_glu_b"][l].rearrange("(gg p) -> p gg", p=128), [glb], allow_slow_non_contiguous=True)
            gw = P.sbuf("s_gw", [128, 2, 256])
            kb.LD(gw[:], prm["s5_glu_w"][l].rearrange("(ct p) o -> p ct o", p=128), [gw])
            uTb = [P.sbuf("s_fu%d" % i, [128, 2, 128]) for i in range(2)]
            yy = [P.sbuf("s_yy%d" % i, [128, 2, 128]) for i in range(2)]
            x2 = [P.sbuf("s_x2%d" % i, [128, 2, 128]) for i in range(2)]
            th = [P.sbuf("s_th%d" % i, [128, 2, 128]) for i in range(2)]
            sgb = [P.sbuf("s_sg%d" % i, [128, 128]) for i in range(2)]
            ob = [P.sbuf("s_ob%d" % i, [128, 128]) for i in range(2)]
            psz = [P.psum("s_psz%d" % i, [128, 128]) for i in range(2)]
            k = 0
            for n in range(NT):
                cols = slice(n * 128, (n + 1) * 128)
                i = n % 2
                kb.LD(uTb[i][:], kb.ZF[Z_SU:Z_SU + 256, cols].rearrange("(gg p) t -> p gg t", p=128), [uTb[i]])
                for gg in range(2):
                    kb.STT(yy[i][:, gg, :], uTb[i][:, gg, :], dsk[:, gg:gg + 1], OACC[:, gg, cols], ALU.mult, ALU.add,
                           [uTb[i], dsk, OACC.s(n)], [yy[i]])
                kb.TT(x2[i][:], yy[i][:], yy[i][:], ALU.mult, [yy[i]], [x2[i]], eng="pool")
                kb.TS(x2[i][:], x2[i][:], 0.044715, 1.0, ALU.mult, ALU.add, [x2[i]], [x2[i]])
                kb.TT(x2[i][:], x2[i][:], yy[i][:], ALU.mult, [x2[i], yy[i]], [x2[i]], eng="pool")
                kb.ACT(th[i][:], x2[i][:], AF.Tanh, [x2[i]], [th[i]], scale=0.7978845608028654)
                kb.TS(th[i][:], th[i][:], 1.0, 0.5, ALU.add, ALU.mult, [th[i]], [th[i]])
                kb.TT(yy[i][:], yy[i][:], th[i][:], ALU.mult, [yy[i], th[i]], [yy[i]], eng="pool")
                for ot in range(2):
                    q = k % 2; k += 1
                    for ct in range(2):
                        kb.MM(psz[q][:], gw[:, ct, ot * 128:(ot + 1) * 128], yy[i][:, ct, :], ct == 0, ct == 1, [gw, yy[i]], [psz[q]])
                    kb.ACT(sgb[q][:], psz[q][:], AF.Sigmoid, [psz[q], glb], [sgb[q]], bias=glb[:, ot:ot + 1])
                    kb.TT(ob[q][:], yy[i][:, ot, :], sgb[q][:], ALU.mult, [yy[i], sgb[q]], [ob[q]])
                    kb.ST(kb.YC[768 + ot * 128:768 + (ot + 1) * 128, cols], ob[q][:], [ob[q]])
```

```python
import numpy as np
import concourse.bass as bass
import concourse.mybir as mybir
from concourse.bass_utils import run_bass_kernel_spmd
from contextlib import ExitStack

F32 = mybir.dt.float32
BF16 = mybir.dt.bfloat16
AF = mybir.ActivationFunctionType
ALU = mybir.AluOpType

ENGS = ("pe", "act", "dve", "pool", "sp")
EPOCH = 16000
N_DMA_SEM = 32


class Buf:
    __slots__ = ("name", "w", "r", "excl", "pe_partial")

    def __init__(self, name="", excl=False):
        self.name = name
        self.w = None
        self.r = []
        self.excl = excl
        self.pe_partial = False


class T:
    def __init__(self, h, name, excl=False):
        self.h = h
        self.name = name
        self.b = Buf(name, excl)
        self.excl = excl
        self.subs = {}

    def __getitem__(self, k):
        return self.h[k]

    def s(self, key):
        if self.excl:
            return self.b
        if key not in self.subs:
            self.subs[key] = Buf("%s.%s" % (self.name, key))
        return self.subs[key]


class Prog:
    def __init__(self, nc):
        self.nc = nc
        self.es = ExitStack()
        self.stack = [self.es]
        self.ops = {e: [] for e in ENGS}
        self.cnt = {e: 0 for e in ENGS}
        self.seen = {e: {} for e in ENGS}
        self.last = {}
        self.dma_k = 0
        self.dma_use = [0] * N_DMA_SEM
        self.dma_sems = [self.es.enter_context(nc.semaphore("dq%d" % i)) for i in range(N_DMA_SEM)]
        self.eng_sems = {}
        self.out_tokens = []
        self.n_ops = 0
        self.uid = 0

    def _nm(self, name):
        self.uid += 1
        return "%s_%d" % (name, self.uid)

    def sbuf(self, name, shape, dt=F32):
        h = self.stack[-1].enter_context(self.nc.sbuf_tensor(self._nm(name), list(shape), dt))
        return T(h, name)

    def psum(self, name, shape, dt=F32):
        n = 1
        for d_ in shape[1:]:
            n *= d_
        nb = (n * 4 + 2047) // 2048
        h = self.stack[-1].enter_context(self.nc.psum_tensor(self._nm(name), [128, nb * 512], F32))
        v = h[0:shape[0], 0:n]
        if len(shape) == 3:
            v = v.rearrange("p (a b) -> p a b", a=shape[1])
        elif len(shape) == 4:
            v = v.rearrange("p (a b c) -> p a b c", a=shape[1], b=shape[2])
        return T(v, name, excl=True)

    def dram(self, name, shape, dt=F32, kind="Internal"):
        h = self.nc.dram_tensor(name, list(shape), dt, kind=kind)
        return T(h.ap(), name)

    class _Scope:
        def __init__(self, p):
            self.p = p

        def __enter__(self):
            st = ExitStack()
            self.p.stack.append(st)
            return st

        def __exit__(self, *a):
            self.p.barrier()
            st = self.p.stack.pop()
            st.close()
            return False

    def scope(self):
        return Prog._Scope(self)

    def _eng_sem(self, e, epoch):
        k = (e, epoch)
        if k not in self.eng_sems:
            self.eng_sems[k] = self.es.enter_context(self.nc.semaphore("s_%s_%d" % (e, epoch)))
        return self.eng_sems[k]

    def _waits(self, eng, reads, writes, extra=(), skip_pe=False):
        need = {}

        def add(tok):
            if tok is None:
                return
            key, val = tok
            if need.get(key, 0) < val:
                need[key] = val
        for b in reads:
            add(b.w)
        for b in writes:
            add(b.w)
            for t in b.r:
                add(t)
        for t in extra:
            add(t)
        out = []
        seen = self.seen[eng]
        for key, val in need.items():
            if skip_pe and key[0] == "e" and key[1] == "pe":
                continue
            if seen.get(key, 0) < val:
                seen[key] = val
                out.append((key, val))
        return out

    @staticmethod
    def _bufs(xs):
        out = []
        for x in xs:
            if x is None:
                continue
            out.append(x.b if isinstance(x, T) else x)
        return out

    def _commit(self, tok, reads, writes):
        self.last[tok[0]] = tok[1]
        for b in reads:
            b.r.append(tok)
            if len(b.r) > 64:
                mx = {}
                for k, v in b.r:
                    if mx.get(k, 0) < v:
                        mx[k] = v
                b.r = list(mx.items())
        for b in writes:
            b.w = tok
            b.r = []
        self.n_ops += 1

    def op(self, eng, fn, reads=(), writes=(), partial=False):
        reads = self._bufs(reads)
        writes = self._bufs(writes)
        ex = [b for b in reads if b.excl]
        if ex:
            reads = [b for b in reads if not b.excl]
            writes = writes + [b for b in ex if b not in writes]
        skip_pe = False
        if eng == "pe":
            skip_pe = (not partial) and all(not b.pe_partial for b in writes)
            for b in writes:
                b.pe_partial = partial
        waits = self._waits(eng, reads, writes, skip_pe=skip_pe)
        self.cnt[eng] += 1
        epoch, val = divmod(self.cnt[eng] - 1, EPOCH)
        tok = (("e", eng, epoch), val + 1)
        self.ops[eng].append((waits, fn, tok))
        self._commit(tok, reads, writes)
        return tok

    def dma(self, out_ap, in_ap, reads=(), writes=(), q="sp", is_output=False, **kw):
        reads = self._bufs(reads)
        writes = self._bufs(writes)
        i = self.dma_k % N_DMA_SEM
        self.dma_k += 1
        prev = self.dma_use[i]
        extra = [(("d", i), 16 * prev)] if prev else []
        waits = self._waits(q, reads, writes, extra)
        self.dma_use[i] = prev + 1
        tok = (("d", i), 16 * (prev + 1))

        def fn(e):
            return e.dma_start(out=out_ap, in_=in_ap, **kw)
        self.ops[q].append((waits, fn, tok))
        self._commit(tok, reads, writes)
        if is_output:
            self.out_tokens.append(tok)
        return tok

    def barrier(self):
        toks = list(self.last.items())
        for e in ENGS:
            waits = self._waits(e, [], [], toks)
            if waits:
                self.ops[e].append((waits, None, None))

    def _sem_of(self, key):
        if key[0] == "d":
            return self.dma_sems[key[1]]
        return self._eng_sem(key[1], key[2])

    def emit(self):
        nc = self.nc
        self.barrier()
        for e in ENGS:
            for waits, fn, tok in self.ops[e]:
                if tok is not None:
                    self._sem_of(tok[0])
                for key, val in waits:
                    self._sem_of(key)
        with nc.Block() as block:
            def run(e, handle):
                for waits, fn, tok in self.ops[e]:
                    for key, val in waits:
                        handle.wait_ge(self._sem_of(key), val)
                    if fn is None:
                        continue
                    ins = fn(handle)
                    key, val = tok
                    ins.then_inc(self._sem_of(key), 16 if key[0] == "d" else 1)

            @block.sync
            def _(h):
                run("sp", h)

            @block.tensor
            def _(h):
                run("pe", h)

            @block.scalar
            def _(h):
                run("act", h)

            @block.vector
            def _(h):
                run("dve", h)

            @block.gpsimd
            def _(h):
                run("pool", h)

    def close(self):
        self.es.close()


D = 1024
S = 4352
NT = 34
LAT0 = 256
DEPTH = 2
EPS = 1e-6
NEG = -30000.0
ORDER = [list(range(NT)), [1, 0] + list(range(NT - 1, 1, -1))]

C_HQ, C_HI, C_HG, C_HFF, C_HFB = 0, 256, 512, 768, 1024
C_RQ, C_RK, C_RV, C_RG = 1280, 1536, 1792, 2048
C_GQKV, C_GG, C_GA, C_GB, C_SU = 2304, 3072, 3328, 3336, 3344
Z_HQ, Z_HG, Z_HFF, Z_HFB, Z_RQ, Z_RK, Z_RG, Z_GQKV, Z_GG, Z_SU = 0, 256, 512, 768, 1024, 1280, 1536, 1792, 2560, 2816
NZF = 3072
FM_MAP = [(Z_HQ, C_HQ, 256), (Z_HG, C_HG, 256), (Z_HFF, C_HFF, 256), (Z_HFB, C_HFB, 256), (Z_RQ, C_RQ, 256),
          (Z_RK, C_RK, 256), (Z_RG, C_RG, 256), (Z_GQKV, C_GQKV, 768), (Z_GG, C_GG, 256), (Z_SU, C_SU, 256)]
FM_BLOCKS = [(zr + i, wc + i) for zr, wc, n in FM_MAP for i in range(0, n, 128)]
NZT = 528

CN = {}


def _const_pack():
    mats = []

    def add(name, m):
        CN[name] = len(mats)
        mats.append(np.asarray(m, np.float32))
    p = np.arange(128)[:, None]
    f = np.arange(128)[None, :]
    add("IDENT", (p == f))
    add("ONES", np.ones((128, 128)))
    add("TRIF", (p <= f))
    add("TRIB", (p >= f))
    add("SUFF", (p > f))
    add("PREB", (p < f))
    add("NLE", np.where(p <= f, 0.0, NEG))
    add("NLT", np.where(p < f, 0.0, NEG))
    add("NGE", np.where(p >= f, 0.0, NEG))
    add("NGT", np.where(p > f, 0.0, NEG))
    for s in (1, 2, 4, 8, 16, 32, 64):
        m = (((p // s) % 2) == 1) & ((f // s) == (p // s) - 1)
        add("MOFF%d" % s, m)
        add("MOFFT%d" % s, m.T)
    add("BLK64", (p // 64) == (f // 64))
    rot = np.zeros((128, 128))
    for m in range(128):
        if (m % 64) < 32:
            rot[m + 32, m] = -1.0
        else:
            rot[m - 32, m] = 1.0
    add("ROT", rot)
    add("IOTAF", np.broadcast_to(f, (128, 128)))
    add("IOTAF1", np.broadcast_to(f + 1, (128, 128)))
    add("RIOTAF", np.broadcast_to(128 - f, (128, 128)))
    add("R127F", np.broadcast_to(127 - f, (128, 128)))
    add("DIFF", f - p)
    add("NDIFF", p - f)
    for h in range(4):
        m = np.zeros((128, 128)); m[h, :] = 1.0
        add("SELH%d" % h, m)
    for hp in range(2):
        m = np.zeros((128, 128)); m[2 * hp, 0:64] = 1.0; m[2 * hp + 1, 64:128] = 1.0
        add("SELP%d" % hp, m)
    cc = np.zeros((128, 128))
    cc[:, 0] = EPS; cc[:, 1] = 1.0; cc[:, 2] = np.arange(128); cc[:, 3] = 127 - np.arange(128)
    cc[:, 5] = -np.pi; cc[:, 6] = -np.arange(128); cc[:, 7] = -(127 - np.arange(128))
    add("CCOL", cc)
    gm = np.zeros((128, 128))
    for g in range(16):
        gm[(g % 8) * 16:(g % 8) * 16 + 16, g] = 1.0
    add("GMASK", gm)
    return np.concatenate(mats, axis=1)


CONST_NP = _const_pack()
NCONST = CONST_NP.shape[1] // 128


def _rope_tables():
    half = 32
    inv = 10000.0 ** (-np.arange(half, dtype=np.float64) / half)
    pos = np.arange(S, dtype=np.float64)
    ang = pos[None, :] * inv[:, None]
    cos = np.cos(ang); sin = np.sin(ang)
    cos128 = np.tile(cos, (4, 1)); sin128 = np.tile(sin, (4, 1))
    return cos128.astype(np.float32), sin128.astype(np.float32)


def _conv_masks():
    m = np.ones((2, 512), np.float32)
    w = np.arange(512) % 64
    m[0, w == 0] = 0.0
    m[1, w == 63] = 0.0
    lat = np.broadcast_to(m[None], (128, 2, 512)).copy()
    c = np.ones((2, 256), np.float32)
    c[0, 0] = 0.0
    c[1, 255] = 0.0
    ctx = np.broadcast_to(c[None], (128, 2, 256)).copy()
    return lat, ctx


class KB:
    def __init__(self, cfg):
        self.cfg = cfg
        nc = bass.Bass("TRN2", target_bir_lowering=False)
        self.nc = nc
        self.P = Prog(nc)
        self.rr = 0

    def MM(self, ps, lhsT, rhs, st, sp, R, W):
        partial = lhsT.partition_size() < 128
        self.P.op("pe", lambda e: e.matmul(ps, lhsT, rhs, start=st, stop=sp), R, W, partial=partial)

    def TR(self, ps, in_, ident, R, W):
        self.P.op("pe", lambda e: e.transpose(ps, in_, ident), R, W)

    def ACT(self, out, in_, func, R, W, **kw):
        self.P.op("act", lambda e: e.activation(out=out, in_=in_, func=func, **kw), R, W)

    def TS(self, out, in0, s1, s2, op0, op1, R, W, eng="dve"):
        if s2 is None:
            self.P.op(eng, lambda e: e.tensor_scalar(out=out, in0=in0, scalar1=s1, scalar2=None, op0=op0), R, W)
        else:
            self.P.op(eng, lambda e: e.tensor_scalar(out=out, in0=in0, scalar1=s1, scalar2=s2, op0=op0, op1=op1), R, W)

    def TT(self, out, in0, in1, op, R, W, eng="dve"):
        self.P.op(eng, lambda e: e.tensor_tensor(out=out, in0=in0, in1=in1, op=op), R, W)

    def STT(self, out, in0, sc, in1, op0, op1, R, W, eng="dve"):
        eng = "dve"
        self.P.op(eng, lambda e: e.scalar_tensor_tensor(out=out, in0=in0, scalar=sc, in1=in1, op0=op0, op1=op1), R, W)

    def CP(self, out, in_, R, W, eng="dve"):
        if eng == "act":
            self.ACT(out, in_, AF.Copy, R, W)
        else:
            self.P.op(eng, lambda e: e.tensor_copy(out=out, in_=in_), R, W)

    def CPRED(self, out, mask, data, R, W):
        self.P.op("dve", lambda e: e.copy_predicated(out=out, mask=mask, data=data), R, W)

    def MS(self, ap, val, W, eng="dve"):
        self.P.op(eng, lambda e: e.memset(ap, val), (), W)

    def RECIP(self, out, in_, R, W):
        self.P.op("dve", lambda e: e.reciprocal(out=out, in_=in_), R, W)

    def SCAN(self, out, d0, d1, R, W):
        self.P.op("dve", lambda e: e.tensor_tensor_scan(out=out, data0=d0, data1=d1, initial=0.0,
                                                        op0=ALU.mult, op1=ALU.add), R, W)

    def LD(self, out, in_, W, R=(), q="sp", **kw):
        self.P.dma(out, in_, reads=R, writes=W, q=q, **kw)

    def ST(self, out, in_, R, W=(), q="pool", **kw):
        self.P.dma(out, in_, reads=R, writes=W, q=q, **kw)

    def evac_eng(self):
        self.rr += 1
        return "act" if self.rr % 2 else "dve"

    def C(self, name):
        i = CN[name]
        return self.const[:, i * 128:(i + 1) * 128]


PARAM_SHAPES = {
    "mod_w": [2, 1024, 6144], "mod_b": [2, 6144], "norm1_g": [2, 1024], "norm2_g": [2, 1024],
    "w_in": [2, 1024, 3600], "hgrn_lb_logits": [2, 2, 256], "hgrn_norm_g": [2, 64],
    "ret_decay_logit": [2, 2, 4], "gdn_conv_w": [2, 3, 3, 768], "gdn_a_log": [2, 2, 4],
    "gdn_dt_bias": [2, 2, 4], "gdn_norm_g": [2, 64], "s5_lam_re": [2, 2, 16, 64],
    "s5_lam_im": [2, 2, 16, 64], "s5_log_dt": [2, 2, 16], "s5_b_re": [2, 16, 64, 16],
    "s5_b_im": [2, 16, 64, 16], "s5_c_re": [2, 16, 16, 64], "s5_c_im": [2, 16, 16, 64],
    "s5_d": [2, 256], "s5_glu_w": [2, 256, 256], "s5_glu_b": [2, 256], "w_out": [2, 1024, 1024],
    "mlp_w1": [2, 1024, 4096], "mlp_w2": [2, 4096, 1024], "final_norm_g": [1024],
}


def declare(kb):
    P = kb.P
    cfg = kb.cfg
    kinds = cfg.get("kinds", {})
    kb.xin = P.dram("xin", [S, D], F32, kind="ExternalInput")
    kb.cvecT = P.dram("cvecT", [1024, 2], F32, kind="ExternalInput")
    kb.prm = {k: P.dram(k, shp, F32, kind="ExternalInput") for k, shp in PARAM_SHAPES.items()}
    kb.constd = P.dram("constp", [128, NCONST * 128], F32, kind="ExternalInput")
    kb.ropec = P.dram("ropec", [128, S], F32, kind="ExternalInput")
    kb.ropes = P.dram("ropes", [128, S], F32, kind="ExternalInput")
    kb.cmlat = P.dram("cmlat", [128, 2, 512], F32, kind="ExternalInput")
    kb.cmctx = P.dram("cmctx", [128, 2, 256], F32, kind="ExternalInput")
    kb.cwin = P.dram("cwin", [128, 2, 642], F32, kind="ExternalInput")
    kb.y = P.dram("y", [4096, D], F32, kind="ExternalOutput")
    kb.XS = P.dram("XS", [S, D], F32, kind=kinds.get("XS", "Internal"))
    kb.ZF = P.dram("ZF", [NZF, S], F32, kind=kinds.get("ZF", "Internal"))
    kb.ZT = P.dram("ZT", [S, NZT], F32, kind=kinds.get("ZT", "Internal"))
    kb.QKVF = P.dram("QKVF", [768, S], F32, kind=kinds.get("QKVF", "Internal"))
    kb.YC = P.dram("YC", [1024, S], F32, kind=kinds.get("YC", "Internal"))
    kb.H2T = P.dram("H2T", [1024, S], BF16, kind=kinds.get("H2T", "Internal"))
    kb.const = P.sbuf("const", [128, NCONST * 128])
    nchunk = 4
    w = NCONST * 128 // nchunk
    for i in range(nchunk):
        a, b = i * w, (i + 1) * w if i < nchunk - 1 else NCONST * 128
        kb.LD(kb.const[:, a:b], kb.constd[:, a:b], [kb.const.s(i)])
    kb.const_bufs = [kb.const.s(i) for i in range(nchunk)]
    kb.CB = kb.const_bufs
    kb.GS1 = P.sbuf("GS1", [128, 8, 2]); kb.SH1 = P.sbuf("SH1", [128, 8, 2])
    kb.GS2 = P.sbuf("GS2", [128, 8, 2]); kb.SH2 = P.sbuf("SH2", [128, 8, 2])
    kb.GATE1 = P.sbuf("GATE1", [128, 2, 1024]); kb.GATE2 = P.sbuf("GATE2", [128, 2, 1024])


def phase_mod(kb, l):
    P = kb.P
    prm = kb.prm
    with P.scope():
        cT = P.sbuf("cT", [128, 8, 2])
        kb.LD(cT[:], kb.cvecT[:].rearrange("(et e) c -> e et c", e=128), [cT])
        sc = P.sbuf("sc", [128, 8, 2])
        kb.ACT(sc[:], cT[:], AF.Silu, [cT], [sc])
        screp = P.sbuf("screp", [128, 8, 2, 128])
        kb.CP(screp[:], sc[:].unsqueeze(3).to_broadcast([128, 8, 2, 128]), [sc], [screp])
        mbf = P.sbuf("mbf", [128, 48])
        kb.LD(mbf[:], prm["mod_b"][l].rearrange("(j p) -> p j", p=128), [mbf], allow_slow_non_contiguous=True)
        ngf = P.sbuf("ngf", [128, 2, 8])
        kb.LD(ngf[:, 0, :], prm["norm1_g"][l].rearrange("(j p) -> p j", p=128), [ngf], allow_slow_non_contiguous=True)
        kb.LD(ngf[:, 1, :], prm["norm2_g"][l].rearrange("(j p) -> p j", p=128), [ngf], allow_slow_non_contiguous=True)
        mbrow = P.sbuf("mbrow", [128, 2, 1024])
        for gi, v in enumerate((2, 5)):
            kb.LD(mbrow[:, gi, :], prm["mod_b"][l][v * 1024:(v + 1) * 1024].partition_broadcast(128), [mbrow])
        wch = [P.sbuf("wch%d" % i, [128, 8, 1024]) for i in range(2)]
        ps_fm = P.psum("ps_fm", [128, 96])
        ps_g = [P.psum("ps_g%d" % i, [128, 512]) for i in range(2)]
        MF = P.sbuf("MF", [128, 48, 2])
        k = 0
        for v in range(6):
            wc = wch[v % 2]
            for et in range(8):
                kb.LD(wc[:, et, :], prm["mod_w"][l][et * 128:(et + 1) * 128, v * 1024:(v + 1) * 1024], [wc])
            for db in range(8):
                col = (v * 8 + db) * 2
                for et in range(8):
                    kb.MM(ps_fm[:, col:col + 2], wc[:, et, db * 128:(db + 1) * 128], sc[:, et, :],
                          et == 0, et == 7, [wc, sc], [ps_fm])
            if v in (2, 5):
                gt = kb.GATE1 if v == 2 else kb.GATE2
                gi = 0 if v == 2 else 1
                for which in range(2):
                    for half in range(2):
                        pg = ps_g[k % 2]; k += 1
                        for et in range(8):
                            kb.MM(pg[:], screp[:, et, which, :], wc[:, et, half * 512:(half + 1) * 512],
                                  et == 0, et == 7, [screp, wc], [pg])
                        kb.TT(gt[:, which, half * 512:(half + 1) * 512], pg[:], mbrow[:, gi, half * 512:(half + 1) * 512],
                              ALU.add, [pg, mbrow], [gt])
        kb.TT(MF[:], ps_fm[:].rearrange("p (j c) -> p j c", c=2), mbf[:].unsqueeze(2).to_broadcast([128, 48, 2]),
              ALU.add, [ps_fm, mbf], [MF])
        tmp = P.sbuf("mtmp", [128, 8, 2])
        kb.TS(tmp[:], MF[:, 8:16, :], 1.0, None, ALU.add, None, [MF], [tmp])
        kb.TT(kb.GS1[:], tmp[:], ngf[:, 0, :].unsqueeze(2).to_broadcast([128, 8, 2]), ALU.mult, [tmp, ngf], [kb.GS1])
        kb.CP(kb.SH1[:], MF[:, 0:8, :], [MF], [kb.SH1])
        tmp2 = P.sbuf("mtmp2", [128, 8, 2])
        kb.TS(tmp2[:], MF[:, 32:40, :], 1.0, None, ALU.add, None, [MF], [tmp2])
        kb.TT(kb.GS2[:], tmp2[:], ngf[:, 1, :].unsqueeze(2).to_broadcast([128, 8, 2]), ALU.mult, [tmp2, ngf], [kb.GS2])
        kb.CP(kb.SH2[:], MF[:, 24:32, :], [MF], [kb.SH2])


def norm_to_fm(kb, xt, hT, col0, GS, SH, which, bufs, R_x):
    P = kb.P
    junk, st, xn, ps_ts = bufs["junk"], bufs["st"], bufs["xn"], bufs["ps_t"]
    kb.MS(st[:, 0:1], 0.0, [st])
    kb.ACT(junk[:], xt[:], AF.Square, [xt], [junk, st], accum_out=st[:, 0:1])
    kb.ACT(st[:, 1:2], st[:, 0:1], AF.Sqrt, [st] + kb.CB, [st], scale=1.0 / D, bias=kb.C("CCOL")[:, 0:1])
    kb.RECIP(st[:, 2:3], st[:, 1:2], [st], [st])
    kb.ACT(xn[:], xt[:], AF.Copy, [xt, st], [xn], scale=st[:, 2:3])
    for half in range(2):
        ps_t = ps_ts[half]
        for q in range(4):
            dt = half * 4 + q
            kb.TR(ps_t[:, q * 128:(q + 1) * 128], xn[:, dt * 128:(dt + 1) * 128], kb.C("IDENT"), [xn] + kb.CB, [ps_t])
        for q in range(4):
            dt = half * 4 + q
            if q % 2 == 0:
                kb.TS(hT[:, dt, col0:col0 + 128], ps_t[:, q * 128:(q + 1) * 128], GS[:, dt, which:which + 1],
                      SH[:, dt, which:which + 1], ALU.mult, ALU.add, [ps_t, GS, SH], [hT])
            else:
                kb.ACT(hT[:, dt, col0:col0 + 128], ps_t[:, q * 128:(q + 1) * 128], AF.Identity, [ps_t, GS, SH], [hT],
                       scale=GS[:, dt, which:which + 1], bias=SH[:, dt, which:which + 1])


def phase_a(kb, l, src):
    P = kb.P
    with P.scope():
        win = P.sbuf("win", [128, 8, 3600], BF16)
        for kt in range(8):
            kb.LD(win[:, kt, :], kb.prm["w_in"][l][kt * 128:(kt + 1) * 128, :], [win.s(kt)], q="pool")
        winb = [win.s(kt) for kt in range(8)]
        xbuf = [P.sbuf("xa%d" % i, [128, 1024]) for i in range(2)]
        hTb = [P.sbuf("hTa%d" % i, [128, 8, 512], BF16) for i in range(2)]
        nb = {"junk": P.sbuf("junk", [128, 1024]), "st": P.sbuf("st", [128, 4]), "xn": P.sbuf("xn", [128, 1024]),
              "ps_t": [P.psum("ps_t%d" % i, [128, 512]) for i in range(2)]}
        ps_f = [P.psum("ps_f%d" % i, [128, 512]) for i in range(3)]
        ps_a = [P.psum("ps_a%d" % i, [128, 512]) for i in range(2)]
        ps_b = P.psum("ps_b", [128, 16])
        stg = [P.sbuf("stg%d" % i, [128, 512]) for i in range(4)]
        stt = [P.sbuf("stt%d" % i, [128, NZT]) for i in range(2)]
        kx = kf = ks = ka = 0
        for gi, t0 in enumerate(range(0, S, 512)):
            n = min(512, S - t0)
            hT = hTb[gi % 2]
            for ti in range(n // 128):
                tt = t0 // 128 + ti
                which = 1 if tt < 2 else 0
                xt = xbuf[kx % 2]; kx += 1
                kb.LD(xt[:], src[tt * 128:(tt + 1) * 128, :], [xt])
                norm_to_fm(kb, xt, hT, ti * 128, kb.GS1, kb.SH1, which, nb, None)
            for (zr, wc) in FM_BLOCKS:
                ps = ps_f[kf % 3]; kf += 1
                for kt in range(8):
                    kb.MM(ps[:, :n], win[:, kt, wc:wc + 128], hT[:, kt, :n], kt == 0, kt == 7, [winb[kt], hT], [ps])
                sg = stg[ks % 4]; ks += 1
                kb.CP(sg[:, :n], ps[:, :n], [ps], [sg], eng=kb.evac_eng())
                kb.ST(kb.ZF[zr:zr + 128, t0:t0 + n], sg[:, :n], [sg])
            for ti in range(n // 128):
                tt = t0 // 128 + ti
                pa = ps_a[ka % 2]
                so = stt[ka % 2]; ka += 1
                for (c0, w0, wn) in ((0, C_HI, 256), (256, C_RV, 256)):
                    for kt in range(8):
                        kb.MM(pa[:, c0:c0 + wn], hT[:, kt, ti * 128:(ti + 1) * 128], win[:, kt, w0:w0 + wn],
                              kt == 0, kt == 7, [winb[kt], hT], [pa])
                for kt in range(8):
                    kb.MM(ps_b[:], hT[:, kt, ti * 128:(ti + 1) * 128], win[:, kt, C_GA:C_GA + 16],
                          kt == 0, kt == 7, [winb[kt], hT], [ps_b])
                kb.CP(so[:, 0:512], pa[:], [pa], [so], eng="act")
                kb.CP(so[:, 512:528], ps_b[:], [ps_b], [so], eng="dve")
                kb.ST(kb.ZT[tt * 128:(tt + 1) * 128, :], so[:], [so])


def build(cfg):
    kb = KB(cfg)
    P = kb.P
    declare(kb)
    P.barrier()
    stages = cfg.get("stages", "all")
    for l in cfg.get("layers", range(DEPTH)):
        src = kb.xin if l == 0 else kb.XS
        if stages == "all" or "M" in stages:
            phase_mod(kb, l)
        if stages == "all" or "A" in stages:
            phase_a(kb, l, src)
        if stages == "all" or "R" in stages:
            (mixer_ret if cfg.get("ret_old") else mixer_ret2)(kb, l)
        if stages == "all" or "H" in stages:
            (mixer_hgrn if cfg.get("hgrn_old") else mixer_hgrn2)(kb, l)
        if stages == "all" or "G" in stages:
            (mixer_gdn if cfg.get("gdn_old") else mixer_gdn2)(kb, l)
        if stages == "all" or "S" in stages:
            (mixer_s5 if cfg.get("s5_old") else mixer_s5_2)(kb, l)
        if stages == "all" or "C" in stages:
            phase_c(kb, l, src)
    P.emit()
    P.close()
    return kb


_CONSTS = None


def host_inputs(inputs, cores=range(8)):
    global _CONSTS
    if _CONSTS is None:
        rc, rs = _rope_tables()
        cl, cc = _conv_masks()
        _CONSTS = {"constp": CONST_NP, "ropec": rc, "ropes": rs, "cmlat": cl, "cmctx": cc, "cwin": _conv_win_masks()}
    maps = []
    for b in cores:
        m = {"xin": np.ascontiguousarray(np.concatenate([inputs["ctx"][b], inputs["x"][b]], axis=0), dtype=np.float32),
             "cvecT": np.ascontiguousarray(np.stack([inputs["c"][b], inputs["c_ctx"]], axis=1), dtype=np.float32)}
        for k in PARAM_SHAPES:
            m[k] = np.ascontiguousarray(inputs[k], dtype=np.float32)
        m.update(_CONSTS)
        maps.append(m)
    return maps


def kernel(**inputs):
    inputs = {k: np.asarray(v) for k, v in inputs.items()}
    kb = build({})
    maps = host_inputs(inputs)
    res = run_bass_kernel_spmd(kb.nc, maps, core_ids=list(range(8)))
    out = np.stack([np.asarray(r["y"]).reshape(4096, D) for r in res.results], axis=0)
    return out.astype(np.float32)


def phase_c(kb, l, src):
    P = kb.P
    last = (l == DEPTH - 1)
    t_start = 2 if last else 0
    with P.scope():
        wout = P.sbuf("wout", [128, 8, 1024], BF16)
        for ft in range(8):
            kb.LD(wout[:, ft, :], kb.prm["w_out"][l][ft * 128:(ft + 1) * 128, :], [wout.s(ft)], q="pool")
        wb = [wout.s(ft) for ft in range(8)]
        ycb = [P.sbuf("yc%d" % i, [128, 8, 128], BF16) for i in range(2)]
        xb = [P.sbuf("xc%d" % i, [128, 1024]) for i in range(2)]
        x1b = [P.sbuf("x1c%d" % i, [128, 1024]) for i in range(2)]
        tmpb = [P.sbuf("tc%d" % i, [128, 512]) for i in range(2)]
        h2b = [P.sbuf("h2c%d" % i, [128, 8, 128], BF16) for i in range(2)]
        nb = {"junk": P.sbuf("junkc", [128, 1024]), "st": P.sbuf("stc", [128, 4]), "xn": P.sbuf("xnc", [128, 1024]),
              "ps_t": [P.psum("ps_tc%d" % i, [128, 512]) for i in range(2)]}
        ps_y = [P.psum("ps_y%d" % i, [128, 512]) for i in range(4)]
        k = 0
        for tt in range(t_start, NT):
            which = 1 if tt < 2 else 0
            yc = ycb[k % 2]; xt = xb[k % 2]; x1 = x1b[k % 2]; h2 = h2b[k % 2]
            cols = slice(tt * 128, (tt + 1) * 128)
            kb.LD(yc[:], kb.YC[:, cols].rearrange("(ft p) t -> p ft t", p=128), [yc], q="pool")
            kb.LD(xt[:], src[cols, :], [xt])
            for half in range(2):
                ps = ps_y[(2 * k + half) % 4]
                for ft in range(8):
                    kb.MM(ps[:], yc[:, ft, :], wout[:, ft, half * 512:(half + 1) * 512], ft == 0, ft == 7,
                          [yc, wb[ft]], [ps])
                tm = tmpb[half]
                kb.TT(tm[:], ps[:], kb.GATE1[:, which, half * 512:(half + 1) * 512], ALU.mult, [ps, kb.GATE1], [tm])
                kb.TT(x1[:, half * 512:(half + 1) * 512], xt[:, half * 512:(half + 1) * 512], tm[:], ALU.add,
                      [xt, tm], [x1], eng="pool")
            kb.ST(kb.XS[cols, :], x1[:], [x1])
            norm_to_fm(kb, x1, h2, 0, kb.GS2, kb.SH2, which, nb, None)
            kb.ST(kb.H2T[:, cols].rearrange("(dt p) t -> p dt t", p=128), h2[:], [h2])
            k += 1
    with P.scope():
        w1 = P.sbuf("w1", [128, 8, 4096], BF16)
        w2 = P.sbuf("w2", [128, 32, 1024], BF16)
        for kt in range(8):
            kb.LD(w1[:, kt, :], kb.prm["mlp_w1"][l][kt * 128:(kt + 1) * 128, :], [w1.s(kt)], q="pool")
        for fb in range(32):
            kb.LD(w2[:, fb, :], kb.prm["mlp_w2"][l][fb * 128:(fb + 1) * 128, :], [w2.s(fb)], q="pool")
        h2b = [P.sbuf("h2d%d" % i, [128, 8, 256], BF16) for i in range(2)]
        uTb = [P.sbuf("uT%d" % i, [128, 16, 256], BF16) for i in range(1)]
        rb = [P.sbuf("relu%d" % i, [128, 256]) for i in range(3)]
        xb = [P.sbuf("xd%d" % i, [128, 1024]) for i in range(2)]
        tmpb = [P.sbuf("td%d" % i, [128, 512]) for i in range(2)]
        ps_u = [P.psum("ps_u%d" % i, [128, 256]) for i in range(3)]
        ps_y = [P.psum("ps_y2%d" % i, [128, 512]) for i in range(4)]
        if last:
            fg = P.sbuf("fg", [128, 1024])
            kb.LD(fg[:], kb.prm["final_norm_g"][:].partition_broadcast(128), [fg])
            stf = P.sbuf("stf", [128, 4])
            xnf = P.sbuf("xnf", [128, 1024])
        k = 0; ku = 0
        for g0 in range(t_start, NT, 2):
            h2 = h2b[k % 2]; uT = uTb[0]
            cols = slice(g0 * 128, (g0 + 2) * 128)
            kb.LD(h2[:], kb.H2T[:, cols].rearrange("(dt p) t -> p dt t", p=128), [h2])
            for hh in range(2):
                for fl in range(16):
                    fb = hh * 16 + fl
                    ps = ps_u[ku % 3]; r = rb[ku % 3]; ku += 1
                    for kt in range(8):
                        kb.MM(ps[:], w1[:, kt, fb * 128:(fb + 1) * 128], h2[:, kt, :], kt == 0, kt == 7, [w1.s(kt), h2], [ps])
                    kb.ACT(r[:], ps[:], AF.Relu, [ps], [r])
                    kb.TT(uT[:, fl, :], r[:], r[:], ALU.mult, [r], [uT.s(fl)], eng=("dve" if fb % 2 else "pool"))
                for ti in range(2):
                    for half in range(2):
                        ps = ps_y[2 * ti + half]
                        for fl in range(16):
                            fb = hh * 16 + fl
                            kb.MM(ps[:], uT[:, fl, ti * 128:(ti + 1) * 128], w2[:, fb, half * 512:(half + 1) * 512],
                                  fb == 0, fb == 31, [uT.s(fl), w2.s(fb)], [ps])
            for ti in range(2):
                tt = g0 + ti
                which = 1 if tt < 2 else 0
                xt = xb[ti]
                rows = slice(tt * 128, (tt + 1) * 128)
                kb.LD(xt[:], kb.XS[rows, :], [xt])
                for half in range(2):
                    ps = ps_y[2 * ti + half]
                    tm = tmpb[half]
                    kb.TT(tm[:], ps[:], kb.GATE2[:, which, half * 512:(half + 1) * 512], ALU.mult, [ps, kb.GATE2], [tm])
                    kb.TT(xt[:, half * 512:(half + 1) * 512], xt[:, half * 512:(half + 1) * 512], tm[:], ALU.add,
                          [xt, tm], [xt], eng="pool")
                if not last:
                    kb.ST(kb.XS[rows, :], xt[:], [xt])
                else:
                    kb.MS(stf[:, 0:1], 0.0, [stf])
                    kb.ACT(xnf[:], xt[:], AF.Square, [xt], [xnf, stf], accum_out=stf[:, 0:1])
                    kb.ACT(stf[:, 1:2], stf[:, 0:1], AF.Sqrt, [stf], [stf], scale=1.0 / D, bias=kb.C("CCOL")[:, 0:1])
                    kb.RECIP(stf[:, 2:3], stf[:, 1:2], [stf], [stf])
                    kb.ACT(xnf[:], xt[:], AF.Copy, [xt, stf], [xnf], scale=stf[:, 2:3])
                    kb.TT(xnf[:], xnf[:], fg[:], ALU.mult, [xnf, fg], [xnf])
                    kb.P.dma(kb.y[(tt - 2) * 128:(tt - 1) * 128, :], xnf[:], reads=[xnf.b], q="pool", is_output=True)
            k += 1


def finalize_gated(kb, OACC, gate_row0, gain, yc_row0, pfx):
    P = kb.P
    def two(nm):
        return [P.sbuf(pfx + nm + "%d" % i, [128, 2, 128]) for i in range(2)]
    gb, sq, rt, eg, ob = two("fg"), two("fsq"), two("frt"), two("feg"), two("fo")
    ps_m = [P.psum(pfx + "fps%d" % i, [128, 2, 128]) for i in range(2)]
    for n in range(NT):
        cols = slice(n * 128, (n + 1) * 128)
        i = n % 2
        g = gb[i]
        kb.LD(g[:], kb.ZF[gate_row0:gate_row0 + 256, cols].rearrange("(hp p) t -> p hp t", p=128), [g])
        o = OACC[:, :, cols]
        kb.TT(sq[i][:], o, o, ALU.mult, [OACC.s(n)], [sq[i]])
        kb.MM(ps_m[i][:].rearrange("p a b -> p (a b)"), kb.C("BLK64"), sq[i][:].rearrange("p a b -> p (a b)"), True, True,
              [sq[i]], [ps_m[i]])
        kb.ACT(rt[i][:], ps_m[i][:], AF.Ln, [ps_m[i]], [rt[i]], scale=1.0 / 64, bias=kb.C("CCOL")[:, 0:1])
        kb.ACT(rt[i][:], rt[i][:], AF.Exp, [rt[i]], [rt[i]], scale=-0.5)
        kb.ACT(eg[i][:], g[:], AF.Exp, [g], [eg[i]], scale=-1.0)
        kb.TS(eg[i][:], eg[i][:], 1.0, None, ALU.add, None, [eg[i]], [eg[i]])
        kb.RECIP(eg[i][:], eg[i][:], [eg[i]], [eg[i]])
        kb.TT(eg[i][:], eg[i][:], g[:], ALU.mult, [eg[i], g], [eg[i]], eng="pool")
        kb.TT(ob[i][:], o, rt[i][:], ALU.mult, [OACC.s(n), rt[i]], [ob[i]])
        if gain is not None:
            kb.STT(ob[i][:], ob[i][:], gain[:, 0:1], eg[i][:], ALU.mult, ALU.mult, [ob[i], gain, eg[i]], [ob[i]])
        else:
            kb.TT(ob[i][:], ob[i][:], eg[i][:], ALU.mult, [ob[i], eg[i]], [ob[i]])
        kb.ST(kb.YC[yc_row0:yc_row0 + 256, cols].rearrange("(hp p) t -> p hp t", p=128), ob[i][:], [ob[i]])


def oacc_write(kb, OACC, hp, n, ps, d):
    cols = slice(n * 128, (n + 1) * 128)
    if d == 0:
        kb.CP(OACC[:, hp, cols], ps[:], [ps], [OACC.s(n)], eng="act")
    else:
        kb.TT(OACC[:, hp, cols], OACC[:, hp, cols], ps[:], ALU.add, [ps], [OACC.s(n)])


def mixer_ret(kb, l):
    P = kb.P
    with P.scope():
        OACC = P.sbuf("r_oacc", [128, 2, S])
        with P.scope():
            lgt = P.sbuf("r_lgt", [128, 8])
            kb.LD(lgt[:], kb.prm["ret_decay_logit"][l].rearrange("d h -> (d h)").partition_broadcast(128), [lgt])
            LG = P.sbuf("r_LG", [128, 8])
            kb.ACT(LG[:], lgt[:], AF.Sigmoid, [lgt], [LG])
            kb.ACT(LG[:], LG[:], AF.Ln, [LG], [LG])
            LGP = P.sbuf("r_LGP", [128, 4])
            for d in range(2):
                for hp in range(2):
                    c = 2 * d + hp
                    kb.CP(LGP[0:64, c:c + 1], LG[0:64, 4 * d + 2 * hp:4 * d + 2 * hp + 1], [LG], [LGP])
                    kb.CP(LGP[64:128, c:c + 1], LG[64:128, 4 * d + 2 * hp + 1:4 * d + 2 * hp + 2], [LG], [LGP])
            MK = [P.sbuf("r_MK%d" % d, [128, 4, 128]) for d in range(2)]
            QDEC = [[P.sbuf("r_QD%d%d" % (d, hp), [128, 128]) for hp in range(2)] for d in range(2)]
            etmp = P.sbuf("r_etmp", [128, 128])
            for d in range(2):
                for h in range(4):
                    kb.ACT(etmp[:], kb.C("DIFF" if d == 0 else "NDIFF"), AF.Exp, [LG], [etmp],
                           scale=LG[:, 4 * d + h:4 * d + h + 1])
                    kb.STT(MK[d][:, h, :], etmp[:], 0.125, kb.C("TRIF" if d == 0 else "TRIB"), ALU.mult, ALU.mult,
                           [etmp], [MK[d]])
                for hp in range(2):
                    kb.ACT(QDEC[d][hp][:], kb.C("IOTAF1" if d == 0 else "RIOTAF"), AF.Exp, [LGP], [QDEC[d][hp]],
                           scale=LGP[:, 2 * d + hp:2 * d + hp + 1])
            KD = P.sbuf("r_KD", [128, 8])
            kb.ACT(KD[:, 0:4], LG[:, 0:4], AF.Exp, [LG], [KD], scale=kb.C("CCOL")[:, 3:4])
            kb.ACT(KD[:, 4:8], LG[:, 4:8], AF.Exp, [LG], [KD], scale=kb.C("CCOL")[:, 2:3])
            kb.TS(KD[:], KD[:], 0.125, None, ALU.mult, None, [KD], [KD])
            CV = P.sbuf("r_CV", [128, 4])
            kb.ACT(CV[:], LGP[:], AF.Exp, [LGP], [CV], scale=128.0)
            qTb = [P.sbuf("r_q%d" % i, [128, 2, 128]) for i in range(2)]
            kTb = [P.sbuf("r_k%d" % i, [128, 2, 128]) for i in range(2)]
            csb = [P.sbuf("r_cs%d" % i, [128, 2, 128]) for i in range(2)]
            Vp = [[P.sbuf("r_vp%d%d" % (i, h), [128, 128]) for h in range(4)] for i in range(2)]
            khp = [[P.sbuf("r_kh%d%d" % (i, h), [128, 128]) for h in range(4)] for i in range(2)]
            for i in range(2):
                for h in range(4):
                    kb.MS(Vp[i][h][:], 0.0, [Vp[i][h]], eng="pool")
                    kb.MS(khp[i][h][:], 0.0, [khp[i][h]], eng="pool")
            t1 = [P.sbuf("r_t1%d" % i, [128, 128]) for i in range(2)]
            t2 = [P.sbuf("r_t2%d" % i, [128, 128]) for i in range(2)]
            qr = [P.sbuf("r_qr%d" % i, [128, 2, 128]) for i in range(2)]
            kr = [P.sbuf("r_kr%d" % i, [128, 2, 128]) for i in range(2)]
            AT = [P.sbuf("r_AT%d" % i, [128, 2, 128]) for i in range(2)]
            qd = [P.sbuf("r_qd%d" % i, [128, 128]) for i in range(2)]
            Sb = [P.sbuf("r_S%d" % hp, [128, 128]) for hp in range(2)]
            ps_r = [P.psum("r_psr%d" % i, [128, 256]) for i in range(2)]
            ps_s = [P.psum("r_pss%d" % i, [128, 2, 128]) for i in range(2)]
            ps_o = [P.psum("r_pso%d" % i, [128, 128]) for i in range(2)]
            ps_k = P.psum("r_psk", [128, 128])
            ps_kv = P.psum("r_pskv", [128, 128])
            it = 0
            for d in range(2):
                for hp in range(2):
                    kb.MS(Sb[hp][:], 0.0, [Sb[hp]])
                for n in ORDER[d]:
                    cols = slice(n * 128, (n + 1) * 128)
                    b = it % 2; it += 1
                    qT, kT, cs = qTb[b], kTb[b], csb[b]
                    kb.LD(qT[:], kb.ZF[Z_RQ:Z_RQ + 256, cols].rearrange("(hp p) t -> p hp t", p=128), [qT])
                    kb.LD(kT[:], kb.ZF[Z_RK:Z_RK + 256, cols].rearrange("(hp p) t -> p hp t", p=128), [kT])
                    kb.LD(cs[:, 0, :], kb.ropec[:, cols], [cs])
                    kb.LD(cs[:, 1, :], kb.ropes[:, cols], [cs])
                    for h in range(4):
                        kb.LD(Vp[b][h][:, 64 * (h % 2):64 * (h % 2) + 64], kb.ZT[cols, 256 + 64 * h:256 + 64 * h + 64],
                              [Vp[b][h]])
                    for hp in range(2):
                        j = (it * 2 + hp) % 2
                        pr = ps_r[j]
                        kb.MM(pr[:, 0:128], kb.C("ROT"), qT[:, hp, :], True, True, [qT], [pr])
                        kb.MM(pr[:, 128:256], kb.C("ROT"), kT[:, hp, :], True, True, [kT], [pr])
                        for (src_, dst, off) in ((qT, qr[b], 0), (kT, kr[b], 128)):
                            kb.TT(t1[j][:], src_[:, hp, :], cs[:, 0, :], ALU.mult, [src_, cs], [t1[j]], eng="pool")
                            kb.TT(t2[j][:], pr[:, off:off + 128], cs[:, 1, :], ALU.mult, [pr, cs], [t2[j]])
                            kb.TT(dst[:, hp, :], t1[j][:], t2[j][:], ALU.add, [t1[j], t2[j]], [dst.s(hp)], eng="pool")
                        pss = ps_s[j]
                        for h2 in range(2):
                            kb.MM(pss[:, h2, :], kr[b][64 * h2:64 * h2 + 64, hp, :], qr[b][64 * h2:64 * h2 + 64, hp, :],
                                  True, True, [kr[b].s(hp), qr[b].s(hp)], [pss])
                        kb.TT(AT[j][:], pss[:], MK[d][:, 2 * hp:2 * hp + 2, :], ALU.mult, [pss, MK[d]], [AT[j]])
                        kb.TT(qd[j][:], qr[b][:, hp, :], QDEC[d][hp][:], ALU.mult, [qr[b].s(hp), QDEC[d][hp]], [qd[j]],
                              eng="pool")
                        po = ps_o[j]
                        kb.MM(po[:], Vp[b][2 * hp][:], AT[j][:, 0, :], True, False, [Vp[b][2 * hp], AT[j]], [po])
                        kb.MM(po[:], Vp[b][2 * hp + 1][:], AT[j][:, 1, :], False, False, [Vp[b][2 * hp + 1], AT[j]], [po])
                        kb.MM(po[:], Sb[hp][:], qd[j][:], False, True, [Sb[hp], qd[j]], [po])
                        oacc_write(kb, OACC, hp, n, po, d)
                        kb.TR(ps_k[:], kr[b][:, hp, :], kb.C("IDENT"), [kr[b].s(hp)], [ps_k])
                        for h2 in range(2):
                            h = 2 * hp + h2
                            kb.ACT(khp[b][h][:, 64 * h2:64 * h2 + 64], ps_k[:, 64 * h2:64 * h2 + 64], AF.Copy,
                                   [ps_k, KD], [khp[b][h]], scale=KD[:, 4 * d + h:4 * d + h + 1])
                        kb.MM(ps_kv[:], khp[b][2 * hp][:], Vp[b][2 * hp][:], True, False,
                              [khp[b][2 * hp], Vp[b][2 * hp]], [ps_kv])
                        kb.MM(ps_kv[:], khp[b][2 * hp + 1][:], Vp[b][2 * hp + 1][:], False, True,
                              [khp[b][2 * hp + 1], Vp[b][2 * hp + 1]], [ps_kv])
                        kb.STT(Sb[hp][:], Sb[hp][:], CV[:, 2 * d + hp:2 * d + hp + 1], ps_kv[:], ALU.mult, ALU.add,
                               [Sb[hp], CV, ps_kv], [Sb[hp]])
        with P.scope():
            finalize_gated(kb, OACC, Z_RG, None, 256, "r_")


def mixer_hgrn(kb, l):
    P = kb.P
    with P.scope():
        OACC = P.sbuf("h_oacc", [128, 2, S])
        with P.scope():
            LB = P.sbuf("h_LB", [128, 4]); OML = P.sbuf("h_OML", [128, 4])
            if l == 0:
                kb.MS(LB[:], 0.0, [LB]); kb.MS(OML[:], 1.0, [OML])
            else:
                lgt = P.sbuf("h_lgt", [128, 8])
                kb.LD(lgt[:], kb.prm["hgrn_lb_logits"][:].rearrange("l d (hp p) -> p (l d hp)", p=128), [lgt],
                      allow_slow_non_contiguous=True)
                kb.TT(LB[:], lgt[:, 4:8], lgt[:, 0:4], ALU.subtract, [lgt], [LB])
                kb.ACT(LB[:], LB[:], AF.Sigmoid, [LB], [LB])
                kb.TS(OML[:], LB[:], -1.0, 1.0, ALU.mult, ALU.add, [LB], [OML])
            G = P.sbuf("h_G", [128, 1])
            for hh in range(2):
                kb.LD(G[64 * hh:64 * hh + 64, :], kb.prm["hgrn_norm_g"][l].rearrange("(p o) -> p o", o=1), [G])
            kb.hgrn_gain = G
            hqb = [P.sbuf("h_q%d" % i, [128, 2, 128]) for i in range(2)]
            hfb = [P.sbuf("h_f%d" % i, [128, 2, 128]) for i in range(2)]
            Vp = [[P.sbuf("h_vp%d%d" % (i, h), [128, 128]) for h in range(4)] for i in range(2)]
            khp = [[P.sbuf("h_kh%d%d" % (i, h), [128, 128]) for h in range(4)] for i in range(2)]
            for i in range(2):
                for h in range(4):
                    kb.MS(Vp[i][h][:], 0.0, [Vp[i][h]], eng="pool")
                    kb.MS(khp[i][h][:], 0.0, [khp[i][h]], eng="pool")
            MREF = [[P.sbuf("h_mr%d%d" % (d, i), [128, 4]) for i in range(2)] for d in range(2)]
            for d in range(2):
                for i in range(2):
                    kb.MS(MREF[d][i][:], 0.0, [MREF[d][i]])

            def two(name, shape=(128, 128)):
                return [P.sbuf("h_%s%d" % (name, i), list(shape)) for i in range(2)]
            qs, sgm, ff, logf, kk, bb, pre = two("qs"), two("sg"), two("ff"), two("lf"), two("kk"), two("bb"), two("pre")
            e1, Ql, e2, Qd = two("e1"), two("Ql"), two("e2"), two("Qd")
            Kt = [two("Kt%d" % r) for r in range(4)]
            ex = two("ex")
            AT = two("AT", (128, 2, 128))
            KhT = two("KhT")
            bend = two("bend", (128, 2))
            Sb = [P.sbuf("h_S%d" % hp, [128, 128]) for hp in range(2)]
            ps_s = [P.psum("h_pss%d" % i, [128, 2, 128]) for i in range(2)]
            ps_o = [P.psum("h_pso%d" % i, [128, 128]) for i in range(2)]
            ps_k = [P.psum("h_psk%d" % i, [128, 128]) for i in range(2)]
            ps_kv = [P.psum("h_pskv%d" % i, [128, 128]) for i in range(2)]
            it = 0
            jj = 0
            for d in range(2):
                zf = Z_HFF if d == 0 else Z_HFB
                for hp in range(2):
                    kb.MS(Sb[hp][:], 0.0, [Sb[hp]])
                for n in ORDER[d]:
                    cols = slice(n * 128, (n + 1) * 128)
                    b = it % 2; it += 1
                    hq, hf = hqb[b], hfb[b]
                    kb.LD(hq[:], kb.ZF[Z_HQ:Z_HQ + 256, cols].rearrange("(hp p) t -> p hp t", p=128), [hq])
                    kb.LD(hf[:], kb.ZF[zf:zf + 256, cols].rearrange("(hp p) t -> p hp t", p=128), [hf])
                    for h in range(4):
                        kb.LD(Vp[b][h][:, 64 * (h % 2):64 * (h % 2) + 64], kb.ZT[cols, 64 * h:64 * h + 64], [Vp[b][h]])
                    for hp in range(2):
                        j = jj % 2; jj += 1
                        c = 2 * d + hp
                        mref = MREF[d][j]
                        kb.ACT(qs[j][:], hq[:, hp, :], AF.Silu, [hq], [qs[j]])
                        kb.ACT(sgm[j][:], hf[:, hp, :], AF.Sigmoid, [hf], [sgm[j]])
                        kb.TS(ff[j][:], sgm[j][:], OML[:, c:c + 1], LB[:, c:c + 1], ALU.mult, ALU.add, [sgm[j], OML, LB], [ff[j]])
                        kb.ACT(logf[j][:], ff[j][:], AF.Ln, [ff[j]], [logf[j]])
                        kb.TS(kk[j][:], ff[j][:], -1.0, 1.0, ALU.mult, ALU.add, [ff[j]], [kk[j]], eng="pool")
                        B = bb[j]
                        if d == 0:
                            kb.SCAN(B[:], kb.C("ONES"), logf[j][:], [logf[j]], [B])
                            kb.CP(mref[:, 1:4], B[:].rearrange("p (r c) -> p r c", c=32)[:, 0:3, 31], [B], [mref])
                            be = B[:, 127:128]
                        else:
                            kb.SCAN(pre[j][:], kb.C("ONES"), logf[j][:], [logf[j]], [pre[j]])
                            kb.STT(B[:], pre[j][:], -1.0, logf[j][:], ALU.mult, ALU.add, [pre[j], logf[j]], [B])
                            kb.TS(B[:], B[:], pre[j][:, 127:128], None, ALU.add, None, [B, pre[j]], [B])
                            kb.CP(mref[:, 0:3], B[:].rearrange("p (r c) -> p r c", c=32)[:, 1:4, 0], [B], [mref])
                            be = B[:, 0:1]
                        kb.TT(e1[j][:].rearrange("p (r c) -> p r c", c=32), B[:].rearrange("p (r c) -> p r c", c=32),
                              mref[:].unsqueeze(2).to_broadcast([128, 4, 32]), ALU.subtract, [B, mref], [e1[j]])
                        kb.ACT(e1[j][:], e1[j][:], AF.Exp, [e1[j]], [e1[j]])
                        kb.STT(Ql[j][:], qs[j][:], 0.125, e1[j][:], ALU.mult, ALU.mult, [qs[j], e1[j]], [Ql[j]], eng="pool")
                        kb.ACT(e2[j][:], B[:], AF.Exp, [B], [e2[j]])
                        kb.STT(Qd[j][:], qs[j][:], 0.125, e2[j][:], ALU.mult, ALU.mult, [qs[j], e2[j]], [Qd[j]], eng="pool")
                        pss = ps_s[j]
                        for r in range(4):
                            kb.ACT(ex[j][:], B[:], AF.Exp, [B, mref], [ex[j]], scale=-1.0, bias=mref[:, r:r + 1])
                            kb.STT(Kt[r][j][:], ex[j][:], 1e26, kk[j][:], ALU.min, ALU.mult, [ex[j], kk[j]], [Kt[r][j]])
                            for h2 in range(2):
                                kb.MM(pss[:, h2, 32 * r:32 * r + 32], Kt[r][j][64 * h2:64 * h2 + 64, :],
                                      Ql[j][64 * h2:64 * h2 + 64, 32 * r:32 * r + 32], True, True,
                                      [Kt[r][j], Ql[j]], [pss])
                        kb.TT(AT[j][:], pss[:], kb.C("TRIF" if d == 0 else "TRIB").unsqueeze(1).to_broadcast([128, 2, 128]),
                              ALU.mult, [pss], [AT[j]])
                        po = ps_o[j]
                        kb.MM(po[:], Vp[b][2 * hp][:], AT[j][:, 0, :], True, False, [Vp[b][2 * hp], AT[j]], [po])
                        kb.MM(po[:], Vp[b][2 * hp + 1][:], AT[j][:, 1, :], False, False, [Vp[b][2 * hp + 1], AT[j]], [po])
                        kb.MM(po[:], Sb[hp][:], Qd[j][:], False, True, [Sb[hp], Qd[j]], [po])
                        oacc_write(kb, OACC, hp, n, po, d)
                        kb.CP(bend[j][:, 0:1], be, [B], [bend[j]])
                        kb.ACT(KhT[j][:], B[:], AF.Exp, [B, bend[j]], [KhT[j]], scale=-1.0, bias=bend[j][:, 0:1])
                        kb.TT(KhT[j][:], KhT[j][:], kk[j][:], ALU.mult, [KhT[j], kk[j]], [KhT[j]], eng="pool")
                        kb.ACT(bend[j][:, 1:2], bend[j][:, 0:1], AF.Exp, [bend[j]], [bend[j]])
                        pk = ps_k[j]
                        kb.TR(pk[:], KhT[j][:], kb.C("IDENT"), [KhT[j]], [pk])
                        for h2 in range(2):
                            h = 2 * hp + h2
                            kb.CP(khp[b][h][:, 64 * h2:64 * h2 + 64], pk[:, 64 * h2:64 * h2 + 64], [pk], [khp[b][h]],
                                  eng=("act" if h2 else "dve"))
                        pkv = ps_kv[j]
                        kb.MM(pkv[:], khp[b][2 * hp][:], Vp[b][2 * hp][:], True, False, [khp[b][2 * hp], Vp[b][2 * hp]], [pkv])
                        kb.MM(pkv[:], khp[b][2 * hp + 1][:], Vp[b][2 * hp + 1][:], False, True,
                              [khp[b][2 * hp + 1], Vp[b][2 * hp + 1]], [pkv])
                        kb.STT(Sb[hp][:], Sb[hp][:], bend[j][:, 1:2], pkv[:], ALU.mult, ALU.add,
                               [Sb[hp], bend[j], pkv], [Sb[hp]])
        with P.scope():
            G = P.sbuf("h_G2", [128, 1])
            for hh in range(2):
                kb.LD(G[64 * hh:64 * hh + 64, :], kb.prm["hgrn_norm_g"][l].rearrange("(p o) -> p o", o=1), [G])
            finalize_gated(kb, OACC, Z_HG, G, 0, "h_")


PI = float(np.pi)


def _sincos(kb, ang, sin_out, cos_out, R, tmp, shape=None):
    P = kb.P
    shp = list(ang.shape)
    with P.scope():
        ki = P.sbuf("sc_ki", shp, mybir.dt.int32)
        kf = P.sbuf("sc_kf", shp)
        r = P.sbuf("sc_r", shp)
        m = P.sbuf("sc_m", shp)
        C1 = 6.28125
        C2 = 2 * PI - C1
        for (shift, out) in ((0.0, sin_out), (PI / 2, cos_out)):
            kb.TS(r[:], ang, shift, None, ALU.add, None, R, [r])
            kb.TS(kf[:], r[:], 1.0 / (2 * PI), None, ALU.mult, None, [r], [kf])
            kb.CP(ki[:], kf[:], [kf], [ki])
            kb.CP(kf[:], ki[:], [ki], [kf])
            kb.STT(r[:], kf[:], -C1, r[:], ALU.mult, ALU.add, [kf, r], [r])
            kb.STT(r[:], kf[:], -C2, r[:], ALU.mult, ALU.add, [kf, r], [r])
            kb.TS(m[:], r[:], PI, 2 * PI, ALU.is_gt, ALU.mult, [r], [m])
            kb.TT(r[:], r[:], m[:], ALU.subtract, [r, m], [r])
            kb.TS(m[:], r[:], -PI, 2 * PI, ALU.is_lt, ALU.mult, [r], [m])
            kb.TT(r[:], r[:], m[:], ALU.add, [r, m], [r])
            kb.ACT(out, r[:], AF.Sin, [r], R)


def mixer_s5(kb, l):
    P = kb.P
    prm = kb.prm
    with P.scope():
        OACC = P.sbuf("s_oacc", [128, 2, S])
        with P.scope():
            WX = P.sbuf("s_WX", [128, 2, 8, 2, 64])
            Cblk = P.sbuf("s_Cblk", [128, 16, 128])
            kb.MS(WX[:], 0.0, [WX], eng="pool")
            kb.MS(Cblk[:], 0.0, [Cblk], eng="pool")
            for g8 in range(8):
                for ri, nm in enumerate(("s5_b_re", "s5_b_im")):
                    for gg in range(2):
                        src = prm[nm][l][8 * gg + g8].rearrange("p c -> c p")
                        kb.LD(WX[16 * g8:16 * g8 + 16, gg, g8, ri, :], src, [WX], allow_slow_non_contiguous=True)
            for g in range(16):
                g8 = g % 8
                kb.LD(Cblk[0:64, g, 16 * g8:16 * g8 + 16], prm["s5_c_re"][l][g].rearrange("c p -> p c"), [Cblk],
                      allow_slow_non_contiguous=True)
                kb.LD(Cblk[64:128, g, 16 * g8:16 * g8 + 16], prm["s5_c_im"][l][g].rearrange("c p -> p c"), [Cblk],
                      allow_slow_non_contiguous=True)
            kb.TS(Cblk[64:128, :, :], Cblk[64:128, :, :], -1.0, None, ALU.mult, None, [Cblk], [Cblk])
            Cb16 = P.sbuf("s_Cb16", [128, 16, 128], BF16)
            kb.CP(Cb16[:], Cblk[:], [Cblk], [Cb16])
            VFr = P.sbuf("s_VFr", [128, 16, 64]); VFi = P.sbuf("s_VFi", [128, 16, 64])
            T1 = P.sbuf("s_T1", [128, 16, 128]); T2 = P.sbuf("s_T2", [128, 16, 128])
            AR = P.sbuf("s_AR", [128, 16]); NAI = P.sbuf("s_NAI", [128, 16])
            for d in range(2):
                with P.scope():
                    lr = P.sbuf("s_lr", [128, 16, 64]); li = P.sbuf("s_li", [128, 16, 64]); dtb = P.sbuf("s_dt", [128, 16])
                    kb.LD(lr[:], prm["s5_lam_re"][l][d].rearrange("g p -> (g p)").partition_broadcast(128), [lr])
                    kb.LD(li[:], prm["s5_lam_im"][l][d].rearrange("g p -> (g p)").partition_broadcast(128), [li])
                    kb.LD(dtb[:], prm["s5_log_dt"][l][d].partition_broadcast(128), [dtb])
                    kb.ACT(dtb[:], dtb[:], AF.Exp, [dtb], [dtb])
                    dt_bc = dtb[:].unsqueeze(2).to_broadcast([128, 16, 64])
                    lrdt = P.sbuf("s_lrdt", [128, 16, 64]); lidt = P.sbuf("s_lidt", [128, 16, 64])
                    kb.TT(lrdt[:], lr[:], dt_bc, ALU.mult, [lr, dtb], [lrdt])
                    kb.TT(lidt[:], li[:], dt_bc, ALU.mult, [li, dtb], [lidt])
                    a = [P.sbuf("s_a%d" % i, [128, 16, 64]) for i in range(8)]
                    mag, ang, sn, cs, tmp, ar, ai, t2 = a
                    kb.ACT(mag[:], lrdt[:], AF.Exp, [lrdt], [mag])
                    _sincos(kb, lidt[:], sn[:], cs[:], [lidt, sn, cs, tmp], tmp[:])
                    kb.TT(ar[:], mag[:], cs[:], ALU.mult, [mag, cs], [ar])
                    kb.TT(ai[:], mag[:], sn[:], ALU.mult, [mag, sn], [ai])
                    den = P.sbuf("s_den", [128, 16, 64]); fr = P.sbuf("s_fr", [128, 16, 64]); fi = P.sbuf("s_fi", [128, 16, 64])
                    kb.TT(den[:], lr[:], lr[:], ALU.mult, [lr], [den])
                    kb.TT(t2[:], li[:], li[:], ALU.mult, [li], [t2])
                    kb.TT(den[:], den[:], t2[:], ALU.add, [den, t2], [den])
                    kb.RECIP(den[:], den[:], [den], [den])
                    kb.TS(ar[:], ar[:], -1.0, None, ALU.add, None, [ar], [ar])
                    kb.TT(fr[:], ar[:], lr[:], ALU.mult, [ar, lr], [fr])
                    kb.TT(t2[:], ai[:], li[:], ALU.mult, [ai, li], [t2])
                    kb.TT(fr[:], fr[:], t2[:], ALU.add, [fr, t2], [fr])
                    kb.TT(fr[:], fr[:], den[:], ALU.mult, [fr, den], [fr])
                    kb.TT(fi[:], ai[:], lr[:], ALU.mult, [ai, lr], [fi])
                    kb.TT(t2[:], ar[:], li[:], ALU.mult, [ar, li], [t2])
                    kb.TT(fi[:], fi[:], t2[:], ALU.subtract, [fi, t2], [fi])
                    kb.TT(fi[:], fi[:], den[:], ALU.mult, [fi, den], [fi])
                    jcol = kb.C("CCOL")[:, 2:3] if d == 0 else kb.C("CCOL")[:, 3:4]
                    njcol = kb.C("CCOL")[:, 6:7] if d == 0 else kb.C("CCOL")[:, 7:8]
                    kb.ACT(mag[:], lrdt[:], AF.Exp, [lrdt], [mag], scale=njcol)
                    kb.TS(ang[:], lidt[:], jcol, None, ALU.mult, None, [lidt], [ang])
                    _sincos(kb, ang[:], sn[:], cs[:], [ang, sn, cs, tmp], tmp[:])
                    vr, vi = ar, ai
                    kb.TT(vr[:], mag[:], cs[:], ALU.mult, [mag, cs], [vr])
                    kb.TT(vi[:], mag[:], sn[:], ALU.mult, [mag, sn], [vi])
                    kb.TS(vi[:], vi[:], -1.0, None, ALU.mult, None, [vi], [vi])
                    kb.TT(VFr[:], vr[:], fr[:], ALU.mult, [vr, fr], [VFr])
                    kb.TT(t2[:], vi[:], fi[:], ALU.mult, [vi, fi], [t2])
                    kb.TT(VFr[:], VFr[:], t2[:], ALU.subtract, [VFr, t2], [VFr])
                    kb.TT(VFi[:], vr[:], fi[:], ALU.mult, [vr, fi], [VFi])
                    kb.TT(t2[:], vi[:], fr[:], ALU.mult, [vi, fr], [t2])
                    kb.TT(VFi[:], VFi[:], t2[:], ALU.add, [VFi, t2], [VFi])
                with P.scope():
                    dtb = P.sbuf("s_dt2", [128, 16])
                    kb.LD(dtb[:], prm["s5_log_dt"][l][d].partition_broadcast(128), [dtb])
                    kb.ACT(dtb[:], dtb[:], AF.Exp, [dtb], [dtb])
                    lrp = P.sbuf("s_lrp", [128, 16]); lip = P.sbuf("s_lip", [128, 16])
                    for hh in range(2):
                        kb.LD(lrp[64 * hh:64 * hh + 64, :], prm["s5_lam_re"][l][d].rearrange("g p -> p g"), [lrp],
                              allow_slow_non_contiguous=True)
                        kb.LD(lip[64 * hh:64 * hh + 64, :], prm["s5_lam_im"][l][d].rearrange("g p -> p g"), [lip],
                              allow_slow_non_contiguous=True)
                    kb.TT(lrp[:], lrp[:], dtb[:], ALU.mult, [lrp, dtb], [lrp])
                    kb.TT(lip[:], lip[:], dtb[:], ALU.mult, [lip, dtb], [lip])
                    b4 = [P.sbuf("s_b%d" % i, [128, 16, 128]) for i in range(4)]
                    arg, sn2, cs2, tmp2 = b4
                    mt = kb.C("IOTAF" if d == 0 else "R127F")
                    mt_bc = mt.unsqueeze(1).to_broadcast([128, 16, 128])
                    kb.TT(arg[:], lrp[:].unsqueeze(2).to_broadcast([128, 16, 128]), mt_bc, ALU.mult, [lrp], [arg])
                    kb.ACT(T1[:], arg[:], AF.Exp, [arg], [T1])
                    kb.TT(arg[:], lip[:].unsqueeze(2).to_broadcast([128, 16, 128]), mt_bc, ALU.mult, [lip, T1], [arg])
                    _sincos(kb, arg[:], sn2[:], cs2[:], [arg, sn2, cs2, tmp2], tmp2[:])
                    kb.TT(T2[:], T1[:], sn2[:], ALU.mult, [T1, sn2], [T2])
                    kb.TS(T2[:], T2[:], -1.0, None, ALU.mult, None, [T2], [T2])
                    kb.TT(T1[:], T1[:], cs2[:], ALU.mult, [T1, cs2], [T1])
                    c4 = [P.sbuf("s_c%d" % i, [128, 16]) for i in range(4)]
                    kb.ACT(c4[0][:], lrp[:], AF.Exp, [lrp], [c4[0]])
                    _sincos(kb, lip[:], c4[1][:], c4[2][:], [lip, c4[1], c4[2], c4[3]], c4[3][:])
                    kb.TT(AR[:], c4[0][:], c4[2][:], ALU.mult, [c4[0], c4[2]], [AR])
                    kb.TT(NAI[:], c4[0][:], c4[1][:], ALU.mult, [c4[0], c4[1]], [NAI])
                    kb.TS(NAI[:], NAI[:], -1.0, None, ALU.mult, None, [NAI], [NAI])
                sweep_scope = P.scope(); sweep_scope.__enter__()
                uTb = [P.sbuf("s_u%d" % i, [128, 2, 128]) for i in range(2)]
                mm_ = [P.sbuf("s_m%d" % i, [128, 8, 64]) for i in range(4)]
                W3 = [P.sbuf("s_W3%d" % i, [128, 8, 3, 64], BF16) for i in range(2)]
                Hb = [P.sbuf("s_Hb%d" % i, [128, 8, 128], BF16) for i in range(2)]
                tri16 = P.sbuf("s_tri16", [128, 128], BF16)
                kb.CP(tri16[:], kb.C("TRIF" if d == 0 else "TRIB"), [], [tri16])
                tP = [P.sbuf("s_tP%d" % i, [128, 8, 128]) for i in range(2)]
                tPs = [P.sbuf("s_tPs%d" % i, [128, 8, 128]) for i in range(2)]
                H1 = [P.sbuf("s_H1%d" % i, [128, 8, 128]) for i in range(2)]
                H2 = [P.sbuf("s_H2%d" % i, [128, 8, 128]) for i in range(2)]
                hend = P.sbuf("s_hend", [128, 16]); hsend = P.sbuf("s_hsend", [128, 16])
                hp_ = P.sbuf("s_hp", [128, 16]); hps_ = P.sbuf("s_hps", [128, 16])
                sm = [P.sbuf("s_sm%d" % i, [128, 16]) for i in range(4)]
                xps = P.psum("s_xps", [128, 1024])
                pps = P.psum("s_pps", [128, 8, 128])
                ppss = P.psum("s_ppss", [128, 8, 128])
                yps = [P.psum("s_yps%d" % i, [128, 128]) for i in range(2)]
                kb.MS(hp_[:], 0.0, [hp_]); kb.MS(hps_[:], 0.0, [hps_])
                te = 127 if d == 0 else 0
                tri = kb.C("TRIF" if d == 0 else "TRIB")
                it = 0
                for n in ORDER[d]:
                    cols = slice(n * 128, (n + 1) * 128)
                    uT = uTb[it % 2]; it += 1
                    kb.LD(uT[:], kb.ZF[Z_SU:Z_SU + 256, cols].rearrange("(gg p) t -> p gg t", p=128), [uT])
                    for gg in range(2):
                        j = gg
                        for half in range(2):
                            kb.MM(xps[:, half * 512:(half + 1) * 512], uT[:, gg, :],
                                  WX[:, gg, half * 4:(half + 1) * 4, :, :].rearrange("q a r p -> q (a r p)"),
                                  True, True, [uT, WX], [xps])
                        xv = xps[:].rearrange("t (g r p) -> t g r p", r=2, p=64)
                        gs = slice(gg * 8, gg * 8 + 8)
                        kb.TT(mm_[0][:], xv[:, :, 0, :], VFr[:, gs, :], ALU.mult, [xps, VFr], [mm_[0]])
                        kb.TT(mm_[1][:], xv[:, :, 1, :], VFi[:, gs, :], ALU.mult, [xps, VFi], [mm_[1]])
                        kb.TT(mm_[2][:], xv[:, :, 0, :], VFi[:, gs, :], ALU.mult, [xps, VFi], [mm_[2]])
                        kb.TT(mm_[3][:], xv[:, :, 1, :], VFr[:, gs, :], ALU.mult, [xps, VFr], [mm_[3]])
                        w3 = W3[j]
                        kb.TT(w3[:, :, 0, :], mm_[0][:], mm_[1][:], ALU.subtract, [mm_[0], mm_[1]], [w3], eng="pool")
                        kb.TT(w3[:, :, 1, :], mm_[2][:], mm_[3][:], ALU.add, [mm_[2], mm_[3]], [w3], eng="pool")
                        kb.TT(w3[:, :, 2, :], mm_[1][:], mm_[0][:], ALU.subtract, [mm_[0], mm_[1]], [w3], eng="pool")
                        for g8 in range(8):
                            kb.MM(pps[:, g8, :], w3[:, g8, 0:2, :].rearrange("q r p -> q (r p)"), tri16[:], True, True, [w3, tri16], [pps])
                            kb.MM(ppss[:, g8, :], w3[:, g8, 1:3, :].rearrange("q r p -> q (r p)"), tri16[:], True, True, [w3, tri16], [ppss])
                        kb.TT(tP[j][:], pps[:], hp_[:, gs].unsqueeze(2).to_broadcast([128, 8, 128]), ALU.add, [pps, hp_], [tP[j]])
                        kb.TT(tPs[j][:], ppss[:], hps_[:, gs].unsqueeze(2).to_broadcast([128, 8, 128]), ALU.add,
                              [ppss, hps_], [tPs[j]])
                        kb.TT(H1[j][:], tP[j][:], T1[:, gs, :], ALU.mult, [tP[j], T1], [H1[j]], eng="pool")
                        kb.TT(H2[j][:], tPs[j][:], T2[:, gs, :], ALU.mult, [tPs[j], T2], [H2[j]])
                        kb.TT(Hb[j][:], H1[j][:], H2[j][:], ALU.add, [H1[j], H2[j]], [Hb[j]], eng="pool")
                        yp = yps[gg]
                        for g8 in range(8):
                            kb.MM(yp[:], Cb16[:, gg * 8 + g8, :], Hb[j][:, g8, :], g8 == 0, g8 == 7, [Cb16, Hb[j]], [yp])
                        oacc_write(kb, OACC, gg, n, yp, d)
                        kb.TT(hend[:, gs], H1[j][:, :, te], H2[j][:, :, te], ALU.add, [H1[j], H2[j]], [hend])
                        kb.TT(sm[0][:, 0:8], tPs[j][:, :, te], T1[:, gs, te], ALU.mult, [tPs[j], T1], [sm[0]])
                        kb.TT(sm[1][:, 0:8], tP[j][:, :, te], T2[:, gs, te], ALU.mult, [tP[j], T2], [sm[1]])
                        kb.TT(hsend[:, gs], sm[0][:, 0:8], sm[1][:, 0:8], ALU.subtract, [sm[0], sm[1]], [hsend])
                    kb.TT(sm[0][:], hend[:], AR[:], ALU.mult, [hend, AR], [sm[0]])
                    kb.TT(sm[1][:], hsend[:], NAI[:], ALU.mult, [hsend, NAI], [sm[1]])
                    kb.TT(sm[2][:], hsend[:], AR[:], ALU.mult, [hsend, AR], [sm[2]])
                    kb.TT(sm[3][:], hend[:], NAI[:], ALU.mult, [hend, NAI], [sm[3]])
                    kb.TT(hp_[:], sm[0][:], sm[1][:], ALU.add, [sm[0], sm[1]], [hp_])
                    kb.TT(hps_[:], sm[2][:], sm[3][:], ALU.subtract, [sm[2], sm[3]], [hps_])
                sweep_scope.__exit__(None, None, None)
        with P.scope():
            dsk = P.sbuf("s_dsk", [128, 2]); glb = P.sbuf("s_glb", [128, 2])
            kb.LD(dsk[:], prm["s5_d"][l].rearrange("(gg p) -> p gg", p=128), [dsk], allow_slow_non_contiguous=True)
            kb.LD(glb[:], prm["s5_glu_b"][l].rearrange("(gg p) -> p gg", p=128), [glb], allow_slow_non_contiguous=True)
            gw = P.sbuf("s_gw", [128, 2, 256])
            kb.LD(gw[:], prm["s5_glu_w"][l].rearrange("(ct p) o -> p ct o", p=128), [gw])
            uTb = [P.sbuf("s_fu%d" % i, [128, 2, 128]) for i in range(2)]
            yy = [P.sbuf("s_yy%d" % i, [128, 2, 128]) for i in range(2)]
            x2 = [P.sbuf("s_x2%d" % i, [128, 2, 128]) for i in range(2)]
            th = [P.sbuf("s_th%d" % i, [128, 2, 128]) for i in range(2)]
            sgb = [P.sbuf("s_sg%d" % i, [128, 128]) for i in range(2)]
            ob = [P.sbuf("s_ob%d" % i, [128, 128]) for i in range(2)]
            psz = [P.psum("s_psz%d" % i, [128, 128]) for i in range(2)]
            k = 0
            for n in range(NT):
                cols = slice(n * 128, (n + 1) * 128)
                i = n % 2
                kb.LD(uTb[i][:], kb.ZF[Z_SU:Z_SU + 256, cols].rearrange("(gg p) t -> p gg t", p=128), [uTb[i]])
                for gg in range(2):
                    kb.STT(yy[i][:, gg, :], uTb[i][:, gg, :], dsk[:, gg:gg + 1], OACC[:, gg, cols], ALU.mult, ALU.add,
                           [uTb[i], dsk, OACC.s(n)], [yy[i]])
                kb.TT(x2[i][:], yy[i][:], yy[i][:], ALU.mult, [yy[i]], [x2[i]], eng="pool")
                kb.TS(x2[i][:], x2[i][:], 0.044715, 1.0, ALU.mult, ALU.add, [x2[i]], [x2[i]])
                kb.TT(x2[i][:], x2[i][:], yy[i][:], ALU.mult, [x2[i], yy[i]], [x2[i]], eng="pool")
                kb.ACT(th[i][:], x2[i][:], AF.Tanh, [x2[i]], [th[i]], scale=0.7978845608028654)
                kb.TS(th[i][:], th[i][:], 1.0, 0.5, ALU.add, ALU.mult, [th[i]], [th[i]])
                kb.TT(yy[i][:], yy[i][:], th[i][:], ALU.mult, [yy[i], th[i]], [yy[i]], eng="pool")
                for ot in range(2):
                    q = k % 2; k += 1
                    for ct in range(2):
                        kb.MM(psz[q][:], gw[:, ct, ot * 128:(ot + 1) * 128], yy[i][:, ct, :], ct == 0, ct == 1, [gw, yy[i]], [psz[q]])
                    kb.ACT(sgb[q][:], psz[q][:], AF.Sigmoid, [psz[q], glb], [sgb[q]], bias=glb[:, ot:ot + 1])
                    kb.TT(ob[q][:], yy[i][:, ot, :], sgb[q][:], ALU.mult, [yy[i], sgb[q]], [ob[q]])
                    kb.ST(kb.YC[768 + ot * 128:768 + (ot + 1) * 128, cols], ob[q][:], [ob[q]])


def gdn_conv(kb, l):
    P = kb.P
    with P.scope():
        CW = P.sbuf("g_cw", [128, 6, 9])
        for kh in range(3):
            for kw in range(3):
                kb.LD(CW[:, :, kh * 3 + kw], kb.prm["gdn_conv_w"][l][kh, kw].rearrange("(ct p) -> p ct", p=128), [CW],
                      allow_slow_non_contiguous=True)
        mlat = P.sbuf("g_mlat", [128, 2, 512]); mctx = P.sbuf("g_mctx", [128, 2, 256])
        kb.LD(mlat[:], kb.cmlat[:], [mlat]); kb.LD(mctx[:], kb.cmctx[:], [mctx])
        Wb = [P.sbuf("g_w%d" % i, [128, 642]) for i in range(2)]
        acc = [[P.sbuf("g_acc%d%d" % (i, j), [128, 512]) for j in range(3)] for i in range(2)]
        sl = [P.sbuf("g_sl%d" % i, [128, 512]) for i in range(2)]
        sq = [P.sbuf("g_sq%d" % i, [128, 512]) for i in range(2)]
        rt = [P.sbuf("g_rt%d" % i, [128, 512]) for i in range(2)]
        ps = [P.psum("g_psn%d" % i, [128, 512]) for i in range(2)]
        spans = [(0, 256, True)] + [(256 + 512 * k, 512, False) for k in range(8)]
        it = 0
        for (t0, L, is_ctx) in spans:
            lo = 0 if is_ctx else 256
            hi = 256 if is_ctx else S
            a = max(lo, t0 - 65); b = min(hi, t0 + L + 65)
            for ct in range(6):
                i = it % 2; it += 1
                W = Wb[i]
                kb.MS(W[:], 0.0, [W], eng="pool")
                kb.LD(W[:, 65 + (a - t0):65 + (b - t0)], kb.ZF[Z_GQKV + ct * 128:Z_GQKV + (ct + 1) * 128, a:b], [W])
                rows = (1,) if is_ctx else (0, 1, 2)
                masks = mctx if is_ctx else mlat
                for dwi, shift in enumerate((-1, 0, 1)):
                    A = acc[i][dwi]
                    eng = "dve"
                    for q, dh in enumerate(rows):
                        o0 = 65 + 64 * (dh - 1) + shift
                        src = W[:, o0:o0 + L]
                        wcol = CW[:, ct, dh * 3 + dwi:dh * 3 + dwi + 1]
                        if q == 0:
                            kb.TS(A[:, :L], src, wcol, None, ALU.mult, None, [W, CW], [A], eng=("pool" if dwi != 1 else "dve"))
                        else:
                            kb.STT(A[:, :L], src, wcol, A[:, :L], ALU.mult, ALU.add, [W, CW, A], [A])
                    if dwi != 1:
                        mi = 0 if dwi == 0 else 1
                        kb.TT(A[:, :L], A[:, :L], masks[:, mi, :L], ALU.mult, [A, masks], [A], eng="pool")
                A0, A1, A2 = acc[i]
                kb.TT(A1[:, :L], A1[:, :L], A0[:, :L], ALU.add, [A0, A1], [A1], eng="pool")
                kb.TT(A1[:, :L], A1[:, :L], A2[:, :L], ALU.add, [A1, A2], [A1], eng="pool")
                kb.ACT(sl[i][:, :L], A1[:, :L], AF.Silu, [A1], [sl[i]])
                if ct < 4:
                    kb.TT(sq[i][:, :L], sl[i][:, :L], sl[i][:, :L], ALU.mult, [sl[i]], [sq[i]], eng="pool")
                    kb.MM(ps[i][:, :L], kb.C("BLK64"), sq[i][:, :L], True, True, [sq[i]], [ps[i]])
                    kb.ACT(rt[i][:, :L], ps[i][:, :L], AF.Sqrt, [ps[i]], [rt[i]], bias=kb.C("CCOL")[:, 0:1])
                    kb.RECIP(rt[i][:, :L], rt[i][:, :L], [rt[i]], [rt[i]])
                    if ct < 2:
                        kb.STT(sl[i][:, :L], sl[i][:, :L], 0.125, rt[i][:, :L], ALU.mult, ALU.mult, [sl[i], rt[i]], [sl[i]])
                    else:
                        kb.TT(sl[i][:, :L], sl[i][:, :L], rt[i][:, :L], ALU.mult, [sl[i], rt[i]], [sl[i]])
                kb.ST(kb.QKVF[ct * 128:(ct + 1) * 128, t0:t0 + L], sl[i][:, :L], [sl[i]])


def mixer_gdn(kb, l):
    P = kb.P
    gdn_conv(kb, l)
    upto = kb.cfg.get("gdn_upto", 99)
    if upto < 1:
        return
    with P.scope():
        OACC = P.sbuf("g_oacc", [128, 2, S])
        with P.scope():
            DTB = P.sbuf("g_dtb", [128, 8]); NEGA = P.sbuf("g_nega", [128, 8])
            kb.LD(DTB[:], kb.prm["gdn_dt_bias"][l].rearrange("d h -> (d h)").partition_broadcast(128), [DTB])
            kb.LD(NEGA[:], kb.prm["gdn_a_log"][l].rearrange("d h -> (d h)").partition_broadcast(128), [NEGA])
            kb.ACT(NEGA[:], NEGA[:], AF.Exp, [NEGA], [NEGA])
            kb.TS(NEGA[:], NEGA[:], -1.0, None, ALU.mult, None, [NEGA], [NEGA])
            qnb = [P.sbuf("g_q%d" % i, [128, 2, 128]) for i in range(2)]
            knb = [P.sbuf("g_k%d" % i, [128, 2, 128]) for i in range(2)]
            vvb = [P.sbuf("g_v%d" % i, [128, 2, 128]) for i in range(2)]
            gabb = [P.sbuf("g_gab%d" % i, [128, 16]) for i in range(2)]

            def sm4(name, w=4):
                return P.sbuf("g_" + name, [128, w])
            xa, ea, loga, beta, lnb = sm4("xa"), sm4("ea"), sm4("loga"), sm4("beta"), sm4("lnb")
            gtm, ngt, ekr, cdec, eg, beg, gpl = sm4("gtm"), sm4("ngt"), sm4("ekr"), sm4("cdec"), sm4("eg"), sm4("beg"), sm4("gpl")
            ROWS = P.sbuf("g_rows", [4, 384])
            LI = P.sbuf("g_LI", [128, 4, 128]); LBT = P.sbuf("g_LBT", [128, 4, 128]); LBm = P.sbuf("g_LB", [128, 4, 128])
            NAT = P.sbuf("g_NAT", [128, 4, 128]); NA = P.sbuf("g_NA", [128, 4, 128]); QKm = P.sbuf("g_QKm", [128, 4, 128])
            Tm = P.sbuf("g_Tm", [128, 4, 128]); Wm = P.sbuf("g_Wm", [128, 4, 128])
            x1 = P.sbuf("g_x1", [128, 4, 128]); y1 = P.sbuf("g_y1", [128, 4, 128])
            tmx = P.sbuf("g_tmx", [128, 4, 128]); tmy = P.sbuf("g_tmy", [128, 4, 128])
            Rm = [P.sbuf("g_R%d" % h, [128, 128]) for h in range(4)]
            khp = [P.sbuf("g_kh%d" % h, [128, 128]) for h in range(4)]
            vnp = [P.sbuf("g_vn%d" % h, [128, 128]) for h in range(4)]
            for h in range(4):
                kb.MS(khp[h][:], 0.0, [khp[h]], eng="pool")
                kb.MS(vnp[h][:], 0.0, [vnp[h]], eng="pool")
            upair = [P.sbuf("g_up%d" % hp, [128, 128]) for hp in range(2)]
            wTp = [P.sbuf("g_wT%d" % hp, [128, 128]) for hp in range(2)]
            EG = [P.sbuf("g_EG%d" % hp, [128, 128]) for hp in range(2)]
            qd = [P.sbuf("g_qd%d" % hp, [128, 128]) for hp in range(2)]
            cdp = [P.sbuf("g_cdp%d" % hp, [128, 1]) for hp in range(2)]
            Sb = [P.sbuf("g_S%d" % hp, [128, 128]) for hp in range(2)]
            B = [P.psum("g_B%d" % i, [128, 512]) for i in range(8)]
            ident = kb.C("IDENT")
            it = 0
            for d in range(2):
                tri = kb.C("TRIF" if d == 0 else "TRIB")
                rem = kb.C("SUFF" if d == 0 else "PREB")
                n_incl = kb.C("NLE" if d == 0 else "NGE")
                n_strT = kb.C("NLT" if d == 0 else "NGT")
                n_str = kb.C("NGT" if d == 0 else "NLT")
                for hp in range(2):
                    kb.MS(Sb[hp][:], 0.0, [Sb[hp]])
                for n in ORDER[d][:kb.cfg.get("ntiles", NT)]:
                    cols = slice(n * 128, (n + 1) * 128)
                    b = it % 2; it += 1
                    qn, kn, vv, gab = qnb[b], knb[b], vvb[b], gabb[b]
                    kb.LD(qn[:], kb.QKVF[0:256, cols].rearrange("(hp p) t -> p hp t", p=128), [qn])
                    kb.LD(kn[:], kb.QKVF[256:512, cols].rearrange("(hp p) t -> p hp t", p=128), [kn])
                    kb.LD(vv[:], kb.QKVF[512:768, cols].rearrange("(hp p) t -> p hp t", p=128), [vv])
                    kb.LD(gab[:], kb.ZT[cols, 512:528], [gab])
                    kb.TT(xa[:], gab[:, 4 * d:4 * d + 4], DTB[:, 4 * d:4 * d + 4], ALU.add, [gab, DTB], [xa])
                    kb.ACT(ea[:], xa[:], AF.Exp, [xa], [ea])
                    kb.ACT(ea[:], ea[:], AF.Ln, [ea], [ea], bias=kb.C("CCOL")[:, 1:2])
                    kb.TT(loga[:], ea[:], NEGA[:, 4 * d:4 * d + 4], ALU.mult, [ea, NEGA], [loga])
                    kb.ACT(beta[:], gab[:, 8 + 4 * d:12 + 4 * d], AF.Sigmoid, [gab], [beta])
                    kb.ACT(lnb[:], beta[:], AF.Ln, [beta], [lnb])
                    kb.MM(B[0][:, 0:4], tri, loga[:], True, True, [loga], [B[0]])
                    kb.MM(B[0][:, 4:8], rem, loga[:], True, True, [loga], [B[0]])
                    kb.MM(B[0][:, 8:12], kb.C("ONES"), loga[:], True, True, [loga], [B[0]])
                    kb.CP(gtm[:], B[0][:, 0:4], [B[0]], [gtm])
                    kb.TS(ngt[:], B[0][:, 0:4], -1.0, None, ALU.mult, None, [B[0]], [ngt])
                    kb.ACT(ekr[:], B[0][:, 4:8], AF.Exp, [B[0]], [ekr])
                    kb.ACT(cdec[:], B[0][:, 8:12], AF.Exp, [B[0]], [cdec])
                    kb.ACT(eg[:], gtm[:], AF.Exp, [gtm], [eg])
                    kb.TT(beg[:], beta[:], eg[:], ALU.mult, [beta, eg], [beg])
                    kb.TT(gpl[:], gtm[:], lnb[:], ALU.add, [gtm, lnb], [gpl])
                    kb.MM(B[1][0:4, 0:128], loga[:], tri, True, True, [loga], [B[1]])
                    kb.MM(B[1][0:4, 128:256], loga[:], tri, True, False, [loga], [B[1]])
                    kb.MM(B[1][0:4, 128:256], lnb[:], ident, False, True, [lnb], [B[1]])
                    kb.CP(ROWS[:, 0:256], B[1][0:4, 0:256], [B[1]], [ROWS])
                    kb.TS(ROWS[:, 256:384], B[1][0:4, 0:128], -1.0, None, ALU.mult, None, [B[1]], [ROWS])
                    if upto < 2:
                        continue
                    for (dst, rsl, negm, bias_t, bank) in ((LI, slice(0, 128), n_incl, ngt, B[2]),
                                                           (LBT, slice(128, 256), n_strT, ngt, B[3]),
                                                           (LBm, slice(256, 384), n_str, gpl, B[2])):
                        for h in range(4):
                            kb.MM(bank[:, h * 128:(h + 1) * 128], kb.C("SELH%d" % h)[0:4, :], ROWS[:, rsl], True, False,
                                  [ROWS], [bank])
                            kb.MM(bank[:, h * 128:(h + 1) * 128], ident, negm, False, True, [], [bank])
                        for h in range(4):
                            kb.ACT(dst[:, h, :], bank[:, h * 128:(h + 1) * 128], AF.Exp, [bank, bias_t], [dst],
                                   bias=bias_t[:, h:h + 1])
                    if upto < 3:
                        continue
                    for h in range(4):
                        hp, h2 = divmod(h, 2)
                        ksl = kn[64 * h2:64 * h2 + 64, hp, :]
                        kb.MM(B[4][:, h * 128:(h + 1) * 128], ksl, ksl, True, True, [kn], [B[4]])
                        kb.MM(B[5][:, h * 128:(h + 1) * 128], ksl, qn[64 * h2:64 * h2 + 64, hp, :], True, True, [kn, qn], [B[5]])
                    b4v = B[4][:].rearrange("p (h t) -> p h t", h=4)
                    b5v = B[5][:].rearrange("p (h t) -> p h t", h=4)
                    kb.STT(NAT[:], b4v, -1.0, LBT[:], ALU.mult, ALU.mult, [B[4], LBT], [NAT])
                    kb.STT(NA[:], b4v, -1.0, LBm[:], ALU.mult, ALU.mult, [B[4], LBm], [NA])
                    kb.TT(QKm[:], b5v, LI[:], ALU.mult, [B[5], LI], [QKm])
                    if upto < 4:
                        continue
                    idb = ident.unsqueeze(1).to_broadcast([128, 4, 128])
                    kb.CP(Tm[:], idb, [], [Tm])
                    kb.CP(Wm[:], idb, [], [Wm], eng="pool")
                    for s_ in (1, 2, 4, 8, 16, 32, 64):
                        mT = kb.C(("MOFF%d" if d == 0 else "MOFFT%d") % s_).unsqueeze(1).to_broadcast([128, 4, 128])
                        mW = kb.C(("MOFFT%d" if d == 0 else "MOFF%d") % s_).unsqueeze(1).to_broadcast([128, 4, 128])
                        for h in range(4):
                            kb.MM(B[2][:, h * 128:(h + 1) * 128], NAT[:, h, :], Tm[:, h, :], True, True, [NAT, Tm], [B[2]])
                        for h in range(4):
                            kb.MM(B[3][:, h * 128:(h + 1) * 128], NA[:, h, :], Wm[:, h, :], True, True, [NA, Wm], [B[3]])
                        kb.CP(x1[:], B[2][:].rearrange("p (h t) -> p h t", h=4), [B[2]], [x1], eng="act")
                        kb.CP(y1[:], B[3][:].rearrange("p (h t) -> p h t", h=4), [B[3]], [y1], eng="dve")
                        for h in range(4):
                            kb.MM(B[4][:, h * 128:(h + 1) * 128], Wm[:, h, :], x1[:, h, :], True, True, [Wm, x1], [B[4]])
                        for h in range(4):
                            kb.MM(B[5][:, h * 128:(h + 1) * 128], Tm[:, h, :], y1[:, h, :], True, True, [Tm, y1], [B[5]])
                        kb.TT(tmx[:], B[4][:].rearrange("p (h t) -> p h t", h=4), mT, ALU.mult, [B[4]], [tmx])
                        kb.TT(tmy[:], B[5][:].rearrange("p (h t) -> p h t", h=4), mW, ALU.mult, [B[5]], [tmy])
                        kb.TT(Tm[:], Tm[:], tmx[:], ALU.add, [Tm, tmx], [Tm], eng="pool")
                        kb.TT(Wm[:], Wm[:], tmy[:], ALU.add, [Wm, tmy], [Wm], eng="pool")
                    if upto < 5:
                        continue
                    for hp in range(2):
                        kb.TR(B[0][:, 128:256], kn[:, hp, :], ident, [kn], [B[0]])
                        kb.TR(B[0][:, 256:384], vv[:, hp, :], ident, [vv], [B[0]])
                        for h2 in range(2):
                            h = 2 * hp + h2
                            kc = slice(64 * h2, 64 * h2 + 64)
                            vc = slice(64 * (1 - h2), 64 * (1 - h2) + 64)
                            kb.TS(Rm[h][:, kc], B[0][:, 128 + 64 * h2:128 + 64 * h2 + 64], beg[:, h:h + 1], None, ALU.mult, None,
                                  [B[0], beg], [Rm[h]])
                            kb.ACT(Rm[h][:, vc], B[0][:, 256 + 64 * h2:256 + 64 * h2 + 64], AF.Copy, [B[0], beta], [Rm[h]],
                                   scale=beta[:, h:h + 1])
                            kb.ACT(khp[h][:, kc], B[0][:, 128 + 64 * h2:128 + 64 * h2 + 64], AF.Copy, [B[0], ekr], [khp[h]],
                                   scale=ekr[:, h:h + 1])
                    if upto < 6:
                        continue
                    for h in range(4):
                        kb.MM(B[2][:, h * 128:(h + 1) * 128], Wm[:, h, :], Rm[h][:], True, True, [Wm, Rm[h]], [B[2]])
                        kb.MM(B[3][:, h * 128:(h + 1) * 128], Rm[h][:], Wm[:, h, :], True, True, [Wm, Rm[h]], [B[3]])
                    for h in range(4):
                        hp, h2 = divmod(h, 2)
                        vc0 = 64 * (1 - h2)
                        kb.CP(upair[hp][:, 64 * h2:64 * h2 + 64], B[2][:, h * 128 + vc0:h * 128 + vc0 + 64], [B[2]], [upair[hp]],
                              eng=("act" if h2 else "dve"))
                        kb.CP(wTp[hp][64 * h2:64 * h2 + 64, :], B[3][64 * h2:64 * h2 + 64, h * 128:(h + 1) * 128], [B[3]], [wTp[hp]],
                              eng=("dve" if h2 else "act"))
                    if upto < 7:
                        continue
                    for hp in range(2):
                        kb.MM(B[1][:, 256:384], kb.C("SELP%d" % hp)[0:4, :], ROWS[:, 0:128], True, True, [ROWS], [B[1]])
                        kb.ACT(EG[hp][:], B[1][:, 256:384], AF.Exp, [B[1]], [EG[hp]])
                        kb.TT(qd[hp][:], qn[:, hp, :], EG[hp][:], ALU.mult, [qn, EG[hp]], [qd[hp]], eng="pool")
                        pws = B[7][:, hp * 128:(hp + 1) * 128]
                        kb.MM(pws, wTp[hp][:], Sb[hp][:], True, True, [wTp[hp], Sb[hp]], [B[7]])
                        for h2 in range(2):
                            h = 2 * hp + h2
                            cs_ = slice(64 * h2, 64 * h2 + 64)
                            kb.TT(vnp[h][:, cs_], upair[hp][:, cs_], B[7][:, hp * 128 + 64 * h2:hp * 128 + 64 * h2 + 64],
                                  ALU.subtract, [upair[hp], B[7]], [vnp[h]])
                        po = B[6][:, hp * 256:hp * 256 + 128]
                        kb.MM(po, Sb[hp][:], qd[hp][:], True, False, [Sb[hp], qd[hp]], [B[6].s(hp)])
                        kb.MM(po, vnp[2 * hp][:], QKm[:, 2 * hp, :], False, False, [vnp[2 * hp], QKm], [B[6].s(hp)])
                        kb.MM(po, vnp[2 * hp + 1][:], QKm[:, 2 * hp + 1, :], False, True, [vnp[2 * hp + 1], QKm], [B[6].s(hp)])
                        cols_ = slice(n * 128, (n + 1) * 128)
                        if d == 0:
                            kb.CP(OACC[:, hp, cols_], po, [B[6].s(hp)], [OACC.s(n)], eng="act")
                        else:
                            kb.TT(OACC[:, hp, cols_], OACC[:, hp, cols_], po, ALU.add, [B[6].s(hp)], [OACC.s(n)])
                        pkv = B[6][:, hp * 256 + 128:hp * 256 + 256]
                        kb.MM(pkv, khp[2 * hp][:], vnp[2 * hp][:], True, False, [khp[2 * hp], vnp[2 * hp]], [B[6].s(2 + hp)])
                        kb.MM(pkv, khp[2 * hp + 1][:], vnp[2 * hp + 1][:], False, True, [khp[2 * hp + 1], vnp[2 * hp + 1]],
                              [B[6].s(2 + hp)])
                        kb.CP(cdp[hp][0:64, :], cdec[0:64, 2 * hp:2 * hp + 1], [cdec], [cdp[hp]])
                        kb.CP(cdp[hp][64:128, :], cdec[64:128, 2 * hp + 1:2 * hp + 2], [cdec], [cdp[hp]])
                        kb.STT(Sb[hp][:], Sb[hp][:], cdp[hp][:, 0:1], pkv, ALU.mult, ALU.add,
                               [Sb[hp], cdp[hp], B[6].s(2 + hp)], [Sb[hp]])
        with P.scope():
            G = P.sbuf("g_G2", [128, 1])
            for hh in range(2):
                kb.LD(G[64 * hh:64 * hh + 64, :], kb.prm["gdn_norm_g"][l].rearrange("(p o) -> p o", o=1), [G])
            finalize_gated(kb, OACC, Z_GG, G, 512, "g_")


class _Ctx:
    pass


def mixer_gdn2(kb, l):
    P = kb.P
    (gdn_conv if kb.cfg.get('conv_old') else gdn_conv2)(kb, l)
    with P.scope():
        OACC = P.sbuf("g_oacc", [128, 2, S])
        kb.MS(OACC[:, 0, :], 0.0, [OACC.s(n) for n in range(NT)], eng="pool")
        kb.MS(OACC[:, 1, :], 0.0, [OACC.s(n) for n in range(NT)], eng="pool")
        with P.scope():
            DTB = P.sbuf("g_dtb", [128, 8]); NEGA = P.sbuf("g_nega", [128, 8])
            kb.LD(DTB[:], kb.prm["gdn_dt_bias"][l].rearrange("d h -> (d h)").partition_broadcast(128), [DTB])
            kb.LD(NEGA[:], kb.prm["gdn_a_log"][l].rearrange("d h -> (d h)").partition_broadcast(128), [NEGA])
            kb.ACT(NEGA[:], NEGA[:], AF.Exp, [NEGA], [NEGA])
            kb.TS(NEGA[:], NEGA[:], -1.0, None, ALU.mult, None, [NEGA], [NEGA])
            ident = kb.C("IDENT")
            idb = ident.unsqueeze(1).to_broadcast([128, 4, 128])
            cxs = []
            for d in range(2):
                cx = _Ctx()
                cx.d = d
                pf = "g%d_" % d
                cx.qnb = [P.sbuf(pf + "q%d" % i, [128, 2, 128]) for i in range(2)]
                cx.knb = [P.sbuf(pf + "k%d" % i, [128, 2, 128]) for i in range(2)]
                cx.vvb = [P.sbuf(pf + "v%d" % i, [128, 2, 128]) for i in range(2)]
                cx.gabb = [P.sbuf(pf + "gab%d" % i, [128, 16]) for i in range(2)]
                for nm in ("xa", "ea", "loga", "beta", "lnb", "gtm", "ngt", "ekr", "cdec", "eg", "beg", "gpl"):
                    setattr(cx, nm, P.sbuf(pf + nm, [128, 4]))
                cx.ROWS = P.sbuf(pf + "rows", [4, 384])
                for nm in ("LI", "LBT", "LBm"):
                    setattr(cx, nm, P.sbuf(pf + nm, [128, 4, 128]))
                cx.QKm = P.sbuf(pf + "QKm", [128, 4, 128], BF16)
                cx.ROWSX = P.sbuf(pf + "rowsx", [4, 3, 4, 128])
                cx.knp = [[P.sbuf(pf + "knp%d%d" % (i, h), [128, 128], BF16) for h in range(4)] for i in range(2)]
                for i in range(2):
                    for h in range(4):
                        kb.MS(cx.knp[i][h][:], 0.0, [cx.knp[i][h]], eng="pool")
                cx.kq16 = [P.sbuf(pf + "kq16%d" % i, [128, 2, 2, 128], BF16) for i in range(2)]
                for nm in ("NAT", "NA", "Tm", "Wm", "x1", "y1", "tmx", "tmy"):
                    setattr(cx, nm, P.sbuf(pf + nm, [128, 4, 128], BF16))
                cx.Rm = [P.sbuf(pf + "R%d" % h, [128, 128], BF16) for h in range(4)]
                cx.khp = [P.sbuf(pf + "kh%d" % h, [128, 128], BF16) for h in range(4)]
                cx.vnp = [P.sbuf(pf + "vn%d" % h, [128, 128], BF16) for h in range(4)]
                for h in range(4):
                    kb.MS(cx.khp[h][:], 0.0, [cx.khp[h]], eng="pool")
                    kb.MS(cx.vnp[h][:], 0.0, [cx.vnp[h]], eng="pool")
                cx.upair = [P.sbuf(pf + "up%d" % hp, [128, 128]) for hp in range(2)]
                cx.wTp = [P.sbuf(pf + "wT%d" % hp, [128, 128]) for hp in range(2)]
                cx.EG = [P.sbuf(pf + "EG%d" % hp, [128, 128]) for hp in range(2)]
                cx.qd = [P.sbuf(pf + "qd%d" % hp, [128, 128]) for hp in range(2)]
                cx.cdp = [P.sbuf(pf + "cdp%d" % hp, [128, 1]) for hp in range(2)]
                cx.Sb = [P.sbuf(pf + "S%d" % hp, [128, 128]) for hp in range(2)]
                for hp in range(2):
                    kb.MS(cx.Sb[hp][:], 0.0, [cx.Sb[hp]])
                cx.B = [P.psum(pf + "B%d" % i, [128, 512]) for i in range(4)]
                cx.tri = kb.C("TRIF" if d == 0 else "TRIB")
                cx.rem = kb.C("SUFF" if d == 0 else "PREB")
                cx.n_incl = kb.C("NLE" if d == 0 else "NGE")
                cx.n_strT = kb.C("NLT" if d == 0 else "NGT")
                cx.n_str = kb.C("NGT" if d == 0 else "NLT")
                cx.it = 0
                cx.id16 = P.sbuf(pf + "id16", [128, 128], BF16)
                kb.CP(cx.id16[:], ident, [], [cx.id16])
                for nm_, cn in (("n_incl4", "NLE" if d == 0 else "NGE"), ("n_strT4", "NLT" if d == 0 else "NGT"),
                                ("n_str4", "NGT" if d == 0 else "NLT")):
                    t_ = P.sbuf(pf + nm_, [128, 4, 128], BF16)
                    kb.CP(t_[:], kb.C(cn).unsqueeze(1).to_broadcast([128, 4, 128]), [], [t_])
                    setattr(cx, nm_, t_[:].rearrange("p h i -> p (h i)"))
                bd = P.sbuf(pf + "bd4", [4, 4, 128])
                for h in range(4):
                    kb.CP(bd[:, h, :], kb.C("SELH%d" % h)[0:4, :], [], [bd])
                cx.bd4 = bd[:]
                cx.mT = {}; cx.mW = {}
                for s_ in (2, 4, 8, 16, 32, 64):
                    for nm_, dct, cn in (("mT", cx.mT, ("MOFF%d" if d == 0 else "MOFFT%d") % s_),
                                         ("mW", cx.mW, ("MOFFT%d" if d == 0 else "MOFF%d") % s_)):
                        mt_ = P.sbuf(pf + nm_ + str(s_), [128, 4, 128], mybir.dt.uint8)
                        kb.CP(mt_[:], kb.C(cn).unsqueeze(1).to_broadcast([128, 4, 128]), [], [mt_])
                        dct[s_] = mt_
                cxs.append(cx)

            def step(cx, n):
                d = cx.d
                Pa, Pb, Pc, Pd = cx.B
                cols = slice(n * 128, (n + 1) * 128)
                b = cx.it % 2; cx.it += 1
                qn, kn, vv, gab = cx.qnb[b], cx.knb[b], cx.vvb[b], cx.gabb[b]
                xa, ea, loga, beta, lnb = cx.xa, cx.ea, cx.loga, cx.beta, cx.lnb
                gtm, ngt, ekr, cdec, eg, beg, gpl = cx.gtm, cx.ngt, cx.ekr, cx.cdec, cx.eg, cx.beg, cx.gpl
                ROWS, LI, LBT, LBm, NAT, NA, QKm = cx.ROWS, cx.LI, cx.LBT, cx.LBm, cx.NAT, cx.NA, cx.QKm
                Tm, Wm, x1, y1, tmx, tmy = cx.Tm, cx.Wm, cx.x1, cx.y1, cx.tmx, cx.tmy
                Rm, khp, vnp, upair, wTp, EG, qd, cdp, Sb = cx.Rm, cx.khp, cx.vnp, cx.upair, cx.wTp, cx.EG, cx.qd, cx.cdp, cx.Sb
                tri = cx.tri
                kb.LD(qn[:], kb.QKVF[0:256, cols].rearrange("(hp p) t -> p hp t", p=128), [qn])
                kb.LD(kn[:], kb.QKVF[256:512, cols].rearrange("(hp p) t -> p hp t", p=128), [kn])
                kb.LD(vv[:], kb.QKVF[512:768, cols].rearrange("(hp p) t -> p hp t", p=128), [vv])
                kb.LD(gab[:], kb.ZT[cols, 512:528], [gab])
                knp = cx.knp[b]; kq16 = cx.kq16[b]
                for h in range(4):
                    kb.LD(knp[h][64 * (h % 2):64 * (h % 2) + 64, :], kb.QKVF[256 + 64 * h:256 + 64 * h + 64, cols], [knp[h]], q="pool")
                kb.LD(kq16[:, 0, :, :], kb.QKVF[256:512, cols].rearrange("(hp p) t -> p hp t", p=128), [kq16], q="pool")
                kb.LD(kq16[:, 1, :, :], kb.QKVF[0:256, cols].rearrange("(hp p) t -> p hp t", p=128), [kq16], q="pool")
                kb.TT(xa[:], gab[:, 4 * d:4 * d + 4], DTB[:, 4 * d:4 * d + 4], ALU.add, [gab, DTB], [xa])
                kb.ACT(ea[:], xa[:], AF.Exp, [xa], [ea])
                kb.ACT(ea[:], ea[:], AF.Ln, [ea], [ea], bias=kb.C("CCOL")[:, 1:2])
                kb.TT(loga[:], ea[:], NEGA[:, 4 * d:4 * d + 4], ALU.mult, [ea, NEGA], [loga])
                kb.ACT(beta[:], gab[:, 8 + 4 * d:12 + 4 * d], AF.Sigmoid, [gab], [beta])
                kb.ACT(lnb[:], beta[:], AF.Ln, [beta], [lnb])
                kb.MM(Pc[:, 0:4], tri, loga[:], True, True, [loga], [Pc])
                kb.MM(Pc[:, 4:8], cx.rem, loga[:], True, True, [loga], [Pc])
                kb.MM(Pc[:, 8:12], kb.C("ONES"), loga[:], True, True, [loga], [Pc])
                kb.CP(gtm[:], Pc[:, 0:4], [Pc], [gtm])
                kb.TS(ngt[:], Pc[:, 0:4], -1.0, None, ALU.mult, None, [Pc], [ngt])
                kb.ACT(ekr[:], Pc[:, 4:8], AF.Exp, [Pc], [ekr])
                kb.ACT(cdec[:], Pc[:, 8:12], AF.Exp, [Pc], [cdec])
                kb.ACT(eg[:], gtm[:], AF.Exp, [gtm], [eg])
                kb.TT(beg[:], beta[:], eg[:], ALU.mult, [beta, eg], [beg])
                kb.TT(gpl[:], gtm[:], lnb[:], ALU.add, [gtm, lnb], [gpl])
                kb.MM(Pd[0:4, 0:128], loga[:], tri, True, True, [loga], [Pd])
                kb.MM(Pd[0:4, 128:256], loga[:], tri, True, False, [loga], [Pd])
                kb.MM(Pd[0:4, 128:256], lnb[:], ident, False, True, [lnb], [Pd])
                kb.CP(ROWS[:, 0:256], Pd[0:4, 0:256], [Pd], [ROWS])
                kb.TS(ROWS[:, 256:384], Pd[0:4, 0:128], -1.0, None, ALU.mult, None, [Pd], [ROWS])
                yield
                kb.TT(cx.ROWSX[:], ROWS[:].rearrange("c (r i) -> c r i", r=3).unsqueeze(2).to_broadcast([4, 3, 4, 128]),
                      cx.bd4.unsqueeze(1).to_broadcast([4, 3, 4, 128]), ALU.mult, [ROWS], [cx.ROWSX])
                for (dst, ri, negm4, bias_t, bank) in ((LI, 0, cx.n_incl4, ngt, Pa), (LBT, 1, cx.n_strT4, ngt, Pb),
                                                       (LBm, 2, cx.n_str4, gpl, Pa)):
                    kb.MM(bank[:], kb.C("ONES")[0:4, :], cx.ROWSX[:, ri, :, :].rearrange("c h i -> c (h i)"), True, False,
                          [cx.ROWSX], [bank])
                    kb.MM(bank[:], cx.id16[:], negm4[:], False, True, [], [bank])
                    yield
                    for h in range(4):
                        kb.ACT(dst[:, h, :], bank[:, h * 128:(h + 1) * 128], AF.Exp, [bank, bias_t], [dst], bias=bias_t[:, h:h + 1])
                    yield
                for h in range(4):
                    hp, h2 = divmod(h, 2)
                    kb.MM(Pa[:, h * 128:(h + 1) * 128], knp[h][:], kq16[:, 0, hp, :], True, True, [knp[h], kq16], [Pa])
                    kb.MM(Pb[:, h * 128:(h + 1) * 128], knp[h][:], kq16[:, 1, hp, :], True, True, [knp[h], kq16], [Pb])
                pav = Pa[:].rearrange("p (h t) -> p h t", h=4)
                pbv = Pb[:].rearrange("p (h t) -> p h t", h=4)
                kb.STT(NAT[:], pav, -1.0, LBT[:], ALU.mult, ALU.mult, [Pa, LBT], [NAT])
                kb.STT(NA[:], pav, -1.0, LBm[:], ALU.mult, ALU.mult, [Pa, LBm], [NA])
                kb.TT(QKm[:], pbv, LI[:], ALU.mult, [Pb, LI], [QKm])
                yield
                mT = kb.C("MOFF1" if d == 0 else "MOFFT1").unsqueeze(1).to_broadcast([128, 4, 128])
                mW = kb.C("MOFFT1" if d == 0 else "MOFF1").unsqueeze(1).to_broadcast([128, 4, 128])
                kb.TT(tmx[:], NA[:], mT, ALU.mult, [NA], [tmx], eng="pool")
                kb.TT(tmy[:], NAT[:], mW, ALU.mult, [NAT], [tmy], eng="pool")
                kb.TT(Tm[:], tmx[:], idb, ALU.add, [tmx], [Tm], eng="pool")
                kb.TT(Wm[:], tmy[:], idb, ALU.add, [tmy], [Wm], eng="pool")
                yield
                for s_ in (2, 4, 8, 16, 32, 64):
                    for h in range(4):
                        kb.MM(Pa[:, h * 128:(h + 1) * 128], NAT[:, h, :], Tm[:, h, :], True, True, [NAT, Tm], [Pa])
                    for h in range(4):
                        kb.MM(Pb[:, h * 128:(h + 1) * 128], NA[:, h, :], Wm[:, h, :], True, True, [NA, Wm], [Pb])
                    yield
                    kb.CP(x1[:], pav, [Pa], [x1], eng="act")
                    kb.CP(y1[:], pbv, [Pb], [y1], eng="act")
                    yield
                    for h in range(4):
                        kb.MM(Pa[:, h * 128:(h + 1) * 128], Wm[:, h, :], x1[:, h, :], True, True, [Wm, x1], [Pa])
                    for h in range(4):
                        kb.MM(Pb[:, h * 128:(h + 1) * 128], Tm[:, h, :], y1[:, h, :], True, True, [Tm, y1], [Pb])
                    yield
                    kb.CPRED(Tm[:], cx.mT[s_][:], pav, [Pa, cx.mT[s_]], [Tm])
                    kb.CPRED(Wm[:], cx.mW[s_][:], pbv, [Pb, cx.mW[s_]], [Wm])
                    yield
                for hp in range(2):
                    kb.TR(Pc[:, 128:256], kn[:, hp, :], ident, [kn], [Pc])
                    kb.TR(Pc[:, 256:384], vv[:, hp, :], ident, [vv], [Pc])
                    for h2 in range(2):
                        h = 2 * hp + h2
                        kc = slice(64 * h2, 64 * h2 + 64)
                        vc = slice(64 * (1 - h2), 64 * (1 - h2) + 64)
                        kb.TS(Rm[h][:, kc], Pc[:, 128 + 64 * h2:128 + 64 * h2 + 64], beg[:, h:h + 1], None, ALU.mult, None,
                              [Pc, beg], [Rm[h]])
                        kb.ACT(Rm[h][:, vc], Pc[:, 256 + 64 * h2:256 + 64 * h2 + 64], AF.Copy, [Pc, beta], [Rm[h]],
                               scale=beta[:, h:h + 1])
                        kb.ACT(khp[h][:, kc], Pc[:, 128 + 64 * h2:128 + 64 * h2 + 64], AF.Copy, [Pc, ekr], [khp[h]],
                               scale=ekr[:, h:h + 1])
                    yield
                for h in range(4):
                    kb.MM(Pa[:, h * 128:(h + 1) * 128], Wm[:, h, :], Rm[h][:], True, True, [Wm, Rm[h]], [Pa])
                    kb.MM(Pb[:, h * 128:(h + 1) * 128], Rm[h][:], Wm[:, h, :], True, True, [Wm, Rm[h]], [Pb])
                for h in range(4):
                    hp, h2 = divmod(h, 2)
                    vc0 = 64 * (1 - h2)
                    kb.CP(upair[hp][:, 64 * h2:64 * h2 + 64], Pa[:, h * 128 + vc0:h * 128 + vc0 + 64], [Pa], [upair[hp]], eng="dve")
                    kb.CP(wTp[hp][64 * h2:64 * h2 + 64, :], Pb[64 * h2:64 * h2 + 64, h * 128:(h + 1) * 128], [Pb], [wTp[hp]], eng="act")
                yield
                for hp in range(2):
                    kb.MM(Pc[:, 384:512], kb.C("SELP%d" % hp)[0:4, :], ROWS[:, 0:128], True, True, [ROWS], [Pc])
                    kb.ACT(EG[hp][:], Pc[:, 384:512], AF.Exp, [Pc], [EG[hp]])
                    kb.TT(qd[hp][:], qn[:, hp, :], EG[hp][:], ALU.mult, [qn, EG[hp]], [qd[hp]], eng="pool")
                    pws = Pc[:, hp * 128:(hp + 1) * 128]
                    kb.MM(pws, wTp[hp][:], Sb[hp][:], True, True, [wTp[hp], Sb[hp]], [Pc])
                    for h2 in range(2):
                        h = 2 * hp + h2
                        cs_ = slice(64 * h2, 64 * h2 + 64)
                        kb.TT(vnp[h][:, cs_], upair[hp][:, cs_], Pc[:, hp * 128 + 64 * h2:hp * 128 + 64 * h2 + 64],
                              ALU.subtract, [upair[hp], Pc], [vnp[h]])
                    po = Pd[:, hp * 256:hp * 256 + 128]
                    kb.MM(po, Sb[hp][:], qd[hp][:], True, False, [Sb[hp], qd[hp]], [Pd])
                    kb.MM(po, vnp[2 * hp][:], QKm[:, 2 * hp, :], False, False, [vnp[2 * hp], QKm], [Pd])
                    kb.MM(po, vnp[2 * hp + 1][:], QKm[:, 2 * hp + 1, :], False, True, [vnp[2 * hp + 1], QKm], [Pd])
                    kb.TT(OACC[:, hp, cols], OACC[:, hp, cols], po, ALU.add, [Pd], [OACC.s(n)])
                    pkv = Pd[:, hp * 256 + 128:hp * 256 + 256]
                    kb.MM(pkv, khp[2 * hp][:], vnp[2 * hp][:], True, False, [khp[2 * hp], vnp[2 * hp]], [Pd])
                    kb.MM(pkv, khp[2 * hp + 1][:], vnp[2 * hp + 1][:], False, True, [khp[2 * hp + 1], vnp[2 * hp + 1]], [Pd])
                    kb.CP(cdp[hp][0:64, :], cdec[0:64, 2 * hp:2 * hp + 1], [cdec], [cdp[hp]])
                    kb.CP(cdp[hp][64:128, :], cdec[64:128, 2 * hp + 1:2 * hp + 2], [cdec], [cdp[hp]])
                    kb.STT(Sb[hp][:], Sb[hp][:], cdp[hp][:, 0:1], pkv, ALU.mult, ALU.add, [Sb[hp], cdp[hp], Pd], [Sb[hp]])
                    yield

            def stream(cx):
                for n in ORDER[cx.d][:kb.cfg.get("ntiles", NT)]:
                    yield from step(cx, n)
            active = [stream(cxs[0]), stream(cxs[1])]
            while active:
                for g_ in list(active):
                    try:
                        next(g_)
                    except StopIteration:
                        active.remove(g_)
        with P.scope():
            G = P.sbuf("g_G2", [128, 1])
            for hh in range(2):
                kb.LD(G[64 * hh:64 * hh + 64, :], kb.prm["gdn_norm_g"][l].rearrange("(p o) -> p o", o=1), [G])
            finalize_gated(kb, OACC, Z_GG, G, 512, "g_")


def run_interleaved(gens):
    active = list(gens)
    while active:
        for g_ in list(active):
            try:
                next(g_)
            except StopIteration:
                active.remove(g_)


def oacc_add(kb, OACC, hp, n, ps):
    cols = slice(n * 128, (n + 1) * 128)
    kb.TT(OACC[:, hp, cols], OACC[:, hp, cols], ps[:], ALU.add, [ps], [OACC.s(n)])


def oacc_zero(kb, OACC):
    for hp in range(2):
        kb.MS(OACC[:, hp, :], 0.0, [OACC.s(n) for n in range(NT)], eng="pool")


def mixer_hgrn2(kb, l):
    P = kb.P
    with P.scope():
        OACC = P.sbuf("h_oacc", [128, 2, S])
        oacc_zero(kb, OACC)
        with P.scope():
            LB = P.sbuf("h_LB", [128, 4]); OML = P.sbuf("h_OML", [128, 4])
            if l == 0:
                kb.MS(LB[:], 0.0, [LB]); kb.MS(OML[:], 1.0, [OML])
            else:
                lgt = P.sbuf("h_lgt", [128, 8])
                kb.LD(lgt[:], kb.prm["hgrn_lb_logits"][:].rearrange("l d (hp p) -> p (l d hp)", p=128), [lgt],
                      allow_slow_non_contiguous=True)
                kb.TT(LB[:], lgt[:, 4:8], lgt[:, 0:4], ALU.subtract, [lgt], [LB])
                kb.ACT(LB[:], LB[:], AF.Sigmoid, [LB], [LB])
                kb.TS(OML[:], LB[:], -1.0, 1.0, ALU.mult, ALU.add, [LB], [OML])

            def make(d):
                pf = "h%d_" % d
                hqb = [P.sbuf(pf + "q%d" % i, [128, 2, 128]) for i in range(2)]
                hfb = [P.sbuf(pf + "f%d" % i, [128, 2, 128]) for i in range(2)]
                Vp = [[P.sbuf(pf + "vp%d%d" % (i, h), [128, 128]) for h in range(4)] for i in range(2)]
                khp = [[P.sbuf(pf + "kh%d%d" % (i, h), [128, 128]) for h in range(4)] for i in range(2)]
                for i in range(2):
                    for h in range(4):
                        kb.MS(Vp[i][h][:], 0.0, [Vp[i][h]], eng="pool")
                        kb.MS(khp[i][h][:], 0.0, [khp[i][h]], eng="pool")
                MREF = [P.sbuf(pf + "mr%d" % i, [128, 4]) for i in range(2)]
                for i in range(2):
                    kb.MS(MREF[i][:], 0.0, [MREF[i]])

                def two(name, shape=(128, 128)):
                    return [P.sbuf(pf + "%s%d" % (name, i), list(shape)) for i in range(2)]
                qs, sgm, ff, logf, kk, bb, pre = two("qs"), two("sg"), two("ff"), two("lf"), two("kk"), two("bb"), two("pre")
                e1, Ql, e2, Qd = two("e1"), two("Ql"), two("e2"), two("Qd")
                Kt = [two("Kt%d" % r) for r in range(4)]
                ex = two("ex")
                AT = two("AT", (128, 2, 128))
                KhT = two("KhT")
                bend = two("bend", (128, 2))
                Sb = [P.sbuf(pf + "S%d" % hp, [128, 128]) for hp in range(2)]
                for hp in range(2):
                    kb.MS(Sb[hp][:], 0.0, [Sb[hp]])
                pss = P.psum(pf + "pss", [128, 2, 128])
                po = P.psum(pf + "pso", [128, 128])
                pk = P.psum(pf + "psk", [128, 128])
                pkv = P.psum(pf + "pskv", [128, 128])
                zf = Z_HFF if d == 0 else Z_HFB
                tri = kb.C("TRIF" if d == 0 else "TRIB").unsqueeze(1).to_broadcast([128, 2, 128])

                def gen():
                    it = 0
                    jj = 0
                    for n in ORDER[d]:
                        cols = slice(n * 128, (n + 1) * 128)
                        b = it % 2; it += 1
                        hq, hf = hqb[b], hfb[b]
                        kb.LD(hq[:], kb.ZF[Z_HQ:Z_HQ + 256, cols].rearrange("(hp p) t -> p hp t", p=128), [hq])
                        kb.LD(hf[:], kb.ZF[zf:zf + 256, cols].rearrange("(hp p) t -> p hp t", p=128), [hf])
                        for h in range(4):
                            kb.LD(Vp[b][h][:, 64 * (h % 2):64 * (h % 2) + 64], kb.ZT[cols, 64 * h:64 * h + 64], [Vp[b][h]])
                        yield
                        for hp in range(2):
                            j = jj % 2; jj += 1
                            c = 2 * d + hp
                            mref = MREF[j]
                            kb.ACT(qs[j][:], hq[:, hp, :], AF.Exp, [hq], [qs[j]], scale=-1.0)
                            kb.TS(qs[j][:], qs[j][:], 1.0, None, ALU.add, None, [qs[j]], [qs[j]])
                            kb.RECIP(qs[j][:], qs[j][:], [qs[j]], [qs[j]])
                            kb.TT(qs[j][:], qs[j][:], hq[:, hp, :], ALU.mult, [qs[j], hq], [qs[j]], eng="pool")
                            kb.ACT(sgm[j][:], hf[:, hp, :], AF.Exp, [hf], [sgm[j]], scale=-1.0)
                            kb.TS(sgm[j][:], sgm[j][:], 1.0, None, ALU.add, None, [sgm[j]], [sgm[j]])
                            kb.RECIP(sgm[j][:], sgm[j][:], [sgm[j]], [sgm[j]])
                            kb.TS(ff[j][:], sgm[j][:], OML[:, c:c + 1], LB[:, c:c + 1], ALU.mult, ALU.add, [sgm[j], OML, LB], [ff[j]])
                            kb.ACT(logf[j][:], ff[j][:], AF.Ln, [ff[j]], [logf[j]])
                            kb.TS(kk[j][:], ff[j][:], -1.0, 1.0, ALU.mult, ALU.add, [ff[j]], [kk[j]], eng="pool")
                            yield
                            B = bb[j]
                            if d == 0:
                                kb.SCAN(B[:], kb.C("ONES"), logf[j][:], [logf[j]], [B])
                                kb.CP(mref[:, 1:4], B[:].rearrange("p (r c) -> p r c", c=32)[:, 0:3, 31], [B], [mref])
                                be = B[:, 127:128]
                            else:
                                kb.SCAN(pre[j][:], kb.C("ONES"), logf[j][:], [logf[j]], [pre[j]])
                                kb.STT(B[:], pre[j][:], -1.0, logf[j][:], ALU.mult, ALU.add, [pre[j], logf[j]], [B])
                                kb.TS(B[:], B[:], pre[j][:, 127:128], None, ALU.add, None, [B, pre[j]], [B])
                                kb.CP(mref[:, 0:3], B[:].rearrange("p (r c) -> p r c", c=32)[:, 1:4, 0], [B], [mref])
                                be = B[:, 0:1]
                            yield
                            kb.TT(e1[j][:].rearrange("p (r c) -> p r c", c=32), B[:].rearrange("p (r c) -> p r c", c=32),
                                  mref[:].unsqueeze(2).to_broadcast([128, 4, 32]), ALU.subtract, [B, mref], [e1[j]])
                            kb.ACT(e1[j][:], e1[j][:], AF.Exp, [e1[j]], [e1[j]])
                            kb.STT(Ql[j][:], qs[j][:], 0.125, e1[j][:], ALU.mult, ALU.mult, [qs[j], e1[j]], [Ql[j]])
                            kb.ACT(e2[j][:], B[:], AF.Exp, [B], [e2[j]])
                            kb.STT(Qd[j][:], qs[j][:], 0.125, e2[j][:], ALU.mult, ALU.mult, [qs[j], e2[j]], [Qd[j]])
                            yield
                            for r in range(4):
                                kb.ACT(ex[j][:], B[:], AF.Exp, [B, mref], [ex[j]], scale=-1.0, bias=mref[:, r:r + 1])
                                kb.STT(Kt[r][j][:], ex[j][:], 1e26, kk[j][:], ALU.min, ALU.mult, [ex[j], kk[j]], [Kt[r][j]])
                                for h2 in range(2):
                                    kb.MM(pss[:, h2, 32 * r:32 * r + 32], Kt[r][j][64 * h2:64 * h2 + 64, :],
                                          Ql[j][64 * h2:64 * h2 + 64, 32 * r:32 * r + 32], True, True,
                                          [Kt[r][j], Ql[j]], [pss])
                                yield
                            kb.TT(AT[j][:], pss[:], tri, ALU.mult, [pss], [AT[j]])
                            yield
                            kb.MM(po[:], Vp[b][2 * hp][:], AT[j][:, 0, :], True, False, [Vp[b][2 * hp], AT[j]], [po])
                            kb.MM(po[:], Vp[b][2 * hp + 1][:], AT[j][:, 1, :], False, False, [Vp[b][2 * hp + 1], AT[j]], [po])
                            kb.MM(po[:], Sb[hp][:], Qd[j][:], False, True, [Sb[hp], Qd[j]], [po])
                            oacc_add(kb, OACC, hp, n, po)
                            kb.CP(bend[j][:, 0:1], be, [B], [bend[j]])
                            kb.ACT(KhT[j][:], B[:], AF.Exp, [B, bend[j]], [KhT[j]], scale=-1.0, bias=bend[j][:, 0:1])
                            kb.TT(KhT[j][:], KhT[j][:], kk[j][:], ALU.mult, [KhT[j], kk[j]], [KhT[j]], eng="pool")
                            kb.ACT(bend[j][:, 1:2], bend[j][:, 0:1], AF.Exp, [bend[j]], [bend[j]])
                            yield
                            kb.TR(pk[:], KhT[j][:], kb.C("IDENT"), [KhT[j]], [pk])
                            for h2 in range(2):
                                h = 2 * hp + h2
                                kb.CP(khp[b][h][:, 64 * h2:64 * h2 + 64], pk[:, 64 * h2:64 * h2 + 64], [pk], [khp[b][h]],
                                      eng=("act" if h2 else "dve"))
                            yield
                            kb.MM(pkv[:], khp[b][2 * hp][:], Vp[b][2 * hp][:], True, False, [khp[b][2 * hp], Vp[b][2 * hp]], [pkv])
                            kb.MM(pkv[:], khp[b][2 * hp + 1][:], Vp[b][2 * hp + 1][:], False, True,
                                  [khp[b][2 * hp + 1], Vp[b][2 * hp + 1]], [pkv])
                            kb.STT(Sb[hp][:], Sb[hp][:], bend[j][:, 1:2], pkv[:], ALU.mult, ALU.add,
                                   [Sb[hp], bend[j], pkv], [Sb[hp]])
                            yield
                return gen()
            run_interleaved([make(0), make(1)])
        with P.scope():
            G = P.sbuf("h_G2", [128, 1])
            for hh in range(2):
                kb.LD(G[64 * hh:64 * hh + 64, :], kb.prm["hgrn_norm_g"][l].rearrange("(p o) -> p o", o=1), [G])
            finalize_gated(kb, OACC, Z_HG, G, 0, "h_")


def mixer_ret2(kb, l):
    P = kb.P
    with P.scope():
        OACC = P.sbuf("r_oacc", [128, 2, S])
        oacc_zero(kb, OACC)
        with P.scope():
            lgt = P.sbuf("r_lgt", [128, 8])
            kb.LD(lgt[:], kb.prm["ret_decay_logit"][l].rearrange("d h -> (d h)").partition_broadcast(128), [lgt])
            LG = P.sbuf("r_LG", [128, 8])
            kb.ACT(LG[:], lgt[:], AF.Sigmoid, [lgt], [LG])
            kb.ACT(LG[:], LG[:], AF.Ln, [LG], [LG])
            LGP = P.sbuf("r_LGP", [128, 4])
            for d in range(2):
                for hp in range(2):
                    c = 2 * d + hp
                    kb.CP(LGP[0:64, c:c + 1], LG[0:64, 4 * d + 2 * hp:4 * d + 2 * hp + 1], [LG], [LGP])
                    kb.CP(LGP[64:128, c:c + 1], LG[64:128, 4 * d + 2 * hp + 1:4 * d + 2 * hp + 2], [LG], [LGP])
            MK = [P.sbuf("r_MK%d" % d, [128, 4, 128]) for d in range(2)]
            QDEC = [[P.sbuf("r_QD%d%d" % (d, hp), [128, 128]) for hp in range(2)] for d in range(2)]
            etmp = P.sbuf("r_etmp", [128, 128])
            for d in range(2):
                for h in range(4):
                    kb.ACT(etmp[:], kb.C("DIFF" if d == 0 else "NDIFF"), AF.Exp, [LG], [etmp],
                           scale=LG[:, 4 * d + h:4 * d + h + 1])
                    kb.STT(MK[d][:, h, :], etmp[:], 0.125, kb.C("TRIF" if d == 0 else "TRIB"), ALU.mult, ALU.mult,
                           [etmp], [MK[d]])
                for hp in range(2):
                    kb.ACT(QDEC[d][hp][:], kb.C("IOTAF1" if d == 0 else "RIOTAF"), AF.Exp, [LGP], [QDEC[d][hp]],
                           scale=LGP[:, 2 * d + hp:2 * d + hp + 1])
            KD = P.sbuf("r_KD", [128, 8])
            kb.ACT(KD[:, 0:4], LG[:, 0:4], AF.Exp, [LG], [KD], scale=kb.C("CCOL")[:, 3:4])
            kb.ACT(KD[:, 4:8], LG[:, 4:8], AF.Exp, [LG], [KD], scale=kb.C("CCOL")[:, 2:3])
            kb.TS(KD[:], KD[:], 0.125, None, ALU.mult, None, [KD], [KD])
            CV = P.sbuf("r_CV", [128, 4])
            kb.ACT(CV[:], LGP[:], AF.Exp, [LGP], [CV], scale=128.0)

            def make(d):
                pf = "r%d_" % d
                qTb = [P.sbuf(pf + "q%d" % i, [128, 2, 128]) for i in range(2)]
                kTb = [P.sbuf(pf + "k%d" % i, [128, 2, 128]) for i in range(2)]
                csb = [P.sbuf(pf + "cs%d" % i, [128, 2, 128]) for i in range(2)]
                Vp = [[P.sbuf(pf + "vp%d%d" % (i, h), [128, 128]) for h in range(4)] for i in range(2)]
                khp = [[P.sbuf(pf + "kh%d%d" % (i, h), [128, 128]) for h in range(4)] for i in range(2)]
                for i in range(2):
                    for h in range(4):
                        kb.MS(Vp[i][h][:], 0.0, [Vp[i][h]], eng="pool")
                        kb.MS(khp[i][h][:], 0.0, [khp[i][h]], eng="pool")
                t1 = [P.sbuf(pf + "t1%d" % i, [128, 128]) for i in range(2)]
                t2 = [P.sbuf(pf + "t2%d" % i, [128, 128]) for i in range(2)]
                qr = [P.sbuf(pf + "qr%d" % i, [128, 2, 128]) for i in range(2)]
                kr = [P.sbuf(pf + "kr%d" % i, [128, 2, 128]) for i in range(2)]
                AT = [P.sbuf(pf + "AT%d" % i, [128, 2, 128]) for i in range(2)]
                qd = [P.sbuf(pf + "qd%d" % i, [128, 128]) for i in range(2)]
                Sb = [P.sbuf(pf + "S%d" % hp, [128, 128]) for hp in range(2)]
                for hp in range(2):
                    kb.MS(Sb[hp][:], 0.0, [Sb[hp]])
                pr = P.psum(pf + "psr", [128, 256])
                pss = P.psum(pf + "pss", [128, 2, 128])
                po = P.psum(pf + "pso", [128, 128])
                pkk = P.psum(pf + "pskk", [128, 256])

                def gen():
                    it = 0
                    jj = 0
                    for n in ORDER[d]:
                        cols = slice(n * 128, (n + 1) * 128)
                        b = it % 2; it += 1
                        qT, kT, cs = qTb[b], kTb[b], csb[b]
                        kb.LD(qT[:], kb.ZF[Z_RQ:Z_RQ + 256, cols].rearrange("(hp p) t -> p hp t", p=128), [qT])
                        kb.LD(kT[:], kb.ZF[Z_RK:Z_RK + 256, cols].rearrange("(hp p) t -> p hp t", p=128), [kT])
                        kb.LD(cs[:, 0, :], kb.ropec[:, cols], [cs])
                        kb.LD(cs[:, 1, :], kb.ropes[:, cols], [cs])
                        for h in range(4):
                            kb.LD(Vp[b][h][:, 64 * (h % 2):64 * (h % 2) + 64], kb.ZT[cols, 256 + 64 * h:256 + 64 * h + 64],
                                  [Vp[b][h]])
                        yield
                        for hp in range(2):
                            j = jj % 2; jj += 1
                            kb.MM(pr[:, 0:128], kb.C("ROT"), qT[:, hp, :], True, True, [qT], [pr])
                            kb.MM(pr[:, 128:256], kb.C("ROT"), kT[:, hp, :], True, True, [kT], [pr])
                            yield
                            for (src_, dst, off) in ((qT, qr[b], 0), (kT, kr[b], 128)):
                                kb.TT(t1[j][:], src_[:, hp, :], cs[:, 0, :], ALU.mult, [src_, cs], [t1[j]], eng="pool")
                                kb.TT(t2[j][:], pr[:, off:off + 128], cs[:, 1, :], ALU.mult, [pr, cs], [t2[j]])
                                kb.TT(dst[:, hp, :], t1[j][:], t2[j][:], ALU.add, [t1[j], t2[j]], [dst.s(hp)], eng="pool")
                                yield
                            for h2 in range(2):
                                kb.MM(pss[:, h2, :], kr[b][64 * h2:64 * h2 + 64, hp, :], qr[b][64 * h2:64 * h2 + 64, hp, :],
                                      True, True, [kr[b].s(hp), qr[b].s(hp)], [pss])
                            yield
                            kb.TT(AT[j][:], pss[:], MK[d][:, 2 * hp:2 * hp + 2, :], ALU.mult, [pss, MK[d]], [AT[j]])
                            kb.TT(qd[j][:], qr[b][:, hp, :], QDEC[d][hp][:], ALU.mult, [qr[b].s(hp), QDEC[d][hp]], [qd[j]],
                                  eng="pool")
                            yield
                            kb.MM(po[:], Vp[b][2 * hp][:], AT[j][:, 0, :], True, False, [Vp[b][2 * hp], AT[j]], [po])
                            kb.MM(po[:], Vp[b][2 * hp + 1][:], AT[j][:, 1, :], False, False, [Vp[b][2 * hp + 1], AT[j]], [po])
                            kb.MM(po[:], Sb[hp][:], qd[j][:], False, True, [Sb[hp], qd[j]], [po])
                            kb.TR(pkk[:, 0:128], kr[b][:, hp, :], kb.C("IDENT"), [kr[b].s(hp)], [pkk])
                            yield
                            oacc_add(kb, OACC, hp, n, po)
                            for h2 in range(2):
                                h = 2 * hp + h2
                                kb.ACT(khp[b][h][:, 64 * h2:64 * h2 + 64], pkk[:, 64 * h2:64 * h2 + 64], AF.Copy,
                                       [pkk, KD], [khp[b][h]], scale=KD[:, 4 * d + h:4 * d + h + 1])
                            yield
                            kb.MM(pkk[:, 128:256], khp[b][2 * hp][:], Vp[b][2 * hp][:], True, False,
                                  [khp[b][2 * hp], Vp[b][2 * hp]], [pkk])
                            kb.MM(pkk[:, 128:256], khp[b][2 * hp + 1][:], Vp[b][2 * hp + 1][:], False, True,
                                  [khp[b][2 * hp + 1], Vp[b][2 * hp + 1]], [pkk])
                            kb.STT(Sb[hp][:], Sb[hp][:], CV[:, 2 * d + hp:2 * d + hp + 1], pkk[:, 128:256], ALU.mult, ALU.add,
                                   [Sb[hp], CV, pkk], [Sb[hp]])
                            yield
                return gen()
            run_interleaved([make(0), make(1)])
        with P.scope():
            finalize_gated(kb, OACC, Z_RG, None, 256, "r_")


def s5_tables(kb, l, d, VFr, VFi, T1, T2, AR, NAI):
    P = kb.P
    prm = kb.prm
    with P.scope():
        lr = P.sbuf("s_lr", [128, 16, 64]); li = P.sbuf("s_li", [128, 16, 64]); dtb = P.sbuf("s_dt", [128, 16])
        kb.LD(lr[:], prm["s5_lam_re"][l][d].rearrange("g p -> (g p)").partition_broadcast(128), [lr])
        kb.LD(li[:], prm["s5_lam_im"][l][d].rearrange("g p -> (g p)").partition_broadcast(128), [li])
        kb.LD(dtb[:], prm["s5_log_dt"][l][d].partition_broadcast(128), [dtb])
        kb.ACT(dtb[:], dtb[:], AF.Exp, [dtb], [dtb])
        dt_bc = dtb[:].unsqueeze(2).to_broadcast([128, 16, 64])
        lrdt = P.sbuf("s_lrdt", [128, 16, 64]); lidt = P.sbuf("s_lidt", [128, 16, 64])
        kb.TT(lrdt[:], lr[:], dt_bc, ALU.mult, [lr, dtb], [lrdt])
        kb.TT(lidt[:], li[:], dt_bc, ALU.mult, [li, dtb], [lidt])
        a = [P.sbuf("s_a%d" % i, [128, 16, 64]) for i in range(8)]
        mag, ang, sn, cs, tmp, ar, ai, t2 = a
        kb.ACT(mag[:], lrdt[:], AF.Exp, [lrdt], [mag])
        _sincos(kb, lidt[:], sn[:], cs[:], [lidt, sn, cs, tmp], tmp[:])
        kb.TT(ar[:], mag[:], cs[:], ALU.mult, [mag, cs], [ar])
        kb.TT(ai[:], mag[:], sn[:], ALU.mult, [mag, sn], [ai])
        den = P.sbuf("s_den", [128, 16, 64]); fr = P.sbuf("s_fr", [128, 16, 64]); fi = P.sbuf("s_fi", [128, 16, 64])
        kb.TT(den[:], lr[:], lr[:], ALU.mult, [lr], [den])
        kb.TT(t2[:], li[:], li[:], ALU.mult, [li], [t2])
        kb.TT(den[:], den[:], t2[:], ALU.add, [den, t2], [den])
        kb.RECIP(den[:], den[:], [den], [den])
        kb.TS(ar[:], ar[:], -1.0, None, ALU.add, None, [ar], [ar])
        kb.TT(fr[:], ar[:], lr[:], ALU.mult, [ar, lr], [fr])
        kb.TT(t2[:], ai[:], li[:], ALU.mult, [ai, li], [t2])
        kb.TT(fr[:], fr[:], t2[:], ALU.add, [fr, t2], [fr])
        kb.TT(fr[:], fr[:], den[:], ALU.mult, [fr, den], [fr])
        kb.TT(fi[:], ai[:], lr[:], ALU.mult, [ai, lr], [fi])
        kb.TT(t2[:], ar[:], li[:], ALU.mult, [ar, li], [t2])
        kb.TT(fi[:], fi[:], t2[:], ALU.subtract, [fi, t2], [fi])
        kb.TT(fi[:], fi[:], den[:], ALU.mult, [fi, den], [fi])
        jcol = kb.C("CCOL")[:, 2:3] if d == 0 else kb.C("CCOL")[:, 3:4]
        njcol = kb.C("CCOL")[:, 6:7] if d == 0 else kb.C("CCOL")[:, 7:8]
        kb.ACT(mag[:], lrdt[:], AF.Exp, [lrdt], [mag], scale=njcol)
        kb.TS(ang[:], lidt[:], jcol, None, ALU.mult, None, [lidt], [ang])
        _sincos(kb, ang[:], sn[:], cs[:], [ang, sn, cs, tmp], tmp[:])
        vr, vi = ar, ai
        kb.TT(vr[:], mag[:], cs[:], ALU.mult, [mag, cs], [vr])
        kb.TT(vi[:], mag[:], sn[:], ALU.mult, [mag, sn], [vi])
        kb.TS(vi[:], vi[:], -1.0, None, ALU.mult, None, [vi], [vi])
        kb.TT(VFr[:], vr[:], fr[:], ALU.mult, [vr, fr], [VFr])
        kb.TT(t2[:], vi[:], fi[:], ALU.mult, [vi, fi], [t2])
        kb.TT(VFr[:], VFr[:], t2[:], ALU.subtract, [VFr, t2], [VFr])
        kb.TT(VFi[:], vr[:], fi[:], ALU.mult, [vr, fi], [VFi])
        kb.TT(t2[:], vi[:], fr[:], ALU.mult, [vi, fr], [t2])
        kb.TT(VFi[:], VFi[:], t2[:], ALU.add, [VFi, t2], [VFi])
    with P.scope():
        dtb = P.sbuf("s_dt2", [128, 16])
        kb.LD(dtb[:], prm["s5_log_dt"][l][d].partition_broadcast(128), [dtb])
        kb.ACT(dtb[:], dtb[:], AF.Exp, [dtb], [dtb])
        lrp = P.sbuf("s_lrp", [128, 16]); lip = P.sbuf("s_lip", [128, 16])
        for hh in range(2):
            kb.LD(lrp[64 * hh:64 * hh + 64, :], prm["s5_lam_re"][l][d].rearrange("g p -> p g"), [lrp],
                  allow_slow_non_contiguous=True)
            kb.LD(lip[64 * hh:64 * hh + 64, :], prm["s5_lam_im"][l][d].rearrange("g p -> p g"), [lip],
                  allow_slow_non_contiguous=True)
        kb.TT(lrp[:], lrp[:], dtb[:], ALU.mult, [lrp, dtb], [lrp])
        kb.TT(lip[:], lip[:], dtb[:], ALU.mult, [lip, dtb], [lip])
        b4 = [P.sbuf("s_b%d" % i, [128, 16, 128]) for i in range(4)]
        arg, sn2, cs2, tmp2 = b4
        mt = kb.C("IOTAF" if d == 0 else "R127F")
        mt_bc = mt.unsqueeze(1).to_broadcast([128, 16, 128])
        kb.TT(arg[:], lrp[:].unsqueeze(2).to_broadcast([128, 16, 128]), mt_bc, ALU.mult, [lrp], [arg])
        kb.ACT(T1[:], arg[:], AF.Exp, [arg], [T1])
        kb.TT(arg[:], lip[:].unsqueeze(2).to_broadcast([128, 16, 128]), mt_bc, ALU.mult, [lip, T1], [arg])
        _sincos(kb, arg[:], sn2[:], cs2[:], [arg, sn2, cs2, tmp2], tmp2[:])
        kb.TT(T2[:], T1[:], sn2[:], ALU.mult, [T1, sn2], [T2])
        kb.TS(T2[:], T2[:], -1.0, None, ALU.mult, None, [T2], [T2])
        kb.TT(T1[:], T1[:], cs2[:], ALU.mult, [T1, cs2], [T1])
        c4 = [P.sbuf("s_c%d" % i, [128, 16]) for i in range(4)]
        kb.ACT(c4[0][:], lrp[:], AF.Exp, [lrp], [c4[0]])
        _sincos(kb, lip[:], c4[1][:], c4[2][:], [lip, c4[1], c4[2], c4[3]], c4[3][:])
        kb.TT(AR[:], c4[0][:], c4[2][:], ALU.mult, [c4[0], c4[2]], [AR])
        kb.TT(NAI[:], c4[0][:], c4[1][:], ALU.mult, [c4[0], c4[1]], [NAI])
        kb.TS(NAI[:], NAI[:], -1.0, None, ALU.mult, None, [NAI], [NAI])


def mixer_s5_2(kb, l):
    P = kb.P
    prm = kb.prm
    with P.scope():
        WX = P.sbuf("s_WX", [128, 2, 8, 2, 64])
        Cblk = P.sbuf("s_Cblk", [128, 16, 128])
        kb.MS(WX[:], 0.0, [WX], eng="pool")
        kb.MS(Cblk[:], 0.0, [Cblk], eng="pool")
        for g8 in range(8):
            for ri, nm in enumerate(("s5_b_re", "s5_b_im")):
                for gg in range(2):
                    src = prm[nm][l][8 * gg + g8].rearrange("p c -> c p")
                    kb.LD(WX[16 * g8:16 * g8 + 16, gg, g8, ri, :], src, [WX], allow_slow_non_contiguous=True)
        for g in range(16):
            g8 = g % 8
            kb.LD(Cblk[0:64, g, 16 * g8:16 * g8 + 16], prm["s5_c_re"][l][g].rearrange("c p -> p c"), [Cblk],
                  allow_slow_non_contiguous=True)
            kb.LD(Cblk[64:128, g, 16 * g8:16 * g8 + 16], prm["s5_c_im"][l][g].rearrange("c p -> p c"), [Cblk],
                  allow_slow_non_contiguous=True)
        kb.TS(Cblk[64:128, :, :], Cblk[64:128, :, :], -1.0, None, ALU.mult, None, [Cblk], [Cblk])
        Cb16 = P.sbuf("s_Cb16", [128, 16, 128], BF16)
        kb.CP(Cb16[:], Cblk[:], [Cblk], [Cb16])

        tabs = []
        for d in range(2):
            VFr = P.sbuf("s_VFr%d" % d, [128, 16, 64]); VFi = P.sbuf("s_VFi%d" % d, [128, 16, 64])
            T1 = P.sbuf("s_T1%d" % d, [128, 16, 128]); T2 = P.sbuf("s_T2%d" % d, [128, 16, 128])
            AR = P.sbuf("s_AR%d" % d, [128, 16]); NAI = P.sbuf("s_NAI%d" % d, [128, 16])
            s5_tables(kb, l, d, VFr, VFi, T1, T2, AR, NAI)
            tabs.append((VFr, VFi, T1, T2, AR, NAI))
        OACC = P.sbuf("s_oacc", [128, 2, S])
        oacc_zero(kb, OACC)
        with P.scope():
            def make(d):
                pf = "s%d_" % d
                VFr, VFi, T1, T2, AR, NAI = tabs[d]
                uTb = [P.sbuf(pf + "u%d" % i, [128, 2, 128]) for i in range(2)]
                mm_ = [P.sbuf(pf + "m%d" % i, [128, 4, 64]) for i in range(4)]
                W3 = [P.sbuf(pf + "W3%d" % i, [128, 4, 3, 64], BF16) for i in range(2)]
                tP = P.sbuf(pf + "tP", [128, 4, 128]); tPs = P.sbuf(pf + "tPs", [128, 4, 128])
                H1 = P.sbuf(pf + "H1", [128, 4, 128]); H2 = P.sbuf(pf + "H2", [128, 4, 128])
                Hb = [P.sbuf(pf + "Hb%d" % i, [128, 4, 128], BF16) for i in range(2)]
                tri16 = P.sbuf(pf + "tri16", [128, 128], BF16)
                kb.CP(tri16[:], kb.C("TRIF" if d == 0 else "TRIB"), [], [tri16])
                hend = P.sbuf(pf + "hend", [128, 16]); hsend = P.sbuf(pf + "hsend", [128, 16])
                hp_ = P.sbuf(pf + "hp", [128, 16]); hps_ = P.sbuf(pf + "hps", [128, 16])
                sm = [P.sbuf(pf + "sm%d" % i, [128, 16]) for i in range(4)]
                kb.MS(hp_[:], 0.0, [hp_]); kb.MS(hps_[:], 0.0, [hps_])
                xps = P.psum(pf + "xps", [128, 512])
                pps = P.psum(pf + "pps", [128, 4, 128])
                ppss = P.psum(pf + "ppss", [128, 4, 128])
                yps = P.psum(pf + "yps", [128, 128])
                te = 127 if d == 0 else 0

                def gen():
                    it = 0
                    kq = 0
                    for n in ORDER[d]:
                        uT = uTb[it % 2]; it += 1
                        cols = slice(n * 128, (n + 1) * 128)
                        kb.LD(uT[:], kb.ZF[Z_SU:Z_SU + 256, cols].rearrange("(gg p) t -> p gg t", p=128), [uT])
                        yield
                        for q in range(4):
                            gg, qq = divmod(q, 2)
                            gs = slice(4 * q, 4 * q + 4)
                            w3 = W3[kq % 2]; hb = Hb[kq % 2]; kq += 1
                            kb.MM(xps[:], uT[:, gg, :], WX[:, gg, 4 * qq:4 * qq + 4, :, :].rearrange("q a r p -> q (a r p)"),
                                  True, True, [uT, WX], [xps])
                            xv = xps[:].rearrange("t (g r p) -> t g r p", r=2, p=64)
                            kb.TT(mm_[0][:], xv[:, :, 0, :], VFr[:, gs, :], ALU.mult, [xps, VFr], [mm_[0]])
                            kb.TT(mm_[1][:], xv[:, :, 1, :], VFi[:, gs, :], ALU.mult, [xps, VFi], [mm_[1]])
                            kb.TT(mm_[2][:], xv[:, :, 0, :], VFi[:, gs, :], ALU.mult, [xps, VFi], [mm_[2]])
                            kb.TT(mm_[3][:], xv[:, :, 1, :], VFr[:, gs, :], ALU.mult, [xps, VFr], [mm_[3]])
                            yield
                            kb.TT(w3[:, :, 0, :], mm_[0][:], mm_[1][:], ALU.subtract, [mm_[0], mm_[1]], [w3], eng="pool")
                            kb.TT(w3[:, :, 1, :], mm_[2][:], mm_[3][:], ALU.add, [mm_[2], mm_[3]], [w3], eng="pool")
                            kb.TT(w3[:, :, 2, :], mm_[1][:], mm_[0][:], ALU.subtract, [mm_[0], mm_[1]], [w3], eng="pool")
                            yield
                            for i in range(4):
                                kb.MM(pps[:, i, :], w3[:, i, 0:2, :].rearrange("q r p -> q (r p)"), tri16[:], True, True, [w3, tri16], [pps])
                            for i in range(4):
                                kb.MM(ppss[:, i, :], w3[:, i, 1:3, :].rearrange("q r p -> q (r p)"), tri16[:], True, True, [w3, tri16], [ppss])
                            yield
                            for i in range(4):
                                g = 4 * q + i
                                kb.ACT(tP[:, i, :], pps[:, i, :], AF.Identity, [pps, hp_], [tP], bias=hp_[:, g:g + 1])
                            for i in range(4):
                                g = 4 * q + i
                                kb.ACT(tPs[:, i, :], ppss[:, i, :], AF.Identity, [ppss, hps_], [tPs], bias=hps_[:, g:g + 1])
                            yield
                            kb.TT(sm[0][:, 0:4], tPs[:, :, te], T1[:, gs, te], ALU.mult, [tPs, T1], [sm[0]])
                            kb.TT(sm[1][:, 0:4], tP[:, :, te], T2[:, gs, te], ALU.mult, [tP, T2], [sm[1]])
                            kb.TT(hsend[:, gs], sm[0][:, 0:4], sm[1][:, 0:4], ALU.subtract, [sm[0], sm[1]], [hsend])
                            kb.TT(sm[2][:, 0:4], tP[:, :, te], T1[:, gs, te], ALU.mult, [tP, T1], [sm[2]])
                            kb.TT(sm[3][:, 0:4], tPs[:, :, te], T2[:, gs, te], ALU.mult, [tPs, T2], [sm[3]])
                            kb.TT(hend[:, gs], sm[2][:, 0:4], sm[3][:, 0:4], ALU.add, [sm[2], sm[3]], [hend])
                            yield
                            kb.TT(H1[:], tP[:], T1[:, gs, :], ALU.mult, [tP, T1], [H1], eng="pool")
                            kb.TT(H2[:], tPs[:], T2[:, gs, :], ALU.mult, [tPs, T2], [H2])
                            yield
                            kb.TT(hb[:], H1[:], H2[:], ALU.add, [H1, H2], [hb], eng="pool")
                            yield
                            for i in range(4):
                                g = 4 * q + i
                                kb.MM(yps[:], Cb16[:, g, :], hb[:, i, :], (g % 8) == 0, (g % 8) == 7, [Cb16, hb], [yps])
                            if qq == 1:
                                oacc_add(kb, OACC, gg, n, yps)
                            yield
                        kb.TT(sm[0][:], hend[:], AR[:], ALU.mult, [hend, AR], [sm[0]])
                        kb.TT(sm[1][:], hsend[:], NAI[:], ALU.mult, [hsend, NAI], [sm[1]])
                        kb.TT(sm[2][:], hsend[:], AR[:], ALU.mult, [hsend, AR], [sm[2]])
                        kb.TT(sm[3][:], hend[:], NAI[:], ALU.mult, [hend, NAI], [sm[3]])
                        kb.TT(hp_[:], sm[0][:], sm[1][:], ALU.add, [sm[0], sm[1]], [hp_])
                        kb.TT(hps_[:], sm[2][:], sm[3][:], ALU.subtract, [sm[2], sm[3]], [hps_])
                        yield
                return gen()
            run_interleaved([make(0), make(1)])
        with P.scope():
            dsk = P.sbuf("s_dsk", [128, 2]); glb = P.sbuf("s_glb", [128, 2])
            kb.LD(dsk[:], prm["s5_d"][l].rearrange("(gg p) -> p gg", p=128), [dsk], allow_slow_non_contiguous=True)
            kb.LD(glb[:], prm["s5_glu_b"][l].rearrange("(gg p) -> p gg", p=128), [glb], allow_slow_non_contiguous=True)
            gw = P.sbuf("s_gw", [128, 2, 256])
            kb.LD(gw[:], prm["s5_glu_w"][l].rearrange("(ct p) o -> p ct o", p=128), [gw])
            uTb = [P.sbuf("s_fu%d" % i, [128, 2, 128]) for i in range(2)]
            yy = [P.sbuf("s_yy%d" % i, [128, 2, 128]) for i in range(2)]
            x2 = [P.sbuf("s_x2%d" % i, [128, 2, 128]) for i in range(2)]
            th = [P.sbuf("s_th%d" % i, [128, 2, 128]) for i in range(2)]
            sgb = [P.sbuf("s_sg%d" % i, [128, 128]) for i in range(2)]
            ob = [P.sbuf("s_ob%d" % i, [128, 128]) for i in range(2)]
            psz = [P.psum("s_psz%d" % i, [128, 128]) for i in range(2)]
            k = 0
            for n in range(NT):
                cols = slice(n * 128, (n + 1) * 128)
                i = n % 2
                kb.LD(uTb[i][:], kb.ZF[Z_SU:Z_SU + 256, cols].rearrange("(gg p) t -> p gg t", p=128), [uTb[i]])
                for gg in range(2):
                    kb.STT(yy[i][:, gg, :], uTb[i][:, gg, :], dsk[:, gg:gg + 1], OACC[:, gg, cols], ALU.mult, ALU.add,
                           [uTb[i], dsk, OACC.s(n)], [yy[i]])
                kb.TT(x2[i][:], yy[i][:], yy[i][:], ALU.mult, [yy[i]], [x2[i]], eng="pool")
                kb.TS(x2[i][:], x2[i][:], 0.044715, 1.0, ALU.mult, ALU.add, [x2[i]], [x2[i]])
                kb.TT(x2[i][:], x2[i][:], yy[i][:], ALU.mult, [x2[i], yy[i]], [x2[i]], eng="pool")
                kb.ACT(th[i][:], x2[i][:], AF.Tanh, [x2[i]], [th[i]], scale=0.7978845608028654)
                kb.TS(th[i][:], th[i][:], 1.0, 0.5, ALU.add, ALU.mult, [th[i]], [th[i]])
                kb.TT(yy[i][:], yy[i][:], th[i][:], ALU.mult, [yy[i], th[i]], [yy[i]], eng="pool")
                for ot in range(2):
                    q = k % 2; k += 1
                    for ct in range(2):
                        kb.MM(psz[q][:], gw[:, ct, ot * 128:(ot + 1) * 128], yy[i][:, ct, :], ct == 0, ct == 1, [gw, yy[i]], [psz[q]])
                    kb.ACT(sgb[q][:], psz[q][:], AF.Sigmoid, [psz[q], glb], [sgb[q]], bias=glb[:, ot:ot + 1])
                    kb.TT(ob[q][:], yy[i][:, ot, :], sgb[q][:], ALU.mult, [yy[i], sgb[q]], [ob[q]])
                    kb.ST(kb.YC[768 + ot * 128:768 + (ot + 1) * 128, cols], ob[q][:], [ob[q]])


def _conv_win_masks():
    tp = np.arange(642) - 65
    w = np.mod(tp, 64)
    m = np.ones((2, 642), np.float32)
    m[0, w == 63] = 0.0
    m[1, w == 0] = 0.0
    return np.broadcast_to(m[None], (128, 2, 642)).copy()


def gdn_conv2(kb, l):
    P = kb.P
    with P.scope():
        CW = P.sbuf("g_cw", [128, 6, 9])
        for kh in range(3):
            for kw in range(3):
                kb.LD(CW[:, :, kh * 3 + kw], kb.prm["gdn_conv_w"][l][kh, kw].rearrange("(ct p) -> p ct", p=128), [CW],
                      allow_slow_non_contiguous=True)
        DW = P.sbuf("g_dw", [128, 6, 9, 128])
        for ct in range(6):
            for tp_ in range(9):
                kb.TS(DW[:, ct, tp_, :], kb.C("IDENT"), CW[:, ct, tp_:tp_ + 1], None, ALU.mult, None, [CW], [DW],
                      eng=("pool" if tp_ % 2 else "dve"))
        wm = P.sbuf("g_wm", [128, 2, 642])
        kb.LD(wm[:], kb.cwin[:], [wm])
        Wb = [P.sbuf("g_w%d" % i, [128, 642]) for i in range(2)]
        WLb = [P.sbuf("g_wl%d" % i, [128, 642]) for i in range(2)]
        WRb = [P.sbuf("g_wr%d" % i, [128, 642]) for i in range(2)]
        sl = [P.sbuf("g_sl%d" % i, [128, 512]) for i in range(2)]
        sq = [P.sbuf("g_sq%d" % i, [128, 512]) for i in range(2)]
        rt = [P.sbuf("g_rt%d" % i, [128, 512]) for i in range(2)]
        psc = [P.psum("g_psc%d" % i, [128, 512]) for i in range(2)]
        ps = [P.psum("g_psn%d" % i, [128, 512]) for i in range(2)]
        spans = [(0, 256, True)] + [(256 + 512 * k, 512, False) for k in range(8)]
        it = 0
        for (t0, L, is_ctx) in spans:
            lo = 0 if is_ctx else 256
            hi = 256 if is_ctx else S
            a = max(lo, t0 - 65); b = min(hi, t0 + L + 65)
            for ct in range(6):
                i = it % 2; it += 1
                W = Wb[i]
                kb.MS(W[:], 0.0, [W], eng="pool")
                kb.LD(W[:, 65 + (a - t0):65 + (b - t0)], kb.ZF[Z_GQKV + ct * 128:Z_GQKV + (ct + 1) * 128, a:b], [W])
                if is_ctx:
                    WL = WR = W
                    rows = (1,)
                else:
                    WL, WR = WLb[i], WRb[i]
                    kb.TT(WL[:], W[:], wm[:, 0, :], ALU.mult, [W, wm], [WL])
                    kb.TT(WR[:], W[:], wm[:, 1, :], ALU.mult, [W, wm], [WR], eng="pool")
                    rows = (0, 1, 2)
                pc = psc[i]
                taps = [(dh, dwi) for dh in rows for dwi in range(3)]
                for q, (dh, dwi) in enumerate(taps):
                    srcT = (WL, W, WR)[dwi]
                    o0 = 65 + 64 * (dh - 1) + (dwi - 1)
                    kb.MM(pc[:, :L], DW[:, ct, dh * 3 + dwi, :], srcT[:, o0:o0 + L], q == 0, q == len(taps) - 1, [DW, srcT], [pc])
                kb.ACT(sl[i][:, :L], pc[:, :L], AF.Silu, [pc], [sl[i]])
                if ct < 4:
                    kb.TT(sq[i][:, :L], sl[i][:, :L], sl[i][:, :L], ALU.mult, [sl[i]], [sq[i]])
                    kb.MM(ps[i][:, :L], kb.C("BLK64"), sq[i][:, :L], True, True, [sq[i]], [ps[i]])
                    kb.ACT(rt[i][:, :L], ps[i][:, :L], AF.Sqrt, [ps[i]], [rt[i]], bias=kb.C("CCOL")[:, 0:1])
                    kb.RECIP(rt[i][:, :L], rt[i][:, :L], [rt[i]], [rt[i]])
                    if ct < 2:
                        kb.STT(sl[i][:, :L], sl[i][:, :L], 0.125, rt[i][:, :L], ALU.mult, ALU.mult, [sl[i], rt[i]], [sl[i]])
                    else:
                        kb.TT(sl[i][:, :L], sl[i][:, :L], rt[i][:, :L], ALU.mult, [sl[i], rt[i]], [sl[i]], eng="pool")
                kb.ST(kb.QKVF[ct * 128:(ct + 1) * 128, t0:t0 + L], sl[i][:, :L], [sl[i]])
```

```python
import numpy as np
import concourse.bass as bass
import concourse.mybir as mybir
from concourse.bass_utils import run_bass_kernel_spmd
from contextlib import ExitStack

F32 = mybir.dt.float32
BF16 = mybir.dt.bfloat16
AF = mybir.ActivationFunctionType
ALU = mybir.AluOpType

ENGS = ("pe", "act", "dve", "pool", "sp")
EPOCH = 16000
N_DMA_SEM = 32


class Buf:
    __slots__ = ("name", "w", "r", "excl", "pe_partial")

    def __init__(self, name="", excl=False):
        self.name = name
        self.w = None
        self.r = []
        self.excl = excl
        self.pe_partial = False


class T:
    def __init__(self, h, name, excl=False):
        self.h = h
        self.name = name
        self.b = Buf(name, excl)
        self.excl = excl
        self.subs = {}

    def __getitem__(self, k):
        return self.h[k]

    def s(self, key):
        if self.excl:
            return self.b
        if key not in self.subs:
            self.subs[key] = Buf("%s.%s" % (self.name, key))
        return self.subs[key]


class Prog:
    def __init__(self, nc):
        self.nc = nc
        self.es = ExitStack()
        self.stack = [self.es]
        self.ops = {e: [] for e in ENGS}
        self.cnt = {e: 0 for e in ENGS}
        self.seen = {e: {} for e in ENGS}
        self.last = {}
        self.dma_k = 0
        self.dma_use = [0] * N_DMA_SEM
        self.dma_sems = [self.es.enter_context(nc.semaphore("dq%d" % i)) for i in range(N_DMA_SEM)]
        self.eng_sems = {}
        self.out_tokens = []
        self.n_ops = 0
        self.uid = 0

    def _nm(self, name):
        self.uid += 1
        return "%s_%d" % (name, self.uid)

    def sbuf(self, name, shape, dt=F32):
        h = self.stack[-1].enter_context(self.nc.sbuf_tensor(self._nm(name), list(shape), dt))
        return T(h, name)

    def psum(self, name, shape, dt=F32):
        n = 1
        for d_ in shape[1:]:
            n *= d_
        nb = (n * 4 + 2047) // 2048
        h = self.stack[-1].enter_context(self.nc.psum_tensor(self._nm(name), [128, nb * 512], F32))
        v = h[0:shape[0], 0:n]
        if len(shape) == 3:
            v = v.rearrange("p (a b) -> p a b", a=shape[1])
        elif len(shape) == 4:
            v = v.rearrange("p (a b c) -> p a b c", a=shape[1], b=shape[2])
        return T(v, name, excl=True)

    def dram(self, name, shape, dt=F32, kind="Internal"):
        h = self.nc.dram_tensor(name, list(shape), dt, kind=kind)
        return T(h.ap(), name)

    class _Scope:
        def __init__(self, p):
            self.p = p

        def __enter__(self):
            st = ExitStack()
            self.p.stack.append(st)
            return st

        def __exit__(self, *a):
            self.p.barrier()
            st = self.p.stack.pop()
            st.close()
            return False

    def scope(self):
        return Prog._Scope(self)

    def _eng_sem(self, e, epoch):
        k = (e, epoch)
        if k not in self.eng_sems:
            self.eng_sems[k] = self.es.enter_context(self.nc.semaphore("s_%s_%d" % (e, epoch)))
        return self.eng_sems[k]

    def _waits(self, eng, reads, writes, extra=(), skip_pe=False):
        need = {}

        def add(tok):
            if tok is None:
                return
            key, val = tok
            if need.get(key, 0) < val:
                need[key] = val
        for b in reads:
            add(b.w)
        for b in writes:
            add(b.w)
            for t in b.r:
                add(t)
        for t in extra:
            add(t)
        out = []
        seen = self.seen[eng]
        for key, val in need.items():
            if skip_pe and key[0] == "e" and key[1] == "pe":
                continue
            if seen.get(key, 0) < val:
                seen[key] = val
                out.append((key, val))
        return out

    @staticmethod
    def _bufs(xs):
        out = []
        for x in xs:
            if x is None:
                continue
            out.append(x.b if isinstance(x, T) else x)
        return out

    def _commit(self, tok, reads, writes):
        self.last[tok[0]] = tok[1]
        for b in reads:
            b.r.append(tok)
            if len(b.r) > 64:
                mx = {}
                for k, v in b.r:
                    if mx.get(k, 0) < v:
                        mx[k] = v
                b.r = list(mx.items())
        for b in writes:
            b.w = tok
            b.r = []
        self.n_ops += 1

    def op(self, eng, fn, reads=(), writes=(), partial=False):
        reads = self._bufs(reads)
        writes = self._bufs(writes)
        ex = [b for b in reads if b.excl]
        if ex:
            reads = [b for b in reads if not b.excl]
            writes = writes + [b for b in ex if b not in writes]
        skip_pe = False
        if eng == "pe":
            skip_pe = (not partial) and all(not b.pe_partial for b in writes)
            for b in writes:
                b.pe_partial = partial
        waits = self._waits(eng, reads, writes, skip_pe=skip_pe)
        self.cnt[eng] += 1
        epoch, val = divmod(self.cnt[eng] - 1, EPOCH)
        tok = (("e", eng, epoch), val + 1)
        self.ops[eng].append((waits, fn, tok))
        self._commit(tok, reads, writes)
        return tok

    def dma(self, out_ap, in_ap, reads=(), writes=(), q="sp", is_output=False, **kw):
        reads = self._bufs(reads)
        writes = self._bufs(writes)
        i = self.dma_k % N_DMA_SEM
        self.dma_k += 1
        prev = self.dma_use[i]
        extra = [(("d", i), 16 * prev)] if prev else []
        waits = self._waits(q, reads, writes, extra)
        self.dma_use[i] = prev + 1
        tok = (("d", i), 16 * (prev + 1))

        def fn(e):
            return e.dma_start(out=out_ap, in_=in_ap, **kw)
        self.ops[q].append((waits, fn, tok))
        self._commit(tok, reads, writes)
        if is_output:
            self.out_tokens.append(tok)
        return tok

    def barrier(self):
        toks = list(self.last.items())
        for e in ENGS:
            waits = self._waits(e, [], [], toks)
            if waits:
                self.ops[e].append((waits, None, None))

    def _sem_of(self, key):
        if key[0] == "d":
            return self.dma_sems[key[1]]
        return self._eng_sem(key[1], key[2])

    def emit(self):
        nc = self.nc
        self.barrier()
        for e in ENGS:
            for waits, fn, tok in self.ops[e]:
                if tok is not None:
                    self._sem_of(tok[0])
                for key, val in waits:
                    self._sem_of(key)
        with nc.Block() as block:
            def run(e, handle):
                for waits, fn, tok in self.ops[e]:
                    for key, val in waits:
                        handle.wait_ge(self._sem_of(key), val)
                    if fn is None:
                        continue
                    ins = fn(handle)
                    key, val = tok
                    ins.then_inc(self._sem_of(key), 16 if key[0] == "d" else 1)

            @block.sync
            def _(h):
                run("sp", h)

            @block.tensor
            def _(h):
                run("pe", h)

            @block.scalar
            def _(h):
                run("act", h)

            @block.vector
            def _(h):
                run("dve", h)

            @block.gpsimd
            def _(h):
                run("pool", h)

    def close(self):
        self.es.close()


D = 1024
S = 4352
NT = 34
LAT0 = 256
DEPTH = 2
EPS = 1e-6
NEG = -30000.0
ORDER = [list(range(NT)), [1, 0] + list(range(NT - 1, 1, -1))]

C_HQ, C_HI, C_HG, C_HFF, C_HFB = 0, 256, 512, 768, 1024
C_RQ, C_RK, C_RV, C_RG = 1280, 1536, 1792, 2048
C_GQKV, C_GG, C_GA, C_GB, C_SU = 2304, 3072, 3328, 3336, 3344
Z_HQ, Z_HG, Z_HFF, Z_HFB, Z_RQ, Z_RK, Z_RG, Z_GQKV, Z_GG, Z_SU = 0, 256, 512, 768, 1024, 1280, 1536, 1792, 2560, 2816
NZF = 3072
FM_MAP = [(Z_HQ, C_HQ, 256), (Z_HG, C_HG, 256), (Z_HFF, C_HFF, 256), (Z_HFB, C_HFB, 256), (Z_RQ, C_RQ, 256),
          (Z_RK, C_RK, 256), (Z_RG, C_RG, 256), (Z_GQKV, C_GQKV, 768), (Z_GG, C_GG, 256), (Z_SU, C_SU, 256)]
FM_BLOCKS = [(zr + i, wc + i) for zr, wc, n in FM_MAP for i in range(0, n, 128)]
NZT = 528

CN = {}


def _const_pack():
    mats = []

    def add(name, m):
        CN[name] = len(mats)
        mats.append(np.asarray(m, np.float32))
    p = np.arange(128)[:, None]
    f = np.arange(128)[None, :]
    add("IDENT", (p == f))
    add("ONES", np.ones((128, 128)))
    add("TRIF", (p <= f))
    add("TRIB", (p >= f))
    add("SUFF", (p > f))
    add("PREB", (p < f))
    add("NLE", np.where(p <= f, 0.0, NEG))
    add("NLT", np.where(p < f, 0.0, NEG))
    add("NGE", np.where(p >= f, 0.0, NEG))
    add("NGT", np.where(p > f, 0.0, NEG))
    for s in (1, 2, 4, 8, 16, 32, 64):
        m = (((p // s) % 2) == 1) & ((f // s) == (p // s) - 1)
        add("MOFF%d" % s, m)
        add("MOFFT%d" % s, m.T)
    add("BLK64", (p // 64) == (f // 64))
    rot = np.zeros((128, 128))
    for m in range(128):
        if (m % 64) < 32:
            rot[m + 32, m] = -1.0
        else:
            rot[m - 32, m] = 1.0
    add("ROT", rot)
    add("IOTAF", np.broadcast_to(f, (128, 128)))
    add("IOTAF1", np.broadcast_to(f + 1, (128, 128)))
    add("RIOTAF", np.broadcast_to(128 - f, (128, 128)))
    add("R127F", np.broadcast_to(127 - f, (128, 128)))
    add("DIFF", f - p)
    add("NDIFF", p - f)
    for h in range(4):
        m = np.zeros((128, 128)); m[h, :] = 1.0
        add("SELH%d" % h, m)
    for hp in range(2):
        m = np.zeros((128, 128)); m[2 * hp, 0:64] = 1.0; m[2 * hp + 1, 64:128] = 1.0
        add("SELP%d" % hp, m)
    cc = np.zeros((128, 128))
    cc[:, 0] = EPS; cc[:, 1] = 1.0; cc[:, 2] = np.arange(128); cc[:, 3] = 127 - np.arange(128)
    cc[:, 5] = -np.pi; cc[:, 6] = -np.arange(128); cc[:, 7] = -(127 - np.arange(128))
    add("CCOL", cc)
    gm = np.zeros((128, 128))
    for g in range(16):
        gm[(g % 8) * 16:(g % 8) * 16 + 16, g] = 1.0
    add("GMASK", gm)
    return np.concatenate(mats, axis=1)


CONST_NP = _const_pack()
NCONST = CONST_NP.shape[1] // 128


def _rope_tables():
    half = 32
    inv = 10000.0 ** (-np.arange(half, dtype=np.float64) / half)
    pos = np.arange(S, dtype=np.float64)
    ang = pos[None, :] * inv[:, None]
    cos = np.cos(ang); sin = np.sin(ang)
    cos128 = np.tile(cos, (4, 1)); sin128 = np.tile(sin, (4, 1))
    return cos128.astype(np.float32), sin128.astype(np.float32)


def _conv_masks():
    m = np.ones((2, 512), np.float32)
    w = np.arange(512) % 64
    m[0, w == 0] = 0.0
    m[1, w == 63] = 0.0
    lat = np.broadcast_to(m[None], (128, 2, 512)).copy()
    c = np.ones((2, 256), np.float32)
    c[0, 0] = 0.0
    c[1, 255] = 0.0
    ctx = np.broadcast_to(c[None], (128, 2, 256)).copy()
    return lat, ctx


class KB:
    def __init__(self, cfg):
        self.cfg = cfg
        nc = bass.Bass("TRN2", target_bir_lowering=False)
        self.nc = nc
        self.P = Prog(nc)
        self.rr = 0

    def MM(self, ps, lhsT, rhs, st, sp, R, W):
        partial = lhsT.partition_size() < 128
        self.P.op("pe", lambda e: e.matmul(ps, lhsT, rhs, start=st, stop=sp), R, W, partial=partial)

    def TR(self, ps, in_, ident, R, W):
        self.P.op("pe", lambda e: e.transpose(ps, in_, ident), R, W)

    def ACT(self, out, in_, func, R, W, **kw):
        self.P.op("act", lambda e: e.activation(out=out, in_=in_, func=func, **kw), R, W)

    def TS(self, out, in0, s1, s2, op0, op1, R, W, eng="dve"):
        if s2 is None:
            self.P.op(eng, lambda e: e.tensor_scalar(out=out, in0=in0, scalar1=s1, scalar2=None, op0=op0), R, W)
        else:
            self.P.op(eng, lambda e: e.tensor_scalar(out=out, in0=in0, scalar1=s1, scalar2=s2, op0=op0, op1=op1), R, W)

    def TT(self, out, in0, in1, op, R, W, eng="dve"):
        self.P.op(eng, lambda e: e.tensor_tensor(out=out, in0=in0, in1=in1, op=op), R, W)

    def STT(self, out, in0, sc, in1, op0, op1, R, W, eng="dve"):
        eng = "dve"
        self.P.op(eng, lambda e: e.scalar_tensor_tensor(out=out, in0=in0, scalar=sc, in1=in1, op0=op0, op1=op1), R, W)

    def CP(self, out, in_, R, W, eng="dve"):
        if eng == "act":
            self.ACT(out, in_, AF.Copy, R, W)
        else:
            self.P.op(eng, lambda e: e.tensor_copy(out=out, in_=in_), R, W)

    def CPRED(self, out, mask, data, R, W):
        self.P.op("dve", lambda e: e.copy_predicated(out=out, mask=mask, data=data), R, W)

    def MS(self, ap, val, W, eng="dve"):
        self.P.op(eng, lambda e: e.memset(ap, val), (), W)

    def RECIP(self, out, in_, R, W):
        self.P.op("dve", lambda e: e.reciprocal(out=out, in_=in_), R, W)

    def SCAN(self, out, d0, d1, R, W):
        self.P.op("dve", lambda e: e.tensor_tensor_scan(out=out, data0=d0, data1=d1, initial=0.0,
                                                        op0=ALU.mult, op1=ALU.add), R, W)

    def LD(self, out, in_, W, R=(), q="sp", **kw):
        self.P.dma(out, in_, reads=R, writes=W, q=q, **kw)

    def ST(self, out, in_, R, W=(), q="pool", **kw):
        self.P.dma(out, in_, reads=R, writes=W, q=q, **kw)

    def evac_eng(self):
        self.rr += 1
        return "act" if self.rr % 2 else "dve"

    def C(self, name):
        i = CN[name]
        return self.const[:, i * 128:(i + 1) * 128]


PARAM_SHAPES = {
    "mod_w": [2, 1024, 6144], "mod_b": [2, 6144], "norm1_g": [2, 1024], "norm2_g": [2, 1024],
    "w_in": [2, 1024, 3600], "hgrn_lb_logits": [2, 2, 256], "hgrn_norm_g": [2, 64],
    "ret_decay_logit": [2, 2, 4], "gdn_conv_w": [2, 3, 3, 768], "gdn_a_log": [2, 2, 4],
    "gdn_dt_bias": [2, 2, 4], "gdn_norm_g": [2, 64], "s5_lam_re": [2, 2, 16, 64],
    "s5_lam_im": [2, 2, 16, 64], "s5_log_dt": [2, 2, 16], "s5_b_re": [2, 16, 64, 16],
    "s5_b_im": [2, 16, 64, 16], "s5_c_re": [2, 16, 16, 64], "s5_c_im": [2, 16, 16, 64],
    "s5_d": [2, 256], "s5_glu_w": [2, 256, 256], "s5_glu_b": [2, 256], "w_out": [2, 1024, 1024],
    "mlp_w1": [2, 1024, 4096], "mlp_w2": [2, 4096, 1024], "final_norm_g": [1024],
}


def declare(kb):
    P = kb.P
    cfg = kb.cfg
    kinds = cfg.get("kinds", {})
    kb.xin = P.dram("xin", [S, D], F32, kind="ExternalInput")
    kb.cvecT = P.dram("cvecT", [1024, 2], F32, kind="ExternalInput")
    kb.prm = {k: P.dram(k, shp, F32, kind="ExternalInput") for k, shp in PARAM_SHAPES.items()}
    kb.constd = P.dram("constp", [128, NCONST * 128], F32, kind="ExternalInput")
    kb.ropec = P.dram("ropec", [128, S], F32, kind="ExternalInput")
    kb.ropes = P.dram("ropes", [128, S], F32, kind="ExternalInput")
    kb.cmlat = P.dram("cmlat", [128, 2, 512], F32, kind="ExternalInput")
    kb.cmctx = P.dram("cmctx", [128, 2, 256], F32, kind="ExternalInput")
    kb.cwin = P.dram("cwin", [128, 2, 642], F32, kind="ExternalInput")
    kb.y = P.dram("y", [4096, D], F32, kind="ExternalOutput")
    kb.XS = P.dram("XS", [S, D], F32, kind=kinds.get("XS", "Internal"))
    kb.ZF = P.dram("ZF", [NZF, S], F32, kind=kinds.get("ZF", "Internal"))
    kb.ZT = P.dram("ZT", [S, NZT], F32, kind=kinds.get("ZT", "Internal"))
    kb.QKVF = P.dram("QKVF", [768, S], F32, kind=kinds.get("QKVF", "Internal"))
    kb.YC = P.dram("YC", [1024, S], F32, kind=kinds.get("YC", "Internal"))
    kb.H2T = P.dram("H2T", [1024, S], BF16, kind=kinds.get("H2T", "Internal"))
    kb.const = P.sbuf("const", [128, NCONST * 128])
    nchunk = 4
    w = NCONST * 128 // nchunk
    for i in range(nchunk):
        a, b = i * w, (i + 1) * w if i < nchunk - 1 else NCONST * 128
        kb.LD(kb.const[:, a:b], kb.constd[:, a:b], [kb.const.s(i)])
    kb.const_bufs = [kb.const.s(i) for i in range(nchunk)]
    kb.CB = kb.const_bufs
    kb.GS1 = P.sbuf("GS1", [128, 8, 2]); kb.SH1 = P.sbuf("SH1", [128, 8, 2])
    kb.GS2 = P.sbuf("GS2", [128, 8, 2]); kb.SH2 = P.sbuf("SH2", [128, 8, 2])
    kb.GATE1 = P.sbuf("GATE1", [128, 2, 1024]); kb.GATE2 = P.sbuf("GATE2", [128, 2, 1024])


def phase_mod(kb, l):
    P = kb.P
    prm = kb.prm
    with P.scope():
        cT = P.sbuf("cT", [128, 8, 2])
        kb.LD(cT[:], kb.cvecT[:].rearrange("(et e) c -> e et c", e=128), [cT])
        sc = P.sbuf("sc", [128, 8, 2])
        kb.ACT(sc[:], cT[:], AF.Silu, [cT], [sc])
        screp = P.sbuf("screp", [128, 8, 2, 128])
        kb.CP(screp[:], sc[:].unsqueeze(3).to_broadcast([128, 8, 2, 128]), [sc], [screp])
        mbf = P.sbuf("mbf", [128, 48])
        kb.LD(mbf[:], prm["mod_b"][l].rearrange("(j p) -> p j", p=128), [mbf], allow_slow_non_contiguous=True)
        ngf = P.sbuf("ngf", [128, 2, 8])
        kb.LD(ngf[:, 0, :], prm["norm1_g"][l].rearrange("(j p) -> p j", p=128), [ngf], allow_slow_non_contiguous=True)
        kb.LD(ngf[:, 1, :], prm["norm2_g"][l].rearrange("(j p) -> p j", p=128), [ngf], allow_slow_non_contiguous=True)
        mbrow = P.sbuf("mbrow", [128, 2, 1024])
        for gi, v in enumerate((2, 5)):
            kb.LD(mbrow[:, gi, :], prm["mod_b"][l][v * 1024:(v + 1) * 1024].partition_broadcast(128), [mbrow])
        wch = [P.sbuf("wch%d" % i, [128, 8, 1024]) for i in range(2)]
        ps_fm = P.psum("ps_fm", [128, 96])
        ps_g = [P.psum("ps_g%d" % i, [128, 512]) for i in range(2)]
        MF = P.sbuf("MF", [128, 48, 2])
        k = 0
        for v in range(6):
            wc = wch[v % 2]
            for et in range(8):
                kb.LD(wc[:, et, :], prm["mod_w"][l][et * 128:(et + 1) * 128, v * 1024:(v + 1) * 1024], [wc])
            for db in range(8):
                col = (v * 8 + db) * 2
                for et in range(8):
                    kb.MM(ps_fm[:, col:col + 2], wc[:, et, db * 128:(db + 1) * 128], sc[:, et, :],
                          et == 0, et == 7, [wc, sc], [ps_fm])
            if v in (2, 5):
                gt = kb.GATE1 if v == 2 else kb.GATE2
                gi = 0 if v == 2 else 1
                for which in range(2):
                    for half in range(2):
                        pg = ps_g[k % 2]; k += 1
                        for et in range(8):
                            kb.MM(pg[:], screp[:, et, which, :], wc[:, et, half * 512:(half + 1) * 512],
                                  et == 0, et == 7, [screp, wc], [pg])
                        kb.TT(gt[:, which, half * 512:(half + 1) * 512], pg[:], mbrow[:, gi, half * 512:(half + 1) * 512],
                              ALU.add, [pg, mbrow], [gt])
        kb.TT(MF[:], ps_fm[:].rearrange("p (j c) -> p j c", c=2), mbf[:].unsqueeze(2).to_broadcast([128, 48, 2]),
              ALU.add, [ps_fm, mbf], [MF])
        tmp = P.sbuf("mtmp", [128, 8, 2])
        kb.TS(tmp[:], MF[:, 8:16, :], 1.0, None, ALU.add, None, [MF], [tmp])
        kb.TT(kb.GS1[:], tmp[:], ngf[:, 0, :].unsqueeze(2).to_broadcast([128, 8, 2]), ALU.mult, [tmp, ngf], [kb.GS1])
        kb.CP(kb.SH1[:], MF[:, 0:8, :], [MF], [kb.SH1])
        tmp2 = P.sbuf("mtmp2", [128, 8, 2])
        kb.TS(tmp2[:], MF[:, 32:40, :], 1.0, None, ALU.add, None, [MF], [tmp2])
        kb.TT(kb.GS2[:], tmp2[:], ngf[:, 1, :].unsqueeze(2).to_broadcast([128, 8, 2]), ALU.mult, [tmp2, ngf], [kb.GS2])
        kb.CP(kb.SH2[:], MF[:, 24:32, :], [MF], [kb.SH2])


def norm_to_fm(kb, xt, hT, col0, GS, SH, which, bufs, R_x):
    P = kb.P
    junk, st, xn, ps_ts = bufs["junk"], bufs["st"], bufs["xn"], bufs["ps_t"]
    kb.MS(st[:, 0:1], 0.0, [st])
    kb.ACT(junk[:], xt[:], AF.Square, [xt], [junk, st], accum_out=st[:, 0:1])
    kb.ACT(st[:, 1:2], st[:, 0:1], AF.Sqrt, [st] + kb.CB, [st], scale=1.0 / D, bias=kb.C("CCOL")[:, 0:1])
    kb.RECIP(st[:, 2:3], st[:, 1:2], [st], [st])
    kb.ACT(xn[:], xt[:], AF.Copy, [xt, st], [xn], scale=st[:, 2:3])
    for half in range(2):
        ps_t = ps_ts[half]
        for q in range(4):
            dt = half * 4 + q
            kb.TR(ps_t[:, q * 128:(q + 1) * 128], xn[:, dt * 128:(dt + 1) * 128], kb.C("IDENT"), [xn] + kb.CB, [ps_t])
        for q in range(4):
            dt = half * 4 + q
            if q % 2 == 0:
                kb.TS(hT[:, dt, col0:col0 + 128], ps_t[:, q * 128:(q + 1) * 128], GS[:, dt, which:which + 1],
                      SH[:, dt, which:which + 1], ALU.mult, ALU.add, [ps_t, GS, SH], [hT])
            else:
                kb.ACT(hT[:, dt, col0:col0 + 128], ps_t[:, q * 128:(q + 1) * 128], AF.Identity, [ps_t, GS, SH], [hT],
                       scale=GS[:, dt, which:which + 1], bias=SH[:, dt, which:which + 1])


def phase_a(kb, l, src):
    P = kb.P
    with P.scope():
        win = P.sbuf("win", [128, 8, 3600], BF16)
        for kt in range(8):
            kb.LD(win[:, kt, :], kb.prm["w_in"][l][kt * 128:(kt + 1) * 128, :], [win.s(kt)], q="pool")
        winb = [win.s(kt) for kt in range(8)]
        xbuf = [P.sbuf("xa%d" % i, [128, 1024]) for i in range(2)]
        hTb = [P.sbuf("hTa%d" % i, [128, 8, 512], BF16) for i in range(2)]
        nb = {"junk": P.sbuf("junk", [128, 1024]), "st": P.sbuf("st", [128, 4]), "xn": P.sbuf("xn", [128, 1024]),
              "ps_t": [P.psum("ps_t%d" % i, [128, 512]) for i in range(2)]}
        ps_f = [P.psum("ps_f%d" % i, [128, 512]) for i in range(3)]
        ps_a = [P.psum("ps_a%d" % i, [128, 512]) for i in range(2)]
        ps_b = P.psum("ps_b", [128, 16])
        stg = [P.sbuf("stg%d" % i, [128, 512]) for i in range(4)]
        stt = [P.sbuf("stt%d" % i, [128, NZT]) for i in range(2)]
        kx = kf = ks = ka = 0
        for gi, t0 in enumerate(range(0, S, 512)):
            n = min(512, S - t0)
            hT = hTb[gi % 2]
            for ti in range(n // 128):
                tt = t0 // 128 + ti
                which = 1 if tt < 2 else 0
                xt = xbuf[kx % 2]; kx += 1
                kb.LD(xt[:], src[tt * 128:(tt + 1) * 128, :], [xt])
                norm_to_fm(kb, xt, hT, ti * 128, kb.GS1, kb.SH1, which, nb, None)
            for (zr, wc) in FM_BLOCKS:
                ps = ps_f[kf % 3]; kf += 1
                for kt in range(8):
                    kb.MM(ps[:, :n], win[:, kt, wc:wc + 128], hT[:, kt, :n], kt == 0, kt == 7, [winb[kt], hT], [ps])
                sg = stg[ks % 4]; ks += 1
                kb.CP(sg[:, :n], ps[:, :n], [ps], [sg], eng=kb.evac_eng())
                kb.ST(kb.ZF[zr:zr + 128, t0:t0 + n], sg[:, :n], [sg])
            for ti in range(n // 128):
                tt = t0 // 128 + ti
                pa = ps_a[ka % 2]
                so = stt[ka % 2]; ka += 1
                for (c0, w0, wn) in ((0, C_HI, 256), (256, C_RV, 256)):
                    for kt in range(8):
                        kb.MM(pa[:, c0:c0 + wn], hT[:, kt, ti * 128:(ti + 1) * 128], win[:, kt, w0:w0 + wn],
                              kt == 0, kt == 7, [winb[kt], hT], [pa])
                for kt in range(8):
                    kb.MM(ps_b[:], hT[:, kt, ti * 128:(ti + 1) * 128], win[:, kt, C_GA:C_GA + 16],
                          kt == 0, kt == 7, [winb[kt], hT], [ps_b])
                kb.CP(so[:, 0:512], pa[:], [pa], [so], eng="act")
                kb.CP(so[:, 512:528], ps_b[:], [ps_b], [so], eng="dve")
                kb.ST(kb.ZT[tt * 128:(tt + 1) * 128, :], so[:], [so])


def build(cfg):
    kb = KB(cfg)
    P = kb.P
    declare(kb)
    P.barrier()
    stages = cfg.get("stages", "all")
    for l in cfg.get("layers", range(DEPTH)):
        src = kb.xin if l == 0 else kb.XS
        if stages == "all" or "M" in stages:
            phase_mod(kb, l)
        if stages == "all" or "A" in stages:
            phase_a(kb, l, src)
        if stages == "all" or "R" in stages:
            (mixer_ret if cfg.get("ret_old") else mixer_ret2)(kb, l)
        if stages == "all" or "H" in stages:
            (mixer_hgrn if cfg.get("hgrn_old") else mixer_hgrn2)(kb, l)
        if stages == "all" or "G" in stages:
            (mixer_gdn if cfg.get("gdn_old") else mixer_gdn2)(kb, l)
        if stages == "all" or "S" in stages:
            (mixer_s5 if cfg.get("s5_old") else mixer_s5_2)(kb, l)
        if stages == "all" or "C" in stages:
            phase_c(kb, l, src)
    P.emit()
    P.close()
    return kb


_CONSTS = None


def host_inputs(inputs, cores=range(8)):
    global _CONSTS
    if _CONSTS is None:
        rc, rs = _rope_tables()
        cl, cc = _conv_masks()
        _CONSTS = {"constp": CONST_NP, "ropec": rc, "ropes": rs, "cmlat": cl, "cmctx": cc, "cwin": _conv_win_masks()}
    maps = []
    for b in cores:
        m = {"xin": np.ascontiguousarray(np.concatenate([inputs["ctx"][b], inputs["x"][b]], axis=0), dtype=np.float32),
             "cvecT": np.ascontiguousarray(np.stack([inputs["c"][b], inputs["c_ctx"]], axis=1), dtype=np.float32)}
        for k in PARAM_SHAPES:
            m[k] = np.ascontiguousarray(inputs[k], dtype=np.float32)
        m.update(_CONSTS)
        maps.append(m)
    return maps


def kernel(**inputs):
    inputs = {k: np.asarray(v) for k, v in inputs.items()}
    kb = build({})
    maps = host_inputs(inputs)
    res = run_bass_kernel_spmd(kb.nc, maps, core_ids=list(range(8)))
    out = np.stack([np.asarray(r["y"]).reshape(4096, D) for r in res.results], axis=0)
    return out.astype(np.float32)


def phase_c(kb, l, src):
    P = kb.P
    last = (l == DEPTH - 1)
    t_start = 2 if last else 0
    with P.scope():
        wout = P.sbuf("wout", [128, 8, 1024], BF16)
        for ft in range(8):
            kb.LD(wout[:, ft, :], kb.prm["w_out"][l][ft * 128:(ft + 1) * 128, :], [wout.s(ft)], q="pool")
        wb = [wout.s(ft) for ft in range(8)]
        ycb = [P.sbuf("yc%d" % i, [128, 8, 128], BF16) for i in range(2)]
        xb = [P.sbuf("xc%d" % i, [128, 1024]) for i in range(2)]
        x1b = [P.sbuf("x1c%d" % i, [128, 1024]) for i in range(2)]
        tmpb = [P.sbuf("tc%d" % i, [128, 512]) for i in range(2)]
        h2b = [P.sbuf("h2c%d" % i, [128, 8, 128], BF16) for i in range(2)]
        nb = {"junk": P.sbuf("junkc", [128, 1024]), "st": P.sbuf("stc", [128, 4]), "xn": P.sbuf("xnc", [128, 1024]),
              "ps_t": [P.psum("ps_tc%d" % i, [128, 512]) for i in range(2)]}
        ps_y = [P.psum("ps_y%d" % i, [128, 512]) for i in range(4)]
        k = 0
        for tt in range(t_start, NT):
            which = 1 if tt < 2 else 0
            yc = ycb[k % 2]; xt = xb[k % 2]; x1 = x1b[k % 2]; h2 = h2b[k % 2]
            cols = slice(tt * 128, (tt + 1) * 128)
            kb.LD(yc[:], kb.YC[:, cols].rearrange("(ft p) t -> p ft t", p=128), [yc], q="pool")
            kb.LD(xt[:], src[cols, :], [xt])
            for half in range(2):
                ps = ps_y[(2 * k + half) % 4]
                for ft in range(8):
                    kb.MM(ps[:], yc[:, ft, :], wout[:, ft, half * 512:(half + 1) * 512], ft == 0, ft == 7,
                          [yc, wb[ft]], [ps])
                tm = tmpb[half]
                kb.TT(tm[:], ps[:], kb.GATE1[:, which, half * 512:(half + 1) * 512], ALU.mult, [ps, kb.GATE1], [tm])
                kb.TT(x1[:, half * 512:(half + 1) * 512], xt[:, half * 512:(half + 1) * 512], tm[:], ALU.add,
                      [xt, tm], [x1], eng="pool")
            kb.ST(kb.XS[cols, :], x1[:], [x1])
            norm_to_fm(kb, x1, h2, 0, kb.GS2, kb.SH2, which, nb, None)
            kb.ST(kb.H2T[:, cols].rearrange("(dt p) t -> p dt t", p=128), h2[:], [h2])
            k += 1
    with P.scope():
        w1 = P.sbuf("w1", [128, 8, 4096], BF16)
        w2 = P.sbuf("w2", [128, 32, 1024], BF16)
        for kt in range(8):
            kb.LD(w1[:, kt, :], kb.prm["mlp_w1"][l][kt * 128:(kt + 1) * 128, :], [w1.s(kt)], q="pool")
        for fb in range(32):
            kb.LD(w2[:, fb, :], kb.prm["mlp_w2"][l][fb * 128:(fb + 1) * 128, :], [w2.s(fb)], q="pool")
        h2b = [P.sbuf("h2d%d" % i, [128, 8, 256], BF16) for i in range(2)]
        uTb = [P.sbuf("uT%d" % i, [128, 16, 256], BF16) for i in range(1)]
        rb = [P.sbuf("relu%d" % i, [128, 256]) for i in range(3)]
        xb = [P.sbuf("xd%d" % i, [128, 1024]) for i in range(2)]
        tmpb = [P.sbuf("td%d" % i, [128, 512]) for i in range(2)]
        ps_u = [P.psum("ps_u%d" % i, [128, 256]) for i in range(3)]
        ps_y = [P.psum("ps_y2%d" % i, [128, 512]) for i in range(4)]
        if last:
            fg = P.sbuf("fg", [128, 1024])
            kb.LD(fg[:], kb.prm["final_norm_g"][:].partition_broadcast(128), [fg])
            stf = P.sbuf("stf", [128, 4])
            xnf = P.sbuf("xnf", [128, 1024])
        k = 0; ku = 0
        for g0 in range(t_start, NT, 2):
            h2 = h2b[k % 2]; uT = uTb[0]
            cols = slice(g0 * 128, (g0 + 2) * 128)
            kb.LD(h2[:], kb.H2T[:, cols].rearrange("(dt p) t -> p dt t", p=128), [h2])
            for hh in range(2):
                for fl in range(16):
                    fb = hh * 16 + fl
                    ps = ps_u[ku % 3]; r = rb[ku % 3]; ku += 1
                    for kt in range(8):
                        kb.MM(ps[:], w1[:, kt, fb * 128:(fb + 1) * 128], h2[:, kt, :], kt == 0, kt == 7, [w1.s(kt), h2], [ps])
                    kb.ACT(r[:], ps[:], AF.Relu, [ps], [r])
                    kb.TT(uT[:, fl, :], r[:], r[:], ALU.mult, [r], [uT.s(fl)], eng=("dve" if fb % 2 else "pool"))
                for ti in range(2):
                    for half in range(2):
                        ps = ps_y[2 * ti + half]
                        for fl in range(16):
                            fb = hh * 16 + fl
                            kb.MM(ps[:], uT[:, fl, ti * 128:(ti + 1) * 128], w2[:, fb, half * 512:(half + 1) * 512],
                                  fb == 0, fb == 31, [uT.s(fl), w2.s(fb)], [ps])
            for ti in range(2):
                tt = g0 + ti
                which = 1 if tt < 2 else 0
                xt = xb[ti]
                rows = slice(tt * 128, (tt + 1) * 128)
                kb.LD(xt[:], kb.XS[rows, :], [xt])
                for half in range(2):
                    ps = ps_y[2 * ti + half]
                    tm = tmpb[half]
                    kb.TT(tm[:], ps[:], kb.GATE2[:, which, half * 512:(half + 1) * 512], ALU.mult, [ps, kb.GATE2], [tm])
                    kb.TT(xt[:, half * 512:(half + 1) * 512], xt[:, half * 512:(half + 1) * 512], tm[:], ALU.add,
                          [xt, tm], [xt], eng="pool")
                if not last:
                    kb.ST(kb.XS[rows, :], xt[:], [xt])
                else:
                    kb.MS(stf[:, 0:1], 0.0, [stf])
                    kb.ACT(xnf[:], xt[:], AF.Square, [xt], [xnf, stf], accum_out=stf[:, 0:1])
                    kb.ACT(stf[:, 1:2], stf[:, 0:1], AF.Sqrt, [stf], [stf], scale=1.0 / D, bias=kb.C("CCOL")[:, 0:1])
                    kb.RECIP(stf[:, 2:3], stf[:, 1:2], [stf], [stf])
                    kb.ACT(xnf[:], xt[:], AF.Copy, [xt, stf], [xnf], scale=stf[:, 2:3])
                    kb.TT(xnf[:], xnf[:], fg[:], ALU.mult, [xnf, fg], [xnf])
                    kb.P.dma(kb.y[(tt - 2) * 128:(tt - 1) * 128, :], xnf[:], reads=[xnf.b], q="pool", is_output=True)
            k += 1


def finalize_gated(kb, OACC, gate_row0, gain, yc_row0, pfx):
    P = kb.P
    def two(nm):
        return [P.sbuf(pfx + nm + "%d" % i, [128, 2, 128]) for i in range(2)]
    gb, sq, rt, eg, ob = two("fg"), two("fsq"), two("frt"), two("feg"), two("fo")
    ps_m = [P.psum(pfx + "fps%d" % i, [128, 2, 128]) for i in range(2)]
    for n in range(NT):
        cols = slice(n * 128, (n + 1) * 128)
        i = n % 2
        g = gb[i]
        kb.LD(g[:], kb.ZF[gate_row0:gate_row0 + 256, cols].rearrange("(hp p) t -> p hp t", p=128), [g])
        o = OACC[:, :, cols]
        kb.TT(sq[i][:], o, o, ALU.mult, [OACC.s(n)], [sq[i]])
        kb.MM(ps_m[i][:].rearrange("p a b -> p (a b)"), kb.C("BLK64"), sq[i][:].rearrange("p a b -> p (a b)"), True, True,
              [sq[i]], [ps_m[i]])
        kb.ACT(rt[i][:], ps_m[i][:], AF.Ln, [ps_m[i]], [rt[i]], scale=1.0 / 64, bias=kb.C("CCOL")[:, 0:1])
        kb.ACT(rt[i][:], rt[i][:], AF.Exp, [rt[i]], [rt[i]], scale=-0.5)
        kb.ACT(eg[i][:], g[:], AF.Exp, [g], [eg[i]], scale=-1.0)
        kb.TS(eg[i][:], eg[i][:], 1.0, None, ALU.add, None, [eg[i]], [eg[i]])
        kb.RECIP(eg[i][:], eg[i][:], [eg[i]], [eg[i]])
        kb.TT(eg[i][:], eg[i][:], g[:], ALU.mult, [eg[i], g], [eg[i]], eng="pool")
        kb.TT(ob[i][:], o, rt[i][:], ALU.mult, [OACC.s(n), rt[i]], [ob[i]])
        if gain is not None:
            kb.STT(ob[i][:], ob[i][:], gain[:, 0:1], eg[i][:], ALU.mult, ALU.mult, [ob[i], gain, eg[i]], [ob[i]])
        else:
            kb.TT(ob[i][:], ob[i][:], eg[i][:], ALU.mult, [ob[i], eg[i]], [ob[i]])
        kb.ST(kb.YC[yc_row0:yc_row0 + 256, cols].rearrange("(hp p) t -> p hp t", p=128), ob[i][:], [ob[i]])


def oacc_write(kb, OACC, hp, n, ps, d):
    cols = slice(n * 128, (n + 1) * 128)
    if d == 0:
        kb.CP(OACC[:, hp, cols], ps[:], [ps], [OACC.s(n)], eng="act")
    else:
        kb.TT(OACC[:, hp, cols], OACC[:, hp, cols], ps[:], ALU.add, [ps], [OACC.s(n)])


def mixer_ret(kb, l):
    P = kb.P
    with P.scope():
        OACC = P.sbuf("r_oacc", [128, 2, S])
        with P.scope():
            lgt = P.sbuf("r_lgt", [128, 8])
            kb.LD(lgt[:], kb.prm["ret_decay_logit"][l].rearrange("d h -> (d h)").partition_broadcast(128), [lgt])
            LG = P.sbuf("r_LG", [128, 8])
            kb.ACT(LG[:], lgt[:], AF.Sigmoid, [lgt], [LG])
            kb.ACT(LG[:], LG[:], AF.Ln, [LG], [LG])
            LGP = P.sbuf("r_LGP", [128, 4])
            for d in range(2):
                for hp in range(2):
                    c = 2 * d + hp
                    kb.CP(LGP[0:64, c:c + 1], LG[0:64, 4 * d + 2 * hp:4 * d + 2 * hp + 1], [LG], [LGP])
                    kb.CP(LGP[64:128, c:c + 1], LG[64:128, 4 * d + 2 * hp + 1:4 * d + 2 * hp + 2], [LG], [LGP])
            MK = [P.sbuf("r_MK%d" % d, [128, 4, 128]) for d in range(2)]
            QDEC = [[P.sbuf("r_QD%d%d" % (d, hp), [128, 128]) for hp in range(2)] for d in range(2)]
            etmp = P.sbuf("r_etmp", [128, 128])
            for d in range(2):
                for h in range(4):
                    kb.ACT(etmp[:], kb.C("DIFF" if d == 0 else "NDIFF"), AF.Exp, [LG], [etmp],
                           scale=LG[:, 4 * d + h:4 * d + h + 1])
                    kb.STT(MK[d][:, h, :], etmp[:], 0.125, kb.C("TRIF" if d == 0 else "TRIB"), ALU.mult, ALU.mult,
                           [etmp], [MK[d]])
                for hp in range(2):
                    kb.ACT(QDEC[d][hp][:], kb.C("IOTAF1" if d == 0 else "RIOTAF"), AF.Exp, [LGP], [QDEC[d][hp]],
                           scale=LGP[:, 2 * d + hp:2 * d + hp + 1])
            KD = P.sbuf("r_KD", [128, 8])
            kb.ACT(KD[:, 0:4], LG[:, 0:4], AF.Exp, [LG], [KD], scale=kb.C("CCOL")[:, 3:4])
            kb.ACT(KD[:, 4:8], LG[:, 4:8], AF.Exp, [LG], [KD], scale=kb.C("CCOL")[:, 2:3])
            kb.TS(KD[:], KD[:], 0.125, None, ALU.mult, None, [KD], [KD])
            CV = P.sbuf("r_CV", [128, 4])
            kb.ACT(CV[:], LGP[:], AF.Exp, [LGP], [CV], scale=128.0)
            qTb = [P.sbuf("r_q%d" % i, [128, 2, 128]) for i in range(2)]
            kTb = [P.sbuf("r_k%d" % i, [128, 2, 128]) for i in range(2)]
            csb = [P.sbuf("r_cs%d" % i, [128, 2, 128]) for i in range(2)]
            Vp = [[P.sbuf("r_vp%d%d" % (i, h), [128, 128]) for h in range(4)] for i in range(2)]
            khp = [[P.sbuf("r_kh%d%d" % (i, h), [128, 128]) for h in range(4)] for i in range(2)]
            for i in range(2):
                for h in range(4):
                    kb.MS(Vp[i][h][:], 0.0, [Vp[i][h]], eng="pool")
                    kb.MS(khp[i][h][:], 0.0, [khp[i][h]], eng="pool")
            t1 = [P.sbuf("r_t1%d" % i, [128, 128]) for i in range(2)]
            t2 = [P.sbuf("r_t2%d" % i, [128, 128]) for i in range(2)]
            qr = [P.sbuf("r_qr%d" % i, [128, 2, 128]) for i in range(2)]
            kr = [P.sbuf("r_kr%d" % i, [128, 2, 128]) for i in range(2)]
            AT = [P.sbuf("r_AT%d" % i, [128, 2, 128]) for i in range(2)]
            qd = [P.sbuf("r_qd%d" % i, [128, 128]) for i in range(2)]
            Sb = [P.sbuf("r_S%d" % hp, [128, 128]) for hp in range(2)]
            ps_r = [P.psum("r_psr%d" % i, [128, 256]) for i in range(2)]
            ps_s = [P.psum("r_pss%d" % i, [128, 2, 128]) for i in range(2)]
            ps_o = [P.psum("r_pso%d" % i, [128, 128]) for i in range(2)]
            ps_k = P.psum("r_psk", [128, 128])
            ps_kv = P.psum("r_pskv", [128, 128])
            it = 0
            for d in range(2):
                for hp in range(2):
                    kb.MS(Sb[hp][:], 0.0, [Sb[hp]])
                for n in ORDER[d]:
                    cols = slice(n * 128, (n + 1) * 128)
                    b = it % 2; it += 1
                    qT, kT, cs = qTb[b], kTb[b], csb[b]
                    kb.LD(qT[:], kb.ZF[Z_RQ:Z_RQ + 256, cols].rearrange("(hp p) t -> p hp t", p=128), [qT])
                    kb.LD(kT[:], kb.ZF[Z_RK:Z_RK + 256, cols].rearrange("(hp p) t -> p hp t", p=128), [kT])
                    kb.LD(cs[:, 0, :], kb.ropec[:, cols], [cs])
                    kb.LD(cs[:, 1, :], kb.ropes[:, cols], [cs])
                    for h in range(4):
                        kb.LD(Vp[b][h][:, 64 * (h % 2):64 * (h % 2) + 64], kb.ZT[cols, 256 + 64 * h:256 + 64 * h + 64],
                              [Vp[b][h]])
                    for hp in range(2):
                        j = (it * 2 + hp) % 2
                        pr = ps_r[j]
                        kb.MM(pr[:, 0:128], kb.C("ROT"), qT[:, hp, :], True, True, [qT], [pr])
                        kb.MM(pr[:, 128:256], kb.C("ROT"), kT[:, hp, :], True, True, [kT], [pr])
                        for (src_, dst, off) in ((qT, qr[b], 0), (kT, kr[b], 128)):
                            kb.TT(t1[j][:], src_[:, hp, :], cs[:, 0, :], ALU.mult, [src_, cs], [t1[j]], eng="pool")
                            kb.TT(t2[j][:], pr[:, off:off + 128], cs[:, 1, :], ALU.mult, [pr, cs], [t2[j]])
                            kb.TT(dst[:, hp, :], t1[j][:], t2[j][:], ALU.add, [t1[j], t2[j]], [dst.s(hp)], eng="pool")
                        pss = ps_s[j]
                        for h2 in range(2):
                            kb.MM(pss[:, h2, :], kr[b][64 * h2:64 * h2 + 64, hp, :], qr[b][64 * h2:64 * h2 + 64, hp, :],
                                  True, True, [kr[b].s(hp), qr[b].s(hp)], [pss])
                        kb.TT(AT[j][:], pss[:], MK[d][:, 2 * hp:2 * hp + 2, :], ALU.mult, [pss, MK[d]], [AT[j]])
                        kb.TT(qd[j][:], qr[b][:, hp, :], QDEC[d][hp][:], ALU.mult, [qr[b].s(hp), QDEC[d][hp]], [qd[j]],
                              eng="pool")
                        po = ps_o[j]
                        kb.MM(po[:], Vp[b][2 * hp][:], AT[j][:, 0, :], True, False, [Vp[b][2 * hp], AT[j]], [po])
                        kb.MM(po[:], Vp[b][2 * hp + 1][:], AT[j][:, 1, :], False, False, [Vp[b][2 * hp + 1], AT[j]], [po])
                        kb.MM(po[:], Sb[hp][:], qd[j][:], False, True, [Sb[hp], qd[j]], [po])
                        oacc_write(kb, OACC, hp, n, po, d)
                        kb.TR(ps_k[:], kr[b][:, hp, :], kb.C("IDENT"), [kr[b].s(hp)], [ps_k])
                        for h2 in range(2):
                            h = 2 * hp + h2
                            kb.ACT(khp[b][h][:, 64 * h2:64 * h2 + 64], ps_k[:, 64 * h2:64 * h2 + 64], AF.Copy,
                                   [ps_k, KD], [khp[b][h]], scale=KD[:, 4 * d + h:4 * d + h + 1])
                        kb.MM(ps_kv[:], khp[b][2 * hp][:], Vp[b][2 * hp][:], True, False,
                              [khp[b][2 * hp], Vp[b][2 * hp]], [ps_kv])
                        kb.MM(ps_kv[:], khp[b][2 * hp + 1][:], Vp[b][2 * hp + 1][:], False, True,
                              [khp[b][2 * hp + 1], Vp[b][2 * hp + 1]], [ps_kv])
                        kb.STT(Sb[hp][:], Sb[hp][:], CV[:, 2 * d + hp:2 * d + hp + 1], ps_kv[:], ALU.mult, ALU.add,
                               [Sb[hp], CV, ps_kv], [Sb[hp]])
        with P.scope():
            finalize_gated(kb, OACC, Z_RG, None, 256, "r_")


def mixer_hgrn(kb, l):
    P = kb.P
    with P.scope():
        OACC = P.sbuf("h_oacc", [128, 2, S])
        with P.scope():
            LB = P.sbuf("h_LB", [128, 4]); OML = P.sbuf("h_OML", [128, 4])
            if l == 0:
                kb.MS(LB[:], 0.0, [LB]); kb.MS(OML[:], 1.0, [OML])
            else:
                lgt = P.sbuf("h_lgt", [128, 8])
                kb.LD(lgt[:], kb.prm["hgrn_lb_logits"][:].rearrange("l d (hp p) -> p (l d hp)", p=128), [lgt],
                      allow_slow_non_contiguous=True)
                kb.TT(LB[:], lgt[:, 4:8], lgt[:, 0:4], ALU.subtract, [lgt], [LB])
                kb.ACT(LB[:], LB[:], AF.Sigmoid, [LB], [LB])
                kb.TS(OML[:], LB[:], -1.0, 1.0, ALU.mult, ALU.add, [LB], [OML])
            G = P.sbuf("h_G", [128, 1])
            for hh in range(2):
                kb.LD(G[64 * hh:64 * hh + 64, :], kb.prm["hgrn_norm_g"][l].rearrange("(p o) -> p o", o=1), [G])
            kb.hgrn_gain = G
            hqb = [P.sbuf("h_q%d" % i, [128, 2, 128]) for i in range(2)]
            hfb = [P.sbuf("h_f%d" % i, [128, 2, 128]) for i in range(2)]
            Vp = [[P.sbuf("h_vp%d%d" % (i, h), [128, 128]) for h in range(4)] for i in range(2)]
            khp = [[P.sbuf("h_kh%d%d" % (i, h), [128, 128]) for h in range(4)] for i in range(2)]
            for i in range(2):
                for h in range(4):
                    kb.MS(Vp[i][h][:], 0.0, [Vp[i][h]], eng="pool")
                    kb.MS(khp[i][h][:], 0.0, [khp[i][h]], eng="pool")
            MREF = [[P.sbuf("h_mr%d%d" % (d, i), [128, 4]) for i in range(2)] for d in range(2)]
            for d in range(2):
                for i in range(2):
                    kb.MS(MREF[d][i][:], 0.0, [MREF[d][i]])

            def two(name, shape=(128, 128)):
                return [P.sbuf("h_%s%d" % (name, i), list(shape)) for i in range(2)]
            qs, sgm, ff, logf, kk, bb, pre = two("qs"), two("sg"), two("ff"), two("lf"), two("kk"), two("bb"), two("pre")
            e1, Ql, e2, Qd = two("e1"), two("Ql"), two("e2"), two("Qd")
            Kt = [two("Kt%d" % r) for r in range(4)]
            ex = two("ex")
            AT = two("AT", (128, 2, 128))
            KhT = two("KhT")
            bend = two("bend", (128, 2))
            Sb = [P.sbuf("h_S%d" % hp, [128, 128]) for hp in range(2)]
            ps_s = [P.psum("h_pss%d" % i, [128, 2, 128]) for i in range(2)]
            ps_o = [P.psum("h_pso%d" % i, [128, 128]) for i in range(2)]
            ps_k = [P.psum("h_psk%d" % i, [128, 128]) for i in range(2)]
            ps_kv = [P.psum("h_pskv%d" % i, [128, 128]) for i in range(2)]
            it = 0
            jj = 0
            for d in range(2):
                zf = Z_HFF if d == 0 else Z_HFB
                for hp in range(2):
                    kb.MS(Sb[hp][:], 0.0, [Sb[hp]])
                for n in ORDER[d]:
                    cols = slice(n * 128, (n + 1) * 128)
                    b = it % 2; it += 1
                    hq, hf = hqb[b], hfb[b]
                    kb.LD(hq[:], kb.ZF[Z_HQ:Z_HQ + 256, cols].rearrange("(hp p) t -> p hp t", p=128), [hq])
                    kb.LD(hf[:], kb.ZF[zf:zf + 256, cols].rearrange("(hp p) t -> p hp t", p=128), [hf])
                    for h in range(4):
                        kb.LD(Vp[b][h][:, 64 * (h % 2):64 * (h % 2) + 64], kb.ZT[cols, 64 * h:64 * h + 64], [Vp[b][h]])
                    for hp in range(2):
                        j = jj % 2; jj += 1
                        c = 2 * d + hp
                        mref = MREF[d][j]
                        kb.ACT(qs[j][:], hq[:, hp, :], AF.Silu, [hq], [qs[j]])
                        kb.ACT(sgm[j][:], hf[:, hp, :], AF.Sigmoid, [hf], [sgm[j]])
                        kb.TS(ff[j][:], sgm[j][:], OML[:, c:c + 1], LB[:, c:c + 1], ALU.mult, ALU.add, [sgm[j], OML, LB], [ff[j]])
                        kb.ACT(logf[j][:], ff[j][:], AF.Ln, [ff[j]], [logf[j]])
                        kb.TS(kk[j][:], ff[j][:], -1.0, 1.0, ALU.mult, ALU.add, [ff[j]], [kk[j]], eng="pool")
                        B = bb[j]
                        if d == 0:
                            kb.SCAN(B[:], kb.C("ONES"), logf[j][:], [logf[j]], [B])
                            kb.CP(mref[:, 1:4], B[:].rearrange("p (r c) -> p r c", c=32)[:, 0:3, 31], [B], [mref])
                            be = B[:, 127:128]
                        else:
                            kb.SCAN(pre[j][:], kb.C("ONES"), logf[j][:], [logf[j]], [pre[j]])
                            kb.STT(B[:], pre[j][:], -1.0, logf[j][:], ALU.mult, ALU.add, [pre[j], logf[j]], [B])
                            kb.TS(B[:], B[:], pre[j][:, 127:128], None, ALU.add, None, [B, pre[j]], [B])
                            kb.CP(mref[:, 0:3], B[:].rearrange("p (r c) -> p r c", c=32)[:, 1:4, 0], [B], [mref])
                            be = B[:, 0:1]
                        kb.TT(e1[j][:].rearrange("p (r c) -> p r c", c=32), B[:].rearrange("p (r c) -> p r c", c=32),
                              mref[:].unsqueeze(2).to_broadcast([128, 4, 32]), ALU.subtract, [B, mref], [e1[j]])
                        kb.ACT(e1[j][:], e1[j][:], AF.Exp, [e1[j]], [e1[j]])
                        kb.STT(Ql[j][:], qs[j][:], 0.125, e1[j][:], ALU.mult, ALU.mult, [qs[j], e1[j]], [Ql[j]], eng="pool")
                        kb.ACT(e2[j][:], B[:], AF.Exp, [B], [e2[j]])
                        kb.STT(Qd[j][:], qs[j][:], 0.125, e2[j][:], ALU.mult, ALU.mult, [qs[j], e2[j]], [Qd[j]], eng="pool")
                        pss = ps_s[j]
                        for r in range(4):
                            kb.ACT(ex[j][:], B[:], AF.Exp, [B, mref], [ex[j]], scale=-1.0, bias=mref[:, r:r + 1])
                            kb.STT(Kt[r][j][:], ex[j][:], 1e26, kk[j][:], ALU.min, ALU.mult, [ex[j], kk[j]], [Kt[r][j]])
                            for h2 in range(2):
                                kb.MM(pss[:, h2, 32 * r:32 * r + 32], Kt[r][j][64 * h2:64 * h2 + 64, :],
                                      Ql[j][64 * h2:64 * h2 + 64, 32 * r:32 * r + 32], True, True,
                                      [Kt[r][j], Ql[j]], [pss])
                        kb.TT(AT[j][:], pss[:], kb.C("TRIF" if d == 0 else "TRIB").unsqueeze(1).to_broadcast([128, 2, 128]),
                              ALU.mult, [pss], [AT[j]])
                        po = ps_o[j]
                        kb.MM(po[:], Vp[b][2 * hp][:], AT[j][:, 0, :], True, False, [Vp[b][2 * hp], AT[j]], [po])
                        kb.MM(po[:], Vp[b][2 * hp + 1][:], AT[j][:, 1, :], False, False, [Vp[b][2 * hp + 1], AT[j]], [po])
                        kb.MM(po[:], Sb[hp][:], Qd[j][:], False, True, [Sb[hp], Qd[j]], [po])
                        oacc_write(kb, OACC, hp, n, po, d)
                        kb.CP(bend[j][:, 0:1], be, [B], [bend[j]])
                        kb.ACT(KhT[j][:], B[:], AF.Exp, [B, bend[j]], [KhT[j]], scale=-1.0, bias=bend[j][:, 0:1])
                        kb.TT(KhT[j][:], KhT[j][:], kk[j][:], ALU.mult, [KhT[j], kk[j]], [KhT[j]], eng="pool")
                        kb.ACT(bend[j][:, 1:2], bend[j][:, 0:1], AF.Exp, [bend[j]], [bend[j]])
                        pk = ps_k[j]
                        kb.TR(pk[:], KhT[j][:], kb.C("IDENT"), [KhT[j]], [pk])
                        for h2 in range(2):
                            h = 2 * hp + h2
                            kb.CP(khp[b][h][:, 64 * h2:64 * h2 + 64], pk[:, 64 * h2:64 * h2 + 64], [pk], [khp[b][h]],
                                  eng=("act" if h2 else "dve"))
                        pkv = ps_kv[j]
                        kb.MM(pkv[:], khp[b][2 * hp][:], Vp[b][2 * hp][:], True, False, [khp[b][2 * hp], Vp[b][2 * hp]], [pkv])
                        kb.MM(pkv[:], khp[b][2 * hp + 1][:], Vp[b][2 * hp + 1][:], False, True,
                              [khp[b][2 * hp + 1], Vp[b][2 * hp + 1]], [pkv])
                        kb.STT(Sb[hp][:], Sb[hp][:], bend[j][:, 1:2], pkv[:], ALU.mult, ALU.add,
                               [Sb[hp], bend[j], pkv], [Sb[hp]])
        with P.scope():
            G = P.sbuf("h_G2", [128, 1])
            for hh in range(2):
                kb.LD(G[64 * hh:64 * hh + 64, :], kb.prm["hgrn_norm_g"][l].rearrange("(p o) -> p o", o=1), [G])
            finalize_gated(kb, OACC, Z_HG, G, 0, "h_")


PI = float(np.pi)


def _sincos(kb, ang, sin_out, cos_out, R, tmp, shape=None):
    P = kb.P
    shp = list(ang.shape)
    with P.scope():
        ki = P.sbuf("sc_ki", shp, mybir.dt.int32)
        kf = P.sbuf("sc_kf", shp)
        r = P.sbuf("sc_r", shp)
        m = P.sbuf("sc_m", shp)
        C1 = 6.28125
        C2 = 2 * PI - C1
        for (shift, out) in ((0.0, sin_out), (PI / 2, cos_out)):
            kb.TS(r[:], ang, shift, None, ALU.add, None, R, [r])
            kb.TS(kf[:], r[:], 1.0 / (2 * PI), None, ALU.mult, None, [r], [kf])
            kb.CP(ki[:], kf[:], [kf], [ki])
            kb.CP(kf[:], ki[:], [ki], [kf])
            kb.STT(r[:], kf[:], -C1, r[:], ALU.mult, ALU.add, [kf, r], [r])
            kb.STT(r[:], kf[:], -C2, r[:], ALU.mult, ALU.add, [kf, r], [r])
            kb.TS(m[:], r[:], PI, 2 * PI, ALU.is_gt, ALU.mult, [r], [m])
            kb.TT(r[:], r[:], m[:], ALU.subtract, [r, m], [r])
            kb.TS(m[:], r[:], -PI, 2 * PI, ALU.is_lt, ALU.mult, [r], [m])
            kb.TT(r[:], r[:], m[:], ALU.add, [r, m], [r])
            kb.ACT(out, r[:], AF.Sin, [r], R)


def mixer_s5(kb, l):
    P = kb.P
    prm = kb.prm
    with P.scope():
        OACC = P.sbuf("s_oacc", [128, 2, S])
        with P.scope():
            WX = P.sbuf("s_WX", [128, 2, 8, 2, 64])
            Cblk = P.sbuf("s_Cblk", [128, 16, 128])
            kb.MS(WX[:], 0.0, [WX], eng="pool")
            kb.MS(Cblk[:], 0.0, [Cblk], eng="pool")
            for g8 in range(8):
                for ri, nm in enumerate(("s5_b_re", "s5_b_im")):
                    for gg in range(2):
                        src = prm[nm][l][8 * gg + g8].rearrange("p c -> c p")
                        kb.LD(WX[16 * g8:16 * g8 + 16, gg, g8, ri, :], src, [WX], allow_slow_non_contiguous=True)
            for g in range(16):
                g8 = g % 8
                kb.LD(Cblk[0:64, g, 16 * g8:16 * g8 + 16], prm["s5_c_re"][l][g].rearrange("c p -> p c"), [Cblk],
                      allow_slow_non_contiguous=True)
                kb.LD(Cblk[64:128, g, 16 * g8:16 * g8 + 16], prm["s5_c_im"][l][g].rearrange("c p -> p c"), [Cblk],
                      allow_slow_non_contiguous=True)
            kb.TS(Cblk[64:128, :, :], Cblk[64:128, :, :], -1.0, None, ALU.mult, None, [Cblk], [Cblk])
            Cb16 = P.sbuf("s_Cb16", [128, 16, 128], BF16)
            kb.CP(Cb16[:], Cblk[:], [Cblk], [Cb16])
            VFr = P.sbuf("s_VFr", [128, 16, 64]); VFi = P.sbuf("s_VFi", [128, 16, 64])
            T1 = P.sbuf("s_T1", [128, 16, 128]); T2 = P.sbuf("s_T2", [128, 16, 128])
            AR = P.sbuf("s_AR", [128, 16]); NAI = P.sbuf("s_NAI", [128, 16])
            for d in range(2):
                with P.scope():
                    lr = P.sbuf("s_lr", [128, 16, 64]); li = P.sbuf("s_li", [128, 16, 64]); dtb = P.sbuf("s_dt", [128, 16])
                    kb.LD(lr[:], prm["s5_lam_re"][l][d].rearrange("g p -> (g p)").partition_broadcast(128), [lr])
                    kb.LD(li[:], prm["s5_lam_im"][l][d].rearrange("g p -> (g p)").partition_broadcast(128), [li])
                    kb.LD(dtb[:], prm["s5_log_dt"][l][d].partition_broadcast(128), [dtb])
                    kb.ACT(dtb[:], dtb[:], AF.Exp, [dtb], [dtb])
                    dt_bc = dtb[:].unsqueeze(2).to_broadcast([128, 16, 64])
                    lrdt = P.sbuf("s_lrdt", [128, 16, 64]); lidt = P.sbuf("s_lidt", [128, 16, 64])
                    kb.TT(lrdt[:], lr[:], dt_bc, ALU.mult, [lr, dtb], [lrdt])
                    kb.TT(lidt[:], li[:], dt_bc, ALU.mult, [li, dtb], [lidt])
                    a = [P.sbuf("s_a%d" % i, [128, 16, 64]) for i in range(8)]
                    mag, ang, sn, cs, tmp, ar, ai, t2 = a
                    kb.ACT(mag[:], lrdt[:], AF.Exp, [lrdt], [mag])
                    _sincos(kb, lidt[:], sn[:], cs[:], [lidt, sn, cs, tmp], tmp[:])
                    kb.TT(ar[:], mag[:], cs[:], ALU.mult, [mag, cs], [ar])
                    kb.TT(ai[:], mag[:], sn[:], ALU.mult, [mag, sn], [ai])
                    den = P.sbuf("s_den", [128, 16, 64]); fr = P.sbuf("s_fr", [128, 16, 64]); fi = P.sbuf("s_fi", [128, 16, 64])
                    kb.TT(den[:], lr[:], lr[:], ALU.mult, [lr], [den])
                    kb.TT(t2[:], li[:], li[:], ALU.mult, [li], [t2])
                    kb.TT(den[:], den[:], t2[:], ALU.add, [den, t2], [den])
                    kb.RECIP(den[:], den[:], [den], [den])
                    kb.TS(ar[:], ar[:], -1.0, None, ALU.add, None, [ar], [ar])
                    kb.TT(fr[:], ar[:], lr[:], ALU.mult, [ar, lr], [fr])
                    kb.TT(t2[:], ai[:], li[:], ALU.mult, [ai, li], [t2])
                    kb.TT(fr[:], fr[:], t2[:], ALU.add, [fr, t2], [fr])
                    kb.TT(fr[:], fr[:], den[:], ALU.mult, [fr, den], [fr])
                    kb.TT(fi[:], ai[:], lr[:], ALU.mult, [ai, lr], [fi])
                    kb.TT(t2[:], ar[:], li[:], ALU.mult, [ar, li], [t2])
                    kb.TT(fi[:], fi[:], t2[:], ALU.subtract, [fi, t2], [fi])
                    kb.TT(fi[:], fi[:], den[:], ALU.mult, [fi, den], [fi])
                    jcol = kb.C("CCOL")[:, 2:3] if d == 0 else kb.C("CCOL")[:, 3:4]
                    njcol = kb.C("CCOL")[:, 6:7] if d == 0 else kb.C("CCOL")[:, 7:8]
                    kb.ACT(mag[:], lrdt[:], AF.Exp, [lrdt], [mag], scale=njcol)
                    kb.TS(ang[:], lidt[:], jcol, None, ALU.mult, None, [lidt], [ang])
                    _sincos(kb, ang[:], sn[:], cs[:], [ang, sn, cs, tmp], tmp[:])
                    vr, vi = ar, ai
                    kb.TT(vr[:], mag[:], cs[:], ALU.mult, [mag, cs], [vr])
                    kb.TT(vi[:], mag[:], sn[:], ALU.mult, [mag, sn], [vi])
                    kb.TS(vi[:], vi[:], -1.0, None, ALU.mult, None, [vi], [vi])
                    kb.TT(VFr[:], vr[:], fr[:], ALU.mult, [vr, fr], [VFr])
                    kb.TT(t2[:], vi[:], fi[:], ALU.mult, [vi, fi], [t2])
                    kb.TT(VFr[:], VFr[:], t2[:], ALU.subtract, [VFr, t2], [VFr])
                    kb.TT(VFi[:], vr[:], fi[:], ALU.mult, [vr, fi], [VFi])
                    kb.TT(t2[:], vi[:], fr[:], ALU.mult, [vi, fr], [t2])
                    kb.TT(VFi[:], VFi[:], t2[:], ALU.add, [VFi, t2], [VFi])
                with P.scope():
                    dtb = P.sbuf("s_dt2", [128, 16])
                    kb.LD(dtb[:], prm["s5_log_dt"][l][d].partition_broadcast(128), [dtb])
                    kb.ACT(dtb[:], dtb[:], AF.Exp, [dtb], [dtb])
                    lrp = P.sbuf("s_lrp", [128, 16]); lip = P.sbuf("s_lip", [128, 16])
                    for hh in range(2):
                        kb.LD(lrp[64 * hh:64 * hh + 64, :], prm["s5_lam_re"][l][d].rearrange("g p -> p g"), [lrp],
                              allow_slow_non_contiguous=True)
                        kb.LD(lip[64 * hh:64 * hh + 64, :], prm["s5_lam_im"][l][d].rearrange("g p -> p g"), [lip],
                              allow_slow_non_contiguous=True)
                    kb.TT(lrp[:], lrp[:], dtb[:], ALU.mult, [lrp, dtb], [lrp])
                    kb.TT(lip[:], lip[:], dtb[:], ALU.mult, [lip, dtb], [lip])
                    b4 = [P.sbuf("s_b%d" % i, [128, 16, 128]) for i in range(4)]
                    arg, sn2, cs2, tmp2 = b4
                    mt = kb.C("IOTAF" if d == 0 else "R127F")
                    mt_bc = mt.unsqueeze(1).to_broadcast([128, 16, 128])
                    kb.TT(arg[:], lrp[:].unsqueeze(2).to_broadcast([128, 16, 128]), mt_bc, ALU.mult, [lrp], [arg])
                    kb.ACT(T1[:], arg[:], AF.Exp, [arg], [T1])
                    kb.TT(arg[:], lip[:].unsqueeze(2).to_broadcast([128, 16, 128]), mt_bc, ALU.mult, [lip, T1], [arg])
                    _sincos(kb, arg[:], sn2[:], cs2[:], [arg, sn2, cs2, tmp2], tmp2[:])
                    kb.TT(T2[:], T1[:], sn2[:], ALU.mult, [T1, sn2], [T2])
                    kb.TS(T2[:], T2[:], -1.0, None, ALU.mult, None, [T2], [T2])
                    kb.TT(T1[:], T1[:], cs2[:], ALU.mult, [T1, cs2], [T1])
                    c4 = [P.sbuf("s_c%d" % i, [128, 16]) for i in range(4)]
                    kb.ACT(c4[0][:], lrp[:], AF.Exp, [lrp], [c4[0]])
                    _sincos(kb, lip[:], c4[1][:], c4[2][:], [lip, c4[1], c4[2], c4[3]], c4[3][:])
                    kb.TT(AR[:], c4[0][:], c4[2][:], ALU.mult, [c4[0], c4[2]], [AR])
                    kb.TT(NAI[:], c4[0][:], c4[1][:], ALU.mult, [c4[0], c4[1]], [NAI])
                    kb.TS(NAI[:], NAI[:], -1.0, None, ALU.mult, None, [NAI], [NAI])
                sweep_scope = P.scope(); sweep_scope.__enter__()
                uTb = [P.sbuf("s_u%d" % i, [128, 2, 128]) for i in range(2)]
                mm_ = [P.sbuf("s_m%d" % i, [128, 8, 64]) for i in range(4)]
                W3 = [P.sbuf("s_W3%d" % i, [128, 8, 3, 64], BF16) for i in range(2)]
                Hb = [P.sbuf("s_Hb%d" % i, [128, 8, 128], BF16) for i in range(2)]
                tri16 = P.sbuf("s_tri16", [128, 128], BF16)
                kb.CP(tri16[:], kb.C("TRIF" if d == 0 else "TRIB"), [], [tri16])
                tP = [P.sbuf("s_tP%d" % i, [128, 8, 128]) for i in range(2)]
                tPs = [P.sbuf("s_tPs%d" % i, [128, 8, 128]) for i in range(2)]
                H1 = [P.sbuf("s_H1%d" % i, [128, 8, 128]) for i in range(2)]
                H2 = [P.sbuf("s_H2%d" % i, [128, 8, 128]) for i in range(2)]
                hend = P.sbuf("s_hend", [128, 16]); hsend = P.sbuf("s_hsend", [128, 16])
                hp_ = P.sbuf("s_hp", [128, 16]); hps_ = P.sbuf("s_hps", [128, 16])
                sm = [P.sbuf("s_sm%d" % i, [128, 16]) for i in range(4)]
                xps = P.psum("s_xps", [128, 1024])
                pps = P.psum("s_pps", [128, 8, 128])
                ppss = P.psum("s_ppss", [128, 8, 128])
                yps = [P.psum("s_yps%d" % i, [128, 128]) for i in range(2)]
                kb.MS(hp_[:], 0.0, [hp_]); kb.MS(hps_[:], 0.0, [hps_])
                te = 127 if d == 0 else 0
                tri = kb.C("TRIF" if d == 0 else "TRIB")
                it = 0
                for n in ORDER[d]:
                    cols = slice(n * 128, (n + 1) * 128)
                    uT = uTb[it % 2]; it += 1
                    kb.LD(uT[:], kb.ZF[Z_SU:Z_SU + 256, cols].rearrange("(gg p) t -> p gg t", p=128), [uT])
                    for gg in range(2):
                        j = gg
                        for half in range(2):
                            kb.MM(xps[:, half * 512:(half + 1) * 512], uT[:, gg, :],
                                  WX[:, gg, half * 4:(half + 1) * 4, :, :].rearrange("q a r p -> q (a r p)"),
                                  True, True, [uT, WX], [xps])
                        xv = xps[:].rearrange("t (g r p) -> t g r p", r=2, p=64)
                        gs = slice(gg * 8, gg * 8 + 8)
                        kb.TT(mm_[0][:], xv[:, :, 0, :], VFr[:, gs, :], ALU.mult, [xps, VFr], [mm_[0]])
                        kb.TT(mm_[1][:], xv[:, :, 1, :], VFi[:, gs, :], ALU.mult, [xps, VFi], [mm_[1]])
                        kb.TT(mm_[2][:], xv[:, :, 0, :], VFi[:, gs, :], ALU.mult, [xps, VFi], [mm_[2]])
                        kb.TT(mm_[3][:], xv[:, :, 1, :], VFr[:, gs, :], ALU.mult, [xps, VFr], [mm_[3]])
                        w3 = W3[j]
                        kb.TT(w3[:, :, 0, :], mm_[0][:], mm_[1][:], ALU.subtract, [mm_[0], mm_[1]], [w3], eng="pool")
                        kb.TT(w3[:, :, 1, :], mm_[2][:], mm_[3][:], ALU.add, [mm_[2], mm_[3]], [w3], eng="pool")
                        kb.TT(w3[:, :, 2, :], mm_[1][:], mm_[0][:], ALU.subtract, [mm_[0], mm_[1]], [w3], eng="pool")
                        for g8 in range(8):
                            kb.MM(pps[:, g8, :], w3[:, g8, 0:2, :].rearrange("q r p -> q (r p)"), tri16[:], True, True, [w3, tri16], [pps])
                            kb.MM(ppss[:, g8, :], w3[:, g8, 1:3, :].rearrange("q r p -> q (r p)"), tri16[:], True, True, [w3, tri16], [ppss])
                        kb.TT(tP[j][:], pps[:], hp_[:, gs].unsqueeze(2).to_broadcast([128, 8, 128]), ALU.add, [pps, hp_], [tP[j]])
                        kb.TT(tPs[j][:], ppss[:], hps_[:, gs].unsqueeze(2).to_broadcast([128, 8, 128]), ALU.add,
                              [ppss, hps_], [tPs[j]])
                        kb.TT(H1[j][:], tP[j][:], T1[:, gs, :], ALU.mult, [tP[j], T1], [H1[j]], eng="pool")
                        kb.TT(H2[j][:], tPs[j][:], T2[:, gs, :], ALU.mult, [tPs[j], T2], [H2[j]])
                        kb.TT(Hb[j][:], H1[j][:], H2[j][:], ALU.add, [H1[j], H2[j]], [Hb[j]], eng="pool")
                        yp = yps[gg]
                        for g8 in range(8):
                            kb.MM(yp[:], Cb16[:, gg * 8 + g8, :], Hb[j][:, g8, :], g8 == 0, g8 == 7, [Cb16, Hb[j]], [yp])
                        oacc_write(kb, OACC, gg, n, yp, d)
                        kb.TT(hend[:, gs], H1[j][:, :, te], H2[j][:, :, te], ALU.add, [H1[j], H2[j]], [hend])
                        kb.TT(sm[0][:, 0:8], tPs[j][:, :, te], T1[:, gs, te], ALU.mult, [tPs[j], T1], [sm[0]])
                        kb.TT(sm[1][:, 0:8], tP[j][:, :, te], T2[:, gs, te], ALU.mult, [tP[j], T2], [sm[1]])
                        kb.TT(hsend[:, gs], sm[0][:, 0:8], sm[1][:, 0:8], ALU.subtract, [sm[0], sm[1]], [hsend])
                    kb.TT(sm[0][:], hend[:], AR[:], ALU.mult, [hend, AR], [sm[0]])
                    kb.TT(sm[1][:], hsend[:], NAI[:], ALU.mult, [hsend, NAI], [sm[1]])
                    kb.TT(sm[2][:], hsend[:], AR[:], ALU.mult, [hsend, AR], [sm[2]])
                    kb.TT(sm[3][:], hend[:], NAI[:], ALU.mult, [hend, NAI], [sm[3]])
                    kb.TT(hp_[:], sm[0][:], sm[1][:], ALU.add, [sm[0], sm[1]], [hp_])
                    kb.TT(hps_[:], sm[2][:], sm[3][:], ALU.subtract, [sm[2], sm[3]], [hps_])
                sweep_scope.__exit__(None, None, None)
        with P.scope():
            dsk = P.sbuf("s_dsk", [128, 2]); glb = P.sbuf("s_glb", [128, 2])
            kb.LD(dsk[:], prm["s5_d"][l].rearrange("(gg p) -> p gg", p=128), [dsk], allow_slow_non_contiguous=True)
            kb.LD(glb[:], prm["s5_glu_b"][l].rearrange("(gg p) -> p gg", p=128), [glb], allow_slow_non_contiguous=True)
            gw = P.sbuf("s_gw", [128, 2, 256])
            kb.LD(gw[:], prm["s5_glu_w"][l].rearrange("(ct p) o -> p ct o", p=128), [gw])
            uTb = [P.sbuf("s_fu%d" % i, [128, 2, 128]) for i in range(2)]
            yy = [P.sbuf("s_yy%d" % i, [128, 2, 128]) for i in range(2)]
            x2 = [P.sbuf("s_x2%d" % i, [128, 2, 128]) for i in range(2)]
            th = [P.sbuf("s_th%d" % i, [128, 2, 128]) for i in range(2)]
            sgb = [P.sbuf("s_sg%d" % i, [128, 128]) for i in range(2)]
            ob = [P.sbuf("s_ob%d" % i, [128, 128]) for i in range(2)]
            psz = [P.psum("s_psz%d" % i, [128, 128]) for i in range(2)]
            k = 0
            for n in range(NT):
                cols = slice(n * 128, (n + 1) * 128)
                i = n % 2
                kb.LD(uTb[i][:], kb.ZF[Z_SU:Z_SU + 256, cols].rearrange("(gg p) t -> p gg t", p=128), [uTb[i]])
                for gg in range(2):
                    kb.STT(yy[i][:, gg, :], uTb[i][:, gg, :], dsk[:, gg:gg + 1], OACC[:, gg, cols], ALU.mult, ALU.add,
                           [uTb[i], dsk, OACC.s(n)], [yy[i]])
                kb.TT(x2[i][:], yy[i][:], yy[i][:], ALU.mult, [yy[i]], [x2[i]], eng="pool")
                kb.TS(x2[i][:], x2[i][:], 0.044715, 1.0, ALU.mult, ALU.add, [x2[i]], [x2[i]])
                kb.TT(x2[i][:], x2[i][:], yy[i][:], ALU.mult, [x2[i], yy[i]], [x2[i]], eng="pool")
                kb.ACT(th[i][:], x2[i][:], AF.Tanh, [x2[i]], [th[i]], scale=0.7978845608028654)
                kb.TS(th[i][:], th[i][:], 1.0, 0.5, ALU.add, ALU.mult, [th[i]], [th[i]])
                kb.TT(yy[i][:], yy[i][:], th[i][:], ALU.mult, [yy[i], th[i]], [yy[i]], eng="pool")
                for ot in range(2):
                    q = k % 2; k += 1
                    for ct in range(2):
                        kb.MM(psz[q][:], gw[:, ct, ot * 128:(ot + 1) * 128], yy[i][:, ct, :], ct == 0, ct == 1, [gw, yy[i]], [psz[q]])
                    kb.ACT(sgb[q][:], psz[q][:], AF.Sigmoid, [psz[q], glb], [sgb[q]], bias=glb[:, ot:ot + 1])
                    kb.TT(ob[q][:], yy[i][:, ot, :], sgb[q][:], ALU.mult, [yy[i], sgb[q]], [ob[q]])
                    kb.ST(kb.YC[768 + ot * 128:768 + (ot + 1) * 128, cols], ob[q][:], [ob[q]])


def gdn_conv(kb, l):
    P = kb.P
    with P.scope():
        CW = P.sbuf("g_cw", [128, 6, 9])
        for kh in range(3):
            for kw in range(3):
                kb.LD(CW[:, :, kh * 3 + kw], kb.prm["gdn_conv_w"][l][kh, kw].rearrange("(ct p) -> p ct", p=128), [CW],
                      allow_slow_non_contiguous=True)
        mlat = P.sbuf("g_mlat", [128, 2, 512]); mctx = P.sbuf("g_mctx", [128, 2, 256])
        kb.LD(mlat[:], kb.cmlat[:], [mlat]); kb.LD(mctx[:], kb.cmctx[:], [mctx])
        Wb = [P.sbuf("g_w%d" % i, [128, 642]) for i in range(2)]
        acc = [[P.sbuf("g_acc%d%d" % (i, j), [128, 512]) for j in range(3)] for i in range(2)]
        sl = [P.sbuf("g_sl%d" % i, [128, 512]) for i in range(2)]
        sq = [P.sbuf("g_sq%d" % i, [128, 512]) for i in range(2)]
        rt = [P.sbuf("g_rt%d" % i, [128, 512]) for i in range(2)]
        ps = [P.psum("g_psn%d" % i, [128, 512]) for i in range(2)]
        spans = [(0, 256, True)] + [(256 + 512 * k, 512, False) for k in range(8)]
        it = 0
        for (t0, L, is_ctx) in spans:
            lo = 0 if is_ctx else 256
            hi = 256 if is_ctx else S
            a = max(lo, t0 - 65); b = min(hi, t0 + L + 65)
            for ct in range(6):
                i = it % 2; it += 1
                W = Wb[i]
                kb.MS(W[:], 0.0, [W], eng="pool")
                kb.LD(W[:, 65 + (a - t0):65 + (b - t0)], kb.ZF[Z_GQKV + ct * 128:Z_GQKV + (ct + 1) * 128, a:b], [W])
                rows = (1,) if is_ctx else (0, 1, 2)
                masks = mctx if is_ctx else mlat
                for dwi, shift in enumerate((-1, 0, 1)):
                    A = acc[i][dwi]
                    eng = "dve"
                    for q, dh in enumerate(rows):
                        o0 = 65 + 64 * (dh - 1) + shift
                        src = W[:, o0:o0 + L]
                        wcol = CW[:, ct, dh * 3 + dwi:dh * 3 + dwi + 1]
                        if q == 0:
                            kb.TS(A[:, :L], src, wcol, None, ALU.mult, None, [W, CW], [A], eng=("pool" if dwi != 1 else "dve"))
                        else:
                            kb.STT(A[:, :L], src, wcol, A[:, :L], ALU.mult, ALU.add, [W, CW, A], [A])
                    if dwi != 1:
                        mi = 0 if dwi == 0 else 1
                        kb.TT(A[:, :L], A[:, :L], masks[:, mi, :L], ALU.mult, [A, masks], [A], eng="pool")
                A0, A1, A2 = acc[i]
                kb.TT(A1[:, :L], A1[:, :L], A0[:, :L], ALU.add, [A0, A1], [A1], eng="pool")
                kb.TT(A1[:, :L], A1[:, :L], A2[:, :L], ALU.add, [A1, A2], [A1], eng="pool")
                kb.ACT(sl[i][:, :L], A1[:, :L], AF.Silu, [A1], [sl[i]])
                if ct < 4:
                    kb.TT(sq[i][:, :L], sl[i][:, :L], sl[i][:, :L], ALU.mult, [sl[i]], [sq[i]], eng="pool")
                    kb.MM(ps[i][:, :L], kb.C("BLK64"), sq[i][:, :L], True, True, [sq[i]], [ps[i]])
                    kb.ACT(rt[i][:, :L], ps[i][:, :L], AF.Sqrt, [ps[i]], [rt[i]], bias=kb.C("CCOL")[:, 0:1])
                    kb.RECIP(rt[i][:, :L], rt[i][:, :L], [rt[i]], [rt[i]])
                    if ct < 2:
                        kb.STT(sl[i][:, :L], sl[i][:, :L], 0.125, rt[i][:, :L], ALU.mult, ALU.mult, [sl[i], rt[i]], [sl[i]])
                    else:
                        kb.TT(sl[i][:, :L], sl[i][:, :L], rt[i][:, :L], ALU.mult, [sl[i], rt[i]], [sl[i]])
                kb.ST(kb.QKVF[ct * 128:(ct + 1) * 128, t0:t0 + L], sl[i][:, :L], [sl[i]])


def mixer_gdn(kb, l):
    P = kb.P
    gdn_conv(kb, l)
    upto = kb.cfg.get("gdn_upto", 99)
    if upto < 1:
        return
    with P.scope():
        OACC = P.sbuf("g_oacc", [128, 2, S])
        with P.scope():
            DTB = P.sbuf("g_dtb", [128, 8]); NEGA = P.sbuf("g_nega", [128, 8])
            kb.LD(DTB[:], kb.prm["gdn_dt_bias"][l].rearrange("d h -> (d h)").partition_broadcast(128), [DTB])
            kb.LD(NEGA[:], kb.prm["gdn_a_log"][l].rearrange("d h -> (d h)").partition_broadcast(128), [NEGA])
            kb.ACT(NEGA[:], NEGA[:], AF.Exp, [NEGA], [NEGA])
            kb.TS(NEGA[:], NEGA[:], -1.0, None, ALU.mult, None, [NEGA], [NEGA])
            qnb = [P.sbuf("g_q%d" % i, [128, 2, 128]) for i in range(2)]
            knb = [P.sbuf("g_k%d" % i, [128, 2, 128]) for i in range(2)]
            vvb = [P.sbuf("g_v%d" % i, [128, 2, 128]) for i in range(2)]
            gabb = [P.sbuf("g_gab%d" % i, [128, 16]) for i in range(2)]

            def sm4(name, w=4):
                return P.sbuf("g_" + name, [128, w])
            xa, ea, loga, beta, lnb = sm4("xa"), sm4("ea"), sm4("loga"), sm4("beta"), sm4("lnb")
            gtm, ngt, ekr, cdec, eg, beg, gpl = sm4("gtm"), sm4("ngt"), sm4("ekr"), sm4("cdec"), sm4("eg"), sm4("beg"), sm4("gpl")
            ROWS = P.sbuf("g_rows", [4, 384])
            LI = P.sbuf("g_LI", [128, 4, 128]); LBT = P.sbuf("g_LBT", [128, 4, 128]); LBm = P.sbuf("g_LB", [128, 4, 128])
            NAT = P.sbuf("g_NAT", [128, 4, 128]); NA = P.sbuf("g_NA", [128, 4, 128]); QKm = P.sbuf("g_QKm", [128, 4, 128])
            Tm = P.sbuf("g_Tm", [128, 4, 128]); Wm = P.sbuf("g_Wm", [128, 4, 128])
            x1 = P.sbuf("g_x1", [128, 4, 128]); y1 = P.sbuf("g_y1", [128, 4, 128])
            tmx = P.sbuf("g_tmx", [128, 4, 128]); tmy = P.sbuf("g_tmy", [128, 4, 128])
            Rm = [P.sbuf("g_R%d" % h, [128, 128]) for h in range(4)]
            khp = [P.sbuf("g_kh%d" % h, [128, 128]) for h in range(4)]
            vnp = [P.sbuf("g_vn%d" % h, [128, 128]) for h in range(4)]
            for h in range(4):
                kb.MS(khp[h][:], 0.0, [khp[h]], eng="pool")
                kb.MS(vnp[h][:], 0.0, [vnp[h]], eng="pool")
            upair = [P.sbuf("g_up%d" % hp, [128, 128]) for hp in range(2)]
            wTp = [P.sbuf("g_wT%d" % hp, [128, 128]) for hp in range(2)]
            EG = [P.sbuf("g_EG%d" % hp, [128, 128]) for hp in range(2)]
            qd = [P.sbuf("g_qd%d" % hp, [128, 128]) for hp in range(2)]
            cdp = [P.sbuf("g_cdp%d" % hp, [128, 1]) for hp in range(2)]
            Sb = [P.sbuf("g_S%d" % hp, [128, 128]) for hp in range(2)]
            B = [P.psum("g_B%d" % i, [128, 512]) for i in range(8)]
            ident = kb.C("IDENT")
            it = 0
            for d in range(2):
                tri = kb.C("TRIF" if d == 0 else "TRIB")
                rem = kb.C("SUFF" if d == 0 else "PREB")
                n_incl = kb.C("NLE" if d == 0 else "NGE")
                n_strT = kb.C("NLT" if d == 0 else "NGT")
                n_str = kb.C("NGT" if d == 0 else "NLT")
                for hp in range(2):
                    kb.MS(Sb[hp][:], 0.0, [Sb[hp]])
                for n in ORDER[d][:kb.cfg.get("ntiles", NT)]:
                    cols = slice(n * 128, (n + 1) * 128)
                    b = it % 2; it += 1
                    qn, kn, vv, gab = qnb[b], knb[b], vvb[b], gabb[b]
                    kb.LD(qn[:], kb.QKVF[0:256, cols].rearrange("(hp p) t -> p hp t", p=128), [qn])
                    kb.LD(kn[:], kb.QKVF[256:512, cols].rearrange("(hp p) t -> p hp t", p=128), [kn])
                    kb.LD(vv[:], kb.QKVF[512:768, cols].rearrange("(hp p) t -> p hp t", p=128), [vv])
                    kb.LD(gab[:], kb.ZT[cols, 512:528], [gab])
                    kb.TT(xa[:], gab[:, 4 * d:4 * d + 4], DTB[:, 4 * d:4 * d + 4], ALU.add, [gab, DTB], [xa])
                    kb.ACT(ea[:], xa[:], AF.Exp, [xa], [ea])
                    kb.ACT(ea[:], ea[:], AF.Ln, [ea], [ea], bias=kb.C("CCOL")[:, 1:2])
                    kb.TT(loga[:], ea[:], NEGA[:, 4 * d:4 * d + 4], ALU.mult, [ea, NEGA], [loga])
                    kb.ACT(beta[:], gab[:, 8 + 4 * d:12 + 4 * d], AF.Sigmoid, [gab], [beta])
                    kb.ACT(lnb[:], beta[:], AF.Ln, [beta], [lnb])
                    kb.MM(B[0][:, 0:4], tri, loga[:], True, True, [loga], [B[0]])
                    kb.MM(B[0][:, 4:8], rem, loga[:], True, True, [loga], [B[0]])
                    kb.MM(B[0][:, 8:12], kb.C("ONES"), loga[:], True, True, [loga], [B[0]])
                    kb.CP(gtm[:], B[0][:, 0:4], [B[0]], [gtm])
                    kb.TS(ngt[:], B[0][:, 0:4], -1.0, None, ALU.mult, None, [B[0]], [ngt])
                    kb.ACT(ekr[:], B[0][:, 4:8], AF.Exp, [B[0]], [ekr])
                    kb.ACT(cdec[:], B[0][:, 8:12], AF.Exp, [B[0]], [cdec])
                    kb.ACT(eg[:], gtm[:], AF.Exp, [gtm], [eg])
                    kb.TT(beg[:], beta[:], eg[:], ALU.mult, [beta, eg], [beg])
                    kb.TT(gpl[:], gtm[:], lnb[:], ALU.add, [gtm, lnb], [gpl])
                    kb.MM(B[1][0:4, 0:128], loga[:], tri, True, True, [loga], [B[1]])
                    kb.MM(B[1][0:4, 128:256], loga[:], tri, True, False, [loga], [B[1]])
                    kb.MM(B[1][0:4, 128:256], lnb[:], ident, False, True, [lnb], [B[1]])
                    kb.CP(ROWS[:, 0:256], B[1][0:4, 0:256], [B[1]], [ROWS])
                    kb.TS(ROWS[:, 256:384], B[1][0:4, 0:128], -1.0, None, ALU.mult, None, [B[1]], [ROWS])
                    if upto < 2:
                        continue
                    for (dst, rsl, negm, bias_t, bank) in ((LI, slice(0, 128), n_incl, ngt, B[2]),
                                                           (LBT, slice(128, 256), n_strT, ngt, B[3]),
                                                           (LBm, slice(256, 384), n_str, gpl, B[2])):
                        for h in range(4):
                            kb.MM(bank[:, h * 128:(h + 1) * 128], kb.C("SELH%d" % h)[0:4, :], ROWS[:, rsl], True, False,
                                  [ROWS], [bank])
                            kb.MM(bank[:, h * 128:(h + 1) * 128], ident, negm, False, True, [], [bank])
                        for h in range(4):
                            kb.ACT(dst[:, h, :], bank[:, h * 128:(h + 1) * 128], AF.Exp, [bank, bias_t], [dst],
                                   bias=bias_t[:, h:h + 1])
                    if upto < 3:
                        continue
                    for h in range(4):
                        hp, h2 = divmod(h, 2)
                        ksl = kn[64 * h2:64 * h2 + 64, hp, :]
                        kb.MM(B[4][:, h * 128:(h + 1) * 128], ksl, ksl, True, True, [kn], [B[4]])
                        kb.MM(B[5][:, h * 128:(h + 1) * 128], ksl, qn[64 * h2:64 * h2 + 64, hp, :], True, True, [kn, qn], [B[5]])
                    b4v = B[4][:].rearrange("p (h t) -> p h t", h=4)
                    b5v = B[5][:].rearrange("p (h t) -> p h t", h=4)
                    kb.STT(NAT[:], b4v, -1.0, LBT[:], ALU.mult, ALU.mult, [B[4], LBT], [NAT])
                    kb.STT(NA[:], b4v, -1.0, LBm[:], ALU.mult, ALU.mult, [B[4], LBm], [NA])
                    kb.TT(QKm[:], b5v, LI[:], ALU.mult, [B[5], LI], [QKm])
                    if upto < 4:
                        continue
                    idb = ident.unsqueeze(1).to_broadcast([128, 4, 128])
                    kb.CP(Tm[:], idb, [], [Tm])
                    kb.CP(Wm[:], idb, [], [Wm], eng="pool")
                    for s_ in (1, 2, 4, 8, 16, 32, 64):
                        mT = kb.C(("MOFF%d" if d == 0 else "MOFFT%d") % s_).unsqueeze(1).to_broadcast([128, 4, 128])
                        mW = kb.C(("MOFFT%d" if d == 0 else "MOFF%d") % s_).unsqueeze(1).to_broadcast([128, 4, 128])
                        for h in range(4):
                            kb.MM(B[2][:, h * 128:(h + 1) * 128], NAT[:, h, :], Tm[:, h, :], True, True, [NAT, Tm], [B[2]])
                        for h in range(4):
                            kb.MM(B[3][:, h * 128:(h + 1) * 128], NA[:, h, :], Wm[:, h, :], True, True, [NA, Wm], [B[3]])
                        kb.CP(x1[:], B[2][:].rearrange("p (h t) -> p h t", h=4), [B[2]], [x1], eng="act")
                        kb.CP(y1[:], B[3][:].rearrange("p (h t) -> p h t", h=4), [B[3]], [y1], eng="dve")
                        for h in range(4):
                            kb.MM(B[4][:, h * 128:(h + 1) * 128], Wm[:, h, :], x1[:, h, :], True, True, [Wm, x1], [B[4]])
                        for h in range(4):
                            kb.MM(B[5][:, h * 128:(h + 1) * 128], Tm[:, h, :], y1[:, h, :], True, True, [Tm, y1], [B[5]])
                        kb.TT(tmx[:], B[4][:].rearrange("p (h t) -> p h t", h=4), mT, ALU.mult, [B[4]], [tmx])
                        kb.TT(tmy[:], B[5][:].rearrange("p (h t) -> p h t", h=4), mW, ALU.mult, [B[5]], [tmy])
                        kb.TT(Tm[:], Tm[:], tmx[:], ALU.add, [Tm, tmx], [Tm], eng="pool")
                        kb.TT(Wm[:], Wm[:], tmy[:], ALU.add, [Wm, tmy], [Wm], eng="pool")
                    if upto < 5:
                        continue
                    for hp in range(2):
                        kb.TR(B[0][:, 128:256], kn[:, hp, :], ident, [kn], [B[0]])
                        kb.TR(B[0][:, 256:384], vv[:, hp, :], ident, [vv], [B[0]])
                        for h2 in range(2):
                            h = 2 * hp + h2
                            kc = slice(64 * h2, 64 * h2 + 64)
                            vc = slice(64 * (1 - h2), 64 * (1 - h2) + 64)
                            kb.TS(Rm[h][:, kc], B[0][:, 128 + 64 * h2:128 + 64 * h2 + 64], beg[:, h:h + 1], None, ALU.mult, None,
                                  [B[0], beg], [Rm[h]])
                            kb.ACT(Rm[h][:, vc], B[0][:, 256 + 64 * h2:256 + 64 * h2 + 64], AF.Copy, [B[0], beta], [Rm[h]],
                                   scale=beta[:, h:h + 1])
                            kb.ACT(khp[h][:, kc], B[0][:, 128 + 64 * h2:128 + 64 * h2 + 64], AF.Copy, [B[0], ekr], [khp[h]],
                                   scale=ekr[:, h:h + 1])
                    if upto < 6:
                        continue
                    for h in range(4):
                        kb.MM(B[2][:, h * 128:(h + 1) * 128], Wm[:, h, :], Rm[h][:], True, True, [Wm, Rm[h]], [B[2]])
                        kb.MM(B[3][:, h * 128:(h + 1) * 128], Rm[h][:], Wm[:, h, :], True, True, [Wm, Rm[h]], [B[3]])
                    for h in range(4):
                        hp, h2 = divmod(h, 2)
                        vc0 = 64 * (1 - h2)
                        kb.CP(upair[hp][:, 64 * h2:64 * h2 + 64], B[2][:, h * 128 + vc0:h * 128 + vc0 + 64], [B[2]], [upair[hp]],
                              eng=("act" if h2 else "dve"))
                        kb.CP(wTp[hp][64 * h2:64 * h2 + 64, :], B[3][64 * h2:64 * h2 + 64, h * 128:(h + 1) * 128], [B[3]], [wTp[hp]],
                              eng=("dve" if h2 else "act"))
                    if upto < 7:
                        continue
                    for hp in range(2):
                        kb.MM(B[1][:, 256:384], kb.C("SELP%d" % hp)[0:4, :], ROWS[:, 0:128], True, True, [ROWS], [B[1]])
                        kb.ACT(EG[hp][:], B[1][:, 256:384], AF.Exp, [B[1]], [EG[hp]])
                        kb.TT(qd[hp][:], qn[:, hp, :], EG[hp][:], ALU.mult, [qn, EG[hp]], [qd[hp]], eng="pool")
                        pws = B[7][:, hp * 128:(hp + 1) * 128]
                        kb.MM(pws, wTp[hp][:], Sb[hp][:], True, True, [wTp[hp], Sb[hp]], [B[7]])
                        for h2 in range(2):
                            h = 2 * hp + h2
                            cs_ = slice(64 * h2, 64 * h2 + 64)
                            kb.TT(vnp[h][:, cs_], upair[hp][:, cs_], B[7][:, hp * 128 + 64 * h2:hp * 128 + 64 * h2 + 64],
                                  ALU.subtract, [upair[hp], B[7]], [vnp[h]])
                        po = B[6][:, hp * 256:hp * 256 + 128]
                        kb.MM(po, Sb[hp][:], qd[hp][:], True, False, [Sb[hp], qd[hp]], [B[6].s(hp)])
                        kb.MM(po, vnp[2 * hp][:], QKm[:, 2 * hp, :], False, False, [vnp[2 * hp], QKm], [B[6].s(hp)])
                        kb.MM(po, vnp[2 * hp + 1][:], QKm[:, 2 * hp + 1, :], False, True, [vnp[2 * hp + 1], QKm], [B[6].s(hp)])
                        cols_ = slice(n * 128, (n + 1) * 128)
                        if d == 0:
                            kb.CP(OACC[:, hp, cols_], po, [B[6].s(hp)], [OACC.s(n)], eng="act")
                        else:
                            kb.TT(OACC[:, hp, cols_], OACC[:, hp, cols_], po, ALU.add, [B[6].s(hp)], [OACC.s(n)])
                        pkv = B[6][:, hp * 256 + 128:hp * 256 + 256]
                        kb.MM(pkv, khp[2 * hp][:], vnp[2 * hp][:], True, False, [khp[2 * hp], vnp[2 * hp]], [B[6].s(2 + hp)])
                        kb.MM(pkv, khp[2 * hp + 1][:], vnp[2 * hp + 1][:], False, True, [khp[2 * hp + 1], vnp[2 * hp + 1]],
                              [B[6].s(2 + hp)])
                        kb.CP(cdp[hp][0:64, :], cdec[0:64, 2 * hp:2 * hp + 1], [cdec], [cdp[hp]])
                        kb.CP(cdp[hp][64:128, :], cdec[64:128, 2 * hp + 1:2 * hp + 2], [cdec], [cdp[hp]])
                        kb.STT(Sb[hp][:], Sb[hp][:], cdp[hp][:, 0:1], pkv, ALU.mult, ALU.add,
                               [Sb[hp], cdp[hp], B[6].s(2 + hp)], [Sb[hp]])
        with P.scope():
            G = P.sbuf("g_G2", [128, 1])
            for hh in range(2):
                kb.LD(G[64 * hh:64 * hh + 64, :], kb.prm["gdn_norm_g"][l].rearrange("(p o) -> p o", o=1), [G])
            finalize_gated(kb, OACC, Z_GG, G, 512, "g_")


class _Ctx:
    pass


def mixer_gdn2(kb, l):
    P = kb.P
    (gdn_conv if kb.cfg.get('conv_old') else gdn_conv2)(kb, l)
    with P.scope():
        OACC = P.sbuf("g_oacc", [128, 2, S])
        kb.MS(OACC[:, 0, :], 0.0, [OACC.s(n) for n in range(NT)], eng="pool")
        kb.MS(OACC[:, 1, :], 0.0, [OACC.s(n) for n in range(NT)], eng="pool")
        with P.scope():
            DTB = P.sbuf("g_dtb", [128, 8]); NEGA = P.sbuf("g_nega", [128, 8])
            kb.LD(DTB[:], kb.prm["gdn_dt_bias"][l].rearrange("d h -> (d h)").partition_broadcast(128), [DTB])
            kb.LD(NEGA[:], kb.prm["gdn_a_log"][l].rearrange("d h -> (d h)").partition_broadcast(128), [NEGA])
            kb.ACT(NEGA[:], NEGA[:], AF.Exp, [NEGA], [NEGA])
            kb.TS(NEGA[:], NEGA[:], -1.0, None, ALU.mult, None, [NEGA], [NEGA])
            ident = kb.C("IDENT")
            idb = ident.unsqueeze(1).to_broadcast([128, 4, 128])
            cxs = []
            for d in range(2):
                cx = _Ctx()
                cx.d = d
                pf = "g%d_" % d
                cx.qnb = [P.sbuf(pf + "q%d" % i, [128, 2, 128]) for i in range(2)]
                cx.knb = [P.sbuf(pf + "k%d" % i, [128, 2, 128]) for i in range(2)]
                cx.vvb = [P.sbuf(pf + "v%d" % i, [128, 2, 128]) for i in range(2)]
                cx.gabb = [P.sbuf(pf + "gab%d" % i, [128, 16]) for i in range(2)]
                for nm in ("xa", "ea", "loga", "beta", "lnb", "gtm", "ngt", "ekr", "cdec", "eg", "beg", "gpl"):
                    setattr(cx, nm, P.sbuf(pf + nm, [128, 4]))
                cx.ROWS = P.sbuf(pf + "rows", [4, 384])
                for nm in ("LI", "LBT", "LBm"):
                    setattr(cx, nm, P.sbuf(pf + nm, [128, 4, 128]))
                cx.QKm = P.sbuf(pf + "QKm", [128, 4, 128], BF16)
                cx.ROWSX = P.sbuf(pf + "rowsx", [4, 3, 4, 128])
                cx.knp = [[P.sbuf(pf + "knp%d%d" % (i, h), [128, 128], BF16) for h in range(4)] for i in range(2)]
                for i in range(2):
                    for h in range(4):
                        kb.MS(cx.knp[i][h][:], 0.0, [cx.knp[i][h]], eng="pool")
                cx.kq16 = [P.sbuf(pf + "kq16%d" % i, [128, 2, 2, 128], BF16) for i in range(2)]
                for nm in ("NAT", "NA", "Tm", "Wm", "x1", "y1", "tmx", "tmy"):
                    setattr(cx, nm, P.sbuf(pf + nm, [128, 4, 128], BF16))
                cx.Rm = [P.sbuf(pf + "R%d" % h, [128, 128], BF16) for h in range(4)]
                cx.khp = [P.sbuf(pf + "kh%d" % h, [128, 128], BF16) for h in range(4)]
                cx.vnp = [P.sbuf(pf + "vn%d" % h, [128, 128], BF16) for h in range(4)]
                for h in range(4):
                    kb.MS(cx.khp[h][:], 0.0, [cx.khp[h]], eng="pool")
                    kb.MS(cx.vnp[h][:], 0.0, [cx.vnp[h]], eng="pool")
                cx.upair = [P.sbuf(pf + "up%d" % hp, [128, 128]) for hp in range(2)]
                cx.wTp = [P.sbuf(pf + "wT%d" % hp, [128, 128]) for hp in range(2)]
                cx.EG = [P.sbuf(pf + "EG%d" % hp, [128, 128]) for hp in range(2)]
                cx.qd = [P.sbuf(pf + "qd%d" % hp, [128, 128]) for hp in range(2)]
                cx.cdp = [P.sbuf(pf + "cdp%d" % hp, [128, 1]) for hp in range(2)]
                cx.Sb = [P.sbuf(pf + "S%d" % hp, [128, 128]) for hp in range(2)]
                for hp in range(2):
                    kb.MS(cx.Sb[hp][:], 0.0, [cx.Sb[hp]])
                cx.B = [P.psum(pf + "B%d" % i, [128, 512]) for i in range(4)]
                cx.tri = kb.C("TRIF" if d == 0 else "TRIB")
                cx.rem = kb.C("SUFF" if d == 0 else "PREB")
                cx.n_incl = kb.C("NLE" if d == 0 else "NGE")
                cx.n_strT = kb.C("NLT" if d == 0 else "NGT")
                cx.n_str = kb.C("NGT" if d == 0 else "NLT")
                cx.it = 0
                cx.id16 = P.sbuf(pf + "id16", [128, 128], BF16)
                kb.CP(cx.id16[:], ident, [], [cx.id16])
                for nm_, cn in (("n_incl4", "NLE" if d == 0 else "NGE"), ("n_strT4", "NLT" if d == 0 else "NGT"),
                                ("n_str4", "NGT" if d == 0 else "NLT")):
                    t_ = P.sbuf(pf + nm_, [128, 4, 128], BF16)
                    kb.CP(t_[:], kb.C(cn).unsqueeze(1).to_broadcast([128, 4, 128]), [], [t_])
                    setattr(cx, nm_, t_[:].rearrange("p h i -> p (h i)"))
                bd = P.sbuf(pf + "bd4", [4, 4, 128])
                for h in range(4):
                    kb.CP(bd[:, h, :], kb.C("SELH%d" % h)[0:4, :], [], [bd])
                cx.bd4 = bd[:]
                cx.mT = {}; cx.mW = {}
                for s_ in (2, 4, 8, 16, 32, 64):
                    for nm_, dct, cn in (("mT", cx.mT, ("MOFF%d" if d == 0 else "MOFFT%d") % s_),
                                         ("mW", cx.mW, ("MOFFT%d" if d == 0 else "MOFF%d") % s_)):
                        mt_ = P.sbuf(pf + nm_ + str(s_), [128, 4, 128], mybir.dt.uint8)
                        kb.CP(mt_[:], kb.C(cn).unsqueeze(1).to_broadcast([128, 4, 128]), [], [mt_])
                        dct[s_] = mt_
                cxs.append(cx)

            def step(cx, n):
                d = cx.d
                Pa, Pb, Pc, Pd = cx.B
                cols = slice(n * 128, (n + 1) * 128)
                b = cx.it % 2; cx.it += 1
                qn, kn, vv, gab = cx.qnb[b], cx.knb[b], cx.vvb[b], cx.gabb[b]
                xa, ea, loga, beta, lnb = cx.xa, cx.ea, cx.loga, cx.beta, cx.lnb
                gtm, ngt, ekr, cdec, eg, beg, gpl = cx.gtm, cx.ngt, cx.ekr, cx.cdec, cx.eg, cx.beg, cx.gpl
                ROWS, LI, LBT, LBm, NAT, NA, QKm = cx.ROWS, cx.LI, cx.LBT, cx.LBm, cx.NAT, cx.NA, cx.QKm
                Tm, Wm, x1, y1, tmx, tmy = cx.Tm, cx.Wm, cx.x1, cx.y1, cx.tmx, cx.tmy
                Rm, khp, vnp, upair, wTp, EG, qd, cdp, Sb = cx.Rm, cx.khp, cx.vnp, cx.upair, cx.wTp, cx.EG, cx.qd, cx.cdp, cx.Sb
                tri = cx.tri
                kb.LD(qn[:], kb.QKVF[0:256, cols].rearrange("(hp p) t -> p hp t", p=128), [qn])
                kb.LD(kn[:], kb.QKVF[256:512, cols].rearrange("(hp p) t -> p hp t", p=128), [kn])
                kb.LD(vv[:], kb.QKVF[512:768, cols].rearrange("(hp p) t -> p hp t", p=128), [vv])
                kb.LD(gab[:], kb.ZT[cols, 512:528], [gab])
                knp = cx.knp[b]; kq16 = cx.kq16[b]
                for h in range(4):
                    kb.LD(knp[h][64 * (h % 2):64 * (h % 2) + 64, :], kb.QKVF[256 + 64 * h:256 + 64 * h + 64, cols], [knp[h]], q="pool")
                kb.LD(kq16[:, 0, :, :], kb.QKVF[256:512, cols].rearrange("(hp p) t -> p hp t", p=128), [kq16], q="pool")
                kb.LD(kq16[:, 1, :, :], kb.QKVF[0:256, cols].rearrange("(hp p) t -> p hp t", p=128), [kq16], q="pool")
                kb.TT(xa[:], gab[:, 4 * d:4 * d + 4], DTB[:, 4 * d:4 * d + 4], ALU.add, [gab, DTB], [xa])
                kb.ACT(ea[:], xa[:], AF.Exp, [xa], [ea])
                kb.ACT(ea[:], ea[:], AF.Ln, [ea], [ea], bias=kb.C("CCOL")[:, 1:2])
                kb.TT(loga[:], ea[:], NEGA[:, 4 * d:4 * d + 4], ALU.mult, [ea, NEGA], [loga])
                kb.ACT(beta[:], gab[:, 8 + 4 * d:12 + 4 * d], AF.Sigmoid, [gab], [beta])
                kb.ACT(lnb[:], beta[:], AF.Ln, [beta], [lnb])
                kb.MM(Pc[:, 0:4], tri, loga[:], True, True, [loga], [Pc])
                kb.MM(Pc[:, 4:8], cx.rem, loga[:], True, True, [loga], [Pc])
                kb.MM(Pc[:, 8:12], kb.C("ONES"), loga[:], True, True, [loga], [Pc])
                kb.CP(gtm[:], Pc[:, 0:4], [Pc], [gtm])
                kb.TS(ngt[:], Pc[:, 0:4], -1.0, None, ALU.mult, None, [Pc], [ngt])
                kb.ACT(ekr[:], Pc[:, 4:8], AF.Exp, [Pc], [ekr])
                kb.ACT(cdec[:], Pc[:, 8:12], AF.Exp, [Pc], [cdec])
                kb.ACT(eg[:], gtm[:], AF.Exp, [gtm], [eg])
                kb.TT(beg[:], beta[:], eg[:], ALU.mult, [beta, eg], [beg])
                kb.TT(gpl[:], gtm[:], lnb[:], ALU.add, [gtm, lnb], [gpl])
                kb.MM(Pd[0:4, 0:128], loga[:], tri, True, True, [loga], [Pd])
                kb.MM(Pd[0:4, 128:256], loga[:], tri, True, False, [loga], [Pd])
                kb.MM(Pd[0:4, 128:256], lnb[:], ident, False, True, [lnb], [Pd])
                kb.CP(ROWS[:, 0:256], Pd[0:4, 0:256], [Pd], [ROWS])
                kb.TS(ROWS[:, 256:384], Pd[0:4, 0:128], -1.0, None, ALU.mult, None, [Pd], [ROWS])
                yield
                kb.TT(cx.ROWSX[:], ROWS[:].rearrange("c (r i) -> c r i", r=3).unsqueeze(2).to_broadcast([4, 3, 4, 128]),
                      cx.bd4.unsqueeze(1).to_broadcast([4, 3, 4, 128]), ALU.mult, [ROWS], [cx.ROWSX])
                for (dst, ri, negm4, bias_t, bank) in ((LI, 0, cx.n_incl4, ngt, Pa), (LBT, 1, cx.n_strT4, ngt, Pb),
                                                       (LBm, 2, cx.n_str4, gpl, Pa)):
                    kb.MM(bank[:], kb.C("ONES")[0:4, :], cx.ROWSX[:, ri, :, :].rearrange("c h i -> c (h i)"), True, False,
                          [cx.ROWSX], [bank])
                    kb.MM(bank[:], cx.id16[:], negm4[:], False, True, [], [bank])
                    yield
                    for h in range(4):
                        kb.ACT(dst[:, h, :], bank[:, h * 128:(h + 1) * 128], AF.Exp, [bank, bias_t], [dst], bias=bias_t[:, h:h + 1])
                    yield
                for h in range(4):
                    hp, h2 = divmod(h, 2)
                    kb.MM(Pa[:, h * 128:(h + 1) * 128], knp[h][:], kq16[:, 0, hp, :], True, True, [knp[h], kq16], [Pa])
                    kb.MM(Pb[:, h * 128:(h + 1) * 128], knp[h][:], kq16[:, 1, hp, :], True, True, [knp[h], kq16], [Pb])
                pav = Pa[:].rearrange("p (h t) -> p h t", h=4)
                pbv = Pb[:].rearrange("p (h t) -> p h t", h=4)
                kb.STT(NAT[:], pav, -1.0, LBT[:], ALU.mult, ALU.mult, [Pa, LBT], [NAT])
                kb.STT(NA[:], pav, -1.0, LBm[:], ALU.mult, ALU.mult, [Pa, LBm], [NA])
                kb.TT(QKm[:], pbv, LI[:], ALU.mult, [Pb, LI], [QKm])
                yield
                mT = kb.C("MOFF1" if d == 0 else "MOFFT1").unsqueeze(1).to_broadcast([128, 4, 128])
                mW = kb.C("MOFFT1" if d == 0 else "MOFF1").unsqueeze(1).to_broadcast([128, 4, 128])
                kb.TT(tmx[:], NA[:], mT, ALU.mult, [NA], [tmx])
                kb.TT(tmy[:], NAT[:], mW, ALU.mult, [NAT], [tmy], eng="pool")
                kb.TT(Tm[:], tmx[:], idb, ALU.add, [tmx], [Tm])
                kb.TT(Wm[:], tmy[:], idb, ALU.add, [tmy], [Wm], eng="pool")
                yield
                for s_ in (2, 4, 8, 16, 32, 64):
                    for h in range(4):
                        kb.MM(Pa[:, h * 128:(h + 1) * 128], NAT[:, h, :], Tm[:, h, :], True, True, [NAT, Tm], [Pa])
                    for h in range(4):
                        kb.MM(Pb[:, h * 128:(h + 1) * 128], NA[:, h, :], Wm[:, h, :], True, True, [NA, Wm], [Pb])
                    yield
                    kb.CP(x1[:], pav, [Pa], [x1], eng="act")
                    kb.CP(y1[:], pbv, [Pb], [y1], eng="act")
                    yield
                    for h in range(4):
                        kb.MM(Pa[:, h * 128:(h + 1) * 128], Wm[:, h, :], x1[:, h, :], True, True, [Wm, x1], [Pa])
                    for h in range(4):
                        kb.MM(Pb[:, h * 128:(h + 1) * 128], Tm[:, h, :], y1[:, h, :], True, True, [Tm, y1], [Pb])
                    yield
                    kb.CPRED(Tm[:], cx.mT[s_][:], pav, [Pa, cx.mT[s_]], [Tm])
                    kb.CPRED(Wm[:], cx.mW[s_][:], pbv, [Pb, cx.mW[s_]], [Wm])
                    yield
                for hp in range(2):
                    kb.TR(Pc[:, 128:256], kn[:, hp, :], ident, [kn], [Pc])
                    kb.TR(Pc[:, 256:384], vv[:, hp, :], ident, [vv], [Pc])
                    for h2 in range(2):
                        h = 2 * hp + h2
                        kc = slice(64 * h2, 64 * h2 + 64)
                        vc = slice(64 * (1 - h2), 64 * (1 - h2) + 64)
                        kb.TS(Rm[h][:, kc], Pc[:, 128 + 64 * h2:128 + 64 * h2 + 64], beg[:, h:h + 1], None, ALU.mult, None,
                              [Pc, beg], [Rm[h]])
                        kb.ACT(Rm[h][:, vc], Pc[:, 256 + 64 * h2:256 + 64 * h2 + 64], AF.Copy, [Pc, beta], [Rm[h]],
                               scale=beta[:, h:h + 1])
                        kb.ACT(khp[h][:, kc], Pc[:, 128 + 64 * h2:128 + 64 * h2 + 64], AF.Copy, [Pc, ekr], [khp[h]],
                               scale=ekr[:, h:h + 1])
                    yield
                for h in range(4):
                    kb.MM(Pa[:, h * 128:(h + 1) * 128], Wm[:, h, :], Rm[h][:], True, True, [Wm, Rm[h]], [Pa])
                    kb.MM(Pb[:, h * 128:(h + 1) * 128], Rm[h][:], Wm[:, h, :], True, True, [Wm, Rm[h]], [Pb])
                for h in range(4):
                    hp, h2 = divmod(h, 2)
                    vc0 = 64 * (1 - h2)
                    kb.CP(upair[hp][:, 64 * h2:64 * h2 + 64], Pa[:, h * 128 + vc0:h * 128 + vc0 + 64], [Pa], [upair[hp]], eng="dve")
                    kb.CP(wTp[hp][64 * h2:64 * h2 + 64, :], Pb[64 * h2:64 * h2 + 64, h * 128:(h + 1) * 128], [Pb], [wTp[hp]], eng="act")
                yield
                for hp in range(2):
                    kb.MM(Pc[:, 384:512], kb.C("SELP%d" % hp)[0:4, :], ROWS[:, 0:128], True, True, [ROWS], [Pc])
                    kb.ACT(EG[hp][:], Pc[:, 384:512], AF.Exp, [Pc], [EG[hp]])
                    kb.TT(qd[hp][:], qn[:, hp, :], EG[hp][:], ALU.mult, [qn, EG[hp]], [qd[hp]], eng="pool")
                    pws = Pc[:, hp * 128:(hp + 1) * 128]
                    kb.MM(pws, wTp[hp][:], Sb[hp][:], True, True, [wTp[hp], Sb[hp]], [Pc])
                    for h2 in range(2):
                        h = 2 * hp + h2
                        cs_ = slice(64 * h2, 64 * h2 + 64)
                        kb.TT(vnp[h][:, cs_], upair[hp][:, cs_], Pc[:, hp * 128 + 64 * h2:hp * 128 + 64 * h2 + 64],
                              ALU.subtract, [upair[hp], Pc], [vnp[h]])
                    po = Pd[:, hp * 256:hp * 256 + 128]
                    kb.MM(po, Sb[hp][:], qd[hp][:], True, False, [Sb[hp], qd[hp]], [Pd])
                    kb.MM(po, vnp[2 * hp][:], QKm[:, 2 * hp, :], False, False, [vnp[2 * hp], QKm], [Pd])
                    kb.MM(po, vnp[2 * hp + 1][:], QKm[:, 2 * hp + 1, :], False, True, [vnp[2 * hp + 1], QKm], [Pd])
                    kb.TT(OACC[:, hp, cols], OACC[:, hp, cols], po, ALU.add, [Pd], [OACC.s(n)])
                    pkv = Pd[:, hp * 256 + 128:hp * 256 + 256]
                    kb.MM(pkv, khp[2 * hp][:], vnp[2 * hp][:], True, False, [khp[2 * hp], vnp[2 * hp]], [Pd])
                    kb.MM(pkv, khp[2 * hp + 1][:], vnp[2 * hp + 1][:], False, True, [khp[2 * hp + 1], vnp[2 * hp + 1]], [Pd])
                    kb.CP(cdp[hp][0:64, :], cdec[0:64, 2 * hp:2 * hp + 1], [cdec], [cdp[hp]])
                    kb.CP(cdp[hp][64:128, :], cdec[64:128, 2 * hp + 1:2 * hp + 2], [cdec], [cdp[hp]])
                    kb.STT(Sb[hp][:], Sb[hp][:], cdp[hp][:, 0:1], pkv, ALU.mult, ALU.add, [Sb[hp], cdp[hp], Pd], [Sb[hp]])
                    yield

            def stream(cx):
                for n in ORDER[cx.d][:kb.cfg.get("ntiles", NT)]:
                    yield from step(cx, n)
            active = [stream(cxs[0]), stream(cxs[1])]
            while active:
                for g_ in list(active):
                    try:
                        next(g_)
                    except StopIteration:
                        active.remove(g_)
        with P.scope():
            G = P.sbuf("g_G2", [128, 1])
            for hh in range(2):
                kb.LD(G[64 * hh:64 * hh + 64, :], kb.prm["gdn_norm_g"][l].rearrange("(p o) -> p o", o=1), [G])
            finalize_gated(kb, OACC, Z_GG, G, 512, "g_")


def run_interleaved(gens):
    active = list(gens)
    while active:
        for g_ in list(active):
            try:
                next(g_)
            except StopIteration:
                active.remove(g_)


def oacc_add(kb, OACC, hp, n, ps):
    cols = slice(n * 128, (n + 1) * 128)
    kb.TT(OACC[:, hp, cols], OACC[:, hp, cols], ps[:], ALU.add, [ps], [OACC.s(n)])


def oacc_zero(kb, OACC):
    for hp in range(2):
        kb.MS(OACC[:, hp, :], 0.0, [OACC.s(n) for n in range(NT)], eng="pool")


def mixer_hgrn2(kb, l):
    P = kb.P
    with P.scope():
        OACC = P.sbuf("h_oacc", [128, 2, S])
        oacc_zero(kb, OACC)
        with P.scope():
            LB = P.sbuf("h_LB", [128, 4]); OML = P.sbuf("h_OML", [128, 4])
            if l == 0:
                kb.MS(LB[:], 0.0, [LB]); kb.MS(OML[:], 1.0, [OML])
            else:
                lgt = P.sbuf("h_lgt", [128, 8])
                kb.LD(lgt[:], kb.prm["hgrn_lb_logits"][:].rearrange("l d (hp p) -> p (l d hp)", p=128), [lgt],
                      allow_slow_non_contiguous=True)
                kb.TT(LB[:], lgt[:, 4:8], lgt[:, 0:4], ALU.subtract, [lgt], [LB])
                kb.ACT(LB[:], LB[:], AF.Sigmoid, [LB], [LB])
                kb.TS(OML[:], LB[:], -1.0, 1.0, ALU.mult, ALU.add, [LB], [OML])

            def make(d):
                pf = "h%d_" % d
                hqb = [P.sbuf(pf + "q%d" % i, [128, 2, 128]) for i in range(2)]
                hfb = [P.sbuf(pf + "f%d" % i, [128, 2, 128]) for i in range(2)]
                Vp = [[P.sbuf(pf + "vp%d%d" % (i, h), [128, 128]) for h in range(4)] for i in range(2)]
                khp = [[P.sbuf(pf + "kh%d%d" % (i, h), [128, 128]) for h in range(4)] for i in range(2)]
                for i in range(2):
                    for h in range(4):
                        kb.MS(Vp[i][h][:], 0.0, [Vp[i][h]], eng="pool")
                        kb.MS(khp[i][h][:], 0.0, [khp[i][h]], eng="pool")
                MREF = [P.sbuf(pf + "mr%d" % i, [128, 4]) for i in range(2)]
                for i in range(2):
                    kb.MS(MREF[i][:], 0.0, [MREF[i]])

                def two(name, shape=(128, 128)):
                    return [P.sbuf(pf + "%s%d" % (name, i), list(shape)) for i in range(2)]
                qs, sgm, ff, logf, kk, bb, pre = two("qs"), two("sg"), two("ff"), two("lf"), two("kk"), two("bb"), two("pre")
                e1, Ql, e2, Qd = two("e1"), two("Ql"), two("e2"), two("Qd")
                Kt = [two("Kt%d" % r) for r in range(4)]
                ex = two("ex")
                AT = two("AT", (128, 2, 128))
                KhT = two("KhT")
                bend = two("bend", (128, 2))
                Sb = [P.sbuf(pf + "S%d" % hp, [128, 128]) for hp in range(2)]
                for hp in range(2):
                    kb.MS(Sb[hp][:], 0.0, [Sb[hp]])
                pss = P.psum(pf + "pss", [128, 2, 128])
                po = P.psum(pf + "pso", [128, 128])
                pk = P.psum(pf + "psk", [128, 128])
                pkv = P.psum(pf + "pskv", [128, 128])
                zf = Z_HFF if d == 0 else Z_HFB
                tri = kb.C("TRIF" if d == 0 else "TRIB").unsqueeze(1).to_broadcast([128, 2, 128])

                def gen():
                    it = 0
                    jj = 0
                    for n in ORDER[d]:
                        cols = slice(n * 128, (n + 1) * 128)
                        b = it % 2; it += 1
                        hq, hf = hqb[b], hfb[b]
                        kb.LD(hq[:], kb.ZF[Z_HQ:Z_HQ + 256, cols].rearrange("(hp p) t -> p hp t", p=128), [hq])
                        kb.LD(hf[:], kb.ZF[zf:zf + 256, cols].rearrange("(hp p) t -> p hp t", p=128), [hf])
                        for h in range(4):
                            kb.LD(Vp[b][h][:, 64 * (h % 2):64 * (h % 2) + 64], kb.ZT[cols, 64 * h:64 * h + 64], [Vp[b][h]])
                        yield
                        for hp in range(2):
                            j = jj % 2; jj += 1
                            c = 2 * d + hp
                            mref = MREF[j]
                            kb.ACT(qs[j][:], hq[:, hp, :], AF.Exp, [hq], [qs[j]], scale=-1.0)
                            kb.TS(qs[j][:], qs[j][:], 1.0, None, ALU.add, None, [qs[j]], [qs[j]])
                            kb.RECIP(qs[j][:], qs[j][:], [qs[j]], [qs[j]])
                            kb.TT(qs[j][:], qs[j][:], hq[:, hp, :], ALU.mult, [qs[j], hq], [qs[j]], eng="pool")
                            kb.ACT(sgm[j][:], hf[:, hp, :], AF.Exp, [hf], [sgm[j]], scale=-1.0)
                            kb.TS(sgm[j][:], sgm[j][:], 1.0, None, ALU.add, None, [sgm[j]], [sgm[j]])
                            kb.RECIP(sgm[j][:], sgm[j][:], [sgm[j]], [sgm[j]])
                            kb.TS(ff[j][:], sgm[j][:], OML[:, c:c + 1], LB[:, c:c + 1], ALU.mult, ALU.add, [sgm[j], OML, LB], [ff[j]])
                            kb.ACT(logf[j][:], ff[j][:], AF.Ln, [ff[j]], [logf[j]])
                            kb.TS(kk[j][:], ff[j][:], -1.0, 1.0, ALU.mult, ALU.add, [ff[j]], [kk[j]], eng="pool")
                            yield
                            B = bb[j]
                            if d == 0:
                                kb.SCAN(B[:], kb.C("ONES"), logf[j][:], [logf[j]], [B])
                                kb.CP(mref[:, 1:4], B[:].rearrange("p (r c) -> p r c", c=32)[:, 0:3, 31], [B], [mref])
                                be = B[:, 127:128]
                            else:
                                kb.SCAN(pre[j][:], kb.C("ONES"), logf[j][:], [logf[j]], [pre[j]])
                                kb.STT(B[:], pre[j][:], -1.0, logf[j][:], ALU.mult, ALU.add, [pre[j], logf[j]], [B])
                                kb.TS(B[:], B[:], pre[j][:, 127:128], None, ALU.add, None, [B, pre[j]], [B])
                                kb.CP(mref[:, 0:3], B[:].rearrange("p (r c) -> p r c", c=32)[:, 1:4, 0], [B], [mref])
                                be = B[:, 0:1]
                            yield
                            kb.TT(e1[j][:].rearrange("p (r c) -> p r c", c=32), B[:].rearrange("p (r c) -> p r c", c=32),
                                  mref[:].unsqueeze(2).to_broadcast([128, 4, 32]), ALU.subtract, [B, mref], [e1[j]])
                            kb.ACT(e1[j][:], e1[j][:], AF.Exp, [e1[j]], [e1[j]])
                            kb.STT(Ql[j][:], qs[j][:], 0.125, e1[j][:], ALU.mult, ALU.mult, [qs[j], e1[j]], [Ql[j]])
                            kb.ACT(e2[j][:], B[:], AF.Exp, [B], [e2[j]])
                            kb.STT(Qd[j][:], qs[j][:], 0.125, e2[j][:], ALU.mult, ALU.mult, [qs[j], e2[j]], [Qd[j]])
                            yield
                            for r in range(4):
                                kb.ACT(ex[j][:], B[:], AF.Exp, [B, mref], [ex[j]], scale=-1.0, bias=mref[:, r:r + 1])
                                kb.STT(Kt[r][j][:], ex[j][:], 1e26, kk[j][:], ALU.min, ALU.mult, [ex[j], kk[j]], [Kt[r][j]])
                                for h2 in range(2):
                                    kb.MM(pss[:, h2, 32 * r:32 * r + 32], Kt[r][j][64 * h2:64 * h2 + 64, :],
                                          Ql[j][64 * h2:64 * h2 + 64, 32 * r:32 * r + 32], True, True,
                                          [Kt[r][j], Ql[j]], [pss])
                                yield
                            kb.TT(AT[j][:], pss[:], tri, ALU.mult, [pss], [AT[j]])
                            yield
                            kb.MM(po[:], Vp[b][2 * hp][:], AT[j][:, 0, :], True, False, [Vp[b][2 * hp], AT[j]], [po])
                            kb.MM(po[:], Vp[b][2 * hp + 1][:], AT[j][:, 1, :], False, False, [Vp[b][2 * hp + 1], AT[j]], [po])
                            kb.MM(po[:], Sb[hp][:], Qd[j][:], False, True, [Sb[hp], Qd[j]], [po])
                            oacc_add(kb, OACC, hp, n, po)
                            kb.CP(bend[j][:, 0:1], be, [B], [bend[j]])
                            kb.ACT(KhT[j][:], B[:], AF.Exp, [B, bend[j]], [KhT[j]], scale=-1.0, bias=bend[j][:, 0:1])
                            kb.TT(KhT[j][:], KhT[j][:], kk[j][:], ALU.mult, [KhT[j], kk[j]], [KhT[j]], eng="pool")
                            kb.ACT(bend[j][:, 1:2], bend[j][:, 0:1], AF.Exp, [bend[j]], [bend[j]])
                            yield
                            kb.TR(pk[:], KhT[j][:], kb.C("IDENT"), [KhT[j]], [pk])
                            for h2 in range(2):
                                h = 2 * hp + h2
                                kb.CP(khp[b][h][:, 64 * h2:64 * h2 + 64], pk[:, 64 * h2:64 * h2 + 64], [pk], [khp[b][h]],
                                      eng=("act" if h2 else "dve"))
                            yield
                            kb.MM(pkv[:], khp[b][2 * hp][:], Vp[b][2 * hp][:], True, False, [khp[b][2 * hp], Vp[b][2 * hp]], [pkv])
                            kb.MM(pkv[:], khp[b][2 * hp + 1][:], Vp[b][2 * hp + 1][:], False, True,
                                  [khp[b][2 * hp + 1], Vp[b][2 * hp + 1]], [pkv])
                            kb.STT(Sb[hp][:], Sb[hp][:], bend[j][:, 1:2], pkv[:], ALU.mult, ALU.add,
                                   [Sb[hp], bend[j], pkv], [Sb[hp]])
                            yield
                return gen()
            run_interleaved([make(0), make(1)])
        with P.scope():
            G = P.sbuf("h_G2", [128, 1])
            for hh in range(2):
                kb.LD(G[64 * hh:64 * hh + 64, :], kb.prm["hgrn_norm_g"][l].rearrange("(p o) -> p o", o=1), [G])
            finalize_gated(kb, OACC, Z_HG, G, 0, "h_")


def mixer_ret2(kb, l):
    P = kb.P
    with P.scope():
        OACC = P.sbuf("r_oacc", [128, 2, S])
        oacc_zero(kb, OACC)
        with P.scope():
            lgt = P.sbuf("r_lgt", [128, 8])
            kb.LD(lgt[:], kb.prm["ret_decay_logit"][l].rearrange("d h -> (d h)").partition_broadcast(128), [lgt])
            LG = P.sbuf("r_LG", [128, 8])
            kb.ACT(LG[:], lgt[:], AF.Sigmoid, [lgt], [LG])
            kb.ACT(LG[:], LG[:], AF.Ln, [LG], [LG])
            LGP = P.sbuf("r_LGP", [128, 4])
            for d in range(2):
                for hp in range(2):
                    c = 2 * d + hp
                    kb.CP(LGP[0:64, c:c + 1], LG[0:64, 4 * d + 2 * hp:4 * d + 2 * hp + 1], [LG], [LGP])
                    kb.CP(LGP[64:128, c:c + 1], LG[64:128, 4 * d + 2 * hp + 1:4 * d + 2 * hp + 2], [LG], [LGP])
            MK = [P.sbuf("r_MK%d" % d, [128, 4, 128]) for d in range(2)]
            QDEC = [[P.sbuf("r_QD%d%d" % (d, hp), [128, 128]) for hp in range(2)] for d in range(2)]
            etmp = P.sbuf("r_etmp", [128, 128])
            for d in range(2):
                for h in range(4):
                    kb.ACT(etmp[:], kb.C("DIFF" if d == 0 else "NDIFF"), AF.Exp, [LG], [etmp],
                           scale=LG[:, 4 * d + h:4 * d + h + 1])
                    kb.STT(MK[d][:, h, :], etmp[:], 0.125, kb.C("TRIF" if d == 0 else "TRIB"), ALU.mult, ALU.mult,
                           [etmp], [MK[d]])
                for hp in range(2):
                    kb.ACT(QDEC[d][hp][:], kb.C("IOTAF1" if d == 0 else "RIOTAF"), AF.Exp, [LGP], [QDEC[d][hp]],
                           scale=LGP[:, 2 * d + hp:2 * d + hp + 1])
            KD = P.sbuf("r_KD", [128, 8])
            kb.ACT(KD[:, 0:4], LG[:, 0:4], AF.Exp, [LG], [KD], scale=kb.C("CCOL")[:, 3:4])
            kb.ACT(KD[:, 4:8], LG[:, 4:8], AF.Exp, [LG], [KD], scale=kb.C("CCOL")[:, 2:3])
            kb.TS(KD[:], KD[:], 0.125, None, ALU.mult, None, [KD], [KD])
            CV = P.sbuf("r_CV", [128, 4])
            kb.ACT(CV[:], LGP[:], AF.Exp, [LGP], [CV], scale=128.0)

            def make(d):
                pf = "r%d_" % d
                qTb = [P.sbuf(pf + "q%d" % i, [128, 2, 128]) for i in range(2)]
                kTb = [P.sbuf(pf + "k%d" % i, [128, 2, 128]) for i in range(2)]
                csb = [P.sbuf(pf + "cs%d" % i, [128, 2, 128]) for i in range(2)]
                Vp = [[P.sbuf(pf + "vp%d%d" % (i, h), [128, 128]) for h in range(4)] for i in range(2)]
                khp = [[P.sbuf(pf + "kh%d%d" % (i, h), [128, 128]) for h in range(4)] for i in range(2)]
                for i in range(2):
                    for h in range(4):
                        kb.MS(Vp[i][h][:], 0.0, [Vp[i][h]], eng="pool")
                        kb.MS(khp[i][h][:], 0.0, [khp[i][h]], eng="pool")
                t1 = [P.sbuf(pf + "t1%d" % i, [128, 128]) for i in range(2)]
                t2 = [P.sbuf(pf + "t2%d" % i, [128, 128]) for i in range(2)]
                qr = [P.sbuf(pf + "qr%d" % i, [128, 2, 128]) for i in range(2)]
                kr = [P.sbuf(pf + "kr%d" % i, [128, 2, 128]) for i in range(2)]
                AT = [P.sbuf(pf + "AT%d" % i, [128, 2, 128]) for i in range(2)]
                qd = [P.sbuf(pf + "qd%d" % i, [128, 128]) for i in range(2)]
                Sb = [P.sbuf(pf + "S%d" % hp, [128, 128]) for hp in range(2)]
                for hp in range(2):
                    kb.MS(Sb[hp][:], 0.0, [Sb[hp]])
                pr = P.psum(pf + "psr", [128, 256])
                pss = P.psum(pf + "pss", [128, 2, 128])
                po = P.psum(pf + "pso", [128, 128])
                pkk = P.psum(pf + "pskk", [128, 256])

                def gen():
                    it = 0
                    jj = 0
                    for n in ORDER[d]:
                        cols = slice(n * 128, (n + 1) * 128)
                        b = it % 2; it += 1
                        qT, kT, cs = qTb[b], kTb[b], csb[b]
                        kb.LD(qT[:], kb.ZF[Z_RQ:Z_RQ + 256, cols].rearrange("(hp p) t -> p hp t", p=128), [qT])
                        kb.LD(kT[:], kb.ZF[Z_RK:Z_RK + 256, cols].rearrange("(hp p) t -> p hp t", p=128), [kT])
                        kb.LD(cs[:, 0, :], kb.ropec[:, cols], [cs])
                        kb.LD(cs[:, 1, :], kb.ropes[:, cols], [cs])
                        for h in range(4):
                            kb.LD(Vp[b][h][:, 64 * (h % 2):64 * (h % 2) + 64], kb.ZT[cols, 256 + 64 * h:256 + 64 * h + 64],
                                  [Vp[b][h]])
                        yield
                        for hp in range(2):
                            j = jj % 2; jj += 1
                            kb.MM(pr[:, 0:128], kb.C("ROT"), qT[:, hp, :], True, True, [qT], [pr])
                            kb.MM(pr[:, 128:256], kb.C("ROT"), kT[:, hp, :], True, True, [kT], [pr])
                            yield
                            for (src_, dst, off) in ((qT, qr[b], 0), (kT, kr[b], 128)):
                                kb.TT(t1[j][:], src_[:, hp, :], cs[:, 0, :], ALU.mult, [src_, cs], [t1[j]], eng="pool")
                                kb.TT(t2[j][:], pr[:, off:off + 128], cs[:, 1, :], ALU.mult, [pr, cs], [t2[j]])
                                kb.TT(dst[:, hp, :], t1[j][:], t2[j][:], ALU.add, [t1[j], t2[j]], [dst.s(hp)], eng="pool")
                                yield
                            for h2 in range(2):
                                kb.MM(pss[:, h2, :], kr[b][64 * h2:64 * h2 + 64, hp, :], qr[b][64 * h2:64 * h2 + 64, hp, :],
                                      True, True, [kr[b].s(hp), qr[b].s(hp)], [pss])
                            yield
                            kb.TT(AT[j][:], pss[:], MK[d][:, 2 * hp:2 * hp + 2, :], ALU.mult, [pss, MK[d]], [AT[j]])
                            kb.TT(qd[j][:], qr[b][:, hp, :], QDEC[d][hp][:], ALU.mult, [qr[b].s(hp), QDEC[d][hp]], [qd[j]],
                                  eng="pool")
                            yield
                            kb.MM(po[:], Vp[b][2 * hp][:], AT[j][:, 0, :], True, False, [Vp[b][2 * hp], AT[j]], [po])
                            kb.MM(po[:], Vp[b][2 * hp + 1][:], AT[j][:, 1, :], False, False, [Vp[b][2 * hp + 1], AT[j]], [po])
                            kb.MM(po[:], Sb[hp][:], qd[j][:], False, True, [Sb[hp], qd[j]], [po])
                            kb.TR(pkk[:, 0:128], kr[b][:, hp, :], kb.C("IDENT"), [kr[b].s(hp)], [pkk])
                            yield
                            oacc_add(kb, OACC, hp, n, po)
                            for h2 in range(2):
                                h = 2 * hp + h2
                                kb.ACT(khp[b][h][:, 64 * h2:64 * h2 + 64], pkk[:, 64 * h2:64 * h2 + 64], AF.Copy,
                                       [pkk, KD], [khp[b][h]], scale=KD[:, 4 * d + h:4 * d + h + 1])
                            yield
                            kb.MM(pkk[:, 128:256], khp[b][2 * hp][:], Vp[b][2 * hp][:], True, False,
                                  [khp[b][2 * hp], Vp[b][2 * hp]], [pkk])
                            kb.MM(pkk[:, 128:256], khp[b][2 * hp + 1][:], Vp[b][2 * hp + 1][:], False, True,
                                  [khp[b][2 * hp + 1], Vp[b][2 * hp + 1]], [pkk])
                            kb.STT(Sb[hp][:], Sb[hp][:], CV[:, 2 * d + hp:2 * d + hp + 1], pkk[:, 128:256], ALU.mult, ALU.add,
                                   [Sb[hp], CV, pkk], [Sb[hp]])
                            yield
                return gen()
            run_interleaved([make(0), make(1)])
        with P.scope():
            finalize_gated(kb, OACC, Z_RG, None, 256, "r_")


def s5_tables(kb, l, d, VFr, VFi, T1, T2, AR, NAI):
    P = kb.P
    prm = kb.prm
    with P.scope():
        lr = P.sbuf("s_lr", [128, 16, 64]); li = P.sbuf("s_li", [128, 16, 64]); dtb = P.sbuf("s_dt", [128, 16])
        kb.LD(lr[:], prm["s5_lam_re"][l][d].rearrange("g p -> (g p)").partition_broadcast(128), [lr])
        kb.LD(li[:], prm["s5_lam_im"][l][d].rearrange("g p -> (g p)").partition_broadcast(128), [li])
        kb.LD(dtb[:], prm["s5_log_dt"][l][d].partition_broadcast(128), [dtb])
        kb.ACT(dtb[:], dtb[:], AF.Exp, [dtb], [dtb])
        dt_bc = dtb[:].unsqueeze(2).to_broadcast([128, 16, 64])
        lrdt = P.sbuf("s_lrdt", [128, 16, 64]); lidt = P.sbuf("s_lidt", [128, 16, 64])
        kb.TT(lrdt[:], lr[:], dt_bc, ALU.mult, [lr, dtb], [lrdt])
        kb.TT(lidt[:], li[:], dt_bc, ALU.mult, [li, dtb], [lidt])
        a = [P.sbuf("s_a%d" % i, [128, 16, 64]) for i in range(8)]
        mag, ang, sn, cs, tmp, ar, ai, t2 = a
        kb.ACT(mag[:], lrdt[:], AF.Exp, [lrdt], [mag])
        _sincos(kb, lidt[:], sn[:], cs[:], [lidt, sn, cs, tmp], tmp[:])
        kb.TT(ar[:], mag[:], cs[:], ALU.mult, [mag, cs], [ar])
        kb.TT(ai[:], mag[:], sn[:], ALU.mult, [mag, sn], [ai])
        den = P.sbuf("s_den", [128, 16, 64]); fr = P.sbuf("s_fr", [128, 16, 64]); fi = P.sbuf("s_fi", [128, 16, 64])
        kb.TT(den[:], lr[:], lr[:], ALU.mult, [lr], [den])
        kb.TT(t2[:], li[:], li[:], ALU.mult, [li], [t2])
        kb.TT(den[:], den[:], t2[:], ALU.add, [den, t2], [den])
        kb.RECIP(den[:], den[:], [den], [den])
        kb.TS(ar[:], ar[:], -1.0, None, ALU.add, None, [ar], [ar])
        kb.TT(fr[:], ar[:], lr[:], ALU.mult, [ar, lr], [fr])
        kb.TT(t2[:], ai[:], li[:], ALU.mult, [ai, li], [t2])
        kb.TT(fr[:], fr[:], t2[:], ALU.add, [fr, t2], [fr])
        kb.TT(fr[:], fr[:], den[:], ALU.mult, [fr, den], [fr])
        kb.TT(fi[:], ai[:], lr[:], ALU.mult, [ai, lr], [fi])
        kb.TT(t2[:], ar[:], li[:], ALU.mult, [ar, li], [t2])
        kb.TT(fi[:], fi[:], t2[:], ALU.subtract, [fi, t2], [fi])
        kb.TT(fi[:], fi[:], den[:], ALU.mult, [fi, den], [fi])
        jcol = kb.C("CCOL")[:, 2:3] if d == 0 else kb.C("CCOL")[:, 3:4]
        njcol = kb.C("CCOL")[:, 6:7] if d == 0 else kb.C("CCOL")[:, 7:8]
        kb.ACT(mag[:], lrdt[:], AF.Exp, [lrdt], [mag], scale=njcol)
        kb.TS(ang[:], lidt[:], jcol, None, ALU.mult, None, [lidt], [ang])
        _sincos(kb, ang[:], sn[:], cs[:], [ang, sn, cs, tmp], tmp[:])
        vr, vi = ar, ai
        kb.TT(vr[:], mag[:], cs[:], ALU.mult, [mag, cs], [vr])
        kb.TT(vi[:], mag[:], sn[:], ALU.mult, [mag, sn], [vi])
        kb.TS(vi[:], vi[:], -1.0, None, ALU.mult, None, [vi], [vi])
        kb.TT(VFr[:], vr[:], fr[:], ALU.mult, [vr, fr], [VFr])
        kb.TT(t2[:], vi[:], fi[:], ALU.mult, [vi, fi], [t2])
        kb.TT(VFr[:], VFr[:], t2[:], ALU.subtract, [VFr, t2], [VFr])
        kb.TT(VFi[:], vr[:], fi[:], ALU.mult, [vr, fi], [VFi])
        kb.TT(t2[:], vi[:], fr[:], ALU.mult, [vi, fr], [t2])
        kb.TT(VFi[:], VFi[:], t2[:], ALU.add, [VFi, t2], [VFi])
    with P.scope():
        dtb = P.sbuf("s_dt2", [128, 16])
        kb.LD(dtb[:], prm["s5_log_dt"][l][d].partition_broadcast(128), [dtb])
        kb.ACT(dtb[:], dtb[:], AF.Exp, [dtb], [dtb])
        lrp = P.sbuf("s_lrp", [128, 16]); lip = P.sbuf("s_lip", [128, 16])
        for hh in range(2):
            kb.LD(lrp[64 * hh:64 * hh + 64, :], prm["s5_lam_re"][l][d].rearrange("g p -> p g"), [lrp],
                  allow_slow_non_contiguous=True)
            kb.LD(lip[64 * hh:64 * hh + 64, :], prm["s5_lam_im"][l][d].rearrange("g p -> p g"), [lip],
                  allow_slow_non_contiguous=True)
        kb.TT(lrp[:], lrp[:], dtb[:], ALU.mult, [lrp, dtb], [lrp])
        kb.TT(lip[:], lip[:], dtb[:], ALU.mult, [lip, dtb], [lip])
        b4 = [P.sbuf("s_b%d" % i, [128, 16, 128]) for i in range(4)]
        arg, sn2, cs2, tmp2 = b4
        mt = kb.C("IOTAF" if d == 0 else "R127F")
        mt_bc = mt.unsqueeze(1).to_broadcast([128, 16, 128])
        kb.TT(arg[:], lrp[:].unsqueeze(2).to_broadcast([128, 16, 128]), mt_bc, ALU.mult, [lrp], [arg])
        kb.ACT(T1[:], arg[:], AF.Exp, [arg], [T1])
        kb.TT(arg[:], lip[:].unsqueeze(2).to_broadcast([128, 16, 128]), mt_bc, ALU.mult, [lip, T1], [arg])
        _sincos(kb, arg[:], sn2[:], cs2[:], [arg, sn2, cs2, tmp2], tmp2[:])
        kb.TT(T2[:], T1[:], sn2[:], ALU.mult, [T1, sn2], [T2])
        kb.TS(T2[:], T2[:], -1.0, None, ALU.mult, None, [T2], [T2])
        kb.TT(T1[:], T1[:], cs2[:], ALU.mult, [T1, cs2], [T1])
        c4 = [P.sbuf("s_c%d" % i, [128, 16]) for i in range(4)]
        kb.ACT(c4[0][:], lrp[:], AF.Exp, [lrp], [c4[0]])
        _sincos(kb, lip[:], c4[1][:], c4[2][:], [lip, c4[1], c4[2], c4[3]], c4[3][:])
        kb.TT(AR[:], c4[0][:], c4[2][:], ALU.mult, [c4[0], c4[2]], [AR])
        kb.TT(NAI[:], c4[0][:], c4[1][:], ALU.mult, [c4[0], c4[1]], [NAI])
        kb.TS(NAI[:], NAI[:], -1.0, None, ALU.mult, None, [NAI], [NAI])


def mixer_s5_2(kb, l):
    P = kb.P
    prm = kb.prm
    with P.scope():
        WX = P.sbuf("s_WX", [128, 2, 8, 2, 64])
        Cblk = P.sbuf("s_Cblk", [128, 16, 128])
        kb.MS(WX[:], 0.0, [WX], eng="pool")
        kb.MS(Cblk[:], 0.0, [Cblk], eng="pool")
        for g8 in range(8):
            for ri, nm in enumerate(("s5_b_re", "s5_b_im")):
                for gg in range(2):
                    src = prm[nm][l][8 * gg + g8].rearrange("p c -> c p")
                    kb.LD(WX[16 * g8:16 * g8 + 16, gg, g8, ri, :], src, [WX], allow_slow_non_contiguous=True)
        for g in range(16):
            g8 = g % 8
            kb.LD(Cblk[0:64, g, 16 * g8:16 * g8 + 16], prm["s5_c_re"][l][g].rearrange("c p -> p c"), [Cblk],
                  allow_slow_non_contiguous=True)
            kb.LD(Cblk[64:128, g, 16 * g8:16 * g8 + 16], prm["s5_c_im"][l][g].rearrange("c p -> p c"), [Cblk],
                  allow_slow_non_contiguous=True)
        kb.TS(Cblk[64:128, :, :], Cblk[64:128, :, :], -1.0, None, ALU.mult, None, [Cblk], [Cblk])
        Cb16 = P.sbuf("s_Cb16", [128, 16, 128], BF16)
        kb.CP(Cb16[:], Cblk[:], [Cblk], [Cb16])

        tabs = []
        for d in range(2):
            VFr = P.sbuf("s_VFr%d" % d, [128, 16, 64]); VFi = P.sbuf("s_VFi%d" % d, [128, 16, 64])
            T1 = P.sbuf("s_T1%d" % d, [128, 16, 128]); T2 = P.sbuf("s_T2%d" % d, [128, 16, 128])
            AR = P.sbuf("s_AR%d" % d, [128, 16]); NAI = P.sbuf("s_NAI%d" % d, [128, 16])
            s5_tables(kb, l, d, VFr, VFi, T1, T2, AR, NAI)
            tabs.append((VFr, VFi, T1, T2, AR, NAI))
        OACC = P.sbuf("s_oacc", [128, 2, S])
        oacc_zero(kb, OACC)
        with P.scope():
            def make(d):
                pf = "s%d_" % d
                VFr, VFi, T1, T2, AR, NAI = tabs[d]
                uTb = [P.sbuf(pf + "u%d" % i, [128, 2, 128]) for i in range(2)]
                mm_ = [P.sbuf(pf + "m%d" % i, [128, 4, 64]) for i in range(4)]
                W3 = [P.sbuf(pf + "W3%d" % i, [128, 4, 3, 64], BF16) for i in range(2)]
                tP = P.sbuf(pf + "tP", [128, 4, 128]); tPs = P.sbuf(pf + "tPs", [128, 4, 128])
                H1 = P.sbuf(pf + "H1", [128, 4, 128]); H2 = P.sbuf(pf + "H2", [128, 4, 128])
                Hb = [P.sbuf(pf + "Hb%d" % i, [128, 4, 128], BF16) for i in range(2)]
                tri16 = P.sbuf(pf + "tri16", [128, 128], BF16)
                kb.CP(tri16[:], kb.C("TRIF" if d == 0 else "TRIB"), [], [tri16])
                hend = P.sbuf(pf + "hend", [128, 16]); hsend = P.sbuf(pf + "hsend", [128, 16])
                hp_ = P.sbuf(pf + "hp", [128, 16]); hps_ = P.sbuf(pf + "hps", [128, 16])
                sm = [P.sbuf(pf + "sm%d" % i, [128, 16]) for i in range(4)]
                kb.MS(hp_[:], 0.0, [hp_]); kb.MS(hps_[:], 0.0, [hps_])
                xps = P.psum(pf + "xps", [128, 512])
                pps = P.psum(pf + "pps", [128, 4, 128])
                ppss = P.psum(pf + "ppss", [128, 4, 128])
                yps = P.psum(pf + "yps", [128, 128])
                te = 127 if d == 0 else 0

                def gen():
                    it = 0
                    kq = 0
                    for n in ORDER[d]:
                        uT = uTb[it % 2]; it += 1
                        cols = slice(n * 128, (n + 1) * 128)
                        kb.LD(uT[:], kb.ZF[Z_SU:Z_SU + 256, cols].rearrange("(gg p) t -> p gg t", p=128), [uT])
                        yield
                        for q in range(4):
                            gg, qq = divmod(q, 2)
                            gs = slice(4 * q, 4 * q + 4)
                            w3 = W3[kq % 2]; hb = Hb[kq % 2]; kq += 1
                            kb.MM(xps[:], uT[:, gg, :], WX[:, gg, 4 * qq:4 * qq + 4, :, :].rearrange("q a r p -> q (a r p)"),
                                  True, True, [uT, WX], [xps])
                            xv = xps[:].rearrange("t (g r p) -> t g r p", r=2, p=64)
                            kb.TT(mm_[0][:], xv[:, :, 0, :], VFr[:, gs, :], ALU.mult, [xps, VFr], [mm_[0]])
                            kb.TT(mm_[1][:], xv[:, :, 1, :], VFi[:, gs, :], ALU.mult, [xps, VFi], [mm_[1]])
                            kb.TT(mm_[2][:], xv[:, :, 0, :], VFi[:, gs, :], ALU.mult, [xps, VFi], [mm_[2]])
                            kb.TT(mm_[3][:], xv[:, :, 1, :], VFr[:, gs, :], ALU.mult, [xps, VFr], [mm_[3]])
                            yield
                            kb.TT(w3[:, :, 0, :], mm_[0][:], mm_[1][:], ALU.subtract, [mm_[0], mm_[1]], [w3])
                            kb.TT(w3[:, :, 1, :], mm_[2][:], mm_[3][:], ALU.add, [mm_[2], mm_[3]], [w3], eng="pool")
                            kb.TT(w3[:, :, 2, :], mm_[1][:], mm_[0][:], ALU.subtract, [mm_[0], mm_[1]], [w3])
                            yield
                            for i in range(4):
                                kb.MM(pps[:, i, :], w3[:, i, 0:2, :].rearrange("q r p -> q (r p)"), tri16[:], True, True, [w3, tri16], [pps])
                            for i in range(4):
                                kb.MM(ppss[:, i, :], w3[:, i, 1:3, :].rearrange("q r p -> q (r p)"), tri16[:], True, True, [w3, tri16], [ppss])
                            yield
                            for i in range(4):
                                g = 4 * q + i
                                kb.ACT(tP[:, i, :], pps[:, i, :], AF.Identity, [pps, hp_], [tP], bias=hp_[:, g:g + 1])
                            for i in range(4):
                                g = 4 * q + i
                                kb.ACT(tPs[:, i, :], ppss[:, i, :], AF.Identity, [ppss, hps_], [tPs], bias=hps_[:, g:g + 1])
                            yield
                            kb.TT(sm[0][:, 0:4], tPs[:, :, te], T1[:, gs, te], ALU.mult, [tPs, T1], [sm[0]])
                            kb.TT(sm[1][:, 0:4], tP[:, :, te], T2[:, gs, te], ALU.mult, [tP, T2], [sm[1]])
                            kb.TT(hsend[:, gs], sm[0][:, 0:4], sm[1][:, 0:4], ALU.subtract, [sm[0], sm[1]], [hsend])
                            kb.TT(sm[2][:, 0:4], tP[:, :, te], T1[:, gs, te], ALU.mult, [tP, T1], [sm[2]])
                            kb.TT(sm[3][:, 0:4], tPs[:, :, te], T2[:, gs, te], ALU.mult, [tPs, T2], [sm[3]])
                            kb.TT(hend[:, gs], sm[2][:, 0:4], sm[3][:, 0:4], ALU.add, [sm[2], sm[3]], [hend])
                            yield
                            kb.TT(H1[:], tP[:], T1[:, gs, :], ALU.mult, [tP, T1], [H1], eng="pool")
                            kb.TT(H2[:], tPs[:], T2[:, gs, :], ALU.mult, [tPs, T2], [H2])
                            yield
                            kb.TT(hb[:], H1[:], H2[:], ALU.add, [H1, H2], [hb], eng="pool")
                            yield
                            for i in range(4):
                                g = 4 * q + i
                                kb.MM(yps[:], Cb16[:, g, :], hb[:, i, :], (g % 8) == 0, (g % 8) == 7, [Cb16, hb], [yps])
                            if qq == 1:
                                oacc_add(kb, OACC, gg, n, yps)
                            yield
                        kb.TT(sm[0][:], hend[:], AR[:], ALU.mult, [hend, AR], [sm[0]])
                        kb.TT(sm[1][:], hsend[:], NAI[:], ALU.mult, [hsend, NAI], [sm[1]])
                        kb.TT(sm[2][:], hsend[:], AR[:], ALU.mult, [hsend, AR], [sm[2]])
                        kb.TT(sm[3][:], hend[:], NAI[:], ALU.mult, [hend, NAI], [sm[3]])
                        kb.TT(hp_[:], sm[0][:], sm[1][:], ALU.add, [sm[0], sm[1]], [hp_])
                        kb.TT(hps_[:], sm[2][:], sm[3][:], ALU.subtract, [sm[2], sm[3]], [hps_])
                        yield
                return gen()
            run_interleaved([make(0), make(1)])
        with P.scope():
            dsk = P.sbuf("s_dsk", [128, 2]); glb = P.sbuf("s_glb", [128, 2])
            kb.LD(dsk[:], prm["s5_d"][l].rearrange("(gg p) -> p gg", p=128), [dsk], allow_slow_non_contiguous=True)
            kb.LD(glb[:], prm["s5_glu_b"][l].rearrange("(gg p) -> p gg", p=128), [glb], allow_slow_non_contiguous=True)
            gw = P.sbuf("s_gw", [128, 2, 256])
            kb.LD(gw[:], prm["s5_glu_w"][l].rearrange("(ct p) o -> p ct o", p=128), [gw])
            uTb = [P.sbuf("s_fu%d" % i, [128, 2, 128]) for i in range(2)]
            yy = [P.sbuf("s_yy%d" % i, [128, 2, 128]) for i in range(2)]
            x2 = [P.sbuf("s_x2%d" % i, [128, 2, 128]) for i in range(2)]
            th = [P.sbuf("s_th%d" % i, [128, 2, 128]) for i in range(2)]
            sgb = [P.sbuf("s_sg%d" % i, [128, 128]) for i in range(2)]
            ob = [P.sbuf("s_ob%d" % i, [128, 128]) for i in range(2)]
            psz = [P.psum("s_psz%d" % i, [128, 128]) for i in range(2)]
            k = 0
            for n in range(NT):
                cols = slice(n * 128, (n + 1) * 128)
                i = n % 2
                kb.LD(uTb[i][:], kb.ZF[Z_SU:Z_SU + 256, cols].rearrange("(gg p) t -> p gg t", p=128), [uTb[i]])
                for gg in range(2):
                    kb.STT(yy[i][:, gg, :], uTb[i][:, gg, :], dsk[:, gg:gg + 1], OACC[:, gg, cols], ALU.mult, ALU.add,
                           [uTb[i], dsk, OACC.s(n)], [yy[i]])
                kb.TT(x2[i][:], yy[i][:], yy[i][:], ALU.mult, [yy[i]], [x2[i]], eng="pool")
                kb.TS(x2[i][:], x2[i][:], 0.044715, 1.0, ALU.mult, ALU.add, [x2[i]], [x2[i]])
                kb.TT(x2[i][:], x2[i][:], yy[i][:], ALU.mult, [x2[i], yy[i]], [x2[i]], eng="pool")
                kb.ACT(th[i][:], x2[i][:], AF.Tanh, [x2[i]], [th[i]], scale=0.7978845608028654)
                kb.TS(th[i][:], th[i][:], 1.0, 0.5, ALU.add, ALU.mult, [th[i]], [th[i]])
                kb.TT(yy[i][:], yy[i][:], th[i][:], ALU.mult, [yy[i], th[i]], [yy[i]], eng="pool")
                for ot in range(2):
                    q = k % 2; k += 1
                    for ct in range(2):
                        kb.MM(psz[q][:], gw[:, ct, ot * 128:(ot + 1) * 128], yy[i][:, ct, :], ct == 0, ct == 1, [gw, yy[i]], [psz[q]])
                    kb.ACT(sgb[q][:], psz[q][:], AF.Sigmoid, [psz[q], glb], [sgb[q]], bias=glb[:, ot:ot + 1])
                    kb.TT(ob[q][:], yy[i][:, ot, :], sgb[q][:], ALU.mult, [yy[i], sgb[q]], [ob[q]])
                    kb.ST(kb.YC[768 + ot * 128:768 + (ot + 1) * 128, cols], ob[q][:], [ob[q]])


def _conv_win_masks():
    tp = np.arange(642) - 65
    w = np.mod(tp, 64)
    m = np.ones((2, 642), np.float32)
    m[0, w == 63] = 0.0
    m[1, w == 0] = 0.0
    return np.broadcast_to(m[None], (128, 2, 642)).copy()


def gdn_conv2(kb, l):
    P = kb.P
    with P.scope():
        CW = P.sbuf("g_cw", [128, 6, 9])
        for kh in range(3):
            for kw in range(3):
                kb.LD(CW[:, :, kh * 3 + kw], kb.prm["gdn_conv_w"][l][kh, kw].rearrange("(ct p) -> p ct", p=128), [CW],
                      allow_slow_non_contiguous=True)
        DW = P.sbuf("g_dw", [128, 6, 9, 128])
        for ct in range(6):
            for tp_ in range(9):
                kb.TS(DW[:, ct, tp_, :], kb.C("IDENT"), CW[:, ct, tp_:tp_ + 1], None, ALU.mult, None, [CW], [DW],
                      eng=("pool" if tp_ % 2 else "dve"))
        wm = P.sbuf("g_wm", [128, 2, 642])
        kb.LD(wm[:], kb.cwin[:], [wm])
        Wb = [P.sbuf("g_w%d" % i, [128, 642]) for i in range(2)]
        WLb = [P.sbuf("g_wl%d" % i, [128, 642]) for i in range(2)]
        WRb = [P.sbuf("g_wr%d" % i, [128, 642]) for i in range(2)]
        sl = [P.sbuf("g_sl%d" % i, [128, 512]) for i in range(2)]
        sq = [P.sbuf("g_sq%d" % i, [128, 512]) for i in range(2)]
        rt = [P.sbuf("g_rt%d" % i, [128, 512]) for i in range(2)]
        psc = [P.psum("g_psc%d" % i, [128, 512]) for i in range(2)]
        ps = [P.psum("g_psn%d" % i, [128, 512]) for i in range(2)]
        spans = [(0, 256, True)] + [(256 + 512 * k, 512, False) for k in range(8)]
        it = 0
        for (t0, L, is_ctx) in spans:
            lo = 0 if is_ctx else 256
            hi = 256 if is_ctx else S
            a = max(lo, t0 - 65); b = min(hi, t0 + L + 65)
            for ct in range(6):
                i = it % 2; it += 1
                W = Wb[i]
                full = (a == t0 - 65) and (b == t0 + L + 65) and L == 512
                if not full:
                    kb.MS(W[:], 0.0, [W], eng="pool")
                kb.LD(W[:, 65 + (a - t0):65 + (b - t0)], kb.ZF[Z_GQKV + ct * 128:Z_GQKV + (ct + 1) * 128, a:b], [W])
                if is_ctx:
                    WL = WR = W
                    rows = (1,)
                else:
                    WL, WR = WLb[i], WRb[i]
                    kb.TT(WL[:], W[:], wm[:, 0, :], ALU.mult, [W, wm], [WL])
                    kb.TT(WR[:], W[:], wm[:, 1, :], ALU.mult, [W, wm], [WR])
                    rows = (0, 1, 2)
                pc = psc[i]
                taps = [(dh, dwi) for dh in rows for dwi in range(3)]
                for q, (dh, dwi) in enumerate(taps):
                    srcT = (WL, W, WR)[dwi]
                    o0 = 65 + 64 * (dh - 1) + (dwi - 1)
                    kb.MM(pc[:, :L], DW[:, ct, dh * 3 + dwi, :], srcT[:, o0:o0 + L], q == 0, q == len(taps) - 1, [DW, srcT], [pc])
                kb.ACT(sl[i][:, :L], pc[:, :L], AF.Silu, [pc], [sl[i]])
                if ct < 4:
                    kb.TT(sq[i][:, :L], sl[i][:, :L], sl[i][:, :L], ALU.mult, [sl[i]], [sq[i]])
                    kb.MM(ps[i][:, :L], kb.C("BLK64"), sq[i][:, :L], True, True, [sq[i]], [ps[i]])
                    kb.ACT(rt[i][:, :L], ps[i][:, :L], AF.Sqrt, [ps[i]], [rt[i]], bias=kb.C("CCOL")[:, 0:1])
                    kb.RECIP(rt[i][:, :L], rt[i][:, :L], [rt[i]], [rt[i]])
                    if ct < 2:
                        kb.STT(sl[i][:, :L], sl[i][:, :L], 0.125, rt[i][:, :L], ALU.mult, ALU.mult, [sl[i], rt[i]], [sl[i]])
                    else:
                        kb.TT(sl[i][:, :L], sl[i][:, :L], rt[i][:, :L], ALU.mult, [sl[i], rt[i]], [sl[i]], eng="pool")
                kb.ST(kb.QKVF[ct * 128:(ct + 1) * 128, t0:t0 + L], sl[i][:, :L], [sl[i]])
```

```python
import numpy as np
import concourse.bass as bass
import concourse.mybir as mybir
from concourse.bass_utils import run_bass_kernel_spmd
from contextlib import ExitStack

F32 = mybir.dt.float32
BF16 = mybir.dt.bfloat16
AF = mybir.ActivationFunctionType
ALU = mybir.AluOpType

ENGS = ("pe", "act", "dve", "pool", "sp")
EPOCH = 16000
N_DMA_SEM = 32


class Buf:
    __slots__ = ("name", "w", "r", "excl", "pe_partial")

    def __init__(self, name="", excl=False):
        self.name = name
        self.w = None
        self.r = []
        self.excl = excl
        self.pe_partial = False


class T:
    def __init__(self, h, name, excl=False):
        self.h = h
        self.name = name
        self.b = Buf(name, excl)
        self.excl = excl
        self.subs = {}

    def __getitem__(self, k):
        return self.h[k]

    def s(self, key):
        if self.excl:
            return self.b
        if key not in self.subs:
            self.subs[key] = Buf("%s.%s" % (self.name, key))
        return self.subs[key]


class Prog:
    def __init__(self, nc):
        self.nc = nc
        self.es = ExitStack()
        self.stack = [self.es]
        self.ops = {e: [] for e in ENGS}
        self.cnt = {e: 0 for e in ENGS}
        self.seen = {e: {} for e in ENGS}
        self.last = {}
        self.dma_k = 0
        self.dma_use = [0] * N_DMA_SEM
        self.dma_sems = [self.es.enter_context(nc.semaphore("dq%d" % i)) for i in range(N_DMA_SEM)]
        self.eng_sems = {}
        self.out_tokens = []
        self.n_ops = 0
        self.uid = 0

    def _nm(self, name):
        self.uid += 1
        return "%s_%d" % (name, self.uid)

    def sbuf(self, name, shape, dt=F32):
        h = self.stack[-1].enter_context(self.nc.sbuf_tensor(self._nm(name), list(shape), dt))
        return T(h, name)

    def psum(self, name, shape, dt=F32):
        n = 1
        for d_ in shape[1:]:
            n *= d_
        nb = (n * 4 + 2047) // 2048
        h = self.stack[-1].enter_context(self.nc.psum_tensor(self._nm(name), [128, nb * 512], F32))
        v = h[0:shape[0], 0:n]
        if len(shape) == 3:
            v = v.rearrange("p (a b) -> p a b", a=shape[1])
        elif len(shape) == 4:
            v = v.rearrange("p (a b c) -> p a b c", a=shape[1], b=shape[2])
        return T(v, name, excl=True)

    def dram(self, name, shape, dt=F32, kind="Internal"):
        h = self.nc.dram_tensor(name, list(shape), dt, kind=kind)
        return T(h.ap(), name)

    class _Scope:
        def __init__(self, p):
            self.p = p

        def __enter__(self):
            st = ExitStack()
            self.p.stack.append(st)
            return st

        def __exit__(self, *a):
            self.p.barrier()
            st = self.p.stack.pop()
            st.close()
            return False

    def scope(self):
        return Prog._Scope(self)

    def _eng_sem(self, e, epoch):
        k = (e, epoch)
        if k not in self.eng_sems:
            self.eng_sems[k] = self.es.enter_context(self.nc.semaphore("s_%s_%d" % (e, epoch)))
        return self.eng_sems[k]

    def _waits(self, eng, reads, writes, extra=(), skip_pe=False):
        need = {}

        def add(tok):
            if tok is None:
                return
            key, val = tok
            if need.get(key, 0) < val:
                need[key] = val
        for b in reads:
            add(b.w)
        for b in writes:
            add(b.w)
            for t in b.r:
                add(t)
        for t in extra:
            add(t)
        out = []
        seen = self.seen[eng]
        for key, val in need.items():
            if skip_pe and key[0] == "e" and key[1] == "pe":
                continue
            if seen.get(key, 0) < val:
                seen[key] = val
                out.append((key, val))
        return out

    @staticmethod
    def _bufs(xs):
        out = []
        for x in xs:
            if x is None:
                continue
            out.append(x.b if isinstance(x, T) else x)
        return out

    def _commit(self, tok, reads, writes):
        self.last[tok[0]] = tok[1]
        for b in reads:
            b.r.append(tok)
            if len(b.r) > 64:
                mx = {}
                for k, v in b.r:
                    if mx.get(k, 0) < v:
                        mx[k] = v
                b.r = list(mx.items())
        for b in writes:
            b.w = tok
            b.r = []
        self.n_ops += 1

    def op(self, eng, fn, reads=(), writes=(), partial=False):
        reads = self._bufs(reads)
        writes = self._bufs(writes)
        ex = [b for b in reads if b.excl]
        if ex:
            reads = [b for b in reads if not b.excl]
            writes = writes + [b for b in ex if b not in writes]
        skip_pe = False
        if eng == "pe":
            skip_pe = (not partial) and all(not b.pe_partial for b in writes)
            for b in writes:
                b.pe_partial = partial
        waits = self._waits(eng, reads, writes, skip_pe=skip_pe)
        self.cnt[eng] += 1
        epoch, val = divmod(self.cnt[eng] - 1, EPOCH)
        tok = (("e", eng, epoch), val + 1)
        self.ops[eng].append((waits, fn, tok))
        self._commit(tok, reads, writes)
        return tok

    def dma(self, out_ap, in_ap, reads=(), writes=(), q="sp", is_output=False, **kw):
        reads = self._bufs(reads)
        writes = self._bufs(writes)
        i = self.dma_k % N_DMA_SEM
        self.dma_k += 1
        prev = self.dma_use[i]
        extra = [(("d", i), 16 * prev)] if prev else []
        waits = self._waits(q, reads, writes, extra)
        self.dma_use[i] = prev + 1
        tok = (("d", i), 16 * (prev + 1))

        def fn(e):
            return e.dma_start(out=out_ap, in_=in_ap, **kw)
        self.ops[q].append((waits, fn, tok))
        self._commit(tok, reads, writes)
        if is_output:
            self.out_tokens.append(tok)
        return tok

    def barrier(self):
        toks = list(self.last.items())
        for e in ENGS:
            waits = self._waits(e, [], [], toks)
            if waits:
                self.ops[e].append((waits, None, None))

    def _sem_of(self, key):
        if key[0] == "d":
            return self.dma_sems[key[1]]
        return self._eng_sem(key[1], key[2])

    def emit(self):
        nc = self.nc
        self.barrier()
        for e in ENGS:
            for waits, fn, tok in self.ops[e]:
                if tok is not None:
                    self._sem_of(tok[0])
                for key, val in waits:
                    self._sem_of(key)
        with nc.Block() as block:
            def run(e, handle):
                for waits, fn, tok in self.ops[e]:
                    for key, val in waits:
                        handle.wait_ge(self._sem_of(key), val)
                    if fn is None:
                        continue
                    ins = fn(handle)
                    key, val = tok
                    ins.then_inc(self._sem_of(key), 16 if key[0] == "d" else 1)

            @block.sync
            def _(h):
                run("sp", h)

            @block.tensor
            def _(h):
                run("pe", h)

            @block.scalar
            def _(h):
                run("act", h)

            @block.vector
            def _(h):
                run("dve", h)

            @block.gpsimd
            def _(h):
                run("pool", h)

    def close(self):
        self.es.close()


D = 1024
S = 4352
NT = 34
LAT0 = 256
DEPTH = 2
EPS = 1e-6
NEG = -30000.0
ORDER = [list(range(NT)), [1, 0] + list(range(NT - 1, 1, -1))]

C_HQ, C_HI, C_HG, C_HFF, C_HFB = 0, 256, 512, 768, 1024
C_RQ, C_RK, C_RV, C_RG = 1280, 1536, 1792, 2048
C_GQKV, C_GG, C_GA, C_GB, C_SU = 2304, 3072, 3328, 3336, 3344
Z_HQ, Z_HG, Z_HFF, Z_HFB, Z_RQ, Z_RK, Z_RG, Z_GQKV, Z_GG, Z_SU = 0, 256, 512, 768, 1024, 1280, 1536, 1792, 2560, 2816
NZF = 3072
FM_MAP = [(Z_HQ, C_HQ, 256), (Z_HG, C_HG, 256), (Z_HFF, C_HFF, 256), (Z_HFB, C_HFB, 256), (Z_RQ, C_RQ, 256),
          (Z_RK, C_RK, 256), (Z_RG, C_RG, 256), (Z_GQKV, C_GQKV, 768), (Z_GG, C_GG, 256), (Z_SU, C_SU, 256)]
FM_BLOCKS = [(zr + i, wc + i) for zr, wc, n in FM_MAP for i in range(0, n, 128)]
NZT = 528

CN = {}


def _const_pack():
    mats = []

    def add(name, m):
        CN[name] = len(mats)
        mats.append(np.asarray(m, np.float32))
    p = np.arange(128)[:, None]
    f = np.arange(128)[None, :]
    add("IDENT", (p == f))
    add("ONES", np.ones((128, 128)))
    add("TRIF", (p <= f))
    add("TRIB", (p >= f))
    add("SUFF", (p > f))
    add("PREB", (p < f))
    add("NLE", np.where(p <= f, 0.0, NEG))
    add("NLT", np.where(p < f, 0.0, NEG))
    add("NGE", np.where(p >= f, 0.0, NEG))
    add("NGT", np.where(p > f, 0.0, NEG))
    for s in (1, 2, 4, 8, 16, 32, 64):
        m = (((p // s) % 2) == 1) & ((f // s) == (p // s) - 1)
        add("MOFF%d" % s, m)
        add("MOFFT%d" % s, m.T)
    add("BLK64", (p // 64) == (f // 64))
    rot = np.zeros((128, 128))
    for m in range(128):
        if (m % 64) < 32:
            rot[m + 32, m] = -1.0
        else:
            rot[m - 32, m] = 1.0
    add("ROT", rot)
    add("IOTAF", np.broadcast_to(f, (128, 128)))
    add("IOTAF1", np.broadcast_to(f + 1, (128, 128)))
    add("RIOTAF", np.broadcast_to(128 - f, (128, 128)))
    add("R127F", np.broadcast_to(127 - f, (128, 128)))
    add("DIFF", f - p)
    add("NDIFF", p - f)
    for h in range(4):
        m = np.zeros((128, 128)); m[h, :] = 1.0
        add("SELH%d" % h, m)
    for hp in range(2):
        m = np.zeros((128, 128)); m[2 * hp, 0:64] = 1.0; m[2 * hp + 1, 64:128] = 1.0
        add("SELP%d" % hp, m)
    cc = np.zeros((128, 128))
    cc[:, 0] = EPS; cc[:, 1] = 1.0; cc[:, 2] = np.arange(128); cc[:, 3] = 127 - np.arange(128)
    cc[:, 5] = -np.pi; cc[:, 6] = -np.arange(128); cc[:, 7] = -(127 - np.arange(128))
    add("CCOL", cc)
    gm = np.zeros((128, 128))
    for g in range(16):
        gm[(g % 8) * 16:(g % 8) * 16 + 16, g] = 1.0
    add("GMASK", gm)
    return np.concatenate(mats, axis=1)


CONST_NP = _const_pack()
NCONST = CONST_NP.shape[1] // 128


def _rope_tables():
    half = 32
    inv = 10000.0 ** (-np.arange(half, dtype=np.float64) / half)
    pos = np.arange(S, dtype=np.float64)
    ang = pos[None, :] * inv[:, None]
    cos = np.cos(ang); sin = np.sin(ang)
    cos128 = np.tile(cos, (4, 1)); sin128 = np.tile(sin, (4, 1))
    return cos128.astype(np.float32), sin128.astype(np.float32)


def _conv_masks():
    m = np.ones((2, 512), np.float32)
    w = np.arange(512) % 64
    m[0, w == 0] = 0.0
    m[1, w == 63] = 0.0
    lat = np.broadcast_to(m[None], (128, 2, 512)).copy()
    c = np.ones((2, 256), np.float32)
    c[0, 0] = 0.0
    c[1, 255] = 0.0
    ctx = np.broadcast_to(c[None], (128, 2, 256)).copy()
    return lat, ctx


class KB:
    def __init__(self, cfg):
        self.cfg = cfg
        nc = bass.Bass("TRN2", target_bir_lowering=False)
        self.nc = nc
        self.P = Prog(nc)
        self.rr = 0

    def MM(self, ps, lhsT, rhs, st, sp, R, W):
        partial = lhsT.partition_size() < 128
        self.P.op("pe", lambda e: e.matmul(ps, lhsT, rhs, start=st, stop=sp), R, W, partial=partial)

    def TR(self, ps, in_, ident, R, W):
        self.P.op("pe", lambda e: e.transpose(ps, in_, ident), R, W)

    def ACT(self, out, in_, func, R, W, **kw):
        self.P.op("act", lambda e: e.activation(out=out, in_=in_, func=func, **kw), R, W)

    def TS(self, out, in0, s1, s2, op0, op1, R, W, eng="dve"):
        if s2 is None:
            self.P.op(eng, lambda e: e.tensor_scalar(out=out, in0=in0, scalar1=s1, scalar2=None, op0=op0), R, W)
        else:
            self.P.op(eng, lambda e: e.tensor_scalar(out=out, in0=in0, scalar1=s1, scalar2=s2, op0=op0, op1=op1), R, W)

    def TT(self, out, in0, in1, op, R, W, eng="dve"):
        self.P.op(eng, lambda e: e.tensor_tensor(out=out, in0=in0, in1=in1, op=op), R, W)

    def STT(self, out, in0, sc, in1, op0, op1, R, W, eng="dve"):
        eng = "dve"
        self.P.op(eng, lambda e: e.scalar_tensor_tensor(out=out, in0=in0, scalar=sc, in1=in1, op0=op0, op1=op1), R, W)

    def CP(self, out, in_, R, W, eng="dve"):
        if eng == "act":
            self.ACT(out, in_, AF.Copy, R, W)
        else:
            self.P.op(eng, lambda e: e.tensor_copy(out=out, in_=in_), R, W)

    def CPRED(self, out, mask, data, R, W):
        self.P.op("dve", lambda e: e.copy_predicated(out=out, mask=mask, data=data), R, W)

    def MS(self, ap, val, W, eng="dve"):
        self.P.op(eng, lambda e: e.memset(ap, val), (), W)

    def RECIP(self, out, in_, R, W):
        self.P.op("dve", lambda e: e.reciprocal(out=out, in_=in_), R, W)

    def SCAN(self, out, d0, d1, R, W):
        self.P.op("dve", lambda e: e.tensor_tensor_scan(out=out, data0=d0, data1=d1, initial=0.0,
                                                        op0=ALU.mult, op1=ALU.add), R, W)

    def LD(self, out, in_, W, R=(), q="sp", **kw):
        self.P.dma(out, in_, reads=R, writes=W, q=q, **kw)

    def ST(self, out, in_, R, W=(), q="pool", **kw):
        self.P.dma(out, in_, reads=R, writes=W, q=q, **kw)

    def evac_eng(self):
        self.rr += 1
        return "act" if self.rr % 2 else "dve"

    def C(self, name):
        i = CN[name]
        return self.const[:, i * 128:(i + 1) * 128]


PARAM_SHAPES = {
    "mod_w": [2, 1024, 6144], "mod_b": [2, 6144], "norm1_g": [2, 1024], "norm2_g": [2, 1024],
    "w_in": [2, 1024, 3600], "hgrn_lb_logits": [2, 2, 256], "hgrn_norm_g": [2, 64],
    "ret_decay_logit": [2, 2, 4], "gdn_conv_w": [2, 3, 3, 768], "gdn_a_log": [2, 2, 4],
    "gdn_dt_bias": [2, 2, 4], "gdn_norm_g": [2, 64], "s5_lam_re": [2, 2, 16, 64],
    "s5_lam_im": [2, 2, 16, 64], "s5_log_dt": [2, 2, 16], "s5_b_re": [2, 16, 64, 16],
    "s5_b_im": [2, 16, 64, 16], "s5_c_re": [2, 16, 16, 64], "s5_c_im": [2, 16, 16, 64],
    "s5_d": [2, 256], "s5_glu_w": [2, 256, 256], "s5_glu_b": [2, 256], "w_out": [2, 1024, 1024],
    "mlp_w1": [2, 1024, 4096], "mlp_w2": [2, 4096, 1024], "final_norm_g": [1024],
}


def declare(kb):
    P = kb.P
    cfg = kb.cfg
    kinds = cfg.get("kinds", {})
    kb.xin = P.dram("xin", [S, D], F32, kind="ExternalInput")
    kb.cvecT = P.dram("cvecT", [1024, 2], F32, kind="ExternalInput")
    kb.prm = {k: P.dram(k, shp, F32, kind="ExternalInput") for k, shp in PARAM_SHAPES.items()}
    kb.constd = P.dram("constp", [128, NCONST * 128], F32, kind="ExternalInput")
    kb.ropec = P.dram("ropec", [128, S], F32, kind="ExternalInput")
    kb.ropes = P.dram("ropes", [128, S], F32, kind="ExternalInput")
    kb.cmlat = P.dram("cmlat", [128, 2, 512], F32, kind="ExternalInput")
    kb.cmctx = P.dram("cmctx", [128, 2, 256], F32, kind="ExternalInput")
    kb.cwin = P.dram("cwin", [128, 2, 642], F32, kind="ExternalInput")
    kb.y = P.dram("y", [4096, D], F32, kind="ExternalOutput")
    kb.XS = P.dram("XS", [S, D], F32, kind=kinds.get("XS", "Internal"))
    kb.ZF = P.dram("ZF", [NZF, S], F32, kind=kinds.get("ZF", "Internal"))
    kb.ZT = P.dram("ZT", [S, NZT], F32, kind=kinds.get("ZT", "Internal"))
    kb.QKVF = P.dram("QKVF", [768, S], F32, kind=kinds.get("QKVF", "Internal"))
    kb.YC = P.dram("YC", [1024, S], F32, kind=kinds.get("YC", "Internal"))
    kb.H2T = P.dram("H2T", [1024, S], BF16, kind=kinds.get("H2T", "Internal"))
    kb.const = P.sbuf("const", [128, NCONST * 128])
    nchunk = 4
    w = NCONST * 128 // nchunk
    for i in range(nchunk):
        a, b = i * w, (i + 1) * w if i < nchunk - 1 else NCONST * 128
        kb.LD(kb.const[:, a:b], kb.constd[:, a:b], [kb.const.s(i)])
    kb.const_bufs = [kb.const.s(i) for i in range(nchunk)]
    kb.CB = kb.const_bufs
    kb.GS1 = P.sbuf("GS1", [128, 8, 2]); kb.SH1 = P.sbuf("SH1", [128, 8, 2])
    kb.GS2 = P.sbuf("GS2", [128, 8, 2]); kb.SH2 = P.sbuf("SH2", [128, 8, 2])
    kb.GATE1 = P.sbuf("GATE1", [128, 2, 1024]); kb.GATE2 = P.sbuf("GATE2", [128, 2, 1024])


def phase_mod(kb, l):
    P = kb.P
    prm = kb.prm
    with P.scope():
        cT = P.sbuf("cT", [128, 8, 2])
        kb.LD(cT[:], kb.cvecT[:].rearrange("(et e) c -> e et c", e=128), [cT])
        sc = P.sbuf("sc", [128, 8, 2])
        kb.ACT(sc[:], cT[:], AF.Silu, [cT], [sc])
        screp = P.sbuf("screp", [128, 8, 2, 128])
        kb.CP(screp[:], sc[:].unsqueeze(3).to_broadcast([128, 8, 2, 128]), [sc], [screp])
        mbf = P.sbuf("mbf", [128, 48])
        kb.LD(mbf[:], prm["mod_b"][l].rearrange("(j p) -> p j", p=128), [mbf], allow_slow_non_contiguous=True)
        ngf = P.sbuf("ngf", [128, 2, 8])
        kb.LD(ngf[:, 0, :], prm["norm1_g"][l].rearrange("(j p) -> p j", p=128), [ngf], allow_slow_non_contiguous=True)
        kb.LD(ngf[:, 1, :], prm["norm2_g"][l].rearrange("(j p) -> p j", p=128), [ngf], allow_slow_non_contiguous=True)
        mbrow = P.sbuf("mbrow", [128, 2, 1024])
        for gi, v in enumerate((2, 5)):
            kb.LD(mbrow[:, gi, :], prm["mod_b"][l][v * 1024:(v + 1) * 1024].partition_broadcast(128), [mbrow])
        wch = [P.sbuf("wch%d" % i, [128, 8, 1024]) for i in range(2)]
        ps_fm = P.psum("ps_fm", [128, 96])
        ps_g = [P.psum("ps_g%d" % i, [128, 512]) for i in range(2)]
        MF = P.sbuf("MF", [128, 48, 2])
        k = 0
        for v in range(6):
            wc = wch[v % 2]
            for et in range(8):
                kb.LD(wc[:, et, :], prm["mod_w"][l][et * 128:(et + 1) * 128, v * 1024:(v + 1) * 1024], [wc])
            for db in range(8):
                col = (v * 8 + db) * 2
                for et in range(8):
                    kb.MM(ps_fm[:, col:col + 2], wc[:, et, db * 128:(db + 1) * 128], sc[:, et, :],
                          et == 0, et == 7, [wc, sc], [ps_fm])
            if v in (2, 5):
                gt = kb.GATE1 if v == 2 else kb.GATE2
                gi = 0 if v == 2 else 1
                for which in range(2):
                    for half in range(2):
                        pg = ps_g[k % 2]; k += 1
                        for et in range(8):
                            kb.MM(pg[:], screp[:, et, which, :], wc[:, et, half * 512:(half + 1) * 512],
                                  et == 0, et == 7, [screp, wc], [pg])
                        kb.TT(gt[:, which, half * 512:(half + 1) * 512], pg[:], mbrow[:, gi, half * 512:(half + 1) * 512],
                              ALU.add, [pg, mbrow], [gt])
        kb.TT(MF[:], ps_fm[:].rearrange("p (j c) -> p j c", c=2), mbf[:].unsqueeze(2).to_broadcast([128, 48, 2]),
              ALU.add, [ps_fm, mbf], [MF])
        tmp = P.sbuf("mtmp", [128, 8, 2])
        kb.TS(tmp[:], MF[:, 8:16, :], 1.0, None, ALU.add, None, [MF], [tmp])
        kb.TT(kb.GS1[:], tmp[:], ngf[:, 0, :].unsqueeze(2).to_broadcast([128, 8, 2]), ALU.mult, [tmp, ngf], [kb.GS1])
        kb.CP(kb.SH1[:], MF[:, 0:8, :], [MF], [kb.SH1])
        tmp2 = P.sbuf("mtmp2", [128, 8, 2])
        kb.TS(tmp2[:], MF[:, 32:40, :], 1.0, None, ALU.add, None, [MF], [tmp2])
        kb.TT(kb.GS2[:], tmp2[:], ngf[:, 1, :].unsqueeze(2).to_broadcast([128, 8, 2]), ALU.mult, [tmp2, ngf], [kb.GS2])
        kb.CP(kb.SH2[:], MF[:, 24:32, :], [MF], [kb.SH2])


def norm_to_fm(kb, xt, hT, col0, GS, SH, which, bufs, R_x):
    P = kb.P
    junk, st, xn, ps_ts = bufs["junk"], bufs["st"], bufs["xn"], bufs["ps_t"]
    kb.MS(st[:, 0:1], 0.0, [st])
    kb.ACT(junk[:], xt[:], AF.Square, [xt], [junk, st], accum_out=st[:, 0:1])
    kb.ACT(st[:, 1:2], st[:, 0:1], AF.Sqrt, [st] + kb.CB, [st], scale=1.0 / D, bias=kb.C("CCOL")[:, 0:1])
    kb.RECIP(st[:, 2:3], st[:, 1:2], [st], [st])
    kb.ACT(xn[:], xt[:], AF.Copy, [xt, st], [xn], scale=st[:, 2:3])
    for half in range(2):
        ps_t = ps_ts[half]
        for q in range(4):
            dt = half * 4 + q
            kb.TR(ps_t[:, q * 128:(q + 1) * 128], xn[:, dt * 128:(dt + 1) * 128], kb.C("IDENT"), [xn] + kb.CB, [ps_t])
        for q in range(4):
            dt = half * 4 + q
            if q % 2 == 0:
                kb.TS(hT[:, dt, col0:col0 + 128], ps_t[:, q * 128:(q + 1) * 128], GS[:, dt, which:which + 1],
                      SH[:, dt, which:which + 1], ALU.mult, ALU.add, [ps_t, GS, SH], [hT])
            else:
                kb.ACT(hT[:, dt, col0:col0 + 128], ps_t[:, q * 128:(q + 1) * 128], AF.Identity, [ps_t, GS, SH], [hT],
                       scale=GS[:, dt, which:which + 1], bias=SH[:, dt, which:which + 1])


def norm_to_fm_g(kb, xt, hT, col0, GS, SH, which, bufs):
    junk, st, xn, ps_ts = bufs["junk"], bufs["st"], bufs["xn"], bufs["ps_t"]
    kb.MS(st[:, 0:1], 0.0, [st])
    kb.ACT(junk[:], xt[:], AF.Square, [xt], [junk, st], accum_out=st[:, 0:1])
    yield
    kb.ACT(st[:, 1:2], st[:, 0:1], AF.Sqrt, [st], [st], scale=1.0 / D, bias=kb.C("CCOL")[:, 0:1])
    kb.RECIP(st[:, 2:3], st[:, 1:2], [st], [st])
    yield
    kb.ACT(xn[:], xt[:], AF.Copy, [xt, st], [xn], scale=st[:, 2:3])
    yield
    for half in range(2):
        ps_t = ps_ts[half]
        for q in range(4):
            dt = half * 4 + q
            kb.TR(ps_t[:, q * 128:(q + 1) * 128], xn[:, dt * 128:(dt + 1) * 128], kb.C("IDENT"), [xn], [ps_t])
        yield
        for q in range(4):
            dt = half * 4 + q
            if q % 2 == 0:
                kb.TS(hT[:, dt, col0:col0 + 128], ps_t[:, q * 128:(q + 1) * 128], GS[:, dt, which:which + 1],
                      SH[:, dt, which:which + 1], ALU.mult, ALU.add, [ps_t, GS, SH], [hT])
            else:
                kb.ACT(hT[:, dt, col0:col0 + 128], ps_t[:, q * 128:(q + 1) * 128], AF.Identity, [ps_t, GS, SH], [hT],
                       scale=GS[:, dt, which:which + 1], bias=SH[:, dt, which:which + 1])
        yield


def phase_a(kb, l, src):
    P = kb.P
    with P.scope():
        win = P.sbuf("win", [128, 8, 3600], BF16)
        for kt in range(8):
            kb.LD(win[:, kt, :], kb.prm["w_in"][l][kt * 128:(kt + 1) * 128, :], [win.s(kt)], q="pool")
        winb = [win.s(kt) for kt in range(8)]
        xbuf = [P.sbuf("xa%d" % i, [128, 1024]) for i in range(2)]
        hTb = [P.sbuf("hTa%d" % i, [128, 8, 512], BF16) for i in range(2)]
        nb = {"junk": P.sbuf("junk", [128, 1024]), "st": P.sbuf("st", [128, 4]), "xn": P.sbuf("xn", [128, 1024]),
              "ps_t": [P.psum("ps_t%d" % i, [128, 512]) for i in range(2)]}
        ps_f = [P.psum("ps_f%d" % i, [128, 512]) for i in range(3)]
        ps_a = [P.psum("ps_a%d" % i, [128, 512]) for i in range(2)]
        ps_b = P.psum("ps_b", [128, 16])
        stg = [P.sbuf("stg%d" % i, [128, 512]) for i in range(4)]
        stt = [P.sbuf("stt%d" % i, [128, NZT]) for i in range(2)]
        kx = kf = ks = ka = 0
        for gi, t0 in enumerate(range(0, S, 512)):
            n = min(512, S - t0)
            hT = hTb[gi % 2]
            for ti in range(n // 128):
                tt = t0 // 128 + ti
                which = 1 if tt < 2 else 0
                xt = xbuf[kx % 2]; kx += 1
                kb.LD(xt[:], src[tt * 128:(tt + 1) * 128, :], [xt])
                norm_to_fm(kb, xt, hT, ti * 128, kb.GS1, kb.SH1, which, nb, None)
            for (zr, wc) in FM_BLOCKS:
                ps = ps_f[kf % 3]; kf += 1
                for kt in range(8):
                    kb.MM(ps[:, :n], win[:, kt, wc:wc + 128], hT[:, kt, :n], kt == 0, kt == 7, [winb[kt], hT], [ps])
                sg = stg[ks % 4]; ks += 1
                kb.CP(sg[:, :n], ps[:, :n], [ps], [sg], eng=kb.evac_eng())
                kb.ST(kb.ZF[zr:zr + 128, t0:t0 + n], sg[:, :n], [sg])
            for ti in range(n // 128):
                tt = t0 // 128 + ti
                pa = ps_a[ka % 2]
                so = stt[ka % 2]; ka += 1
                for (c0, w0, wn) in ((0, C_HI, 256), (256, C_RV, 256)):
                    for kt in range(8):
                        kb.MM(pa[:, c0:c0 + wn], hT[:, kt, ti * 128:(ti + 1) * 128], win[:, kt, w0:w0 + wn],
                              kt == 0, kt == 7, [winb[kt], hT], [pa])
                for kt in range(8):
                    kb.MM(ps_b[:], hT[:, kt, ti * 128:(ti + 1) * 128], win[:, kt, C_GA:C_GA + 16],
                          kt == 0, kt == 7, [winb[kt], hT], [ps_b])
                kb.CP(so[:, 0:512], pa[:], [pa], [so], eng="act")
                kb.CP(so[:, 512:528], ps_b[:], [ps_b], [so], eng="dve")
                kb.ST(kb.ZT[tt * 128:(tt + 1) * 128, :], so[:], [so])


def build(cfg):
    kb = KB(cfg)
    P = kb.P
    declare(kb)
    P.barrier()
    stages = cfg.get("stages", "all")
    for l in cfg.get("layers", range(DEPTH)):
        src = kb.xin if l == 0 else kb.XS
        if stages == "all" or "M" in stages:
            phase_mod(kb, l)
        if stages == "all" or "A" in stages:
            phase_a(kb, l, src)
        if stages == "all" or "R" in stages:
            (mixer_ret if cfg.get("ret_old") else mixer_ret2)(kb, l)
        if stages == "all" or "H" in stages:
            (mixer_hgrn if cfg.get("hgrn_old") else mixer_hgrn2)(kb, l)
        if stages == "all" or "G" in stages:
            (mixer_gdn if cfg.get("gdn_old") else mixer_gdn2)(kb, l)
        if stages == "all" or "S" in stages:
            (mixer_s5 if cfg.get("s5_old") else mixer_s5_2)(kb, l)
        if stages == "all" or "C" in stages:
            phase_c(kb, l, src)
    P.emit()
    P.close()
    return kb


_CONSTS = None


def host_inputs(inputs, cores=range(8)):
    global _CONSTS
    if _CONSTS is None:
        rc, rs = _rope_tables()
        cl, cc = _conv_masks()
        _CONSTS = {"constp": CONST_NP, "ropec": rc, "ropes": rs, "cmlat": cl, "cmctx": cc, "cwin": _conv_win_masks()}
    maps = []
    for b in cores:
        m = {"xin": np.ascontiguousarray(np.concatenate([inputs["ctx"][b], inputs["x"][b]], axis=0), dtype=np.float32),
             "cvecT": np.ascontiguousarray(np.stack([inputs["c"][b], inputs["c_ctx"]], axis=1), dtype=np.float32)}
        for k in PARAM_SHAPES:
            m[k] = np.ascontiguousarray(inputs[k], dtype=np.float32)
        m.update(_CONSTS)
        maps.append(m)
    return maps


def kernel(**inputs):
    inputs = {k: np.asarray(v) for k, v in inputs.items()}
    kb = build({})
    maps = host_inputs(inputs)
    res = run_bass_kernel_spmd(kb.nc, maps, core_ids=list(range(8)))
    out = np.stack([np.asarray(r["y"]).reshape(4096, D) for r in res.results], axis=0)
    return out.astype(np.float32)


def phase_c(kb, l, src):
    P = kb.P
    last = (l == DEPTH - 1)
    t_start = 2 if last else 0
    with P.scope():
        wout = P.sbuf("wout", [128, 8, 1024], BF16)
        for ft in range(8):
            kb.LD(wout[:, ft, :], kb.prm["w_out"][l][ft * 128:(ft + 1) * 128, :], [wout.s(ft)], q="pool")
        wb = [wout.s(ft) for ft in range(8)]

        def make(si):
            pf = "c%d_" % si
            yc = P.sbuf(pf + "yc", [128, 8, 128], BF16)
            xt = P.sbuf(pf + "x", [128, 1024]); x1 = P.sbuf(pf + "x1", [128, 1024])
            tmpb = [P.sbuf(pf + "t%d" % i, [128, 512]) for i in range(2)]
            h2 = P.sbuf(pf + "h2", [128, 8, 128], BF16)
            nb = {"junk": P.sbuf(pf + "junk", [128, 1024]), "st": P.sbuf(pf + "st", [128, 4]), "xn": P.sbuf(pf + "xn", [128, 1024]),
                  "ps_t": [P.psum(pf + "pst%d" % i, [128, 512]) for i in range(2)]}
            ps_y = [P.psum(pf + "psy%d" % i, [128, 512]) for i in range(2)]

            def gen():
                for tt in range(t_start + si, NT, 2):
                    which = 1 if tt < 2 else 0
                    cols = slice(tt * 128, (tt + 1) * 128)
                    kb.LD(yc[:], kb.YC[:, cols].rearrange("(ft p) t -> p ft t", p=128), [yc], q="pool")
                    kb.LD(xt[:], src[cols, :], [xt])
                    yield
                    for half in range(2):
                        ps = ps_y[half]
                        for ft in range(8):
                            kb.MM(ps[:], yc[:, ft, :], wout[:, ft, half * 512:(half + 1) * 512], ft == 0, ft == 7,
                                  [yc, wb[ft]], [ps])
                        yield
                    for half in range(2):
                        ps = ps_y[half]
                        tm = tmpb[half]
                        kb.TT(tm[:], ps[:], kb.GATE1[:, which, half * 512:(half + 1) * 512], ALU.mult, [ps, kb.GATE1], [tm])
                        kb.TT(x1[:, half * 512:(half + 1) * 512], xt[:, half * 512:(half + 1) * 512], tm[:], ALU.add,
                              [xt, tm], [x1], eng="pool")
                        yield
                    kb.ST(kb.XS[cols, :], x1[:], [x1])
                    yield from norm_to_fm_g(kb, x1, h2, 0, kb.GS2, kb.SH2, which, nb)
                    kb.ST(kb.H2T[:, cols].rearrange("(dt p) t -> p dt t", p=128), h2[:], [h2])
                    yield
            return gen()
        run_interleaved([make(0), make(1)])
    with P.scope():
        w1 = P.sbuf("w1", [128, 8, 4096], BF16)
        w2 = P.sbuf("w2", [128, 32, 1024], BF16)
        for kt in range(8):
            kb.LD(w1[:, kt, :], kb.prm["mlp_w1"][l][kt * 128:(kt + 1) * 128, :], [w1.s(kt)], q="pool")
        for fb in range(32):
            kb.LD(w2[:, fb, :], kb.prm["mlp_w2"][l][fb * 128:(fb + 1) * 128, :], [w2.s(fb)], q="pool")
        h2b = [P.sbuf("h2d%d" % i, [128, 8, 256], BF16) for i in range(2)]
        uTb = [P.sbuf("uT%d" % i, [128, 16, 256], BF16) for i in range(1)]
        rb = [P.sbuf("relu%d" % i, [128, 256]) for i in range(3)]
        xb = [P.sbuf("xd%d" % i, [128, 1024]) for i in range(2)]
        tmpb = [P.sbuf("td%d" % i, [128, 512]) for i in range(2)]
        ps_u = [P.psum("ps_u%d" % i, [128, 256]) for i in range(3)]
        ps_y = [P.psum("ps_y2%d" % i, [128, 512]) for i in range(4)]
        if last:
            fg = P.sbuf("fg", [128, 1024])
            kb.LD(fg[:], kb.prm["final_norm_g"][:].partition_broadcast(128), [fg])
            stf = P.sbuf("stf", [128, 4])
            xnf = P.sbuf("xnf", [128, 1024])
        k = 0; ku = 0
        for g0 in range(t_start, NT, 2):
            h2 = h2b[k % 2]; uT = uTb[0]
            cols = slice(g0 * 128, (g0 + 2) * 128)
            kb.LD(h2[:], kb.H2T[:, cols].rearrange("(dt p) t -> p dt t", p=128), [h2])
            for hh in range(2):
                for fl in range(16):
                    fb = hh * 16 + fl
                    ps = ps_u[ku % 3]; r = rb[ku % 3]; ku += 1
                    for kt in range(8):
                        kb.MM(ps[:], w1[:, kt, fb * 128:(fb + 1) * 128], h2[:, kt, :], kt == 0, kt == 7, [w1.s(kt), h2], [ps])
                    kb.ACT(r[:], ps[:], AF.Relu, [ps], [r])
                    kb.TT(uT[:, fl, :], r[:], r[:], ALU.mult, [r], [uT.s(fl)], eng=("dve" if fb % 2 else "pool"))
                for ti in range(2):
                    for half in range(2):
                        ps = ps_y[2 * ti + half]
                        for fl in range(16):
                            fb = hh * 16 + fl
                            kb.MM(ps[:], uT[:, fl, ti * 128:(ti + 1) * 128], w2[:, fb, half * 512:(half + 1) * 512],
                                  fb == 0, fb == 31, [uT.s(fl), w2.s(fb)], [ps])
            for ti in range(2):
                tt = g0 + ti
                which = 1 if tt < 2 else 0
                xt = xb[ti]
                rows = slice(tt * 128, (tt + 1) * 128)
                kb.LD(xt[:], kb.XS[rows, :], [xt])
                for half in range(2):
                    ps = ps_y[2 * ti + half]
                    tm = tmpb[half]
                    kb.TT(tm[:], ps[:], kb.GATE2[:, which, half * 512:(half + 1) * 512], ALU.mult, [ps, kb.GATE2], [tm])
                    kb.TT(xt[:, half * 512:(half + 1) * 512], xt[:, half * 512:(half + 1) * 512], tm[:], ALU.add,
                          [xt, tm], [xt], eng="pool")
                if not last:
                    kb.ST(kb.XS[rows, :], xt[:], [xt])
                else:
                    kb.MS(stf[:, 0:1], 0.0, [stf])
                    kb.ACT(xnf[:], xt[:], AF.Square, [xt], [xnf, stf], accum_out=stf[:, 0:1])
                    kb.ACT(stf[:, 1:2], stf[:, 0:1], AF.Sqrt, [stf], [stf], scale=1.0 / D, bias=kb.C("CCOL")[:, 0:1])
                    kb.RECIP(stf[:, 2:3], stf[:, 1:2], [stf], [stf])
                    kb.ACT(xnf[:], xt[:], AF.Copy, [xt, stf], [xnf], scale=stf[:, 2:3])
                    kb.TT(xnf[:], xnf[:], fg[:], ALU.mult, [xnf, fg], [xnf])
                    kb.P.dma(kb.y[(tt - 2) * 128:(tt - 1) * 128, :], xnf[:], reads=[xnf.b], q="pool", is_output=True)
            k += 1


def finalize_gated(kb, OACC, gate_row0, gain, yc_row0, pfx):
    P = kb.P
    def two(nm):
        return [P.sbuf(pfx + nm + "%d" % i, [128, 2, 128]) for i in range(2)]
    gb, sq, rt, eg, ob = two("fg"), two("fsq"), two("frt"), two("feg"), two("fo")
    ps_m = [P.psum(pfx + "fps%d" % i, [128, 2, 128]) for i in range(2)]
    for n in range(NT):
        cols = slice(n * 128, (n + 1) * 128)
        i = n % 2
        g = gb[i]
        kb.LD(g[:], kb.ZF[gate_row0:gate_row0 + 256, cols].rearrange("(hp p) t -> p hp t", p=128), [g])
        o = OACC[:, :, cols]
        kb.TT(sq[i][:], o, o, ALU.mult, [OACC.s(n)], [sq[i]])
        kb.MM(ps_m[i][:].rearrange("p a b -> p (a b)"), kb.C("BLK64"), sq[i][:].rearrange("p a b -> p (a b)"), True, True,
              [sq[i]], [ps_m[i]])
        kb.ACT(rt[i][:], ps_m[i][:], AF.Ln, [ps_m[i]], [rt[i]], scale=1.0 / 64, bias=kb.C("CCOL")[:, 0:1])
        kb.ACT(rt[i][:], rt[i][:], AF.Exp, [rt[i]], [rt[i]], scale=-0.5)
        kb.ACT(eg[i][:], g[:], AF.Exp, [g], [eg[i]], scale=-1.0)
        kb.TS(eg[i][:], eg[i][:], 1.0, None, ALU.add, None, [eg[i]], [eg[i]])
        kb.RECIP(eg[i][:], eg[i][:], [eg[i]], [eg[i]])
        kb.TT(eg[i][:], eg[i][:], g[:], ALU.mult, [eg[i], g], [eg[i]], eng="pool")
        kb.TT(ob[i][:], o, rt[i][:], ALU.mult, [OACC.s(n), rt[i]], [ob[i]])
        if gain is not None:
            kb.STT(ob[i][:], ob[i][:], gain[:, 0:1], eg[i][:], ALU.mult, ALU.mult, [ob[i], gain, eg[i]], [ob[i]])
        else:
            kb.TT(ob[i][:], ob[i][:], eg[i][:], ALU.mult, [ob[i], eg[i]], [ob[i]])
        kb.ST(kb.YC[yc_row0:yc_row0 + 256, cols].rearrange("(hp p) t -> p hp t", p=128), ob[i][:], [ob[i]])


def oacc_write(kb, OACC, hp, n, ps, d):
    cols = slice(n * 128, (n + 1) * 128)
    if d == 0:
        kb.CP(OACC[:, hp, cols], ps[:], [ps], [OACC.s(n)], eng="act")
    else:
        kb.TT(OACC[:, hp, cols], OACC[:, hp, cols], ps[:], ALU.add, [ps], [OACC.s(n)])


def mixer_ret(kb, l):
    P = kb.P
    with P.scope():
        OACC = P.sbuf("r_oacc", [128, 2, S])
        with P.scope():
            lgt = P.sbuf("r_lgt", [128, 8])
            kb.LD(lgt[:], kb.prm["ret_decay_logit"][l].rearrange("d h -> (d h)").partition_broadcast(128), [lgt])
            LG = P.sbuf("r_LG", [128, 8])
            kb.ACT(LG[:], lgt[:], AF.Sigmoid, [lgt], [LG])
            kb.ACT(LG[:], LG[:], AF.Ln, [LG], [LG])
            LGP = P.sbuf("r_LGP", [128, 4])
            for d in range(2):
                for hp in range(2):
                    c = 2 * d + hp
                    kb.CP(LGP[0:64, c:c + 1], LG[0:64, 4 * d + 2 * hp:4 * d + 2 * hp + 1], [LG], [LGP])
                    kb.CP(LGP[64:128, c:c + 1], LG[64:128, 4 * d + 2 * hp + 1:4 * d + 2 * hp + 2], [LG], [LGP])
            MK = [P.sbuf("r_MK%d" % d, [128, 4, 128]) for d in range(2)]
            QDEC = [[P.sbuf("r_QD%d%d" % (d, hp), [128, 128]) for hp in range(2)] for d in range(2)]
            etmp = P.sbuf("r_etmp", [128, 128])
            for d in range(2):
                for h in range(4):
                    kb.ACT(etmp[:], kb.C("DIFF" if d == 0 else "NDIFF"), AF.Exp, [LG], [etmp],
                           scale=LG[:, 4 * d + h:4 * d + h + 1])
                    kb.STT(MK[d][:, h, :], etmp[:], 0.125, kb.C("TRIF" if d == 0 else "TRIB"), ALU.mult, ALU.mult,
                           [etmp], [MK[d]])
                for hp in range(2):
                    kb.ACT(QDEC[d][hp][:], kb.C("IOTAF1" if d == 0 else "RIOTAF"), AF.Exp, [LGP], [QDEC[d][hp]],
                           scale=LGP[:, 2 * d + hp:2 * d + hp + 1])
            KD = P.sbuf("r_KD", [128, 8])
            kb.ACT(KD[:, 0:4], LG[:, 0:4], AF.Exp, [LG], [KD], scale=kb.C("CCOL")[:, 3:4])
            kb.ACT(KD[:, 4:8], LG[:, 4:8], AF.Exp, [LG], [KD], scale=kb.C("CCOL")[:, 2:3])
            kb.TS(KD[:], KD[:], 0.125, None, ALU.mult, None, [KD], [KD])
            CV = P.sbuf("r_CV", [128, 4])
            kb.ACT(CV[:], LGP[:], AF.Exp, [LGP], [CV], scale=128.0)
            qTb = [P.sbuf("r_q%d" % i, [128, 2, 128]) for i in range(2)]
            kTb = [P.sbuf("r_k%d" % i, [128, 2, 128]) for i in range(2)]
            csb = [P.sbuf("r_cs%d" % i, [128, 2, 128]) for i in range(2)]
            Vp = [[P.sbuf("r_vp%d%d" % (i, h), [128, 128]) for h in range(4)] for i in range(2)]
            khp = [[P.sbuf("r_kh%d%d" % (i, h), [128, 128]) for h in range(4)] for i in range(2)]
            for i in range(2):
                for h in range(4):
                    kb.MS(Vp[i][h][:], 0.0, [Vp[i][h]], eng="pool")
                    kb.MS(khp[i][h][:], 0.0, [khp[i][h]], eng="pool")
            t1 = [P.sbuf("r_t1%d" % i, [128, 128]) for i in range(2)]
            t2 = [P.sbuf("r_t2%d" % i, [128, 128]) for i in range(2)]
            qr = [P.sbuf("r_qr%d" % i, [128, 2, 128]) for i in range(2)]
            kr = [P.sbuf("r_kr%d" % i, [128, 2, 128]) for i in range(2)]
            AT = [P.sbuf("r_AT%d" % i, [128, 2, 128]) for i in range(2)]
            qd = [P.sbuf("r_qd%d" % i, [128, 128]) for i in range(2)]
            Sb = [P.sbuf("r_S%d" % hp, [128, 128]) for hp in range(2)]
            ps_r = [P.psum("r_psr%d" % i, [128, 256]) for i in range(2)]
            ps_s = [P.psum("r_pss%d" % i, [128, 2, 128]) for i in range(2)]
            ps_o = [P.psum("r_pso%d" % i, [128, 128]) for i in range(2)]
            ps_k = P.psum("r_psk", [128, 128])
            ps_kv = P.psum("r_pskv", [128, 128])
            it = 0
            for d in range(2):
                for hp in range(2):
                    kb.MS(Sb[hp][:], 0.0, [Sb[hp]])
                for n in ORDER[d]:
                    cols = slice(n * 128, (n + 1) * 128)
                    b = it % 2; it += 1
                    qT, kT, cs = qTb[b], kTb[b], csb[b]
                    kb.LD(qT[:], kb.ZF[Z_RQ:Z_RQ + 256, cols].rearrange("(hp p) t -> p hp t", p=128), [qT])
                    kb.LD(kT[:], kb.ZF[Z_RK:Z_RK + 256, cols].rearrange("(hp p) t -> p hp t", p=128), [kT])
                    kb.LD(cs[:, 0, :], kb.ropec[:, cols], [cs])
                    kb.LD(cs[:, 1, :], kb.ropes[:, cols], [cs])
                    for h in range(4):
                        kb.LD(Vp[b][h][:, 64 * (h % 2):64 * (h % 2) + 64], kb.ZT[cols, 256 + 64 * h:256 + 64 * h + 64],
                              [Vp[b][h]])
                    for hp in range(2):
                        j = (it * 2 + hp) % 2
                        pr = ps_r[j]
                        kb.MM(pr[:, 0:128], kb.C("ROT"), qT[:, hp, :], True, True, [qT], [pr])
                        kb.MM(pr[:, 128:256], kb.C("ROT"), kT[:, hp, :], True, True, [kT], [pr])
                        for (src_, dst, off) in ((qT, qr[b], 0), (kT, kr[b], 128)):
                            kb.TT(t1[j][:], src_[:, hp, :], cs[:, 0, :], ALU.mult, [src_, cs], [t1[j]], eng="pool")
                            kb.TT(t2[j][:], pr[:, off:off + 128], cs[:, 1, :], ALU.mult, [pr, cs], [t2[j]])
                            kb.TT(dst[:, hp, :], t1[j][:], t2[j][:], ALU.add, [t1[j], t2[j]], [dst.s(hp)], eng="pool")
                        pss = ps_s[j]
                        for h2 in range(2):
                            kb.MM(pss[:, h2, :], kr[b][64 * h2:64 * h2 + 64, hp, :], qr[b][64 * h2:64 * h2 + 64, hp, :],
                                  True, True, [kr[b].s(hp), qr[b].s(hp)], [pss])
                        kb.TT(AT[j][:], pss[:], MK[d][:, 2 * hp:2 * hp + 2, :], ALU.mult, [pss, MK[d]], [AT[j]])
                        kb.TT(qd[j][:], qr[b][:, hp, :], QDEC[d][hp][:], ALU.mult, [qr[b].s(hp), QDEC[d][hp]], [qd[j]],
                              eng="pool")
                        po = ps_o[j]
                        kb.MM(po[:], Vp[b][2 * hp][:], AT[j][:, 0, :], True, False, [Vp[b][2 * hp], AT[j]], [po])
                        kb.MM(po[:], Vp[b][2 * hp + 1][:], AT[j][:, 1, :], False, False, [Vp[b][2 * hp + 1], AT[j]], [po])
                        kb.MM(po[:], Sb[hp][:], qd[j][:], False, True, [Sb[hp], qd[j]], [po])
                        oacc_write(kb, OACC, hp, n, po, d)
                        kb.TR(ps_k[:], kr[b][:, hp, :], kb.C("IDENT"), [kr[b].s(hp)], [ps_k])
                        for h2 in range(2):
                            h = 2 * hp + h2
                            kb.ACT(khp[b][h][:, 64 * h2:64 * h2 + 64], ps_k[:, 64 * h2:64 * h2 + 64], AF.Copy,
                                   [ps_k, KD], [khp[b][h]], scale=KD[:, 4 * d + h:4 * d + h + 1])
                        kb.MM(ps_kv[:], khp[b][2 * hp][:], Vp[b][2 * hp][:], True, False,
                              [khp[b][2 * hp], Vp[b][2 * hp]], [ps_kv])
                        kb.MM(ps_kv[:], khp[b][2 * hp + 1][:], Vp[b][2 * hp + 1][:], False, True,
                              [khp[b][2 * hp + 1], Vp[b][2 * hp + 1]], [ps_kv])
                        kb.STT(Sb[hp][:], Sb[hp][:], CV[:, 2 * d + hp:2 * d + hp + 1], ps_kv[:], ALU.mult, ALU.add,
                               [Sb[hp], CV, ps_kv], [Sb[hp]])
        with P.scope():
            finalize_gated(kb, OACC, Z_RG, None, 256, "r_")


def mixer_hgrn(kb, l):
    P = kb.P
    with P.scope():
        OACC = P.sbuf("h_oacc", [128, 2, S])
        with P.scope():
            LB = P.sbuf("h_LB", [128, 4]); OML = P.sbuf("h_OML", [128, 4])
            if l == 0:
                kb.MS(LB[:], 0.0, [LB]); kb.MS(OML[:], 1.0, [OML])
            else:
                lgt = P.sbuf("h_lgt", [128, 8])
                kb.LD(lgt[:], kb.prm["hgrn_lb_logits"][:].rearrange("l d (hp p) -> p (l d hp)", p=128), [lgt],
                      allow_slow_non_contiguous=True)
                kb.TT(LB[:], lgt[:, 4:8], lgt[:, 0:4], ALU.subtract, [lgt], [LB])
                kb.ACT(LB[:], LB[:], AF.Sigmoid, [LB], [LB])
                kb.TS(OML[:], LB[:], -1.0, 1.0, ALU.mult, ALU.add, [LB], [OML])
            G = P.sbuf("h_G", [128, 1])
            for hh in range(2):
                kb.LD(G[64 * hh:64 * hh + 64, :], kb.prm["hgrn_norm_g"][l].rearrange("(p o) -> p o", o=1), [G])
            kb.hgrn_gain = G
            hqb = [P.sbuf("h_q%d" % i, [128, 2, 128]) for i in range(2)]
            hfb = [P.sbuf("h_f%d" % i, [128, 2, 128]) for i in range(2)]
            Vp = [[P.sbuf("h_vp%d%d" % (i, h), [128, 128]) for h in range(4)] for i in range(2)]
            khp = [[P.sbuf("h_kh%d%d" % (i, h), [128, 128]) for h in range(4)] for i in range(2)]
            for i in range(2):
                for h in range(4):
                    kb.MS(Vp[i][h][:], 0.0, [Vp[i][h]], eng="pool")
                    kb.MS(khp[i][h][:], 0.0, [khp[i][h]], eng="pool")
            MREF = [[P.sbuf("h_mr%d%d" % (d, i), [128, 4]) for i in range(2)] for d in range(2)]
            for d in range(2):
                for i in range(2):
                    kb.MS(MREF[d][i][:], 0.0, [MREF[d][i]])

            def two(name, shape=(128, 128)):
                return [P.sbuf("h_%s%d" % (name, i), list(shape)) for i in range(2)]
            qs, sgm, ff, logf, kk, bb, pre = two("qs"), two("sg"), two("ff"), two("lf"), two("kk"), two("bb"), two("pre")
            e1, Ql, e2, Qd = two("e1"), two("Ql"), two("e2"), two("Qd")
            Kt = [two("Kt%d" % r) for r in range(4)]
            ex = two("ex")
            AT = two("AT", (128, 2, 128))
            KhT = two("KhT")
            bend = two("bend", (128, 2))
            Sb = [P.sbuf("h_S%d" % hp, [128, 128]) for hp in range(2)]
            ps_s = [P.psum("h_pss%d" % i, [128, 2, 128]) for i in range(2)]
            ps_o = [P.psum("h_pso%d" % i, [128, 128]) for i in range(2)]
            ps_k = [P.psum("h_psk%d" % i, [128, 128]) for i in range(2)]
            ps_kv = [P.psum("h_pskv%d" % i, [128, 128]) for i in range(2)]
            it = 0
            jj = 0
            for d in range(2):
                zf = Z_HFF if d == 0 else Z_HFB
                for hp in range(2):
                    kb.MS(Sb[hp][:], 0.0, [Sb[hp]])
                for n in ORDER[d]:
                    cols = slice(n * 128, (n + 1) * 128)
                    b = it % 2; it += 1
                    hq, hf = hqb[b], hfb[b]
                    kb.LD(hq[:], kb.ZF[Z_HQ:Z_HQ + 256, cols].rearrange("(hp p) t -> p hp t", p=128), [hq])
                    kb.LD(hf[:], kb.ZF[zf:zf + 256, cols].rearrange("(hp p) t -> p hp t", p=128), [hf])
                    for h in range(4):
                        kb.LD(Vp[b][h][:, 64 * (h % 2):64 * (h % 2) + 64], kb.ZT[cols, 64 * h:64 * h + 64], [Vp[b][h]])
                    for hp in range(2):
                        j = jj % 2; jj += 1
                        c = 2 * d + hp
                        mref = MREF[d][j]
                        kb.ACT(qs[j][:], hq[:, hp, :], AF.Silu, [hq], [qs[j]])
                        kb.ACT(sgm[j][:], hf[:, hp, :], AF.Sigmoid, [hf], [sgm[j]])
                        kb.TS(ff[j][:], sgm[j][:], OML[:, c:c + 1], LB[:, c:c + 1], ALU.mult, ALU.add, [sgm[j], OML, LB], [ff[j]])
                        kb.ACT(logf[j][:], ff[j][:], AF.Ln, [ff[j]], [logf[j]])
                        kb.TS(kk[j][:], ff[j][:], -1.0, 1.0, ALU.mult, ALU.add, [ff[j]], [kk[j]], eng="pool")
                        B = bb[j]
                        if d == 0:
                            kb.SCAN(B[:], kb.C("ONES"), logf[j][:], [logf[j]], [B])
                            kb.CP(mref[:, 1:4], B[:].rearrange("p (r c) -> p r c", c=32)[:, 0:3, 31], [B], [mref])
                            be = B[:, 127:128]
                        else:
                            kb.SCAN(pre[j][:], kb.C("ONES"), logf[j][:], [logf[j]], [pre[j]])
                            kb.STT(B[:], pre[j][:], -1.0, logf[j][:], ALU.mult, ALU.add, [pre[j], logf[j]], [B])
                            kb.TS(B[:], B[:], pre[j][:, 127:128], None, ALU.add, None, [B, pre[j]], [B])
                            kb.CP(mref[:, 0:3], B[:].rearrange("p (r c) -> p r c", c=32)[:, 1:4, 0], [B], [mref])
                            be = B[:, 0:1]
                        kb.TT(e1[j][:].rearrange("p (r c) -> p r c", c=32), B[:].rearrange("p (r c) -> p r c", c=32),
                              mref[:].unsqueeze(2).to_broadcast([128, 4, 32]), ALU.subtract, [B, mref], [e1[j]])
                        kb.ACT(e1[j][:], e1[j][:], AF.Exp, [e1[j]], [e1[j]])
                        kb.STT(Ql[j][:], qs[j][:], 0.125, e1[j][:], ALU.mult, ALU.mult, [qs[j], e1[j]], [Ql[j]], eng="pool")
                        kb.ACT(e2[j][:], B[:], AF.Exp, [B], [e2[j]])
                        kb.STT(Qd[j][:], qs[j][:], 0.125, e2[j][:], ALU.mult, ALU.mult, [qs[j], e2[j]], [Qd[j]], eng="pool")
                        pss = ps_s[j]
                        for r in range(4):
                            kb.ACT(ex[j][:], B[:], AF.Exp, [B, mref], [ex[j]], scale=-1.0, bias=mref[:, r:r + 1])
                            kb.STT(Kt[r][j][:], ex[j][:], 1e26, kk[j][:], ALU.min, ALU.mult, [ex[j], kk[j]], [Kt[r][j]])
                            for h2 in range(2):
                                kb.MM(pss[:, h2, 32 * r:32 * r + 32], Kt[r][j][64 * h2:64 * h2 + 64, :],
                                      Ql[j][64 * h2:64 * h2 + 64, 32 * r:32 * r + 32], True, True,
                                      [Kt[r][j], Ql[j]], [pss])
                        kb.TT(AT[j][:], pss[:], kb.C("TRIF" if d == 0 else "TRIB").unsqueeze(1).to_broadcast([128, 2, 128]),
                              ALU.mult, [pss], [AT[j]])
                        po = ps_o[j]
                        kb.MM(po[:], Vp[b][2 * hp][:], AT[j][:, 0, :], True, False, [Vp[b][2 * hp], AT[j]], [po])
                        kb.MM(po[:], Vp[b][2 * hp + 1][:], AT[j][:, 1, :], False, False, [Vp[b][2 * hp + 1], AT[j]], [po])
                        kb.MM(po[:], Sb[hp][:], Qd[j][:], False, True, [Sb[hp], Qd[j]], [po])
                        oacc_write(kb, OACC, hp, n, po, d)
                        kb.CP(bend[j][:, 0:1], be, [B], [bend[j]])
                        kb.ACT(KhT[j][:], B[:], AF.Exp, [B, bend[j]], [KhT[j]], scale=-1.0, bias=bend[j][:, 0:1])
                        kb.TT(KhT[j][:], KhT[j][:], kk[j][:], ALU.mult, [KhT[j], kk[j]], [KhT[j]], eng="pool")
                        kb.ACT(bend[j][:, 1:2], bend[j][:, 0:1], AF.Exp, [bend[j]], [bend[j]])
                        pk = ps_k[j]
                        kb.TR(pk[:], KhT[j][:], kb.C("IDENT"), [KhT[j]], [pk])
                        for h2 in range(2):
                            h = 2 * hp + h2
                            kb.CP(khp[b][h][:, 64 * h2:64 * h2 + 64], pk[:, 64 * h2:64 * h2 + 64], [pk], [khp[b][h]],
                                  eng=("act" if h2 else "dve"))
                        pkv = ps_kv[j]
                        kb.MM(pkv[:], khp[b][2 * hp][:], Vp[b][2 * hp][:], True, False, [khp[b][2 * hp], Vp[b][2 * hp]], [pkv])
                        kb.MM(pkv[:], khp[b][2 * hp + 1][:], Vp[b][2 * hp + 1][:], False, True,
                              [khp[b][2 * hp + 1], Vp[b][2 * hp + 1]], [pkv])
                        kb.STT(Sb[hp][:], Sb[hp][:], bend[j][:, 1:2], pkv[:], ALU.mult, ALU.add,
                               [Sb[hp], bend[j], pkv], [Sb[hp]])
        with P.scope():
            G = P.sbuf("h_G2", [128, 1])
            for hh in range(2):
                kb.LD(G[64 * hh:64 * hh + 64, :], kb.prm["hgrn_norm_g"][l].rearrange("(p o) -> p o", o=1), [G])
            finalize_gated(kb, OACC, Z_HG, G, 0, "h_")


PI = float(np.pi)


def _sincos(kb, ang, sin_out, cos_out, R, tmp, shape=None):
    P = kb.P
    shp = list(ang.shape)
    with P.scope():
        ki = P.sbuf("sc_ki", shp, mybir.dt.int32)
        kf = P.sbuf("sc_kf", shp)
        r = P.sbuf("sc_r", shp)
        m = P.sbuf("sc_m", shp)
        C1 = 6.28125
        C2 = 2 * PI - C1
        for (shift, out) in ((0.0, sin_out), (PI / 2, cos_out)):
            kb.TS(r[:], ang, shift, None, ALU.add, None, R, [r])
            kb.TS(kf[:], r[:], 1.0 / (2 * PI), None, ALU.mult, None, [r], [kf])
            kb.CP(ki[:], kf[:], [kf], [ki])
            kb.CP(kf[:], ki[:], [ki], [kf])
            kb.STT(r[:], kf[:], -C1, r[:], ALU.mult, ALU.add, [kf, r], [r])
            kb.STT(r[:], kf[:], -C2, r[:], ALU.mult, ALU.add, [kf, r], [r])
            kb.TS(m[:], r[:], PI, 2 * PI, ALU.is_gt, ALU.mult, [r], [m])
            kb.TT(r[:], r[:], m[:], ALU.subtract, [r, m], [r])
            kb.TS(m[:], r[:], -PI, 2 * PI, ALU.is_lt, ALU.mult, [r], [m])
            kb.TT(r[:], r[:], m[:], ALU.add, [r, m], [r])
            kb.ACT(out, r[:], AF.Sin, [r], R)


def mixer_s5(kb, l):
    P = kb.P
    prm = kb.prm
    with P.scope():
        OACC = P.sbuf("s_oacc", [128, 2, S])
        with P.scope():
            WX = P.sbuf("s_WX", [128, 2, 8, 2, 64])
            Cblk = P.sbuf("s_Cblk", [128, 16, 128])
            kb.MS(WX[:], 0.0, [WX], eng="pool")
            kb.MS(Cblk[:], 0.0, [Cblk], eng="pool")
            for g8 in range(8):
                for ri, nm in enumerate(("s5_b_re", "s5_b_im")):
                    for gg in range(2):
                        src = prm[nm][l][8 * gg + g8].rearrange("p c -> c p")
                        kb.LD(WX[16 * g8:16 * g8 + 16, gg, g8, ri, :], src, [WX], allow_slow_non_contiguous=True)
            for g in range(16):
                g8 = g % 8
                kb.LD(Cblk[0:64, g, 16 * g8:16 * g8 + 16], prm["s5_c_re"][l][g].rearrange("c p -> p c"), [Cblk],
                      allow_slow_non_contiguous=True)
                kb.LD(Cblk[64:128, g, 16 * g8:16 * g8 + 16], prm["s5_c_im"][l][g].rearrange("c p -> p c"), [Cblk],
                      allow_slow_non_contiguous=True)
            kb.TS(Cblk[64:128, :, :], Cblk[64:128, :, :], -1.0, None, ALU.mult, None, [Cblk], [Cblk])
            Cb16 = P.sbuf("s_Cb16", [128, 16, 128], BF16)
            kb.CP(Cb16[:], Cblk[:], [Cblk], [Cb16])
            VFr = P.sbuf("s_VFr", [128, 16, 64]); VFi = P.sbuf("s_VFi", [128, 16, 64])
            T1 = P.sbuf("s_T1", [128, 16, 128]); T2 = P.sbuf("s_T2", [128, 16, 128])
            AR = P.sbuf("s_AR", [128, 16]); NAI = P.sbuf("s_NAI", [128, 16])
            for d in range(2):
                with P.scope():
                    lr = P.sbuf("s_lr", [128, 16, 64]); li = P.sbuf("s_li", [128, 16, 64]); dtb = P.sbuf("s_dt", [128, 16])
                    kb.LD(lr[:], prm["s5_lam_re"][l][d].rearrange("g p -> (g p)").partition_broadcast(128), [lr])
                    kb.LD(li[:], prm["s5_lam_im"][l][d].rearrange("g p -> (g p)").partition_broadcast(128), [li])
                    kb.LD(dtb[:], prm["s5_log_dt"][l][d].partition_broadcast(128), [dtb])
                    kb.ACT(dtb[:], dtb[:], AF.Exp, [dtb], [dtb])
                    dt_bc = dtb[:].unsqueeze(2).to_broadcast([128, 16, 64])
                    lrdt = P.sbuf("s_lrdt", [128, 16, 64]); lidt = P.sbuf("s_lidt", [128, 16, 64])
                    kb.TT(lrdt[:], lr[:], dt_bc, ALU.mult, [lr, dtb], [lrdt])
                    kb.TT(lidt[:], li[:], dt_bc, ALU.mult, [li, dtb], [lidt])
                    a = [P.sbuf("s_a%d" % i, [128, 16, 64]) for i in range(8)]
                    mag, ang, sn, cs, tmp, ar, ai, t2 = a
                    kb.ACT(mag[:], lrdt[:], AF.Exp, [lrdt], [mag])
                    _sincos(kb, lidt[:], sn[:], cs[:], [lidt, sn, cs, tmp], tmp[:])
                    kb.TT(ar[:], mag[:], cs[:], ALU.mult, [mag, cs], [ar])
                    kb.TT(ai[:], mag[:], sn[:], ALU.mult, [mag, sn], [ai])
                    den = P.sbuf("s_den", [128, 16, 64]); fr = P.sbuf("s_fr", [128, 16, 64]); fi = P.sbuf("s_fi", [128, 16, 64])
                    kb.TT(den[:], lr[:], lr[:], ALU.mult, [lr], [den])
                    kb.TT(t2[:], li[:], li[:], ALU.mult, [li], [t2])
                    kb.TT(den[:], den[:], t2[:], ALU.add, [den, t2], [den])
                    kb.RECIP(den[:], den[:], [den], [den])
                    kb.TS(ar[:], ar[:], -1.0, None, ALU.add, None, [ar], [ar])
                    kb.TT(fr[:], ar[:], lr[:], ALU.mult, [ar, lr], [fr])
                    kb.TT(t2[:], ai[:], li[:], ALU.mult, [ai, li], [t2])
                    kb.TT(fr[:], fr[:], t2[:], ALU.add, [fr, t2], [fr])
                    kb.TT(fr[:], fr[:], den[:], ALU.mult, [fr, den], [fr])
                    kb.TT(fi[:], ai[:], lr[:], ALU.mult, [ai, lr], [fi])
                    kb.TT(t2[:], ar[:], li[:], ALU.mult, [ar, li], [t2])
                    kb.TT(fi[:], fi[:], t2[:], ALU.subtract, [fi, t2], [fi])
                    kb.TT(fi[:], fi[:], den[:], ALU.mult, [fi, den], [fi])
                    jcol = kb.C("CCOL")[:, 2:3] if d == 0 else kb.C("CCOL")[:, 3:4]
                    njcol = kb.C("CCOL")[:, 6:7] if d == 0 else kb.C("CCOL")[:, 7:8]
                    kb.ACT(mag[:], lrdt[:], AF.Exp, [lrdt], [mag], scale=njcol)
                    kb.TS(ang[:], lidt[:], jcol, None, ALU.mult, None, [lidt], [ang])
                    _sincos(kb, ang[:], sn[:], cs[:], [ang, sn, cs, tmp], tmp[:])
                    vr, vi = ar, ai
                    kb.TT(vr[:], mag[:], cs[:], ALU.mult, [mag, cs], [vr])
                    kb.TT(vi[:], mag[:], sn[:], ALU.mult, [mag, sn], [vi])
                    kb.TS(vi[:], vi[:], -1.0, None, ALU.mult, None, [vi], [vi])
                    kb.TT(VFr[:], vr[:], fr[:], ALU.mult, [vr, fr], [VFr])
                    kb.TT(t2[:], vi[:], fi[:], ALU.mult, [vi, fi], [t2])
                    kb.TT(VFr[:], VFr[:], t2[:], ALU.subtract, [VFr, t2], [VFr])
                    kb.TT(VFi[:], vr[:], fi[:], ALU.mult, [vr, fi], [VFi])
                    kb.TT(t2[:], vi[:], fr[:], ALU.mult, [vi, fr], [t2])
                    kb.TT(VFi[:], VFi[:], t2[:], ALU.add, [VFi, t2], [VFi])
                with P.scope():
                    dtb = P.sbuf("s_dt2", [128, 16])
                    kb.LD(dtb[:], prm["s5_log_dt"][l][d].partition_broadcast(128), [dtb])
                    kb.ACT(dtb[:], dtb[:], AF.Exp, [dtb], [dtb])
                    lrp = P.sbuf("s_lrp", [128, 16]); lip = P.sbuf("s_lip", [128, 16])
                    for hh in range(2):
                        kb.LD(lrp[64 * hh:64 * hh + 64, :], prm["s5_lam_re"][l][d].rearrange("g p -> p g"), [lrp],
                              allow_slow_non_contiguous=True)
                        kb.LD(lip[64 * hh:64 * hh + 64, :], prm["s5_lam_im"][l][d].rearrange("g p -> p g"), [lip],
                              allow_slow_non_contiguous=True)
                    kb.TT(lrp[:], lrp[:], dtb[:], ALU.mult, [lrp, dtb], [lrp])
                    kb.TT(lip[:], lip[:], dtb[:], ALU.mult, [lip, dtb], [lip])
                    b4 = [P.sbuf("s_b%d" % i, [128, 16, 128]) for i in range(4)]
                    arg, sn2, cs2, tmp2 = b4
                    mt = kb.C("IOTAF" if d == 0 else "R127F")
                    mt_bc = mt.unsqueeze(1).to_broadcast([128, 16, 128])
                    kb.TT(arg[:], lrp[:].unsqueeze(2).to_broadcast([128, 16, 128]), mt_bc, ALU.mult, [lrp], [arg])
                    kb.ACT(T1[:], arg[:], AF.Exp, [arg], [T1])
                    kb.TT(arg[:], lip[:].unsqueeze(2).to_broadcast([128, 16, 128]), mt_bc, ALU.mult, [lip, T1], [arg])
                    _sincos(kb, arg[:], sn2[:], cs2[:], [arg, sn2, cs2, tmp2], tmp2[:])
                    kb.TT(T2[:], T1[:], sn2[:], ALU.mult, [T1, sn2], [T2])
                    kb.TS(T2[:], T2[:], -1.0, None, ALU.mult, None, [T2], [T2])
                    kb.TT(T1[:], T1[:], cs2[:], ALU.mult, [T1, cs2], [T1])
                    c4 = [P.sbuf("s_c%d" % i, [128, 16]) for i in range(4)]
                    kb.ACT(c4[0][:], lrp[:], AF.Exp, [lrp], [c4[0]])
                    _sincos(kb, lip[:], c4[1][:], c4[2][:], [lip, c4[1], c4[2], c4[3]], c4[3][:])
                    kb.TT(AR[:], c4[0][:], c4[2][:], ALU.mult, [c4[0], c4[2]], [AR])
                    kb.TT(NAI[:], c4[0][:], c4[1][:], ALU.mult, [c4[0], c4[1]], [NAI])
                    kb.TS(NAI[:], NAI[:], -1.0, None, ALU.mult, None, [NAI], [NAI])
                sweep_scope = P.scope(); sweep_scope.__enter__()
                uTb = [P.sbuf("s_u%d" % i, [128, 2, 128]) for i in range(2)]
                mm_ = [P.sbuf("s_m%d" % i, [128, 8, 64]) for i in range(4)]
                W3 = [P.sbuf("s_W3%d" % i, [128, 8, 3, 64], BF16) for i in range(2)]
                Hb = [P.sbuf("s_Hb%d" % i, [128, 8, 128], BF16) for i in range(2)]
                tri16 = P.sbuf("s_tri16", [128, 128], BF16)
                kb.CP(tri16[:], kb.C("TRIF" if d == 0 else "TRIB"), [], [tri16])
                tP = [P.sbuf("s_tP%d" % i, [128, 8, 128]) for i in range(2)]
                tPs = [P.sbuf("s_tPs%d" % i, [128, 8, 128]) for i in range(2)]
                H1 = [P.sbuf("s_H1%d" % i, [128, 8, 128]) for i in range(2)]
                H2 = [P.sbuf("s_H2%d" % i, [128, 8, 128]) for i in range(2)]
                hend = P.sbuf("s_hend", [128, 16]); hsend = P.sbuf("s_hsend", [128, 16])
                hp_ = P.sbuf("s_hp", [128, 16]); hps_ = P.sbuf("s_hps", [128, 16])
                sm = [P.sbuf("s_sm%d" % i, [128, 16]) for i in range(4)]
                xps = P.psum("s_xps", [128, 1024])
                pps = P.psum("s_pps", [128, 8, 128])
                ppss = P.psum("s_ppss", [128, 8, 128])
                yps = [P.psum("s_yps%d" % i, [128, 128]) for i in range(2)]
                kb.MS(hp_[:], 0.0, [hp_]); kb.MS(hps_[:], 0.0, [hps_])
                te = 127 if d == 0 else 0
                tri = kb.C("TRIF" if d == 0 else "TRIB")
                it = 0
                for n in ORDER[d]:
                    cols = slice(n * 128, (n + 1) * 128)
                    uT = uTb[it % 2]; it += 1
                    kb.LD(uT[:], kb.ZF[Z_SU:Z_SU + 256, cols].rearrange("(gg p) t -> p gg t", p=128), [uT])
                    for gg in range(2):
                        j = gg
                        for half in range(2):
                            kb.MM(xps[:, half * 512:(half + 1) * 512], uT[:, gg, :],
                                  WX[:, gg, half * 4:(half + 1) * 4, :, :].rearrange("q a r p -> q (a r p)"),
                                  True, True, [uT, WX], [xps])
                        xv = xps[:].rearrange("t (g r p) -> t g r p", r=2, p=64)
                        gs = slice(gg * 8, gg * 8 + 8)
                        kb.TT(mm_[0][:], xv[:, :, 0, :], VFr[:, gs, :], ALU.mult, [xps, VFr], [mm_[0]])
                        kb.TT(mm_[1][:], xv[:, :, 1, :], VFi[:, gs, :], ALU.mult, [xps, VFi], [mm_[1]])
                        kb.TT(mm_[2][:], xv[:, :, 0, :], VFi[:, gs, :], ALU.mult, [xps, VFi], [mm_[2]])
                        kb.TT(mm_[3][:], xv[:, :, 1, :], VFr[:, gs, :], ALU.mult, [xps, VFr], [mm_[3]])
                        w3 = W3[j]
                        kb.TT(w3[:, :, 0, :], mm_[0][:], mm_[1][:], ALU.subtract, [mm_[0], mm_[1]], [w3], eng="pool")
                        kb.TT(w3[:, :, 1, :], mm_[2][:], mm_[3][:], ALU.add, [mm_[2], mm_[3]], [w3], eng="pool")
                        kb.TT(w3[:, :, 2, :], mm_[1][:], mm_[0][:], ALU.subtract, [mm_[0], mm_[1]], [w3], eng="pool")
                        for g8 in range(8):
                            kb.MM(pps[:, g8, :], w3[:, g8, 0:2, :].rearrange("q r p -> q (r p)"), tri16[:], True, True, [w3, tri16], [pps])
                            kb.MM(ppss[:, g8, :], w3[:, g8, 1:3, :].rearrange("q r p -> q (r p)"), tri16[:], True, True, [w3, tri16], [ppss])
                        kb.TT(tP[j][:], pps[:], hp_[:, gs].unsqueeze(2).to_broadcast([128, 8, 128]), ALU.add, [pps, hp_], [tP[j]])
                        kb.TT(tPs[j][:], ppss[:], hps_[:, gs].unsqueeze(2).to_broadcast([128, 8, 128]), ALU.add,
                              [ppss, hps_], [tPs[j]])
                        kb.TT(H1[j][:], tP[j][:], T1[:, gs, :], ALU.mult, [tP[j], T1], [H1[j]], eng="pool")
                        kb.TT(H2[j][:], tPs[j][:], T2[:, gs, :], ALU.mult, [tPs[j], T2], [H2[j]])
                        kb.TT(Hb[j][:], H1[j][:], H2[j][:], ALU.add, [H1[j], H2[j]], [Hb[j]], eng="pool")
                        yp = yps[gg]
                        for g8 in range(8):
                            kb.MM(yp[:], Cb16[:, gg * 8 + g8, :], Hb[j][:, g8, :], g8 == 0, g8 == 7, [Cb16, Hb[j]], [yp])
                        oacc_write(kb, OACC, gg, n, yp, d)
                        kb.TT(hend[:, gs], H1[j][:, :, te], H2[j][:, :, te], ALU.add, [H1[j], H2[j]], [hend])
                        kb.TT(sm[0][:, 0:8], tPs[j][:, :, te], T1[:, gs, te], ALU.mult, [tPs[j], T1], [sm[0]])
                        kb.TT(sm[1][:, 0:8], tP[j][:, :, te], T2[:, gs, te], ALU.mult, [tP[j], T2], [sm[1]])
                        kb.TT(hsend[:, gs], sm[0][:, 0:8], sm[1][:, 0:8], ALU.subtract, [sm[0], sm[1]], [hsend])
                    kb.TT(sm[0][:], hend[:], AR[:], ALU.mult, [hend, AR], [sm[0]])
                    kb.TT(sm[1][:], hsend[:], NAI[:], ALU.mult, [hsend, NAI], [sm[1]])
                    kb.TT(sm[2][:], hsend[:], AR[:], ALU.mult, [hsend, AR], [sm[2]])
                    kb.TT(sm[3][:], hend[:], NAI[:], ALU.mult, [hend, NAI], [sm[3]])
                    kb.TT(hp_[:], sm[0][:], sm[1][:], ALU.add, [sm[0], sm[1]], [hp_])
                    kb.TT(hps_[:], sm[2][:], sm[3][:], ALU.subtract, [sm[2], sm[3]], [hps_])
                sweep_scope.__exit__(None, None, None)
        with P.scope():
            dsk = P.sbuf("s_dsk", [128, 2]); glb = P.sbuf("s_glb", [128, 2])
            kb.LD(dsk[:], prm["s5_d"][l].rearrange("(gg p) -> p gg", p=128), [dsk], allow_slow_non_contiguous=True)
            kb.LD(glb[:], prm["s5_glu_b"][l].rearrange("(gg p) -> p gg", p=128), [glb], allow_slow_non_contiguous=True)
            gw = P.sbuf("s_gw", [128, 2, 256])
            kb.LD(gw[:], prm["s5_glu_w"][l].rearrange("(ct p) o -> p ct o", p=128), [gw])
            uTb = [P.sbuf("s_fu%d" % i, [128, 2, 128]) for i in range(2)]
            yy = [P.sbuf("s_yy%d" % i, [128, 2, 128]) for i in range(2)]
            x2 = [P.sbuf("s_x2%d" % i, [128, 2, 128]) for i in range(2)]
            th = [P.sbuf("s_th%d" % i, [128, 2, 128]) for i in range(2)]
            sgb = [P.sbuf("s_sg%d" % i, [128, 128]) for i in range(2)]
            ob = [P.sbuf("s_ob%d" % i, [128, 128]) for i in range(2)]
            psz = [P.psum("s_psz%d" % i, [128, 128]) for i in range(2)]
            k = 0
            for n in range(NT):
                cols = slice(n * 128, (n + 1) * 128)
                i = n % 2
                kb.LD(uTb[i][:], kb.ZF[Z_SU:Z_SU + 256, cols].rearrange("(gg p) t -> p gg t", p=128), [uTb[i]])
                for gg in range(2):
                    kb.STT(yy[i][:, gg, :], uTb[i][:, gg, :], dsk[:, gg:gg + 1], OACC[:, gg, cols], ALU.mult, ALU.add,
                           [uTb[i], dsk, OACC.s(n)], [yy[i]])
                kb.TT(x2[i][:], yy[i][:], yy[i][:], ALU.mult, [yy[i]], [x2[i]], eng="pool")
                kb.TS(x2[i][:], x2[i][:], 0.044715, 1.0, ALU.mult, ALU.add, [x2[i]], [x2[i]])
                kb.TT(x2[i][:], x2[i][:], yy[i][:], ALU.mult, [x2[i], yy[i]], [x2[i]], eng="pool")
                kb.ACT(th[i][:], x2[i][:], AF.Tanh, [x2[i]], [th[i]], scale=0.7978845608028654)
                kb.TS(th[i][:], th[i][:], 1.0, 0.5, ALU.add, ALU.mult, [th[i]], [th[i]])
                kb.TT(yy[i][:], yy[i][:], th[i][:], ALU.mult, [yy[i], th[i]], [yy[i]], eng="pool")
                for ot in range(2):
                    q = k % 2; k += 1
                    for ct in range(2):
                        kb.MM(psz[q][:], gw[:, ct, ot * 128:(ot + 1) * 128], yy[i][:, ct, :], ct == 0, ct == 1, [gw, yy[i]], [psz[q]])
                    kb.ACT(sgb[q][:], psz[q][:], AF.Sigmoid, [psz[q], glb], [sgb[q]], bias=glb[:, ot:ot + 1])
                    kb.TT(ob[q][:], yy[i][:, ot, :], sgb[q][:], ALU.mult, [yy[i], sgb[q]], [ob[q]])
                    kb.ST(kb.YC[768 + ot * 128:768 + (ot + 1) * 128, cols], ob[q][:], [ob[q]])


def gdn_conv(kb, l):
    P = kb.P
    with P.scope():
        CW = P.sbuf("g_cw", [128, 6, 9])
        for kh in range(3):
            for kw in range(3):
                kb.LD(CW[:, :, kh * 3 + kw], kb.prm["gdn_conv_w"][l][kh, kw].rearrange("(ct p) -> p ct", p=128), [CW],
                      allow_slow_non_contiguous=True)
        mlat = P.sbuf("g_mlat", [128, 2, 512]); mctx = P.sbuf("g_mctx", [128, 2, 256])
        kb.LD(mlat[:], kb.cmlat[:], [mlat]); kb.LD(mctx[:], kb.cmctx[:], [mctx])
        Wb = [P.sbuf("g_w%d" % i, [128, 642]) for i in range(2)]
        acc = [[P.sbuf("g_acc%d%d" % (i, j), [128, 512]) for j in range(3)] for i in range(2)]
        sl = [P.sbuf("g_sl%d" % i, [128, 512]) for i in range(2)]
        sq = [P.sbuf("g_sq%d" % i, [128, 512]) for i in range(2)]
        rt = [P.sbuf("g_rt%d" % i, [128, 512]) for i in range(2)]
        ps = [P.psum("g_psn%d" % i, [128, 512]) for i in range(2)]
        spans = [(0, 256, True)] + [(256 + 512 * k, 512, False) for k in range(8)]
        it = 0
        for (t0, L, is_ctx) in spans:
            lo = 0 if is_ctx else 256
            hi = 256 if is_ctx else S
            a = max(lo, t0 - 65); b = min(hi, t0 + L + 65)
            for ct in range(6):
                i = it % 2; it += 1
                W = Wb[i]
                kb.MS(W[:], 0.0, [W], eng="pool")
                kb.LD(W[:, 65 + (a - t0):65 + (b - t0)], kb.ZF[Z_GQKV + ct * 128:Z_GQKV + (ct + 1) * 128, a:b], [W])
                rows = (1,) if is_ctx else (0, 1, 2)
                masks = mctx if is_ctx else mlat
                for dwi, shift in enumerate((-1, 0, 1)):
                    A = acc[i][dwi]
                    eng = "dve"
                    for q, dh in enumerate(rows):
                        o0 = 65 + 64 * (dh - 1) + shift
                        src = W[:, o0:o0 + L]
                        wcol = CW[:, ct, dh * 3 + dwi:dh * 3 + dwi + 1]
                        if q == 0:
                            kb.TS(A[:, :L], src, wcol, None, ALU.mult, None, [W, CW], [A], eng=("pool" if dwi != 1 else "dve"))
                        else:
                            kb.STT(A[:, :L], src, wcol, A[:, :L], ALU.mult, ALU.add, [W, CW, A], [A])
                    if dwi != 1:
                        mi = 0 if dwi == 0 else 1
                        kb.TT(A[:, :L], A[:, :L], masks[:, mi, :L], ALU.mult, [A, masks], [A], eng="pool")
                A0, A1, A2 = acc[i]
                kb.TT(A1[:, :L], A1[:, :L], A0[:, :L], ALU.add, [A0, A1], [A1], eng="pool")
                kb.TT(A1[:, :L], A1[:, :L], A2[:, :L], ALU.add, [A1, A2], [A1], eng="pool")
                kb.ACT(sl[i][:, :L], A1[:, :L], AF.Silu, [A1], [sl[i]])
                if ct < 4:
                    kb.TT(sq[i][:, :L], sl[i][:, :L], sl[i][:, :L], ALU.mult, [sl[i]], [sq[i]], eng="pool")
                    kb.MM(ps[i][:, :L], kb.C("BLK64"), sq[i][:, :L], True, True, [sq[i]], [ps[i]])
                    kb.ACT(rt[i][:, :L], ps[i][:, :L], AF.Sqrt, [ps[i]], [rt[i]], bias=kb.C("CCOL")[:, 0:1])
                    kb.RECIP(rt[i][:, :L], rt[i][:, :L], [rt[i]], [rt[i]])
                    if ct < 2:
                        kb.STT(sl[i][:, :L], sl[i][:, :L], 0.125, rt[i][:, :L], ALU.mult, ALU.mult, [sl[i], rt[i]], [sl[i]])
                    else:
                        kb.TT(sl[i][:, :L], sl[i][:, :L], rt[i][:, :L], ALU.mult, [sl[i], rt[i]], [sl[i]])
                kb.ST(kb.QKVF[ct * 128:(ct + 1) * 128, t0:t0 + L], sl[i][:, :L], [sl[i]])


def mixer_gdn(kb, l):
    P = kb.P
    gdn_conv(kb, l)
    upto = kb.cfg.get("gdn_upto", 99)
    if upto < 1:
        return
    with P.scope():
        OACC = P.sbuf("g_oacc", [128, 2, S])
        with P.scope():
            DTB = P.sbuf("g_dtb", [128, 8]); NEGA = P.sbuf("g_nega", [128, 8])
            kb.LD(DTB[:], kb.prm["gdn_dt_bias"][l].rearrange("d h -> (d h)").partition_broadcast(128), [DTB])
            kb.LD(NEGA[:], kb.prm["gdn_a_log"][l].rearrange("d h -> (d h)").partition_broadcast(128), [NEGA])
            kb.ACT(NEGA[:], NEGA[:], AF.Exp, [NEGA], [NEGA])
            kb.TS(NEGA[:], NEGA[:], -1.0, None, ALU.mult, None, [NEGA], [NEGA])
            qnb = [P.sbuf("g_q%d" % i, [128, 2, 128]) for i in range(2)]
            knb = [P.sbuf("g_k%d" % i, [128, 2, 128]) for i in range(2)]
            vvb = [P.sbuf("g_v%d" % i, [128, 2, 128]) for i in range(2)]
            gabb = [P.sbuf("g_gab%d" % i, [128, 16]) for i in range(2)]

            def sm4(name, w=4):
                return P.sbuf("g_" + name, [128, w])
            xa, ea, loga, beta, lnb = sm4("xa"), sm4("ea"), sm4("loga"), sm4("beta"), sm4("lnb")
            gtm, ngt, ekr, cdec, eg, beg, gpl = sm4("gtm"), sm4("ngt"), sm4("ekr"), sm4("cdec"), sm4("eg"), sm4("beg"), sm4("gpl")
            ROWS = P.sbuf("g_rows", [4, 384])
            LI = P.sbuf("g_LI", [128, 4, 128]); LBT = P.sbuf("g_LBT", [128, 4, 128]); LBm = P.sbuf("g_LB", [128, 4, 128])
            NAT = P.sbuf("g_NAT", [128, 4, 128]); NA = P.sbuf("g_NA", [128, 4, 128]); QKm = P.sbuf("g_QKm", [128, 4, 128])
            Tm = P.sbuf("g_Tm", [128, 4, 128]); Wm = P.sbuf("g_Wm", [128, 4, 128])
            x1 = P.sbuf("g_x1", [128, 4, 128]); y1 = P.sbuf("g_y1", [128, 4, 128])
            tmx = P.sbuf("g_tmx", [128, 4, 128]); tmy = P.sbuf("g_tmy", [128, 4, 128])
            Rm = [P.sbuf("g_R%d" % h, [128, 128]) for h in range(4)]
            khp = [P.sbuf("g_kh%d" % h, [128, 128]) for h in range(4)]
            vnp = [P.sbuf("g_vn%d" % h, [128, 128]) for h in range(4)]
            for h in range(4):
                kb.MS(khp[h][:], 0.0, [khp[h]], eng="pool")
                kb.MS(vnp[h][:], 0.0, [vnp[h]], eng="pool")
            upair = [P.sbuf("g_up%d" % hp, [128, 128]) for hp in range(2)]
            wTp = [P.sbuf("g_wT%d" % hp, [128, 128]) for hp in range(2)]
            EG = [P.sbuf("g_EG%d" % hp, [128, 128]) for hp in range(2)]
            qd = [P.sbuf("g_qd%d" % hp, [128, 128]) for hp in range(2)]
            cdp = [P.sbuf("g_cdp%d" % hp, [128, 1]) for hp in range(2)]
            Sb = [P.sbuf("g_S%d" % hp, [128, 128]) for hp in range(2)]
            B = [P.psum("g_B%d" % i, [128, 512]) for i in range(8)]
            ident = kb.C("IDENT")
            it = 0
            for d in range(2):
                tri = kb.C("TRIF" if d == 0 else "TRIB")
                rem = kb.C("SUFF" if d == 0 else "PREB")
                n_incl = kb.C("NLE" if d == 0 else "NGE")
                n_strT = kb.C("NLT" if d == 0 else "NGT")
                n_str = kb.C("NGT" if d == 0 else "NLT")
                for hp in range(2):
                    kb.MS(Sb[hp][:], 0.0, [Sb[hp]])
                for n in ORDER[d][:kb.cfg.get("ntiles", NT)]:
                    cols = slice(n * 128, (n + 1) * 128)
                    b = it % 2; it += 1
                    qn, kn, vv, gab = qnb[b], knb[b], vvb[b], gabb[b]
                    kb.LD(qn[:], kb.QKVF[0:256, cols].rearrange("(hp p) t -> p hp t", p=128), [qn])
                    kb.LD(kn[:], kb.QKVF[256:512, cols].rearrange("(hp p) t -> p hp t", p=128), [kn])
                    kb.LD(vv[:], kb.QKVF[512:768, cols].rearrange("(hp p) t -> p hp t", p=128), [vv])
                    kb.LD(gab[:], kb.ZT[cols, 512:528], [gab])
                    kb.TT(xa[:], gab[:, 4 * d:4 * d + 4], DTB[:, 4 * d:4 * d + 4], ALU.add, [gab, DTB], [xa])
                    kb.ACT(ea[:], xa[:], AF.Exp, [xa], [ea])
                    kb.ACT(ea[:], ea[:], AF.Ln, [ea], [ea], bias=kb.C("CCOL")[:, 1:2])
                    kb.TT(loga[:], ea[:], NEGA[:, 4 * d:4 * d + 4], ALU.mult, [ea, NEGA], [loga])
                    kb.ACT(beta[:], gab[:, 8 + 4 * d:12 + 4 * d], AF.Sigmoid, [gab], [beta])
                    kb.ACT(lnb[:], beta[:], AF.Ln, [beta], [lnb])
                    kb.MM(B[0][:, 0:4], tri, loga[:], True, True, [loga], [B[0]])
                    kb.MM(B[0][:, 4:8], rem, loga[:], True, True, [loga], [B[0]])
                    kb.MM(B[0][:, 8:12], kb.C("ONES"), loga[:], True, True, [loga], [B[0]])
                    kb.CP(gtm[:], B[0][:, 0:4], [B[0]], [gtm])
                    kb.TS(ngt[:], B[0][:, 0:4], -1.0, None, ALU.mult, None, [B[0]], [ngt])
                    kb.ACT(ekr[:], B[0][:, 4:8], AF.Exp, [B[0]], [ekr])
                    kb.ACT(cdec[:], B[0][:, 8:12], AF.Exp, [B[0]], [cdec])
                    kb.ACT(eg[:], gtm[:], AF.Exp, [gtm], [eg])
                    kb.TT(beg[:], beta[:], eg[:], ALU.mult, [beta, eg], [beg])
                    kb.TT(gpl[:], gtm[:], lnb[:], ALU.add, [gtm, lnb], [gpl])
                    kb.MM(B[1][0:4, 0:128], loga[:], tri, True, True, [loga], [B[1]])
                    kb.MM(B[1][0:4, 128:256], loga[:], tri, True, False, [loga], [B[1]])
                    kb.MM(B[1][0:4, 128:256], lnb[:], ident, False, True, [lnb], [B[1]])
                    kb.CP(ROWS[:, 0:256], B[1][0:4, 0:256], [B[1]], [ROWS])
                    kb.TS(ROWS[:, 256:384], B[1][0:4, 0:128], -1.0, None, ALU.mult, None, [B[1]], [ROWS])
                    if upto < 2:
                        continue
                    for (dst, rsl, negm, bias_t, bank) in ((LI, slice(0, 128), n_incl, ngt, B[2]),
                                                           (LBT, slice(128, 256), n_strT, ngt, B[3]),
                                                           (LBm, slice(256, 384), n_str, gpl, B[2])):
                        for h in range(4):
                            kb.MM(bank[:, h * 128:(h + 1) * 128], kb.C("SELH%d" % h)[0:4, :], ROWS[:, rsl], True, False,
                                  [ROWS], [bank])
                            kb.MM(bank[:, h * 128:(h + 1) * 128], ident, negm, False, True, [], [bank])
                        for h in range(4):
                            kb.ACT(dst[:, h, :], bank[:, h * 128:(h + 1) * 128], AF.Exp, [bank, bias_t], [dst],
                                   bias=bias_t[:, h:h + 1])
                    if upto < 3:
                        continue
                    for h in range(4):
                        hp, h2 = divmod(h, 2)
                        ksl = kn[64 * h2:64 * h2 + 64, hp, :]
                        kb.MM(B[4][:, h * 128:(h + 1) * 128], ksl, ksl, True, True, [kn], [B[4]])
                        kb.MM(B[5][:, h * 128:(h + 1) * 128], ksl, qn[64 * h2:64 * h2 + 64, hp, :], True, True, [kn, qn], [B[5]])
                    b4v = B[4][:].rearrange("p (h t) -> p h t", h=4)
                    b5v = B[5][:].rearrange("p (h t) -> p h t", h=4)
                    kb.STT(NAT[:], b4v, -1.0, LBT[:], ALU.mult, ALU.mult, [B[4], LBT], [NAT])
                    kb.STT(NA[:], b4v, -1.0, LBm[:], ALU.mult, ALU.mult, [B[4], LBm], [NA])
                    kb.TT(QKm[:], b5v, LI[:], ALU.mult, [B[5], LI], [QKm])
                    if upto < 4:
                        continue
                    idb = ident.unsqueeze(1).to_broadcast([128, 4, 128])
                    kb.CP(Tm[:], idb, [], [Tm])
                    kb.CP(Wm[:], idb, [], [Wm], eng="pool")
                    for s_ in (1, 2, 4, 8, 16, 32, 64):
                        mT = kb.C(("MOFF%d" if d == 0 else "MOFFT%d") % s_).unsqueeze(1).to_broadcast([128, 4, 128])
                        mW = kb.C(("MOFFT%d" if d == 0 else "MOFF%d") % s_).unsqueeze(1).to_broadcast([128, 4, 128])
                        for h in range(4):
                            kb.MM(B[2][:, h * 128:(h + 1) * 128], NAT[:, h, :], Tm[:, h, :], True, True, [NAT, Tm], [B[2]])
                        for h in range(4):
                            kb.MM(B[3][:, h * 128:(h + 1) * 128], NA[:, h, :], Wm[:, h, :], True, True, [NA, Wm], [B[3]])
                        kb.CP(x1[:], B[2][:].rearrange("p (h t) -> p h t", h=4), [B[2]], [x1], eng="act")
                        kb.CP(y1[:], B[3][:].rearrange("p (h t) -> p h t", h=4), [B[3]], [y1], eng="dve")
                        for h in range(4):
                            kb.MM(B[4][:, h * 128:(h + 1) * 128], Wm[:, h, :], x1[:, h, :], True, True, [Wm, x1], [B[4]])
                        for h in range(4):
                            kb.MM(B[5][:, h * 128:(h + 1) * 128], Tm[:, h, :], y1[:, h, :], True, True, [Tm, y1], [B[5]])
                        kb.TT(tmx[:], B[4][:].rearrange("p (h t) -> p h t", h=4), mT, ALU.mult, [B[4]], [tmx])
                        kb.TT(tmy[:], B[5][:].rearrange("p (h t) -> p h t", h=4), mW, ALU.mult, [B[5]], [tmy])
                        kb.TT(Tm[:], Tm[:], tmx[:], ALU.add, [Tm, tmx], [Tm], eng="pool")
                        kb.TT(Wm[:], Wm[:], tmy[:], ALU.add, [Wm, tmy], [Wm], eng="pool")
                    if upto < 5:
                        continue
                    for hp in range(2):
                        kb.TR(B[0][:, 128:256], kn[:, hp, :], ident, [kn], [B[0]])
                        kb.TR(B[0][:, 256:384], vv[:, hp, :], ident, [vv], [B[0]])
                        for h2 in range(2):
                            h = 2 * hp + h2
                            kc = slice(64 * h2, 64 * h2 + 64)
                            vc = slice(64 * (1 - h2), 64 * (1 - h2) + 64)
                            kb.TS(Rm[h][:, kc], B[0][:, 128 + 64 * h2:128 + 64 * h2 + 64], beg[:, h:h + 1], None, ALU.mult, None,
                                  [B[0], beg], [Rm[h]])
                            kb.ACT(Rm[h][:, vc], B[0][:, 256 + 64 * h2:256 + 64 * h2 + 64], AF.Copy, [B[0], beta], [Rm[h]],
                                   scale=beta[:, h:h + 1])
                            kb.ACT(khp[h][:, kc], B[0][:, 128 + 64 * h2:128 + 64 * h2 + 64], AF.Copy, [B[0], ekr], [khp[h]],
                                   scale=ekr[:, h:h + 1])
                    if upto < 6:
                        continue
                    for h in range(4):
                        kb.MM(B[2][:, h * 128:(h + 1) * 128], Wm[:, h, :], Rm[h][:], True, True, [Wm, Rm[h]], [B[2]])
                        kb.MM(B[3][:, h * 128:(h + 1) * 128], Rm[h][:], Wm[:, h, :], True, True, [Wm, Rm[h]], [B[3]])
                    for h in range(4):
                        hp, h2 = divmod(h, 2)
                        vc0 = 64 * (1 - h2)
                        kb.CP(upair[hp][:, 64 * h2:64 * h2 + 64], B[2][:, h * 128 + vc0:h * 128 + vc0 + 64], [B[2]], [upair[hp]],
                              eng=("act" if h2 else "dve"))
                        kb.CP(wTp[hp][64 * h2:64 * h2 + 64, :], B[3][64 * h2:64 * h2 + 64, h * 128:(h + 1) * 128], [B[3]], [wTp[hp]],
                              eng=("dve" if h2 else "act"))
                    if upto < 7:
                        continue
                    for hp in range(2):
                        kb.MM(B[1][:, 256:384], kb.C("SELP%d" % hp)[0:4, :], ROWS[:, 0:128], True, True, [ROWS], [B[1]])
                        kb.ACT(EG[hp][:], B[1][:, 256:384], AF.Exp, [B[1]], [EG[hp]])
                        kb.TT(qd[hp][:], qn[:, hp, :], EG[hp][:], ALU.mult, [qn, EG[hp]], [qd[hp]], eng="pool")
                        pws = B[7][:, hp * 128:(hp + 1) * 128]
                        kb.MM(pws, wTp[hp][:], Sb[hp][:], True, True, [wTp[hp], Sb[hp]], [B[7]])
                        for h2 in range(2):
                            h = 2 * hp + h2
                            cs_ = slice(64 * h2, 64 * h2 + 64)
                            kb.TT(vnp[h][:, cs_], upair[hp][:, cs_], B[7][:, hp * 128 + 64 * h2:hp * 128 + 64 * h2 + 64],
                                  ALU.subtract, [upair[hp], B[7]], [vnp[h]])
                        po = B[6][:, hp * 256:hp * 256 + 128]
                        kb.MM(po, Sb[hp][:], qd[hp][:], True, False, [Sb[hp], qd[hp]], [B[6].s(hp)])
                        kb.MM(po, vnp[2 * hp][:], QKm[:, 2 * hp, :], False, False, [vnp[2 * hp], QKm], [B[6].s(hp)])
                        kb.MM(po, vnp[2 * hp + 1][:], QKm[:, 2 * hp + 1, :], False, True, [vnp[2 * hp + 1], QKm], [B[6].s(hp)])
                        cols_ = slice(n * 128, (n + 1) * 128)
                        if d == 0:
                            kb.CP(OACC[:, hp, cols_], po, [B[6].s(hp)], [OACC.s(n)], eng="act")
                        else:
                            kb.TT(OACC[:, hp, cols_], OACC[:, hp, cols_], po, ALU.add, [B[6].s(hp)], [OACC.s(n)])
                        pkv = B[6][:, hp * 256 + 128:hp * 256 + 256]
                        kb.MM(pkv, khp[2 * hp][:], vnp[2 * hp][:], True, False, [khp[2 * hp], vnp[2 * hp]], [B[6].s(2 + hp)])
                        kb.MM(pkv, khp[2 * hp + 1][:], vnp[2 * hp + 1][:], False, True, [khp[2 * hp + 1], vnp[2 * hp + 1]],
                              [B[6].s(2 + hp)])
                        kb.CP(cdp[hp][0:64, :], cdec[0:64, 2 * hp:2 * hp + 1], [cdec], [cdp[hp]])
                        kb.CP(cdp[hp][64:128, :], cdec[64:128, 2 * hp + 1:2 * hp + 2], [cdec], [cdp[hp]])
                        kb.STT(Sb[hp][:], Sb[hp][:], cdp[hp][:, 0:1], pkv, ALU.mult, ALU.add,
                               [Sb[hp], cdp[hp], B[6].s(2 + hp)], [Sb[hp]])
        with P.scope():
            G = P.sbuf("g_G2", [128, 1])
            for hh in range(2):
                kb.LD(G[64 * hh:64 * hh + 64, :], kb.prm["gdn_norm_g"][l].rearrange("(p o) -> p o", o=1), [G])
            finalize_gated(kb, OACC, Z_GG, G, 512, "g_")


class _Ctx:
    pass


def mixer_gdn2(kb, l):
    P = kb.P
    (gdn_conv if kb.cfg.get('conv_old') else gdn_conv2)(kb, l)
    with P.scope():
        OACC = P.sbuf("g_oacc", [128, 2, S])
        kb.MS(OACC[:, 0, :], 0.0, [OACC.s(n) for n in range(NT)], eng="pool")
        kb.MS(OACC[:, 1, :], 0.0, [OACC.s(n) for n in range(NT)], eng="pool")
        with P.scope():
            DTB = P.sbuf("g_dtb", [128, 8]); NEGA = P.sbuf("g_nega", [128, 8])
            kb.LD(DTB[:], kb.prm["gdn_dt_bias"][l].rearrange("d h -> (d h)").partition_broadcast(128), [DTB])
            kb.LD(NEGA[:], kb.prm["gdn_a_log"][l].rearrange("d h -> (d h)").partition_broadcast(128), [NEGA])
            kb.ACT(NEGA[:], NEGA[:], AF.Exp, [NEGA], [NEGA])
            kb.TS(NEGA[:], NEGA[:], -1.0, None, ALU.mult, None, [NEGA], [NEGA])
            ident = kb.C("IDENT")
            idb = ident.unsqueeze(1).to_broadcast([128, 4, 128])
            cxs = []
            for d in range(2):
                cx = _Ctx()
                cx.d = d
                pf = "g%d_" % d
                cx.qnb = [P.sbuf(pf + "q%d" % i, [128, 2, 128]) for i in range(2)]
                cx.knb = [P.sbuf(pf + "k%d" % i, [128, 2, 128]) for i in range(2)]
                cx.vvb = [P.sbuf(pf + "v%d" % i, [128, 2, 128]) for i in range(2)]
                cx.gabb = [P.sbuf(pf + "gab%d" % i, [128, 16]) for i in range(2)]
                for nm in ("xa", "ea", "loga", "beta", "lnb", "gtm", "ngt", "ekr", "cdec", "eg", "beg", "gpl"):
                    setattr(cx, nm, P.sbuf(pf + nm, [128, 4]))
                cx.ROWS = P.sbuf(pf + "rows", [4, 384])
                for nm in ("LI", "LBT", "LBm"):
                    setattr(cx, nm, P.sbuf(pf + nm, [128, 4, 128]))
                cx.QKm = P.sbuf(pf + "QKm", [128, 4, 128], BF16)
                cx.ROWSX = P.sbuf(pf + "rowsx", [4, 3, 4, 128])
                cx.knp = [[P.sbuf(pf + "knp%d%d" % (i, h), [128, 128], BF16) for h in range(4)] for i in range(2)]
                for i in range(2):
                    for h in range(4):
                        kb.MS(cx.knp[i][h][:], 0.0, [cx.knp[i][h]], eng="pool")
                cx.kq16 = [P.sbuf(pf + "kq16%d" % i, [128, 2, 2, 128], BF16) for i in range(2)]
                for nm in ("NAT", "NA", "Tm", "Wm", "x1", "y1", "tmx", "tmy"):
                    setattr(cx, nm, P.sbuf(pf + nm, [128, 4, 128], BF16))
                cx.Rm = [P.sbuf(pf + "R%d" % h, [128, 128], BF16) for h in range(4)]
                cx.khp = [P.sbuf(pf + "kh%d" % h, [128, 128], BF16) for h in range(4)]
                cx.vnp = [P.sbuf(pf + "vn%d" % h, [128, 128], BF16) for h in range(4)]
                for h in range(4):
                    kb.MS(cx.khp[h][:], 0.0, [cx.khp[h]], eng="pool")
                    kb.MS(cx.vnp[h][:], 0.0, [cx.vnp[h]], eng="pool")
                cx.upair = [P.sbuf(pf + "up%d" % hp, [128, 128]) for hp in range(2)]
                cx.wTp = [P.sbuf(pf + "wT%d" % hp, [128, 128]) for hp in range(2)]
                cx.EG = [P.sbuf(pf + "EG%d" % hp, [128, 128]) for hp in range(2)]
                cx.qd = [P.sbuf(pf + "qd%d" % hp, [128, 128]) for hp in range(2)]
                cx.cdp = [P.sbuf(pf + "cdp%d" % hp, [128, 1]) for hp in range(2)]
                cx.Sb = [P.sbuf(pf + "S%d" % hp, [128, 128]) for hp in range(2)]
                for hp in range(2):
                    kb.MS(cx.Sb[hp][:], 0.0, [cx.Sb[hp]])
                cx.B = [P.psum(pf + "B%d" % i, [128, 512]) for i in range(4)]
                cx.tri = kb.C("TRIF" if d == 0 else "TRIB")
                cx.rem = kb.C("SUFF" if d == 0 else "PREB")
                cx.n_incl = kb.C("NLE" if d == 0 else "NGE")
                cx.n_strT = kb.C("NLT" if d == 0 else "NGT")
                cx.n_str = kb.C("NGT" if d == 0 else "NLT")
                cx.it = 0
                cx.id16 = P.sbuf(pf + "id16", [128, 128], BF16)
                kb.CP(cx.id16[:], ident, [], [cx.id16])
                for nm_, cn in (("n_incl4", "NLE" if d == 0 else "NGE"), ("n_strT4", "NLT" if d == 0 else "NGT"),
                                ("n_str4", "NGT" if d == 0 else "NLT")):
                    t_ = P.sbuf(pf + nm_, [128, 4, 128], BF16)
                    kb.CP(t_[:], kb.C(cn).unsqueeze(1).to_broadcast([128, 4, 128]), [], [t_])
                    setattr(cx, nm_, t_[:].rearrange("p h i -> p (h i)"))
                bd = P.sbuf(pf + "bd4", [4, 4, 128])
                for h in range(4):
                    kb.CP(bd[:, h, :], kb.C("SELH%d" % h)[0:4, :], [], [bd])
                cx.bd4 = bd[:]
                cx.mT = {}; cx.mW = {}
                for s_ in (2, 4, 8, 16, 32, 64):
                    for nm_, dct, cn in (("mT", cx.mT, ("MOFF%d" if d == 0 else "MOFFT%d") % s_),
                                         ("mW", cx.mW, ("MOFFT%d" if d == 0 else "MOFF%d") % s_)):
                        mt_ = P.sbuf(pf + nm_ + str(s_), [128, 4, 128], mybir.dt.uint8)
                        kb.CP(mt_[:], kb.C(cn).unsqueeze(1).to_broadcast([128, 4, 128]), [], [mt_])
                        dct[s_] = mt_
                cxs.append(cx)

            def step(cx, n):
                d = cx.d
                Pa, Pb, Pc, Pd = cx.B
                cols = slice(n * 128, (n + 1) * 128)
                b = cx.it % 2; cx.it += 1
                qn, kn, vv, gab = cx.qnb[b], cx.knb[b], cx.vvb[b], cx.gabb[b]
                xa, ea, loga, beta, lnb = cx.xa, cx.ea, cx.loga, cx.beta, cx.lnb
                gtm, ngt, ekr, cdec, eg, beg, gpl = cx.gtm, cx.ngt, cx.ekr, cx.cdec, cx.eg, cx.beg, cx.gpl
                ROWS, LI, LBT, LBm, NAT, NA, QKm = cx.ROWS, cx.LI, cx.LBT, cx.LBm, cx.NAT, cx.NA, cx.QKm
                Tm, Wm, x1, y1, tmx, tmy = cx.Tm, cx.Wm, cx.x1, cx.y1, cx.tmx, cx.tmy
                Rm, khp, vnp, upair, wTp, EG, qd, cdp, Sb = cx.Rm, cx.khp, cx.vnp, cx.upair, cx.wTp, cx.EG, cx.qd, cx.cdp, cx.Sb
                tri = cx.tri
                kb.LD(qn[:], kb.QKVF[0:256, cols].rearrange("(hp p) t -> p hp t", p=128), [qn])
                kb.LD(kn[:], kb.QKVF[256:512, cols].rearrange("(hp p) t -> p hp t", p=128), [kn])
                kb.LD(vv[:], kb.QKVF[512:768, cols].rearrange("(hp p) t -> p hp t", p=128), [vv])
                kb.LD(gab[:], kb.ZT[cols, 512:528], [gab])
                knp = cx.knp[b]; kq16 = cx.kq16[b]
                for h in range(4):
                    kb.LD(knp[h][64 * (h % 2):64 * (h % 2) + 64, :], kb.QKVF[256 + 64 * h:256 + 64 * h + 64, cols], [knp[h]], q="pool")
                kb.LD(kq16[:, 0, :, :], kb.QKVF[256:512, cols].rearrange("(hp p) t -> p hp t", p=128), [kq16], q="pool")
                kb.LD(kq16[:, 1, :, :], kb.QKVF[0:256, cols].rearrange("(hp p) t -> p hp t", p=128), [kq16], q="pool")
                kb.TT(xa[:], gab[:, 4 * d:4 * d + 4], DTB[:, 4 * d:4 * d + 4], ALU.add, [gab, DTB], [xa])
                kb.ACT(ea[:], xa[:], AF.Exp, [xa], [ea])
                kb.ACT(ea[:], ea[:], AF.Ln, [ea], [ea], bias=kb.C("CCOL")[:, 1:2])
                kb.TT(loga[:], ea[:], NEGA[:, 4 * d:4 * d + 4], ALU.mult, [ea, NEGA], [loga])
                kb.ACT(beta[:], gab[:, 8 + 4 * d:12 + 4 * d], AF.Sigmoid, [gab], [beta])
                kb.ACT(lnb[:], beta[:], AF.Ln, [beta], [lnb])
                kb.MM(Pc[:, 0:4], tri, loga[:], True, True, [loga], [Pc])
                kb.MM(Pc[:, 4:8], cx.rem, loga[:], True, True, [loga], [Pc])
                kb.MM(Pc[:, 8:12], kb.C("ONES"), loga[:], True, True, [loga], [Pc])
                kb.CP(gtm[:], Pc[:, 0:4], [Pc], [gtm])
                kb.TS(ngt[:], Pc[:, 0:4], -1.0, None, ALU.mult, None, [Pc], [ngt])
                kb.ACT(ekr[:], Pc[:, 4:8], AF.Exp, [Pc], [ekr])
                kb.ACT(cdec[:], Pc[:, 8:12], AF.Exp, [Pc], [cdec])
                kb.ACT(eg[:], gtm[:], AF.Exp, [gtm], [eg])
                kb.TT(beg[:], beta[:], eg[:], ALU.mult, [beta, eg], [beg])
                kb.TT(gpl[:], gtm[:], lnb[:], ALU.add, [gtm, lnb], [gpl])
                kb.MM(Pd[0:4, 0:128], loga[:], tri, True, True, [loga], [Pd])
                kb.MM(Pd[0:4, 128:256], loga[:], tri, True, False, [loga], [Pd])
                kb.MM(Pd[0:4, 128:256], lnb[:], ident, False, True, [lnb], [Pd])
                kb.CP(ROWS[:, 0:256], Pd[0:4, 0:256], [Pd], [ROWS])
                kb.TS(ROWS[:, 256:384], Pd[0:4, 0:128], -1.0, None, ALU.mult, None, [Pd], [ROWS])
                yield
                kb.TT(cx.ROWSX[:], ROWS[:].rearrange("c (r i) -> c r i", r=3).unsqueeze(2).to_broadcast([4, 3, 4, 128]),
                      cx.bd4.unsqueeze(1).to_broadcast([4, 3, 4, 128]), ALU.mult, [ROWS], [cx.ROWSX])
                for (dst, ri, negm4, bias_t, bank) in ((LI, 0, cx.n_incl4, ngt, Pa), (LBT, 1, cx.n_strT4, ngt, Pb),
                                                       (LBm, 2, cx.n_str4, gpl, Pa)):
                    kb.MM(bank[:], kb.C("ONES")[0:4, :], cx.ROWSX[:, ri, :, :].rearrange("c h i -> c (h i)"), True, False,
                          [cx.ROWSX], [bank])
                    kb.MM(bank[:], cx.id16[:], negm4[:], False, True, [], [bank])
                    yield
                    for h in range(4):
                        kb.ACT(dst[:, h, :], bank[:, h * 128:(h + 1) * 128], AF.Exp, [bank, bias_t], [dst], bias=bias_t[:, h:h + 1])
                    yield
                for h in range(4):
                    hp, h2 = divmod(h, 2)
                    kb.MM(Pa[:, h * 128:(h + 1) * 128], knp[h][:], kq16[:, 0, hp, :], True, True, [knp[h], kq16], [Pa])
                    kb.MM(Pb[:, h * 128:(h + 1) * 128], knp[h][:], kq16[:, 1, hp, :], True, True, [knp[h], kq16], [Pb])
                pav = Pa[:].rearrange("p (h t) -> p h t", h=4)
                pbv = Pb[:].rearrange("p (h t) -> p h t", h=4)
                kb.STT(NAT[:], pav, -1.0, LBT[:], ALU.mult, ALU.mult, [Pa, LBT], [NAT])
                kb.STT(NA[:], pav, -1.0, LBm[:], ALU.mult, ALU.mult, [Pa, LBm], [NA])
                kb.TT(QKm[:], pbv, LI[:], ALU.mult, [Pb, LI], [QKm])
                yield
                mT = kb.C("MOFF1" if d == 0 else "MOFFT1").unsqueeze(1).to_broadcast([128, 4, 128])
                mW = kb.C("MOFFT1" if d == 0 else "MOFF1").unsqueeze(1).to_broadcast([128, 4, 128])
                kb.TT(tmx[:], NA[:], mT, ALU.mult, [NA], [tmx])
                kb.TT(tmy[:], NAT[:], mW, ALU.mult, [NAT], [tmy], eng="pool")
                kb.TT(Tm[:], tmx[:], idb, ALU.add, [tmx], [Tm])
                kb.TT(Wm[:], tmy[:], idb, ALU.add, [tmy], [Wm], eng="pool")
                yield
                for s_ in (2, 4, 8, 16, 32, 64):
                    for h in range(4):
                        kb.MM(Pa[:, h * 128:(h + 1) * 128], NAT[:, h, :], Tm[:, h, :], True, True, [NAT, Tm], [Pa])
                    for h in range(4):
                        kb.MM(Pb[:, h * 128:(h + 1) * 128], NA[:, h, :], Wm[:, h, :], True, True, [NA, Wm], [Pb])
                    yield
                    kb.CP(x1[:], pav, [Pa], [x1], eng="act")
                    kb.CP(y1[:], pbv, [Pb], [y1], eng="act")
                    yield
                    for h in range(4):
                        kb.MM(Pa[:, h * 128:(h + 1) * 128], Wm[:, h, :], x1[:, h, :], True, True, [Wm, x1], [Pa])
                    for h in range(4):
                        kb.MM(Pb[:, h * 128:(h + 1) * 128], Tm[:, h, :], y1[:, h, :], True, True, [Tm, y1], [Pb])
                    yield
                    kb.CPRED(Tm[:], cx.mT[s_][:], pav, [Pa, cx.mT[s_]], [Tm])
                    kb.CPRED(Wm[:], cx.mW[s_][:], pbv, [Pb, cx.mW[s_]], [Wm])
                    yield
                for hp in range(2):
                    kb.TR(Pc[:, 128:256], kn[:, hp, :], ident, [kn], [Pc])
                    kb.TR(Pc[:, 256:384], vv[:, hp, :], ident, [vv], [Pc])
                    for h2 in range(2):
                        h = 2 * hp + h2
                        kc = slice(64 * h2, 64 * h2 + 64)
                        vc = slice(64 * (1 - h2), 64 * (1 - h2) + 64)
                        kb.TS(Rm[h][:, kc], Pc[:, 128 + 64 * h2:128 + 64 * h2 + 64], beg[:, h:h + 1], None, ALU.mult, None,
                              [Pc, beg], [Rm[h]])
                        kb.ACT(Rm[h][:, vc], Pc[:, 256 + 64 * h2:256 + 64 * h2 + 64], AF.Copy, [Pc, beta], [Rm[h]],
                               scale=beta[:, h:h + 1])
                        kb.ACT(khp[h][:, kc], Pc[:, 128 + 64 * h2:128 + 64 * h2 + 64], AF.Copy, [Pc, ekr], [khp[h]],
                               scale=ekr[:, h:h + 1])
                    yield
                for h in range(4):
                    kb.MM(Pa[:, h * 128:(h + 1) * 128], Wm[:, h, :], Rm[h][:], True, True, [Wm, Rm[h]], [Pa])
                    kb.MM(Pb[:, h * 128:(h + 1) * 128], Rm[h][:], Wm[:, h, :], True, True, [Wm, Rm[h]], [Pb])
                for h in range(4):
                    hp, h2 = divmod(h, 2)
                    vc0 = 64 * (1 - h2)
                    kb.CP(upair[hp][:, 64 * h2:64 * h2 + 64], Pa[:, h * 128 + vc0:h * 128 + vc0 + 64], [Pa], [upair[hp]], eng="dve")
                    kb.CP(wTp[hp][64 * h2:64 * h2 + 64, :], Pb[64 * h2:64 * h2 + 64, h * 128:(h + 1) * 128], [Pb], [wTp[hp]], eng="act")
                yield
                for hp in range(2):
                    kb.MM(Pc[:, 384:512], kb.C("SELP%d" % hp)[0:4, :], ROWS[:, 0:128], True, True, [ROWS], [Pc])
                    kb.ACT(EG[hp][:], Pc[:, 384:512], AF.Exp, [Pc], [EG[hp]])
                    kb.TT(qd[hp][:], qn[:, hp, :], EG[hp][:], ALU.mult, [qn, EG[hp]], [qd[hp]], eng="pool")
                    pws = Pc[:, hp * 128:(hp + 1) * 128]
                    kb.MM(pws, wTp[hp][:], Sb[hp][:], True, True, [wTp[hp], Sb[hp]], [Pc])
                    for h2 in range(2):
                        h = 2 * hp + h2
                        cs_ = slice(64 * h2, 64 * h2 + 64)
                        kb.TT(vnp[h][:, cs_], upair[hp][:, cs_], Pc[:, hp * 128 + 64 * h2:hp * 128 + 64 * h2 + 64],
                              ALU.subtract, [upair[hp], Pc], [vnp[h]])
                    po = Pd[:, hp * 256:hp * 256 + 128]
                    kb.MM(po, Sb[hp][:], qd[hp][:], True, False, [Sb[hp], qd[hp]], [Pd])
                    kb.MM(po, vnp[2 * hp][:], QKm[:, 2 * hp, :], False, False, [vnp[2 * hp], QKm], [Pd])
                    kb.MM(po, vnp[2 * hp + 1][:], QKm[:, 2 * hp + 1, :], False, True, [vnp[2 * hp + 1], QKm], [Pd])
                    kb.TT(OACC[:, hp, cols], OACC[:, hp, cols], po, ALU.add, [Pd], [OACC.s(n)])
                    pkv = Pd[:, hp * 256 + 128:hp * 256 + 256]
                    kb.MM(pkv, khp[2 * hp][:], vnp[2 * hp][:], True, False, [khp[2 * hp], vnp[2 * hp]], [Pd])
                    kb.MM(pkv, khp[2 * hp + 1][:], vnp[2 * hp + 1][:], False, True, [khp[2 * hp + 1], vnp[2 * hp + 1]], [Pd])
                    kb.CP(cdp[hp][0:64, :], cdec[0:64, 2 * hp:2 * hp + 1], [cdec], [cdp[hp]])
                    kb.CP(cdp[hp][64:128, :], cdec[64:128, 2 * hp + 1:2 * hp + 2], [cdec], [cdp[hp]])
                    kb.STT(Sb[hp][:], Sb[hp][:], cdp[hp][:, 0:1], pkv, ALU.mult, ALU.add, [Sb[hp], cdp[hp], Pd], [Sb[hp]])
                    yield

            def stream(cx):
                for n in ORDER[cx.d][:kb.cfg.get("ntiles", NT)]:
                    yield from step(cx, n)
            active = [stream(cxs[0]), stream(cxs[1])]
            while active:
                for g_ in list(active):
                    try:
                        next(g_)
                    except StopIteration:
                        active.remove(g_)
        with P.scope():
            G = P.sbuf("g_G2", [128, 1])
            for hh in range(2):
                kb.LD(G[64 * hh:64 * hh + 64, :], kb.prm["gdn_norm_g"][l].rearrange("(p o) -> p o", o=1), [G])
            finalize_gated(kb, OACC, Z_GG, G, 512, "g_")


def run_interleaved(gens):
    active = list(gens)
    while active:
        for g_ in list(active):
            try:
                next(g_)
            except StopIteration:
                active.remove(g_)


def oacc_add(kb, OACC, hp, n, ps):
    cols = slice(n * 128, (n + 1) * 128)
    kb.TT(OACC[:, hp, cols], OACC[:, hp, cols], ps[:], ALU.add, [ps], [OACC.s(n)])


def oacc_zero(kb, OACC):
    for hp in range(2):
        kb.MS(OACC[:, hp, :], 0.0, [OACC.s(n) for n in range(NT)], eng="pool")


def mixer_hgrn2(kb, l):
    P = kb.P
    with P.scope():
        OACC = P.sbuf("h_oacc", [128, 2, S])
        oacc_zero(kb, OACC)
        with P.scope():
            LB = P.sbuf("h_LB", [128, 4]); OML = P.sbuf("h_OML", [128, 4])
            if l == 0:
                kb.MS(LB[:], 0.0, [LB]); kb.MS(OML[:], 1.0, [OML])
            else:
                lgt = P.sbuf("h_lgt", [128, 8])
                kb.LD(lgt[:], kb.prm["hgrn_lb_logits"][:].rearrange("l d (hp p) -> p (l d hp)", p=128), [lgt],
                      allow_slow_non_contiguous=True)
                kb.TT(LB[:], lgt[:, 4:8], lgt[:, 0:4], ALU.subtract, [lgt], [LB])
                kb.ACT(LB[:], LB[:], AF.Sigmoid, [LB], [LB])
                kb.TS(OML[:], LB[:], -1.0, 1.0, ALU.mult, ALU.add, [LB], [OML])

            def make(d):
                pf = "h%d_" % d
                hqb = [P.sbuf(pf + "q%d" % i, [128, 2, 128]) for i in range(2)]
                hfb = [P.sbuf(pf + "f%d" % i, [128, 2, 128]) for i in range(2)]
                Vp = [[P.sbuf(pf + "vp%d%d" % (i, h), [128, 128], BF16) for h in range(4)] for i in range(2)]
                khp = [[P.sbuf(pf + "kh%d%d" % (i, h), [128, 128], BF16) for h in range(4)] for i in range(2)]
                Qlp = [[P.sbuf(pf + "qlp%d%d" % (i, h2), [128, 128], BF16) for h2 in range(2)] for i in range(2)]
                for i in range(2):
                    for h2 in range(2):
                        kb.MS(Qlp[i][h2][:], 0.0, [Qlp[i][h2]], eng="pool")
                for i in range(2):
                    for h in range(4):
                        kb.MS(Vp[i][h][:], 0.0, [Vp[i][h]], eng="pool")
                        kb.MS(khp[i][h][:], 0.0, [khp[i][h]], eng="pool")
                MREF = [P.sbuf(pf + "mr%d" % i, [128, 4]) for i in range(2)]
                for i in range(2):
                    kb.MS(MREF[i][:], 0.0, [MREF[i]])

                def two(name, shape=(128, 128)):
                    return [P.sbuf(pf + "%s%d" % (name, i), list(shape)) for i in range(2)]
                qs, sgm, ff, logf, kk, bb, pre = two("qs"), two("sg"), two("ff"), two("lf"), two("kk"), two("bb"), two("pre")
                e1, Ql, e2, Qd = two("e1"), two("Ql"), two("e2"), two("Qd")
                Kt = [[P.sbuf(pf + "Kt%d%d" % (r, i), [128, 128], BF16) for i in range(2)] for r in range(4)]
                ex = two("ex")
                AT = [P.sbuf(pf + "AT%d" % i, [128, 2, 128], BF16) for i in range(2)]
                KhT = two("KhT")
                bend = two("bend", (128, 2))
                Sb = [P.sbuf(pf + "S%d" % hp, [128, 128]) for hp in range(2)]
                for hp in range(2):
                    kb.MS(Sb[hp][:], 0.0, [Sb[hp]])
                pss = P.psum(pf + "pss", [128, 2, 128])
                po = P.psum(pf + "pso", [128, 128])
                pk = P.psum(pf + "psk", [128, 128])
                pkv = P.psum(pf + "pskv", [128, 128])
                zf = Z_HFF if d == 0 else Z_HFB
                tri = kb.C("TRIF" if d == 0 else "TRIB").unsqueeze(1).to_broadcast([128, 2, 128])

                def gen():
                    it = 0
                    jj = 0
                    for n in ORDER[d]:
                        cols = slice(n * 128, (n + 1) * 128)
                        b = it % 2; it += 1
                        hq, hf = hqb[b], hfb[b]
                        kb.LD(hq[:], kb.ZF[Z_HQ:Z_HQ + 256, cols].rearrange("(hp p) t -> p hp t", p=128), [hq])
                        kb.LD(hf[:], kb.ZF[zf:zf + 256, cols].rearrange("(hp p) t -> p hp t", p=128), [hf])
                        for h in range(4):
                            kb.LD(Vp[b][h][:, 64 * (h % 2):64 * (h % 2) + 64], kb.ZT[cols, 64 * h:64 * h + 64], [Vp[b][h]], q="pool")
                        yield
                        for hp in range(2):
                            j = jj % 2; jj += 1
                            c = 2 * d + hp
                            mref = MREF[j]
                            kb.ACT(qs[j][:], hq[:, hp, :], AF.Exp, [hq], [qs[j]], scale=-1.0)
                            kb.TS(qs[j][:], qs[j][:], 1.0, None, ALU.add, None, [qs[j]], [qs[j]])
                            kb.RECIP(qs[j][:], qs[j][:], [qs[j]], [qs[j]])
                            kb.TT(qs[j][:], qs[j][:], hq[:, hp, :], ALU.mult, [qs[j], hq], [qs[j]], eng="pool")
                            kb.ACT(sgm[j][:], hf[:, hp, :], AF.Exp, [hf], [sgm[j]], scale=-1.0)
                            kb.TS(sgm[j][:], sgm[j][:], 1.0, None, ALU.add, None, [sgm[j]], [sgm[j]])
                            kb.RECIP(sgm[j][:], sgm[j][:], [sgm[j]], [sgm[j]])
                            kb.TS(ff[j][:], sgm[j][:], OML[:, c:c + 1], LB[:, c:c + 1], ALU.mult, ALU.add, [sgm[j], OML, LB], [ff[j]])
                            kb.ACT(logf[j][:], ff[j][:], AF.Ln, [ff[j]], [logf[j]])
                            kb.TS(kk[j][:], ff[j][:], -1.0, 1.0, ALU.mult, ALU.add, [ff[j]], [kk[j]], eng="pool")
                            yield
                            B = bb[j]
                            if d == 0:
                                kb.SCAN(B[:], kb.C("ONES"), logf[j][:], [logf[j]], [B])
                                kb.CP(mref[:, 1:4], B[:].rearrange("p (r c) -> p r c", c=32)[:, 0:3, 31], [B], [mref])
                                be = B[:, 127:128]
                            else:
                                kb.SCAN(pre[j][:], kb.C("ONES"), logf[j][:], [logf[j]], [pre[j]])
                                kb.STT(B[:], pre[j][:], -1.0, logf[j][:], ALU.mult, ALU.add, [pre[j], logf[j]], [B])
                                kb.TS(B[:], B[:], pre[j][:, 127:128], None, ALU.add, None, [B, pre[j]], [B])
                                kb.CP(mref[:, 0:3], B[:].rearrange("p (r c) -> p r c", c=32)[:, 1:4, 0], [B], [mref])
                                be = B[:, 0:1]
                            yield
                            kb.TT(e1[j][:].rearrange("p (r c) -> p r c", c=32), B[:].rearrange("p (r c) -> p r c", c=32),
                                  mref[:].unsqueeze(2).to_broadcast([128, 4, 32]), ALU.subtract, [B, mref], [e1[j]])
                            kb.ACT(e1[j][:], e1[j][:], AF.Exp, [e1[j]], [e1[j]])
                            for h2 in range(2):
                                rs_ = slice(64 * h2, 64 * h2 + 64)
                                kb.STT(Qlp[j][h2][rs_, :], qs[j][rs_, :], 0.125, e1[j][rs_, :], ALU.mult, ALU.mult,
                                       [qs[j], e1[j]], [Qlp[j][h2]])
                            kb.ACT(e2[j][:], B[:], AF.Exp, [B], [e2[j]])
                            kb.STT(Qd[j][:], qs[j][:], 0.125, e2[j][:], ALU.mult, ALU.mult, [qs[j], e2[j]], [Qd[j]])
                            yield
                            for r in range(4):
                                kb.ACT(ex[j][:], B[:], AF.Exp, [B, mref], [ex[j]], scale=-1.0, bias=mref[:, r:r + 1])
                                kb.STT(Kt[r][j][:], ex[j][:], 1e26, kk[j][:], ALU.min, ALU.mult, [ex[j], kk[j]], [Kt[r][j]])
                                for h2 in range(2):
                                    kb.MM(pss[:, h2, 32 * r:32 * r + 32], Kt[r][j][:],
                                          Qlp[j][h2][:, 32 * r:32 * r + 32], True, True,
                                          [Kt[r][j], Qlp[j][h2]], [pss])
                                yield
                            kb.TT(AT[j][:], pss[:], tri, ALU.mult, [pss], [AT[j]])
                            yield
                            kb.MM(po[:], Vp[b][2 * hp][:], AT[j][:, 0, :], True, False, [Vp[b][2 * hp], AT[j]], [po])
                            kb.MM(po[:], Vp[b][2 * hp + 1][:], AT[j][:, 1, :], False, False, [Vp[b][2 * hp + 1], AT[j]], [po])
                            kb.MM(po[:], Sb[hp][:], Qd[j][:], False, True, [Sb[hp], Qd[j]], [po])
                            oacc_add(kb, OACC, hp, n, po)
                            kb.CP(bend[j][:, 0:1], be, [B], [bend[j]])
                            kb.ACT(KhT[j][:], B[:], AF.Exp, [B, bend[j]], [KhT[j]], scale=-1.0, bias=bend[j][:, 0:1])
                            kb.TT(KhT[j][:], KhT[j][:], kk[j][:], ALU.mult, [KhT[j], kk[j]], [KhT[j]], eng="pool")
                            kb.ACT(bend[j][:, 1:2], bend[j][:, 0:1], AF.Exp, [bend[j]], [bend[j]])
                            yield
                            kb.TR(pk[:], KhT[j][:], kb.C("IDENT"), [KhT[j]], [pk])
                            for h2 in range(2):
                                h = 2 * hp + h2
                                kb.CP(khp[b][h][:, 64 * h2:64 * h2 + 64], pk[:, 64 * h2:64 * h2 + 64], [pk], [khp[b][h]],
                                      eng=("act" if h2 else "dve"))
                            yield
                            kb.MM(pkv[:], khp[b][2 * hp][:], Vp[b][2 * hp][:], True, False, [khp[b][2 * hp], Vp[b][2 * hp]], [pkv])
                            kb.MM(pkv[:], khp[b][2 * hp + 1][:], Vp[b][2 * hp + 1][:], False, True,
                                  [khp[b][2 * hp + 1], Vp[b][2 * hp + 1]], [pkv])
                            kb.STT(Sb[hp][:], Sb[hp][:], bend[j][:, 1:2], pkv[:], ALU.mult, ALU.add,
                                   [Sb[hp], bend[j], pkv], [Sb[hp]])
                            yield
                return gen()
            run_interleaved([make(0), make(1)])
        with P.scope():
            G = P.sbuf("h_G2", [128, 1])
            for hh in range(2):
                kb.LD(G[64 * hh:64 * hh + 64, :], kb.prm["hgrn_norm_g"][l].rearrange("(p o) -> p o", o=1), [G])
            finalize_gated(kb, OACC, Z_HG, G, 0, "h_")


def mixer_ret2(kb, l):
    P = kb.P
    with P.scope():
        OACC = P.sbuf("r_oacc", [128, 2, S])
        oacc_zero(kb, OACC)
        with P.scope():
            lgt = P.sbuf("r_lgt", [128, 8])
            kb.LD(lgt[:], kb.prm["ret_decay_logit"][l].rearrange("d h -> (d h)").partition_broadcast(128), [lgt])
            LG = P.sbuf("r_LG", [128, 8])
            kb.ACT(LG[:], lgt[:], AF.Sigmoid, [lgt], [LG])
            kb.ACT(LG[:], LG[:], AF.Ln, [LG], [LG])
            LGP = P.sbuf("r_LGP", [128, 4])
            for d in range(2):
                for hp in range(2):
                    c = 2 * d + hp
                    kb.CP(LGP[0:64, c:c + 1], LG[0:64, 4 * d + 2 * hp:4 * d + 2 * hp + 1], [LG], [LGP])
                    kb.CP(LGP[64:128, c:c + 1], LG[64:128, 4 * d + 2 * hp + 1:4 * d + 2 * hp + 2], [LG], [LGP])
            MK = [P.sbuf("r_MK%d" % d, [128, 4, 128]) for d in range(2)]
            QDEC = [[P.sbuf("r_QD%d%d" % (d, hp), [128, 128]) for hp in range(2)] for d in range(2)]
            etmp = P.sbuf("r_etmp", [128, 128])
            for d in range(2):
                for h in range(4):
                    kb.ACT(etmp[:], kb.C("DIFF" if d == 0 else "NDIFF"), AF.Exp, [LG], [etmp],
                           scale=LG[:, 4 * d + h:4 * d + h + 1])
                    kb.STT(MK[d][:, h, :], etmp[:], 0.125, kb.C("TRIF" if d == 0 else "TRIB"), ALU.mult, ALU.mult,
                           [etmp], [MK[d]])
                for hp in range(2):
                    kb.ACT(QDEC[d][hp][:], kb.C("IOTAF1" if d == 0 else "RIOTAF"), AF.Exp, [LGP], [QDEC[d][hp]],
                           scale=LGP[:, 2 * d + hp:2 * d + hp + 1])
            KD = P.sbuf("r_KD", [128, 8])
            kb.ACT(KD[:, 0:4], LG[:, 0:4], AF.Exp, [LG], [KD], scale=kb.C("CCOL")[:, 3:4])
            kb.ACT(KD[:, 4:8], LG[:, 4:8], AF.Exp, [LG], [KD], scale=kb.C("CCOL")[:, 2:3])
            kb.TS(KD[:], KD[:], 0.125, None, ALU.mult, None, [KD], [KD])
            CV = P.sbuf("r_CV", [128, 4])
            kb.ACT(CV[:], LGP[:], AF.Exp, [LGP], [CV], scale=128.0)

            def make(d):
                pf = "r%d_" % d
                qTb = [P.sbuf(pf + "q%d" % i, [128, 2, 128]) for i in range(2)]
                kTb = [P.sbuf(pf + "k%d" % i, [128, 2, 128]) for i in range(2)]
                csb = [P.sbuf(pf + "cs%d" % i, [128, 2, 128]) for i in range(2)]
                Vp = [[P.sbuf(pf + "vp%d%d" % (i, h), [128, 128], BF16) for h in range(4)] for i in range(2)]
                khp = [[P.sbuf(pf + "kh%d%d" % (i, h), [128, 128], BF16) for h in range(4)] for i in range(2)]
                qrp = [[P.sbuf(pf + "qrp%d%d" % (i, h2), [128, 128], BF16) for h2 in range(2)] for i in range(2)]
                for i in range(2):
                    for h2 in range(2):
                        kb.MS(qrp[i][h2][:], 0.0, [qrp[i][h2]], eng="pool")
                kr16 = [P.sbuf(pf + "kr16%d" % i, [128, 128], BF16) for i in range(2)]
                for i in range(2):
                    for h in range(4):
                        kb.MS(Vp[i][h][:], 0.0, [Vp[i][h]], eng="pool")
                        kb.MS(khp[i][h][:], 0.0, [khp[i][h]], eng="pool")
                t1 = [P.sbuf(pf + "t1%d" % i, [128, 128]) for i in range(2)]
                t2 = [P.sbuf(pf + "t2%d" % i, [128, 128]) for i in range(2)]
                qr = [P.sbuf(pf + "qr%d" % i, [128, 2, 128]) for i in range(2)]
                kr = [P.sbuf(pf + "kr%d" % i, [128, 2, 128]) for i in range(2)]
                AT = [P.sbuf(pf + "AT%d" % i, [128, 2, 128], BF16) for i in range(2)]
                qd = [P.sbuf(pf + "qd%d" % i, [128, 128]) for i in range(2)]
                Sb = [P.sbuf(pf + "S%d" % hp, [128, 128]) for hp in range(2)]
                for hp in range(2):
                    kb.MS(Sb[hp][:], 0.0, [Sb[hp]])
                pr = P.psum(pf + "psr", [128, 256])
                pss = P.psum(pf + "pss", [128, 2, 128])
                po = P.psum(pf + "pso", [128, 128])
                pkk = P.psum(pf + "pskk", [128, 256])

                def gen():
                    it = 0
                    jj = 0
                    for n in ORDER[d]:
                        cols = slice(n * 128, (n + 1) * 128)
                        b = it % 2; it += 1
                        qT, kT, cs = qTb[b], kTb[b], csb[b]
                        kb.LD(qT[:], kb.ZF[Z_RQ:Z_RQ + 256, cols].rearrange("(hp p) t -> p hp t", p=128), [qT])
                        kb.LD(kT[:], kb.ZF[Z_RK:Z_RK + 256, cols].rearrange("(hp p) t -> p hp t", p=128), [kT])
                        kb.LD(cs[:, 0, :], kb.ropec[:, cols], [cs])
                        kb.LD(cs[:, 1, :], kb.ropes[:, cols], [cs])
                        for h in range(4):
                            kb.LD(Vp[b][h][:, 64 * (h % 2):64 * (h % 2) + 64], kb.ZT[cols, 256 + 64 * h:256 + 64 * h + 64],
                                  [Vp[b][h]], q="pool")
                        yield
                        for hp in range(2):
                            j = jj % 2; jj += 1
                            kb.MM(pr[:, 0:128], kb.C("ROT"), qT[:, hp, :], True, True, [qT], [pr])
                            kb.MM(pr[:, 128:256], kb.C("ROT"), kT[:, hp, :], True, True, [kT], [pr])
                            yield
                            for (src_, dst, off) in ((qT, qr[b], 0), (kT, kr[b], 128)):
                                kb.TT(t1[j][:], src_[:, hp, :], cs[:, 0, :], ALU.mult, [src_, cs], [t1[j]], eng="pool")
                                kb.TT(t2[j][:], pr[:, off:off + 128], cs[:, 1, :], ALU.mult, [pr, cs], [t2[j]])
                                kb.TT(dst[:, hp, :], t1[j][:], t2[j][:], ALU.add, [t1[j], t2[j]], [dst.s(hp)], eng="pool")
                                yield
                            kb.CP(kr16[j][:], kr[b][:, hp, :], [kr[b].s(hp)], [kr16[j]], eng="act")
                            for h2 in range(2):
                                rs_ = slice(64 * h2, 64 * h2 + 64)
                                kb.CP(qrp[j][h2][rs_, :], qr[b][rs_, hp, :], [qr[b].s(hp)], [qrp[j][h2]], eng="act")
                            yield
                            for h2 in range(2):
                                kb.MM(pss[:, h2, :], kr16[j][:], qrp[j][h2][:], True, True, [kr16[j], qrp[j][h2]], [pss])
                            yield
                            kb.TT(AT[j][:], pss[:], MK[d][:, 2 * hp:2 * hp + 2, :], ALU.mult, [pss, MK[d]], [AT[j]])
                            kb.TT(qd[j][:], qr[b][:, hp, :], QDEC[d][hp][:], ALU.mult, [qr[b].s(hp), QDEC[d][hp]], [qd[j]],
                                  eng="pool")
                            yield
                            kb.MM(po[:], Vp[b][2 * hp][:], AT[j][:, 0, :], True, False, [Vp[b][2 * hp], AT[j]], [po])
                            kb.MM(po[:], Vp[b][2 * hp + 1][:], AT[j][:, 1, :], False, False, [Vp[b][2 * hp + 1], AT[j]], [po])
                            kb.MM(po[:], Sb[hp][:], qd[j][:], False, True, [Sb[hp], qd[j]], [po])
                            kb.TR(pkk[:, 0:128], kr[b][:, hp, :], kb.C("IDENT"), [kr[b].s(hp)], [pkk])
                            yield
                            oacc_add(kb, OACC, hp, n, po)
                            for h2 in range(2):
                                h = 2 * hp + h2
                                kb.ACT(khp[b][h][:, 64 * h2:64 * h2 + 64], pkk[:, 64 * h2:64 * h2 + 64], AF.Copy,
                                       [pkk, KD], [khp[b][h]], scale=KD[:, 4 * d + h:4 * d + h + 1])
                            yield
                            kb.MM(pkk[:, 128:256], khp[b][2 * hp][:], Vp[b][2 * hp][:], True, False,
                                  [khp[b][2 * hp], Vp[b][2 * hp]], [pkk])
                            kb.MM(pkk[:, 128:256], khp[b][2 * hp + 1][:], Vp[b][2 * hp + 1][:], False, True,
                                  [khp[b][2 * hp + 1], Vp[b][2 * hp + 1]], [pkk])
                            kb.STT(Sb[hp][:], Sb[hp][:], CV[:, 2 * d + hp:2 * d + hp + 1], pkk[:, 128:256], ALU.mult, ALU.add,
                                   [Sb[hp], CV, pkk], [Sb[hp]])
                            yield
                return gen()
            run_interleaved([make(0), make(1)])
        with P.scope():
            finalize_gated(kb, OACC, Z_RG, None, 256, "r_")


def s5_tables(kb, l, d, VFr, VFi, T1, T2, AR, NAI):
    P = kb.P
    prm = kb.prm
    with P.scope():
        lr = P.sbuf("s_lr", [128, 16, 64]); li = P.sbuf("s_li", [128, 16, 64]); dtb = P.sbuf("s_dt", [128, 16])
        kb.LD(lr[:], prm["s5_lam_re"][l][d].rearrange("g p -> (g p)").partition_broadcast(128), [lr])
        kb.LD(li[:], prm["s5_lam_im"][l][d].rearrange("g p -> (g p)").partition_broadcast(128), [li])
        kb.LD(dtb[:], prm["s5_log_dt"][l][d].partition_broadcast(128), [dtb])
        kb.ACT(dtb[:], dtb[:], AF.Exp, [dtb], [dtb])
        dt_bc = dtb[:].unsqueeze(2).to_broadcast([128, 16, 64])
        lrdt = P.sbuf("s_lrdt", [128, 16, 64]); lidt = P.sbuf("s_lidt", [128, 16, 64])
        kb.TT(lrdt[:], lr[:], dt_bc, ALU.mult, [lr, dtb], [lrdt])
        kb.TT(lidt[:], li[:], dt_bc, ALU.mult, [li, dtb], [lidt])
        a = [P.sbuf("s_a%d" % i, [128, 16, 64]) for i in range(8)]
        mag, ang, sn, cs, tmp, ar, ai, t2 = a
        kb.ACT(mag[:], lrdt[:], AF.Exp, [lrdt], [mag])
        _sincos(kb, lidt[:], sn[:], cs[:], [lidt, sn, cs, tmp], tmp[:])
        kb.TT(ar[:], mag[:], cs[:], ALU.mult, [mag, cs], [ar])
        kb.TT(ai[:], mag[:], sn[:], ALU.mult, [mag, sn], [ai])
        den = P.sbuf("s_den", [128, 16, 64]); fr = P.sbuf("s_fr", [128, 16, 64]); fi = P.sbuf("s_fi", [128, 16, 64])
        kb.TT(den[:], lr[:], lr[:], ALU.mult, [lr], [den])
        kb.TT(t2[:], li[:], li[:], ALU.mult, [li], [t2])
        kb.TT(den[:], den[:], t2[:], ALU.add, [den, t2], [den])
        kb.RECIP(den[:], den[:], [den], [den])
        kb.TS(ar[:], ar[:], -1.0, None, ALU.add, None, [ar], [ar])
        kb.TT(fr[:], ar[:], lr[:], ALU.mult, [ar, lr], [fr])
        kb.TT(t2[:], ai[:], li[:], ALU.mult, [ai, li], [t2])
        kb.TT(fr[:], fr[:], t2[:], ALU.add, [fr, t2], [fr])
        kb.TT(fr[:], fr[:], den[:], ALU.mult, [fr, den], [fr])
        kb.TT(fi[:], ai[:], lr[:], ALU.mult, [ai, lr], [fi])
        kb.TT(t2[:], ar[:], li[:], ALU.mult, [ar, li], [t2])
        kb.TT(fi[:], fi[:], t2[:], ALU.subtract, [fi, t2], [fi])
        kb.TT(fi[:], fi[:], den[:], ALU.mult, [fi, den], [fi])
        jcol = kb.C("CCOL")[:, 2:3] if d == 0 else kb.C("CCOL")[:, 3:4]
        njcol = kb.C("CCOL")[:, 6:7] if d == 0 else kb.C("CCOL")[:, 7:8]
        kb.ACT(mag[:], lrdt[:], AF.Exp, [lrdt], [mag], scale=njcol)
        kb.TS(ang[:], lidt[:], jcol, None, ALU.mult, None, [lidt], [ang])
        _sincos(kb, ang[:], sn[:], cs[:], [ang, sn, cs, tmp], tmp[:])
        vr, vi = ar, ai
        kb.TT(vr[:], mag[:], cs[:], ALU.mult, [mag, cs], [vr])
        kb.TT(vi[:], mag[:], sn[:], ALU.mult, [mag, sn], [vi])
        kb.TS(vi[:], vi[:], -1.0, None, ALU.mult, None, [vi], [vi])
        kb.TT(VFr[:], vr[:], fr[:], ALU.mult, [vr, fr], [VFr])
        kb.TT(t2[:], vi[:], fi[:], ALU.mult, [vi, fi], [t2])
        kb.TT(VFr[:], VFr[:], t2[:], ALU.subtract, [VFr, t2], [VFr])
        kb.TT(VFi[:], vr[:], fi[:], ALU.mult, [vr, fi], [VFi])
        kb.TT(t2[:], vi[:], fr[:], ALU.mult, [vi, fr], [t2])
        kb.TT(VFi[:], VFi[:], t2[:], ALU.add, [VFi, t2], [VFi])
    with P.scope():
        dtb = P.sbuf("s_dt2", [128, 16])
        kb.LD(dtb[:], prm["s5_log_dt"][l][d].partition_broadcast(128), [dtb])
        kb.ACT(dtb[:], dtb[:], AF.Exp, [dtb], [dtb])
        lrp = P.sbuf("s_lrp", [128, 16]); lip = P.sbuf("s_lip", [128, 16])
        for hh in range(2):
            kb.LD(lrp[64 * hh:64 * hh + 64, :], prm["s5_lam_re"][l][d].rearrange("g p -> p g"), [lrp],
                  allow_slow_non_contiguous=True)
            kb.LD(lip[64 * hh:64 * hh + 64, :], prm["s5_lam_im"][l][d].rearrange("g p -> p g"), [lip],
                  allow_slow_non_contiguous=True)
        kb.TT(lrp[:], lrp[:], dtb[:], ALU.mult, [lrp, dtb], [lrp])
        kb.TT(lip[:], lip[:], dtb[:], ALU.mult, [lip, dtb], [lip])
        b4 = [P.sbuf("s_b%d" % i, [128, 16, 128]) for i in range(4)]
        arg, sn2, cs2, tmp2 = b4
        mt = kb.C("IOTAF" if d == 0 else "R127F")
        mt_bc = mt.unsqueeze(1).to_broadcast([128, 16, 128])
        kb.TT(arg[:], lrp[:].unsqueeze(2).to_broadcast([128, 16, 128]), mt_bc, ALU.mult, [lrp], [arg])
        kb.ACT(T1[:], arg[:], AF.Exp, [arg], [T1])
        kb.TT(arg[:], lip[:].unsqueeze(2).to_broadcast([128, 16, 128]), mt_bc, ALU.mult, [lip, T1], [arg])
        _sincos(kb, arg[:], sn2[:], cs2[:], [arg, sn2, cs2, tmp2], tmp2[:])
        kb.TT(T2[:], T1[:], sn2[:], ALU.mult, [T1, sn2], [T2])
        kb.TS(T2[:], T2[:], -1.0, None, ALU.mult, None, [T2], [T2])
        kb.TT(T1[:], T1[:], cs2[:], ALU.mult, [T1, cs2], [T1])
        c4 = [P.sbuf("s_c%d" % i, [128, 16]) for i in range(4)]
        kb.ACT(c4[0][:], lrp[:], AF.Exp, [lrp], [c4[0]])
        _sincos(kb, lip[:], c4[1][:], c4[2][:], [lip, c4[1], c4[2], c4[3]], c4[3][:])
        kb.TT(AR[:], c4[0][:], c4[2][:], ALU.mult, [c4[0], c4[2]], [AR])
        kb.TT(NAI[:], c4[0][:], c4[1][:], ALU.mult, [c4[0], c4[1]], [NAI])
        kb.TS(NAI[:], NAI[:], -1.0, None, ALU.mult, None, [NAI], [NAI])


def mixer_s5_2(kb, l):
    P = kb.P
    prm = kb.prm
    with P.scope():
        WX = P.sbuf("s_WX", [128, 2, 8, 2, 64])
        Cblk = P.sbuf("s_Cblk", [128, 16, 128])
        kb.MS(WX[:], 0.0, [WX], eng="pool")
        kb.MS(Cblk[:], 0.0, [Cblk], eng="pool")
        for g8 in range(8):
            for ri, nm in enumerate(("s5_b_re", "s5_b_im")):
                for gg in range(2):
                    src = prm[nm][l][8 * gg + g8].rearrange("p c -> c p")
                    kb.LD(WX[16 * g8:16 * g8 + 16, gg, g8, ri, :], src, [WX], allow_slow_non_contiguous=True)
        for g in range(16):
            g8 = g % 8
            kb.LD(Cblk[0:64, g, 16 * g8:16 * g8 + 16], prm["s5_c_re"][l][g].rearrange("c p -> p c"), [Cblk],
                  allow_slow_non_contiguous=True)
            kb.LD(Cblk[64:128, g, 16 * g8:16 * g8 + 16], prm["s5_c_im"][l][g].rearrange("c p -> p c"), [Cblk],
                  allow_slow_non_contiguous=True)
        kb.TS(Cblk[64:128, :, :], Cblk[64:128, :, :], -1.0, None, ALU.mult, None, [Cblk], [Cblk])
        Cb16 = P.sbuf("s_Cb16", [128, 16, 128], BF16)
        kb.CP(Cb16[:], Cblk[:], [Cblk], [Cb16])

        tabs = []
        for d in range(2):
            VFr = P.sbuf("s_VFr%d" % d, [128, 16, 64]); VFi = P.sbuf("s_VFi%d" % d, [128, 16, 64])
            T1 = P.sbuf("s_T1%d" % d, [128, 16, 128]); T2 = P.sbuf("s_T2%d" % d, [128, 16, 128])
            AR = P.sbuf("s_AR%d" % d, [128, 16]); NAI = P.sbuf("s_NAI%d" % d, [128, 16])
            s5_tables(kb, l, d, VFr, VFi, T1, T2, AR, NAI)
            tabs.append((VFr, VFi, T1, T2, AR, NAI))
        OACC = P.sbuf("s_oacc", [128, 2, S])
        oacc_zero(kb, OACC)
        with P.scope():
            def make(d):
                pf = "s%d_" % d
                VFr, VFi, T1, T2, AR, NAI = tabs[d]
                uTb = [P.sbuf(pf + "u%d" % i, [128, 2, 128]) for i in range(2)]
                mm_ = [P.sbuf(pf + "m%d" % i, [128, 4, 64]) for i in range(4)]
                W3 = [P.sbuf(pf + "W3%d" % i, [128, 4, 3, 64], BF16) for i in range(2)]
                tP = P.sbuf(pf + "tP", [128, 4, 128]); tPs = P.sbuf(pf + "tPs", [128, 4, 128])
                H1 = P.sbuf(pf + "H1", [128, 4, 128]); H2 = P.sbuf(pf + "H2", [128, 4, 128])
                Hb = [P.sbuf(pf + "Hb%d" % i, [128, 4, 128], BF16) for i in range(2)]
                tri16 = P.sbuf(pf + "tri16", [128, 128], BF16)
                kb.CP(tri16[:], kb.C("TRIF" if d == 0 else "TRIB"), [], [tri16])
                hend = P.sbuf(pf + "hend", [128, 16]); hsend = P.sbuf(pf + "hsend", [128, 16])
                hp_ = P.sbuf(pf + "hp", [128, 16]); hps_ = P.sbuf(pf + "hps", [128, 16])
                sm = [P.sbuf(pf + "sm%d" % i, [128, 16]) for i in range(4)]
                kb.MS(hp_[:], 0.0, [hp_]); kb.MS(hps_[:], 0.0, [hps_])
                xps = P.psum(pf + "xps", [128, 512])
                pps = P.psum(pf + "pps", [128, 4, 128])
                ppss = P.psum(pf + "ppss", [128, 4, 128])
                yps = P.psum(pf + "yps", [128, 128])
                te = 127 if d == 0 else 0

                def gen():
                    it = 0
                    kq = 0
                    for n in ORDER[d]:
                        uT = uTb[it % 2]; it += 1
                        cols = slice(n * 128, (n + 1) * 128)
                        kb.LD(uT[:], kb.ZF[Z_SU:Z_SU + 256, cols].rearrange("(gg p) t -> p gg t", p=128), [uT])
                        yield
                        for q in range(4):
                            gg, qq = divmod(q, 2)
                            gs = slice(4 * q, 4 * q + 4)
                            w3 = W3[kq % 2]; hb = Hb[kq % 2]; kq += 1
                            kb.MM(xps[:], uT[:, gg, :], WX[:, gg, 4 * qq:4 * qq + 4, :, :].rearrange("q a r p -> q (a r p)"),
                                  True, True, [uT, WX], [xps])
                            xv = xps[:].rearrange("t (g r p) -> t g r p", r=2, p=64)
                            kb.TT(mm_[0][:], xv[:, :, 0, :], VFr[:, gs, :], ALU.mult, [xps, VFr], [mm_[0]])
                            kb.TT(mm_[1][:], xv[:, :, 1, :], VFi[:, gs, :], ALU.mult, [xps, VFi], [mm_[1]])
                            kb.TT(mm_[2][:], xv[:, :, 0, :], VFi[:, gs, :], ALU.mult, [xps, VFi], [mm_[2]])
                            kb.TT(mm_[3][:], xv[:, :, 1, :], VFr[:, gs, :], ALU.mult, [xps, VFr], [mm_[3]])
                            yield
                            kb.TT(w3[:, :, 0, :], mm_[0][:], mm_[1][:], ALU.subtract, [mm_[0], mm_[1]], [w3])
                            kb.TT(w3[:, :, 1, :], mm_[2][:], mm_[3][:], ALU.add, [mm_[2], mm_[3]], [w3], eng="pool")
                            kb.TT(w3[:, :, 2, :], mm_[1][:], mm_[0][:], ALU.subtract, [mm_[0], mm_[1]], [w3])
                            yield
                            for i in range(4):
                                kb.MM(pps[:, i, :], w3[:, i, 0:2, :].rearrange("q r p -> q (r p)"), tri16[:], True, True, [w3, tri16], [pps])
                            for i in range(4):
                                kb.MM(ppss[:, i, :], w3[:, i, 1:3, :].rearrange("q r p -> q (r p)"), tri16[:], True, True, [w3, tri16], [ppss])
                            yield
                            for i in range(4):
                                g = 4 * q + i
                                kb.ACT(tP[:, i, :], pps[:, i, :], AF.Identity, [pps, hp_], [tP], bias=hp_[:, g:g + 1])
                            for i in range(4):
                                g = 4 * q + i
                                kb.ACT(tPs[:, i, :], ppss[:, i, :], AF.Identity, [ppss, hps_], [tPs], bias=hps_[:, g:g + 1])
                            yield
                            kb.TT(sm[0][:, 0:4], tPs[:, :, te], T1[:, gs, te], ALU.mult, [tPs, T1], [sm[0]])
                            kb.TT(sm[1][:, 0:4], tP[:, :, te], T2[:, gs, te], ALU.mult, [tP, T2], [sm[1]])
                            kb.TT(hsend[:, gs], sm[0][:, 0:4], sm[1][:, 0:4], ALU.subtract, [sm[0], sm[1]], [hsend])
                            kb.TT(sm[2][:, 0:4], tP[:, :, te], T1[:, gs, te], ALU.mult, [tP, T1], [sm[2]])
                            kb.TT(sm[3][:, 0:4], tPs[:, :, te], T2[:, gs, te], ALU.mult, [tPs, T2], [sm[3]])
                            kb.TT(hend[:, gs], sm[2][:, 0:4], sm[3][:, 0:4], ALU.add, [sm[2], sm[3]], [hend])
                            yield
                            kb.TT(H1[:], tP[:], T1[:, gs, :], ALU.mult, [tP, T1], [H1], eng="pool")
                            kb.TT(H2[:], tPs[:], T2[:, gs, :], ALU.mult, [tPs, T2], [H2])
                            yield
                            kb.TT(hb[:], H1[:], H2[:], ALU.add, [H1, H2], [hb], eng="pool")
                            yield
                            for i in range(4):
                                g = 4 * q + i
                                kb.MM(yps[:], Cb16[:, g, :], hb[:, i, :], (g % 8) == 0, (g % 8) == 7, [Cb16, hb], [yps])
                            if qq == 1:
                                oacc_add(kb, OACC, gg, n, yps)
                            yield
                        kb.TT(sm[0][:], hend[:], AR[:], ALU.mult, [hend, AR], [sm[0]])
                        kb.TT(sm[1][:], hsend[:], NAI[:], ALU.mult, [hsend, NAI], [sm[1]])
                        kb.TT(sm[2][:], hsend[:], AR[:], ALU.mult, [hsend, AR], [sm[2]])
                        kb.TT(sm[3][:], hend[:], NAI[:], ALU.mult, [hend, NAI], [sm[3]])
                        kb.TT(hp_[:], sm[0][:], sm[1][:], ALU.add, [sm[0], sm[1]], [hp_])
                        kb.TT(hps_[:], sm[2][:], sm[3][:], ALU.subtract, [sm[2], sm[3]], [hps_])
                        yield
                return gen()
            run_interleaved([make(0), make(1)])
        with P.scope():
            dsk = P.sbuf("s_dsk", [128, 2]); glb = P.sbuf("s_glb", [128, 2])
            kb.LD(dsk[:], prm["s5_d"][l].rearrange("(gg p) -> p gg", p=128), [dsk], allow_slow_non_contiguous=True)
            kb.LD(glb[:], prm["s5_glu_b"][l].rearrange("(gg p) -> p gg", p=128), [glb], allow_slow_non_contiguous=True)
            gw = P.sbuf("s_gw", [128, 2, 256])
            kb.LD(gw[:], prm["s5_glu_w"][l].rearrange("(ct p) o -> p ct o", p=128), [gw])
            uTb = [P.sbuf("s_fu%d" % i, [128, 2, 128]) for i in range(2)]
            yy = [P.sbuf("s_yy%d" % i, [128, 2, 128]) for i in range(2)]
            x2 = [P.sbuf("s_x2%d" % i, [128, 2, 128]) for i in range(2)]
            th = [P.sbuf("s_th%d" % i, [128, 2, 128]) for i in range(2)]
            sgb = [P.sbuf("s_sg%d" % i, [128, 128]) for i in range(2)]
            ob = [P.sbuf("s_ob%d" % i, [128, 128]) for i in range(2)]
            psz = [P.psum("s_psz%d" % i, [128, 128]) for i in range(2)]
            k = 0
            for n in range(NT):
                cols = slice(n * 128, (n + 1) * 128)
                i = n % 2
                kb.LD(uTb[i][:], kb.ZF[Z_SU:Z_SU + 256, cols].rearrange("(gg p) t -> p gg t", p=128), [uTb[i]])
                for gg in range(2):
                    kb.STT(yy[i][:, gg, :], uTb[i][:, gg, :], dsk[:, gg:gg + 1], OACC[:, gg, cols], ALU.mult, ALU.add,
                           [uTb[i], dsk, OACC.s(n)], [yy[i]])
                kb.TT(x2[i][:], yy[i][:], yy[i][:], ALU.mult, [yy[i]], [x2[i]], eng="pool")
                kb.TS(x2[i][:], x2[i][:], 0.044715, 1.0, ALU.mult, ALU.add, [x2[i]], [x2[i]])
                kb.TT(x2[i][:], x2[i][:], yy[i][:], ALU.mult, [x2[i], yy[i]], [x2[i]], eng="pool")
                kb.ACT(th[i][:], x2[i][:], AF.Tanh, [x2[i]], [th[i]], scale=0.7978845608028654)
                kb.TS(th[i][:], th[i][:], 1.0, 0.5, ALU.add, ALU.mult, [th[i]], [th[i]])
                kb.TT(yy[i][:], yy[i][:], th[i][:], ALU.mult, [yy[i], th[i]], [yy[i]], eng="pool")
                for ot in range(2):
                    q = k % 2; k += 1
                    for ct in range(2):
                        kb.MM(psz[q][:], gw[:, ct, ot * 128:(ot + 1) * 128], yy[i][:, ct, :], ct == 0, ct == 1, [gw, yy[i]], [psz[q]])
                    kb.ACT(sgb[q][:], psz[q][:], AF.Sigmoid, [psz[q], glb], [sgb[q]], bias=glb[:, ot:ot + 1])
                    kb.TT(ob[q][:], yy[i][:, ot, :], sgb[q][:], ALU.mult, [yy[i], sgb[q]], [ob[q]])
                    kb.ST(kb.YC[768 + ot * 128:768 + (ot + 1) * 128, cols], ob[q][:], [ob[q]])


def _conv_win_masks():
    tp = np.arange(642) - 65
    w = np.mod(tp, 64)
    m = np.ones((2, 642), np.float32)
    m[0, w == 63] = 0.0
    m[1, w == 0] = 0.0
    return np.broadcast_to(m[None], (128, 2, 642)).copy()


def gdn_conv2(kb, l):
    P = kb.P
    with P.scope():
        CW = P.sbuf("g_cw", [128, 6, 9])
        for kh in range(3):
            for kw in range(3):
                kb.LD(CW[:, :, kh * 3 + kw], kb.prm["gdn_conv_w"][l][kh, kw].rearrange("(ct p) -> p ct", p=128), [CW],
                      allow_slow_non_contiguous=True)
        DW = P.sbuf("g_dw", [128, 6, 9, 128])
        for ct in range(6):
            for tp_ in range(9):
                kb.TS(DW[:, ct, tp_, :], kb.C("IDENT"), CW[:, ct, tp_:tp_ + 1], None, ALU.mult, None, [CW], [DW],
                      eng=("pool" if tp_ % 2 else "dve"))
        wm = P.sbuf("g_wm", [128, 2, 642])
        kb.LD(wm[:], kb.cwin[:], [wm])
        Wb = [P.sbuf("g_w%d" % i, [128, 642]) for i in range(2)]
        WLb = [P.sbuf("g_wl%d" % i, [128, 642]) for i in range(2)]
        WRb = [P.sbuf("g_wr%d" % i, [128, 642]) for i in range(2)]
        sl = [P.sbuf("g_sl%d" % i, [128, 512]) for i in range(2)]
        sq = [P.sbuf("g_sq%d" % i, [128, 512]) for i in range(2)]
        rt = [P.sbuf("g_rt%d" % i, [128, 512]) for i in range(2)]
        psc = [P.psum("g_psc%d" % i, [128, 512]) for i in range(2)]
        ps = [P.psum("g_psn%d" % i, [128, 512]) for i in range(2)]
        spans = [(0, 256, True)] + [(256 + 512 * k, 512, False) for k in range(8)]
        it = 0
        for (t0, L, is_ctx) in spans:
            lo = 0 if is_ctx else 256
            hi = 256 if is_ctx else S
            a = max(lo, t0 - 65); b = min(hi, t0 + L + 65)
            for ct in range(6):
                i = it % 2; it += 1
                W = Wb[i]
                full = (a == t0 - 65) and (b == t0 + L + 65) and L == 512
                if not full:
                    kb.MS(W[:], 0.0, [W], eng="pool")
                kb.LD(W[:, 65 + (a - t0):65 + (b - t0)], kb.ZF[Z_GQKV + ct * 128:Z_GQKV + (ct + 1) * 128, a:b], [W])
                if is_ctx:
                    WL = WR = W
                    rows = (1,)
                else:
                    WL, WR = WLb[i], WRb[i]
                    kb.TT(WL[:], W[:], wm[:, 0, :], ALU.mult, [W, wm], [WL])
                    kb.TT(WR[:], W[:], wm[:, 1, :], ALU.mult, [W, wm], [WR])
                    rows = (0, 1, 2)
                pc = psc[i]
                taps = [(dh, dwi) for dh in rows for dwi in range(3)]
                for q, (dh, dwi) in enumerate(taps):
                    srcT = (WL, W, WR)[dwi]
                    o0 = 65 + 64 * (dh - 1) + (dwi - 1)
                    kb.MM(pc[:, :L], DW[:, ct, dh * 3 + dwi, :], srcT[:, o0:o0 + L], q == 0, q == len(taps) - 1, [DW, srcT], [pc])
                kb.ACT(sl[i][:, :L], pc[:, :L], AF.Silu, [pc], [sl[i]])
                if ct < 4:
                    kb.TT(sq[i][:, :L], sl[i][:, :L], sl[i][:, :L], ALU.mult, [sl[i]], [sq[i]])
                    kb.MM(ps[i][:, :L], kb.C("BLK64"), sq[i][:, :L], True, True, [sq[i]], [ps[i]])
                    kb.ACT(rt[i][:, :L], ps[i][:, :L], AF.Sqrt, [ps[i]], [rt[i]], bias=kb.C("CCOL")[:, 0:1])
                    kb.RECIP(rt[i][:, :L], rt[i][:, :L], [rt[i]], [rt[i]])
                    if ct < 2:
                        kb.STT(sl[i][:, :L], sl[i][:, :L], 0.125, rt[i][:, :L], ALU.mult, ALU.mult, [sl[i], rt[i]], [sl[i]])
                    else:
                        kb.TT(sl[i][:, :L], sl[i][:, :L], rt[i][:, :L], ALU.mult, [sl[i], rt[i]], [sl[i]], eng="pool")
                kb.ST(kb.QKVF[ct * 128:(ct + 1) * 128, t0:t0 + L], sl[i][:, :L], [sl[i]])
```

```python
import numpy as np
import concourse.bass as bass
import concourse.mybir as mybir
from concourse.bass_utils import run_bass_kernel_spmd
from contextlib import ExitStack

F32 = mybir.dt.float32
BF16 = mybir.dt.bfloat16
AF = mybir.ActivationFunctionType
ALU = mybir.AluOpType

ENGS = ("pe", "act", "dve", "pool", "sp")
EPOCH = 16000
N_DMA_SEM = 32


class Buf:
    __slots__ = ("name", "w", "r", "excl", "pe_partial")

    def __init__(self, name="", excl=False):
        self.name = name
        self.w = None
        self.r = []
        self.excl = excl
        self.pe_partial = False


class T:
    def __init__(self, h, name, excl=False):
        self.h = h
        self.name = name
        self.b = Buf(name, excl)
        self.excl = excl
        self.subs = {}

    def __getitem__(self, k):
        return self.h[k]

    def s(self, key):
        if self.excl:
            return self.b
        if key not in self.subs:
            self.subs[key] = Buf("%s.%s" % (self.name, key))
        return self.subs[key]


class Prog:
    def __init__(self, nc):
        self.nc = nc
        self.es = ExitStack()
        self.stack = [self.es]
        self.ops = {e: [] for e in ENGS}
        self.cnt = {e: 0 for e in ENGS}
        self.seen = {e: {} for e in ENGS}
        self.last = {}
        self.dma_k = 0
        self.dma_use = [0] * N_DMA_SEM
        self.dma_sems = [self.es.enter_context(nc.semaphore("dq%d" % i)) for i in range(N_DMA_SEM)]
        self.eng_sems = {}
        self.out_tokens = []
        self.n_ops = 0
        self.uid = 0

    def _nm(self, name):
        self.uid += 1
        return "%s_%d" % (name, self.uid)

    def sbuf(self, name, shape, dt=F32):
        h = self.stack[-1].enter_context(self.nc.sbuf_tensor(self._nm(name), list(shape), dt))
        return T(h, name)

    def psum(self, name, shape, dt=F32):
        n = 1
        for d_ in shape[1:]:
            n *= d_
        nb = (n * 4 + 2047) // 2048
        h = self.stack[-1].enter_context(self.nc.psum_tensor(self._nm(name), [128, nb * 512], F32))
        v = h[0:shape[0], 0:n]
        if len(shape) == 3:
            v = v.rearrange("p (a b) -> p a b", a=shape[1])
        elif len(shape) == 4:
            v = v.rearrange("p (a b c) -> p a b c", a=shape[1], b=shape[2])
        return T(v, name, excl=True)

    def dram(self, name, shape, dt=F32, kind="Internal"):
        h = self.nc.dram_tensor(name, list(shape), dt, kind=kind)
        return T(h.ap(), name)

    class _Scope:
        def __init__(self, p):
            self.p = p

        def __enter__(self):
            st = ExitStack()
            self.p.stack.append(st)
            return st

        def __exit__(self, *a):
            self.p.barrier()
            st = self.p.stack.pop()
            st.close()
            return False

    def scope(self):
        return Prog._Scope(self)

    def _eng_sem(self, e, epoch):
        k = (e, epoch)
        if k not in self.eng_sems:
            self.eng_sems[k] = self.es.enter_context(self.nc.semaphore("s_%s_%d" % (e, epoch)))
        return self.eng_sems[k]

    def _waits(self, eng, reads, writes, extra=(), skip_pe=False):
        need = {}

        def add(tok):
            if tok is None:
                return
            key, val = tok
            if need.get(key, 0) < val:
                need[key] = val
        for b in reads:
            add(b.w)
        for b in writes:
            add(b.w)
            for t in b.r:
                add(t)
        for t in extra:
            add(t)
        out = []
        seen = self.seen[eng]
        for key, val in need.items():
            if skip_pe and key[0] == "e" and key[1] == "pe":
                continue
            if seen.get(key, 0) < val:
                seen[key] = val
                out.append((key, val))
        return out

    @staticmethod
    def _bufs(xs):
        out = []
        for x in xs:
            if x is None:
                continue
            out.append(x.b if isinstance(x, T) else x)
        return out

    def _commit(self, tok, reads, writes):
        self.last[tok[0]] = tok[1]
        for b in reads:
            b.r.append(tok)
            if len(b.r) > 64:
                mx = {}
                for k, v in b.r:
                    if mx.get(k, 0) < v:
                        mx[k] = v
                b.r = list(mx.items())
        for b in writes:
            b.w = tok
            b.r = []
        self.n_ops += 1

    def op(self, eng, fn, reads=(), writes=(), partial=False):
        reads = self._bufs(reads)
        writes = self._bufs(writes)
        ex = [b for b in reads if b.excl]
        if ex:
            reads = [b for b in reads if not b.excl]
            writes = writes + [b for b in ex if b not in writes]
        skip_pe = False
        if eng == "pe":
            skip_pe = (not partial) and all(not b.pe_partial for b in writes)
            for b in writes:
                b.pe_partial = partial
        waits = self._waits(eng, reads, writes, skip_pe=skip_pe)
        self.cnt[eng] += 1
        epoch, val = divmod(self.cnt[eng] - 1, EPOCH)
        tok = (("e", eng, epoch), val + 1)
        self.ops[eng].append((waits, fn, tok))
        self._commit(tok, reads, writes)
        return tok

    def dma(self, out_ap, in_ap, reads=(), writes=(), q="sp", is_output=False, **kw):
        reads = self._bufs(reads)
        writes = self._bufs(writes)
        i = self.dma_k % N_DMA_SEM
        self.dma_k += 1
        prev = self.dma_use[i]
        extra = [(("d", i), 16 * prev)] if prev else []
        waits = self._waits(q, reads, writes, extra)
        self.dma_use[i] = prev + 1
        tok = (("d", i), 16 * (prev + 1))

        def fn(e):
            return e.dma_start(out=out_ap, in_=in_ap, **kw)
        self.ops[q].append((waits, fn, tok))
        self._commit(tok, reads, writes)
        if is_output:
            self.out_tokens.append(tok)
        return tok

    def barrier(self):
        toks = list(self.last.items())
        for e in ENGS:
            waits = self._waits(e, [], [], toks)
            if waits:
                self.ops[e].append((waits, None, None))

    def _sem_of(self, key):
        if key[0] == "d":
            return self.dma_sems[key[1]]
        return self._eng_sem(key[1], key[2])

    def emit(self):
        nc = self.nc
        self.barrier()
        for e in ENGS:
            for waits, fn, tok in self.ops[e]:
                if tok is not None:
                    self._sem_of(tok[0])
                for key, val in waits:
                    self._sem_of(key)
        with nc.Block() as block:
            def run(e, handle):
                for waits, fn, tok in self.ops[e]:
                    for key, val in waits:
                        handle.wait_ge(self._sem_of(key), val)
                    if fn is None:
                        continue
                    ins = fn(handle)
                    key, val = tok
                    ins.then_inc(self._sem_of(key), 16 if key[0] == "d" else 1)

            @block.sync
            def _(h):
                run("sp", h)

            @block.tensor
            def _(h):
                run("pe", h)

            @block.scalar
            def _(h):
                run("act", h)

            @block.vector
            def _(h):
                run("dve", h)

            @block.gpsimd
            def _(h):
                run("pool", h)

    def close(self):
        self.es.close()


D = 1024
S = 4352
NT = 34
LAT0 = 256
DEPTH = 2
EPS = 1e-6
NEG = -30000.0
ORDER = [list(range(NT)), [1, 0] + list(range(NT - 1, 1, -1))]

C_HQ, C_HI, C_HG, C_HFF, C_HFB = 0, 256, 512, 768, 1024
C_RQ, C_RK, C_RV, C_RG = 1280, 1536, 1792, 2048
C_GQKV, C_GG, C_GA, C_GB, C_SU = 2304, 3072, 3328, 3336, 3344
Z_HQ, Z_HG, Z_HFF, Z_HFB, Z_RQ, Z_RK, Z_RG, Z_GQKV, Z_GG, Z_SU = 0, 256, 512, 768, 1024, 1280, 1536, 1792, 2560, 2816
NZF = 3072
FM_MAP = [(Z_HQ, C_HQ, 256), (Z_HG, C_HG, 256), (Z_HFF, C_HFF, 256), (Z_HFB, C_HFB, 256), (Z_RQ, C_RQ, 256),
          (Z_RK, C_RK, 256), (Z_RG, C_RG, 256), (Z_GQKV, C_GQKV, 768), (Z_GG, C_GG, 256), (Z_SU, C_SU, 256)]
FM_BLOCKS = [(zr + i, wc + i) for zr, wc, n in FM_MAP for i in range(0, n, 128)]
NZT = 528

CN = {}


def _const_pack():
    mats = []

    def add(name, m):
        CN[name] = len(mats)
        mats.append(np.asarray(m, np.float32))
    p = np.arange(128)[:, None]
    f = np.arange(128)[None, :]
    add("IDENT", (p == f))
    add("ONES", np.ones((128, 128)))
    add("TRIF", (p <= f))
    add("TRIB", (p >= f))
    add("SUFF", (p > f))
    add("PREB", (p < f))
    add("NLE", np.where(p <= f, 0.0, NEG))
    add("NLT", np.where(p < f, 0.0, NEG))
    add("NGE", np.where(p >= f, 0.0, NEG))
    add("NGT", np.where(p > f, 0.0, NEG))
    for s in (1, 2, 4, 8, 16, 32, 64):
        m = (((p // s) % 2) == 1) & ((f // s) == (p // s) - 1)
        add("MOFF%d" % s, m)
        add("MOFFT%d" % s, m.T)
    add("BLK64", (p // 64) == (f // 64))
    rot = np.zeros((128, 128))
    for m in range(128):
        if (m % 64) < 32:
            rot[m + 32, m] = -1.0
        else:
            rot[m - 32, m] = 1.0
    add("ROT", rot)
    add("IOTAF", np.broadcast_to(f, (128, 128)))
    add("IOTAF1", np.broadcast_to(f + 1, (128, 128)))
    add("RIOTAF", np.broadcast_to(128 - f, (128, 128)))
    add("R127F", np.broadcast_to(127 - f, (128, 128)))
    add("DIFF", f - p)
    add("NDIFF", p - f)
    for h in range(4):
        m = np.zeros((128, 128)); m[h, :] = 1.0
        add("SELH%d" % h, m)
    for hp in range(2):
        m = np.zeros((128, 128)); m[2 * hp, 0:64] = 1.0; m[2 * hp + 1, 64:128] = 1.0
        add("SELP%d" % hp, m)
    cc = np.zeros((128, 128))
    cc[:, 0] = EPS; cc[:, 1] = 1.0; cc[:, 2] = np.arange(128); cc[:, 3] = 127 - np.arange(128)
    cc[:, 5] = -np.pi; cc[:, 6] = -np.arange(128); cc[:, 7] = -(127 - np.arange(128))
    add("CCOL", cc)
    gm = np.zeros((128, 128))
    for g in range(16):
        gm[(g % 8) * 16:(g % 8) * 16 + 16, g] = 1.0
    add("GMASK", gm)
    return np.concatenate(mats, axis=1)


CONST_NP = _const_pack()
NCONST = CONST_NP.shape[1] // 128


def _rope_tables():
    half = 32
    inv = 10000.0 ** (-np.arange(half, dtype=np.float64) / half)
    pos = np.arange(S, dtype=np.float64)
    ang = pos[None, :] * inv[:, None]
    cos = np.cos(ang); sin = np.sin(ang)
    cos128 = np.tile(cos, (4, 1)); sin128 = np.tile(sin, (4, 1))
    return cos128.astype(np.float32), sin128.astype(np.float32)


def _conv_masks():
    m = np.ones((2, 512), np.float32)
    w = np.arange(512) % 64
    m[0, w == 0] = 0.0
    m[1, w == 63] = 0.0
    lat = np.broadcast_to(m[None], (128, 2, 512)).copy()
    c = np.ones((2, 256), np.float32)
    c[0, 0] = 0.0
    c[1, 255] = 0.0
    ctx = np.broadcast_to(c[None], (128, 2, 256)).copy()
    return lat, ctx


class KB:
    def __init__(self, cfg):
        self.cfg = cfg
        nc = bass.Bass("TRN2", target_bir_lowering=False)
        self.nc = nc
        self.P = Prog(nc)
        self.rr = 0

    def MM(self, ps, lhsT, rhs, st, sp, R, W):
        partial = lhsT.partition_size() < 128
        self.P.op("pe", lambda e: e.matmul(ps, lhsT, rhs, start=st, stop=sp), R, W, partial=partial)

    def TR(self, ps, in_, ident, R, W):
        self.P.op("pe", lambda e: e.transpose(ps, in_, ident), R, W)

    def ACT(self, out, in_, func, R, W, **kw):
        self.P.op("act", lambda e: e.activation(out=out, in_=in_, func=func, **kw), R, W)

    def TS(self, out, in0, s1, s2, op0, op1, R, W, eng="dve"):
        if s2 is None:
            self.P.op(eng, lambda e: e.tensor_scalar(out=out, in0=in0, scalar1=s1, scalar2=None, op0=op0), R, W)
        else:
            self.P.op(eng, lambda e: e.tensor_scalar(out=out, in0=in0, scalar1=s1, scalar2=s2, op0=op0, op1=op1), R, W)

    def TT(self, out, in0, in1, op, R, W, eng="dve"):
        self.P.op(eng, lambda e: e.tensor_tensor(out=out, in0=in0, in1=in1, op=op), R, W)

    def STT(self, out, in0, sc, in1, op0, op1, R, W, eng="dve"):
        eng = "dve"
        self.P.op(eng, lambda e: e.scalar_tensor_tensor(out=out, in0=in0, scalar=sc, in1=in1, op0=op0, op1=op1), R, W)

    def CP(self, out, in_, R, W, eng="dve"):
        if eng == "act":
            self.ACT(out, in_, AF.Copy, R, W)
        else:
            self.P.op(eng, lambda e: e.tensor_copy(out=out, in_=in_), R, W)

    def CPRED(self, out, mask, data, R, W):
        self.P.op("dve", lambda e: e.copy_predicated(out=out, mask=mask, data=data), R, W)

    def MS(self, ap, val, W, eng="dve"):
        self.P.op(eng, lambda e: e.memset(ap, val), (), W)

    def RECIP(self, out, in_, R, W):
        self.P.op("dve", lambda e: e.reciprocal(out=out, in_=in_), R, W)

    def SCAN(self, out, d0, d1, R, W):
        self.P.op("dve", lambda e: e.tensor_tensor_scan(out=out, data0=d0, data1=d1, initial=0.0,
                                                        op0=ALU.mult, op1=ALU.add), R, W)

    def LD(self, out, in_, W, R=(), q="sp", **kw):
        self.P.dma(out, in_, reads=R, writes=W, q=q, **kw)

    def ST(self, out, in_, R, W=(), q="pool", **kw):
        self.P.dma(out, in_, reads=R, writes=W, q=q, **kw)

    def evac_eng(self):
        self.rr += 1
        return "act" if self.rr % 2 else "dve"

    def C(self, name):
        i = CN[name]
        return self.const[:, i * 128:(i + 1) * 128]


PARAM_SHAPES = {
    "mod_w": [2, 1024, 6144], "mod_b": [2, 6144], "norm1_g": [2, 1024], "norm2_g": [2, 1024],
    "w_in": [2, 1024, 3600], "hgrn_lb_logits": [2, 2, 256], "hgrn_norm_g": [2, 64],
    "ret_decay_logit": [2, 2, 4], "gdn_conv_w": [2, 3, 3, 768], "gdn_a_log": [2, 2, 4],
    "gdn_dt_bias": [2, 2, 4], "gdn_norm_g": [2, 64], "s5_lam_re": [2, 2, 16, 64],
    "s5_lam_im": [2, 2, 16, 64], "s5_log_dt": [2, 2, 16], "s5_b_re": [2, 16, 64, 16],
    "s5_b_im": [2, 16, 64, 16], "s5_c_re": [2, 16, 16, 64], "s5_c_im": [2, 16, 16, 64],
    "s5_d": [2, 256], "s5_glu_w": [2, 256, 256], "s5_glu_b": [2, 256], "w_out": [2, 1024, 1024],
    "mlp_w1": [2, 1024, 4096], "mlp_w2": [2, 4096, 1024], "final_norm_g": [1024],
}


def declare(kb):
    P = kb.P
    cfg = kb.cfg
    kinds = cfg.get("kinds", {})
    kb.xin = P.dram("xin", [S, D], F32, kind="ExternalInput")
    kb.cvecT = P.dram("cvecT", [1024, 2], F32, kind="ExternalInput")
    kb.prm = {k: P.dram(k, shp, F32, kind="ExternalInput") for k, shp in PARAM_SHAPES.items()}
    kb.constd = P.dram("constp", [128, NCONST * 128], F32, kind="ExternalInput")
    kb.ropec = P.dram("ropec", [128, S], F32, kind="ExternalInput")
    kb.ropes = P.dram("ropes", [128, S], F32, kind="ExternalInput")
    kb.cmlat = P.dram("cmlat", [128, 2, 512], F32, kind="ExternalInput")
    kb.cmctx = P.dram("cmctx", [128, 2, 256], F32, kind="ExternalInput")
    kb.cwin = P.dram("cwin", [128, 2, 642], F32, kind="ExternalInput")
    kb.y = P.dram("y", [4096, D], F32, kind="ExternalOutput")
    kb.XS = P.dram("XS", [S, D], F32, kind=kinds.get("XS", "Internal"))
    kb.ZF = P.dram("ZF", [NZF, S], F32, kind=kinds.get("ZF", "Internal"))
    kb.ZT = P.dram("ZT", [S, NZT], F32, kind=kinds.get("ZT", "Internal"))
    kb.QKVF = P.dram("QKVF", [768, S], F32, kind=kinds.get("QKVF", "Internal"))
    kb.YC = P.dram("YC", [1024, S], F32, kind=kinds.get("YC", "Internal"))
    kb.H2T = P.dram("H2T", [1024, S], BF16, kind=kinds.get("H2T", "Internal"))
    kb.const = P.sbuf("const", [128, NCONST * 128])
    nchunk = 4
    w = NCONST * 128 // nchunk
    for i in range(nchunk):
        a, b = i * w, (i + 1) * w if i < nchunk - 1 else NCONST * 128
        kb.LD(kb.const[:, a:b], kb.constd[:, a:b], [kb.const.s(i)])
    kb.const_bufs = [kb.const.s(i) for i in range(nchunk)]
    kb.CB = kb.const_bufs
    kb.GS1 = P.sbuf("GS1", [128, 8, 2]); kb.SH1 = P.sbuf("SH1", [128, 8, 2])
    kb.GS2 = P.sbuf("GS2", [128, 8, 2]); kb.SH2 = P.sbuf("SH2", [128, 8, 2])
    kb.GATE1 = P.sbuf("GATE1", [128, 2, 1024]); kb.GATE2 = P.sbuf("GATE2", [128, 2, 1024])


def phase_mod(kb, l):
    P = kb.P
    prm = kb.prm
    with P.scope():
        cT = P.sbuf("cT", [128, 8, 2])
        kb.LD(cT[:], kb.cvecT[:].rearrange("(et e) c -> e et c", e=128), [cT])
        sc = P.sbuf("sc", [128, 8, 2])
        kb.ACT(sc[:], cT[:], AF.Silu, [cT], [sc])
        screp = P.sbuf("screp", [128, 8, 2, 128])
        kb.CP(screp[:], sc[:].unsqueeze(3).to_broadcast([128, 8, 2, 128]), [sc], [screp])
        mbf = P.sbuf("mbf", [128, 48])
        kb.LD(mbf[:], prm["mod_b"][l].rearrange("(j p) -> p j", p=128), [mbf], allow_slow_non_contiguous=True)
        ngf = P.sbuf("ngf", [128, 2, 8])
        kb.LD(ngf[:, 0, :], prm["norm1_g"][l].rearrange("(j p) -> p j", p=128), [ngf], allow_slow_non_contiguous=True)
        kb.LD(ngf[:, 1, :], prm["norm2_g"][l].rearrange("(j p) -> p j", p=128), [ngf], allow_slow_non_contiguous=True)
        mbrow = P.sbuf("mbrow", [128, 2, 1024])
        for gi, v in enumerate((2, 5)):
            kb.LD(mbrow[:, gi, :], prm["mod_b"][l][v * 1024:(v + 1) * 1024].partition_broadcast(128), [mbrow])
        wch = [P.sbuf("wch%d" % i, [128, 8, 1024]) for i in range(2)]
        ps_fm = P.psum("ps_fm", [128, 96])
        ps_g = [P.psum("ps_g%d" % i, [128, 512]) for i in range(2)]
        MF = P.sbuf("MF", [128, 48, 2])
        k = 0
        for v in range(6):
            wc = wch[v % 2]
            for et in range(8):
                kb.LD(wc[:, et, :], prm["mod_w"][l][et * 128:(et + 1) * 128, v * 1024:(v + 1) * 1024], [wc])
            for db in range(8):
                col = (v * 8 + db) * 2
                for et in range(8):
                    kb.MM(ps_fm[:, col:col + 2], wc[:, et, db * 128:(db + 1) * 128], sc[:, et, :],
                          et == 0, et == 7, [wc, sc], [ps_fm])
            if v in (2, 5):
                gt = kb.GATE1 if v == 2 else kb.GATE2
                gi = 0 if v == 2 else 1
                for which in range(2):
                    for half in range(2):
                        pg = ps_g[k % 2]; k += 1
                        for et in range(8):
                            kb.MM(pg[:], screp[:, et, which, :], wc[:, et, half * 512:(half + 1) * 512],
                                  et == 0, et == 7, [screp, wc], [pg])
                        kb.TT(gt[:, which, half * 512:(half + 1) * 512], pg[:], mbrow[:, gi, half * 512:(half + 1) * 512],
                              ALU.add, [pg, mbrow], [gt])
        kb.TT(MF[:], ps_fm[:].rearrange("p (j c) -> p j c", c=2), mbf[:].unsqueeze(2).to_broadcast([128, 48, 2]),
              ALU.add, [ps_fm, mbf], [MF])
        tmp = P.sbuf("mtmp", [128, 8, 2])
        kb.TS(tmp[:], MF[:, 8:16, :], 1.0, None, ALU.add, None, [MF], [tmp])
        kb.TT(kb.GS1[:], tmp[:], ngf[:, 0, :].unsqueeze(2).to_broadcast([128, 8, 2]), ALU.mult, [tmp, ngf], [kb.GS1])
        kb.CP(kb.SH1[:], MF[:, 0:8, :], [MF], [kb.SH1])
        tmp2 = P.sbuf("mtmp2", [128, 8, 2])
        kb.TS(tmp2[:], MF[:, 32:40, :], 1.0, None, ALU.add, None, [MF], [tmp2])
        kb.TT(kb.GS2[:], tmp2[:], ngf[:, 1, :].unsqueeze(2).to_broadcast([128, 8, 2]), ALU.mult, [tmp2, ngf], [kb.GS2])
        kb.CP(kb.SH2[:], MF[:, 24:32, :], [MF], [kb.SH2])


def norm_to_fm(kb, xt, hT, col0, GS, SH, which, bufs, R_x):
    P = kb.P
    junk, st, xn, ps_ts = bufs["junk"], bufs["st"], bufs["xn"], bufs["ps_t"]
    kb.MS(st[:, 0:1], 0.0, [st])
    kb.ACT(junk[:], xt[:], AF.Square, [xt], [junk, st], accum_out=st[:, 0:1])
    kb.ACT(st[:, 1:2], st[:, 0:1], AF.Sqrt, [st] + kb.CB, [st], scale=1.0 / D, bias=kb.C("CCOL")[:, 0:1])
    kb.RECIP(st[:, 2:3], st[:, 1:2], [st], [st])
    kb.ACT(xn[:], xt[:], AF.Copy, [xt, st], [xn], scale=st[:, 2:3])
    for half in range(2):
        ps_t = ps_ts[half]
        for q in range(4):
            dt = half * 4 + q
            kb.TR(ps_t[:, q * 128:(q + 1) * 128], xn[:, dt * 128:(dt + 1) * 128], kb.C("IDENT"), [xn] + kb.CB, [ps_t])
        for q in range(4):
            dt = half * 4 + q
            if q % 2 == 0:
                kb.TS(hT[:, dt, col0:col0 + 128], ps_t[:, q * 128:(q + 1) * 128], GS[:, dt, which:which + 1],
                      SH[:, dt, which:which + 1], ALU.mult, ALU.add, [ps_t, GS, SH], [hT])
            else:
                kb.ACT(hT[:, dt, col0:col0 + 128], ps_t[:, q * 128:(q + 1) * 128], AF.Identity, [ps_t, GS, SH], [hT],
                       scale=GS[:, dt, which:which + 1], bias=SH[:, dt, which:which + 1])


def norm_to_fm_g(kb, xt, hT, col0, GS, SH, which, bufs):
    junk, st, xn, ps_ts = bufs["junk"], bufs["st"], bufs["xn"], bufs["ps_t"]
    kb.MS(st[:, 0:1], 0.0, [st])
    kb.ACT(junk[:], xt[:], AF.Square, [xt], [junk, st], accum_out=st[:, 0:1])
    yield
    kb.ACT(st[:, 1:2], st[:, 0:1], AF.Sqrt, [st], [st], scale=1.0 / D, bias=kb.C("CCOL")[:, 0:1])
    kb.RECIP(st[:, 2:3], st[:, 1:2], [st], [st])
    yield
    kb.ACT(xn[:], xt[:], AF.Copy, [xt, st], [xn], scale=st[:, 2:3])
    yield
    for half in range(2):
        ps_t = ps_ts[half]
        for q in range(4):
            dt = half * 4 + q
            kb.TR(ps_t[:, q * 128:(q + 1) * 128], xn[:, dt * 128:(dt + 1) * 128], kb.C("IDENT"), [xn], [ps_t])
        yield
        for q in range(4):
            dt = half * 4 + q
            if q % 2 == 0:
                kb.TS(hT[:, dt, col0:col0 + 128], ps_t[:, q * 128:(q + 1) * 128], GS[:, dt, which:which + 1],
                      SH[:, dt, which:which + 1], ALU.mult, ALU.add, [ps_t, GS, SH], [hT])
            else:
                kb.ACT(hT[:, dt, col0:col0 + 128], ps_t[:, q * 128:(q + 1) * 128], AF.Identity, [ps_t, GS, SH], [hT],
                       scale=GS[:, dt, which:which + 1], bias=SH[:, dt, which:which + 1])
        yield


def phase_a(kb, l, src):
    P = kb.P
    with P.scope():
        win = P.sbuf("win", [128, 8, 3600], BF16)
        for kt in range(8):
            kb.LD(win[:, kt, :], kb.prm["w_in"][l][kt * 128:(kt + 1) * 128, :], [win.s(kt)], q="pool")
        winb = [win.s(kt) for kt in range(8)]
        xbuf = [P.sbuf("xa%d" % i, [128, 1024]) for i in range(2)]
        hTb = [P.sbuf("hTa%d" % i, [128, 8, 512], BF16) for i in range(2)]
        nb = {"junk": P.sbuf("junk", [128, 1024]), "st": P.sbuf("st", [128, 4]), "xn": P.sbuf("xn", [128, 1024]),
              "ps_t": [P.psum("ps_t%d" % i, [128, 512]) for i in range(2)]}
        ps_f = [P.psum("ps_f%d" % i, [128, 512]) for i in range(3)]
        ps_a = [P.psum("ps_a%d" % i, [128, 512]) for i in range(2)]
        ps_b = P.psum("ps_b", [128, 16])
        stg = [P.sbuf("stg%d" % i, [128, 512]) for i in range(4)]
        stt = [P.sbuf("stt%d" % i, [128, NZT]) for i in range(2)]
        kx = kf = ks = ka = 0
        for gi, t0 in enumerate(range(0, S, 512)):
            n = min(512, S - t0)
            hT = hTb[gi % 2]
            for ti in range(n // 128):
                tt = t0 // 128 + ti
                which = 1 if tt < 2 else 0
                xt = xbuf[kx % 2]; kx += 1
                kb.LD(xt[:], src[tt * 128:(tt + 1) * 128, :], [xt])
                norm_to_fm(kb, xt, hT, ti * 128, kb.GS1, kb.SH1, which, nb, None)
            for (zr, wc) in FM_BLOCKS:
                ps = ps_f[kf % 3]; kf += 1
                for kt in range(8):
                    kb.MM(ps[:, :n], win[:, kt, wc:wc + 128], hT[:, kt, :n], kt == 0, kt == 7, [winb[kt], hT], [ps])
                sg = stg[ks % 4]; ks += 1
                kb.CP(sg[:, :n], ps[:, :n], [ps], [sg], eng=kb.evac_eng())
                kb.ST(kb.ZF[zr:zr + 128, t0:t0 + n], sg[:, :n], [sg])
            for ti in range(n // 128):
                tt = t0 // 128 + ti
                pa = ps_a[ka % 2]
                so = stt[ka % 2]; ka += 1
                for (c0, w0, wn) in ((0, C_HI, 256), (256, C_RV, 256)):
                    for kt in range(8):
                        kb.MM(pa[:, c0:c0 + wn], hT[:, kt, ti * 128:(ti + 1) * 128], win[:, kt, w0:w0 + wn],
                              kt == 0, kt == 7, [winb[kt], hT], [pa])
                for kt in range(8):
                    kb.MM(ps_b[:], hT[:, kt, ti * 128:(ti + 1) * 128], win[:, kt, C_GA:C_GA + 16],
                          kt == 0, kt == 7, [winb[kt], hT], [ps_b])
                kb.CP(so[:, 0:512], pa[:], [pa], [so], eng="act")
                kb.CP(so[:, 512:528], ps_b[:], [ps_b], [so], eng="dve")
                kb.ST(kb.ZT[tt * 128:(tt + 1) * 128, :], so[:], [so])


def build(cfg):
    kb = KB(cfg)
    P = kb.P
    declare(kb)
    P.barrier()
    stages = cfg.get("stages", "all")
    for l in cfg.get("layers", range(DEPTH)):
        src = kb.xin if l == 0 else kb.XS
        if stages == "all" or "M" in stages:
            phase_mod(kb, l)
        if stages == "all" or "A" in stages:
            phase_a(kb, l, src)
        if stages == "all" or "R" in stages:
            (mixer_ret if cfg.get("ret_old") else mixer_ret2)(kb, l)
        if stages == "all" or "H" in stages:
            (mixer_hgrn if cfg.get("hgrn_old") else mixer_hgrn2)(kb, l)
        if stages == "all" or "G" in stages:
            (mixer_gdn if cfg.get("gdn_old") else mixer_gdn2)(kb, l)
        if stages == "all" or "S" in stages:
            (mixer_s5 if cfg.get("s5_old") else mixer_s5_2)(kb, l)
        if stages == "all" or "C" in stages:
            phase_c(kb, l, src)
    P.emit()
    P.close()
    return kb


_CONSTS = None


def host_inputs(inputs, cores=range(8)):
    global _CONSTS
    if _CONSTS is None:
        rc, rs = _rope_tables()
        cl, cc = _conv_masks()
        _CONSTS = {"constp": CONST_NP, "ropec": rc, "ropes": rs, "cmlat": cl, "cmctx": cc, "cwin": _conv_win_masks()}
    maps = []
    for b in cores:
        m = {"xin": np.ascontiguousarray(np.concatenate([inputs["ctx"][b], inputs["x"][b]], axis=0), dtype=np.float32),
             "cvecT": np.ascontiguousarray(np.stack([inputs["c"][b], inputs["c_ctx"]], axis=1), dtype=np.float32)}
        for k in PARAM_SHAPES:
            m[k] = np.ascontiguousarray(inputs[k], dtype=np.float32)
        m.update(_CONSTS)
        maps.append(m)
    return maps


def kernel(**inputs):
    inputs = {k: np.asarray(v) for k, v in inputs.items()}
    kb = build({})
    maps = host_inputs(inputs)
    res = run_bass_kernel_spmd(kb.nc, maps, core_ids=list(range(8)))
    out = np.stack([np.asarray(r["y"]).reshape(4096, D) for r in res.results], axis=0)
    return out.astype(np.float32)


def phase_c(kb, l, src):
    P = kb.P
    last = (l == DEPTH - 1)
    t_start = 2 if last else 0
    with P.scope():
        wout = P.sbuf("wout", [128, 8, 1024], BF16)
        for ft in range(8):
            kb.LD(wout[:, ft, :], kb.prm["w_out"][l][ft * 128:(ft + 1) * 128, :], [wout.s(ft)], q="pool")
        wb = [wout.s(ft) for ft in range(8)]

        def make(si):
            pf = "c%d_" % si
            yc = P.sbuf(pf + "yc", [128, 8, 128], BF16)
            xt = P.sbuf(pf + "x", [128, 1024]); x1 = P.sbuf(pf + "x1", [128, 1024])
            tmpb = [P.sbuf(pf + "t%d" % i, [128, 512]) for i in range(2)]
            h2 = P.sbuf(pf + "h2", [128, 8, 128], BF16)
            nb = {"junk": P.sbuf(pf + "junk", [128, 1024]), "st": P.sbuf(pf + "st", [128, 4]), "xn": P.sbuf(pf + "xn", [128, 1024]),
                  "ps_t": [P.psum(pf + "pst%d" % i, [128, 512]) for i in range(2)]}
            ps_y = [P.psum(pf + "psy%d" % i, [128, 512]) for i in range(2)]

            def gen():
                for tt in range(t_start + si, NT, 2):
                    which = 1 if tt < 2 else 0
                    cols = slice(tt * 128, (tt + 1) * 128)
                    kb.LD(yc[:], kb.YC[:, cols].rearrange("(ft p) t -> p ft t", p=128), [yc], q="pool")
                    kb.LD(xt[:], src[cols, :], [xt])
                    yield
                    for half in range(2):
                        ps = ps_y[half]
                        for ft in range(8):
                            kb.MM(ps[:], yc[:, ft, :], wout[:, ft, half * 512:(half + 1) * 512], ft == 0, ft == 7,
                                  [yc, wb[ft]], [ps])
                        yield
                    for half in range(2):
                        ps = ps_y[half]
                        tm = tmpb[half]
                        kb.TT(tm[:], ps[:], kb.GATE1[:, which, half * 512:(half + 1) * 512], ALU.mult, [ps, kb.GATE1], [tm])
                        kb.TT(x1[:, half * 512:(half + 1) * 512], xt[:, half * 512:(half + 1) * 512], tm[:], ALU.add,
                              [xt, tm], [x1], eng="pool")
                        yield
                    kb.ST(kb.XS[cols, :], x1[:], [x1])
                    yield from norm_to_fm_g(kb, x1, h2, 0, kb.GS2, kb.SH2, which, nb)
                    kb.ST(kb.H2T[:, cols].rearrange("(dt p) t -> p dt t", p=128), h2[:], [h2])
                    yield
            return gen()
        run_interleaved([make(0), make(1)])
    with P.scope():
        w1 = P.sbuf("w1", [128, 8, 4096], BF16)
        w2 = P.sbuf("w2", [128, 32, 1024], BF16)
        for kt in range(8):
            kb.LD(w1[:, kt, :], kb.prm["mlp_w1"][l][kt * 128:(kt + 1) * 128, :], [w1.s(kt)], q="pool")
        for fb in range(32):
            kb.LD(w2[:, fb, :], kb.prm["mlp_w2"][l][fb * 128:(fb + 1) * 128, :], [w2.s(fb)], q="pool")
        h2b = [P.sbuf("h2d%d" % i, [128, 8, 256], BF16) for i in range(2)]
        uTb = [P.sbuf("uT%d" % i, [128, 16, 256], BF16) for i in range(1)]
        rb = [P.sbuf("relu%d" % i, [128, 256]) for i in range(3)]
        xb = [P.sbuf("xd%d" % i, [128, 1024]) for i in range(2)]
        tmpb = [P.sbuf("td%d" % i, [128, 512]) for i in range(2)]
        ps_u = [P.psum("ps_u%d" % i, [128, 256]) for i in range(3)]
        ps_y = [P.psum("ps_y2%d" % i, [128, 512]) for i in range(4)]
        if last:
            fg = P.sbuf("fg", [128, 1024])
            kb.LD(fg[:], kb.prm["final_norm_g"][:].partition_broadcast(128), [fg])
            stf = P.sbuf("stf", [128, 4])
            xnf = P.sbuf("xnf", [128, 1024])
        k = 0; ku = 0
        for g0 in range(t_start, NT, 2):
            h2 = h2b[k % 2]; uT = uTb[0]
            cols = slice(g0 * 128, (g0 + 2) * 128)
            kb.LD(h2[:], kb.H2T[:, cols].rearrange("(dt p) t -> p dt t", p=128), [h2])
            for hh in range(2):
                for fl in range(16):
                    fb = hh * 16 + fl
                    ps = ps_u[ku % 3]; r = rb[ku % 3]; ku += 1
                    for kt in range(8):
                        kb.MM(ps[:], w1[:, kt, fb * 128:(fb + 1) * 128], h2[:, kt, :], kt == 0, kt == 7, [w1.s(kt), h2], [ps])
                    kb.ACT(r[:], ps[:], AF.Relu, [ps], [r])
                    kb.TT(uT[:, fl, :], r[:], r[:], ALU.mult, [r], [uT.s(fl)], eng=("dve" if fb % 2 else "pool"))
                for ti in range(2):
                    for half in range(2):
                        ps = ps_y[2 * ti + half]
                        for fl in range(16):
                            fb = hh * 16 + fl
                            kb.MM(ps[:], uT[:, fl, ti * 128:(ti + 1) * 128], w2[:, fb, half * 512:(half + 1) * 512],
                                  fb == 0, fb == 31, [uT.s(fl), w2.s(fb)], [ps])
            for ti in range(2):
                tt = g0 + ti
                which = 1 if tt < 2 else 0
                xt = xb[ti]
                rows = slice(tt * 128, (tt + 1) * 128)
                kb.LD(xt[:], kb.XS[rows, :], [xt])
                for half in range(2):
                    ps = ps_y[2 * ti + half]
                    tm = tmpb[half]
                    kb.TT(tm[:], ps[:], kb.GATE2[:, which, half * 512:(half + 1) * 512], ALU.mult, [ps, kb.GATE2], [tm])
                    kb.TT(xt[:, half * 512:(half + 1) * 512], xt[:, half * 512:(half + 1) * 512], tm[:], ALU.add,
                          [xt, tm], [xt], eng="pool")
                if not last:
                    kb.ST(kb.XS[rows, :], xt[:], [xt])
                else:
                    kb.MS(stf[:, 0:1], 0.0, [stf])
                    kb.ACT(xnf[:], xt[:], AF.Square, [xt], [xnf, stf], accum_out=stf[:, 0:1])
                    kb.ACT(stf[:, 1:2], stf[:, 0:1], AF.Sqrt, [stf], [stf], scale=1.0 / D, bias=kb.C("CCOL")[:, 0:1])
                    kb.RECIP(stf[:, 2:3], stf[:, 1:2], [stf], [stf])
                    kb.ACT(xnf[:], xt[:], AF.Copy, [xt, stf], [xnf], scale=stf[:, 2:3])
                    kb.TT(xnf[:], xnf[:], fg[:], ALU.mult, [xnf, fg], [xnf])
                    kb.P.dma(kb.y[(tt - 2) * 128:(tt - 1) * 128, :], xnf[:], reads=[xnf.b], q="pool", is_output=True)
            k += 1


def finalize_gated(kb, OACC, gate_row0, gain, yc_row0, pfx):
    P = kb.P
    def two(nm):
        return [P.sbuf(pfx + nm + "%d" % i, [128, 2, 128]) for i in range(2)]
    gb, sq, rt, eg, ob = two("fg"), two("fsq"), two("frt"), two("feg"), two("fo")
    ps_m = [P.psum(pfx + "fps%d" % i, [128, 2, 128]) for i in range(2)]
    for n in range(NT):
        cols = slice(n * 128, (n + 1) * 128)
        i = n % 2
        g = gb[i]
        kb.LD(g[:], kb.ZF[gate_row0:gate_row0 + 256, cols].rearrange("(hp p) t -> p hp t", p=128), [g])
        o = OACC[:, :, cols]
        kb.TT(sq[i][:], o, o, ALU.mult, [OACC.s(n)], [sq[i]])
        kb.MM(ps_m[i][:].rearrange("p a b -> p (a b)"), kb.C("BLK64"), sq[i][:].rearrange("p a b -> p (a b)"), True, True,
              [sq[i]], [ps_m[i]])
        kb.ACT(rt[i][:], ps_m[i][:], AF.Ln, [ps_m[i]], [rt[i]], scale=1.0 / 64, bias=kb.C("CCOL")[:, 0:1])
        kb.ACT(rt[i][:], rt[i][:], AF.Exp, [rt[i]], [rt[i]], scale=-0.5)
        kb.ACT(eg[i][:], g[:], AF.Exp, [g], [eg[i]], scale=-1.0)
        kb.TS(eg[i][:], eg[i][:], 1.0, None, ALU.add, None, [eg[i]], [eg[i]])
        kb.RECIP(eg[i][:], eg[i][:], [eg[i]], [eg[i]])
        kb.TT(eg[i][:], eg[i][:], g[:], ALU.mult, [eg[i], g], [eg[i]], eng="pool")
        kb.TT(ob[i][:], o, rt[i][:], ALU.mult, [OACC.s(n), rt[i]], [ob[i]])
        if gain is not None:
            kb.STT(ob[i][:], ob[i][:], gain[:, 0:1], eg[i][:], ALU.mult, ALU.mult, [ob[i], gain, eg[i]], [ob[i]])
        else:
            kb.TT(ob[i][:], ob[i][:], eg[i][:], ALU.mult, [ob[i], eg[i]], [ob[i]])
        kb.ST(kb.YC[yc_row0:yc_row0 + 256, cols].rearrange("(hp p) t -> p hp t", p=128), ob[i][:], [ob[i]])


def oacc_write(kb, OACC, hp, n, ps, d):
    cols = slice(n * 128, (n + 1) * 128)
    if d == 0:
        kb.CP(OACC[:, hp, cols], ps[:], [ps], [OACC.s(n)], eng="act")
    else:
        kb.TT(OACC[:, hp, cols], OACC[:, hp, cols], ps[:], ALU.add, [ps], [OACC.s(n)])


def mixer_ret(kb, l):
    P = kb.P
    with P.scope():
        OACC = P.sbuf("r_oacc", [128, 2, S])
        with P.scope():
            lgt = P.sbuf("r_lgt", [128, 8])
            kb.LD(lgt[:], kb.prm["ret_decay_logit"][l].rearrange("d h -> (d h)").partition_broadcast(128), [lgt])
            LG = P.sbuf("r_LG", [128, 8])
            kb.ACT(LG[:], lgt[:], AF.Sigmoid, [lgt], [LG])
            kb.ACT(LG[:], LG[:], AF.Ln, [LG], [LG])
            LGP = P.sbuf("r_LGP", [128, 4])
            for d in range(2):
                for hp in range(2):
                    c = 2 * d + hp
                    kb.CP(LGP[0:64, c:c + 1], LG[0:64, 4 * d + 2 * hp:4 * d + 2 * hp + 1], [LG], [LGP])
                    kb.CP(LGP[64:128, c:c + 1], LG[64:128, 4 * d + 2 * hp + 1:4 * d + 2 * hp + 2], [LG], [LGP])
            MK = [P.sbuf("r_MK%d" % d, [128, 4, 128]) for d in range(2)]
            QDEC = [[P.sbuf("r_QD%d%d" % (d, hp), [128, 128]) for hp in range(2)] for d in range(2)]
            etmp = P.sbuf("r_etmp", [128, 128])
            for d in range(2):
                for h in range(4):
                    kb.ACT(etmp[:], kb.C("DIFF" if d == 0 else "NDIFF"), AF.Exp, [LG], [etmp],
                           scale=LG[:, 4 * d + h:4 * d + h + 1])
                    kb.STT(MK[d][:, h, :], etmp[:], 0.125, kb.C("TRIF" if d == 0 else "TRIB"), ALU.mult, ALU.mult,
                           [etmp], [MK[d]])
                for hp in range(2):
                    kb.ACT(QDEC[d][hp][:], kb.C("IOTAF1" if d == 0 else "RIOTAF"), AF.Exp, [LGP], [QDEC[d][hp]],
                           scale=LGP[:, 2 * d + hp:2 * d + hp + 1])
            KD = P.sbuf("r_KD", [128, 8])
            kb.ACT(KD[:, 0:4], LG[:, 0:4], AF.Exp, [LG], [KD], scale=kb.C("CCOL")[:, 3:4])
            kb.ACT(KD[:, 4:8], LG[:, 4:8], AF.Exp, [LG], [KD], scale=kb.C("CCOL")[:, 2:3])
            kb.TS(KD[:], KD[:], 0.125, None, ALU.mult, None, [KD], [KD])
            CV = P.sbuf("r_CV", [128, 4])
            kb.ACT(CV[:], LGP[:], AF.Exp, [LGP], [CV], scale=128.0)
            qTb = [P.sbuf("r_q%d" % i, [128, 2, 128]) for i in range(2)]
            kTb = [P.sbuf("r_k%d" % i, [128, 2, 128]) for i in range(2)]
            csb = [P.sbuf("r_cs%d" % i, [128, 2, 128]) for i in range(2)]
            Vp = [[P.sbuf("r_vp%d%d" % (i, h), [128, 128]) for h in range(4)] for i in range(2)]
            khp = [[P.sbuf("r_kh%d%d" % (i, h), [128, 128]) for h in range(4)] for i in range(2)]
            for i in range(2):
                for h in range(4):
                    kb.MS(Vp[i][h][:], 0.0, [Vp[i][h]], eng="pool")
                    kb.MS(khp[i][h][:], 0.0, [khp[i][h]], eng="pool")
            t1 = [P.sbuf("r_t1%d" % i, [128, 128]) for i in range(2)]
            t2 = [P.sbuf("r_t2%d" % i, [128, 128]) for i in range(2)]
            qr = [P.sbuf("r_qr%d" % i, [128, 2, 128]) for i in range(2)]
            kr = [P.sbuf("r_kr%d" % i, [128, 2, 128]) for i in range(2)]
            AT = [P.sbuf("r_AT%d" % i, [128, 2, 128]) for i in range(2)]
            qd = [P.sbuf("r_qd%d" % i, [128, 128]) for i in range(2)]
            Sb = [P.sbuf("r_S%d" % hp, [128, 128]) for hp in range(2)]
            ps_r = [P.psum("r_psr%d" % i, [128, 256]) for i in range(2)]
            ps_s = [P.psum("r_pss%d" % i, [128, 2, 128]) for i in range(2)]
            ps_o = [P.psum("r_pso%d" % i, [128, 128]) for i in range(2)]
            ps_k = P.psum("r_psk", [128, 128])
            ps_kv = P.psum("r_pskv", [128, 128])
            it = 0
            for d in range(2):
                for hp in range(2):
                    kb.MS(Sb[hp][:], 0.0, [Sb[hp]])
                for n in ORDER[d]:
                    cols = slice(n * 128, (n + 1) * 128)
                    b = it % 2; it += 1
                    qT, kT, cs = qTb[b], kTb[b], csb[b]
                    kb.LD(qT[:], kb.ZF[Z_RQ:Z_RQ + 256, cols].rearrange("(hp p) t -> p hp t", p=128), [qT])
                    kb.LD(kT[:], kb.ZF[Z_RK:Z_RK + 256, cols].rearrange("(hp p) t -> p hp t", p=128), [kT])
                    kb.LD(cs[:, 0, :], kb.ropec[:, cols], [cs])
                    kb.LD(cs[:, 1, :], kb.ropes[:, cols], [cs])
                    for h in range(4):
                        kb.LD(Vp[b][h][:, 64 * (h % 2):64 * (h % 2) + 64], kb.ZT[cols, 256 + 64 * h:256 + 64 * h + 64],
                              [Vp[b][h]])
                    for hp in range(2):
                        j = (it * 2 + hp) % 2
                        pr = ps_r[j]
                        kb.MM(pr[:, 0:128], kb.C("ROT"), qT[:, hp, :], True, True, [qT], [pr])
                        kb.MM(pr[:, 128:256], kb.C("ROT"), kT[:, hp, :], True, True, [kT], [pr])
                        for (src_, dst, off) in ((qT, qr[b], 0), (kT, kr[b], 128)):
                            kb.TT(t1[j][:], src_[:, hp, :], cs[:, 0, :], ALU.mult, [src_, cs], [t1[j]], eng="pool")
                            kb.TT(t2[j][:], pr[:, off:off + 128], cs[:, 1, :], ALU.mult, [pr, cs], [t2[j]])
                            kb.TT(dst[:, hp, :], t1[j][:], t2[j][:], ALU.add, [t1[j], t2[j]], [dst.s(hp)], eng="pool")
                        pss = ps_s[j]
                        for h2 in range(2):
                            kb.MM(pss[:, h2, :], kr[b][64 * h2:64 * h2 + 64, hp, :], qr[b][64 * h2:64 * h2 + 64, hp, :],
                                  True, True, [kr[b].s(hp), qr[b].s(hp)], [pss])
                        kb.TT(AT[j][:], pss[:], MK[d][:, 2 * hp:2 * hp + 2, :], ALU.mult, [pss, MK[d]], [AT[j]])
                        kb.TT(qd[j][:], qr[b][:, hp, :], QDEC[d][hp][:], ALU.mult, [qr[b].s(hp), QDEC[d][hp]], [qd[j]],
                              eng="pool")
                        po = ps_o[j]
                        kb.MM(po[:], Vp[b][2 * hp][:], AT[j][:, 0, :], True, False, [Vp[b][2 * hp], AT[j]], [po])
                        kb.MM(po[:], Vp[b][2 * hp + 1][:], AT[j][:, 1, :], False, False, [Vp[b][2 * hp + 1], AT[j]], [po])
                        kb.MM(po[:], Sb[hp][:], qd[j][:], False, True, [Sb[hp], qd[j]], [po])
                        oacc_write(kb, OACC, hp, n, po, d)
                        kb.TR(ps_k[:], kr[b][:, hp, :], kb.C("IDENT"), [kr[b].s(hp)], [ps_k])
                        for h2 in range(2):
                            h = 2 * hp + h2
                            kb.ACT(khp[b][h][:, 64 * h2:64 * h2 + 64], ps_k[:, 64 * h2:64 * h2 + 64], AF.Copy,
                                   [ps_k, KD], [khp[b][h]], scale=KD[:, 4 * d + h:4 * d + h + 1])
                        kb.MM(ps_kv[:], khp[b][2 * hp][:], Vp[b][2 * hp][:], True, False,
                              [khp[b][2 * hp], Vp[b][2 * hp]], [ps_kv])
                        kb.MM(ps_kv[:], khp[b][2 * hp + 1][:], Vp[b][2 * hp + 1][:], False, True,
                              [khp[b][2 * hp + 1], Vp[b][2 * hp + 1]], [ps_kv])
                        kb.STT(Sb[hp][:], Sb[hp][:], CV[:, 2 * d + hp:2 * d + hp + 1], ps_kv[:], ALU.mult, ALU.add,
                               [Sb[hp], CV, ps_kv], [Sb[hp]])
        with P.scope():
            finalize_gated(kb, OACC, Z_RG, None, 256, "r_")


def mixer_hgrn(kb, l):
    P = kb.P
    with P.scope():
        OACC = P.sbuf("h_oacc", [128, 2, S])
        with P.scope():
            LB = P.sbuf("h_LB", [128, 4]); OML = P.sbuf("h_OML", [128, 4])
            if l == 0:
                kb.MS(LB[:], 0.0, [LB]); kb.MS(OML[:], 1.0, [OML])
            else:
                lgt = P.sbuf("h_lgt", [128, 8])
                kb.LD(lgt[:], kb.prm["hgrn_lb_logits"][:].rearrange("l d (hp p) -> p (l d hp)", p=128), [lgt],
                      allow_slow_non_contiguous=True)
                kb.TT(LB[:], lgt[:, 4:8], lgt[:, 0:4], ALU.subtract, [lgt], [LB])
                kb.ACT(LB[:], LB[:], AF.Sigmoid, [LB], [LB])
                kb.TS(OML[:], LB[:], -1.0, 1.0, ALU.mult, ALU.add, [LB], [OML])
            G = P.sbuf("h_G", [128, 1])
            for hh in range(2):
                kb.LD(G[64 * hh:64 * hh + 64, :], kb.prm["hgrn_norm_g"][l].rearrange("(p o) -> p o", o=1), [G])
            kb.hgrn_gain = G
            hqb = [P.sbuf("h_q%d" % i, [128, 2, 128]) for i in range(2)]
            hfb = [P.sbuf("h_f%d" % i, [128, 2, 128]) for i in range(2)]
            Vp = [[P.sbuf("h_vp%d%d" % (i, h), [128, 128]) for h in range(4)] for i in range(2)]
            khp = [[P.sbuf("h_kh%d%d" % (i, h), [128, 128]) for h in range(4)] for i in range(2)]
            for i in range(2):
                for h in range(4):
                    kb.MS(Vp[i][h][:], 0.0, [Vp[i][h]], eng="pool")
                    kb.MS(khp[i][h][:], 0.0, [khp[i][h]], eng="pool")
            MREF = [[P.sbuf("h_mr%d%d" % (d, i), [128, 4]) for i in range(2)] for d in range(2)]
            for d in range(2):
                for i in range(2):
                    kb.MS(MREF[d][i][:], 0.0, [MREF[d][i]])

            def two(name, shape=(128, 128)):
                return [P.sbuf("h_%s%d" % (name, i), list(shape)) for i in range(2)]
            qs, sgm, ff, logf, kk, bb, pre = two("qs"), two("sg"), two("ff"), two("lf"), two("kk"), two("bb"), two("pre")
            e1, Ql, e2, Qd = two("e1"), two("Ql"), two("e2"), two("Qd")
            Kt = [two("Kt%d" % r) for r in range(4)]
            ex = two("ex")
            AT = two("AT", (128, 2, 128))
            KhT = two("KhT")
            bend = two("bend", (128, 2))
            Sb = [P.sbuf("h_S%d" % hp, [128, 128]) for hp in range(2)]
            ps_s = [P.psum("h_pss%d" % i, [128, 2, 128]) for i in range(2)]
            ps_o = [P.psum("h_pso%d" % i, [128, 128]) for i in range(2)]
            ps_k = [P.psum("h_psk%d" % i, [128, 128]) for i in range(2)]
            ps_kv = [P.psum("h_pskv%d" % i, [128, 128]) for i in range(2)]
            it = 0
            jj = 0
            for d in range(2):
                zf = Z_HFF if d == 0 else Z_HFB
                for hp in range(2):
                    kb.MS(Sb[hp][:], 0.0, [Sb[hp]])
                for n in ORDER[d]:
                    cols = slice(n * 128, (n + 1) * 128)
                    b = it % 2; it += 1
                    hq, hf = hqb[b], hfb[b]
                    kb.LD(hq[:], kb.ZF[Z_HQ:Z_HQ + 256, cols].rearrange("(hp p) t -> p hp t", p=128), [hq])
                    kb.LD(hf[:], kb.ZF[zf:zf + 256, cols].rearrange("(hp p) t -> p hp t", p=128), [hf])
                    for h in range(4):
                        kb.LD(Vp[b][h][:, 64 * (h % 2):64 * (h % 2) + 64], kb.ZT[cols, 64 * h:64 * h + 64], [Vp[b][h]])
                    for hp in range(2):
                        j = jj % 2; jj += 1
                        c = 2 * d + hp
                        mref = MREF[d][j]
                        kb.ACT(qs[j][:], hq[:, hp, :], AF.Silu, [hq], [qs[j]])
                        kb.ACT(sgm[j][:], hf[:, hp, :], AF.Sigmoid, [hf], [sgm[j]])
                        kb.TS(ff[j][:], sgm[j][:], OML[:, c:c + 1], LB[:, c:c + 1], ALU.mult, ALU.add, [sgm[j], OML, LB], [ff[j]])
                        kb.ACT(logf[j][:], ff[j][:], AF.Ln, [ff[j]], [logf[j]])
                        kb.TS(kk[j][:], ff[j][:], -1.0, 1.0, ALU.mult, ALU.add, [ff[j]], [kk[j]], eng="pool")
                        B = bb[j]
                        if d == 0:
                            kb.SCAN(B[:], kb.C("ONES"), logf[j][:], [logf[j]], [B])
                            kb.CP(mref[:, 1:4], B[:].rearrange("p (r c) -> p r c", c=32)[:, 0:3, 31], [B], [mref])
                            be = B[:, 127:128]
                        else:
                            kb.SCAN(pre[j][:], kb.C("ONES"), logf[j][:], [logf[j]], [pre[j]])
                            kb.STT(B[:], pre[j][:], -1.0, logf[j][:], ALU.mult, ALU.add, [pre[j], logf[j]], [B])
                            kb.TS(B[:], B[:], pre[j][:, 127:128], None, ALU.add, None, [B, pre[j]], [B])
                            kb.CP(mref[:, 0:3], B[:].rearrange("p (r c) -> p r c", c=32)[:, 1:4, 0], [B], [mref])
                            be = B[:, 0:1]
                        kb.TT(e1[j][:].rearrange("p (r c) -> p r c", c=32), B[:].rearrange("p (r c) -> p r c", c=32),
                              mref[:].unsqueeze(2).to_broadcast([128, 4, 32]), ALU.subtract, [B, mref], [e1[j]])
                        kb.ACT(e1[j][:], e1[j][:], AF.Exp, [e1[j]], [e1[j]])
                        kb.STT(Ql[j][:], qs[j][:], 0.125, e1[j][:], ALU.mult, ALU.mult, [qs[j], e1[j]], [Ql[j]], eng="pool")
                        kb.ACT(e2[j][:], B[:], AF.Exp, [B], [e2[j]])
                        kb.STT(Qd[j][:], qs[j][:], 0.125, e2[j][:], ALU.mult, ALU.mult, [qs[j], e2[j]], [Qd[j]], eng="pool")
                        pss = ps_s[j]
                        for r in range(4):
                            kb.ACT(ex[j][:], B[:], AF.Exp, [B, mref], [ex[j]], scale=-1.0, bias=mref[:, r:r + 1])
                            kb.STT(Kt[r][j][:], ex[j][:], 1e26, kk[j][:], ALU.min, ALU.mult, [ex[j], kk[j]], [Kt[r][j]])
                            for h2 in range(2):
                                kb.MM(pss[:, h2, 32 * r:32 * r + 32], Kt[r][j][64 * h2:64 * h2 + 64, :],
                                      Ql[j][64 * h2:64 * h2 + 64, 32 * r:32 * r + 32], True, True,
                                      [Kt[r][j], Ql[j]], [pss])
                        kb.TT(AT[j][:], pss[:], kb.C("TRIF" if d == 0 else "TRIB").unsqueeze(1).to_broadcast([128, 2, 128]),
                              ALU.mult, [pss], [AT[j]])
                        po = ps_o[j]
                        kb.MM(po[:], Vp[b][2 * hp][:], AT[j][:, 0, :], True, False, [Vp[b][2 * hp], AT[j]], [po])
                        kb.MM(po[:], Vp[b][2 * hp + 1][:], AT[j][:, 1, :], False, False, [Vp[b][2 * hp + 1], AT[j]], [po])
                        kb.MM(po[:], Sb[hp][:], Qd[j][:], False, True, [Sb[hp], Qd[j]], [po])
                        oacc_write(kb, OACC, hp, n, po, d)
                        kb.CP(bend[j][:, 0:1], be, [B], [bend[j]])
                        kb.ACT(KhT[j][:], B[:], AF.Exp, [B, bend[j]], [KhT[j]], scale=-1.0, bias=bend[j][:, 0:1])
                        kb.TT(KhT[j][:], KhT[j][:], kk[j][:], ALU.mult, [KhT[j], kk[j]], [KhT[j]], eng="pool")
                        kb.ACT(bend[j][:, 1:2], bend[j][:, 0:1], AF.Exp, [bend[j]], [bend[j]])
                        pk = ps_k[j]
                        kb.TR(pk[:], KhT[j][:], kb.C("IDENT"), [KhT[j]], [pk])
                        for h2 in range(2):
                            h = 2 * hp + h2
                            kb.CP(khp[b][h][:, 64 * h2:64 * h2 + 64], pk[:, 64 * h2:64 * h2 + 64], [pk], [khp[b][h]],
                                  eng=("act" if h2 else "dve"))
                        pkv = ps_kv[j]
                        kb.MM(pkv[:], khp[b][2 * hp][:], Vp[b][2 * hp][:], True, False, [khp[b][2 * hp], Vp[b][2 * hp]], [pkv])
                        kb.MM(pkv[:], khp[b][2 * hp + 1][:], Vp[b][2 * hp + 1][:], False, True,
                              [khp[b][2 * hp + 1], Vp[b][2 * hp + 1]], [pkv])
                        kb.STT(Sb[hp][:], Sb[hp][:], bend[j][:, 1:2], pkv[:], ALU.mult, ALU.add,
                               [Sb[hp], bend[j], pkv], [Sb[hp]])
        with P.scope():
            G = P.sbuf("h_G2", [128, 1])
            for hh in range(2):
                kb.LD(G[64 * hh:64 * hh + 64, :], kb.prm["hgrn_norm_g"][l].rearrange("(p o) -> p o", o=1), [G])
            finalize_gated(kb, OACC, Z_HG, G, 0, "h_")


PI = float(np.pi)


def _sincos(kb, ang, sin_out, cos_out, R, tmp, shape=None):
    P = kb.P
    shp = list(ang.shape)
    with P.scope():
        ki = P.sbuf("sc_ki", shp, mybir.dt.int32)
        kf = P.sbuf("sc_kf", shp)
        r = P.sbuf("sc_r", shp)
        m = P.sbuf("sc_m", shp)
        C1 = 6.28125
        C2 = 2 * PI - C1
        for (shift, out) in ((0.0, sin_out), (PI / 2, cos_out)):
            kb.TS(r[:], ang, shift, None, ALU.add, None, R, [r])
            kb.TS(kf[:], r[:], 1.0 / (2 * PI), None, ALU.mult, None, [r], [kf])
            kb.CP(ki[:], kf[:], [kf], [ki])
            kb.CP(kf[:], ki[:], [ki], [kf])
            kb.STT(r[:], kf[:], -C1, r[:], ALU.mult, ALU.add, [kf, r], [r])
            kb.STT(r[:], kf[:], -C2, r[:], ALU.mult, ALU.add, [kf, r], [r])
            kb.TS(m[:], r[:], PI, 2 * PI, ALU.is_gt, ALU.mult, [r], [m])
            kb.TT(r[:], r[:], m[:], ALU.subtract, [r, m], [r])
            kb.TS(m[:], r[:], -PI, 2 * PI, ALU.is_lt, ALU.mult, [r], [m])
            kb.TT(r[:], r[:], m[:], ALU.add, [r, m], [r])
            kb.ACT(out, r[:], AF.Sin, [r], R)


def mixer_s5(kb, l):
    P = kb.P
    prm = kb.prm
    with P.scope():
        OACC = P.sbuf("s_oacc", [128, 2, S])
        with P.scope():
            WX = P.sbuf("s_WX", [128, 2, 8, 2, 64])
            Cblk = P.sbuf("s_Cblk", [128, 16, 128])
            kb.MS(WX[:], 0.0, [WX], eng="pool")
            kb.MS(Cblk[:], 0.0, [Cblk], eng="pool")
            for g8 in range(8):
                for ri, nm in enumerate(("s5_b_re", "s5_b_im")):
                    for gg in range(2):
                        src = prm[nm][l][8 * gg + g8].rearrange("p c -> c p")
                        kb.LD(WX[16 * g8:16 * g8 + 16, gg, g8, ri, :], src, [WX], allow_slow_non_contiguous=True)
            for g in range(16):
                g8 = g % 8
                kb.LD(Cblk[0:64, g, 16 * g8:16 * g8 + 16], prm["s5_c_re"][l][g].rearrange("c p -> p c"), [Cblk],
                      allow_slow_non_contiguous=True)
                kb.LD(Cblk[64:128, g, 16 * g8:16 * g8 + 16], prm["s5_c_im"][l][g].rearrange("c p -> p c"), [Cblk],
                      allow_slow_non_contiguous=True)
            kb.TS(Cblk[64:128, :, :], Cblk[64:128, :, :], -1.0, None, ALU.mult, None, [Cblk], [Cblk])
            Cb16 = P.sbuf("s_Cb16", [128, 16, 128], BF16)
            kb.CP(Cb16[:], Cblk[:], [Cblk], [Cb16])
            VFr = P.sbuf("s_VFr", [128, 16, 64]); VFi = P.sbuf("s_VFi", [128, 16, 64])
            T1 = P.sbuf("s_T1", [128, 16, 128]); T2 = P.sbuf("s_T2", [128, 16, 128])
            AR = P.sbuf("s_AR", [128, 16]); NAI = P.sbuf("s_NAI", [128, 16])
            for d in range(2):
                with P.scope():
                    lr = P.sbuf("s_lr", [128, 16, 64]); li = P.sbuf("s_li", [128, 16, 64]); dtb = P.sbuf("s_dt", [128, 16])
                    kb.LD(lr[:], prm["s5_lam_re"][l][d].rearrange("g p -> (g p)").partition_broadcast(128), [lr])
                    kb.LD(li[:], prm["s5_lam_im"][l][d].rearrange("g p -> (g p)").partition_broadcast(128), [li])
                    kb.LD(dtb[:], prm["s5_log_dt"][l][d].partition_broadcast(128), [dtb])
                    kb.ACT(dtb[:], dtb[:], AF.Exp, [dtb], [dtb])
                    dt_bc = dtb[:].unsqueeze(2).to_broadcast([128, 16, 64])
                    lrdt = P.sbuf("s_lrdt", [128, 16, 64]); lidt = P.sbuf("s_lidt", [128, 16, 64])
                    kb.TT(lrdt[:], lr[:], dt_bc, ALU.mult, [lr, dtb], [lrdt])
                    kb.TT(lidt[:], li[:], dt_bc, ALU.mult, [li, dtb], [lidt])
                    a = [P.sbuf("s_a%d" % i, [128, 16, 64]) for i in range(8)]
                    mag, ang, sn, cs, tmp, ar, ai, t2 = a
                    kb.ACT(mag[:], lrdt[:], AF.Exp, [lrdt], [mag])
                    _sincos(kb, lidt[:], sn[:], cs[:], [lidt, sn, cs, tmp], tmp[:])
                    kb.TT(ar[:], mag[:], cs[:], ALU.mult, [mag, cs], [ar])
                    kb.TT(ai[:], mag[:], sn[:], ALU.mult, [mag, sn], [ai])
                    den = P.sbuf("s_den", [128, 16, 64]); fr = P.sbuf("s_fr", [128, 16, 64]); fi = P.sbuf("s_fi", [128, 16, 64])
                    kb.TT(den[:], lr[:], lr[:], ALU.mult, [lr], [den])
                    kb.TT(t2[:], li[:], li[:], ALU.mult, [li], [t2])
                    kb.TT(den[:], den[:], t2[:], ALU.add, [den, t2], [den])
                    kb.RECIP(den[:], den[:], [den], [den])
                    kb.TS(ar[:], ar[:], -1.0, None, ALU.add, None, [ar], [ar])
                    kb.TT(fr[:], ar[:], lr[:], ALU.mult, [ar, lr], [fr])
                    kb.TT(t2[:], ai[:], li[:], ALU.mult, [ai, li], [t2])
                    kb.TT(fr[:], fr[:], t2[:], ALU.add, [fr, t2], [fr])
                    kb.TT(fr[:], fr[:], den[:], ALU.mult, [fr, den], [fr])
                    kb.TT(fi[:], ai[:], lr[:], ALU.mult, [ai, lr], [fi])
                    kb.TT(t2[:], ar[:], li[:], ALU.mult, [ar, li], [t2])
                    kb.TT(fi[:], fi[:], t2[:], ALU.subtract, [fi, t2], [fi])
                    kb.TT(fi[:], fi[:], den[:], ALU.mult, [fi, den], [fi])
                    jcol = kb.C("CCOL")[:, 2:3] if d == 0 else kb.C("CCOL")[:, 3:4]
                    njcol = kb.C("CCOL")[:, 6:7] if d == 0 else kb.C("CCOL")[:, 7:8]
                    kb.ACT(mag[:], lrdt[:], AF.Exp, [lrdt], [mag], scale=njcol)
                    kb.TS(ang[:], lidt[:], jcol, None, ALU.mult, None, [lidt], [ang])
                    _sincos(kb, ang[:], sn[:], cs[:], [ang, sn, cs, tmp], tmp[:])
                    vr, vi = ar, ai
                    kb.TT(vr[:], mag[:], cs[:], ALU.mult, [mag, cs], [vr])
                    kb.TT(vi[:], mag[:], sn[:], ALU.mult, [mag, sn], [vi])
                    kb.TS(vi[:], vi[:], -1.0, None, ALU.mult, None, [vi], [vi])
                    kb.TT(VFr[:], vr[:], fr[:], ALU.mult, [vr, fr], [VFr])
                    kb.TT(t2[:], vi[:], fi[:], ALU.mult, [vi, fi], [t2])
                    kb.TT(VFr[:], VFr[:], t2[:], ALU.subtract, [VFr, t2], [VFr])
                    kb.TT(VFi[:], vr[:], fi[:], ALU.mult, [vr, fi], [VFi])
                    kb.TT(t2[:], vi[:], fr[:], ALU.mult, [vi, fr], [t2])
                    kb.TT(VFi[:], VFi[:], t2[:], ALU.add, [VFi, t2], [VFi])
                with P.scope():
                    dtb = P.sbuf("s_dt2", [128, 16])
                    kb.LD(dtb[:], prm["s5_log_dt"][l][d].partition_broadcast(128), [dtb])
                    kb.ACT(dtb[:], dtb[:], AF.Exp, [dtb], [dtb])
                    lrp = P.sbuf("s_lrp", [128, 16]); lip = P.sbuf("s_lip", [128, 16])
                    for hh in range(2):
                        kb.LD(lrp[64 * hh:64 * hh + 64, :], prm["s5_lam_re"][l][d].rearrange("g p -> p g"), [lrp],
                              allow_slow_non_contiguous=True)
                        kb.LD(lip[64 * hh:64 * hh + 64, :], prm["s5_lam_im"][l][d].rearrange("g p -> p g"), [lip],
                              allow_slow_non_contiguous=True)
                    kb.TT(lrp[:], lrp[:], dtb[:], ALU.mult, [lrp, dtb], [lrp])
                    kb.TT(lip[:], lip[:], dtb[:], ALU.mult, [lip, dtb], [lip])
                    b4 = [P.sbuf("s_b%d" % i, [128, 16, 128]) for i in range(4)]
                    arg, sn2, cs2, tmp2 = b4
                    mt = kb.C("IOTAF" if d == 0 else "R127F")
                    mt_bc = mt.unsqueeze(1).to_broadcast([128, 16, 128])
                    kb.TT(arg[:], lrp[:].unsqueeze(2).to_broadcast([128, 16, 128]), mt_bc, ALU.mult, [lrp], [arg])
                    kb.ACT(T1[:], arg[:], AF.Exp, [arg], [T1])
                    kb.TT(arg[:], lip[:].unsqueeze(2).to_broadcast([128, 16, 128]), mt_bc, ALU.mult, [lip, T1], [arg])
                    _sincos(kb, arg[:], sn2[:], cs2[:], [arg, sn2, cs2, tmp2], tmp2[:])
                    kb.TT(T2[:], T1[:], sn2[:], ALU.mult, [T1, sn2], [T2])
                    kb.TS(T2[:], T2[:], -1.0, None, ALU.mult, None, [T2], [T2])
                    kb.TT(T1[:], T1[:], cs2[:], ALU.mult, [T1, cs2], [T1])
                    c4 = [P.sbuf("s_c%d" % i, [128, 16]) for i in range(4)]
                    kb.ACT(c4[0][:], lrp[:], AF.Exp, [lrp], [c4[0]])
                    _sincos(kb, lip[:], c4[1][:], c4[2][:], [lip, c4[1], c4[2], c4[3]], c4[3][:])
                    kb.TT(AR[:], c4[0][:], c4[2][:], ALU.mult, [c4[0], c4[2]], [AR])
                    kb.TT(NAI[:], c4[0][:], c4[1][:], ALU.mult, [c4[0], c4[1]], [NAI])
                    kb.TS(NAI[:], NAI[:], -1.0, None, ALU.mult, None, [NAI], [NAI])
                sweep_scope = P.scope(); sweep_scope.__enter__()
                uTb = [P.sbuf("s_u%d" % i, [128, 2, 128]) for i in range(2)]
                mm_ = [P.sbuf("s_m%d" % i, [128, 8, 64]) for i in range(4)]
                W3 = [P.sbuf("s_W3%d" % i, [128, 8, 3, 64], BF16) for i in range(2)]
                Hb = [P.sbuf("s_Hb%d" % i, [128, 8, 128], BF16) for i in range(2)]
                tri16 = P.sbuf("s_tri16", [128, 128], BF16)
                kb.CP(tri16[:], kb.C("TRIF" if d == 0 else "TRIB"), [], [tri16])
                tP = [P.sbuf("s_tP%d" % i, [128, 8, 128]) for i in range(2)]
                tPs = [P.sbuf("s_tPs%d" % i, [128, 8, 128]) for i in range(2)]
                H1 = [P.sbuf("s_H1%d" % i, [128, 8, 128]) for i in range(2)]
                H2 = [P.sbuf("s_H2%d" % i, [128, 8, 128]) for i in range(2)]
                hend = P.sbuf("s_hend", [128, 16]); hsend = P.sbuf("s_hsend", [128, 16])
                hp_ = P.sbuf("s_hp", [128, 16]); hps_ = P.sbuf("s_hps", [128, 16])
                sm = [P.sbuf("s_sm%d" % i, [128, 16]) for i in range(4)]
                xps = P.psum("s_xps", [128, 1024])
                pps = P.psum("s_pps", [128, 8, 128])
                ppss = P.psum("s_ppss", [128, 8, 128])
                yps = [P.psum("s_yps%d" % i, [128, 128]) for i in range(2)]
                kb.MS(hp_[:], 0.0, [hp_]); kb.MS(hps_[:], 0.0, [hps_])
                te = 127 if d == 0 else 0
                tri = kb.C("TRIF" if d == 0 else "TRIB")
                it = 0
                for n in ORDER[d]:
                    cols = slice(n * 128, (n + 1) * 128)
                    uT = uTb[it % 2]; it += 1
                    kb.LD(uT[:], kb.ZF[Z_SU:Z_SU + 256, cols].rearrange("(gg p) t -> p gg t", p=128), [uT])
                    for gg in range(2):
                        j = gg
                        for half in range(2):
                            kb.MM(xps[:, half * 512:(half + 1) * 512], uT[:, gg, :],
                                  WX[:, gg, half * 4:(half + 1) * 4, :, :].rearrange("q a r p -> q (a r p)"),
                                  True, True, [uT, WX], [xps])
                        xv = xps[:].rearrange("t (g r p) -> t g r p", r=2, p=64)
                        gs = slice(gg * 8, gg * 8 + 8)
                        kb.TT(mm_[0][:], xv[:, :, 0, :], VFr[:, gs, :], ALU.mult, [xps, VFr], [mm_[0]])
                        kb.TT(mm_[1][:], xv[:, :, 1, :], VFi[:, gs, :], ALU.mult, [xps, VFi], [mm_[1]])
                        kb.TT(mm_[2][:], xv[:, :, 0, :], VFi[:, gs, :], ALU.mult, [xps, VFi], [mm_[2]])
                        kb.TT(mm_[3][:], xv[:, :, 1, :], VFr[:, gs, :], ALU.mult, [xps, VFr], [mm_[3]])
                        w3 = W3[j]
                        kb.TT(w3[:, :, 0, :], mm_[0][:], mm_[1][:], ALU.subtract, [mm_[0], mm_[1]], [w3], eng="pool")
                        kb.TT(w3[:, :, 1, :], mm_[2][:], mm_[3][:], ALU.add, [mm_[2], mm_[3]], [w3], eng="pool")
                        kb.TT(w3[:, :, 2, :], mm_[1][:], mm_[0][:], ALU.subtract, [mm_[0], mm_[1]], [w3], eng="pool")
                        for g8 in range(8):
                            kb.MM(pps[:, g8, :], w3[:, g8, 0:2, :].rearrange("q r p -> q (r p)"), tri16[:], True, True, [w3, tri16], [pps])
                            kb.MM(ppss[:, g8, :], w3[:, g8, 1:3, :].rearrange("q r p -> q (r p)"), tri16[:], True, True, [w3, tri16], [ppss])
                        kb.TT(tP[j][:], pps[:], hp_[:, gs].unsqueeze(2).to_broadcast([128, 8, 128]), ALU.add, [pps, hp_], [tP[j]])
                        kb.TT(tPs[j][:], ppss[:], hps_[:, gs].unsqueeze(2).to_broadcast([128, 8, 128]), ALU.add,
                              [ppss, hps_], [tPs[j]])
                        kb.TT(H1[j][:], tP[j][:], T1[:, gs, :], ALU.mult, [tP[j], T1], [H1[j]], eng="pool")
                        kb.TT(H2[j][:], tPs[j][:], T2[:, gs, :], ALU.mult, [tPs[j], T2], [H2[j]])
                        kb.TT(Hb[j][:], H1[j][:], H2[j][:], ALU.add, [H1[j], H2[j]], [Hb[j]], eng="pool")
                        yp = yps[gg]
                        for g8 in range(8):
                            kb.MM(yp[:], Cb16[:, gg * 8 + g8, :], Hb[j][:, g8, :], g8 == 0, g8 == 7, [Cb16, Hb[j]], [yp])
                        oacc_write(kb, OACC, gg, n, yp, d)
                        kb.TT(hend[:, gs], H1[j][:, :, te], H2[j][:, :, te], ALU.add, [H1[j], H2[j]], [hend])
                        kb.TT(sm[0][:, 0:8], tPs[j][:, :, te], T1[:, gs, te], ALU.mult, [tPs[j], T1], [sm[0]])
                        kb.TT(sm[1][:, 0:8], tP[j][:, :, te], T2[:, gs, te], ALU.mult, [tP[j], T2], [sm[1]])
                        kb.TT(hsend[:, gs], sm[0][:, 0:8], sm[1][:, 0:8], ALU.subtract, [sm[0], sm[1]], [hsend])
                    kb.TT(sm[0][:], hend[:], AR[:], ALU.mult, [hend, AR], [sm[0]])
                    kb.TT(sm[1][:], hsend[:], NAI[:], ALU.mult, [hsend, NAI], [sm[1]])
                    kb.TT(sm[2][:], hsend[:], AR[:], ALU.mult, [hsend, AR], [sm[2]])
                    kb.TT(sm[3][:], hend[:], NAI[:], ALU.mult, [hend, NAI], [sm[3]])
                    kb.TT(hp_[:], sm[0][:], sm[1][:], ALU.add, [sm[0], sm[1]], [hp_])
                    kb.TT(hps_[:], sm[2][:], sm[3][:], ALU.subtract, [sm[2], sm[3]], [hps_])
                sweep_scope.__exit__(None, None, None)
        with P.scope():
            dsk = P.sbuf("s_dsk", [128, 2]); glb = P.sbuf("s_glb", [128, 2])
            kb.LD(dsk[:], prm["s5_d"][l].rearrange("(gg p) -> p gg", p=128), [dsk], allow_slow_non_contiguous=True)
            kb.LD(glb[:], prm["s5_glu_b"][l].rearrange("(gg p) -> p gg", p=128), [glb], allow_slow_non_contiguous=True)
            gw = P.sbuf("s_gw", [128, 2, 256])
            kb.LD(gw[:], prm["s5_glu_w"][l].rearrange("(ct p) o -> p ct o", p=128), [gw])
            uTb = [P.sbuf("s_fu%d" % i, [128, 2, 128]) for i in range(2)]
            yy = [P.sbuf("s_yy%d" % i, [128, 2, 128]) for i in range(2)]
            x2 = [P.sbuf("s_x2%d" % i, [128, 2, 128]) for i in range(2)]
            th = [P.sbuf("s_th%d" % i, [128, 2, 128]) for i in range(2)]
            sgb = [P.sbuf("s_sg%d" % i, [128, 128]) for i in range(2)]
            ob = [P.sbuf("s_ob%d" % i, [128, 128]) for i in range(2)]
            psz = [P.psum("s_psz%d" % i, [128, 128]) for i in range(2)]
            k = 0
            for n in range(NT):
                cols = slice(n * 128, (n + 1) * 128)
                i = n % 2
                kb.LD(uTb[i][:], kb.ZF[Z_SU:Z_SU + 256, cols].rearrange("(gg p) t -> p gg t", p=128), [uTb[i]])
                for gg in range(2):
                    kb.STT(yy[i][:, gg, :], uTb[i][:, gg, :], dsk[:, gg:gg + 1], OACC[:, gg, cols], ALU.mult, ALU.add,
                           [uTb[i], dsk, OACC.s(n)], [yy[i]])
                kb.TT(x2[i][:], yy[i][:], yy[i][:], ALU.mult, [yy[i]], [x2[i]], eng="pool")
                kb.TS(x2[i][:], x2[i][:], 0.044715, 1.0, ALU.mult, ALU.add, [x2[i]], [x2[i]])
                kb.TT(x2[i][:], x2[i][:], yy[i][:], ALU.mult, [x2[i], yy[i]], [x2[i]], eng="pool")
                kb.ACT(th[i][:], x2[i][:], AF.Tanh, [x2[i]], [th[i]], scale=0.7978845608028654)
                kb.TS(th[i][:], th[i][:], 1.0, 0.5, ALU.add, ALU.mult, [th[i]], [th[i]])
                kb.TT(yy[i][:], yy[i][:], th[i][:], ALU.mult, [yy[i], th[i]], [yy[i]], eng="pool")
                for ot in range(2):
                    q = k % 2; k += 1
                    for ct in range(2):
                        kb.MM(psz[q][:], gw[:, ct, ot * 128:(ot + 1) * 128], yy[i][:, ct, :], ct == 0, ct == 1, [gw, yy[i]], [psz[q]])
                    kb.ACT(sgb[q][:], psz[q][:], AF.Sigmoid, [psz[q], glb], [sgb[q]], bias=glb[:, ot:ot + 1])
                    kb.TT(ob[q][:], yy[i][:, ot, :], sgb[q][:], ALU.mult, [yy[i], sgb[q]], [ob[q]])
                    kb.ST(kb.YC[768 + ot * 128:768 + (ot + 1) * 128, cols], ob[q][:], [ob[q]])


def gdn_conv(kb, l):
    P = kb.P
    with P.scope():
        CW = P.sbuf("g_cw", [128, 6, 9])
        for kh in range(3):
            for kw in range(3):
                kb.LD(CW[:, :, kh * 3 + kw], kb.prm["gdn_conv_w"][l][kh, kw].rearrange("(ct p) -> p ct", p=128), [CW],
                      allow_slow_non_contiguous=True)
        mlat = P.sbuf("g_mlat", [128, 2, 512]); mctx = P.sbuf("g_mctx", [128, 2, 256])
        kb.LD(mlat[:], kb.cmlat[:], [mlat]); kb.LD(mctx[:], kb.cmctx[:], [mctx])
        Wb = [P.sbuf("g_w%d" % i, [128, 642]) for i in range(2)]
        acc = [[P.sbuf("g_acc%d%d" % (i, j), [128, 512]) for j in range(3)] for i in range(2)]
        sl = [P.sbuf("g_sl%d" % i, [128, 512]) for i in range(2)]
        sq = [P.sbuf("g_sq%d" % i, [128, 512]) for i in range(2)]
        rt = [P.sbuf("g_rt%d" % i, [128, 512]) for i in range(2)]
        ps = [P.psum("g_psn%d" % i, [128, 512]) for i in range(2)]
        spans = [(0, 256, True)] + [(256 + 512 * k, 512, False) for k in range(8)]
        it = 0
        for (t0, L, is_ctx) in spans:
            lo = 0 if is_ctx else 256
            hi = 256 if is_ctx else S
            a = max(lo, t0 - 65); b = min(hi, t0 + L + 65)
            for ct in range(6):
                i = it % 2; it += 1
                W = Wb[i]
                kb.MS(W[:], 0.0, [W], eng="pool")
                kb.LD(W[:, 65 + (a - t0):65 + (b - t0)], kb.ZF[Z_GQKV + ct * 128:Z_GQKV + (ct + 1) * 128, a:b], [W])
                rows = (1,) if is_ctx else (0, 1, 2)
                masks = mctx if is_ctx else mlat
                for dwi, shift in enumerate((-1, 0, 1)):
                    A = acc[i][dwi]
                    eng = "dve"
                    for q, dh in enumerate(rows):
                        o0 = 65 + 64 * (dh - 1) + shift
                        src = W[:, o0:o0 + L]
                        wcol = CW[:, ct, dh * 3 + dwi:dh * 3 + dwi + 1]
                        if q == 0:
                            kb.TS(A[:, :L], src, wcol, None, ALU.mult, None, [W, CW], [A], eng=("pool" if dwi != 1 else "dve"))
                        else:
                            kb.STT(A[:, :L], src, wcol, A[:, :L], ALU.mult, ALU.add, [W, CW, A], [A])
                    if dwi != 1:
                        mi = 0 if dwi == 0 else 1
                        kb.TT(A[:, :L], A[:, :L], masks[:, mi, :L], ALU.mult, [A, masks], [A], eng="pool")
                A0, A1, A2 = acc[i]
                kb.TT(A1[:, :L], A1[:, :L], A0[:, :L], ALU.add, [A0, A1], [A1], eng="pool")
                kb.TT(A1[:, :L], A1[:, :L], A2[:, :L], ALU.add, [A1, A2], [A1], eng="pool")
                kb.ACT(sl[i][:, :L], A1[:, :L], AF.Silu, [A1], [sl[i]])
                if ct < 4:
                    kb.TT(sq[i][:, :L], sl[i][:, :L], sl[i][:, :L], ALU.mult, [sl[i]], [sq[i]], eng="pool")
                    kb.MM(ps[i][:, :L], kb.C("BLK64"), sq[i][:, :L], True, True, [sq[i]], [ps[i]])
                    kb.ACT(rt[i][:, :L], ps[i][:, :L], AF.Sqrt, [ps[i]], [rt[i]], bias=kb.C("CCOL")[:, 0:1])
                    kb.RECIP(rt[i][:, :L], rt[i][:, :L], [rt[i]], [rt[i]])
                    if ct < 2:
                        kb.STT(sl[i][:, :L], sl[i][:, :L], 0.125, rt[i][:, :L], ALU.mult, ALU.mult, [sl[i], rt[i]], [sl[i]])
                    else:
                        kb.TT(sl[i][:, :L], sl[i][:, :L], rt[i][:, :L], ALU.mult, [sl[i], rt[i]], [sl[i]])
                kb.ST(kb.QKVF[ct * 128:(ct + 1) * 128, t0:t0 + L], sl[i][:, :L], [sl[i]])


def mixer_gdn(kb, l):
    P = kb.P
    gdn_conv(kb, l)
    upto = kb.cfg.get("gdn_upto", 99)
    if upto < 1:
        return
    with P.scope():
        OACC = P.sbuf("g_oacc", [128, 2, S])
        with P.scope():
            DTB = P.sbuf("g_dtb", [128, 8]); NEGA = P.sbuf("g_nega", [128, 8])
            kb.LD(DTB[:], kb.prm["gdn_dt_bias"][l].rearrange("d h -> (d h)").partition_broadcast(128), [DTB])
            kb.LD(NEGA[:], kb.prm["gdn_a_log"][l].rearrange("d h -> (d h)").partition_broadcast(128), [NEGA])
            kb.ACT(NEGA[:], NEGA[:], AF.Exp, [NEGA], [NEGA])
            kb.TS(NEGA[:], NEGA[:], -1.0, None, ALU.mult, None, [NEGA], [NEGA])
            qnb = [P.sbuf("g_q%d" % i, [128, 2, 128]) for i in range(2)]
            knb = [P.sbuf("g_k%d" % i, [128, 2, 128]) for i in range(2)]
            vvb = [P.sbuf("g_v%d" % i, [128, 2, 128]) for i in range(2)]
            gabb = [P.sbuf("g_gab%d" % i, [128, 16]) for i in range(2)]

            def sm4(name, w=4):
                return P.sbuf("g_" + name, [128, w])
            xa, ea, loga, beta, lnb = sm4("xa"), sm4("ea"), sm4("loga"), sm4("beta"), sm4("lnb")
            gtm, ngt, ekr, cdec, eg, beg, gpl = sm4("gtm"), sm4("ngt"), sm4("ekr"), sm4("cdec"), sm4("eg"), sm4("beg"), sm4("gpl")
            ROWS = P.sbuf("g_rows", [4, 384])
            LI = P.sbuf("g_LI", [128, 4, 128]); LBT = P.sbuf("g_LBT", [128, 4, 128]); LBm = P.sbuf("g_LB", [128, 4, 128])
            NAT = P.sbuf("g_NAT", [128, 4, 128]); NA = P.sbuf("g_NA", [128, 4, 128]); QKm = P.sbuf("g_QKm", [128, 4, 128])
            Tm = P.sbuf("g_Tm", [128, 4, 128]); Wm = P.sbuf("g_Wm", [128, 4, 128])
            x1 = P.sbuf("g_x1", [128, 4, 128]); y1 = P.sbuf("g_y1", [128, 4, 128])
            tmx = P.sbuf("g_tmx", [128, 4, 128]); tmy = P.sbuf("g_tmy", [128, 4, 128])
            Rm = [P.sbuf("g_R%d" % h, [128, 128]) for h in range(4)]
            khp = [P.sbuf("g_kh%d" % h, [128, 128]) for h in range(4)]
            vnp = [P.sbuf("g_vn%d" % h, [128, 128]) for h in range(4)]
            for h in range(4):
                kb.MS(khp[h][:], 0.0, [khp[h]], eng="pool")
                kb.MS(vnp[h][:], 0.0, [vnp[h]], eng="pool")
            upair = [P.sbuf("g_up%d" % hp, [128, 128]) for hp in range(2)]
            wTp = [P.sbuf("g_wT%d" % hp, [128, 128]) for hp in range(2)]
            EG = [P.sbuf("g_EG%d" % hp, [128, 128]) for hp in range(2)]
            qd = [P.sbuf("g_qd%d" % hp, [128, 128]) for hp in range(2)]
            cdp = [P.sbuf("g_cdp%d" % hp, [128, 1]) for hp in range(2)]
            Sb = [P.sbuf("g_S%d" % hp, [128, 128]) for hp in range(2)]
            B = [P.psum("g_B%d" % i, [128, 512]) for i in range(8)]
            ident = kb.C("IDENT")
            it = 0
            for d in range(2):
                tri = kb.C("TRIF" if d == 0 else "TRIB")
                rem = kb.C("SUFF" if d == 0 else "PREB")
                n_incl = kb.C("NLE" if d == 0 else "NGE")
                n_strT = kb.C("NLT" if d == 0 else "NGT")
                n_str = kb.C("NGT" if d == 0 else "NLT")
                for hp in range(2):
                    kb.MS(Sb[hp][:], 0.0, [Sb[hp]])
                for n in ORDER[d][:kb.cfg.get("ntiles", NT)]:
                    cols = slice(n * 128, (n + 1) * 128)
                    b = it % 2; it += 1
                    qn, kn, vv, gab = qnb[b], knb[b], vvb[b], gabb[b]
                    kb.LD(qn[:], kb.QKVF[0:256, cols].rearrange("(hp p) t -> p hp t", p=128), [qn])
                    kb.LD(kn[:], kb.QKVF[256:512, cols].rearrange("(hp p) t -> p hp t", p=128), [kn])
                    kb.LD(vv[:], kb.QKVF[512:768, cols].rearrange("(hp p) t -> p hp t", p=128), [vv])
                    kb.LD(gab[:], kb.ZT[cols, 512:528], [gab])
                    kb.TT(xa[:], gab[:, 4 * d:4 * d + 4], DTB[:, 4 * d:4 * d + 4], ALU.add, [gab, DTB], [xa])
                    kb.ACT(ea[:], xa[:], AF.Exp, [xa], [ea])
                    kb.ACT(ea[:], ea[:], AF.Ln, [ea], [ea], bias=kb.C("CCOL")[:, 1:2])
                    kb.TT(loga[:], ea[:], NEGA[:, 4 * d:4 * d + 4], ALU.mult, [ea, NEGA], [loga])
                    kb.ACT(beta[:], gab[:, 8 + 4 * d:12 + 4 * d], AF.Sigmoid, [gab], [beta])
                    kb.ACT(lnb[:], beta[:], AF.Ln, [beta], [lnb])
                    kb.MM(B[0][:, 0:4], tri, loga[:], True, True, [loga], [B[0]])
                    kb.MM(B[0][:, 4:8], rem, loga[:], True, True, [loga], [B[0]])
                    kb.MM(B[0][:, 8:12], kb.C("ONES"), loga[:], True, True, [loga], [B[0]])
                    kb.CP(gtm[:], B[0][:, 0:4], [B[0]], [gtm])
                    kb.TS(ngt[:], B[0][:, 0:4], -1.0, None, ALU.mult, None, [B[0]], [ngt])
                    kb.ACT(ekr[:], B[0][:, 4:8], AF.Exp, [B[0]], [ekr])
                    kb.ACT(cdec[:], B[0][:, 8:12], AF.Exp, [B[0]], [cdec])
                    kb.ACT(eg[:], gtm[:], AF.Exp, [gtm], [eg])
                    kb.TT(beg[:], beta[:], eg[:], ALU.mult, [beta, eg], [beg])
                    kb.TT(gpl[:], gtm[:], lnb[:], ALU.add, [gtm, lnb], [gpl])
                    kb.MM(B[1][0:4, 0:128], loga[:], tri, True, True, [loga], [B[1]])
                    kb.MM(B[1][0:4, 128:256], loga[:], tri, True, False, [loga], [B[1]])
                    kb.MM(B[1][0:4, 128:256], lnb[:], ident, False, True, [lnb], [B[1]])
                    kb.CP(ROWS[:, 0:256], B[1][0:4, 0:256], [B[1]], [ROWS])
                    kb.TS(ROWS[:, 256:384], B[1][0:4, 0:128], -1.0, None, ALU.mult, None, [B[1]], [ROWS])
                    if upto < 2:
                        continue
                    for (dst, rsl, negm, bias_t, bank) in ((LI, slice(0, 128), n_incl, ngt, B[2]),
                                                           (LBT, slice(128, 256), n_strT, ngt, B[3]),
                                                           (LBm, slice(256, 384), n_str, gpl, B[2])):
                        for h in range(4):
                            kb.MM(bank[:, h * 128:(h + 1) * 128], kb.C("SELH%d" % h)[0:4, :], ROWS[:, rsl], True, False,
                                  [ROWS], [bank])
                            kb.MM(bank[:, h * 128:(h + 1) * 128], ident, negm, False, True, [], [bank])
                        for h in range(4):
                            kb.ACT(dst[:, h, :], bank[:, h * 128:(h + 1) * 128], AF.Exp, [bank, bias_t], [dst],
                                   bias=bias_t[:, h:h + 1])
                    if upto < 3:
                        continue
                    for h in range(4):
                        hp, h2 = divmod(h, 2)
                        ksl = kn[64 * h2:64 * h2 + 64, hp, :]
                        kb.MM(B[4][:, h * 128:(h + 1) * 128], ksl, ksl, True, True, [kn], [B[4]])
                        kb.MM(B[5][:, h * 128:(h + 1) * 128], ksl, qn[64 * h2:64 * h2 + 64, hp, :], True, True, [kn, qn], [B[5]])
                    b4v = B[4][:].rearrange("p (h t) -> p h t", h=4)
                    b5v = B[5][:].rearrange("p (h t) -> p h t", h=4)
                    kb.STT(NAT[:], b4v, -1.0, LBT[:], ALU.mult, ALU.mult, [B[4], LBT], [NAT])
                    kb.STT(NA[:], b4v, -1.0, LBm[:], ALU.mult, ALU.mult, [B[4], LBm], [NA])
                    kb.TT(QKm[:], b5v, LI[:], ALU.mult, [B[5], LI], [QKm])
                    if upto < 4:
                        continue
                    idb = ident.unsqueeze(1).to_broadcast([128, 4, 128])
                    kb.CP(Tm[:], idb, [], [Tm])
                    kb.CP(Wm[:], idb, [], [Wm], eng="pool")
                    for s_ in (1, 2, 4, 8, 16, 32, 64):
                        mT = kb.C(("MOFF%d" if d == 0 else "MOFFT%d") % s_).unsqueeze(1).to_broadcast([128, 4, 128])
                        mW = kb.C(("MOFFT%d" if d == 0 else "MOFF%d") % s_).unsqueeze(1).to_broadcast([128, 4, 128])
                        for h in range(4):
                            kb.MM(B[2][:, h * 128:(h + 1) * 128], NAT[:, h, :], Tm[:, h, :], True, True, [NAT, Tm], [B[2]])
                        for h in range(4):
                            kb.MM(B[3][:, h * 128:(h + 1) * 128], NA[:, h, :], Wm[:, h, :], True, True, [NA, Wm], [B[3]])
                        kb.CP(x1[:], B[2][:].rearrange("p (h t) -> p h t", h=4), [B[2]], [x1], eng="act")
                        kb.CP(y1[:], B[3][:].rearrange("p (h t) -> p h t", h=4), [B[3]], [y1], eng="dve")
                        for h in range(4):
                            kb.MM(B[4][:, h * 128:(h + 1) * 128], Wm[:, h, :], x1[:, h, :], True, True, [Wm, x1], [B[4]])
                        for h in range(4):
                            kb.MM(B[5][:, h * 128:(h + 1) * 128], Tm[:, h, :], y1[:, h, :], True, True, [Tm, y1], [B[5]])
                        kb.TT(tmx[:], B[4][:].rearrange("p (h t) -> p h t", h=4), mT, ALU.mult, [B[4]], [tmx])
                        kb.TT(tmy[:], B[5][:].rearrange("p (h t) -> p h t", h=4), mW, ALU.mult, [B[5]], [tmy])
                        kb.TT(Tm[:], Tm[:], tmx[:], ALU.add, [Tm, tmx], [Tm], eng="pool")
                        kb.TT(Wm[:], Wm[:], tmy[:], ALU.add, [Wm, tmy], [Wm], eng="pool")
                    if upto < 5:
                        continue
                    for hp in range(2):
                        kb.TR(B[0][:, 128:256], kn[:, hp, :], ident, [kn], [B[0]])
                        kb.TR(B[0][:, 256:384], vv[:, hp, :], ident, [vv], [B[0]])
                        for h2 in range(2):
                            h = 2 * hp + h2
                            kc = slice(64 * h2, 64 * h2 + 64)
                            vc = slice(64 * (1 - h2), 64 * (1 - h2) + 64)
                            kb.TS(Rm[h][:, kc], B[0][:, 128 + 64 * h2:128 + 64 * h2 + 64], beg[:, h:h + 1], None, ALU.mult, None,
                                  [B[0], beg], [Rm[h]])
                            kb.ACT(Rm[h][:, vc], B[0][:, 256 + 64 * h2:256 + 64 * h2 + 64], AF.Copy, [B[0], beta], [Rm[h]],
                                   scale=beta[:, h:h + 1])
                            kb.ACT(khp[h][:, kc], B[0][:, 128 + 64 * h2:128 + 64 * h2 + 64], AF.Copy, [B[0], ekr], [khp[h]],
                                   scale=ekr[:, h:h + 1])
                    if upto < 6:
                        continue
                    for h in range(4):
                        kb.MM(B[2][:, h * 128:(h + 1) * 128], Wm[:, h, :], Rm[h][:], True, True, [Wm, Rm[h]], [B[2]])
                        kb.MM(B[3][:, h * 128:(h + 1) * 128], Rm[h][:], Wm[:, h, :], True, True, [Wm, Rm[h]], [B[3]])
                    for h in range(4):
                        hp, h2 = divmod(h, 2)
                        vc0 = 64 * (1 - h2)
                        kb.CP(upair[hp][:, 64 * h2:64 * h2 + 64], B[2][:, h * 128 + vc0:h * 128 + vc0 + 64], [B[2]], [upair[hp]],
                              eng=("act" if h2 else "dve"))
                        kb.CP(wTp[hp][64 * h2:64 * h2 + 64, :], B[3][64 * h2:64 * h2 + 64, h * 128:(h + 1) * 128], [B[3]], [wTp[hp]],
                              eng=("dve" if h2 else "act"))
                    if upto < 7:
                        continue
                    for hp in range(2):
                        kb.MM(B[1][:, 256:384], kb.C("SELP%d" % hp)[0:4, :], ROWS[:, 0:128], True, True, [ROWS], [B[1]])
                        kb.ACT(EG[hp][:], B[1][:, 256:384], AF.Exp, [B[1]], [EG[hp]])
                        kb.TT(qd[hp][:], qn[:, hp, :], EG[hp][:], ALU.mult, [qn, EG[hp]], [qd[hp]], eng="pool")
                        pws = B[7][:, hp * 128:(hp + 1) * 128]
                        kb.MM(pws, wTp[hp][:], Sb[hp][:], True, True, [wTp[hp], Sb[hp]], [B[7]])
                        for h2 in range(2):
                            h = 2 * hp + h2
                            cs_ = slice(64 * h2, 64 * h2 + 64)
                            kb.TT(vnp[h][:, cs_], upair[hp][:, cs_], B[7][:, hp * 128 + 64 * h2:hp * 128 + 64 * h2 + 64],
                                  ALU.subtract, [upair[hp], B[7]], [vnp[h]])
                        po = B[6][:, hp * 256:hp * 256 + 128]
                        kb.MM(po, Sb[hp][:], qd[hp][:], True, False, [Sb[hp], qd[hp]], [B[6].s(hp)])
                        kb.MM(po, vnp[2 * hp][:], QKm[:, 2 * hp, :], False, False, [vnp[2 * hp], QKm], [B[6].s(hp)])
                        kb.MM(po, vnp[2 * hp + 1][:], QKm[:, 2 * hp + 1, :], False, True, [vnp[2 * hp + 1], QKm], [B[6].s(hp)])
                        cols_ = slice(n * 128, (n + 1) * 128)
                        if d == 0:
                            kb.CP(OACC[:, hp, cols_], po, [B[6].s(hp)], [OACC.s(n)], eng="act")
                        else:
                            kb.TT(OACC[:, hp, cols_], OACC[:, hp, cols_], po, ALU.add, [B[6].s(hp)], [OACC.s(n)])
                        pkv = B[6][:, hp * 256 + 128:hp * 256 + 256]
                        kb.MM(pkv, khp[2 * hp][:], vnp[2 * hp][:], True, False, [khp[2 * hp], vnp[2 * hp]], [B[6].s(2 + hp)])
                        kb.MM(pkv, khp[2 * hp + 1][:], vnp[2 * hp + 1][:], False, True, [khp[2 * hp + 1], vnp[2 * hp + 1]],
                              [B[6].s(2 + hp)])
                        kb.CP(cdp[hp][0:64, :], cdec[0:64, 2 * hp:2 * hp + 1], [cdec], [cdp[hp]])
                        kb.CP(cdp[hp][64:128, :], cdec[64:128, 2 * hp + 1:2 * hp + 2], [cdec], [cdp[hp]])
                        kb.STT(Sb[hp][:], Sb[hp][:], cdp[hp][:, 0:1], pkv, ALU.mult, ALU.add,
                               [Sb[hp], cdp[hp], B[6].s(2 + hp)], [Sb[hp]])
        with P.scope():
            G = P.sbuf("g_G2", [128, 1])
            for hh in range(2):
                kb.LD(G[64 * hh:64 * hh + 64, :], kb.prm["gdn_norm_g"][l].rearrange("(p o) -> p o", o=1), [G])
            finalize_gated(kb, OACC, Z_GG, G, 512, "g_")


class _Ctx:
    pass


def mixer_gdn2(kb, l):
    P = kb.P
    (gdn_conv if kb.cfg.get('conv_old') else gdn_conv2)(kb, l)
    with P.scope():
        OACC = P.sbuf("g_oacc", [128, 2, S])
        kb.MS(OACC[:, 0, :], 0.0, [OACC.s(n) for n in range(NT)], eng="pool")
        kb.MS(OACC[:, 1, :], 0.0, [OACC.s(n) for n in range(NT)], eng="pool")
        with P.scope():
            DTB = P.sbuf("g_dtb", [128, 8]); NEGA = P.sbuf("g_nega", [128, 8])
            kb.LD(DTB[:], kb.prm["gdn_dt_bias"][l].rearrange("d h -> (d h)").partition_broadcast(128), [DTB])
            kb.LD(NEGA[:], kb.prm["gdn_a_log"][l].rearrange("d h -> (d h)").partition_broadcast(128), [NEGA])
            kb.ACT(NEGA[:], NEGA[:], AF.Exp, [NEGA], [NEGA])
            kb.TS(NEGA[:], NEGA[:], -1.0, None, ALU.mult, None, [NEGA], [NEGA])
            ident = kb.C("IDENT")
            idb = ident.unsqueeze(1).to_broadcast([128, 4, 128])
            cxs = []
            for d in range(2):
                cx = _Ctx()
                cx.d = d
                pf = "g%d_" % d
                cx.qnb = [P.sbuf(pf + "q%d" % i, [128, 2, 128]) for i in range(2)]
                cx.knb = [P.sbuf(pf + "k%d" % i, [128, 2, 128]) for i in range(2)]
                cx.vvb = [P.sbuf(pf + "v%d" % i, [128, 2, 128]) for i in range(2)]
                cx.gabb = [P.sbuf(pf + "gab%d" % i, [128, 16]) for i in range(2)]
                for nm in ("xa", "ea", "loga", "beta", "lnb", "gtm", "ngt", "ekr", "cdec", "eg", "beg", "gpl"):
                    setattr(cx, nm, P.sbuf(pf + nm, [128, 4]))
                cx.ROWS = P.sbuf(pf + "rows", [4, 384])
                for nm in ("LI", "LBT", "LBm"):
                    setattr(cx, nm, P.sbuf(pf + nm, [128, 4, 128]))
                cx.QKm = P.sbuf(pf + "QKm", [128, 4, 128], BF16)
                cx.ROWSX = P.sbuf(pf + "rowsx", [4, 3, 4, 128])
                cx.knp = [[P.sbuf(pf + "knp%d%d" % (i, h), [128, 128], BF16) for h in range(4)] for i in range(2)]
                for i in range(2):
                    for h in range(4):
                        kb.MS(cx.knp[i][h][:], 0.0, [cx.knp[i][h]], eng="pool")
                cx.kq16 = [P.sbuf(pf + "kq16%d" % i, [128, 2, 2, 128], BF16) for i in range(2)]
                for nm in ("NAT", "NA", "Tm", "Wm", "x1", "y1", "tmx", "tmy"):
                    setattr(cx, nm, P.sbuf(pf + nm, [128, 4, 128], BF16))
                cx.Rm = [P.sbuf(pf + "R%d" % h, [128, 128], BF16) for h in range(4)]
                cx.khp = [P.sbuf(pf + "kh%d" % h, [128, 128], BF16) for h in range(4)]
                cx.vnp = [P.sbuf(pf + "vn%d" % h, [128, 128], BF16) for h in range(4)]
                for h in range(4):
                    kb.MS(cx.khp[h][:], 0.0, [cx.khp[h]], eng="pool")
                    kb.MS(cx.vnp[h][:], 0.0, [cx.vnp[h]], eng="pool")
                cx.upair = [P.sbuf(pf + "up%d" % hp, [128, 128]) for hp in range(2)]
                cx.wTp = [P.sbuf(pf + "wT%d" % hp, [128, 128]) for hp in range(2)]
                cx.EG = [P.sbuf(pf + "EG%d" % hp, [128, 128]) for hp in range(2)]
                cx.qd = [P.sbuf(pf + "qd%d" % hp, [128, 128]) for hp in range(2)]
                cx.cdp = [P.sbuf(pf + "cdp%d" % hp, [128, 1]) for hp in range(2)]
                cx.Sb = [P.sbuf(pf + "S%d" % hp, [128, 128]) for hp in range(2)]
                for hp in range(2):
                    kb.MS(cx.Sb[hp][:], 0.0, [cx.Sb[hp]])
                cx.B = [P.psum(pf + "B%d" % i, [128, 512]) for i in range(4)]
                cx.tri = kb.C("TRIF" if d == 0 else "TRIB")
                cx.rem = kb.C("SUFF" if d == 0 else "PREB")
                cx.n_incl = kb.C("NLE" if d == 0 else "NGE")
                cx.n_strT = kb.C("NLT" if d == 0 else "NGT")
                cx.n_str = kb.C("NGT" if d == 0 else "NLT")
                cx.it = 0
                cx.id16 = P.sbuf(pf + "id16", [128, 128], BF16)
                kb.CP(cx.id16[:], ident, [], [cx.id16])
                for nm_, cn in (("n_incl4", "NLE" if d == 0 else "NGE"), ("n_strT4", "NLT" if d == 0 else "NGT"),
                                ("n_str4", "NGT" if d == 0 else "NLT")):
                    t_ = P.sbuf(pf + nm_, [128, 4, 128], BF16)
                    kb.CP(t_[:], kb.C(cn).unsqueeze(1).to_broadcast([128, 4, 128]), [], [t_])
                    setattr(cx, nm_, t_[:].rearrange("p h i -> p (h i)"))
                bd = P.sbuf(pf + "bd4", [4, 4, 128])
                for h in range(4):
                    kb.CP(bd[:, h, :], kb.C("SELH%d" % h)[0:4, :], [], [bd])
                cx.bd4 = bd[:]
                cx.mT = {}; cx.mW = {}
                for s_ in (2, 4, 8, 16, 32, 64):
                    for nm_, dct, cn in (("mT", cx.mT, ("MOFF%d" if d == 0 else "MOFFT%d") % s_),
                                         ("mW", cx.mW, ("MOFFT%d" if d == 0 else "MOFF%d") % s_)):
                        mt_ = P.sbuf(pf + nm_ + str(s_), [128, 4, 128], mybir.dt.uint8)
                        kb.CP(mt_[:], kb.C(cn).unsqueeze(1).to_broadcast([128, 4, 128]), [], [mt_])
                        dct[s_] = mt_
                cxs.append(cx)

            def step(cx, n):
                d = cx.d
                Pa, Pb, Pc, Pd = cx.B
                cols = slice(n * 128, (n + 1) * 128)
                b = cx.it % 2; cx.it += 1
                qn, kn, vv, gab = cx.qnb[b], cx.knb[b], cx.vvb[b], cx.gabb[b]
                xa, ea, loga, beta, lnb = cx.xa, cx.ea, cx.loga, cx.beta, cx.lnb
                gtm, ngt, ekr, cdec, eg, beg, gpl = cx.gtm, cx.ngt, cx.ekr, cx.cdec, cx.eg, cx.beg, cx.gpl
                ROWS, LI, LBT, LBm, NAT, NA, QKm = cx.ROWS, cx.LI, cx.LBT, cx.LBm, cx.NAT, cx.NA, cx.QKm
                Tm, Wm, x1, y1, tmx, tmy = cx.Tm, cx.Wm, cx.x1, cx.y1, cx.tmx, cx.tmy
                Rm, khp, vnp, upair, wTp, EG, qd, cdp, Sb = cx.Rm, cx.khp, cx.vnp, cx.upair, cx.wTp, cx.EG, cx.qd, cx.cdp, cx.Sb
                tri = cx.tri
                kb.LD(qn[:], kb.QKVF[0:256, cols].rearrange("(hp p) t -> p hp t", p=128), [qn])
                kb.LD(kn[:], kb.QKVF[256:512, cols].rearrange("(hp p) t -> p hp t", p=128), [kn])
                kb.LD(vv[:], kb.QKVF[512:768, cols].rearrange("(hp p) t -> p hp t", p=128), [vv])
                kb.LD(gab[:], kb.ZT[cols, 512:528], [gab])
                knp = cx.knp[b]; kq16 = cx.kq16[b]
                for h in range(4):
                    kb.LD(knp[h][64 * (h % 2):64 * (h % 2) + 64, :], kb.QKVF[256 + 64 * h:256 + 64 * h + 64, cols], [knp[h]], q="pool")
                kb.LD(kq16[:, 0, :, :], kb.QKVF[256:512, cols].rearrange("(hp p) t -> p hp t", p=128), [kq16], q="pool")
                kb.LD(kq16[:, 1, :, :], kb.QKVF[0:256, cols].rearrange("(hp p) t -> p hp t", p=128), [kq16], q="pool")
                kb.TT(xa[:], gab[:, 4 * d:4 * d + 4], DTB[:, 4 * d:4 * d + 4], ALU.add, [gab, DTB], [xa])
                kb.ACT(ea[:], xa[:], AF.Exp, [xa], [ea])
                kb.ACT(ea[:], ea[:], AF.Ln, [ea], [ea], bias=kb.C("CCOL")[:, 1:2])
                kb.TT(loga[:], ea[:], NEGA[:, 4 * d:4 * d + 4], ALU.mult, [ea, NEGA], [loga])
                kb.ACT(beta[:], gab[:, 8 + 4 * d:12 + 4 * d], AF.Sigmoid, [gab], [beta])
                kb.ACT(lnb[:], beta[:], AF.Ln, [beta], [lnb])
                kb.MM(Pc[:, 0:4], tri, loga[:], True, True, [loga], [Pc])
                kb.MM(Pc[:, 4:8], cx.rem, loga[:], True, True, [loga], [Pc])
                kb.MM(Pc[:, 8:12], kb.C("ONES"), loga[:], True, True, [loga], [Pc])
                kb.CP(gtm[:], Pc[:, 0:4], [Pc], [gtm])
                kb.TS(ngt[:], Pc[:, 0:4], -1.0, None, ALU.mult, None, [Pc], [ngt])
                kb.ACT(ekr[:], Pc[:, 4:8], AF.Exp, [Pc], [ekr])
                kb.ACT(cdec[:], Pc[:, 8:12], AF.Exp, [Pc], [cdec])
                kb.ACT(eg[:], gtm[:], AF.Exp, [gtm], [eg])
                kb.TT(beg[:], beta[:], eg[:], ALU.mult, [beta, eg], [beg])
                kb.TT(gpl[:], gtm[:], lnb[:], ALU.add, [gtm, lnb], [gpl])
                kb.MM(Pd[0:4, 0:128], loga[:], tri, True, True, [loga], [Pd])
                kb.MM(Pd[0:4, 128:256], loga[:], tri, True, False, [loga], [Pd])
                kb.MM(Pd[0:4, 128:256], lnb[:], ident, False, True, [lnb], [Pd])
                kb.CP(ROWS[:, 0:256], Pd[0:4, 0:256], [Pd], [ROWS])
                kb.TS(ROWS[:, 256:384], Pd[0:4, 0:128], -1.0, None, ALU.mult, None, [Pd], [ROWS])
                yield
                kb.TT(cx.ROWSX[:], ROWS[:].rearrange("c (r i) -> c r i", r=3).unsqueeze(2).to_broadcast([4, 3, 4, 128]),
                      cx.bd4.unsqueeze(1).to_broadcast([4, 3, 4, 128]), ALU.mult, [ROWS], [cx.ROWSX])
                for (dst, ri, negm4, bias_t, bank) in ((LI, 0, cx.n_incl4, ngt, Pa), (LBT, 1, cx.n_strT4, ngt, Pb),
                                                       (LBm, 2, cx.n_str4, gpl, Pa)):
                    kb.MM(bank[:], kb.C("ONES")[0:4, :], cx.ROWSX[:, ri, :, :].rearrange("c h i -> c (h i)"), True, False,
                          [cx.ROWSX], [bank])
                    kb.MM(bank[:], cx.id16[:], negm4[:], False, True, [], [bank])
                    yield
                    for h in range(4):
                        kb.ACT(dst[:, h, :], bank[:, h * 128:(h + 1) * 128], AF.Exp, [bank, bias_t], [dst], bias=bias_t[:, h:h + 1])
                    yield
                for h in range(4):
                    hp, h2 = divmod(h, 2)
                    kb.MM(Pa[:, h * 128:(h + 1) * 128], knp[h][:], kq16[:, 0, hp, :], True, True, [knp[h], kq16], [Pa])
                    kb.MM(Pb[:, h * 128:(h + 1) * 128], knp[h][:], kq16[:, 1, hp, :], True, True, [knp[h], kq16], [Pb])
                pav = Pa[:].rearrange("p (h t) -> p h t", h=4)
                pbv = Pb[:].rearrange("p (h t) -> p h t", h=4)
                kb.STT(NAT[:], pav, -1.0, LBT[:], ALU.mult, ALU.mult, [Pa, LBT], [NAT])
                kb.STT(NA[:], pav, -1.0, LBm[:], ALU.mult, ALU.mult, [Pa, LBm], [NA])
                kb.TT(QKm[:], pbv, LI[:], ALU.mult, [Pb, LI], [QKm])
                yield
                mT = kb.C("MOFF1" if d == 0 else "MOFFT1").unsqueeze(1).to_broadcast([128, 4, 128])
                mW = kb.C("MOFFT1" if d == 0 else "MOFF1").unsqueeze(1).to_broadcast([128, 4, 128])
                kb.TT(tmx[:], NA[:], mT, ALU.mult, [NA], [tmx])
                kb.TT(tmy[:], NAT[:], mW, ALU.mult, [NAT], [tmy], eng="pool")
                kb.TT(Tm[:], tmx[:], idb, ALU.add, [tmx], [Tm])
                kb.TT(Wm[:], tmy[:], idb, ALU.add, [tmy], [Wm], eng="pool")
                yield
                for s_ in (2, 4, 8, 16, 32, 64):
                    for h in range(4):
                        kb.MM(Pa[:, h * 128:(h + 1) * 128], NAT[:, h, :], Tm[:, h, :], True, True, [NAT, Tm], [Pa])
                    for h in range(4):
                        kb.MM(Pb[:, h * 128:(h + 1) * 128], NA[:, h, :], Wm[:, h, :], True, True, [NA, Wm], [Pb])
                    yield
                    kb.CP(x1[:], pav, [Pa], [x1], eng="act")
                    kb.CP(y1[:], pbv, [Pb], [y1], eng="act")
                    yield
                    for h in range(4):
                        kb.MM(Pa[:, h * 128:(h + 1) * 128], Wm[:, h, :], x1[:, h, :], True, True, [Wm, x1], [Pa])
                    for h in range(4):
                        kb.MM(Pb[:, h * 128:(h + 1) * 128], Tm[:, h, :], y1[:, h, :], True, True, [Tm, y1], [Pb])
                    yield
                    kb.CPRED(Tm[:], cx.mT[s_][:], pav, [Pa, cx.mT[s_]], [Tm])
                    kb.CPRED(Wm[:], cx.mW[s_][:], pbv, [Pb, cx.mW[s_]], [Wm])
                    yield
                for hp in range(2):
                    kb.TR(Pc[:, 128:256], kn[:, hp, :], ident, [kn], [Pc])
                    kb.TR(Pc[:, 256:384], vv[:, hp, :], ident, [vv], [Pc])
                    for h2 in range(2):
                        h = 2 * hp + h2
                        kc = slice(64 * h2, 64 * h2 + 64)
                        vc = slice(64 * (1 - h2), 64 * (1 - h2) + 64)
                        kb.TS(Rm[h][:, kc], Pc[:, 128 + 64 * h2:128 + 64 * h2 + 64], beg[:, h:h + 1], None, ALU.mult, None,
                              [Pc, beg], [Rm[h]])
                        kb.ACT(Rm[h][:, vc], Pc[:, 256 + 64 * h2:256 + 64 * h2 + 64], AF.Copy, [Pc, beta], [Rm[h]],
                               scale=beta[:, h:h + 1])
                        kb.ACT(khp[h][:, kc], Pc[:, 128 + 64 * h2:128 + 64 * h2 + 64], AF.Copy, [Pc, ekr], [khp[h]],
                               scale=ekr[:, h:h + 1])
                    yield
                for h in range(4):
                    kb.MM(Pa[:, h * 128:(h + 1) * 128], Wm[:, h, :], Rm[h][:], True, True, [Wm, Rm[h]], [Pa])
                    kb.MM(Pb[:, h * 128:(h + 1) * 128], Rm[h][:], Wm[:, h, :], True, True, [Wm, Rm[h]], [Pb])
                for h in range(4):
                    hp, h2 = divmod(h, 2)
                    vc0 = 64 * (1 - h2)
                    kb.CP(upair[hp][:, 64 * h2:64 * h2 + 64], Pa[:, h * 128 + vc0:h * 128 + vc0 + 64], [Pa], [upair[hp]], eng="dve")
                    kb.CP(wTp[hp][64 * h2:64 * h2 + 64, :], Pb[64 * h2:64 * h2 + 64, h * 128:(h + 1) * 128], [Pb], [wTp[hp]], eng="act")
                yield
                for hp in range(2):
                    kb.MM(Pc[:, 384:512], kb.C("SELP%d" % hp)[0:4, :], ROWS[:, 0:128], True, True, [ROWS], [Pc])
                    kb.ACT(EG[hp][:], Pc[:, 384:512], AF.Exp, [Pc], [EG[hp]])
                    kb.TT(qd[hp][:], qn[:, hp, :], EG[hp][:], ALU.mult, [qn, EG[hp]], [qd[hp]], eng="pool")
                    pws = Pc[:, hp * 128:(hp + 1) * 128]
                    kb.MM(pws, wTp[hp][:], Sb[hp][:], True, True, [wTp[hp], Sb[hp]], [Pc])
                    for h2 in range(2):
                        h = 2 * hp + h2
                        cs_ = slice(64 * h2, 64 * h2 + 64)
                        kb.TT(vnp[h][:, cs_], upair[hp][:, cs_], Pc[:, hp * 128 + 64 * h2:hp * 128 + 64 * h2 + 64],
                              ALU.subtract, [upair[hp], Pc], [vnp[h]])
                    po = Pd[:, hp * 256:hp * 256 + 128]
                    kb.MM(po, Sb[hp][:], qd[hp][:], True, False, [Sb[hp], qd[hp]], [Pd])
                    kb.MM(po, vnp[2 * hp][:], QKm[:, 2 * hp, :], False, False, [vnp[2 * hp], QKm], [Pd])
                    kb.MM(po, vnp[2 * hp + 1][:], QKm[:, 2 * hp + 1, :], False, True, [vnp[2 * hp + 1], QKm], [Pd])
                    kb.TT(OACC[:, hp, cols], OACC[:, hp, cols], po, ALU.add, [Pd], [OACC.s(n)])
                    pkv = Pd[:, hp * 256 + 128:hp * 256 + 256]
                    kb.MM(pkv, khp[2 * hp][:], vnp[2 * hp][:], True, False, [khp[2 * hp], vnp[2 * hp]], [Pd])
                    kb.MM(pkv, khp[2 * hp + 1][:], vnp[2 * hp + 1][:], False, True, [khp[2 * hp + 1], vnp[2 * hp + 1]], [Pd])
                    kb.CP(cdp[hp][0:64, :], cdec[0:64, 2 * hp:2 * hp + 1], [cdec], [cdp[hp]])
                    kb.CP(cdp[hp][64:128, :], cdec[64:128, 2 * hp + 1:2 * hp + 2], [cdec], [cdp[hp]])
                    kb.STT(Sb[hp][:], Sb[hp][:], cdp[hp][:, 0:1], pkv, ALU.mult, ALU.add, [Sb[hp], cdp[hp], Pd], [Sb[hp]])
                    yield

            def stream(cx):
                for n in ORDER[cx.d][:kb.cfg.get("ntiles", NT)]:
                    yield from step(cx, n)
            active = [stream(cxs[0]), stream(cxs[1])]
            for _ in range(kb.cfg.get("g_off", 19)):
                next(active[0])
            while active:
                for g_ in list(active):
                    try:
                        next(g_)
                    except StopIteration:
                        active.remove(g_)
        with P.scope():
            G = P.sbuf("g_G2", [128, 1])
            for hh in range(2):
                kb.LD(G[64 * hh:64 * hh + 64, :], kb.prm["gdn_norm_g"][l].rearrange("(p o) -> p o", o=1), [G])
            finalize_gated(kb, OACC, Z_GG, G, 512, "g_")


def run_interleaved(gens, offset=0):
    active = list(gens)
    for _ in range(offset):
        try:
            next(active[0])
        except StopIteration:
            active.pop(0)
            break
    while active:
        for g_ in list(active):
            try:
                next(g_)
            except StopIteration:
                active.remove(g_)


def oacc_add(kb, OACC, hp, n, ps):
    cols = slice(n * 128, (n + 1) * 128)
    kb.TT(OACC[:, hp, cols], OACC[:, hp, cols], ps[:], ALU.add, [ps], [OACC.s(n)])


def oacc_zero(kb, OACC):
    for hp in range(2):
        kb.MS(OACC[:, hp, :], 0.0, [OACC.s(n) for n in range(NT)], eng="pool")


def mixer_hgrn2(kb, l):
    P = kb.P
    with P.scope():
        OACC = P.sbuf("h_oacc", [128, 2, S])
        oacc_zero(kb, OACC)
        with P.scope():
            LB = P.sbuf("h_LB", [128, 4]); OML = P.sbuf("h_OML", [128, 4])
            if l == 0:
                kb.MS(LB[:], 0.0, [LB]); kb.MS(OML[:], 1.0, [OML])
            else:
                lgt = P.sbuf("h_lgt", [128, 8])
                kb.LD(lgt[:], kb.prm["hgrn_lb_logits"][:].rearrange("l d (hp p) -> p (l d hp)", p=128), [lgt],
                      allow_slow_non_contiguous=True)
                kb.TT(LB[:], lgt[:, 4:8], lgt[:, 0:4], ALU.subtract, [lgt], [LB])
                kb.ACT(LB[:], LB[:], AF.Sigmoid, [LB], [LB])
                kb.TS(OML[:], LB[:], -1.0, 1.0, ALU.mult, ALU.add, [LB], [OML])

            def make(d):
                pf = "h%d_" % d
                hqb = [P.sbuf(pf + "q%d" % i, [128, 2, 128]) for i in range(2)]
                hfb = [P.sbuf(pf + "f%d" % i, [128, 2, 128]) for i in range(2)]
                Vp = [[P.sbuf(pf + "vp%d%d" % (i, h), [128, 128], BF16) for h in range(4)] for i in range(2)]
                khp = [[P.sbuf(pf + "kh%d%d" % (i, h), [128, 128], BF16) for h in range(4)] for i in range(2)]
                Qlp = [[P.sbuf(pf + "qlp%d%d" % (i, h2), [128, 128], BF16) for h2 in range(2)] for i in range(2)]
                for i in range(2):
                    for h2 in range(2):
                        kb.MS(Qlp[i][h2][:], 0.0, [Qlp[i][h2]], eng="pool")
                for i in range(2):
                    for h in range(4):
                        kb.MS(Vp[i][h][:], 0.0, [Vp[i][h]], eng="pool")
                        kb.MS(khp[i][h][:], 0.0, [khp[i][h]], eng="pool")
                MREF = [P.sbuf(pf + "mr%d" % i, [128, 4]) for i in range(2)]
                for i in range(2):
                    kb.MS(MREF[i][:], 0.0, [MREF[i]])

                def two(name, shape=(128, 128)):
                    return [P.sbuf(pf + "%s%d" % (name, i), list(shape)) for i in range(2)]
                qs, sgm, ff, logf, kk, bb, pre = two("qs"), two("sg"), two("ff"), two("lf"), two("kk"), two("bb"), two("pre")
                e1, Ql, e2, Qd = two("e1"), two("Ql"), two("e2"), two("Qd")
                Kt = [[P.sbuf(pf + "Kt%d%d" % (r, i), [128, 128], BF16) for i in range(2)] for r in range(4)]
                ex = two("ex")
                AT = [P.sbuf(pf + "AT%d" % i, [128, 2, 128], BF16) for i in range(2)]
                KhT = two("KhT")
                bend = two("bend", (128, 2))
                Sb = [P.sbuf(pf + "S%d" % hp, [128, 128]) for hp in range(2)]
                for hp in range(2):
                    kb.MS(Sb[hp][:], 0.0, [Sb[hp]])
                pss = P.psum(pf + "pss", [128, 2, 128])
                po = P.psum(pf + "pso", [128, 128])
                pk = P.psum(pf + "psk", [128, 128])
                pkv = P.psum(pf + "pskv", [128, 128])
                zf = Z_HFF if d == 0 else Z_HFB
                tri = kb.C("TRIF" if d == 0 else "TRIB").unsqueeze(1).to_broadcast([128, 2, 128])

                def gen():
                    it = 0
                    jj = 0
                    for n in ORDER[d]:
                        cols = slice(n * 128, (n + 1) * 128)
                        b = it % 2; it += 1
                        hq, hf = hqb[b], hfb[b]
                        kb.LD(hq[:], kb.ZF[Z_HQ:Z_HQ + 256, cols].rearrange("(hp p) t -> p hp t", p=128), [hq])
                        kb.LD(hf[:], kb.ZF[zf:zf + 256, cols].rearrange("(hp p) t -> p hp t", p=128), [hf])
                        for h in range(4):
                            kb.LD(Vp[b][h][:, 64 * (h % 2):64 * (h % 2) + 64], kb.ZT[cols, 64 * h:64 * h + 64], [Vp[b][h]], q="pool")
                        yield
                        for hp in range(2):
                            j = jj % 2; jj += 1
                            c = 2 * d + hp
                            mref = MREF[j]
                            kb.ACT(qs[j][:], hq[:, hp, :], AF.Exp, [hq], [qs[j]], scale=-1.0)
                            kb.TS(qs[j][:], qs[j][:], 1.0, None, ALU.add, None, [qs[j]], [qs[j]])
                            kb.RECIP(qs[j][:], qs[j][:], [qs[j]], [qs[j]])
                            kb.TT(qs[j][:], qs[j][:], hq[:, hp, :], ALU.mult, [qs[j], hq], [qs[j]], eng="pool")
                            kb.ACT(sgm[j][:], hf[:, hp, :], AF.Exp, [hf], [sgm[j]], scale=-1.0)
                            kb.TS(sgm[j][:], sgm[j][:], 1.0, None, ALU.add, None, [sgm[j]], [sgm[j]])
                            kb.RECIP(sgm[j][:], sgm[j][:], [sgm[j]], [sgm[j]])
                            kb.TS(ff[j][:], sgm[j][:], OML[:, c:c + 1], LB[:, c:c + 1], ALU.mult, ALU.add, [sgm[j], OML, LB], [ff[j]])
                            kb.ACT(logf[j][:], ff[j][:], AF.Ln, [ff[j]], [logf[j]])
                            kb.TS(kk[j][:], ff[j][:], -1.0, 1.0, ALU.mult, ALU.add, [ff[j]], [kk[j]], eng="pool")
                            yield
                            B = bb[j]
                            if d == 0:
                                kb.SCAN(B[:], kb.C("ONES"), logf[j][:], [logf[j]], [B])
                                kb.CP(mref[:, 1:4], B[:].rearrange("p (r c) -> p r c", c=32)[:, 0:3, 31], [B], [mref])
                                be = B[:, 127:128]
                            else:
                                kb.SCAN(pre[j][:], kb.C("ONES"), logf[j][:], [logf[j]], [pre[j]])
                                kb.STT(B[:], pre[j][:], -1.0, logf[j][:], ALU.mult, ALU.add, [pre[j], logf[j]], [B])
                                kb.TS(B[:], B[:], pre[j][:, 127:128], None, ALU.add, None, [B, pre[j]], [B])
                                kb.CP(mref[:, 0:3], B[:].rearrange("p (r c) -> p r c", c=32)[:, 1:4, 0], [B], [mref])
                                be = B[:, 0:1]
                            yield
                            kb.TT(e1[j][:].rearrange("p (r c) -> p r c", c=32), B[:].rearrange("p (r c) -> p r c", c=32),
                                  mref[:].unsqueeze(2).to_broadcast([128, 4, 32]), ALU.subtract, [B, mref], [e1[j]])
                            kb.ACT(e1[j][:], e1[j][:], AF.Exp, [e1[j]], [e1[j]])
                            for h2 in range(2):
                                rs_ = slice(64 * h2, 64 * h2 + 64)
                                kb.STT(Qlp[j][h2][rs_, :], qs[j][rs_, :], 0.125, e1[j][rs_, :], ALU.mult, ALU.mult,
                                       [qs[j], e1[j]], [Qlp[j][h2]])
                            kb.ACT(e2[j][:], B[:], AF.Exp, [B], [e2[j]])
                            kb.STT(Qd[j][:], qs[j][:], 0.125, e2[j][:], ALU.mult, ALU.mult, [qs[j], e2[j]], [Qd[j]])
                            yield
                            for r in range(4):
                                kb.ACT(ex[j][:], B[:], AF.Exp, [B, mref], [ex[j]], scale=-1.0, bias=mref[:, r:r + 1])
                                kb.STT(Kt[r][j][:], ex[j][:], 1e26, kk[j][:], ALU.min, ALU.mult, [ex[j], kk[j]], [Kt[r][j]])
                                for h2 in range(2):
                                    kb.MM(pss[:, h2, 32 * r:32 * r + 32], Kt[r][j][:],
                                          Qlp[j][h2][:, 32 * r:32 * r + 32], True, True,
                                          [Kt[r][j], Qlp[j][h2]], [pss])
                                yield
                            kb.TT(AT[j][:], pss[:], tri, ALU.mult, [pss], [AT[j]])
                            yield
                            kb.MM(po[:], Vp[b][2 * hp][:], AT[j][:, 0, :], True, False, [Vp[b][2 * hp], AT[j]], [po])
                            kb.MM(po[:], Vp[b][2 * hp + 1][:], AT[j][:, 1, :], False, False, [Vp[b][2 * hp + 1], AT[j]], [po])
                            kb.MM(po[:], Sb[hp][:], Qd[j][:], False, True, [Sb[hp], Qd[j]], [po])
                            oacc_add(kb, OACC, hp, n, po)
                            kb.CP(bend[j][:, 0:1], be, [B], [bend[j]])
                            kb.ACT(KhT[j][:], B[:], AF.Exp, [B, bend[j]], [KhT[j]], scale=-1.0, bias=bend[j][:, 0:1])
                            kb.TT(KhT[j][:], KhT[j][:], kk[j][:], ALU.mult, [KhT[j], kk[j]], [KhT[j]], eng="pool")
                            kb.ACT(bend[j][:, 1:2], bend[j][:, 0:1], AF.Exp, [bend[j]], [bend[j]])
                            yield
                            kb.TR(pk[:], KhT[j][:], kb.C("IDENT"), [KhT[j]], [pk])
                            for h2 in range(2):
                                h = 2 * hp + h2
                                kb.CP(khp[b][h][:, 64 * h2:64 * h2 + 64], pk[:, 64 * h2:64 * h2 + 64], [pk], [khp[b][h]],
                                      eng=("act" if h2 else "dve"))
                            yield
                            kb.MM(pkv[:], khp[b][2 * hp][:], Vp[b][2 * hp][:], True, False, [khp[b][2 * hp], Vp[b][2 * hp]], [pkv])
                            kb.MM(pkv[:], khp[b][2 * hp + 1][:], Vp[b][2 * hp + 1][:], False, True,
                                  [khp[b][2 * hp + 1], Vp[b][2 * hp + 1]], [pkv])
                            kb.STT(Sb[hp][:], Sb[hp][:], bend[j][:, 1:2], pkv[:], ALU.mult, ALU.add,
                                   [Sb[hp], bend[j], pkv], [Sb[hp]])
                            yield
                return gen()
            run_interleaved([make(0), make(1)], offset=kb.cfg.get("h_off", 11))
        with P.scope():
            G = P.sbuf("h_G2", [128, 1])
            for hh in range(2):
                kb.LD(G[64 * hh:64 * hh + 64, :], kb.prm["hgrn_norm_g"][l].rearrange("(p o) -> p o", o=1), [G])
            finalize_gated(kb, OACC, Z_HG, G, 0, "h_")


def mixer_ret2(kb, l):
    P = kb.P
    with P.scope():
        OACC = P.sbuf("r_oacc", [128, 2, S])
        oacc_zero(kb, OACC)
        with P.scope():
            lgt = P.sbuf("r_lgt", [128, 8])
            kb.LD(lgt[:], kb.prm["ret_decay_logit"][l].rearrange("d h -> (d h)").partition_broadcast(128), [lgt])
            LG = P.sbuf("r_LG", [128, 8])
            kb.ACT(LG[:], lgt[:], AF.Sigmoid, [lgt], [LG])
            kb.ACT(LG[:], LG[:], AF.Ln, [LG], [LG])
            LGP = P.sbuf("r_LGP", [128, 4])
            for d in range(2):
                for hp in range(2):
                    c = 2 * d + hp
                    kb.CP(LGP[0:64, c:c + 1], LG[0:64, 4 * d + 2 * hp:4 * d + 2 * hp + 1], [LG], [LGP])
                    kb.CP(LGP[64:128, c:c + 1], LG[64:128, 4 * d + 2 * hp + 1:4 * d + 2 * hp + 2], [LG], [LGP])
            MK = [P.sbuf("r_MK%d" % d, [128, 4, 128]) for d in range(2)]
            QDEC = [[P.sbuf("r_QD%d%d" % (d, hp), [128, 128]) for hp in range(2)] for d in range(2)]
            etmp = P.sbuf("r_etmp", [128, 128])
            for d in range(2):
                for h in range(4):
                    kb.ACT(etmp[:], kb.C("DIFF" if d == 0 else "NDIFF"), AF.Exp, [LG], [etmp],
                           scale=LG[:, 4 * d + h:4 * d + h + 1])
                    kb.STT(MK[d][:, h, :], etmp[:], 0.125, kb.C("TRIF" if d == 0 else "TRIB"), ALU.mult, ALU.mult,
                           [etmp], [MK[d]])
                for hp in range(2):
                    kb.ACT(QDEC[d][hp][:], kb.C("IOTAF1" if d == 0 else "RIOTAF"), AF.Exp, [LGP], [QDEC[d][hp]],
                           scale=LGP[:, 2 * d + hp:2 * d + hp + 1])
            KD = P.sbuf("r_KD", [128, 8])
            kb.ACT(KD[:, 0:4], LG[:, 0:4], AF.Exp, [LG], [KD], scale=kb.C("CCOL")[:, 3:4])
            kb.ACT(KD[:, 4:8], LG[:, 4:8], AF.Exp, [LG], [KD], scale=kb.C("CCOL")[:, 2:3])
            kb.TS(KD[:], KD[:], 0.125, None, ALU.mult, None, [KD], [KD])
            CV = P.sbuf("r_CV", [128, 4])
            kb.ACT(CV[:], LGP[:], AF.Exp, [LGP], [CV], scale=128.0)

            def make(d):
                pf = "r%d_" % d
                qTb = [P.sbuf(pf + "q%d" % i, [128, 2, 128]) for i in range(2)]
                kTb = [P.sbuf(pf + "k%d" % i, [128, 2, 128]) for i in range(2)]
                csb = [P.sbuf(pf + "cs%d" % i, [128, 2, 128]) for i in range(2)]
                Vp = [[P.sbuf(pf + "vp%d%d" % (i, h), [128, 128], BF16) for h in range(4)] for i in range(2)]
                khp = [[P.sbuf(pf + "kh%d%d" % (i, h), [128, 128], BF16) for h in range(4)] for i in range(2)]
                qrp = [[P.sbuf(pf + "qrp%d%d" % (i, h2), [128, 128], BF16) for h2 in range(2)] for i in range(2)]
                for i in range(2):
                    for h2 in range(2):
                        kb.MS(qrp[i][h2][:], 0.0, [qrp[i][h2]], eng="pool")
                kr16 = [P.sbuf(pf + "kr16%d" % i, [128, 128], BF16) for i in range(2)]
                for i in range(2):
                    for h in range(4):
                        kb.MS(Vp[i][h][:], 0.0, [Vp[i][h]], eng="pool")
                        kb.MS(khp[i][h][:], 0.0, [khp[i][h]], eng="pool")
                t1 = [P.sbuf(pf + "t1%d" % i, [128, 128]) for i in range(2)]
                t2 = [P.sbuf(pf + "t2%d" % i, [128, 128]) for i in range(2)]
                qr = [P.sbuf(pf + "qr%d" % i, [128, 2, 128]) for i in range(2)]
                kr = [P.sbuf(pf + "kr%d" % i, [128, 2, 128]) for i in range(2)]
                AT = [P.sbuf(pf + "AT%d" % i, [128, 2, 128], BF16) for i in range(2)]
                qd = [P.sbuf(pf + "qd%d" % i, [128, 128]) for i in range(2)]
                Sb = [P.sbuf(pf + "S%d" % hp, [128, 128]) for hp in range(2)]
                for hp in range(2):
                    kb.MS(Sb[hp][:], 0.0, [Sb[hp]])
                pr = P.psum(pf + "psr", [128, 256])
                pss = P.psum(pf + "pss", [128, 2, 128])
                po = P.psum(pf + "pso", [128, 128])
                pkk = P.psum(pf + "pskk", [128, 256])

                def gen():
                    it = 0
                    jj = 0
                    for n in ORDER[d]:
                        cols = slice(n * 128, (n + 1) * 128)
                        b = it % 2; it += 1
                        qT, kT, cs = qTb[b], kTb[b], csb[b]
                        kb.LD(qT[:], kb.ZF[Z_RQ:Z_RQ + 256, cols].rearrange("(hp p) t -> p hp t", p=128), [qT])
                        kb.LD(kT[:], kb.ZF[Z_RK:Z_RK + 256, cols].rearrange("(hp p) t -> p hp t", p=128), [kT])
                        kb.LD(cs[:, 0, :], kb.ropec[:, cols], [cs])
                        kb.LD(cs[:, 1, :], kb.ropes[:, cols], [cs])
                        for h in range(4):
                            kb.LD(Vp[b][h][:, 64 * (h % 2):64 * (h % 2) + 64], kb.ZT[cols, 256 + 64 * h:256 + 64 * h + 64],
                                  [Vp[b][h]], q="pool")
                        yield
                        for hp in range(2):
                            j = jj % 2; jj += 1
                            kb.MM(pr[:, 0:128], kb.C("ROT"), qT[:, hp, :], True, True, [qT], [pr])
                            kb.MM(pr[:, 128:256], kb.C("ROT"), kT[:, hp, :], True, True, [kT], [pr])
                            yield
                            for (src_, dst, off) in ((qT, qr[b], 0), (kT, kr[b], 128)):
                                kb.TT(t1[j][:], src_[:, hp, :], cs[:, 0, :], ALU.mult, [src_, cs], [t1[j]])
                                kb.TT(t2[j][:], pr[:, off:off + 128], cs[:, 1, :], ALU.mult, [pr, cs], [t2[j]])
                                kb.TT(dst[:, hp, :], t1[j][:], t2[j][:], ALU.add, [t1[j], t2[j]], [dst.s(hp)], eng="pool")
                                yield
                            kb.CP(kr16[j][:], kr[b][:, hp, :], [kr[b].s(hp)], [kr16[j]], eng="act")
                            for h2 in range(2):
                                rs_ = slice(64 * h2, 64 * h2 + 64)
                                kb.CP(qrp[j][h2][rs_, :], qr[b][rs_, hp, :], [qr[b].s(hp)], [qrp[j][h2]], eng="act")
                            yield
                            for h2 in range(2):
                                kb.MM(pss[:, h2, :], kr16[j][:], qrp[j][h2][:], True, True, [kr16[j], qrp[j][h2]], [pss])
                            yield
                            kb.TT(AT[j][:], pss[:], MK[d][:, 2 * hp:2 * hp + 2, :], ALU.mult, [pss, MK[d]], [AT[j]])
                            kb.TT(qd[j][:], qr[b][:, hp, :], QDEC[d][hp][:], ALU.mult, [qr[b].s(hp), QDEC[d][hp]], [qd[j]],
                                  eng="pool")
                            yield
                            kb.MM(po[:], Vp[b][2 * hp][:], AT[j][:, 0, :], True, False, [Vp[b][2 * hp], AT[j]], [po])
                            kb.MM(po[:], Vp[b][2 * hp + 1][:], AT[j][:, 1, :], False, False, [Vp[b][2 * hp + 1], AT[j]], [po])
                            kb.MM(po[:], Sb[hp][:], qd[j][:], False, True, [Sb[hp], qd[j]], [po])
                            kb.TR(pkk[:, 0:128], kr[b][:, hp, :], kb.C("IDENT"), [kr[b].s(hp)], [pkk])
                            yield
                            oacc_add(kb, OACC, hp, n, po)
                            for h2 in range(2):
                                h = 2 * hp + h2
                                kb.ACT(khp[b][h][:, 64 * h2:64 * h2 + 64], pkk[:, 64 * h2:64 * h2 + 64], AF.Copy,
                                       [pkk, KD], [khp[b][h]], scale=KD[:, 4 * d + h:4 * d + h + 1])
                            yield
                            kb.MM(pkk[:, 128:256], khp[b][2 * hp][:], Vp[b][2 * hp][:], True, False,
                                  [khp[b][2 * hp], Vp[b][2 * hp]], [pkk])
                            kb.MM(pkk[:, 128:256], khp[b][2 * hp + 1][:], Vp[b][2 * hp + 1][:], False, True,
                                  [khp[b][2 * hp + 1], Vp[b][2 * hp + 1]], [pkk])
                            kb.STT(Sb[hp][:], Sb[hp][:], CV[:, 2 * d + hp:2 * d + hp + 1], pkk[:, 128:256], ALU.mult, ALU.add,
                                   [Sb[hp], CV, pkk], [Sb[hp]])
                            yield
                return gen()
            run_interleaved([make(0), make(1)])
        with P.scope():
            finalize_gated(kb, OACC, Z_RG, None, 256, "r_")


def s5_tables(kb, l, d, VFr, VFi, T1, T2, AR, NAI):
    P = kb.P
    prm = kb.prm
    with P.scope():
        lr = P.sbuf("s_lr", [128, 16, 64]); li = P.sbuf("s_li", [128, 16, 64]); dtb = P.sbuf("s_dt", [128, 16])
        kb.LD(lr[:], prm["s5_lam_re"][l][d].rearrange("g p -> (g p)").partition_broadcast(128), [lr])
        kb.LD(li[:], prm["s5_lam_im"][l][d].rearrange("g p -> (g p)").partition_broadcast(128), [li])
        kb.LD(dtb[:], prm["s5_log_dt"][l][d].partition_broadcast(128), [dtb])
        kb.ACT(dtb[:], dtb[:], AF.Exp, [dtb], [dtb])
        dt_bc = dtb[:].unsqueeze(2).to_broadcast([128, 16, 64])
        lrdt = P.sbuf("s_lrdt", [128, 16, 64]); lidt = P.sbuf("s_lidt", [128, 16, 64])
        kb.TT(lrdt[:], lr[:], dt_bc, ALU.mult, [lr, dtb], [lrdt])
        kb.TT(lidt[:], li[:], dt_bc, ALU.mult, [li, dtb], [lidt])
        a = [P.sbuf("s_a%d" % i, [128, 16, 64]) for i in range(8)]
        mag, ang, sn, cs, tmp, ar, ai, t2 = a
        kb.ACT(mag[:], lrdt[:], AF.Exp, [lrdt], [mag])
        _sincos(kb, lidt[:], sn[:], cs[:], [lidt, sn, cs, tmp], tmp[:])
        kb.TT(ar[:], mag[:], cs[:], ALU.mult, [mag, cs], [ar])
        kb.TT(ai[:], mag[:], sn[:], ALU.mult, [mag, sn], [ai])
        den = P.sbuf("s_den", [128, 16, 64]); fr = P.sbuf("s_fr", [128, 16, 64]); fi = P.sbuf("s_fi", [128, 16, 64])
        kb.TT(den[:], lr[:], lr[:], ALU.mult, [lr], [den])
        kb.TT(t2[:], li[:], li[:], ALU.mult, [li], [t2])
        kb.TT(den[:], den[:], t2[:], ALU.add, [den, t2], [den])
        kb.RECIP(den[:], den[:], [den], [den])
        kb.TS(ar[:], ar[:], -1.0, None, ALU.add, None, [ar], [ar])
        kb.TT(fr[:], ar[:], lr[:], ALU.mult, [ar, lr], [fr])
        kb.TT(t2[:], ai[:], li[:], ALU.mult, [ai, li], [t2])
        kb.TT(fr[:], fr[:], t2[:], ALU.add, [fr, t2], [fr])
        kb.TT(fr[:], fr[:], den[:], ALU.mult, [fr, den], [fr])
        kb.TT(fi[:], ai[:], lr[:], ALU.mult, [ai, lr], [fi])
        kb.TT(t2[:], ar[:], li[:], ALU.mult, [ar, li], [t2])
        kb.TT(fi[:], fi[:], t2[:], ALU.subtract, [fi, t2], [fi])
        kb.TT(fi[:], fi[:], den[:], ALU.mult, [fi, den], [fi])
        jcol = kb.C("CCOL")[:, 2:3] if d == 0 else kb.C("CCOL")[:, 3:4]
        njcol = kb.C("CCOL")[:, 6:7] if d == 0 else kb.C("CCOL")[:, 7:8]
        kb.ACT(mag[:], lrdt[:], AF.Exp, [lrdt], [mag], scale=njcol)
        kb.TS(ang[:], lidt[:], jcol, None, ALU.mult, None, [lidt], [ang])
        _sincos(kb, ang[:], sn[:], cs[:], [ang, sn, cs, tmp], tmp[:])
        vr, vi = ar, ai
        kb.TT(vr[:], mag[:], cs[:], ALU.mult, [mag, cs], [vr])
        kb.TT(vi[:], mag[:], sn[:], ALU.mult, [mag, sn], [vi])
        kb.TS(vi[:], vi[:], -1.0, None, ALU.mult, None, [vi], [vi])
        kb.TT(VFr[:], vr[:], fr[:], ALU.mult, [vr, fr], [VFr])
        kb.TT(t2[:], vi[:], fi[:], ALU.mult, [vi, fi], [t2])
        kb.TT(VFr[:], VFr[:], t2[:], ALU.subtract, [VFr, t2], [VFr])
        kb.TT(VFi[:], vr[:], fi[:], ALU.mult, [vr, fi], [VFi])
        kb.TT(t2[:], vi[:], fr[:], ALU.mult, [vi, fr], [t2])
        kb.TT(VFi[:], VFi[:], t2[:], ALU.add, [VFi, t2], [VFi])
    with P.scope():
        dtb = P.sbuf("s_dt2", [128, 16])
        kb.LD(dtb[:], prm["s5_log_dt"][l][d].partition_broadcast(128), [dtb])
        kb.ACT(dtb[:], dtb[:], AF.Exp, [dtb], [dtb])
        lrp = P.sbuf("s_lrp", [128, 16]); lip = P.sbuf("s_lip", [128, 16])
        for hh in range(2):
            kb.LD(lrp[64 * hh:64 * hh + 64, :], prm["s5_lam_re"][l][d].rearrange("g p -> p g"), [lrp],
                  allow_slow_non_contiguous=True)
            kb.LD(lip[64 * hh:64 * hh + 64, :], prm["s5_lam_im"][l][d].rearrange("g p -> p g"), [lip],
                  allow_slow_non_contiguous=True)
        kb.TT(lrp[:], lrp[:], dtb[:], ALU.mult, [lrp, dtb], [lrp])
        kb.TT(lip[:], lip[:], dtb[:], ALU.mult, [lip, dtb], [lip])
        b4 = [P.sbuf("s_b%d" % i, [128, 16, 128]) for i in range(4)]
        arg, sn2, cs2, tmp2 = b4
        mt = kb.C("IOTAF" if d == 0 else "R127F")
        mt_bc = mt.unsqueeze(1).to_broadcast([128, 16, 128])
        kb.TT(arg[:], lrp[:].unsqueeze(2).to_broadcast([128, 16, 128]), mt_bc, ALU.mult, [lrp], [arg])
        kb.ACT(T1[:], arg[:], AF.Exp, [arg], [T1])
        kb.TT(arg[:], lip[:].unsqueeze(2).to_broadcast([128, 16, 128]), mt_bc, ALU.mult, [lip, T1], [arg])
        _sincos(kb, arg[:], sn2[:], cs2[:], [arg, sn2, cs2, tmp2], tmp2[:])
        kb.TT(T2[:], T1[:], sn2[:], ALU.mult, [T1, sn2], [T2])
        kb.TS(T2[:], T2[:], -1.0, None, ALU.mult, None, [T2], [T2])
        kb.TT(T1[:], T1[:], cs2[:], ALU.mult, [T1, cs2], [T1])
        c4 = [P.sbuf("s_c%d" % i, [128, 16]) for i in range(4)]
        kb.ACT(c4[0][:], lrp[:], AF.Exp, [lrp], [c4[0]])
        _sincos(kb, lip[:], c4[1][:], c4[2][:], [lip, c4[1], c4[2], c4[3]], c4[3][:])
        kb.TT(AR[:], c4[0][:], c4[2][:], ALU.mult, [c4[0], c4[2]], [AR])
        kb.TT(NAI[:], c4[0][:], c4[1][:], ALU.mult, [c4[0], c4[1]], [NAI])
        kb.TS(NAI[:], NAI[:], -1.0, None, ALU.mult, None, [NAI], [NAI])


def mixer_s5_2(kb, l):
    P = kb.P
    prm = kb.prm
    with P.scope():
        WX = P.sbuf("s_WX", [128, 2, 8, 2, 64])
        Cblk = P.sbuf("s_Cblk", [128, 16, 128])
        kb.MS(WX[:], 0.0, [WX], eng="pool")
        kb.MS(Cblk[:], 0.0, [Cblk], eng="pool")
        for g8 in range(8):
            for ri, nm in enumerate(("s5_b_re", "s5_b_im")):
                for gg in range(2):
                    src = prm[nm][l][8 * gg + g8].rearrange("p c -> c p")
                    kb.LD(WX[16 * g8:16 * g8 + 16, gg, g8, ri, :], src, [WX], allow_slow_non_contiguous=True)
        for g in range(16):
            g8 = g % 8
            kb.LD(Cblk[0:64, g, 16 * g8:16 * g8 + 16], prm["s5_c_re"][l][g].rearrange("c p -> p c"), [Cblk],
                  allow_slow_non_contiguous=True)
            kb.LD(Cblk[64:128, g, 16 * g8:16 * g8 + 16], prm["s5_c_im"][l][g].rearrange("c p -> p c"), [Cblk],
                  allow_slow_non_contiguous=True)
        kb.TS(Cblk[64:128, :, :], Cblk[64:128, :, :], -1.0, None, ALU.mult, None, [Cblk], [Cblk])
        Cb16 = P.sbuf("s_Cb16", [128, 16, 128], BF16)
        kb.CP(Cb16[:], Cblk[:], [Cblk], [Cb16])

        tabs = []
        for d in range(2):
            VFr = P.sbuf("s_VFr%d" % d, [128, 16, 64]); VFi = P.sbuf("s_VFi%d" % d, [128, 16, 64])
            T1 = P.sbuf("s_T1%d" % d, [128, 16, 128]); T2 = P.sbuf("s_T2%d" % d, [128, 16, 128])
            AR = P.sbuf("s_AR%d" % d, [128, 16]); NAI = P.sbuf("s_NAI%d" % d, [128, 16])
            s5_tables(kb, l, d, VFr, VFi, T1, T2, AR, NAI)
            tabs.append((VFr, VFi, T1, T2, AR, NAI))
        OACC = P.sbuf("s_oacc", [128, 2, S])
        oacc_zero(kb, OACC)
        with P.scope():
            def make(d):
                pf = "s%d_" % d
                VFr, VFi, T1, T2, AR, NAI = tabs[d]
                uTb = [P.sbuf(pf + "u%d" % i, [128, 2, 128]) for i in range(2)]
                mm_ = [P.sbuf(pf + "m%d" % i, [128, 4, 64]) for i in range(4)]
                W3 = [P.sbuf(pf + "W3%d" % i, [128, 4, 3, 64], BF16) for i in range(2)]
                tP = P.sbuf(pf + "tP", [128, 4, 128]); tPs = P.sbuf(pf + "tPs", [128, 4, 128])
                H1 = P.sbuf(pf + "H1", [128, 4, 128]); H2 = P.sbuf(pf + "H2", [128, 4, 128])
                Hb = [P.sbuf(pf + "Hb%d" % i, [128, 4, 128], BF16) for i in range(2)]
                tri16 = P.sbuf(pf + "tri16", [128, 128], BF16)
                kb.CP(tri16[:], kb.C("TRIF" if d == 0 else "TRIB"), [], [tri16])
                hend = P.sbuf(pf + "hend", [128, 16]); hsend = P.sbuf(pf + "hsend", [128, 16])
                hp_ = P.sbuf(pf + "hp", [128, 16]); hps_ = P.sbuf(pf + "hps", [128, 16])
                sm = [P.sbuf(pf + "sm%d" % i, [128, 16]) for i in range(4)]
                kb.MS(hp_[:], 0.0, [hp_]); kb.MS(hps_[:], 0.0, [hps_])
                xps = P.psum(pf + "xps", [128, 512])
                pps = P.psum(pf + "pps", [128, 4, 128])
                ppss = P.psum(pf + "ppss", [128, 4, 128])
                yps = P.psum(pf + "yps", [128, 128])
                te = 127 if d == 0 else 0

                def gen():
                    it = 0
                    kq = 0
                    for n in ORDER[d]:
                        uT = uTb[it % 2]; it += 1
                        cols = slice(n * 128, (n + 1) * 128)
                        kb.LD(uT[:], kb.ZF[Z_SU:Z_SU + 256, cols].rearrange("(gg p) t -> p gg t", p=128), [uT])
                        yield
                        for q in range(4):
                            gg, qq = divmod(q, 2)
                            gs = slice(4 * q, 4 * q + 4)
                            w3 = W3[kq % 2]; hb = Hb[kq % 2]; kq += 1
                            kb.MM(xps[:], uT[:, gg, :], WX[:, gg, 4 * qq:4 * qq + 4, :, :].rearrange("q a r p -> q (a r p)"),
                                  True, True, [uT, WX], [xps])
                            xv = xps[:].rearrange("t (g r p) -> t g r p", r=2, p=64)
                            kb.TT(mm_[0][:], xv[:, :, 0, :], VFr[:, gs, :], ALU.mult, [xps, VFr], [mm_[0]])
                            kb.TT(mm_[1][:], xv[:, :, 1, :], VFi[:, gs, :], ALU.mult, [xps, VFi], [mm_[1]])
                            kb.TT(mm_[2][:], xv[:, :, 0, :], VFi[:, gs, :], ALU.mult, [xps, VFi], [mm_[2]])
                            kb.TT(mm_[3][:], xv[:, :, 1, :], VFr[:, gs, :], ALU.mult, [xps, VFr], [mm_[3]])
                            yield
                            kb.TT(w3[:, :, 0, :], mm_[0][:], mm_[1][:], ALU.subtract, [mm_[0], mm_[1]], [w3])
                            kb.TT(w3[:, :, 1, :], mm_[2][:], mm_[3][:], ALU.add, [mm_[2], mm_[3]], [w3], eng="pool")
                            kb.TT(w3[:, :, 2, :], mm_[1][:], mm_[0][:], ALU.subtract, [mm_[0], mm_[1]], [w3])
                            yield
                            for i in range(4):
                                kb.MM(pps[:, i, :], w3[:, i, 0:2, :].rearrange("q r p -> q (r p)"), tri16[:], True, True, [w3, tri16], [pps])
                            for i in range(4):
                                kb.MM(ppss[:, i, :], w3[:, i, 1:3, :].rearrange("q r p -> q (r p)"), tri16[:], True, True, [w3, tri16], [ppss])
                            yield
                            kb.TT(tP[:], pps[:], hp_[:, gs].unsqueeze(2).to_broadcast([128, 4, 128]), ALU.add, [pps, hp_], [tP])
                            for i in range(4):
                                g = 4 * q + i
                                kb.ACT(tPs[:, i, :], ppss[:, i, :], AF.Identity, [ppss, hps_], [tPs], bias=hps_[:, g:g + 1])
                            yield
                            kb.TT(sm[0][:, 0:4], tPs[:, :, te], T1[:, gs, te], ALU.mult, [tPs, T1], [sm[0]])
                            kb.TT(sm[1][:, 0:4], tP[:, :, te], T2[:, gs, te], ALU.mult, [tP, T2], [sm[1]])
                            kb.TT(hsend[:, gs], sm[0][:, 0:4], sm[1][:, 0:4], ALU.subtract, [sm[0], sm[1]], [hsend])
                            kb.TT(sm[2][:, 0:4], tP[:, :, te], T1[:, gs, te], ALU.mult, [tP, T1], [sm[2]])
                            kb.TT(sm[3][:, 0:4], tPs[:, :, te], T2[:, gs, te], ALU.mult, [tPs, T2], [sm[3]])
                            kb.TT(hend[:, gs], sm[2][:, 0:4], sm[3][:, 0:4], ALU.add, [sm[2], sm[3]], [hend])
                            yield
                            kb.TT(H1[:], tP[:], T1[:, gs, :], ALU.mult, [tP, T1], [H1], eng="pool")
                            kb.TT(H2[:], tPs[:], T2[:, gs, :], ALU.mult, [tPs, T2], [H2])
                            yield
                            kb.TT(hb[:], H1[:], H2[:], ALU.add, [H1, H2], [hb])
                            yield
                            for i in range(4):
                                g = 4 * q + i
                                kb.MM(yps[:], Cb16[:, g, :], hb[:, i, :], (g % 8) == 0, (g % 8) == 7, [Cb16, hb], [yps])
                            if qq == 1:
                                oacc_add(kb, OACC, gg, n, yps)
                            yield
                        kb.TT(sm[0][:], hend[:], AR[:], ALU.mult, [hend, AR], [sm[0]])
                        kb.TT(sm[1][:], hsend[:], NAI[:], ALU.mult, [hsend, NAI], [sm[1]])
                        kb.TT(sm[2][:], hsend[:], AR[:], ALU.mult, [hsend, AR], [sm[2]])
                        kb.TT(sm[3][:], hend[:], NAI[:], ALU.mult, [hend, NAI], [sm[3]])
                        kb.TT(hp_[:], sm[0][:], sm[1][:], ALU.add, [sm[0], sm[1]], [hp_])
                        kb.TT(hps_[:], sm[2][:], sm[3][:], ALU.subtract, [sm[2], sm[3]], [hps_])
                        yield
                return gen()
            run_interleaved([make(0), make(1)])
        with P.scope():
            dsk = P.sbuf("s_dsk", [128, 2]); glb = P.sbuf("s_glb", [128, 2])
            kb.LD(dsk[:], prm["s5_d"][l].rearrange("(gg p) -> p gg", p=128), [dsk], allow_slow_non_contiguous=True)
            kb.LD(glb[:], prm["s5_glu_b"][l].rearrange("(gg p) -> p gg", p=128), [glb], allow_slow_non_contiguous=True)
            gw = P.sbuf("s_gw", [128, 2, 256])
            kb.LD(gw[:], prm["s5_glu_w"][l].rearrange("(ct p) o -> p ct o", p=128), [gw])
            uTb = [P.sbuf("s_fu%d" % i, [128, 2, 128]) for i in range(2)]
            yy = [P.sbuf("s_yy%d" % i, [128, 2, 128]) for i in range(2)]
            x2 = [P.sbuf("s_x2%d" % i, [128, 2, 128]) for i in range(2)]
            th = [P.sbuf("s_th%d" % i, [128, 2, 128]) for i in range(2)]
            sgb = [P.sbuf("s_sg%d" % i, [128, 128]) for i in range(2)]
            ob = [P.sbuf("s_ob%d" % i, [128, 128]) for i in range(2)]
            psz = [P.psum("s_psz%d" % i, [128, 128]) for i in range(2)]
            k = 0
            for n in range(NT):
                cols = slice(n * 128, (n + 1) * 128)
                i = n % 2
                kb.LD(uTb[i][:], kb.ZF[Z_SU:Z_SU + 256, cols].rearrange("(gg p) t -> p gg t", p=128), [uTb[i]])
                for gg in range(2):
                    kb.STT(yy[i][:, gg, :], uTb[i][:, gg, :], dsk[:, gg:gg + 1], OACC[:, gg, cols], ALU.mult, ALU.add,
                           [uTb[i], dsk, OACC.s(n)], [yy[i]])
                kb.TT(x2[i][:], yy[i][:], yy[i][:], ALU.mult, [yy[i]], [x2[i]], eng="pool")
                kb.TS(x2[i][:], x2[i][:], 0.044715, 1.0, ALU.mult, ALU.add, [x2[i]], [x2[i]])
                kb.TT(x2[i][:], x2[i][:], yy[i][:], ALU.mult, [x2[i], yy[i]], [x2[i]], eng="pool")
                kb.ACT(th[i][:], x2[i][:], AF.Tanh, [x2[i]], [th[i]], scale=0.7978845608028654)
                kb.TS(th[i][:], th[i][:], 1.0, 0.5, ALU.add, ALU.mult, [th[i]], [th[i]])
                kb.TT(yy[i][:], yy[i][:], th[i][:], ALU.mult, [yy[i], th[i]], [yy[i]], eng="pool")
                for ot in range(2):
                    q = k % 2; k += 1
                    for ct in range(2):
                        kb.MM(psz[q][:], gw[:, ct, ot * 128:(ot + 1) * 128], yy[i][:, ct, :], ct == 0, ct == 1, [gw, yy[i]], [psz[q]])
                    kb.ACT(sgb[q][:], psz[q][:], AF.Sigmoid, [psz[q], glb], [sgb[q]], bias=glb[:, ot:ot + 1])
                    kb.TT(ob[q][:], yy[i][:, ot, :], sgb[q][:], ALU.mult, [yy[i], sgb[q]], [ob[q]])
                    kb.ST(kb.YC[768 + ot * 128:768 + (ot + 1) * 128, cols], ob[q][:], [ob[q]])


def _conv_win_masks():
    tp = np.arange(642) - 65
    w = np.mod(tp, 64)
    m = np.ones((2, 642), np.float32)
    m[0, w == 63] = 0.0
    m[1, w == 0] = 0.0
    return np.broadcast_to(m[None], (128, 2, 642)).copy()


def gdn_conv2(kb, l):
    P = kb.P
    with P.scope():
        CW = P.sbuf("g_cw", [128, 6, 9])
        for kh in range(3):
            for kw in range(3):
                kb.LD(CW[:, :, kh * 3 + kw], kb.prm["gdn_conv_w"][l][kh, kw].rearrange("(ct p) -> p ct", p=128), [CW],
                      allow_slow_non_contiguous=True)
        DW = P.sbuf("g_dw", [128, 6, 9, 128])
        for ct in range(6):
            for tp_ in range(9):
                kb.TS(DW[:, ct, tp_, :], kb.C("IDENT"), CW[:, ct, tp_:tp_ + 1], None, ALU.mult, None, [CW], [DW],
                      eng=("pool" if tp_ % 2 else "dve"))
        wm = P.sbuf("g_wm", [128, 2, 642])
        kb.LD(wm[:], kb.cwin[:], [wm])
        Wb = [P.sbuf("g_w%d" % i, [128, 642]) for i in range(2)]
        WLb = [P.sbuf("g_wl%d" % i, [128, 642]) for i in range(2)]
        WRb = [P.sbuf("g_wr%d" % i, [128, 642]) for i in range(2)]
        sl = [P.sbuf("g_sl%d" % i, [128, 512]) for i in range(2)]
        sq = [P.sbuf("g_sq%d" % i, [128, 512]) for i in range(2)]
        rt = [P.sbuf("g_rt%d" % i, [128, 512]) for i in range(2)]
        psc = [P.psum("g_psc%d" % i, [128, 512]) for i in range(2)]
        ps = [P.psum("g_psn%d" % i, [128, 512]) for i in range(2)]
        spans = [(0, 256, True)] + [(256 + 512 * k, 512, False) for k in range(8)]
        it = 0
        for (t0, L, is_ctx) in spans:
            lo = 0 if is_ctx else 256
            hi = 256 if is_ctx else S
            a = max(lo, t0 - 65); b = min(hi, t0 + L + 65)
            for ct in range(6):
                i = it % 2; it += 1
                W = Wb[i]
                full = (a == t0 - 65) and (b == t0 + L + 65) and L == 512
                if not full:
                    kb.MS(W[:], 0.0, [W], eng="pool")
                kb.LD(W[:, 65 + (a - t0):65 + (b - t0)], kb.ZF[Z_GQKV + ct * 128:Z_GQKV + (ct + 1) * 128, a:b], [W])
                if is_ctx:
                    WL = WR = W
                    rows = (1,)
                else:
                    WL, WR = WLb[i], WRb[i]
                    kb.TT(WL[:], W[:], wm[:, 0, :], ALU.mult, [W, wm], [WL])
                    kb.TT(WR[:], W[:], wm[:, 1, :], ALU.mult, [W, wm], [WR])
                    rows = (0, 1, 2)
                pc = psc[i]
                taps = [(dh, dwi) for dh in rows for dwi in range(3)]
                for q, (dh, dwi) in enumerate(taps):
                    srcT = (WL, W, WR)[dwi]
                    o0 = 65 + 64 * (dh - 1) + (dwi - 1)
                    kb.MM(pc[:, :L], DW[:, ct, dh * 3 + dwi, :], srcT[:, o0:o0 + L], q == 0, q == len(taps) - 1, [DW, srcT], [pc])
                kb.ACT(sl[i][:, :L], pc[:, :L], AF.Silu, [pc], [sl[i]])
                if ct < 4:
                    kb.TT(sq[i][:, :L], sl[i][:, :L], sl[i][:, :L], ALU.mult, [sl[i]], [sq[i]])
                    kb.MM(ps[i][:, :L], kb.C("BLK64"), sq[i][:, :L], True, True, [sq[i]], [ps[i]])
                    kb.ACT(rt[i][:, :L], ps[i][:, :L], AF.Sqrt, [ps[i]], [rt[i]], bias=kb.C("CCOL")[:, 0:1])
                    kb.RECIP(rt[i][:, :L], rt[i][:, :L], [rt[i]], [rt[i]])
                    if ct < 2:
                        kb.STT(sl[i][:, :L], sl[i][:, :L], 0.125, rt[i][:, :L], ALU.mult, ALU.mult, [sl[i], rt[i]], [sl[i]])
                    else:
                        kb.TT(sl[i][:, :L], sl[i][:, :L], rt[i][:, :L], ALU.mult, [sl[i], rt[i]], [sl[i]], eng="pool")
                kb.ST(kb.QKVF[ct * 128:(ct + 1) * 128, t0:t0 + L], sl[i][:, :L], [sl[i]])
```

```python
import numpy as np
import concourse.bass as bass
import concourse.mybir as mybir
from concourse.bass_utils import run_bass_kernel_spmd
from contextlib import ExitStack

F32 = mybir.dt.float32
BF16 = mybir.dt.bfloat16
AF = mybir.ActivationFunctionType
ALU = mybir.AluOpType

ENGS = ("pe", "act", "dve", "pool", "sp")
EPOCH = 16000
N_DMA_SEM = 32


class Buf:
    __slots__ = ("name", "w", "r", "excl", "pe_partial")

    def __init__(self, name="", excl=False):
        self.name = name
        self.w = None
        self.r = []
        self.excl = excl
        self.pe_partial = False


class T:
    def __init__(self, h, name, excl=False):
        self.h = h
        self.name = name
        self.b = Buf(name, excl)
        self.excl = excl
        self.subs = {}

    def __getitem__(self, k):
        return self.h[k]

    def s(self, key):
        if self.excl:
            return self.b
        if key not in self.subs:
            self.subs[key] = Buf("%s.%s" % (self.name, key))
        return self.subs[key]


class Prog:
    def __init__(self, nc):
        self.nc = nc
        self.es = ExitStack()
        self.stack = [self.es]
        self.ops = {e: [] for e in ENGS}
        self.cnt = {e: 0 for e in ENGS}
        self.seen = {e: {} for e in ENGS}
        self.last = {}
        self.dma_k = 0
        self.dma_use = [0] * N_DMA_SEM
        self.dma_sems = [self.es.enter_context(nc.semaphore("dq%d" % i)) for i in range(N_DMA_SEM)]
        self.eng_sems = {}
        self.out_tokens = []
        self.n_ops = 0
        self.uid = 0

    def _nm(self, name):
        self.uid += 1
        return "%s_%d" % (name, self.uid)

    def sbuf(self, name, shape, dt=F32):
        h = self.stack[-1].enter_context(self.nc.sbuf_tensor(self._nm(name), list(shape), dt))
        return T(h, name)

    def psum(self, name, shape, dt=F32):
        n = 1
        for d_ in shape[1:]:
            n *= d_
        nb = (n * 4 + 2047) // 2048
        h = self.stack[-1].enter_context(self.nc.psum_tensor(self._nm(name), [128, nb * 512], F32))
        v = h[0:shape[0], 0:n]
        if len(shape) == 3:
            v = v.rearrange("p (a b) -> p a b", a=shape[1])
        elif len(shape) == 4:
            v = v.rearrange("p (a b c) -> p a b c", a=shape[1], b=shape[2])
        return T(v, name, excl=True)

    def dram(self, name, shape, dt=F32, kind="Internal"):
        h = self.nc.dram_tensor(name, list(shape), dt, kind=kind)
        return T(h.ap(), name)

    class _Scope:
        def __init__(self, p):
            self.p = p

        def __enter__(self):
            st = ExitStack()
            self.p.stack.append(st)
            return st

        def __exit__(self, *a):
            self.p.barrier()
            st = self.p.stack.pop()
            st.close()
            return False

    def scope(self):
        return Prog._Scope(self)

    def _eng_sem(self, e, epoch):
        k = (e, epoch)
        if k not in self.eng_sems:
            self.eng_sems[k] = self.es.enter_context(self.nc.semaphore("s_%s_%d" % (e, epoch)))
        return self.eng_sems[k]

    def _waits(self, eng, reads, writes, extra=(), skip_pe=False):
        need = {}

        def add(tok):
            if tok is None:
                return
            key, val = tok
            if need.get(key, 0) < val:
                need[key] = val
        for b in reads:
            add(b.w)
        for b in writes:
            add(b.w)
            for t in b.r:
                add(t)
        for t in extra:
            add(t)
        out = []
        seen = self.seen[eng]
        for key, val in need.items():
            if skip_pe and key[0] == "e" and key[1] == "pe":
                continue
            if seen.get(key, 0) < val:
                seen[key] = val
                out.append((key, val))
        return out

    @staticmethod
    def _bufs(xs):
        out = []
        for x in xs:
            if x is None:
                continue
            out.append(x.b if isinstance(x, T) else x)
        return out

    def _commit(self, tok, reads, writes):
        self.last[tok[0]] = tok[1]
        for b in reads:
            b.r.append(tok)
            if len(b.r) > 64:
                mx = {}
                for k, v in b.r:
                    if mx.get(k, 0) < v:
                        mx[k] = v
                b.r = list(mx.items())
        for b in writes:
            b.w = tok
            b.r = []
        self.n_ops += 1

    def op(self, eng, fn, reads=(), writes=(), partial=False):
        reads = self._bufs(reads)
        writes = self._bufs(writes)
        ex = [b for b in reads if b.excl]
        if ex:
            reads = [b for b in reads if not b.excl]
            writes = writes + [b for b in ex if b not in writes]
        skip_pe = False
        if eng == "pe":
            skip_pe = (not partial) and all(not b.pe_partial for b in writes)
            for b in writes:
                b.pe_partial = partial
        waits = self._waits(eng, reads, writes, skip_pe=skip_pe)
        self.cnt[eng] += 1
        epoch, val = divmod(self.cnt[eng] - 1, EPOCH)
        tok = (("e", eng, epoch), val + 1)
        self.ops[eng].append((waits, fn, tok))
        self._commit(tok, reads, writes)
        return tok

    def dma(self, out_ap, in_ap, reads=(), writes=(), q="sp", is_output=False, **kw):
        reads = self._bufs(reads)
        writes = self._bufs(writes)
        i = self.dma_k % N_DMA_SEM
        self.dma_k += 1
        prev = self.dma_use[i]
        extra = [(("d", i), 16 * prev)] if prev else []
        waits = self._waits(q, reads, writes, extra)
        self.dma_use[i] = prev + 1
        tok = (("d", i), 16 * (prev + 1))

        def fn(e):
            return e.dma_start(out=out_ap, in_=in_ap, **kw)
        self.ops[q].append((waits, fn, tok))
        self._commit(tok, reads, writes)
        if is_output:
            self.out_tokens.append(tok)
        return tok

    def barrier(self):
        toks = list(self.last.items())
        for e in ENGS:
            waits = self._waits(e, [], [], toks)
            if waits:
                self.ops[e].append((waits, None, None))

    def _sem_of(self, key):
        if key[0] == "d":
            return self.dma_sems[key[1]]
        return self._eng_sem(key[1], key[2])

    def emit(self):
        nc = self.nc
        self.barrier()
        for e in ENGS:
            for waits, fn, tok in self.ops[e]:
                if tok is not None:
                    self._sem_of(tok[0])
                for key, val in waits:
                    self._sem_of(key)
        with nc.Block() as block:
            def run(e, handle):
                for waits, fn, tok in self.ops[e]:
                    for key, val in waits:
                        handle.wait_ge(self._sem_of(key), val)
                    if fn is None:
                        continue
                    ins = fn(handle)
                    key, val = tok
                    ins.then_inc(self._sem_of(key), 16 if key[0] == "d" else 1)

            @block.sync
            def _(h):
                run("sp", h)

            @block.tensor
            def _(h):
                run("pe", h)

            @block.scalar
            def _(h):
                run("act", h)

            @block.vector
            def _(h):
                run("dve", h)

            @block.gpsimd
            def _(h):
                run("pool", h)

    def close(self):
        self.es.close()


D = 1024
S = 4352
NT = 34
LAT0 = 256
DEPTH = 2
EPS = 1e-6
NEG = -30000.0
ORDER = [list(range(NT)), [1, 0] + list(range(NT - 1, 1, -1))]

C_HQ, C_HI, C_HG, C_HFF, C_HFB = 0, 256, 512, 768, 1024
C_RQ, C_RK, C_RV, C_RG = 1280, 1536, 1792, 2048
C_GQKV, C_GG, C_GA, C_GB, C_SU = 2304, 3072, 3328, 3336, 3344
Z_HQ, Z_HG, Z_HFF, Z_HFB, Z_RQ, Z_RK, Z_RG, Z_GQKV, Z_GG, Z_SU = 0, 256, 512, 768, 1024, 1280, 1536, 1792, 2560, 2816
NZF = 3072
FM_MAP = [(Z_HQ, C_HQ, 256), (Z_HG, C_HG, 256), (Z_HFF, C_HFF, 256), (Z_HFB, C_HFB, 256), (Z_RQ, C_RQ, 256),
          (Z_RK, C_RK, 256), (Z_RG, C_RG, 256), (Z_GQKV, C_GQKV, 768), (Z_GG, C_GG, 256), (Z_SU, C_SU, 256)]
FM_BLOCKS = [(zr + i, wc + i) for zr, wc, n in FM_MAP for i in range(0, n, 128)]
NZT = 528

CN = {}


def _const_pack():
    mats = []

    def add(name, m):
        CN[name] = len(mats)
        mats.append(np.asarray(m, np.float32))
    p = np.arange(128)[:, None]
    f = np.arange(128)[None, :]
    add("IDENT", (p == f))
    add("ONES", np.ones((128, 128)))
    add("TRIF", (p <= f))
    add("TRIB", (p >= f))
    add("SUFF", (p > f))
    add("PREB", (p < f))
    add("NLE", np.where(p <= f, 0.0, NEG))
    add("NLT", np.where(p < f, 0.0, NEG))
    add("NGE", np.where(p >= f, 0.0, NEG))
    add("NGT", np.where(p > f, 0.0, NEG))
    for s in (1, 2, 4, 8, 16, 32, 64):
        m = (((p // s) % 2) == 1) & ((f // s) == (p // s) - 1)
        add("MOFF%d" % s, m)
        add("MOFFT%d" % s, m.T)
    add("BLK64", (p // 64) == (f // 64))
    rot = np.zeros((128, 128))
    for m in range(128):
        if (m % 64) < 32:
            rot[m + 32, m] = -1.0
        else:
            rot[m - 32, m] = 1.0
    add("ROT", rot)
    add("IOTAF", np.broadcast_to(f, (128, 128)))
    add("IOTAF1", np.broadcast_to(f + 1, (128, 128)))
    add("RIOTAF", np.broadcast_to(128 - f, (128, 128)))
    add("R127F", np.broadcast_to(127 - f, (128, 128)))
    add("DIFF", f - p)
    add("NDIFF", p - f)
    for h in range(4):
        m = np.zeros((128, 128)); m[h, :] = 1.0
        add("SELH%d" % h, m)
    for hp in range(2):
        m = np.zeros((128, 128)); m[2 * hp, 0:64] = 1.0; m[2 * hp + 1, 64:128] = 1.0
        add("SELP%d" % hp, m)
    cc = np.zeros((128, 128))
    cc[:, 0] = EPS; cc[:, 1] = 1.0; cc[:, 2] = np.arange(128); cc[:, 3] = 127 - np.arange(128)
    cc[:, 5] = -np.pi; cc[:, 6] = -np.arange(128); cc[:, 7] = -(127 - np.arange(128))
    add("CCOL", cc)
    gm = np.zeros((128, 128))
    for g in range(16):
        gm[(g % 8) * 16:(g % 8) * 16 + 16, g] = 1.0
    add("GMASK", gm)
    return np.concatenate(mats, axis=1)


CONST_NP = _const_pack()
NCONST = CONST_NP.shape[1] // 128


def _rope_tables():
    half = 32
    inv = 10000.0 ** (-np.arange(half, dtype=np.float64) / half)
    pos = np.arange(S, dtype=np.float64)
    ang = pos[None, :] * inv[:, None]
    cos = np.cos(ang); sin = np.sin(ang)
    cos128 = np.tile(cos, (4, 1)); sin128 = np.tile(sin, (4, 1))
    return cos128.astype(np.float32), sin128.astype(np.float32)


def _conv_masks():
    m = np.ones((2, 512), np.float32)
    w = np.arange(512) % 64
    m[0, w == 0] = 0.0
    m[1, w == 63] = 0.0
    lat = np.broadcast_to(m[None], (128, 2, 512)).copy()
    c = np.ones((2, 256), np.float32)
    c[0, 0] = 0.0
    c[1, 255] = 0.0
    ctx = np.broadcast_to(c[None], (128, 2, 256)).copy()
    return lat, ctx


class KB:
    def __init__(self, cfg):
        self.cfg = cfg
        nc = bass.Bass("TRN2", target_bir_lowering=False)
        self.nc = nc
        self.P = Prog(nc)
        self.rr = 0

    def MM(self, ps, lhsT, rhs, st, sp, R, W):
        partial = lhsT.partition_size() < 128
        self.P.op("pe", lambda e: e.matmul(ps, lhsT, rhs, start=st, stop=sp), R, W, partial=partial)

    def TR(self, ps, in_, ident, R, W):
        self.P.op("pe", lambda e: e.transpose(ps, in_, ident), R, W)

    def ACT(self, out, in_, func, R, W, **kw):
        self.P.op("act", lambda e: e.activation(out=out, in_=in_, func=func, **kw), R, W)

    def TS(self, out, in0, s1, s2, op0, op1, R, W, eng="dve"):
        if s2 is None:
            self.P.op(eng, lambda e: e.tensor_scalar(out=out, in0=in0, scalar1=s1, scalar2=None, op0=op0), R, W)
        else:
            self.P.op(eng, lambda e: e.tensor_scalar(out=out, in0=in0, scalar1=s1, scalar2=s2, op0=op0, op1=op1), R, W)

    def TT(self, out, in0, in1, op, R, W, eng="dve"):
        self.P.op(eng, lambda e: e.tensor_tensor(out=out, in0=in0, in1=in1, op=op), R, W)

    def STT(self, out, in0, sc, in1, op0, op1, R, W, eng="dve"):
        eng = "dve"
        self.P.op(eng, lambda e: e.scalar_tensor_tensor(out=out, in0=in0, scalar=sc, in1=in1, op0=op0, op1=op1), R, W)

    def CP(self, out, in_, R, W, eng="dve"):
        if eng == "act":
            self.ACT(out, in_, AF.Copy, R, W)
        else:
            self.P.op(eng, lambda e: e.tensor_copy(out=out, in_=in_), R, W)

    def CPRED(self, out, mask, data, R, W):
        self.P.op("dve", lambda e: e.copy_predicated(out=out, mask=mask, data=data), R, W)

    def MS(self, ap, val, W, eng="dve"):
        self.P.op(eng, lambda e: e.memset(ap, val), (), W)

    def RECIP(self, out, in_, R, W):
        self.P.op("dve", lambda e: e.reciprocal(out=out, in_=in_), R, W)

    def SCAN(self, out, d0, d1, R, W):
        self.P.op("dve", lambda e: e.tensor_tensor_scan(out=out, data0=d0, data1=d1, initial=0.0,
                                                        op0=ALU.mult, op1=ALU.add), R, W)

    def LD(self, out, in_, W, R=(), q="sp", **kw):
        self.P.dma(out, in_, reads=R, writes=W, q=q, **kw)

    def ST(self, out, in_, R, W=(), q="pool", **kw):
        self.P.dma(out, in_, reads=R, writes=W, q=q, **kw)

    def evac_eng(self):
        self.rr += 1
        return "act" if self.rr % 2 else "dve"

    def C(self, name):
        i = CN[name]
        return self.const[:, i * 128:(i + 1) * 128]


PARAM_SHAPES = {
    "mod_w": [2, 1024, 6144], "mod_b": [2, 6144], "norm1_g": [2, 1024], "norm2_g": [2, 1024],
    "w_in": [2, 1024, 3600], "hgrn_lb_logits": [2, 2, 256], "hgrn_norm_g": [2, 64],
    "ret_decay_logit": [2, 2, 4], "gdn_conv_w": [2, 3, 3, 768], "gdn_a_log": [2, 2, 4],
    "gdn_dt_bias": [2, 2, 4], "gdn_norm_g": [2, 64], "s5_lam_re": [2, 2, 16, 64],
    "s5_lam_im": [2, 2, 16, 64], "s5_log_dt": [2, 2, 16], "s5_b_re": [2, 16, 64, 16],
    "s5_b_im": [2, 16, 64, 16], "s5_c_re": [2, 16, 16, 64], "s5_c_im": [2, 16, 16, 64],
    "s5_d": [2, 256], "s5_glu_w": [2, 256, 256], "s5_glu_b": [2, 256], "w_out": [2, 1024, 1024],
    "mlp_w1": [2, 1024, 4096], "mlp_w2": [2, 4096, 1024], "final_norm_g": [1024],
}


def declare(kb):
    P = kb.P
    cfg = kb.cfg
    kinds = cfg.get("kinds", {})
    kb.xin = P.dram("xin", [S, D], F32, kind="ExternalInput")
    kb.cvecT = P.dram("cvecT", [1024, 2], F32, kind="ExternalInput")
    kb.prm = {k: P.dram(k, shp, F32, kind="ExternalInput") for k, shp in PARAM_SHAPES.items()}
    kb.constd = P.dram("constp", [128, NCONST * 128], F32, kind="ExternalInput")
    kb.ropec = P.dram("ropec", [128, S], F32, kind="ExternalInput")
    kb.ropes = P.dram("ropes", [128, S], F32, kind="ExternalInput")
    kb.cmlat = P.dram("cmlat", [128, 2, 512], F32, kind="ExternalInput")
    kb.cmctx = P.dram("cmctx", [128, 2, 256], F32, kind="ExternalInput")
    kb.cwin = P.dram("cwin", [128, 2, 642], F32, kind="ExternalInput")
    kb.y = P.dram("y", [4096, D], F32, kind="ExternalOutput")
    kb.XS = P.dram("XS", [S, D], F32, kind=kinds.get("XS", "Internal"))
    kb.ZF = P.dram("ZF", [NZF, S], F32, kind=kinds.get("ZF", "Internal"))
    kb.ZT = P.dram("ZT", [S, NZT], F32, kind=kinds.get("ZT", "Internal"))
    kb.QKVF = P.dram("QKVF", [768, S], F32, kind=kinds.get("QKVF", "Internal"))
    kb.YC = P.dram("YC", [1024, S], F32, kind=kinds.get("YC", "Internal"))
    kb.H2T = P.dram("H2T", [1024, S], BF16, kind=kinds.get("H2T", "Internal"))
    kb.const = P.sbuf("const", [128, NCONST * 128])
    nchunk = 4
    w = NCONST * 128 // nchunk
    for i in range(nchunk):
        a, b = i * w, (i + 1) * w if i < nchunk - 1 else NCONST * 128
        kb.LD(kb.const[:, a:b], kb.constd[:, a:b], [kb.const.s(i)])
    kb.const_bufs = [kb.const.s(i) for i in range(nchunk)]
    kb.CB = kb.const_bufs
    kb.GS1 = P.sbuf("GS1", [128, 8, 2]); kb.SH1 = P.sbuf("SH1", [128, 8, 2])
    kb.GS2 = P.sbuf("GS2", [128, 8, 2]); kb.SH2 = P.sbuf("SH2", [128, 8, 2])
    kb.GATE1 = P.sbuf("GATE1", [128, 2, 1024]); kb.GATE2 = P.sbuf("GATE2", [128, 2, 1024])


def phase_mod(kb, l):
    P = kb.P
    prm = kb.prm
    with P.scope():
        cT = P.sbuf("cT", [128, 8, 2])
        kb.LD(cT[:], kb.cvecT[:].rearrange("(et e) c -> e et c", e=128), [cT])
        sc = P.sbuf("sc", [128, 8, 2])
        kb.ACT(sc[:], cT[:], AF.Silu, [cT], [sc])
        screp = P.sbuf("screp", [128, 8, 2, 128])
        kb.CP(screp[:], sc[:].unsqueeze(3).to_broadcast([128, 8, 2, 128]), [sc], [screp])
        mbf = P.sbuf("mbf", [128, 48])
        kb.LD(mbf[:], prm["mod_b"][l].rearrange("(j p) -> p j", p=128), [mbf], allow_slow_non_contiguous=True)
        ngf = P.sbuf("ngf", [128, 2, 8])
        kb.LD(ngf[:, 0, :], prm["norm1_g"][l].rearrange("(j p) -> p j", p=128), [ngf], allow_slow_non_contiguous=True)
        kb.LD(ngf[:, 1, :], prm["norm2_g"][l].rearrange("(j p) -> p j", p=128), [ngf], allow_slow_non_contiguous=True)
        mbrow = P.sbuf("mbrow", [128, 2, 1024])
        for gi, v in enumerate((2, 5)):
            kb.LD(mbrow[:, gi, :], prm["mod_b"][l][v * 1024:(v + 1) * 1024].partition_broadcast(128), [mbrow])
        wch = [P.sbuf("wch%d" % i, [128, 8, 1024]) for i in range(2)]
        ps_fm = P.psum("ps_fm", [128, 96])
        ps_g = [P.psum("ps_g%d" % i, [128, 512]) for i in range(2)]
        MF = P.sbuf("MF", [128, 48, 2])
        k = 0
        for v in range(6):
            wc = wch[v % 2]
            for et in range(8):
                kb.LD(wc[:, et, :], prm["mod_w"][l][et * 128:(et + 1) * 128, v * 1024:(v + 1) * 1024], [wc])
            for db in range(8):
                col = (v * 8 + db) * 2
                for et in range(8):
                    kb.MM(ps_fm[:, col:col + 2], wc[:, et, db * 128:(db + 1) * 128], sc[:, et, :],
                          et == 0, et == 7, [wc, sc], [ps_fm])
            if v in (2, 5):
                gt = kb.GATE1 if v == 2 else kb.GATE2
                gi = 0 if v == 2 else 1
                for which in range(2):
                    for half in range(2):
                        pg = ps_g[k % 2]; k += 1
                        for et in range(8):
                            kb.MM(pg[:], screp[:, et, which, :], wc[:, et, half * 512:(half + 1) * 512],
                                  et == 0, et == 7, [screp, wc], [pg])
                        kb.TT(gt[:, which, half * 512:(half + 1) * 512], pg[:], mbrow[:, gi, half * 512:(half + 1) * 512],
                              ALU.add, [pg, mbrow], [gt])
        kb.TT(MF[:], ps_fm[:].rearrange("p (j c) -> p j c", c=2), mbf[:].unsqueeze(2).to_broadcast([128, 48, 2]),
              ALU.add, [ps_fm, mbf], [MF])
        tmp = P.sbuf("mtmp", [128, 8, 2])
        kb.TS(tmp[:], MF[:, 8:16, :], 1.0, None, ALU.add, None, [MF], [tmp])
        kb.TT(kb.GS1[:], tmp[:], ngf[:, 0, :].unsqueeze(2).to_broadcast([128, 8, 2]), ALU.mult, [tmp, ngf], [kb.GS1])
        kb.CP(kb.SH1[:], MF[:, 0:8, :], [MF], [kb.SH1])
        tmp2 = P.sbuf("mtmp2", [128, 8, 2])
        kb.TS(tmp2[:], MF[:, 32:40, :], 1.0, None, ALU.add, None, [MF], [tmp2])
        kb.TT(kb.GS2[:], tmp2[:], ngf[:, 1, :].unsqueeze(2).to_broadcast([128, 8, 2]), ALU.mult, [tmp2, ngf], [kb.GS2])
        kb.CP(kb.SH2[:], MF[:, 24:32, :], [MF], [kb.SH2])


def norm_to_fm(kb, xt, hT, col0, GS, SH, which, bufs, R_x):
    P = kb.P
    junk, st, xn, ps_ts = bufs["junk"], bufs["st"], bufs["xn"], bufs["ps_t"]
    kb.MS(st[:, 0:1], 0.0, [st])
    kb.ACT(junk[:], xt[:], AF.Square, [xt], [junk, st], accum_out=st[:, 0:1])
    kb.ACT(st[:, 1:2], st[:, 0:1], AF.Sqrt, [st] + kb.CB, [st], scale=1.0 / D, bias=kb.C("CCOL")[:, 0:1])
    kb.RECIP(st[:, 2:3], st[:, 1:2], [st], [st])
    kb.ACT(xn[:], xt[:], AF.Copy, [xt, st], [xn], scale=st[:, 2:3])
    for half in range(2):
        ps_t = ps_ts[half]
        for q in range(4):
            dt = half * 4 + q
            kb.TR(ps_t[:, q * 128:(q + 1) * 128], xn[:, dt * 128:(dt + 1) * 128], kb.C("IDENT"), [xn] + kb.CB, [ps_t])
        for q in range(4):
            dt = half * 4 + q
            if q % 2 == 0:
                kb.TS(hT[:, dt, col0:col0 + 128], ps_t[:, q * 128:(q + 1) * 128], GS[:, dt, which:which + 1],
                      SH[:, dt, which:which + 1], ALU.mult, ALU.add, [ps_t, GS, SH], [hT])
            else:
                kb.ACT(hT[:, dt, col0:col0 + 128], ps_t[:, q * 128:(q + 1) * 128], AF.Identity, [ps_t, GS, SH], [hT],
                       scale=GS[:, dt, which:which + 1], bias=SH[:, dt, which:which + 1])


def norm_to_fm_g(kb, xt, hT, col0, GS, SH, which, bufs):
    junk, st, xn, ps_ts = bufs["junk"], bufs["st"], bufs["xn"], bufs["ps_t"]
    kb.MS(st[:, 0:1], 0.0, [st])
    kb.ACT(junk[:], xt[:], AF.Square, [xt], [junk, st], accum_out=st[:, 0:1])
    yield
    kb.ACT(st[:, 1:2], st[:, 0:1], AF.Sqrt, [st], [st], scale=1.0 / D, bias=kb.C("CCOL")[:, 0:1])
    kb.RECIP(st[:, 2:3], st[:, 1:2], [st], [st])
    yield
    kb.ACT(xn[:], xt[:], AF.Copy, [xt, st], [xn], scale=st[:, 2:3])
    yield
    for half in range(2):
        ps_t = ps_ts[half]
        for q in range(4):
            dt = half * 4 + q
            kb.TR(ps_t[:, q * 128:(q + 1) * 128], xn[:, dt * 128:(dt + 1) * 128], kb.C("IDENT"), [xn], [ps_t])
        yield
        for q in range(4):
            dt = half * 4 + q
            if q % 2 == 0:
                kb.TS(hT[:, dt, col0:col0 + 128], ps_t[:, q * 128:(q + 1) * 128], GS[:, dt, which:which + 1],
                      SH[:, dt, which:which + 1], ALU.mult, ALU.add, [ps_t, GS, SH], [hT])
            else:
                kb.ACT(hT[:, dt, col0:col0 + 128], ps_t[:, q * 128:(q + 1) * 128], AF.Identity, [ps_t, GS, SH], [hT],
                       scale=GS[:, dt, which:which + 1], bias=SH[:, dt, which:which + 1])
        yield


def phase_a(kb, l, src):
    P = kb.P
    with P.scope():
        win = P.sbuf("win", [128, 8, 3600], BF16)
        for kt in range(8):
            kb.LD(win[:, kt, :], kb.prm["w_in"][l][kt * 128:(kt + 1) * 128, :], [win.s(kt)], q="pool")
        winb = [win.s(kt) for kt in range(8)]
        xbuf = [P.sbuf("xa%d" % i, [128, 1024]) for i in range(2)]
        hTb = [P.sbuf("hTa%d" % i, [128, 8, 512], BF16) for i in range(2)]
        nb = {"junk": P.sbuf("junk", [128, 1024]), "st": P.sbuf("st", [128, 4]), "xn": P.sbuf("xn", [128, 1024]),
              "ps_t": [P.psum("ps_t%d" % i, [128, 512]) for i in range(2)]}
        ps_f = [P.psum("ps_f%d" % i, [128, 512]) for i in range(3)]
        ps_a = [P.psum("ps_a%d" % i, [128, 512]) for i in range(2)]
        ps_b = P.psum("ps_b", [128, 16])
        stg = [P.sbuf("stg%d" % i, [128, 512]) for i in range(4)]
        stt = [P.sbuf("stt%d" % i, [128, NZT]) for i in range(2)]
        kx = kf = ks = ka = 0
        for gi, t0 in enumerate(range(0, S, 512)):
            n = min(512, S - t0)
            hT = hTb[gi % 2]
            for ti in range(n // 128):
                tt = t0 // 128 + ti
                which = 1 if tt < 2 else 0
                xt = xbuf[kx % 2]; kx += 1
                kb.LD(xt[:], src[tt * 128:(tt + 1) * 128, :], [xt])
                norm_to_fm(kb, xt, hT, ti * 128, kb.GS1, kb.SH1, which, nb, None)
            for (zr, wc) in FM_BLOCKS:
                ps = ps_f[kf % 3]; kf += 1
                for kt in range(8):
                    kb.MM(ps[:, :n], win[:, kt, wc:wc + 128], hT[:, kt, :n], kt == 0, kt == 7, [winb[kt], hT], [ps])
                sg = stg[ks % 4]; ks += 1
                kb.CP(sg[:, :n], ps[:, :n], [ps], [sg], eng=kb.evac_eng())
                kb.ST(kb.ZF[zr:zr + 128, t0:t0 + n], sg[:, :n], [sg])
            for ti in range(n // 128):
                tt = t0 // 128 + ti
                pa = ps_a[ka % 2]
                so = stt[ka % 2]; ka += 1
                for (c0, w0, wn) in ((0, C_HI, 256), (256, C_RV, 256)):
                    for kt in range(8):
                        kb.MM(pa[:, c0:c0 + wn], hT[:, kt, ti * 128:(ti + 1) * 128], win[:, kt, w0:w0 + wn],
                              kt == 0, kt == 7, [winb[kt], hT], [pa])
                for kt in range(8):
                    kb.MM(ps_b[:], hT[:, kt, ti * 128:(ti + 1) * 128], win[:, kt, C_GA:C_GA + 16],
                          kt == 0, kt == 7, [winb[kt], hT], [ps_b])
                kb.CP(so[:, 0:512], pa[:], [pa], [so], eng="act")
                kb.CP(so[:, 512:528], ps_b[:], [ps_b], [so], eng="dve")
                kb.ST(kb.ZT[tt * 128:(tt + 1) * 128, :], so[:], [so])


def build(cfg):
    kb = KB(cfg)
    P = kb.P
    declare(kb)
    P.barrier()
    stages = cfg.get("stages", "all")
    for l in cfg.get("layers", range(DEPTH)):
        src = kb.xin if l == 0 else kb.XS
        if stages == "all" or "M" in stages:
            phase_mod(kb, l)
        if stages == "all" or "A" in stages:
            phase_a(kb, l, src)
        if stages == "all" or "R" in stages:
            (mixer_ret if cfg.get("ret_old") else mixer_ret2)(kb, l)
        if stages == "all" or "H" in stages:
            (mixer_hgrn if cfg.get("hgrn_old") else mixer_hgrn2)(kb, l)
        if stages == "all" or "G" in stages:
            (mixer_gdn if cfg.get("gdn_old") else mixer_gdn2)(kb, l)
        if stages == "all" or "S" in stages:
            (mixer_s5 if cfg.get("s5_old") else mixer_s5_2)(kb, l)
        if stages == "all" or "C" in stages:
            phase_c(kb, l, src)
    P.emit()
    P.close()
    return kb


_CONSTS = None


def host_inputs(inputs, cores=range(8)):
    global _CONSTS
    if _CONSTS is None:
        rc, rs = _rope_tables()
        cl, cc = _conv_masks()
        _CONSTS = {"constp": CONST_NP, "ropec": rc, "ropes": rs, "cmlat": cl, "cmctx": cc, "cwin": _conv_win_masks()}
    maps = []
    for b in cores:
        m = {"xin": np.ascontiguousarray(np.concatenate([inputs["ctx"][b], inputs["x"][b]], axis=0), dtype=np.float32),
             "cvecT": np.ascontiguousarray(np.stack([inputs["c"][b], inputs["c_ctx"]], axis=1), dtype=np.float32)}
        for k in PARAM_SHAPES:
            m[k] = np.ascontiguousarray(inputs[k], dtype=np.float32)
        m.update(_CONSTS)
        maps.append(m)
    return maps


def kernel(**inputs):
    inputs = {k: np.asarray(v) for k, v in inputs.items()}
    kb = build({})
    maps = host_inputs(inputs)
    res = run_bass_kernel_spmd(kb.nc, maps, core_ids=list(range(8)))
    out = np.stack([np.asarray(r["y"]).reshape(4096, D) for r in res.results], axis=0)
    return out.astype(np.float32)


def phase_c(kb, l, src):
    P = kb.P
    last = (l == DEPTH - 1)
    t_start = 2 if last else 0
    with P.scope():
        wout = P.sbuf("wout", [128, 8, 1024], BF16)
        for ft in range(8):
            kb.LD(wout[:, ft, :], kb.prm["w_out"][l][ft * 128:(ft + 1) * 128, :], [wout.s(ft)], q="pool")
        wb = [wout.s(ft) for ft in range(8)]

        def make(si):
            pf = "c%d_" % si
            yc = P.sbuf(pf + "yc", [128, 8, 128], BF16)
            xt = P.sbuf(pf + "x", [128, 1024]); x1 = P.sbuf(pf + "x1", [128, 1024])
            tmpb = [P.sbuf(pf + "t%d" % i, [128, 512]) for i in range(2)]
            h2 = P.sbuf(pf + "h2", [128, 8, 128], BF16)
            nb = {"junk": P.sbuf(pf + "junk", [128, 1024]), "st": P.sbuf(pf + "st", [128, 4]), "xn": P.sbuf(pf + "xn", [128, 1024]),
                  "ps_t": [P.psum(pf + "pst%d" % i, [128, 512]) for i in range(2)]}
            ps_y = [P.psum(pf + "psy%d" % i, [128, 512]) for i in range(2)]

            def gen():
                for tt in range(t_start + si, NT, 2):
                    which = 1 if tt < 2 else 0
                    cols = slice(tt * 128, (tt + 1) * 128)
                    kb.LD(yc[:], kb.YC[:, cols].rearrange("(ft p) t -> p ft t", p=128), [yc], q="pool")
                    kb.LD(xt[:], src[cols, :], [xt])
                    yield
                    for half in range(2):
                        ps = ps_y[half]
                        for ft in range(8):
                            kb.MM(ps[:], yc[:, ft, :], wout[:, ft, half * 512:(half + 1) * 512], ft == 0, ft == 7,
                                  [yc, wb[ft]], [ps])
                        yield
                    for half in range(2):
                        ps = ps_y[half]
                        tm = tmpb[half]
                        kb.TT(tm[:], ps[:], kb.GATE1[:, which, half * 512:(half + 1) * 512], ALU.mult, [ps, kb.GATE1], [tm])
                        kb.TT(x1[:, half * 512:(half + 1) * 512], xt[:, half * 512:(half + 1) * 512], tm[:], ALU.add,
                              [xt, tm], [x1], eng="pool")
                        yield
                    kb.ST(kb.XS[cols, :], x1[:], [x1])
                    yield from norm_to_fm_g(kb, x1, h2, 0, kb.GS2, kb.SH2, which, nb)
                    kb.ST(kb.H2T[:, cols].rearrange("(dt p) t -> p dt t", p=128), h2[:], [h2])
                    yield
            return gen()
        run_interleaved([make(0), make(1)])
    with P.scope():
        w1 = P.sbuf("w1", [128, 8, 4096], BF16)
        w2 = P.sbuf("w2", [128, 32, 1024], BF16)
        for kt in range(8):
            kb.LD(w1[:, kt, :], kb.prm["mlp_w1"][l][kt * 128:(kt + 1) * 128, :], [w1.s(kt)], q="pool")
        for fb in range(32):
            kb.LD(w2[:, fb, :], kb.prm["mlp_w2"][l][fb * 128:(fb + 1) * 128, :], [w2.s(fb)], q="pool")
        h2b = [P.sbuf("h2d%d" % i, [128, 8, 256], BF16) for i in range(2)]
        uTb = [P.sbuf("uT%d" % i, [128, 16, 256], BF16) for i in range(1)]
        rb = [P.sbuf("relu%d" % i, [128, 256]) for i in range(3)]
        xb = [P.sbuf("xd%d" % i, [128, 1024]) for i in range(2)]
        tmpb = [P.sbuf("td%d" % i, [128, 512]) for i in range(2)]
        ps_u = [P.psum("ps_u%d" % i, [128, 256]) for i in range(3)]
        ps_y = [P.psum("ps_y2%d" % i, [128, 512]) for i in range(4)]
        if last:
            fg = P.sbuf("fg", [128, 1024])
            kb.LD(fg[:], kb.prm["final_norm_g"][:].partition_broadcast(128), [fg])
            stf = P.sbuf("stf", [128, 4])
            xnf = P.sbuf("xnf", [128, 1024])
        k = 0; ku = 0
        for g0 in range(t_start, NT, 2):
            h2 = h2b[k % 2]; uT = uTb[0]
            cols = slice(g0 * 128, (g0 + 2) * 128)
            kb.LD(h2[:], kb.H2T[:, cols].rearrange("(dt p) t -> p dt t", p=128), [h2])
            for hh in range(2):
                for fl in range(16):
                    fb = hh * 16 + fl
                    ps = ps_u[ku % 3]; r = rb[ku % 3]; ku += 1
                    for kt in range(8):
                        kb.MM(ps[:], w1[:, kt, fb * 128:(fb + 1) * 128], h2[:, kt, :], kt == 0, kt == 7, [w1.s(kt), h2], [ps])
                    kb.ACT(r[:], ps[:], AF.Relu, [ps], [r])
                    kb.TT(uT[:, fl, :], r[:], r[:], ALU.mult, [r], [uT.s(fl)], eng=("dve" if fb % 2 else "pool"))
                for ti in range(2):
                    for half in range(2):
                        ps = ps_y[2 * ti + half]
                        for fl in range(16):
                            fb = hh * 16 + fl
                            kb.MM(ps[:], uT[:, fl, ti * 128:(ti + 1) * 128], w2[:, fb, half * 512:(half + 1) * 512],
                                  fb == 0, fb == 31, [uT.s(fl), w2.s(fb)], [ps])
            for ti in range(2):
                tt = g0 + ti
                which = 1 if tt < 2 else 0
                xt = xb[ti]
                rows = slice(tt * 128, (tt + 1) * 128)
                kb.LD(xt[:], kb.XS[rows, :], [xt])
                for half in range(2):
                    ps = ps_y[2 * ti + half]
                    tm = tmpb[half]
                    kb.TT(tm[:], ps[:], kb.GATE2[:, which, half * 512:(half + 1) * 512], ALU.mult, [ps, kb.GATE2], [tm])
                    kb.TT(xt[:, half * 512:(half + 1) * 512], xt[:, half * 512:(half + 1) * 512], tm[:], ALU.add,
                          [xt, tm], [xt], eng="pool")
                if not last:
                    kb.ST(kb.XS[rows, :], xt[:], [xt])
                else:
                    kb.MS(stf[:, 0:1], 0.0, [stf])
                    kb.ACT(xnf[:], xt[:], AF.Square, [xt], [xnf, stf], accum_out=stf[:, 0:1])
                    kb.ACT(stf[:, 1:2], stf[:, 0:1], AF.Sqrt, [stf], [stf], scale=1.0 / D, bias=kb.C("CCOL")[:, 0:1])
                    kb.RECIP(stf[:, 2:3], stf[:, 1:2], [stf], [stf])
                    kb.ACT(xnf[:], xt[:], AF.Copy, [xt, stf], [xnf], scale=stf[:, 2:3])
                    kb.TT(xnf[:], xnf[:], fg[:], ALU.mult, [xnf, fg], [xnf])
                    kb.P.dma(kb.y[(tt - 2) * 128:(tt - 1) * 128, :], xnf[:], reads=[xnf.b], q="pool", is_output=True)
            k += 1


def finalize_gated(kb, OACC, gate_row0, gain, yc_row0, pfx):
    P = kb.P
    def two(nm):
        return [P.sbuf(pfx + nm + "%d" % i, [128, 2, 128]) for i in range(2)]
    gb, sq, rt, eg, ob = two("fg"), two("fsq"), two("frt"), two("feg"), two("fo")
    ps_m = [P.psum(pfx + "fps%d" % i, [128, 2, 128]) for i in range(2)]
    for n in range(NT):
        cols = slice(n * 128, (n + 1) * 128)
        i = n % 2
        g = gb[i]
        kb.LD(g[:], kb.ZF[gate_row0:gate_row0 + 256, cols].rearrange("(hp p) t -> p hp t", p=128), [g])
        o = OACC[:, :, cols]
        kb.TT(sq[i][:], o, o, ALU.mult, [OACC.s(n)], [sq[i]])
        kb.MM(ps_m[i][:].rearrange("p a b -> p (a b)"), kb.C("BLK64"), sq[i][:].rearrange("p a b -> p (a b)"), True, True,
              [sq[i]], [ps_m[i]])
        kb.ACT(rt[i][:], ps_m[i][:], AF.Ln, [ps_m[i]], [rt[i]], scale=1.0 / 64, bias=kb.C("CCOL")[:, 0:1])
        kb.ACT(rt[i][:], rt[i][:], AF.Exp, [rt[i]], [rt[i]], scale=-0.5)
        kb.ACT(eg[i][:], g[:], AF.Exp, [g], [eg[i]], scale=-1.0)
        kb.TS(eg[i][:], eg[i][:], 1.0, None, ALU.add, None, [eg[i]], [eg[i]])
        kb.RECIP(eg[i][:], eg[i][:], [eg[i]], [eg[i]])
        kb.TT(eg[i][:], eg[i][:], g[:], ALU.mult, [eg[i], g], [eg[i]], eng="pool")
        kb.TT(ob[i][:], o, rt[i][:], ALU.mult, [OACC.s(n), rt[i]], [ob[i]])
        if gain is not None:
            kb.STT(ob[i][:], ob[i][:], gain[:, 0:1], eg[i][:], ALU.mult, ALU.mult, [ob[i], gain, eg[i]], [ob[i]])
        else:
            kb.TT(ob[i][:], ob[i][:], eg[i][:], ALU.mult, [ob[i], eg[i]], [ob[i]])
        kb.ST(kb.YC[yc_row0:yc_row0 + 256, cols].rearrange("(hp p) t -> p hp t", p=128), ob[i][:], [ob[i]])


def oacc_write(kb, OACC, hp, n, ps, d):
    cols = slice(n * 128, (n + 1) * 128)
    if d == 0:
        kb.CP(OACC[:, hp, cols], ps[:], [ps], [OACC.s(n)], eng="act")
    else:
        kb.TT(OACC[:, hp, cols], OACC[:, hp, cols], ps[:], ALU.add, [ps], [OACC.s(n)])


def mixer_ret(kb, l):
    P = kb.P
    with P.scope():
        OACC = P.sbuf("r_oacc", [128, 2, S])
        with P.scope():
            lgt = P.sbuf("r_lgt", [128, 8])
            kb.LD(lgt[:], kb.prm["ret_decay_logit"][l].rearrange("d h -> (d h)").partition_broadcast(128), [lgt])
            LG = P.sbuf("r_LG", [128, 8])
            kb.ACT(LG[:], lgt[:], AF.Sigmoid, [lgt], [LG])
            kb.ACT(LG[:], LG[:], AF.Ln, [LG], [LG])
            LGP = P.sbuf("r_LGP", [128, 4])
            for d in range(2):
                for hp in range(2):
                    c = 2 * d + hp
                    kb.CP(LGP[0:64, c:c + 1], LG[0:64, 4 * d + 2 * hp:4 * d + 2 * hp + 1], [LG], [LGP])
                    kb.CP(LGP[64:128, c:c + 1], LG[64:128, 4 * d + 2 * hp + 1:4 * d + 2 * hp + 2], [LG], [LGP])
            MK = [P.sbuf("r_MK%d" % d, [128, 4, 128]) for d in range(2)]
            QDEC = [[P.sbuf("r_QD%d%d" % (d, hp), [128, 128]) for hp in range(2)] for d in range(2)]
            etmp = P.sbuf("r_etmp", [128, 128])
            for d in range(2):
                for h in range(4):
                    kb.ACT(etmp[:], kb.C("DIFF" if d == 0 else "NDIFF"), AF.Exp, [LG], [etmp],
                           scale=LG[:, 4 * d + h:4 * d + h + 1])
                    kb.STT(MK[d][:, h, :], etmp[:], 0.125, kb.C("TRIF" if d == 0 else "TRIB"), ALU.mult, ALU.mult,
                           [etmp], [MK[d]])
                for hp in range(2):
                    kb.ACT(QDEC[d][hp][:], kb.C("IOTAF1" if d == 0 else "RIOTAF"), AF.Exp, [LGP], [QDEC[d][hp]],
                           scale=LGP[:, 2 * d + hp:2 * d + hp + 1])
            KD = P.sbuf("r_KD", [128, 8])
            kb.ACT(KD[:, 0:4], LG[:, 0:4], AF.Exp, [LG], [KD], scale=kb.C("CCOL")[:, 3:4])
            kb.ACT(KD[:, 4:8], LG[:, 4:8], AF.Exp, [LG], [KD], scale=kb.C("CCOL")[:, 2:3])
            kb.TS(KD[:], KD[:], 0.125, None, ALU.mult, None, [KD], [KD])
            CV = P.sbuf("r_CV", [128, 4])
            kb.ACT(CV[:], LGP[:], AF.Exp, [LGP], [CV], scale=128.0)
            qTb = [P.sbuf("r_q%d" % i, [128, 2, 128]) for i in range(2)]
            kTb = [P.sbuf("r_k%d" % i, [128, 2, 128]) for i in range(2)]
            csb = [P.sbuf("r_cs%d" % i, [128, 2, 128]) for i in range(2)]
            Vp = [[P.sbuf("r_vp%d%d" % (i, h), [128, 128]) for h in range(4)] for i in range(2)]
            khp = [[P.sbuf("r_kh%d%d" % (i, h), [128, 128]) for h in range(4)] for i in range(2)]
            for i in range(2):
                for h in range(4):
                    kb.MS(Vp[i][h][:], 0.0, [Vp[i][h]], eng="pool")
                    kb.MS(khp[i][h][:], 0.0, [khp[i][h]], eng="pool")
            t1 = [P.sbuf("r_t1%d" % i, [128, 128]) for i in range(2)]
            t2 = [P.sbuf("r_t2%d" % i, [128, 128]) for i in range(2)]
            qr = [P.sbuf("r_qr%d" % i, [128, 2, 128]) for i in range(2)]
            kr = [P.sbuf("r_kr%d" % i, [128, 2, 128]) for i in range(2)]
            AT = [P.sbuf("r_AT%d" % i, [128, 2, 128]) for i in range(2)]
            qd = [P.sbuf("r_qd%d" % i, [128, 128]) for i in range(2)]
            Sb = [P.sbuf("r_S%d" % hp, [128, 128]) for hp in range(2)]
            ps_r = [P.psum("r_psr%d" % i, [128, 256]) for i in range(2)]
            ps_s = [P.psum("r_pss%d" % i, [128, 2, 128]) for i in range(2)]
            ps_o = [P.psum("r_pso%d" % i, [128, 128]) for i in range(2)]
            ps_k = P.psum("r_psk", [128, 128])
            ps_kv = P.psum("r_pskv", [128, 128])
            it = 0
            for d in range(2):
                for hp in range(2):
                    kb.MS(Sb[hp][:], 0.0, [Sb[hp]])
                for n in ORDER[d]:
                    cols = slice(n * 128, (n + 1) * 128)
                    b = it % 2; it += 1
                    qT, kT, cs = qTb[b], kTb[b], csb[b]
                    kb.LD(qT[:], kb.ZF[Z_RQ:Z_RQ + 256, cols].rearrange("(hp p) t -> p hp t", p=128), [qT])
                    kb.LD(kT[:], kb.ZF[Z_RK:Z_RK + 256, cols].rearrange("(hp p) t -> p hp t", p=128), [kT])
                    kb.LD(cs[:, 0, :], kb.ropec[:, cols], [cs])
                    kb.LD(cs[:, 1, :], kb.ropes[:, cols], [cs])
                    for h in range(4):
                        kb.LD(Vp[b][h][:, 64 * (h % 2):64 * (h % 2) + 64], kb.ZT[cols, 256 + 64 * h:256 + 64 * h + 64],
                              [Vp[b][h]])
                    for hp in range(2):
                        j = (it * 2 + hp) % 2
                        pr = ps_r[j]
                        kb.MM(pr[:, 0:128], kb.C("ROT"), qT[:, hp, :], True, True, [qT], [pr])
                        kb.MM(pr[:, 128:256], kb.C("ROT"), kT[:, hp, :], True, True, [kT], [pr])
                        for (src_, dst, off) in ((qT, qr[b], 0), (kT, kr[b], 128)):
                            kb.TT(t1[j][:], src_[:, hp, :], cs[:, 0, :], ALU.mult, [src_, cs], [t1[j]], eng="pool")
                            kb.TT(t2[j][:], pr[:, off:off + 128], cs[:, 1, :], ALU.mult, [pr, cs], [t2[j]])
                            kb.TT(dst[:, hp, :], t1[j][:], t2[j][:], ALU.add, [t1[j], t2[j]], [dst.s(hp)], eng="pool")
                        pss = ps_s[j]
                        for h2 in range(2):
                            kb.MM(pss[:, h2, :], kr[b][64 * h2:64 * h2 + 64, hp, :], qr[b][64 * h2:64 * h2 + 64, hp, :],
                                  True, True, [kr[b].s(hp), qr[b].s(hp)], [pss])
                        kb.TT(AT[j][:], pss[:], MK[d][:, 2 * hp:2 * hp + 2, :], ALU.mult, [pss, MK[d]], [AT[j]])
                        kb.TT(qd[j][:], qr[b][:, hp, :], QDEC[d][hp][:], ALU.mult, [qr[b].s(hp), QDEC[d][hp]], [qd[j]],
                              eng="pool")
                        po = ps_o[j]
                        kb.MM(po[:], Vp[b][2 * hp][:], AT[j][:, 0, :], True, False, [Vp[b][2 * hp], AT[j]], [po])
                        kb.MM(po[:], Vp[b][2 * hp + 1][:], AT[j][:, 1, :], False, False, [Vp[b][2 * hp + 1], AT[j]], [po])
                        kb.MM(po[:], Sb[hp][:], qd[j][:], False, True, [Sb[hp], qd[j]], [po])
                        oacc_write(kb, OACC, hp, n, po, d)
                        kb.TR(ps_k[:], kr[b][:, hp, :], kb.C("IDENT"), [kr[b].s(hp)], [ps_k])
                        for h2 in range(2):
                            h = 2 * hp + h2
                            kb.ACT(khp[b][h][:, 64 * h2:64 * h2 + 64], ps_k[:, 64 * h2:64 * h2 + 64], AF.Copy,
                                   [ps_k, KD], [khp[b][h]], scale=KD[:, 4 * d + h:4 * d + h + 1])
                        kb.MM(ps_kv[:], khp[b][2 * hp][:], Vp[b][2 * hp][:], True, False,
                              [khp[b][2 * hp], Vp[b][2 * hp]], [ps_kv])
                        kb.MM(ps_kv[:], khp[b][2 * hp + 1][:], Vp[b][2 * hp + 1][:], False, True,
                              [khp[b][2 * hp + 1], Vp[b][2 * hp + 1]], [ps_kv])
                        kb.STT(Sb[hp][:], Sb[hp][:], CV[:, 2 * d + hp:2 * d + hp + 1], ps_kv[:], ALU.mult, ALU.add,
                               [Sb[hp], CV, ps_kv], [Sb[hp]])
        with P.scope():
            finalize_gated(kb, OACC, Z_RG, None, 256, "r_")


def mixer_hgrn(kb, l):
    P = kb.P
    with P.scope():
        OACC = P.sbuf("h_oacc", [128, 2, S])
        with P.scope():
            LB = P.sbuf("h_LB", [128, 4]); OML = P.sbuf("h_OML", [128, 4])
            if l == 0:
                kb.MS(LB[:], 0.0, [LB]); kb.MS(OML[:], 1.0, [OML])
            else:
                lgt = P.sbuf("h_lgt", [128, 8])
                kb.LD(lgt[:], kb.prm["hgrn_lb_logits"][:].rearrange("l d (hp p) -> p (l d hp)", p=128), [lgt],
                      allow_slow_non_contiguous=True)
                kb.TT(LB[:], lgt[:, 4:8], lgt[:, 0:4], ALU.subtract, [lgt], [LB])
                kb.ACT(LB[:], LB[:], AF.Sigmoid, [LB], [LB])
                kb.TS(OML[:], LB[:], -1.0, 1.0, ALU.mult, ALU.add, [LB], [OML])
            G = P.sbuf("h_G", [128, 1])
            for hh in range(2):
                kb.LD(G[64 * hh:64 * hh + 64, :], kb.prm["hgrn_norm_g"][l].rearrange("(p o) -> p o", o=1), [G])
            kb.hgrn_gain = G
            hqb = [P.sbuf("h_q%d" % i, [128, 2, 128]) for i in range(2)]
            hfb = [P.sbuf("h_f%d" % i, [128, 2, 128]) for i in range(2)]
            Vp = [[P.sbuf("h_vp%d%d" % (i, h), [128, 128]) for h in range(4)] for i in range(2)]
            khp = [[P.sbuf("h_kh%d%d" % (i, h), [128, 128]) for h in range(4)] for i in range(2)]
            for i in range(2):
                for h in range(4):
                    kb.MS(Vp[i][h][:], 0.0, [Vp[i][h]], eng="pool")
                    kb.MS(khp[i][h][:], 0.0, [khp[i][h]], eng="pool")
            MREF = [[P.sbuf("h_mr%d%d" % (d, i), [128, 4]) for i in range(2)] for d in range(2)]
            for d in range(2):
                for i in range(2):
                    kb.MS(MREF[d][i][:], 0.0, [MREF[d][i]])

            def two(name, shape=(128, 128)):
                return [P.sbuf("h_%s%d" % (name, i), list(shape)) for i in range(2)]
            qs, sgm, ff, logf, kk, bb, pre = two("qs"), two("sg"), two("ff"), two("lf"), two("kk"), two("bb"), two("pre")
            e1, Ql, e2, Qd = two("e1"), two("Ql"), two("e2"), two("Qd")
            Kt = [two("Kt%d" % r) for r in range(4)]
            ex = two("ex")
            AT = two("AT", (128, 2, 128))
            KhT = two("KhT")
            bend = two("bend", (128, 2))
            Sb = [P.sbuf("h_S%d" % hp, [128, 128]) for hp in range(2)]
            ps_s = [P.psum("h_pss%d" % i, [128, 2, 128]) for i in range(2)]
            ps_o = [P.psum("h_pso%d" % i, [128, 128]) for i in range(2)]
            ps_k = [P.psum("h_psk%d" % i, [128, 128]) for i in range(2)]
            ps_kv = [P.psum("h_pskv%d" % i, [128, 128]) for i in range(2)]
            it = 0
            jj = 0
            for d in range(2):
                zf = Z_HFF if d == 0 else Z_HFB
                for hp in range(2):
                    kb.MS(Sb[hp][:], 0.0, [Sb[hp]])
                for n in ORDER[d]:
                    cols = slice(n * 128, (n + 1) * 128)
                    b = it % 2; it += 1
                    hq, hf = hqb[b], hfb[b]
                    kb.LD(hq[:], kb.ZF[Z_HQ:Z_HQ + 256, cols].rearrange("(hp p) t -> p hp t", p=128), [hq])
                    kb.LD(hf[:], kb.ZF[zf:zf + 256, cols].rearrange("(hp p) t -> p hp t", p=128), [hf])
                    for h in range(4):
                        kb.LD(Vp[b][h][:, 64 * (h % 2):64 * (h % 2) + 64], kb.ZT[cols, 64 * h:64 * h + 64], [Vp[b][h]])
                    for hp in range(2):
                        j = jj % 2; jj += 1
                        c = 2 * d + hp
                        mref = MREF[d][j]
                        kb.ACT(qs[j][:], hq[:, hp, :], AF.Silu, [hq], [qs[j]])
                        kb.ACT(sgm[j][:], hf[:, hp, :], AF.Sigmoid, [hf], [sgm[j]])
                        kb.TS(ff[j][:], sgm[j][:], OML[:, c:c + 1], LB[:, c:c + 1], ALU.mult, ALU.add, [sgm[j], OML, LB], [ff[j]])
                        kb.ACT(logf[j][:], ff[j][:], AF.Ln, [ff[j]], [logf[j]])
                        kb.TS(kk[j][:], ff[j][:], -1.0, 1.0, ALU.mult, ALU.add, [ff[j]], [kk[j]], eng="pool")
                        B = bb[j]
                        if d == 0:
                            kb.SCAN(B[:], kb.C("ONES"), logf[j][:], [logf[j]], [B])
                            kb.CP(mref[:, 1:4], B[:].rearrange("p (r c) -> p r c", c=32)[:, 0:3, 31], [B], [mref])
                            be = B[:, 127:128]
                        else:
                            kb.SCAN(pre[j][:], kb.C("ONES"), logf[j][:], [logf[j]], [pre[j]])
                            kb.STT(B[:], pre[j][:], -1.0, logf[j][:], ALU.mult, ALU.add, [pre[j], logf[j]], [B])
                            kb.TS(B[:], B[:], pre[j][:, 127:128], None, ALU.add, None, [B, pre[j]], [B])
                            kb.CP(mref[:, 0:3], B[:].rearrange("p (r c) -> p r c", c=32)[:, 1:4, 0], [B], [mref])
                            be = B[:, 0:1]
                        kb.TT(e1[j][:].rearrange("p (r c) -> p r c", c=32), B[:].rearrange("p (r c) -> p r c", c=32),
                              mref[:].unsqueeze(2).to_broadcast([128, 4, 32]), ALU.subtract, [B, mref], [e1[j]])
                        kb.ACT(e1[j][:], e1[j][:], AF.Exp, [e1[j]], [e1[j]])
                        kb.STT(Ql[j][:], qs[j][:], 0.125, e1[j][:], ALU.mult, ALU.mult, [qs[j], e1[j]], [Ql[j]], eng="pool")
                        kb.ACT(e2[j][:], B[:], AF.Exp, [B], [e2[j]])
                        kb.STT(Qd[j][:], qs[j][:], 0.125, e2[j][:], ALU.mult, ALU.mult, [qs[j], e2[j]], [Qd[j]], eng="pool")
                        pss = ps_s[j]
                        for r in range(4):
                            kb.ACT(ex[j][:], B[:], AF.Exp, [B, mref], [ex[j]], scale=-1.0, bias=mref[:, r:r + 1])
                            kb.STT(Kt[r][j][:], ex[j][:], 1e26, kk[j][:], ALU.min, ALU.mult, [ex[j], kk[j]], [Kt[r][j]])
                            for h2 in range(2):
                                kb.MM(pss[:, h2, 32 * r:32 * r + 32], Kt[r][j][64 * h2:64 * h2 + 64, :],
                                      Ql[j][64 * h2:64 * h2 + 64, 32 * r:32 * r + 32], True, True,
                                      [Kt[r][j], Ql[j]], [pss])
                        kb.TT(AT[j][:], pss[:], kb.C("TRIF" if d == 0 else "TRIB").unsqueeze(1).to_broadcast([128, 2, 128]),
                              ALU.mult, [pss], [AT[j]])
                        po = ps_o[j]
                        kb.MM(po[:], Vp[b][2 * hp][:], AT[j][:, 0, :], True, False, [Vp[b][2 * hp], AT[j]], [po])
                        kb.MM(po[:], Vp[b][2 * hp + 1][:], AT[j][:, 1, :], False, False, [Vp[b][2 * hp + 1], AT[j]], [po])
                        kb.MM(po[:], Sb[hp][:], Qd[j][:], False, True, [Sb[hp], Qd[j]], [po])
                        oacc_write(kb, OACC, hp, n, po, d)
                        kb.CP(bend[j][:, 0:1], be, [B], [bend[j]])
                        kb.ACT(KhT[j][:], B[:], AF.Exp, [B, bend[j]], [KhT[j]], scale=-1.0, bias=bend[j][:, 0:1])
                        kb.TT(KhT[j][:], KhT[j][:], kk[j][:], ALU.mult, [KhT[j], kk[j]], [KhT[j]], eng="pool")
                        kb.ACT(bend[j][:, 1:2], bend[j][:, 0:1], AF.Exp, [bend[j]], [bend[j]])
                        pk = ps_k[j]
                        kb.TR(pk[:], KhT[j][:], kb.C("IDENT"), [KhT[j]], [pk])
                        for h2 in range(2):
                            h = 2 * hp + h2
                            kb.CP(khp[b][h][:, 64 * h2:64 * h2 + 64], pk[:, 64 * h2:64 * h2 + 64], [pk], [khp[b][h]],
                                  eng=("act" if h2 else "dve"))
                        pkv = ps_kv[j]
                        kb.MM(pkv[:], khp[b][2 * hp][:], Vp[b][2 * hp][:], True, False, [khp[b][2 * hp], Vp[b][2 * hp]], [pkv])
                        kb.MM(pkv[:], khp[b][2 * hp + 1][:], Vp[b][2 * hp + 1][:], False, True,
                              [khp[b][2 * hp + 1], Vp[b][2 * hp + 1]], [pkv])
                        kb.STT(Sb[hp][:], Sb[hp][:], bend[j][:, 1:2], pkv[:], ALU.mult, ALU.add,
                               [Sb[hp], bend[j], pkv], [Sb[hp]])
        with P.scope():
            G = P.sbuf("h_G2", [128, 1])
            for hh in range(2):
                kb.LD(G[64 * hh:64 * hh + 64, :], kb.prm["hgrn_norm_g"][l].rearrange("(p o) -> p o", o=1), [G])
            finalize_gated(kb, OACC, Z_HG, G, 0, "h_")


PI = float(np.pi)


def _sincos(kb, ang, sin_out, cos_out, R, tmp, shape=None):
    P = kb.P
    shp = list(ang.shape)
    with P.scope():
        ki = P.sbuf("sc_ki", shp, mybir.dt.int32)
        kf = P.sbuf("sc_kf", shp)
        r = P.sbuf("sc_r", shp)
        m = P.sbuf("sc_m", shp)
        C1 = 6.28125
        C2 = 2 * PI - C1
        for (shift, out) in ((0.0, sin_out), (PI / 2, cos_out)):
            kb.TS(r[:], ang, shift, None, ALU.add, None, R, [r])
            kb.TS(kf[:], r[:], 1.0 / (2 * PI), None, ALU.mult, None, [r], [kf])
            kb.CP(ki[:], kf[:], [kf], [ki])
            kb.CP(kf[:], ki[:], [ki], [kf])
            kb.STT(r[:], kf[:], -C1, r[:], ALU.mult, ALU.add, [kf, r], [r])
            kb.STT(r[:], kf[:], -C2, r[:], ALU.mult, ALU.add, [kf, r], [r])
            kb.TS(m[:], r[:], PI, 2 * PI, ALU.is_gt, ALU.mult, [r], [m])
            kb.TT(r[:], r[:], m[:], ALU.subtract, [r, m], [r])
            kb.TS(m[:], r[:], -PI, 2 * PI, ALU.is_lt, ALU.mult, [r], [m])
            kb.TT(r[:], r[:], m[:], ALU.add, [r, m], [r])
            kb.ACT(out, r[:], AF.Sin, [r], R)


def mixer_s5(kb, l):
    P = kb.P
    prm = kb.prm
    with P.scope():
        OACC = P.sbuf("s_oacc", [128, 2, S])
        with P.scope():
            WX = P.sbuf("s_WX", [128, 2, 8, 2, 64])
            Cblk = P.sbuf("s_Cblk", [128, 16, 128])
            kb.MS(WX[:], 0.0, [WX], eng="pool")
            kb.MS(Cblk[:], 0.0, [Cblk], eng="pool")
            for g8 in range(8):
                for ri, nm in enumerate(("s5_b_re", "s5_b_im")):
                    for gg in range(2):
                        src = prm[nm][l][8 * gg + g8].rearrange("p c -> c p")
                        kb.LD(WX[16 * g8:16 * g8 + 16, gg, g8, ri, :], src, [WX], allow_slow_non_contiguous=True)
            for g in range(16):
                g8 = g % 8
                kb.LD(Cblk[0:64, g, 16 * g8:16 * g8 + 16], prm["s5_c_re"][l][g].rearrange("c p -> p c"), [Cblk],
                      allow_slow_non_contiguous=True)
                kb.LD(Cblk[64:128, g, 16 * g8:16 * g8 + 16], prm["s5_c_im"][l][g].rearrange("c p -> p c"), [Cblk],
                      allow_slow_non_contiguous=True)
            kb.TS(Cblk[64:128, :, :], Cblk[64:128, :, :], -1.0, None, ALU.mult, None, [Cblk], [Cblk])
            Cb16 = P.sbuf("s_Cb16", [128, 16, 128], BF16)
            kb.CP(Cb16[:], Cblk[:], [Cblk], [Cb16])
            VFr = P.sbuf("s_VFr", [128, 16, 64]); VFi = P.sbuf("s_VFi", [128, 16, 64])
            T1 = P.sbuf("s_T1", [128, 16, 128]); T2 = P.sbuf("s_T2", [128, 16, 128])
            AR = P.sbuf("s_AR", [128, 16]); NAI = P.sbuf("s_NAI", [128, 16])
            for d in range(2):
                with P.scope():
                    lr = P.sbuf("s_lr", [128, 16, 64]); li = P.sbuf("s_li", [128, 16, 64]); dtb = P.sbuf("s_dt", [128, 16])
                    kb.LD(lr[:], prm["s5_lam_re"][l][d].rearrange("g p -> (g p)").partition_broadcast(128), [lr])
                    kb.LD(li[:], prm["s5_lam_im"][l][d].rearrange("g p -> (g p)").partition_broadcast(128), [li])
                    kb.LD(dtb[:], prm["s5_log_dt"][l][d].partition_broadcast(128), [dtb])
                    kb.ACT(dtb[:], dtb[:], AF.Exp, [dtb], [dtb])
                    dt_bc = dtb[:].unsqueeze(2).to_broadcast([128, 16, 64])
                    lrdt = P.sbuf("s_lrdt", [128, 16, 64]); lidt = P.sbuf("s_lidt", [128, 16, 64])
                    kb.TT(lrdt[:], lr[:], dt_bc, ALU.mult, [lr, dtb], [lrdt])
                    kb.TT(lidt[:], li[:], dt_bc, ALU.mult, [li, dtb], [lidt])
                    a = [P.sbuf("s_a%d" % i, [128, 16, 64]) for i in range(8)]
                    mag, ang, sn, cs, tmp, ar, ai, t2 = a
                    kb.ACT(mag[:], lrdt[:], AF.Exp, [lrdt], [mag])
                    _sincos(kb, lidt[:], sn[:], cs[:], [lidt, sn, cs, tmp], tmp[:])
                    kb.TT(ar[:], mag[:], cs[:], ALU.mult, [mag, cs], [ar])
                    kb.TT(ai[:], mag[:], sn[:], ALU.mult, [mag, sn], [ai])
                    den = P.sbuf("s_den", [128, 16, 64]); fr = P.sbuf("s_fr", [128, 16, 64]); fi = P.sbuf("s_fi", [128, 16, 64])
                    kb.TT(den[:], lr[:], lr[:], ALU.mult, [lr], [den])
                    kb.TT(t2[:], li[:], li[:], ALU.mult, [li], [t2])
                    kb.TT(den[:], den[:], t2[:], ALU.add, [den, t2], [den])
                    kb.RECIP(den[:], den[:], [den], [den])
                    kb.TS(ar[:], ar[:], -1.0, None, ALU.add, None, [ar], [ar])
                    kb.TT(fr[:], ar[:], lr[:], ALU.mult, [ar, lr], [fr])
                    kb.TT(t2[:], ai[:], li[:], ALU.mult, [ai, li], [t2])
                    kb.TT(fr[:], fr[:], t2[:], ALU.add, [fr, t2], [fr])
                    kb.TT(fr[:], fr[:], den[:], ALU.mult, [fr, den], [fr])
                    kb.TT(fi[:], ai[:], lr[:], ALU.mult, [ai, lr], [fi])
                    kb.TT(t2[:], ar[:], li[:], ALU.mult, [ar, li], [t2])
                    kb.TT(fi[:], fi[:], t2[:], ALU.subtract, [fi, t2], [fi])
                    kb.TT(fi[:], fi[:], den[:], ALU.mult, [fi, den], [fi])
                    jcol = kb.C("CCOL")[:, 2:3] if d == 0 else kb.C("CCOL")[:, 3:4]
                    njcol = kb.C("CCOL")[:, 6:7] if d == 0 else kb.C("CCOL")[:, 7:8]
                    kb.ACT(mag[:], lrdt[:], AF.Exp, [lrdt], [mag], scale=njcol)
                    kb.TS(ang[:], lidt[:], jcol, None, ALU.mult, None, [lidt], [ang])
                    _sincos(kb, ang[:], sn[:], cs[:], [ang, sn, cs, tmp], tmp[:])
                    vr, vi = ar, ai
                    kb.TT(vr[:], mag[:], cs[:], ALU.mult, [mag, cs], [vr])
                    kb.TT(vi[:], mag[:], sn[:], ALU.mult, [mag, sn], [vi])
                    kb.TS(vi[:], vi[:], -1.0, None, ALU.mult, None, [vi], [vi])
                    kb.TT(VFr[:], vr[:], fr[:], ALU.mult, [vr, fr], [VFr])
                    kb.TT(t2[:], vi[:], fi[:], ALU.mult, [vi, fi], [t2])
                    kb.TT(VFr[:], VFr[:], t2[:], ALU.subtract, [VFr, t2], [VFr])
                    kb.TT(VFi[:], vr[:], fi[:], ALU.mult, [vr, fi], [VFi])
                    kb.TT(t2[:], vi[:], fr[:], ALU.mult, [vi, fr], [t2])
                    kb.TT(VFi[:], VFi[:], t2[:], ALU.add, [VFi, t2], [VFi])
                with P.scope():
                    dtb = P.sbuf("s_dt2", [128, 16])
                    kb.LD(dtb[:], prm["s5_log_dt"][l][d].partition_broadcast(128), [dtb])
                    kb.ACT(dtb[:], dtb[:], AF.Exp, [dtb], [dtb])
                    lrp = P.sbuf("s_lrp", [128, 16]); lip = P.sbuf("s_lip", [128, 16])
                    for hh in range(2):
                        kb.LD(lrp[64 * hh:64 * hh + 64, :], prm["s5_lam_re"][l][d].rearrange("g p -> p g"), [lrp],
                              allow_slow_non_contiguous=True)
                        kb.LD(lip[64 * hh:64 * hh + 64, :], prm["s5_lam_im"][l][d].rearrange("g p -> p g"), [lip],
                              allow_slow_non_contiguous=True)
                    kb.TT(lrp[:], lrp[:], dtb[:], ALU.mult, [lrp, dtb], [lrp])
                    kb.TT(lip[:], lip[:], dtb[:], ALU.mult, [lip, dtb], [lip])
                    b4 = [P.sbuf("s_b%d" % i, [128, 16, 128]) for i in range(4)]
                    arg, sn2, cs2, tmp2 = b4
                    mt = kb.C("IOTAF" if d == 0 else "R127F")
                    mt_bc = mt.unsqueeze(1).to_broadcast([128, 16, 128])
                    kb.TT(arg[:], lrp[:].unsqueeze(2).to_broadcast([128, 16, 128]), mt_bc, ALU.mult, [lrp], [arg])
                    kb.ACT(T1[:], arg[:], AF.Exp, [arg], [T1])
                    kb.TT(arg[:], lip[:].unsqueeze(2).to_broadcast([128, 16, 128]), mt_bc, ALU.mult, [lip, T1], [arg])
                    _sincos(kb, arg[:], sn2[:], cs2[:], [arg, sn2, cs2, tmp2], tmp2[:])
                    kb.TT(T2[:], T1[:], sn2[:], ALU.mult, [T1, sn2], [T2])
                    kb.TS(T2[:], T2[:], -1.0, None, ALU.mult, None, [T2], [T2])
                    kb.TT(T1[:], T1[:], cs2[:], ALU.mult, [T1, cs2], [T1])
                    c4 = [P.sbuf("s_c%d" % i, [128, 16]) for i in range(4)]
                    kb.ACT(c4[0][:], lrp[:], AF.Exp, [lrp], [c4[0]])
                    _sincos(kb, lip[:], c4[1][:], c4[2][:], [lip, c4[1], c4[2], c4[3]], c4[3][:])
                    kb.TT(AR[:], c4[0][:], c4[2][:], ALU.mult, [c4[0], c4[2]], [AR])
                    kb.TT(NAI[:], c4[0][:], c4[1][:], ALU.mult, [c4[0], c4[1]], [NAI])
                    kb.TS(NAI[:], NAI[:], -1.0, None, ALU.mult, None, [NAI], [NAI])
                sweep_scope = P.scope(); sweep_scope.__enter__()
                uTb = [P.sbuf("s_u%d" % i, [128, 2, 128]) for i in range(2)]
                mm_ = [P.sbuf("s_m%d" % i, [128, 8, 64]) for i in range(4)]
                W3 = [P.sbuf("s_W3%d" % i, [128, 8, 3, 64], BF16) for i in range(2)]
                Hb = [P.sbuf("s_Hb%d" % i, [128, 8, 128], BF16) for i in range(2)]
                tri16 = P.sbuf("s_tri16", [128, 128], BF16)
                kb.CP(tri16[:], kb.C("TRIF" if d == 0 else "TRIB"), [], [tri16])
                tP = [P.sbuf("s_tP%d" % i, [128, 8, 128]) for i in range(2)]
                tPs = [P.sbuf("s_tPs%d" % i, [128, 8, 128]) for i in range(2)]
                H1 = [P.sbuf("s_H1%d" % i, [128, 8, 128]) for i in range(2)]
                H2 = [P.sbuf("s_H2%d" % i, [128, 8, 128]) for i in range(2)]
                hend = P.sbuf("s_hend", [128, 16]); hsend = P.sbuf("s_hsend", [128, 16])
                hp_ = P.sbuf("s_hp", [128, 16]); hps_ = P.sbuf("s_hps", [128, 16])
                sm = [P.sbuf("s_sm%d" % i, [128, 16]) for i in range(4)]
                xps = P.psum("s_xps", [128, 1024])
                pps = P.psum("s_pps", [128, 8, 128])
                ppss = P.psum("s_ppss", [128, 8, 128])
                yps = [P.psum("s_yps%d" % i, [128, 128]) for i in range(2)]
                kb.MS(hp_[:], 0.0, [hp_]); kb.MS(hps_[:], 0.0, [hps_])
                te = 127 if d == 0 else 0
                tri = kb.C("TRIF" if d == 0 else "TRIB")
                it = 0
                for n in ORDER[d]:
                    cols = slice(n * 128, (n + 1) * 128)
                    uT = uTb[it % 2]; it += 1
                    kb.LD(uT[:], kb.ZF[Z_SU:Z_SU + 256, cols].rearrange("(gg p) t -> p gg t", p=128), [uT])
                    for gg in range(2):
                        j = gg
                        for half in range(2):
                            kb.MM(xps[:, half * 512:(half + 1) * 512], uT[:, gg, :],
                                  WX[:, gg, half * 4:(half + 1) * 4, :, :].rearrange("q a r p -> q (a r p)"),
                                  True, True, [uT, WX], [xps])
                        xv = xps[:].rearrange("t (g r p) -> t g r p", r=2, p=64)
                        gs = slice(gg * 8, gg * 8 + 8)
                        kb.TT(mm_[0][:], xv[:, :, 0, :], VFr[:, gs, :], ALU.mult, [xps, VFr], [mm_[0]])
                        kb.TT(mm_[1][:], xv[:, :, 1, :], VFi[:, gs, :], ALU.mult, [xps, VFi], [mm_[1]])
                        kb.TT(mm_[2][:], xv[:, :, 0, :], VFi[:, gs, :], ALU.mult, [xps, VFi], [mm_[2]])
                        kb.TT(mm_[3][:], xv[:, :, 1, :], VFr[:, gs, :], ALU.mult, [xps, VFr], [mm_[3]])
                        w3 = W3[j]
                        kb.TT(w3[:, :, 0, :], mm_[0][:], mm_[1][:], ALU.subtract, [mm_[0], mm_[1]], [w3], eng="pool")
                        kb.TT(w3[:, :, 1, :], mm_[2][:], mm_[3][:], ALU.add, [mm_[2], mm_[3]], [w3], eng="pool")
                        kb.TT(w3[:, :, 2, :], mm_[1][:], mm_[0][:], ALU.subtract, [mm_[0], mm_[1]], [w3], eng="pool")
                        for g8 in range(8):
                            kb.MM(pps[:, g8, :], w3[:, g8, 0:2, :].rearrange("q r p -> q (r p)"), tri16[:], True, True, [w3, tri16], [pps])
                            kb.MM(ppss[:, g8, :], w3[:, g8, 1:3, :].rearrange("q r p -> q (r p)"), tri16[:], True, True, [w3, tri16], [ppss])
                        kb.TT(tP[j][:], pps[:], hp_[:, gs].unsqueeze(2).to_broadcast([128, 8, 128]), ALU.add, [pps, hp_], [tP[j]])
                        kb.TT(tPs[j][:], ppss[:], hps_[:, gs].unsqueeze(2).to_broadcast([128, 8, 128]), ALU.add,
                              [ppss, hps_], [tPs[j]])
                        kb.TT(H1[j][:], tP[j][:], T1[:, gs, :], ALU.mult, [tP[j], T1], [H1[j]], eng="pool")
                        kb.TT(H2[j][:], tPs[j][:], T2[:, gs, :], ALU.mult, [tPs[j], T2], [H2[j]])
                        kb.TT(Hb[j][:], H1[j][:], H2[j][:], ALU.add, [H1[j], H2[j]], [Hb[j]], eng="pool")
                        yp = yps[gg]
                        for g8 in range(8):
                            kb.MM(yp[:], Cb16[:, gg * 8 + g8, :], Hb[j][:, g8, :], g8 == 0, g8 == 7, [Cb16, Hb[j]], [yp])
                        oacc_write(kb, OACC, gg, n, yp, d)
                        kb.TT(hend[:, gs], H1[j][:, :, te], H2[j][:, :, te], ALU.add, [H1[j], H2[j]], [hend])
                        kb.TT(sm[0][:, 0:8], tPs[j][:, :, te], T1[:, gs, te], ALU.mult, [tPs[j], T1], [sm[0]])
                        kb.TT(sm[1][:, 0:8], tP[j][:, :, te], T2[:, gs, te], ALU.mult, [tP[j], T2], [sm[1]])
                        kb.TT(hsend[:, gs], sm[0][:, 0:8], sm[1][:, 0:8], ALU.subtract, [sm[0], sm[1]], [hsend])
                    kb.TT(sm[0][:], hend[:], AR[:], ALU.mult, [hend, AR], [sm[0]])
                    kb.TT(sm[1][:], hsend[:], NAI[:], ALU.mult, [hsend, NAI], [sm[1]])
                    kb.TT(sm[2][:], hsend[:], AR[:], ALU.mult, [hsend, AR], [sm[2]])
                    kb.TT(sm[3][:], hend[:], NAI[:], ALU.mult, [hend, NAI], [sm[3]])
                    kb.TT(hp_[:], sm[0][:], sm[1][:], ALU.add, [sm[0], sm[1]], [hp_])
                    kb.TT(hps_[:], sm[2][:], sm[3][:], ALU.subtract, [sm[2], sm[3]], [hps_])
                sweep_scope.__exit__(None, None, None)
        with P.scope():
            dsk = P.sbuf("s_dsk", [128, 2]); glb = P.sbuf("s_glb", [128, 2])
            kb.LD(dsk[:], prm["s5_d"][l].rearrange("(gg p) -> p gg", p=128), [dsk], allow_slow_non_contiguous=True)
            kb.LD(glb[:], prm["s5_glu_b"][l].rearrange("(gg p) -> p gg", p=128), [glb], allow_slow_non_contiguous=True)
            gw = P.sbuf("s_gw", [128, 2, 256])
            kb.LD(gw[:], prm["s5_glu_w"][l].rearrange("(ct p) o -> p ct o", p=128), [gw])
            uTb = [P.sbuf("s_fu%d" % i, [128, 2, 128]) for i in range(2)]
            yy = [P.sbuf("s_yy%d" % i, [128, 2, 128]) for i in range(2)]
            x2 = [P.sbuf("s_x2%d" % i, [128, 2, 128]) for i in range(2)]
            th = [P.sbuf("s_th%d" % i, [128, 2, 128]) for i in range(2)]
            sgb = [P.sbuf("s_sg%d" % i, [128, 128]) for i in range(2)]
            ob = [P.sbuf("s_ob%d" % i, [128, 128]) for i in range(2)]
            psz = [P.psum("s_psz%d" % i, [128, 128]) for i in range(2)]
            k = 0
            for n in range(NT):
                cols = slice(n * 128, (n + 1) * 128)
                i = n % 2
                kb.LD(uTb[i][:], kb.ZF[Z_SU:Z_SU + 256, cols].rearrange("(gg p) t -> p gg t", p=128), [uTb[i]])
                for gg in range(2):
                    kb.STT(yy[i][:, gg, :], uTb[i][:, gg, :], dsk[:, gg:gg + 1], OACC[:, gg, cols], ALU.mult, ALU.add,
                           [uTb[i], dsk, OACC.s(n)], [yy[i]])
                kb.TT(x2[i][:], yy[i][:], yy[i][:], ALU.mult, [yy[i]], [x2[i]], eng="pool")
                kb.TS(x2[i][:], x2[i][:], 0.044715, 1.0, ALU.mult, ALU.add, [x2[i]], [x2[i]])
                kb.TT(x2[i][:], x2[i][:], yy[i][:], ALU.mult, [x2[i], yy[i]], [x2[i]], eng="pool")
                kb.ACT(th[i][:], x2[i][:], AF.Tanh, [x2[i]], [th[i]], scale=0.7978845608028654)
                kb.TS(th[i][:], th[i][:], 1.0, 0.5, ALU.add, ALU.mult, [th[i]], [th[i]])
                kb.TT(yy[i][:], yy[i][:], th[i][:], ALU.mult, [yy[i], th[i]], [yy[i]], eng="pool")
                for ot in range(2):
                    q = k % 2; k += 1
                    for ct in range(2):
                        kb.MM(psz[q][:], gw[:, ct, ot * 128:(ot + 1) * 128], yy[i][:, ct, :], ct == 0, ct == 1, [gw, yy[i]], [psz[q]])
                    kb.ACT(sgb[q][:], psz[q][:], AF.Sigmoid, [psz[q], glb], [sgb[q]], bias=glb[:, ot:ot + 1])
                    kb.TT(ob[q][:], yy[i][:, ot, :], sgb[q][:], ALU.mult, [yy[i], sgb[q]], [ob[q]])
                    kb.ST(kb.YC[768 + ot * 128:768 + (ot + 1) * 128, cols], ob[q][:], [ob[q]])


def gdn_conv(kb, l):
    P = kb.P
    with P.scope():
        CW = P.sbuf("g_cw", [128, 6, 9])
        for kh in range(3):
            for kw in range(3):
                kb.LD(CW[:, :, kh * 3 + kw], kb.prm["gdn_conv_w"][l][kh, kw].rearrange("(ct p) -> p ct", p=128), [CW],
                      allow_slow_non_contiguous=True)
        mlat = P.sbuf("g_mlat", [128, 2, 512]); mctx = P.sbuf("g_mctx", [128, 2, 256])
        kb.LD(mlat[:], kb.cmlat[:], [mlat]); kb.LD(mctx[:], kb.cmctx[:], [mctx])
        Wb = [P.sbuf("g_w%d" % i, [128, 642]) for i in range(2)]
        acc = [[P.sbuf("g_acc%d%d" % (i, j), [128, 512]) for j in range(3)] for i in range(2)]
        sl = [P.sbuf("g_sl%d" % i, [128, 512]) for i in range(2)]
        sq = [P.sbuf("g_sq%d" % i, [128, 512]) for i in range(2)]
        rt = [P.sbuf("g_rt%d" % i, [128, 512]) for i in range(2)]
        ps = [P.psum("g_psn%d" % i, [128, 512]) for i in range(2)]
        spans = [(0, 256, True)] + [(256 + 512 * k, 512, False) for k in range(8)]
        it = 0
        for (t0, L, is_ctx) in spans:
            lo = 0 if is_ctx else 256
            hi = 256 if is_ctx else S
            a = max(lo, t0 - 65); b = min(hi, t0 + L + 65)
            for ct in range(6):
                i = it % 2; it += 1
                W = Wb[i]
                kb.MS(W[:], 0.0, [W], eng="pool")
                kb.LD(W[:, 65 + (a - t0):65 + (b - t0)], kb.ZF[Z_GQKV + ct * 128:Z_GQKV + (ct + 1) * 128, a:b], [W])
                rows = (1,) if is_ctx else (0, 1, 2)
                masks = mctx if is_ctx else mlat
                for dwi, shift in enumerate((-1, 0, 1)):
                    A = acc[i][dwi]
                    eng = "dve"
                    for q, dh in enumerate(rows):
                        o0 = 65 + 64 * (dh - 1) + shift
                        src = W[:, o0:o0 + L]
                        wcol = CW[:, ct, dh * 3 + dwi:dh * 3 + dwi + 1]
                        if q == 0:
                            kb.TS(A[:, :L], src, wcol, None, ALU.mult, None, [W, CW], [A], eng=("pool" if dwi != 1 else "dve"))
                        else:
                            kb.STT(A[:, :L], src, wcol, A[:, :L], ALU.mult, ALU.add, [W, CW, A], [A])
                    if dwi != 1:
                        mi = 0 if dwi == 0 else 1
                        kb.TT(A[:, :L], A[:, :L], masks[:, mi, :L], ALU.mult, [A, masks], [A], eng="pool")
                A0, A1, A2 = acc[i]
                kb.TT(A1[:, :L], A1[:, :L], A0[:, :L], ALU.add, [A0, A1], [A1], eng="pool")
                kb.TT(A1[:, :L], A1[:, :L], A2[:, :L], ALU.add, [A1, A2], [A1], eng="pool")
                kb.ACT(sl[i][:, :L], A1[:, :L], AF.Silu, [A1], [sl[i]])
                if ct < 4:
                    kb.TT(sq[i][:, :L], sl[i][:, :L], sl[i][:, :L], ALU.mult, [sl[i]], [sq[i]], eng="pool")
                    kb.MM(ps[i][:, :L], kb.C("BLK64"), sq[i][:, :L], True, True, [sq[i]], [ps[i]])
                    kb.ACT(rt[i][:, :L], ps[i][:, :L], AF.Sqrt, [ps[i]], [rt[i]], bias=kb.C("CCOL")[:, 0:1])
                    kb.RECIP(rt[i][:, :L], rt[i][:, :L], [rt[i]], [rt[i]])
                    if ct < 2:
                        kb.STT(sl[i][:, :L], sl[i][:, :L], 0.125, rt[i][:, :L], ALU.mult, ALU.mult, [sl[i], rt[i]], [sl[i]])
                    else:
                        kb.TT(sl[i][:, :L], sl[i][:, :L], rt[i][:, :L], ALU.mult, [sl[i], rt[i]], [sl[i]])
                kb.ST(kb.QKVF[ct * 128:(ct + 1) * 128, t0:t0 + L], sl[i][:, :L], [sl[i]])


def mixer_gdn(kb, l):
    P = kb.P
    gdn_conv(kb, l)
    upto = kb.cfg.get("gdn_upto", 99)
    if upto < 1:
        return
    with P.scope():
        OACC = P.sbuf("g_oacc", [128, 2, S])
        with P.scope():
            DTB = P.sbuf("g_dtb", [128, 8]); NEGA = P.sbuf("g_nega", [128, 8])
            kb.LD(DTB[:], kb.prm["gdn_dt_bias"][l].rearrange("d h -> (d h)").partition_broadcast(128), [DTB])
            kb.LD(NEGA[:], kb.prm["gdn_a_log"][l].rearrange("d h -> (d h)").partition_broadcast(128), [NEGA])
            kb.ACT(NEGA[:], NEGA[:], AF.Exp, [NEGA], [NEGA])
            kb.TS(NEGA[:], NEGA[:], -1.0, None, ALU.mult, None, [NEGA], [NEGA])
            qnb = [P.sbuf("g_q%d" % i, [128, 2, 128]) for i in range(2)]
            knb = [P.sbuf("g_k%d" % i, [128, 2, 128]) for i in range(2)]
            vvb = [P.sbuf("g_v%d" % i, [128, 2, 128]) for i in range(2)]
            gabb = [P.sbuf("g_gab%d" % i, [128, 16]) for i in range(2)]

            def sm4(name, w=4):
                return P.sbuf("g_" + name, [128, w])
            xa, ea, loga, beta, lnb = sm4("xa"), sm4("ea"), sm4("loga"), sm4("beta"), sm4("lnb")
            gtm, ngt, ekr, cdec, eg, beg, gpl = sm4("gtm"), sm4("ngt"), sm4("ekr"), sm4("cdec"), sm4("eg"), sm4("beg"), sm4("gpl")
            ROWS = P.sbuf("g_rows", [4, 384])
            LI = P.sbuf("g_LI", [128, 4, 128]); LBT = P.sbuf("g_LBT", [128, 4, 128]); LBm = P.sbuf("g_LB", [128, 4, 128])
            NAT = P.sbuf("g_NAT", [128, 4, 128]); NA = P.sbuf("g_NA", [128, 4, 128]); QKm = P.sbuf("g_QKm", [128, 4, 128])
            Tm = P.sbuf("g_Tm", [128, 4, 128]); Wm = P.sbuf("g_Wm", [128, 4, 128])
            x1 = P.sbuf("g_x1", [128, 4, 128]); y1 = P.sbuf("g_y1", [128, 4, 128])
            tmx = P.sbuf("g_tmx", [128, 4, 128]); tmy = P.sbuf("g_tmy", [128, 4, 128])
            Rm = [P.sbuf("g_R%d" % h, [128, 128]) for h in range(4)]
            khp = [P.sbuf("g_kh%d" % h, [128, 128]) for h in range(4)]
            vnp = [P.sbuf("g_vn%d" % h, [128, 128]) for h in range(4)]
            for h in range(4):
                kb.MS(khp[h][:], 0.0, [khp[h]], eng="pool")
                kb.MS(vnp[h][:], 0.0, [vnp[h]], eng="pool")
            upair = [P.sbuf("g_up%d" % hp, [128, 128]) for hp in range(2)]
            wTp = [P.sbuf("g_wT%d" % hp, [128, 128]) for hp in range(2)]
            EG = [P.sbuf("g_EG%d" % hp, [128, 128]) for hp in range(2)]
            qd = [P.sbuf("g_qd%d" % hp, [128, 128]) for hp in range(2)]
            cdp = [P.sbuf("g_cdp%d" % hp, [128, 1]) for hp in range(2)]
            Sb = [P.sbuf("g_S%d" % hp, [128, 128]) for hp in range(2)]
            B = [P.psum("g_B%d" % i, [128, 512]) for i in range(8)]
            ident = kb.C("IDENT")
            it = 0
            for d in range(2):
                tri = kb.C("TRIF" if d == 0 else "TRIB")
                rem = kb.C("SUFF" if d == 0 else "PREB")
                n_incl = kb.C("NLE" if d == 0 else "NGE")
                n_strT = kb.C("NLT" if d == 0 else "NGT")
                n_str = kb.C("NGT" if d == 0 else "NLT")
                for hp in range(2):
                    kb.MS(Sb[hp][:], 0.0, [Sb[hp]])
                for n in ORDER[d][:kb.cfg.get("ntiles", NT)]:
                    cols = slice(n * 128, (n + 1) * 128)
                    b = it % 2; it += 1
                    qn, kn, vv, gab = qnb[b], knb[b], vvb[b], gabb[b]
                    kb.LD(qn[:], kb.QKVF[0:256, cols].rearrange("(hp p) t -> p hp t", p=128), [qn])
                    kb.LD(kn[:], kb.QKVF[256:512, cols].rearrange("(hp p) t -> p hp t", p=128), [kn])
                    kb.LD(vv[:], kb.QKVF[512:768, cols].rearrange("(hp p) t -> p hp t", p=128), [vv])
                    kb.LD(gab[:], kb.ZT[cols, 512:528], [gab])
                    kb.TT(xa[:], gab[:, 4 * d:4 * d + 4], DTB[:, 4 * d:4 * d + 4], ALU.add, [gab, DTB], [xa])
                    kb.ACT(ea[:], xa[:], AF.Exp, [xa], [ea])
                    kb.ACT(ea[:], ea[:], AF.Ln, [ea], [ea], bias=kb.C("CCOL")[:, 1:2])
                    kb.TT(loga[:], ea[:], NEGA[:, 4 * d:4 * d + 4], ALU.mult, [ea, NEGA], [loga])
                    kb.ACT(beta[:], gab[:, 8 + 4 * d:12 + 4 * d], AF.Sigmoid, [gab], [beta])
                    kb.ACT(lnb[:], beta[:], AF.Ln, [beta], [lnb])
                    kb.MM(B[0][:, 0:4], tri, loga[:], True, True, [loga], [B[0]])
                    kb.MM(B[0][:, 4:8], rem, loga[:], True, True, [loga], [B[0]])
                    kb.MM(B[0][:, 8:12], kb.C("ONES"), loga[:], True, True, [loga], [B[0]])
                    kb.CP(gtm[:], B[0][:, 0:4], [B[0]], [gtm])
                    kb.TS(ngt[:], B[0][:, 0:4], -1.0, None, ALU.mult, None, [B[0]], [ngt])
                    kb.ACT(ekr[:], B[0][:, 4:8], AF.Exp, [B[0]], [ekr])
                    kb.ACT(cdec[:], B[0][:, 8:12], AF.Exp, [B[0]], [cdec])
                    kb.ACT(eg[:], gtm[:], AF.Exp, [gtm], [eg])
                    kb.TT(beg[:], beta[:], eg[:], ALU.mult, [beta, eg], [beg])
                    kb.TT(gpl[:], gtm[:], lnb[:], ALU.add, [gtm, lnb], [gpl])
                    kb.MM(B[1][0:4, 0:128], loga[:], tri, True, True, [loga], [B[1]])
                    kb.MM(B[1][0:4, 128:256], loga[:], tri, True, False, [loga], [B[1]])
                    kb.MM(B[1][0:4, 128:256], lnb[:], ident, False, True, [lnb], [B[1]])
                    kb.CP(ROWS[:, 0:256], B[1][0:4, 0:256], [B[1]], [ROWS])
                    kb.TS(ROWS[:, 256:384], B[1][0:4, 0:128], -1.0, None, ALU.mult, None, [B[1]], [ROWS])
                    if upto < 2:
                        continue
                    for (dst, rsl, negm, bias_t, bank) in ((LI, slice(0, 128), n_incl, ngt, B[2]),
                                                           (LBT, slice(128, 256), n_strT, ngt, B[3]),
                                                           (LBm, slice(256, 384), n_str, gpl, B[2])):
                        for h in range(4):
                            kb.MM(bank[:, h * 128:(h + 1) * 128], kb.C("SELH%d" % h)[0:4, :], ROWS[:, rsl], True, False,
                                  [ROWS], [bank])
                            kb.MM(bank[:, h * 128:(h + 1) * 128], ident, negm, False, True, [], [bank])
                        for h in range(4):
                            kb.ACT(dst[:, h, :], bank[:, h * 128:(h + 1) * 128], AF.Exp, [bank, bias_t], [dst],
                                   bias=bias_t[:, h:h + 1])
                    if upto < 3:
                        continue
                    for h in range(4):
                        hp, h2 = divmod(h, 2)
                        ksl = kn[64 * h2:64 * h2 + 64, hp, :]
                        kb.MM(B[4][:, h * 128:(h + 1) * 128], ksl, ksl, True, True, [kn], [B[4]])
                        kb.MM(B[5][:, h * 128:(h + 1) * 128], ksl, qn[64 * h2:64 * h2 + 64, hp, :], True, True, [kn, qn], [B[5]])
                    b4v = B[4][:].rearrange("p (h t) -> p h t", h=4)
                    b5v = B[5][:].rearrange("p (h t) -> p h t", h=4)
                    kb.STT(NAT[:], b4v, -1.0, LBT[:], ALU.mult, ALU.mult, [B[4], LBT], [NAT])
                    kb.STT(NA[:], b4v, -1.0, LBm[:], ALU.mult, ALU.mult, [B[4], LBm], [NA])
                    kb.TT(QKm[:], b5v, LI[:], ALU.mult, [B[5], LI], [QKm])
                    if upto < 4:
                        continue
                    idb = ident.unsqueeze(1).to_broadcast([128, 4, 128])
                    kb.CP(Tm[:], idb, [], [Tm])
                    kb.CP(Wm[:], idb, [], [Wm], eng="pool")
                    for s_ in (1, 2, 4, 8, 16, 32, 64):
                        mT = kb.C(("MOFF%d" if d == 0 else "MOFFT%d") % s_).unsqueeze(1).to_broadcast([128, 4, 128])
                        mW = kb.C(("MOFFT%d" if d == 0 else "MOFF%d") % s_).unsqueeze(1).to_broadcast([128, 4, 128])
                        for h in range(4):
                            kb.MM(B[2][:, h * 128:(h + 1) * 128], NAT[:, h, :], Tm[:, h, :], True, True, [NAT, Tm], [B[2]])
                        for h in range(4):
                            kb.MM(B[3][:, h * 128:(h + 1) * 128], NA[:, h, :], Wm[:, h, :], True, True, [NA, Wm], [B[3]])
                        kb.CP(x1[:], B[2][:].rearrange("p (h t) -> p h t", h=4), [B[2]], [x1], eng="act")
                        kb.CP(y1[:], B[3][:].rearrange("p (h t) -> p h t", h=4), [B[3]], [y1], eng="dve")
                        for h in range(4):
                            kb.MM(B[4][:, h * 128:(h + 1) * 128], Wm[:, h, :], x1[:, h, :], True, True, [Wm, x1], [B[4]])
                        for h in range(4):
                            kb.MM(B[5][:, h * 128:(h + 1) * 128], Tm[:, h, :], y1[:, h, :], True, True, [Tm, y1], [B[5]])
                        kb.TT(tmx[:], B[4][:].rearrange("p (h t) -> p h t", h=4), mT, ALU.mult, [B[4]], [tmx])
                        kb.TT(tmy[:], B[5][:].rearrange("p (h t) -> p h t", h=4), mW, ALU.mult, [B[5]], [tmy])
                        kb.TT(Tm[:], Tm[:], tmx[:], ALU.add, [Tm, tmx], [Tm], eng="pool")
                        kb.TT(Wm[:], Wm[:], tmy[:], ALU.add, [Wm, tmy], [Wm], eng="pool")
                    if upto < 5:
                        continue
                    for hp in range(2):
                        kb.TR(B[0][:, 128:256], kn[:, hp, :], ident, [kn], [B[0]])
                        kb.TR(B[0][:, 256:384], vv[:, hp, :], ident, [vv], [B[0]])
                        for h2 in range(2):
                            h = 2 * hp + h2
                            kc = slice(64 * h2, 64 * h2 + 64)
                            vc = slice(64 * (1 - h2), 64 * (1 - h2) + 64)
                            kb.TS(Rm[h][:, kc], B[0][:, 128 + 64 * h2:128 + 64 * h2 + 64], beg[:, h:h + 1], None, ALU.mult, None,
                                  [B[0], beg], [Rm[h]])
                            kb.ACT(Rm[h][:, vc], B[0][:, 256 + 64 * h2:256 + 64 * h2 + 64], AF.Copy, [B[0], beta], [Rm[h]],
                                   scale=beta[:, h:h + 1])
                            kb.ACT(khp[h][:, kc], B[0][:, 128 + 64 * h2:128 + 64 * h2 + 64], AF.Copy, [B[0], ekr], [khp[h]],
                                   scale=ekr[:, h:h + 1])
                    if upto < 6:
                        continue
                    for h in range(4):
                        kb.MM(B[2][:, h * 128:(h + 1) * 128], Wm[:, h, :], Rm[h][:], True, True, [Wm, Rm[h]], [B[2]])
                        kb.MM(B[3][:, h * 128:(h + 1) * 128], Rm[h][:], Wm[:, h, :], True, True, [Wm, Rm[h]], [B[3]])
                    for h in range(4):
                        hp, h2 = divmod(h, 2)
                        vc0 = 64 * (1 - h2)
                        kb.CP(upair[hp][:, 64 * h2:64 * h2 + 64], B[2][:, h * 128 + vc0:h * 128 + vc0 + 64], [B[2]], [upair[hp]],
                              eng=("act" if h2 else "dve"))
                        kb.CP(wTp[hp][64 * h2:64 * h2 + 64, :], B[3][64 * h2:64 * h2 + 64, h * 128:(h + 1) * 128], [B[3]], [wTp[hp]],
                              eng=("dve" if h2 else "act"))
                    if upto < 7:
                        continue
                    for hp in range(2):
                        kb.MM(B[1][:, 256:384], kb.C("SELP%d" % hp)[0:4, :], ROWS[:, 0:128], True, True, [ROWS], [B[1]])
                        kb.ACT(EG[hp][:], B[1][:, 256:384], AF.Exp, [B[1]], [EG[hp]])
                        kb.TT(qd[hp][:], qn[:, hp, :], EG[hp][:], ALU.mult, [qn, EG[hp]], [qd[hp]], eng="pool")
                        pws = B[7][:, hp * 128:(hp + 1) * 128]
                        kb.MM(pws, wTp[hp][:], Sb[hp][:], True, True, [wTp[hp], Sb[hp]], [B[7]])
                        for h2 in range(2):
                            h = 2 * hp + h2
                            cs_ = slice(64 * h2, 64 * h2 + 64)
                            kb.TT(vnp[h][:, cs_], upair[hp][:, cs_], B[7][:, hp * 128 + 64 * h2:hp * 128 + 64 * h2 + 64],
                                  ALU.subtract, [upair[hp], B[7]], [vnp[h]])
                        po = B[6][:, hp * 256:hp * 256 + 128]
                        kb.MM(po, Sb[hp][:], qd[hp][:], True, False, [Sb[hp], qd[hp]], [B[6].s(hp)])
                        kb.MM(po, vnp[2 * hp][:], QKm[:, 2 * hp, :], False, False, [vnp[2 * hp], QKm], [B[6].s(hp)])
                        kb.MM(po, vnp[2 * hp + 1][:], QKm[:, 2 * hp + 1, :], False, True, [vnp[2 * hp + 1], QKm], [B[6].s(hp)])
                        cols_ = slice(n * 128, (n + 1) * 128)
                        if d == 0:
                            kb.CP(OACC[:, hp, cols_], po, [B[6].s(hp)], [OACC.s(n)], eng="act")
                        else:
                            kb.TT(OACC[:, hp, cols_], OACC[:, hp, cols_], po, ALU.add, [B[6].s(hp)], [OACC.s(n)])
                        pkv = B[6][:, hp * 256 + 128:hp * 256 + 256]
                        kb.MM(pkv, khp[2 * hp][:], vnp[2 * hp][:], True, False, [khp[2 * hp], vnp[2 * hp]], [B[6].s(2 + hp)])
                        kb.MM(pkv, khp[2 * hp + 1][:], vnp[2 * hp + 1][:], False, True, [khp[2 * hp + 1], vnp[2 * hp + 1]],
                              [B[6].s(2 + hp)])
                        kb.CP(cdp[hp][0:64, :], cdec[0:64, 2 * hp:2 * hp + 1], [cdec], [cdp[hp]])
                        kb.CP(cdp[hp][64:128, :], cdec[64:128, 2 * hp + 1:2 * hp + 2], [cdec], [cdp[hp]])
                        kb.STT(Sb[hp][:], Sb[hp][:], cdp[hp][:, 0:1], pkv, ALU.mult, ALU.add,
                               [Sb[hp], cdp[hp], B[6].s(2 + hp)], [Sb[hp]])
        with P.scope():
            G = P.sbuf("g_G2", [128, 1])
            for hh in range(2):
                kb.LD(G[64 * hh:64 * hh + 64, :], kb.prm["gdn_norm_g"][l].rearrange("(p o) -> p o", o=1), [G])
            finalize_gated(kb, OACC, Z_GG, G, 512, "g_")


class _Ctx:
    pass


def mixer_gdn2(kb, l):
    P = kb.P
    (gdn_conv if kb.cfg.get('conv_old') else gdn_conv2)(kb, l)
    with P.scope():
        OACC = P.sbuf("g_oacc", [128, 2, S])
        kb.MS(OACC[:, 0, :], 0.0, [OACC.s(n) for n in range(NT)], eng="pool")
        kb.MS(OACC[:, 1, :], 0.0, [OACC.s(n) for n in range(NT)], eng="pool")
        with P.scope():
            DTB = P.sbuf("g_dtb", [128, 8]); NEGA = P.sbuf("g_nega", [128, 8])
            kb.LD(DTB[:], kb.prm["gdn_dt_bias"][l].rearrange("d h -> (d h)").partition_broadcast(128), [DTB])
            kb.LD(NEGA[:], kb.prm["gdn_a_log"][l].rearrange("d h -> (d h)").partition_broadcast(128), [NEGA])
            kb.ACT(NEGA[:], NEGA[:], AF.Exp, [NEGA], [NEGA])
            kb.TS(NEGA[:], NEGA[:], -1.0, None, ALU.mult, None, [NEGA], [NEGA])
            ident = kb.C("IDENT")
            idb = ident.unsqueeze(1).to_broadcast([128, 4, 128])
            cxs = []
            for d in range(2):
                cx = _Ctx()
                cx.d = d
                pf = "g%d_" % d
                cx.qnb = [P.sbuf(pf + "q%d" % i, [128, 2, 128]) for i in range(2)]
                cx.knb = [P.sbuf(pf + "k%d" % i, [128, 2, 128]) for i in range(2)]
                cx.vvb = [P.sbuf(pf + "v%d" % i, [128, 2, 128]) for i in range(2)]
                cx.gabb = [P.sbuf(pf + "gab%d" % i, [128, 16]) for i in range(2)]
                for nm in ("xa", "ea", "loga", "beta", "lnb", "gtm", "ngt", "ekr", "cdec", "eg", "beg", "gpl"):
                    setattr(cx, nm, P.sbuf(pf + nm, [128, 4]))
                cx.ROWS = P.sbuf(pf + "rows", [4, 384])
                for nm in ("LI", "LBT", "LBm"):
                    setattr(cx, nm, P.sbuf(pf + nm, [128, 4, 128]))
                cx.QKm = P.sbuf(pf + "QKm", [128, 4, 128], BF16)
                cx.ROWSX = P.sbuf(pf + "rowsx", [4, 3, 4, 128])
                cx.knp = [[P.sbuf(pf + "knp%d%d" % (i, h), [128, 128], BF16) for h in range(4)] for i in range(2)]
                for i in range(2):
                    for h in range(4):
                        kb.MS(cx.knp[i][h][:], 0.0, [cx.knp[i][h]], eng="pool")
                cx.kq16 = [P.sbuf(pf + "kq16%d" % i, [128, 2, 2, 128], BF16) for i in range(2)]
                for nm in ("NAT", "NA", "Tm", "Wm", "x1", "y1", "tmx", "tmy"):
                    setattr(cx, nm, P.sbuf(pf + nm, [128, 4, 128], BF16))
                cx.Rm = [P.sbuf(pf + "R%d" % h, [128, 128], BF16) for h in range(4)]
                cx.khp = [P.sbuf(pf + "kh%d" % h, [128, 128], BF16) for h in range(4)]
                cx.vnp = [P.sbuf(pf + "vn%d" % h, [128, 128], BF16) for h in range(4)]
                for h in range(4):
                    kb.MS(cx.khp[h][:], 0.0, [cx.khp[h]], eng="pool")
                    kb.MS(cx.vnp[h][:], 0.0, [cx.vnp[h]], eng="pool")
                cx.upair = [P.sbuf(pf + "up%d" % hp, [128, 128]) for hp in range(2)]
                cx.wTp = [P.sbuf(pf + "wT%d" % hp, [128, 128]) for hp in range(2)]
                cx.EG = [P.sbuf(pf + "EG%d" % hp, [128, 128]) for hp in range(2)]
                cx.qd = [P.sbuf(pf + "qd%d" % hp, [128, 128]) for hp in range(2)]
                cx.cdp = [P.sbuf(pf + "cdp%d" % hp, [128, 1]) for hp in range(2)]
                cx.Sb = [P.sbuf(pf + "S%d" % hp, [128, 128]) for hp in range(2)]
                for hp in range(2):
                    kb.MS(cx.Sb[hp][:], 0.0, [cx.Sb[hp]])
                cx.B = [P.psum(pf + "B%d" % i, [128, 512]) for i in range(4)]
                cx.tri = kb.C("TRIF" if d == 0 else "TRIB")
                cx.rem = kb.C("SUFF" if d == 0 else "PREB")
                cx.n_incl = kb.C("NLE" if d == 0 else "NGE")
                cx.n_strT = kb.C("NLT" if d == 0 else "NGT")
                cx.n_str = kb.C("NGT" if d == 0 else "NLT")
                cx.it = 0
                cx.id16 = P.sbuf(pf + "id16", [128, 128], BF16)
                kb.CP(cx.id16[:], ident, [], [cx.id16])
                for nm_, cn in (("n_incl4", "NLE" if d == 0 else "NGE"), ("n_strT4", "NLT" if d == 0 else "NGT"),
                                ("n_str4", "NGT" if d == 0 else "NLT")):
                    t_ = P.sbuf(pf + nm_, [128, 4, 128], BF16)
                    kb.CP(t_[:], kb.C(cn).unsqueeze(1).to_broadcast([128, 4, 128]), [], [t_])
                    setattr(cx, nm_, t_[:].rearrange("p h i -> p (h i)"))
                bd = P.sbuf(pf + "bd4", [4, 4, 128])
                for h in range(4):
                    kb.CP(bd[:, h, :], kb.C("SELH%d" % h)[0:4, :], [], [bd])
                cx.bd4 = bd[:]
                cx.mT = {}; cx.mW = {}
                for s_ in (2, 4, 8, 16, 32, 64):
                    for nm_, dct, cn in (("mT", cx.mT, ("MOFF%d" if d == 0 else "MOFFT%d") % s_),
                                         ("mW", cx.mW, ("MOFFT%d" if d == 0 else "MOFF%d") % s_)):
                        mt_ = P.sbuf(pf + nm_ + str(s_), [128, 4, 128], mybir.dt.uint8)
                        kb.CP(mt_[:], kb.C(cn).unsqueeze(1).to_broadcast([128, 4, 128]), [], [mt_])
                        dct[s_] = mt_
                cxs.append(cx)

            def step(cx, n):
                d = cx.d
                Pa, Pb, Pc, Pd = cx.B
                cols = slice(n * 128, (n + 1) * 128)
                b = cx.it % 2; cx.it += 1
                qn, kn, vv, gab = cx.qnb[b], cx.knb[b], cx.vvb[b], cx.gabb[b]
                xa, ea, loga, beta, lnb = cx.xa, cx.ea, cx.loga, cx.beta, cx.lnb
                gtm, ngt, ekr, cdec, eg, beg, gpl = cx.gtm, cx.ngt, cx.ekr, cx.cdec, cx.eg, cx.beg, cx.gpl
                ROWS, LI, LBT, LBm, NAT, NA, QKm = cx.ROWS, cx.LI, cx.LBT, cx.LBm, cx.NAT, cx.NA, cx.QKm
                Tm, Wm, x1, y1, tmx, tmy = cx.Tm, cx.Wm, cx.x1, cx.y1, cx.tmx, cx.tmy
                Rm, khp, vnp, upair, wTp, EG, qd, cdp, Sb = cx.Rm, cx.khp, cx.vnp, cx.upair, cx.wTp, cx.EG, cx.qd, cx.cdp, cx.Sb
                tri = cx.tri
                kb.LD(qn[:], kb.QKVF[0:256, cols].rearrange("(hp p) t -> p hp t", p=128), [qn])
                kb.LD(kn[:], kb.QKVF[256:512, cols].rearrange("(hp p) t -> p hp t", p=128), [kn])
                kb.LD(vv[:], kb.QKVF[512:768, cols].rearrange("(hp p) t -> p hp t", p=128), [vv])
                kb.LD(gab[:], kb.ZT[cols, 512:528], [gab])
                knp = cx.knp[b]; kq16 = cx.kq16[b]
                for h in range(4):
                    kb.LD(knp[h][64 * (h % 2):64 * (h % 2) + 64, :], kb.QKVF[256 + 64 * h:256 + 64 * h + 64, cols], [knp[h]], q="pool")
                kb.LD(kq16[:, 0, :, :], kb.QKVF[256:512, cols].rearrange("(hp p) t -> p hp t", p=128), [kq16], q="pool")
                kb.LD(kq16[:, 1, :, :], kb.QKVF[0:256, cols].rearrange("(hp p) t -> p hp t", p=128), [kq16], q="pool")
                kb.TT(xa[:], gab[:, 4 * d:4 * d + 4], DTB[:, 4 * d:4 * d + 4], ALU.add, [gab, DTB], [xa])
                kb.ACT(ea[:], xa[:], AF.Exp, [xa], [ea])
                kb.ACT(ea[:], ea[:], AF.Ln, [ea], [ea], bias=kb.C("CCOL")[:, 1:2])
                kb.TT(loga[:], ea[:], NEGA[:, 4 * d:4 * d + 4], ALU.mult, [ea, NEGA], [loga])
                kb.ACT(beta[:], gab[:, 8 + 4 * d:12 + 4 * d], AF.Sigmoid, [gab], [beta])
                kb.ACT(lnb[:], beta[:], AF.Ln, [beta], [lnb])
                kb.MM(Pc[:, 0:4], tri, loga[:], True, True, [loga], [Pc])
                kb.MM(Pc[:, 4:8], cx.rem, loga[:], True, True, [loga], [Pc])
                kb.MM(Pc[:, 8:12], kb.C("ONES"), loga[:], True, True, [loga], [Pc])
                kb.CP(gtm[:], Pc[:, 0:4], [Pc], [gtm])
                kb.TS(ngt[:], Pc[:, 0:4], -1.0, None, ALU.mult, None, [Pc], [ngt])
                kb.ACT(ekr[:], Pc[:, 4:8], AF.Exp, [Pc], [ekr])
                kb.ACT(cdec[:], Pc[:, 8:12], AF.Exp, [Pc], [cdec])
                kb.ACT(eg[:], gtm[:], AF.Exp, [gtm], [eg])
                kb.TT(beg[:], beta[:], eg[:], ALU.mult, [beta, eg], [beg])
                kb.TT(gpl[:], gtm[:], lnb[:], ALU.add, [gtm, lnb], [gpl])
                kb.MM(Pd[0:4, 0:128], loga[:], tri, True, True, [loga], [Pd])
                kb.MM(Pd[0:4, 128:256], loga[:], tri, True, False, [loga], [Pd])
                kb.MM(Pd[0:4, 128:256], lnb[:], ident, False, True, [lnb], [Pd])
                kb.CP(ROWS[:, 0:256], Pd[0:4, 0:256], [Pd], [ROWS])
                kb.TS(ROWS[:, 256:384], Pd[0:4, 0:128], -1.0, None, ALU.mult, None, [Pd], [ROWS])
                yield
                kb.TT(cx.ROWSX[:], ROWS[:].rearrange("c (r i) -> c r i", r=3).unsqueeze(2).to_broadcast([4, 3, 4, 128]),
                      cx.bd4.unsqueeze(1).to_broadcast([4, 3, 4, 128]), ALU.mult, [ROWS], [cx.ROWSX])
                for (dst, ri, negm4, bias_t, bank) in ((LI, 0, cx.n_incl4, ngt, Pa), (LBT, 1, cx.n_strT4, ngt, Pb),
                                                       (LBm, 2, cx.n_str4, gpl, Pa)):
                    kb.MM(bank[:], kb.C("ONES")[0:4, :], cx.ROWSX[:, ri, :, :].rearrange("c h i -> c (h i)"), True, False,
                          [cx.ROWSX], [bank])
                    kb.MM(bank[:], cx.id16[:], negm4[:], False, True, [], [bank])
                    yield
                    for h in range(4):
                        kb.ACT(dst[:, h, :], bank[:, h * 128:(h + 1) * 128], AF.Exp, [bank, bias_t], [dst], bias=bias_t[:, h:h + 1])
                    yield
                for h in range(4):
                    hp, h2 = divmod(h, 2)
                    kb.MM(Pa[:, h * 128:(h + 1) * 128], knp[h][:], kq16[:, 0, hp, :], True, True, [knp[h], kq16], [Pa])
                    kb.MM(Pb[:, h * 128:(h + 1) * 128], knp[h][:], kq16[:, 1, hp, :], True, True, [knp[h], kq16], [Pb])
                pav = Pa[:].rearrange("p (h t) -> p h t", h=4)
                pbv = Pb[:].rearrange("p (h t) -> p h t", h=4)
                kb.STT(NAT[:], pav, -1.0, LBT[:], ALU.mult, ALU.mult, [Pa, LBT], [NAT])
                kb.STT(NA[:], pav, -1.0, LBm[:], ALU.mult, ALU.mult, [Pa, LBm], [NA])
                kb.TT(QKm[:], pbv, LI[:], ALU.mult, [Pb, LI], [QKm])
                yield
                mT = kb.C("MOFF1" if d == 0 else "MOFFT1").unsqueeze(1).to_broadcast([128, 4, 128])
                mW = kb.C("MOFFT1" if d == 0 else "MOFF1").unsqueeze(1).to_broadcast([128, 4, 128])
                kb.TT(tmx[:], NA[:], mT, ALU.mult, [NA], [tmx])
                kb.TT(tmy[:], NAT[:], mW, ALU.mult, [NAT], [tmy], eng="pool")
                kb.TT(Tm[:], tmx[:], idb, ALU.add, [tmx], [Tm])
                kb.TT(Wm[:], tmy[:], idb, ALU.add, [tmy], [Wm], eng="pool")
                yield
                for s_ in (2, 4, 8, 16, 32, 64):
                    for h in range(4):
                        kb.MM(Pa[:, h * 128:(h + 1) * 128], NAT[:, h, :], Tm[:, h, :], True, True, [NAT, Tm], [Pa])
                    for h in range(4):
                        kb.MM(Pb[:, h * 128:(h + 1) * 128], NA[:, h, :], Wm[:, h, :], True, True, [NA, Wm], [Pb])
                    yield
                    kb.CP(x1[:], pav, [Pa], [x1], eng="act")
                    kb.CP(y1[:], pbv, [Pb], [y1], eng="act")
                    yield
                    for h in range(4):
                        kb.MM(Pa[:, h * 128:(h + 1) * 128], Wm[:, h, :], x1[:, h, :], True, True, [Wm, x1], [Pa])
                    for h in range(4):
                        kb.MM(Pb[:, h * 128:(h + 1) * 128], Tm[:, h, :], y1[:, h, :], True, True, [Tm, y1], [Pb])
                    yield
                    kb.CPRED(Tm[:], cx.mT[s_][:], pav, [Pa, cx.mT[s_]], [Tm])
                    kb.CPRED(Wm[:], cx.mW[s_][:], pbv, [Pb, cx.mW[s_]], [Wm])
                    yield
                for hp in range(2):
                    kb.TR(Pc[:, 128:256], kn[:, hp, :], ident, [kn], [Pc])
                    kb.TR(Pc[:, 256:384], vv[:, hp, :], ident, [vv], [Pc])
                    for h2 in range(2):
                        h = 2 * hp + h2
                        kc = slice(64 * h2, 64 * h2 + 64)
                        vc = slice(64 * (1 - h2), 64 * (1 - h2) + 64)
                        kb.TS(Rm[h][:, kc], Pc[:, 128 + 64 * h2:128 + 64 * h2 + 64], beg[:, h:h + 1], None, ALU.mult, None,
                              [Pc, beg], [Rm[h]])
                        kb.ACT(Rm[h][:, vc], Pc[:, 256 + 64 * h2:256 + 64 * h2 + 64], AF.Copy, [Pc, beta], [Rm[h]],
                               scale=beta[:, h:h + 1])
                        kb.ACT(khp[h][:, kc], Pc[:, 128 + 64 * h2:128 + 64 * h2 + 64], AF.Copy, [Pc, ekr], [khp[h]],
                               scale=ekr[:, h:h + 1])
                    yield
                for h in range(4):
                    kb.MM(Pa[:, h * 128:(h + 1) * 128], Wm[:, h, :], Rm[h][:], True, True, [Wm, Rm[h]], [Pa])
                    kb.MM(Pb[:, h * 128:(h + 1) * 128], Rm[h][:], Wm[:, h, :], True, True, [Wm, Rm[h]], [Pb])
                for h in range(4):
                    hp, h2 = divmod(h, 2)
                    vc0 = 64 * (1 - h2)
                    kb.CP(upair[hp][:, 64 * h2:64 * h2 + 64], Pa[:, h * 128 + vc0:h * 128 + vc0 + 64], [Pa], [upair[hp]], eng="dve")
                    kb.CP(wTp[hp][64 * h2:64 * h2 + 64, :], Pb[64 * h2:64 * h2 + 64, h * 128:(h + 1) * 128], [Pb], [wTp[hp]], eng="act")
                yield
                for hp in range(2):
                    kb.MM(Pc[:, 384:512], kb.C("SELP%d" % hp)[0:4, :], ROWS[:, 0:128], True, True, [ROWS], [Pc])
                    kb.ACT(EG[hp][:], Pc[:, 384:512], AF.Exp, [Pc], [EG[hp]])
                    kb.TT(qd[hp][:], qn[:, hp, :], EG[hp][:], ALU.mult, [qn, EG[hp]], [qd[hp]], eng="pool")
                    pws = Pc[:, hp * 128:(hp + 1) * 128]
                    kb.MM(pws, wTp[hp][:], Sb[hp][:], True, True, [wTp[hp], Sb[hp]], [Pc])
                    for h2 in range(2):
                        h = 2 * hp + h2
                        cs_ = slice(64 * h2, 64 * h2 + 64)
                        kb.TT(vnp[h][:, cs_], upair[hp][:, cs_], Pc[:, hp * 128 + 64 * h2:hp * 128 + 64 * h2 + 64],
                              ALU.subtract, [upair[hp], Pc], [vnp[h]])
                    po = Pd[:, hp * 256:hp * 256 + 128]
                    kb.MM(po, Sb[hp][:], qd[hp][:], True, False, [Sb[hp], qd[hp]], [Pd])
                    kb.MM(po, vnp[2 * hp][:], QKm[:, 2 * hp, :], False, False, [vnp[2 * hp], QKm], [Pd])
                    kb.MM(po, vnp[2 * hp + 1][:], QKm[:, 2 * hp + 1, :], False, True, [vnp[2 * hp + 1], QKm], [Pd])
                    kb.TT(OACC[:, hp, cols], OACC[:, hp, cols], po, ALU.add, [Pd], [OACC.s(n)])
                    pkv = Pd[:, hp * 256 + 128:hp * 256 + 256]
                    kb.MM(pkv, khp[2 * hp][:], vnp[2 * hp][:], True, False, [khp[2 * hp], vnp[2 * hp]], [Pd])
                    kb.MM(pkv, khp[2 * hp + 1][:], vnp[2 * hp + 1][:], False, True, [khp[2 * hp + 1], vnp[2 * hp + 1]], [Pd])
                    kb.CP(cdp[hp][0:64, :], cdec[0:64, 2 * hp:2 * hp + 1], [cdec], [cdp[hp]])
                    kb.CP(cdp[hp][64:128, :], cdec[64:128, 2 * hp + 1:2 * hp + 2], [cdec], [cdp[hp]])
                    kb.STT(Sb[hp][:], Sb[hp][:], cdp[hp][:, 0:1], pkv, ALU.mult, ALU.add, [Sb[hp], cdp[hp], Pd], [Sb[hp]])
                    yield

            def stream(cx):
                for n in ORDER[cx.d][:kb.cfg.get("ntiles", NT)]:
                    yield from step(cx, n)
            active = [stream(cxs[0]), stream(cxs[1])]
            for _ in range(kb.cfg.get("g_off", 19)):
                next(active[0])
            while active:
                for g_ in list(active):
                    try:
                        next(g_)
                    except StopIteration:
                        active.remove(g_)
        with P.scope():
            G = P.sbuf("g_G2", [128, 1])
            for hh in range(2):
                kb.LD(G[64 * hh:64 * hh + 64, :], kb.prm["gdn_norm_g"][l].rearrange("(p o) -> p o", o=1), [G])
            finalize_gated(kb, OACC, Z_GG, G, 512, "g_")


def run_interleaved(gens, offset=0):
    active = list(gens)
    for _ in range(offset):
        try:
            next(active[0])
        except StopIteration:
            active.pop(0)
            break
    while active:
        for g_ in list(active):
            try:
                next(g_)
            except StopIteration:
                active.remove(g_)


def oacc_add(kb, OACC, hp, n, ps):
    cols = slice(n * 128, (n + 1) * 128)
    kb.TT(OACC[:, hp, cols], OACC[:, hp, cols], ps[:], ALU.add, [ps], [OACC.s(n)])


def oacc_zero(kb, OACC):
    for hp in range(2):
        kb.MS(OACC[:, hp, :], 0.0, [OACC.s(n) for n in range(NT)], eng="pool")


def mixer_hgrn2(kb, l):
    P = kb.P
    with P.scope():
        OACC = P.sbuf("h_oacc", [128, 2, S])
        oacc_zero(kb, OACC)
        with P.scope():
            LB = P.sbuf("h_LB", [128, 4]); OML = P.sbuf("h_OML", [128, 4])
            if l == 0:
                kb.MS(LB[:], 0.0, [LB]); kb.MS(OML[:], 1.0, [OML])
            else:
                lgt = P.sbuf("h_lgt", [128, 8])
                kb.LD(lgt[:], kb.prm["hgrn_lb_logits"][:].rearrange("l d (hp p) -> p (l d hp)", p=128), [lgt],
                      allow_slow_non_contiguous=True)
                kb.TT(LB[:], lgt[:, 4:8], lgt[:, 0:4], ALU.subtract, [lgt], [LB])
                kb.ACT(LB[:], LB[:], AF.Sigmoid, [LB], [LB])
                kb.TS(OML[:], LB[:], -1.0, 1.0, ALU.mult, ALU.add, [LB], [OML])

            def make(d):
                pf = "h%d_" % d
                hqb = [P.sbuf(pf + "q%d" % i, [128, 2, 128]) for i in range(2)]
                hfb = [P.sbuf(pf + "f%d" % i, [128, 2, 128]) for i in range(2)]
                Vp = [[P.sbuf(pf + "vp%d%d" % (i, h), [128, 128], BF16) for h in range(4)] for i in range(2)]
                khp = [[P.sbuf(pf + "kh%d%d" % (i, h), [128, 128], BF16) for h in range(4)] for i in range(2)]
                Qlp = [[P.sbuf(pf + "qlp%d%d" % (i, h2), [128, 128], BF16) for h2 in range(2)] for i in range(2)]
                for i in range(2):
                    for h2 in range(2):
                        kb.MS(Qlp[i][h2][:], 0.0, [Qlp[i][h2]], eng="pool")
                for i in range(2):
                    for h in range(4):
                        kb.MS(Vp[i][h][:], 0.0, [Vp[i][h]], eng="pool")
                        kb.MS(khp[i][h][:], 0.0, [khp[i][h]], eng="pool")
                MREF = [P.sbuf(pf + "mr%d" % i, [128, 4]) for i in range(2)]
                for i in range(2):
                    kb.MS(MREF[i][:], 0.0, [MREF[i]])

                def two(name, shape=(128, 128)):
                    return [P.sbuf(pf + "%s%d" % (name, i), list(shape)) for i in range(2)]
                qs, sgm, ff, logf, kk, bb, pre = two("qs"), two("sg"), two("ff"), two("lf"), two("kk"), two("bb"), two("pre")
                e1, Ql, e2, Qd = two("e1"), two("Ql"), two("e2"), two("Qd")
                Kt = [[P.sbuf(pf + "Kt%d%d" % (r, i), [128, 128], BF16) for i in range(2)] for r in range(4)]
                ex = two("ex")
                AT = [P.sbuf(pf + "AT%d" % i, [128, 2, 128], BF16) for i in range(2)]
                KhT = two("KhT")
                bend = two("bend", (128, 2))
                Sb = [P.sbuf(pf + "S%d" % hp, [128, 128]) for hp in range(2)]
                for hp in range(2):
                    kb.MS(Sb[hp][:], 0.0, [Sb[hp]])
                pss = P.psum(pf + "pss", [128, 2, 128])
                po = P.psum(pf + "pso", [128, 128])
                pk = P.psum(pf + "psk", [128, 128])
                pkv = P.psum(pf + "pskv", [128, 128])
                zf = Z_HFF if d == 0 else Z_HFB
                tri = kb.C("TRIF" if d == 0 else "TRIB").unsqueeze(1).to_broadcast([128, 2, 128])

                def gen():
                    it = 0
                    jj = 0
                    for n in ORDER[d]:
                        cols = slice(n * 128, (n + 1) * 128)
                        b = it % 2; it += 1
                        hq, hf = hqb[b], hfb[b]
                        kb.LD(hq[:], kb.ZF[Z_HQ:Z_HQ + 256, cols].rearrange("(hp p) t -> p hp t", p=128), [hq])
                        kb.LD(hf[:], kb.ZF[zf:zf + 256, cols].rearrange("(hp p) t -> p hp t", p=128), [hf])
                        for h in range(4):
                            kb.LD(Vp[b][h][:, 64 * (h % 2):64 * (h % 2) + 64], kb.ZT[cols, 64 * h:64 * h + 64], [Vp[b][h]], q="pool")
                        yield
                        for hp in range(2):
                            j = jj % 2; jj += 1
                            c = 2 * d + hp
                            mref = MREF[j]
                            kb.ACT(qs[j][:], hq[:, hp, :], AF.Exp, [hq], [qs[j]], scale=-1.0)
                            kb.TS(qs[j][:], qs[j][:], 1.0, None, ALU.add, None, [qs[j]], [qs[j]])
                            kb.RECIP(qs[j][:], qs[j][:], [qs[j]], [qs[j]])
                            kb.TT(qs[j][:], qs[j][:], hq[:, hp, :], ALU.mult, [qs[j], hq], [qs[j]], eng="pool")
                            kb.ACT(sgm[j][:], hf[:, hp, :], AF.Exp, [hf], [sgm[j]], scale=-1.0)
                            kb.TS(sgm[j][:], sgm[j][:], 1.0, None, ALU.add, None, [sgm[j]], [sgm[j]])
                            kb.RECIP(sgm[j][:], sgm[j][:], [sgm[j]], [sgm[j]])
                            kb.TS(ff[j][:], sgm[j][:], OML[:, c:c + 1], LB[:, c:c + 1], ALU.mult, ALU.add, [sgm[j], OML, LB], [ff[j]])
                            kb.ACT(logf[j][:], ff[j][:], AF.Ln, [ff[j]], [logf[j]])
                            kb.TS(kk[j][:], ff[j][:], -1.0, 1.0, ALU.mult, ALU.add, [ff[j]], [kk[j]], eng="pool")
                            yield
                            B = bb[j]
                            if d == 0:
                                kb.SCAN(B[:], kb.C("ONES"), logf[j][:], [logf[j]], [B])
                                kb.CP(mref[:, 1:4], B[:].rearrange("p (r c) -> p r c", c=32)[:, 0:3, 31], [B], [mref])
                                be = B[:, 127:128]
                            else:
                                kb.SCAN(pre[j][:], kb.C("ONES"), logf[j][:], [logf[j]], [pre[j]])
                                kb.STT(B[:], pre[j][:], -1.0, logf[j][:], ALU.mult, ALU.add, [pre[j], logf[j]], [B])
                                kb.TS(B[:], B[:], pre[j][:, 127:128], None, ALU.add, None, [B, pre[j]], [B])
                                kb.CP(mref[:, 0:3], B[:].rearrange("p (r c) -> p r c", c=32)[:, 1:4, 0], [B], [mref])
                                be = B[:, 0:1]
                            yield
                            kb.TT(e1[j][:].rearrange("p (r c) -> p r c", c=32), B[:].rearrange("p (r c) -> p r c", c=32),
                                  mref[:].unsqueeze(2).to_broadcast([128, 4, 32]), ALU.subtract, [B, mref], [e1[j]])
                            kb.ACT(e1[j][:], e1[j][:], AF.Exp, [e1[j]], [e1[j]])
                            for h2 in range(2):
                                rs_ = slice(64 * h2, 64 * h2 + 64)
                                kb.STT(Qlp[j][h2][rs_, :], qs[j][rs_, :], 0.125, e1[j][rs_, :], ALU.mult, ALU.mult,
                                       [qs[j], e1[j]], [Qlp[j][h2]])
                            kb.ACT(e2[j][:], B[:], AF.Exp, [B], [e2[j]])
                            kb.STT(Qd[j][:], qs[j][:], 0.125, e2[j][:], ALU.mult, ALU.mult, [qs[j], e2[j]], [Qd[j]])
                            yield
                            for r in range(4):
                                kb.ACT(ex[j][:], B[:], AF.Exp, [B, mref], [ex[j]], scale=-1.0, bias=mref[:, r:r + 1])
                                kb.STT(Kt[r][j][:], ex[j][:], 1e26, kk[j][:], ALU.min, ALU.mult, [ex[j], kk[j]], [Kt[r][j]])
                                for h2 in range(2):
                                    kb.MM(pss[:, h2, 32 * r:32 * r + 32], Kt[r][j][:],
                                          Qlp[j][h2][:, 32 * r:32 * r + 32], True, True,
                                          [Kt[r][j], Qlp[j][h2]], [pss])
                                yield
                            kb.TT(AT[j][:], pss[:], tri, ALU.mult, [pss], [AT[j]])
                            yield
                            kb.MM(po[:], Vp[b][2 * hp][:], AT[j][:, 0, :], True, False, [Vp[b][2 * hp], AT[j]], [po])
                            kb.MM(po[:], Vp[b][2 * hp + 1][:], AT[j][:, 1, :], False, False, [Vp[b][2 * hp + 1], AT[j]], [po])
                            kb.MM(po[:], Sb[hp][:], Qd[j][:], False, True, [Sb[hp], Qd[j]], [po])
                            oacc_add(kb, OACC, hp, n, po)
                            kb.CP(bend[j][:, 0:1], be, [B], [bend[j]])
                            kb.ACT(KhT[j][:], B[:], AF.Exp, [B, bend[j]], [KhT[j]], scale=-1.0, bias=bend[j][:, 0:1])
                            kb.TT(KhT[j][:], KhT[j][:], kk[j][:], ALU.mult, [KhT[j], kk[j]], [KhT[j]], eng="pool")
                            kb.ACT(bend[j][:, 1:2], bend[j][:, 0:1], AF.Exp, [bend[j]], [bend[j]])
                            yield
                            kb.TR(pk[:], KhT[j][:], kb.C("IDENT"), [KhT[j]], [pk])
                            for h2 in range(2):
                                h = 2 * hp + h2
                                kb.CP(khp[b][h][:, 64 * h2:64 * h2 + 64], pk[:, 64 * h2:64 * h2 + 64], [pk], [khp[b][h]],
                                      eng=("act" if h2 else "dve"))
                            yield
                            kb.MM(pkv[:], khp[b][2 * hp][:], Vp[b][2 * hp][:], True, False, [khp[b][2 * hp], Vp[b][2 * hp]], [pkv])
                            kb.MM(pkv[:], khp[b][2 * hp + 1][:], Vp[b][2 * hp + 1][:], False, True,
                                  [khp[b][2 * hp + 1], Vp[b][2 * hp + 1]], [pkv])
                            kb.STT(Sb[hp][:], Sb[hp][:], bend[j][:, 1:2], pkv[:], ALU.mult, ALU.add,
                                   [Sb[hp], bend[j], pkv], [Sb[hp]])
                            yield
                return gen()
            run_interleaved([make(0), make(1)], offset=kb.cfg.get("h_off", 11))
        with P.scope():
            G = P.sbuf("h_G2", [128, 1])
            for hh in range(2):
                kb.LD(G[64 * hh:64 * hh + 64, :], kb.prm["hgrn_norm_g"][l].rearrange("(p o) -> p o", o=1), [G])
            finalize_gated(kb, OACC, Z_HG, G, 0, "h_")


def mixer_ret2(kb, l):
    P = kb.P
    with P.scope():
        OACC = P.sbuf("r_oacc", [128, 2, S])
        oacc_zero(kb, OACC)
        with P.scope():
            lgt = P.sbuf("r_lgt", [128, 8])
            kb.LD(lgt[:], kb.prm["ret_decay_logit"][l].rearrange("d h -> (d h)").partition_broadcast(128), [lgt])
            LG = P.sbuf("r_LG", [128, 8])
            kb.ACT(LG[:], lgt[:], AF.Sigmoid, [lgt], [LG])
            kb.ACT(LG[:], LG[:], AF.Ln, [LG], [LG])
            LGP = P.sbuf("r_LGP", [128, 4])
            for d in range(2):
                for hp in range(2):
                    c = 2 * d + hp
                    kb.CP(LGP[0:64, c:c + 1], LG[0:64, 4 * d + 2 * hp:4 * d + 2 * hp + 1], [LG], [LGP])
                    kb.CP(LGP[64:128, c:c + 1], LG[64:128, 4 * d + 2 * hp + 1:4 * d + 2 * hp + 2], [LG], [LGP])
            MK = [P.sbuf("r_MK%d" % d, [128, 4, 128]) for d in range(2)]
            QDEC = [[P.sbuf("r_QD%d%d" % (d, hp), [128, 128]) for hp in range(2)] for d in range(2)]
            etmp = P.sbuf("r_etmp", [128, 128])
            for d in range(2):
                for h in range(4):
                    kb.ACT(etmp[:], kb.C("DIFF" if d == 0 else "NDIFF"), AF.Exp, [LG], [etmp],
                           scale=LG[:, 4 * d + h:4 * d + h + 1])
                    kb.STT(MK[d][:, h, :], etmp[:], 0.125, kb.C("TRIF" if d == 0 else "TRIB"), ALU.mult, ALU.mult,
                           [etmp], [MK[d]])
                for hp in range(2):
                    kb.ACT(QDEC[d][hp][:], kb.C("IOTAF1" if d == 0 else "RIOTAF"), AF.Exp, [LGP], [QDEC[d][hp]],
                           scale=LGP[:, 2 * d + hp:2 * d + hp + 1])
            KD = P.sbuf("r_KD", [128, 8])
            kb.ACT(KD[:, 0:4], LG[:, 0:4], AF.Exp, [LG], [KD], scale=kb.C("CCOL")[:, 3:4])
            kb.ACT(KD[:, 4:8], LG[:, 4:8], AF.Exp, [LG], [KD], scale=kb.C("CCOL")[:, 2:3])
            kb.TS(KD[:], KD[:], 0.125, None, ALU.mult, None, [KD], [KD])
            CV = P.sbuf("r_CV", [128, 4])
            kb.ACT(CV[:], LGP[:], AF.Exp, [LGP], [CV], scale=128.0)

            def make(d):
                pf = "r%d_" % d
                qTb = [P.sbuf(pf + "q%d" % i, [128, 2, 128]) for i in range(2)]
                kTb = [P.sbuf(pf + "k%d" % i, [128, 2, 128]) for i in range(2)]
                csb = [P.sbuf(pf + "cs%d" % i, [128, 2, 128]) for i in range(2)]
                Vp = [[P.sbuf(pf + "vp%d%d" % (i, h), [128, 128], BF16) for h in range(4)] for i in range(2)]
                khp = [[P.sbuf(pf + "kh%d%d" % (i, h), [128, 128], BF16) for h in range(4)] for i in range(2)]
                qrp = [[P.sbuf(pf + "qrp%d%d" % (i, h2), [128, 128], BF16) for h2 in range(2)] for i in range(2)]
                for i in range(2):
                    for h2 in range(2):
                        kb.MS(qrp[i][h2][:], 0.0, [qrp[i][h2]], eng="pool")
                kr16 = [P.sbuf(pf + "kr16%d" % i, [128, 128], BF16) for i in range(2)]
                for i in range(2):
                    for h in range(4):
                        kb.MS(Vp[i][h][:], 0.0, [Vp[i][h]], eng="pool")
                        kb.MS(khp[i][h][:], 0.0, [khp[i][h]], eng="pool")
                t1 = [P.sbuf(pf + "t1%d" % i, [128, 128]) for i in range(2)]
                t2 = [P.sbuf(pf + "t2%d" % i, [128, 128]) for i in range(2)]
                qr = [P.sbuf(pf + "qr%d" % i, [128, 2, 128]) for i in range(2)]
                kr = [P.sbuf(pf + "kr%d" % i, [128, 2, 128]) for i in range(2)]
                AT = [P.sbuf(pf + "AT%d" % i, [128, 2, 128], BF16) for i in range(2)]
                qd = [P.sbuf(pf + "qd%d" % i, [128, 128]) for i in range(2)]
                Sb = [P.sbuf(pf + "S%d" % hp, [128, 128]) for hp in range(2)]
                for hp in range(2):
                    kb.MS(Sb[hp][:], 0.0, [Sb[hp]])
                pr = P.psum(pf + "psr", [128, 256])
                pss = P.psum(pf + "pss", [128, 2, 128])
                po = P.psum(pf + "pso", [128, 128])
                pkk = P.psum(pf + "pskk", [128, 256])

                def gen():
                    it = 0
                    jj = 0
                    for n in ORDER[d]:
                        cols = slice(n * 128, (n + 1) * 128)
                        b = it % 2; it += 1
                        qT, kT, cs = qTb[b], kTb[b], csb[b]
                        kb.LD(qT[:], kb.ZF[Z_RQ:Z_RQ + 256, cols].rearrange("(hp p) t -> p hp t", p=128), [qT])
                        kb.LD(kT[:], kb.ZF[Z_RK:Z_RK + 256, cols].rearrange("(hp p) t -> p hp t", p=128), [kT])
                        kb.LD(cs[:, 0, :], kb.ropec[:, cols], [cs])
                        kb.LD(cs[:, 1, :], kb.ropes[:, cols], [cs])
                        for h in range(4):
                            kb.LD(Vp[b][h][:, 64 * (h % 2):64 * (h % 2) + 64], kb.ZT[cols, 256 + 64 * h:256 + 64 * h + 64],
                                  [Vp[b][h]], q="pool")
                        yield
                        for hp in range(2):
                            j = jj % 2; jj += 1
                            kb.MM(pr[:, 0:128], kb.C("ROT"), qT[:, hp, :], True, True, [qT], [pr])
                            kb.MM(pr[:, 128:256], kb.C("ROT"), kT[:, hp, :], True, True, [kT], [pr])
                            yield
                            for (src_, dst, off) in ((qT, qr[b], 0), (kT, kr[b], 128)):
                                kb.TT(t1[j][:], src_[:, hp, :], cs[:, 0, :], ALU.mult, [src_, cs], [t1[j]])
                                kb.TT(t2[j][:], pr[:, off:off + 128], cs[:, 1, :], ALU.mult, [pr, cs], [t2[j]])
                                kb.TT(dst[:, hp, :], t1[j][:], t2[j][:], ALU.add, [t1[j], t2[j]], [dst.s(hp)], eng="pool")
                                yield
                            kb.CP(kr16[j][:], kr[b][:, hp, :], [kr[b].s(hp)], [kr16[j]], eng="act")
                            for h2 in range(2):
                                rs_ = slice(64 * h2, 64 * h2 + 64)
                                kb.CP(qrp[j][h2][rs_, :], qr[b][rs_, hp, :], [qr[b].s(hp)], [qrp[j][h2]], eng="act")
                            yield
                            for h2 in range(2):
                                kb.MM(pss[:, h2, :], kr16[j][:], qrp[j][h2][:], True, True, [kr16[j], qrp[j][h2]], [pss])
                            yield
                            kb.TT(AT[j][:], pss[:], MK[d][:, 2 * hp:2 * hp + 2, :], ALU.mult, [pss, MK[d]], [AT[j]])
                            kb.TT(qd[j][:], qr[b][:, hp, :], QDEC[d][hp][:], ALU.mult, [qr[b].s(hp), QDEC[d][hp]], [qd[j]],
                                  eng="pool")
                            yield
                            kb.MM(po[:], Vp[b][2 * hp][:], AT[j][:, 0, :], True, False, [Vp[b][2 * hp], AT[j]], [po])
                            kb.MM(po[:], Vp[b][2 * hp + 1][:], AT[j][:, 1, :], False, False, [Vp[b][2 * hp + 1], AT[j]], [po])
                            kb.MM(po[:], Sb[hp][:], qd[j][:], False, True, [Sb[hp], qd[j]], [po])
                            kb.TR(pkk[:, 0:128], kr[b][:, hp, :], kb.C("IDENT"), [kr[b].s(hp)], [pkk])
                            yield
                            oacc_add(kb, OACC, hp, n, po)
                            for h2 in range(2):
                                h = 2 * hp + h2
                                kb.ACT(khp[b][h][:, 64 * h2:64 * h2 + 64], pkk[:, 64 * h2:64 * h2 + 64], AF.Copy,
                                       [pkk, KD], [khp[b][h]], scale=KD[:, 4 * d + h:4 * d + h + 1])
                            yield
                            kb.MM(pkk[:, 128:256], khp[b][2 * hp][:], Vp[b][2 * hp][:], True, False,
                                  [khp[b][2 * hp], Vp[b][2 * hp]], [pkk])
                            kb.MM(pkk[:, 128:256], khp[b][2 * hp + 1][:], Vp[b][2 * hp + 1][:], False, True,
                                  [khp[b][2 * hp + 1], Vp[b][2 * hp + 1]], [pkk])
                            kb.STT(Sb[hp][:], Sb[hp][:], CV[:, 2 * d + hp:2 * d + hp + 1], pkk[:, 128:256], ALU.mult, ALU.add,
                                   [Sb[hp], CV, pkk], [Sb[hp]])
                            yield
                return gen()
            run_interleaved([make(0), make(1)], offset=kb.cfg.get("r_off", 8))
        with P.scope():
            finalize_gated(kb, OACC, Z_RG, None, 256, "r_")


def s5_tables(kb, l, d, VFr, VFi, T1, T2, AR, NAI):
    P = kb.P
    prm = kb.prm
    with P.scope():
        lr = P.sbuf("s_lr", [128, 16, 64]); li = P.sbuf("s_li", [128, 16, 64]); dtb = P.sbuf("s_dt", [128, 16])
        kb.LD(lr[:], prm["s5_lam_re"][l][d].rearrange("g p -> (g p)").partition_broadcast(128), [lr])
        kb.LD(li[:], prm["s5_lam_im"][l][d].rearrange("g p -> (g p)").partition_broadcast(128), [li])
        kb.LD(dtb[:], prm["s5_log_dt"][l][d].partition_broadcast(128), [dtb])
        kb.ACT(dtb[:], dtb[:], AF.Exp, [dtb], [dtb])
        dt_bc = dtb[:].unsqueeze(2).to_broadcast([128, 16, 64])
        lrdt = P.sbuf("s_lrdt", [128, 16, 64]); lidt = P.sbuf("s_lidt", [128, 16, 64])
        kb.TT(lrdt[:], lr[:], dt_bc, ALU.mult, [lr, dtb], [lrdt])
        kb.TT(lidt[:], li[:], dt_bc, ALU.mult, [li, dtb], [lidt])
        a = [P.sbuf("s_a%d" % i, [128, 16, 64]) for i in range(8)]
        mag, ang, sn, cs, tmp, ar, ai, t2 = a
        kb.ACT(mag[:], lrdt[:], AF.Exp, [lrdt], [mag])
        _sincos(kb, lidt[:], sn[:], cs[:], [lidt, sn, cs, tmp], tmp[:])
        kb.TT(ar[:], mag[:], cs[:], ALU.mult, [mag, cs], [ar])
        kb.TT(ai[:], mag[:], sn[:], ALU.mult, [mag, sn], [ai])
        den = P.sbuf("s_den", [128, 16, 64]); fr = P.sbuf("s_fr", [128, 16, 64]); fi = P.sbuf("s_fi", [128, 16, 64])
        kb.TT(den[:], lr[:], lr[:], ALU.mult, [lr], [den])
        kb.TT(t2[:], li[:], li[:], ALU.mult, [li], [t2])
        kb.TT(den[:], den[:], t2[:], ALU.add, [den, t2], [den])
        kb.RECIP(den[:], den[:], [den], [den])
        kb.TS(ar[:], ar[:], -1.0, None, ALU.add, None, [ar], [ar])
        kb.TT(fr[:], ar[:], lr[:], ALU.mult, [ar, lr], [fr])
        kb.TT(t2[:], ai[:], li[:], ALU.mult, [ai, li], [t2])
        kb.TT(fr[:], fr[:], t2[:], ALU.add, [fr, t2], [fr])
        kb.TT(fr[:], fr[:], den[:], ALU.mult, [fr, den], [fr])
        kb.TT(fi[:], ai[:], lr[:], ALU.mult, [ai, lr], [fi])
        kb.TT(t2[:], ar[:], li[:], ALU.mult, [ar, li], [t2])
        kb.TT(fi[:], fi[:], t2[:], ALU.subtract, [fi, t2], [fi])
        kb.TT(fi[:], fi[:], den[:], ALU.mult, [fi, den], [fi])
        jcol = kb.C("CCOL")[:, 2:3] if d == 0 else kb.C("CCOL")[:, 3:4]
        njcol = kb.C("CCOL")[:, 6:7] if d == 0 else kb.C("CCOL")[:, 7:8]
        kb.ACT(mag[:], lrdt[:], AF.Exp, [lrdt], [mag], scale=njcol)
        kb.TS(ang[:], lidt[:], jcol, None, ALU.mult, None, [lidt], [ang])
        _sincos(kb, ang[:], sn[:], cs[:], [ang, sn, cs, tmp], tmp[:])
        vr, vi = ar, ai
        kb.TT(vr[:], mag[:], cs[:], ALU.mult, [mag, cs], [vr])
        kb.TT(vi[:], mag[:], sn[:], ALU.mult, [mag, sn], [vi])
        kb.TS(vi[:], vi[:], -1.0, None, ALU.mult, None, [vi], [vi])
        kb.TT(VFr[:], vr[:], fr[:], ALU.mult, [vr, fr], [VFr])
        kb.TT(t2[:], vi[:], fi[:], ALU.mult, [vi, fi], [t2])
        kb.TT(VFr[:], VFr[:], t2[:], ALU.subtract, [VFr, t2], [VFr])
        kb.TT(VFi[:], vr[:], fi[:], ALU.mult, [vr, fi], [VFi])
        kb.TT(t2[:], vi[:], fr[:], ALU.mult, [vi, fr], [t2])
        kb.TT(VFi[:], VFi[:], t2[:], ALU.add, [VFi, t2], [VFi])
    with P.scope():
        dtb = P.sbuf("s_dt2", [128, 16])
        kb.LD(dtb[:], prm["s5_log_dt"][l][d].partition_broadcast(128), [dtb])
        kb.ACT(dtb[:], dtb[:], AF.Exp, [dtb], [dtb])
        lrp = P.sbuf("s_lrp", [128, 16]); lip = P.sbuf("s_lip", [128, 16])
        for hh in range(2):
            kb.LD(lrp[64 * hh:64 * hh + 64, :], prm["s5_lam_re"][l][d].rearrange("g p -> p g"), [lrp],
                  allow_slow_non_contiguous=True)
            kb.LD(lip[64 * hh:64 * hh + 64, :], prm["s5_lam_im"][l][d].rearrange("g p -> p g"), [lip],
                  allow_slow_non_contiguous=True)
        kb.TT(lrp[:], lrp[:], dtb[:], ALU.mult, [lrp, dtb], [lrp])
        kb.TT(lip[:], lip[:], dtb[:], ALU.mult, [lip, dtb], [lip])
        b4 = [P.sbuf("s_b%d" % i, [128, 16, 128]) for i in range(4)]
        arg, sn2, cs2, tmp2 = b4
        mt = kb.C("IOTAF" if d == 0 else "R127F")
        mt_bc = mt.unsqueeze(1).to_broadcast([128, 16, 128])
        kb.TT(arg[:], lrp[:].unsqueeze(2).to_broadcast([128, 16, 128]), mt_bc, ALU.mult, [lrp], [arg])
        kb.ACT(T1[:], arg[:], AF.Exp, [arg], [T1])
        kb.TT(arg[:], lip[:].unsqueeze(2).to_broadcast([128, 16, 128]), mt_bc, ALU.mult, [lip, T1], [arg])
        _sincos(kb, arg[:], sn2[:], cs2[:], [arg, sn2, cs2, tmp2], tmp2[:])
        kb.TT(T2[:], T1[:], sn2[:], ALU.mult, [T1, sn2], [T2])
        kb.TS(T2[:], T2[:], -1.0, None, ALU.mult, None, [T2], [T2])
        kb.TT(T1[:], T1[:], cs2[:], ALU.mult, [T1, cs2], [T1])
        c4 = [P.sbuf("s_c%d" % i, [128, 16]) for i in range(4)]
        kb.ACT(c4[0][:], lrp[:], AF.Exp, [lrp], [c4[0]])
        _sincos(kb, lip[:], c4[1][:], c4[2][:], [lip, c4[1], c4[2], c4[3]], c4[3][:])
        kb.TT(AR[:], c4[0][:], c4[2][:], ALU.mult, [c4[0], c4[2]], [AR])
        kb.TT(NAI[:], c4[0][:], c4[1][:], ALU.mult, [c4[0], c4[1]], [NAI])
        kb.TS(NAI[:], NAI[:], -1.0, None, ALU.mult, None, [NAI], [NAI])


def mixer_s5_2(kb, l):
    P = kb.P
    prm = kb.prm
    with P.scope():
        WX = P.sbuf("s_WX", [128, 2, 8, 2, 64])
        Cblk = P.sbuf("s_Cblk", [128, 16, 128])
        kb.MS(WX[:], 0.0, [WX], eng="pool")
        kb.MS(Cblk[:], 0.0, [Cblk], eng="pool")
        for g8 in range(8):
            for ri, nm in enumerate(("s5_b_re", "s5_b_im")):
                for gg in range(2):
                    src = prm[nm][l][8 * gg + g8].rearrange("p c -> c p")
                    kb.LD(WX[16 * g8:16 * g8 + 16, gg, g8, ri, :], src, [WX], allow_slow_non_contiguous=True)
        for g in range(16):
            g8 = g % 8
            kb.LD(Cblk[0:64, g, 16 * g8:16 * g8 + 16], prm["s5_c_re"][l][g].rearrange("c p -> p c"), [Cblk],
                  allow_slow_non_contiguous=True)
            kb.LD(Cblk[64:128, g, 16 * g8:16 * g8 + 16], prm["s5_c_im"][l][g].rearrange("c p -> p c"), [Cblk],
                  allow_slow_non_contiguous=True)
        kb.TS(Cblk[64:128, :, :], Cblk[64:128, :, :], -1.0, None, ALU.mult, None, [Cblk], [Cblk])
        Cb16 = P.sbuf("s_Cb16", [128, 16, 128], BF16)
        kb.CP(Cb16[:], Cblk[:], [Cblk], [Cb16])

        tabs = []
        for d in range(2):
            VFr = P.sbuf("s_VFr%d" % d, [128, 16, 64]); VFi = P.sbuf("s_VFi%d" % d, [128, 16, 64])
            T1 = P.sbuf("s_T1%d" % d, [128, 16, 128]); T2 = P.sbuf("s_T2%d" % d, [128, 16, 128])
            AR = P.sbuf("s_AR%d" % d, [128, 16]); NAI = P.sbuf("s_NAI%d" % d, [128, 16])
            s5_tables(kb, l, d, VFr, VFi, T1, T2, AR, NAI)
            tabs.append((VFr, VFi, T1, T2, AR, NAI))
        OACC = P.sbuf("s_oacc", [128, 2, S])
        oacc_zero(kb, OACC)
        with P.scope():
            def make(d):
                pf = "s%d_" % d
                VFr, VFi, T1, T2, AR, NAI = tabs[d]
                uTb = [P.sbuf(pf + "u%d" % i, [128, 2, 128]) for i in range(2)]
                mm_ = [P.sbuf(pf + "m%d" % i, [128, 4, 64]) for i in range(4)]
                W3 = [P.sbuf(pf + "W3%d" % i, [128, 4, 3, 64], BF16) for i in range(2)]
                tP = P.sbuf(pf + "tP", [128, 4, 128]); tPs = P.sbuf(pf + "tPs", [128, 4, 128])
                H1 = P.sbuf(pf + "H1", [128, 4, 128]); H2 = P.sbuf(pf + "H2", [128, 4, 128])
                Hb = [P.sbuf(pf + "Hb%d" % i, [128, 4, 128], BF16) for i in range(2)]
                tri16 = P.sbuf(pf + "tri16", [128, 128], BF16)
                kb.CP(tri16[:], kb.C("TRIF" if d == 0 else "TRIB"), [], [tri16])
                hend = P.sbuf(pf + "hend", [128, 16]); hsend = P.sbuf(pf + "hsend", [128, 16])
                hp_ = P.sbuf(pf + "hp", [128, 16]); hps_ = P.sbuf(pf + "hps", [128, 16])
                sm = [P.sbuf(pf + "sm%d" % i, [128, 16]) for i in range(4)]
                kb.MS(hp_[:], 0.0, [hp_]); kb.MS(hps_[:], 0.0, [hps_])
                xps = P.psum(pf + "xps", [128, 512])
                pps = P.psum(pf + "pps", [128, 4, 128])
                ppss = P.psum(pf + "ppss", [128, 4, 128])
                yps = P.psum(pf + "yps", [128, 128])
                te = 127 if d == 0 else 0

                def gen():
                    it = 0
                    kq = 0
                    for n in ORDER[d]:
                        uT = uTb[it % 2]; it += 1
                        cols = slice(n * 128, (n + 1) * 128)
                        kb.LD(uT[:], kb.ZF[Z_SU:Z_SU + 256, cols].rearrange("(gg p) t -> p gg t", p=128), [uT])
                        yield
                        for q in range(4):
                            gg, qq = divmod(q, 2)
                            gs = slice(4 * q, 4 * q + 4)
                            w3 = W3[kq % 2]; hb = Hb[kq % 2]; kq += 1
                            kb.MM(xps[:], uT[:, gg, :], WX[:, gg, 4 * qq:4 * qq + 4, :, :].rearrange("q a r p -> q (a r p)"),
                                  True, True, [uT, WX], [xps])
                            xv = xps[:].rearrange("t (g r p) -> t g r p", r=2, p=64)
                            kb.TT(mm_[0][:], xv[:, :, 0, :], VFr[:, gs, :], ALU.mult, [xps, VFr], [mm_[0]])
                            kb.TT(mm_[1][:], xv[:, :, 1, :], VFi[:, gs, :], ALU.mult, [xps, VFi], [mm_[1]])
                            kb.TT(mm_[2][:], xv[:, :, 0, :], VFi[:, gs, :], ALU.mult, [xps, VFi], [mm_[2]])
                            kb.TT(mm_[3][:], xv[:, :, 1, :], VFr[:, gs, :], ALU.mult, [xps, VFr], [mm_[3]])
                            yield
                            kb.TT(w3[:, :, 0, :], mm_[0][:], mm_[1][:], ALU.subtract, [mm_[0], mm_[1]], [w3])
                            kb.TT(w3[:, :, 1, :], mm_[2][:], mm_[3][:], ALU.add, [mm_[2], mm_[3]], [w3], eng="pool")
                            kb.TT(w3[:, :, 2, :], mm_[1][:], mm_[0][:], ALU.subtract, [mm_[0], mm_[1]], [w3])
                            yield
                            for i in range(4):
                                kb.MM(pps[:, i, :], w3[:, i, 0:2, :].rearrange("q r p -> q (r p)"), tri16[:], True, True, [w3, tri16], [pps])
                            for i in range(4):
                                kb.MM(ppss[:, i, :], w3[:, i, 1:3, :].rearrange("q r p -> q (r p)"), tri16[:], True, True, [w3, tri16], [ppss])
                            yield
                            kb.TT(tP[:], pps[:], hp_[:, gs].unsqueeze(2).to_broadcast([128, 4, 128]), ALU.add, [pps, hp_], [tP])
                            for i in range(4):
                                g = 4 * q + i
                                kb.ACT(tPs[:, i, :], ppss[:, i, :], AF.Identity, [ppss, hps_], [tPs], bias=hps_[:, g:g + 1])
                            yield
                            kb.TT(sm[0][:, 0:4], tPs[:, :, te], T1[:, gs, te], ALU.mult, [tPs, T1], [sm[0]])
                            kb.TT(sm[1][:, 0:4], tP[:, :, te], T2[:, gs, te], ALU.mult, [tP, T2], [sm[1]])
                            kb.TT(hsend[:, gs], sm[0][:, 0:4], sm[1][:, 0:4], ALU.subtract, [sm[0], sm[1]], [hsend])
                            kb.TT(sm[2][:, 0:4], tP[:, :, te], T1[:, gs, te], ALU.mult, [tP, T1], [sm[2]])
                            kb.TT(sm[3][:, 0:4], tPs[:, :, te], T2[:, gs, te], ALU.mult, [tPs, T2], [sm[3]])
                            kb.TT(hend[:, gs], sm[2][:, 0:4], sm[3][:, 0:4], ALU.add, [sm[2], sm[3]], [hend])
                            yield
                            kb.TT(H1[:], tP[:], T1[:, gs, :], ALU.mult, [tP, T1], [H1], eng="pool")
                            kb.TT(H2[:], tPs[:], T2[:, gs, :], ALU.mult, [tPs, T2], [H2])
                            yield
                            kb.TT(hb[:], H1[:], H2[:], ALU.add, [H1, H2], [hb])
                            yield
                            for i in range(4):
                                g = 4 * q + i
                                kb.MM(yps[:], Cb16[:, g, :], hb[:, i, :], (g % 8) == 0, (g % 8) == 7, [Cb16, hb], [yps])
                            if qq == 1:
                                oacc_add(kb, OACC, gg, n, yps)
                            yield
                        kb.TT(sm[0][:], hend[:], AR[:], ALU.mult, [hend, AR], [sm[0]])
                        kb.TT(sm[1][:], hsend[:], NAI[:], ALU.mult, [hsend, NAI], [sm[1]])
                        kb.TT(sm[2][:], hsend[:], AR[:], ALU.mult, [hsend, AR], [sm[2]])
                        kb.TT(sm[3][:], hend[:], NAI[:], ALU.mult, [hend, NAI], [sm[3]])
                        kb.TT(hp_[:], sm[0][:], sm[1][:], ALU.add, [sm[0], sm[1]], [hp_])
                        kb.TT(hps_[:], sm[2][:], sm[3][:], ALU.subtract, [sm[2], sm[3]], [hps_])
                        yield
                return gen()
            run_interleaved([make(0), make(1)], offset=kb.cfg.get("s_off", 17))
        with P.scope():
            dsk = P.sbuf("s_dsk", [128, 2]); glb = P.sbuf("s_glb", [128, 2])
            kb.LD(dsk[:], prm["s5_d"][l].rearrange("(gg p) -> p gg", p=128), [dsk], allow_slow_non_contiguous=True)
            kb.LD(glb[:], prm["s5_glu_b"][l].rearrange("(gg p) -> p gg", p=128), [glb], allow_slow_non_contiguous=True)
            gw = P.sbuf("s_gw", [128, 2, 256])
            kb.LD(gw[:], prm["s5_glu_w"][l].rearrange("(ct p) o -> p ct o", p=128), [gw])
            uTb = [P.sbuf("s_fu%d" % i, [128, 2, 128]) for i in range(2)]
            yy = [P.sbuf("s_yy%d" % i, [128, 2, 128]) for i in range(2)]
            x2 = [P.sbuf("s_x2%d" % i, [128, 2, 128]) for i in range(2)]
            th = [P.sbuf("s_th%d" % i, [128, 2, 128]) for i in range(2)]
            sgb = [P.sbuf("s_sg%d" % i, [128, 128]) for i in range(2)]
            ob = [P.sbuf("s_ob%d" % i, [128, 128]) for i in range(2)]
            psz = [P.psum("s_psz%d" % i, [128, 128]) for i in range(2)]
            k = 0
            for n in range(NT):
                cols = slice(n * 128, (n + 1) * 128)
                i = n % 2
                kb.LD(uTb[i][:], kb.ZF[Z_SU:Z_SU + 256, cols].rearrange("(gg p) t -> p gg t", p=128), [uTb[i]])
                for gg in range(2):
                    kb.STT(yy[i][:, gg, :], uTb[i][:, gg, :], dsk[:, gg:gg + 1], OACC[:, gg, cols], ALU.mult, ALU.add,
                           [uTb[i], dsk, OACC.s(n)], [yy[i]])
                kb.TT(x2[i][:], yy[i][:], yy[i][:], ALU.mult, [yy[i]], [x2[i]], eng="pool")
                kb.TS(x2[i][:], x2[i][:], 0.044715, 1.0, ALU.mult, ALU.add, [x2[i]], [x2[i]])
                kb.TT(x2[i][:], x2[i][:], yy[i][:], ALU.mult, [x2[i], yy[i]], [x2[i]], eng="pool")
                kb.ACT(th[i][:], x2[i][:], AF.Tanh, [x2[i]], [th[i]], scale=0.7978845608028654)
                kb.TS(th[i][:], th[i][:], 1.0, 0.5, ALU.add, ALU.mult, [th[i]], [th[i]])
                kb.TT(yy[i][:], yy[i][:], th[i][:], ALU.mult, [yy[i], th[i]], [yy[i]], eng="pool")
                for ot in range(2):
                    q = k % 2; k += 1
                    for ct in range(2):
                        kb.MM(psz[q][:], gw[:, ct, ot * 128:(ot + 1) * 128], yy[i][:, ct, :], ct == 0, ct == 1, [gw, yy[i]], [psz[q]])
                    kb.ACT(sgb[q][:], psz[q][:], AF.Sigmoid, [psz[q], glb], [sgb[q]], bias=glb[:, ot:ot + 1])
                    kb.TT(ob[q][:], yy[i][:, ot, :], sgb[q][:], ALU.mult, [yy[i], sgb[q]], [ob[q]])
                    kb.ST(kb.YC[768 + ot * 128:768 + (ot + 1) * 128, cols], ob[q][:], [ob[q]])


def _conv_win_masks():
    tp = np.arange(642) - 65
    w = np.mod(tp, 64)
    m = np.ones((2, 642), np.float32)
    m[0, w == 63] = 0.0
    m[1, w == 0] = 0.0
    return np.broadcast_to(m[None], (128, 2, 642)).copy()


def gdn_conv2(kb, l):
    P = kb.P
    with P.scope():
        CW = P.sbuf("g_cw", [128, 6, 9])
        for kh in range(3):
            for kw in range(3):
                kb.LD(CW[:, :, kh * 3 + kw], kb.prm["gdn_conv_w"][l][kh, kw].rearrange("(ct p) -> p ct", p=128), [CW],
                      allow_slow_non_contiguous=True)
        DW = P.sbuf("g_dw", [128, 6, 9, 128], BF16)
        for ct in range(6):
            for tp_ in range(9):
                kb.TS(DW[:, ct, tp_, :], kb.C("IDENT"), CW[:, ct, tp_:tp_ + 1], None, ALU.mult, None, [CW], [DW],
                      eng=("pool" if tp_ % 2 else "dve"))
        wm = P.sbuf("g_wm", [128, 2, 642])
        kb.LD(wm[:], kb.cwin[:], [wm])
        Wb = [P.sbuf("g_w%d" % i, [128, 642]) for i in range(2)]
        WLb = [P.sbuf("g_wl%d" % i, [128, 642], BF16) for i in range(2)]
        WRb = [P.sbuf("g_wr%d" % i, [128, 642], BF16) for i in range(2)]
        WCb = [P.sbuf("g_wc%d" % i, [128, 642], BF16) for i in range(2)]
        sl = [P.sbuf("g_sl%d" % i, [128, 512]) for i in range(2)]
        sq = [P.sbuf("g_sq%d" % i, [128, 512]) for i in range(2)]
        rt = [P.sbuf("g_rt%d" % i, [128, 512]) for i in range(2)]
        psc = [P.psum("g_psc%d" % i, [128, 512]) for i in range(2)]
        ps = [P.psum("g_psn%d" % i, [128, 512]) for i in range(2)]
        spans = [(0, 256, True)] + [(256 + 512 * k, 512, False) for k in range(8)]
        it = 0
        for (t0, L, is_ctx) in spans:
            lo = 0 if is_ctx else 256
            hi = 256 if is_ctx else S
            a = max(lo, t0 - 65); b = min(hi, t0 + L + 65)
            for ct in range(6):
                i = it % 2; it += 1
                W = Wb[i]
                full = (a == t0 - 65) and (b == t0 + L + 65) and L == 512
                if not full:
                    kb.MS(W[:], 0.0, [W], eng="pool")
                kb.LD(W[:, 65 + (a - t0):65 + (b - t0)], kb.ZF[Z_GQKV + ct * 128:Z_GQKV + (ct + 1) * 128, a:b], [W])
                WC = WCb[i]
                kb.CP(WC[:], W[:], [W], [WC], eng="act")
                if is_ctx:
                    WL = WR = WC
                    rows = (1,)
                else:
                    WL, WR = WLb[i], WRb[i]
                    kb.TT(WL[:], W[:], wm[:, 0, :], ALU.mult, [W, wm], [WL])
                    kb.TT(WR[:], W[:], wm[:, 1, :], ALU.mult, [W, wm], [WR])
                    rows = (0, 1, 2)
                pc = psc[i]
                taps = [(dh, dwi) for dh in rows for dwi in range(3)]
                for q, (dh, dwi) in enumerate(taps):
                    srcT = (WL, WC, WR)[dwi]
                    o0 = 65 + 64 * (dh - 1) + (dwi - 1)
                    kb.MM(pc[:, :L], DW[:, ct, dh * 3 + dwi, :], srcT[:, o0:o0 + L], q == 0, q == len(taps) - 1, [DW, srcT], [pc])
                kb.ACT(sl[i][:, :L], pc[:, :L], AF.Silu, [pc], [sl[i]])
                if ct < 4:
                    kb.TT(sq[i][:, :L], sl[i][:, :L], sl[i][:, :L], ALU.mult, [sl[i]], [sq[i]])
                    kb.MM(ps[i][:, :L], kb.C("BLK64"), sq[i][:, :L], True, True, [sq[i]], [ps[i]])
                    kb.ACT(rt[i][:, :L], ps[i][:, :L], AF.Sqrt, [ps[i]], [rt[i]], bias=kb.C("CCOL")[:, 0:1])
                    kb.RECIP(rt[i][:, :L], rt[i][:, :L], [rt[i]], [rt[i]])
                    if ct < 2:
                        kb.STT(sl[i][:, :L], sl[i][:, :L], 0.125, rt[i][:, :L], ALU.mult, ALU.mult, [sl[i], rt[i]], [sl[i]])
                    else:
                        kb.TT(sl[i][:, :L], sl[i][:, :L], rt[i][:, :L], ALU.mult, [sl[i], rt[i]], [sl[i]], eng="pool")
                kb.ST(kb.QKVF[ct * 128:(ct + 1) * 128, t0:t0 + L], sl[i][:, :L], [sl[i]])
```

```python
import numpy as np
import concourse.bass as bass
import concourse.mybir as mybir
from concourse.bass_utils import run_bass_kernel_spmd
from contextlib import ExitStack

F32 = mybir.dt.float32
BF16 = mybir.dt.bfloat16
AF = mybir.ActivationFunctionType
ALU = mybir.AluOpType

ENGS = ("pe", "act", "dve", "pool", "sp")
EPOCH = 16000
N_DMA_SEM = 32


class Buf:
    __slots__ = ("name", "w", "r", "excl", "pe_partial")

    def __init__(self, name="", excl=False):
        self.name = name
        self.w = None
        self.r = []
        self.excl = excl
        self.pe_partial = False


class T:
    def __init__(self, h, name, excl=False):
        self.h = h
        self.name = name
        self.b = Buf(name, excl)
        self.excl = excl
        self.subs = {}

    def __getitem__(self, k):
        return self.h[k]

    def s(self, key):
        if self.excl:
            return self.b
        if key not in self.subs:
            self.subs[key] = Buf("%s.%s" % (self.name, key))
        return self.subs[key]


class Prog:
    def __init__(self, nc):
        self.nc = nc
        self.es = ExitStack()
        self.stack = [self.es]
        self.ops = {e: [] for e in ENGS}
        self.cnt = {e: 0 for e in ENGS}
        self.seen = {e: {} for e in ENGS}
        self.last = {}
        self.dma_k = 0
        self.dma_use = [0] * N_DMA_SEM
        self.dma_sems = [self.es.enter_context(nc.semaphore("dq%d" % i)) for i in range(N_DMA_SEM)]
        self.eng_sems = {}
        self.out_tokens = []
        self.n_ops = 0
        self.uid = 0

    def _nm(self, name):
        self.uid += 1
        return "%s_%d" % (name, self.uid)

    def sbuf(self, name, shape, dt=F32):
        h = self.stack[-1].enter_context(self.nc.sbuf_tensor(self._nm(name), list(shape), dt))
        return T(h, name)

    def psum(self, name, shape, dt=F32):
        n = 1
        for d_ in shape[1:]:
            n *= d_
        nb = (n * 4 + 2047) // 2048
        h = self.stack[-1].enter_context(self.nc.psum_tensor(self._nm(name), [128, nb * 512], F32))
        v = h[0:shape[0], 0:n]
        if len(shape) == 3:
            v = v.rearrange("p (a b) -> p a b", a=shape[1])
        elif len(shape) == 4:
            v = v.rearrange("p (a b c) -> p a b c", a=shape[1], b=shape[2])
        return T(v, name, excl=True)

    def dram(self, name, shape, dt=F32, kind="Internal"):
        h = self.nc.dram_tensor(name, list(shape), dt, kind=kind)
        return T(h.ap(), name)

    class _Scope:
        def __init__(self, p):
            self.p = p

        def __enter__(self):
            st = ExitStack()
            self.p.stack.append(st)
            return st

        def __exit__(self, *a):
            self.p.barrier()
            st = self.p.stack.pop()
            st.close()
            return False

    def scope(self):
        return Prog._Scope(self)

    def _eng_sem(self, e, epoch):
        k = (e, epoch)
        if k not in self.eng_sems:
            self.eng_sems[k] = self.es.enter_context(self.nc.semaphore("s_%s_%d" % (e, epoch)))
        return self.eng_sems[k]

    def _waits(self, eng, reads, writes, extra=(), skip_pe=False):
        need = {}

        def add(tok):
            if tok is None:
                return
            key, val = tok
            if need.get(key, 0) < val:
                need[key] = val
        for b in reads:
            add(b.w)
        for b in writes:
            add(b.w)
            for t in b.r:
                add(t)
        for t in extra:
            add(t)
        out = []
        seen = self.seen[eng]
        for key, val in need.items():
            if skip_pe and key[0] == "e" and key[1] == "pe":
                continue
            if seen.get(key, 0) < val:
                seen[key] = val
                out.append((key, val))
        return out

    @staticmethod
    def _bufs(xs):
        out = []
        for x in xs:
            if x is None:
                continue
            out.append(x.b if isinstance(x, T) else x)
        return out

    def _commit(self, tok, reads, writes):
        self.last[tok[0]] = tok[1]
        for b in reads:
            b.r.append(tok)
            if len(b.r) > 64:
                mx = {}
                for k, v in b.r:
                    if mx.get(k, 0) < v:
                        mx[k] = v
                b.r = list(mx.items())
        for b in writes:
            b.w = tok
            b.r = []
        self.n_ops += 1

    def op(self, eng, fn, reads=(), writes=(), partial=False):
        reads = self._bufs(reads)
        writes = self._bufs(writes)
        ex = [b for b in reads if b.excl]
        if ex:
            reads = [b for b in reads if not b.excl]
            writes = writes + [b for b in ex if b not in writes]
        skip_pe = False
        if eng == "pe":
            skip_pe = (not partial) and all(not b.pe_partial for b in writes)
            for b in writes:
                b.pe_partial = partial
        waits = self._waits(eng, reads, writes, skip_pe=skip_pe)
        self.cnt[eng] += 1
        epoch, val = divmod(self.cnt[eng] - 1, EPOCH)
        tok = (("e", eng, epoch), val + 1)
        self.ops[eng].append((waits, fn, tok))
        self._commit(tok, reads, writes)
        return tok

    def dma(self, out_ap, in_ap, reads=(), writes=(), q="sp", is_output=False, **kw):
        reads = self._bufs(reads)
        writes = self._bufs(writes)
        i = self.dma_k % N_DMA_SEM
        self.dma_k += 1
        prev = self.dma_use[i]
        extra = [(("d", i), 16 * prev)] if prev else []
        waits = self._waits(q, reads, writes, extra)
        self.dma_use[i] = prev + 1
        tok = (("d", i), 16 * (prev + 1))

        def fn(e):
            return e.dma_start(out=out_ap, in_=in_ap, **kw)
        self.ops[q].append((waits, fn, tok))
        self._commit(tok, reads, writes)
        if is_output:
            self.out_tokens.append(tok)
        return tok

    def barrier(self):
        toks = list(self.last.items())
        for e in ENGS:
            waits = self._waits(e, [], [], toks)
            if waits:
                self.ops[e].append((waits, None, None))

    def _sem_of(self, key):
        if key[0] == "d":
            return self.dma_sems[key[1]]
        return self._eng_sem(key[1], key[2])

    def emit(self):
        nc = self.nc
        self.barrier()
        for e in ENGS:
            for waits, fn, tok in self.ops[e]:
                if tok is not None:
                    self._sem_of(tok[0])
                for key, val in waits:
                    self._sem_of(key)
        with nc.Block() as block:
            def run(e, handle):
                for waits, fn, tok in self.ops[e]:
                    for key, val in waits:
                        handle.wait_ge(self._sem_of(key), val)
                    if fn is None:
                        continue
                    ins = fn(handle)
                    key, val = tok
                    ins.then_inc(self._sem_of(key), 16 if key[0] == "d" else 1)

            @block.sync
            def _(h):
                run("sp", h)

            @block.tensor
            def _(h):
                run("pe", h)

            @block.scalar
            def _(h):
                run("act", h)

            @block.vector
            def _(h):
                run("dve", h)

            @block.gpsimd
            def _(h):
                run("pool", h)

    def close(self):
        self.es.close()


D = 1024
S = 4352
NT = 34
LAT0 = 256
DEPTH = 2
EPS = 1e-6
NEG = -30000.0
ORDER = [list(range(NT)), [1, 0] + list(range(NT - 1, 1, -1))]

C_HQ, C_HI, C_HG, C_HFF, C_HFB = 0, 256, 512, 768, 1024
C_RQ, C_RK, C_RV, C_RG = 1280, 1536, 1792, 2048
C_GQKV, C_GG, C_GA, C_GB, C_SU = 2304, 3072, 3328, 3336, 3344
Z_HQ, Z_HG, Z_HFF, Z_HFB, Z_RQ, Z_RK, Z_RG, Z_GQKV, Z_GG, Z_SU = 0, 256, 512, 768, 1024, 1280, 1536, 1792, 2560, 2816
NZF = 3072
FM_MAP = [(Z_HQ, C_HQ, 256), (Z_HG, C_HG, 256), (Z_HFF, C_HFF, 256), (Z_HFB, C_HFB, 256), (Z_RQ, C_RQ, 256),
          (Z_RK, C_RK, 256), (Z_RG, C_RG, 256), (Z_GQKV, C_GQKV, 768), (Z_GG, C_GG, 256), (Z_SU, C_SU, 256)]
FM_BLOCKS = [(zr + i, wc + i) for zr, wc, n in FM_MAP for i in range(0, n, 128)]
NZT = 528

CN = {}


def _const_pack():
    mats = []

    def add(name, m):
        CN[name] = len(mats)
        mats.append(np.asarray(m, np.float32))
    p = np.arange(128)[:, None]
    f = np.arange(128)[None, :]
    add("IDENT", (p == f))
    add("ONES", np.ones((128, 128)))
    add("TRIF", (p <= f))
    add("TRIB", (p >= f))
    add("SUFF", (p > f))
    add("PREB", (p < f))
    add("NLE", np.where(p <= f, 0.0, NEG))
    add("NLT", np.where(p < f, 0.0, NEG))
    add("NGE", np.where(p >= f, 0.0, NEG))
    add("NGT", np.where(p > f, 0.0, NEG))
    for s in (1, 2, 4, 8, 16, 32, 64):
        m = (((p // s) % 2) == 1) & ((f // s) == (p // s) - 1)
        add("MOFF%d" % s, m)
        add("MOFFT%d" % s, m.T)
    add("BLK64", (p // 64) == (f // 64))
    rot = np.zeros((128, 128))
    for m in range(128):
        if (m % 64) < 32:
            rot[m + 32, m] = -1.0
        else:
            rot[m - 32, m] = 1.0
    add("ROT", rot)
    add("IOTAF", np.broadcast_to(f, (128, 128)))
    add("IOTAF1", np.broadcast_to(f + 1, (128, 128)))
    add("RIOTAF", np.broadcast_to(128 - f, (128, 128)))
    add("R127F", np.broadcast_to(127 - f, (128, 128)))
    add("DIFF", f - p)
    add("NDIFF", p - f)
    for h in range(4):
        m = np.zeros((128, 128)); m[h, :] = 1.0
        add("SELH%d" % h, m)
    for hp in range(2):
        m = np.zeros((128, 128)); m[2 * hp, 0:64] = 1.0; m[2 * hp + 1, 64:128] = 1.0
        add("SELP%d" % hp, m)
    cc = np.zeros((128, 128))
    cc[:, 0] = EPS; cc[:, 1] = 1.0; cc[:, 2] = np.arange(128); cc[:, 3] = 127 - np.arange(128)
    cc[:, 5] = -np.pi; cc[:, 6] = -np.arange(128); cc[:, 7] = -(127 - np.arange(128))
    add("CCOL", cc)
    gm = np.zeros((128, 128))
    for g in range(16):
        gm[(g % 8) * 16:(g % 8) * 16 + 16, g] = 1.0
    add("GMASK", gm)
    return np.concatenate(mats, axis=1)


CONST_NP = _const_pack()
NCONST = CONST_NP.shape[1] // 128


def _rope_tables():
    half = 32
    inv = 10000.0 ** (-np.arange(half, dtype=np.float64) / half)
    pos = np.arange(S, dtype=np.float64)
    ang = pos[None, :] * inv[:, None]
    cos = np.cos(ang); sin = np.sin(ang)
    cos128 = np.tile(cos, (4, 1)); sin128 = np.tile(sin, (4, 1))
    return cos128.astype(np.float32), sin128.astype(np.float32)


def _conv_masks():
    m = np.ones((2, 512), np.float32)
    w = np.arange(512) % 64
    m[0, w == 0] = 0.0
    m[1, w == 63] = 0.0
    lat = np.broadcast_to(m[None], (128, 2, 512)).copy()
    c = np.ones((2, 256), np.float32)
    c[0, 0] = 0.0
    c[1, 255] = 0.0
    ctx = np.broadcast_to(c[None], (128, 2, 256)).copy()
    return lat, ctx


class KB:
    def __init__(self, cfg):
        self.cfg = cfg
        nc = bass.Bass("TRN2", target_bir_lowering=False)
        self.nc = nc
        self.P = Prog(nc)
        self.rr = 0

    def MM(self, ps, lhsT, rhs, st, sp, R, W):
        partial = lhsT.partition_size() < 128
        self.P.op("pe", lambda e: e.matmul(ps, lhsT, rhs, start=st, stop=sp), R, W, partial=partial)

    def TR(self, ps, in_, ident, R, W):
        self.P.op("pe", lambda e: e.transpose(ps, in_, ident), R, W)

    def ACT(self, out, in_, func, R, W, **kw):
        self.P.op("act", lambda e: e.activation(out=out, in_=in_, func=func, **kw), R, W)

    def TS(self, out, in0, s1, s2, op0, op1, R, W, eng="dve"):
        if s2 is None:
            self.P.op(eng, lambda e: e.tensor_scalar(out=out, in0=in0, scalar1=s1, scalar2=None, op0=op0), R, W)
        else:
            self.P.op(eng, lambda e: e.tensor_scalar(out=out, in0=in0, scalar1=s1, scalar2=s2, op0=op0, op1=op1), R, W)

    def TT(self, out, in0, in1, op, R, W, eng="dve"):
        self.P.op(eng, lambda e: e.tensor_tensor(out=out, in0=in0, in1=in1, op=op), R, W)

    def STT(self, out, in0, sc, in1, op0, op1, R, W, eng="dve"):
        eng = "dve"
        self.P.op(eng, lambda e: e.scalar_tensor_tensor(out=out, in0=in0, scalar=sc, in1=in1, op0=op0, op1=op1), R, W)

    def CP(self, out, in_, R, W, eng="dve"):
        if eng == "act":
            self.ACT(out, in_, AF.Copy, R, W)
        else:
            self.P.op(eng, lambda e: e.tensor_copy(out=out, in_=in_), R, W)

    def CPRED(self, out, mask, data, R, W):
        self.P.op("dve", lambda e: e.copy_predicated(out=out, mask=mask, data=data), R, W)

    def MS(self, ap, val, W, eng="dve"):
        self.P.op(eng, lambda e: e.memset(ap, val), (), W)

    def RECIP(self, out, in_, R, W):
        self.P.op("dve", lambda e: e.reciprocal(out=out, in_=in_), R, W)

    def SCAN(self, out, d0, d1, R, W):
        self.P.op("dve", lambda e: e.tensor_tensor_scan(out=out, data0=d0, data1=d1, initial=0.0,
                                                        op0=ALU.mult, op1=ALU.add), R, W)

    def LD(self, out, in_, W, R=(), q="sp", **kw):
        self.P.dma(out, in_, reads=R, writes=W, q=q, **kw)

    def ST(self, out, in_, R, W=(), q="pool", **kw):
        self.P.dma(out, in_, reads=R, writes=W, q=q, **kw)

    def evac_eng(self):
        self.rr += 1
        return "act" if self.rr % 2 else "dve"

    def C(self, name):
        i = CN[name]
        return self.const[:, i * 128:(i + 1) * 128]


PARAM_SHAPES = {
    "mod_w": [2, 1024, 6144], "mod_b": [2, 6144], "norm1_g": [2, 1024], "norm2_g": [2, 1024],
    "w_in": [2, 1024, 3600], "hgrn_lb_logits": [2, 2, 256], "hgrn_norm_g": [2, 64],
    "ret_decay_logit": [2, 2, 4], "gdn_conv_w": [2, 3, 3, 768], "gdn_a_log": [2, 2, 4],
    "gdn_dt_bias": [2, 2, 4], "gdn_norm_g": [2, 64], "s5_lam_re": [2, 2, 16, 64],
    "s5_lam_im": [2, 2, 16, 64], "s5_log_dt": [2, 2, 16], "s5_b_re": [2, 16, 64, 16],
    "s5_b_im": [2, 16, 64, 16], "s5_c_re": [2, 16, 16, 64], "s5_c_im": [2, 16, 16, 64],
    "s5_d": [2, 256], "s5_glu_w": [2, 256, 256], "s5_glu_b": [2, 256], "w_out": [2, 1024, 1024],
    "mlp_w1": [2, 1024, 4096], "mlp_w2": [2, 4096, 1024], "final_norm_g": [1024],
}


def declare(kb):
    P = kb.P
    cfg = kb.cfg
    kinds = cfg.get("kinds", {})
    kb.xin = P.dram("xin", [S, D], F32, kind="ExternalInput")
    kb.cvecT = P.dram("cvecT", [1024, 2], F32, kind="ExternalInput")
    kb.prm = {k: P.dram(k, shp, F32, kind="ExternalInput") for k, shp in PARAM_SHAPES.items()}
    kb.constd = P.dram("constp", [128, NCONST * 128], F32, kind="ExternalInput")
    kb.ropec = P.dram("ropec", [128, S], F32, kind="ExternalInput")
    kb.ropes = P.dram("ropes", [128, S], F32, kind="ExternalInput")
    kb.cmlat = P.dram("cmlat", [128, 2, 512], F32, kind="ExternalInput")
    kb.cmctx = P.dram("cmctx", [128, 2, 256], F32, kind="ExternalInput")
    kb.cwin = P.dram("cwin", [128, 2, 642], F32, kind="ExternalInput")
    kb.y = P.dram("y", [4096, D], F32, kind="ExternalOutput")
    kb.XS = P.dram("XS", [S, D], F32, kind=kinds.get("XS", "Internal"))
    kb.ZF = P.dram("ZF", [NZF, S], F32, kind=kinds.get("ZF", "Internal"))
    kb.ZT = P.dram("ZT", [S, NZT], F32, kind=kinds.get("ZT", "Internal"))
    kb.QKVF = P.dram("QKVF", [768, S], F32, kind=kinds.get("QKVF", "Internal"))
    kb.YC = P.dram("YC", [1024, S], F32, kind=kinds.get("YC", "Internal"))
    kb.H2T = P.dram("H2T", [1024, S], BF16, kind=kinds.get("H2T", "Internal"))
    kb.const = P.sbuf("const", [128, NCONST * 128])
    nchunk = 4
    w = NCONST * 128 // nchunk
    for i in range(nchunk):
        a, b = i * w, (i + 1) * w if i < nchunk - 1 else NCONST * 128
        kb.LD(kb.const[:, a:b], kb.constd[:, a:b], [kb.const.s(i)])
    kb.const_bufs = [kb.const.s(i) for i in range(nchunk)]
    kb.CB = kb.const_bufs
    kb.GS1 = P.sbuf("GS1", [128, 8, 2]); kb.SH1 = P.sbuf("SH1", [128, 8, 2])
    kb.GS2 = P.sbuf("GS2", [128, 8, 2]); kb.SH2 = P.sbuf("SH2", [128, 8, 2])
    kb.GATE1 = P.sbuf("GATE1", [128, 2, 1024]); kb.GATE2 = P.sbuf("GATE2", [128, 2, 1024])


def phase_mod(kb, l):
    P = kb.P
    prm = kb.prm
    with P.scope():
        cT = P.sbuf("cT", [128, 8, 2])
        kb.LD(cT[:], kb.cvecT[:].rearrange("(et e) c -> e et c", e=128), [cT])
        sc = P.sbuf("sc", [128, 8, 2])
        kb.ACT(sc[:], cT[:], AF.Silu, [cT], [sc])
        screp = P.sbuf("screp", [128, 8, 2, 128])
        kb.CP(screp[:], sc[:].unsqueeze(3).to_broadcast([128, 8, 2, 128]), [sc], [screp])
        mbf = P.sbuf("mbf", [128, 48])
        kb.LD(mbf[:], prm["mod_b"][l].rearrange("(j p) -> p j", p=128), [mbf], allow_slow_non_contiguous=True)
        ngf = P.sbuf("ngf", [128, 2, 8])
        kb.LD(ngf[:, 0, :], prm["norm1_g"][l].rearrange("(j p) -> p j", p=128), [ngf], allow_slow_non_contiguous=True)
        kb.LD(ngf[:, 1, :], prm["norm2_g"][l].rearrange("(j p) -> p j", p=128), [ngf], allow_slow_non_contiguous=True)
        mbrow = P.sbuf("mbrow", [128, 2, 1024])
        for gi, v in enumerate((2, 5)):
            kb.LD(mbrow[:, gi, :], prm["mod_b"][l][v * 1024:(v + 1) * 1024].partition_broadcast(128), [mbrow])
        wch = [P.sbuf("wch%d" % i, [128, 8, 1024]) for i in range(2)]
        ps_fm = P.psum("ps_fm", [128, 96])
        ps_g = [P.psum("ps_g%d" % i, [128, 512]) for i in range(2)]
        MF = P.sbuf("MF", [128, 48, 2])
        k = 0
        for v in range(6):
            wc = wch[v % 2]
            for et in range(8):
                kb.LD(wc[:, et, :], prm["mod_w"][l][et * 128:(et + 1) * 128, v * 1024:(v + 1) * 1024], [wc])
            for db in range(8):
                col = (v * 8 + db) * 2
                for et in range(8):
                    kb.MM(ps_fm[:, col:col + 2], wc[:, et, db * 128:(db + 1) * 128], sc[:, et, :],
                          et == 0, et == 7, [wc, sc], [ps_fm])
            if v in (2, 5):
                gt = kb.GATE1 if v == 2 else kb.GATE2
                gi = 0 if v == 2 else 1
                for which in range(2):
                    for half in range(2):
                        pg = ps_g[k % 2]; k += 1
                        for et in range(8):
                            kb.MM(pg[:], screp[:, et, which, :], wc[:, et, half * 512:(half + 1) * 512],
                                  et == 0, et == 7, [screp, wc], [pg])
                        kb.TT(gt[:, which, half * 512:(half + 1) * 512], pg[:], mbrow[:, gi, half * 512:(half + 1) * 512],
                              ALU.add, [pg, mbrow], [gt])
        kb.TT(MF[:], ps_fm[:].rearrange("p (j c) -> p j c", c=2), mbf[:].unsqueeze(2).to_broadcast([128, 48, 2]),
              ALU.add, [ps_fm, mbf], [MF])
        tmp = P.sbuf("mtmp", [128, 8, 2])
        kb.TS(tmp[:], MF[:, 8:16, :], 1.0, None, ALU.add, None, [MF], [tmp])
        kb.TT(kb.GS1[:], tmp[:], ngf[:, 0, :].unsqueeze(2).to_broadcast([128, 8, 2]), ALU.mult, [tmp, ngf], [kb.GS1])
        kb.CP(kb.SH1[:], MF[:, 0:8, :], [MF], [kb.SH1])
        tmp2 = P.sbuf("mtmp2", [128, 8, 2])
        kb.TS(tmp2[:], MF[:, 32:40, :], 1.0, None, ALU.add, None, [MF], [tmp2])
        kb.TT(kb.GS2[:], tmp2[:], ngf[:, 1, :].unsqueeze(2).to_broadcast([128, 8, 2]), ALU.mult, [tmp2, ngf], [kb.GS2])
        kb.CP(kb.SH2[:], MF[:, 24:32, :], [MF], [kb.SH2])


def norm_to_fm(kb, xt, hT, col0, GS, SH, which, bufs, R_x):
    P = kb.P
    junk, st, xn, ps_ts = bufs["junk"], bufs["st"], bufs["xn"], bufs["ps_t"]
    kb.MS(st[:, 0:1], 0.0, [st])
    kb.ACT(junk[:], xt[:], AF.Square, [xt], [junk, st], accum_out=st[:, 0:1])
    kb.ACT(st[:, 1:2], st[:, 0:1], AF.Sqrt, [st] + kb.CB, [st], scale=1.0 / D, bias=kb.C("CCOL")[:, 0:1])
    kb.RECIP(st[:, 2:3], st[:, 1:2], [st], [st])
    kb.ACT(xn[:], xt[:], AF.Copy, [xt, st], [xn], scale=st[:, 2:3])
    for half in range(2):
        ps_t = ps_ts[half]
        for q in range(4):
            dt = half * 4 + q
            kb.TR(ps_t[:, q * 128:(q + 1) * 128], xn[:, dt * 128:(dt + 1) * 128], kb.C("IDENT"), [xn] + kb.CB, [ps_t])
        for q in range(4):
            dt = half * 4 + q
            if q % 2 == 0:
                kb.TS(hT[:, dt, col0:col0 + 128], ps_t[:, q * 128:(q + 1) * 128], GS[:, dt, which:which + 1],
                      SH[:, dt, which:which + 1], ALU.mult, ALU.add, [ps_t, GS, SH], [hT])
            else:
                kb.ACT(hT[:, dt, col0:col0 + 128], ps_t[:, q * 128:(q + 1) * 128], AF.Identity, [ps_t, GS, SH], [hT],
                       scale=GS[:, dt, which:which + 1], bias=SH[:, dt, which:which + 1])


def norm_to_fm_g(kb, xt, hT, col0, GS, SH, which, bufs):
    junk, st, xn, ps_ts = bufs["junk"], bufs["st"], bufs["xn"], bufs["ps_t"]
    kb.MS(st[:, 0:1], 0.0, [st])
    kb.ACT(junk[:], xt[:], AF.Square, [xt], [junk, st], accum_out=st[:, 0:1])
    yield
    kb.ACT(st[:, 1:2], st[:, 0:1], AF.Sqrt, [st], [st], scale=1.0 / D, bias=kb.C("CCOL")[:, 0:1])
    kb.RECIP(st[:, 2:3], st[:, 1:2], [st], [st])
    yield
    kb.ACT(xn[:], xt[:], AF.Copy, [xt, st], [xn], scale=st[:, 2:3])
    yield
    for half in range(2):
        ps_t = ps_ts[half]
        for q in range(4):
            dt = half * 4 + q
            kb.TR(ps_t[:, q * 128:(q + 1) * 128], xn[:, dt * 128:(dt + 1) * 128], kb.C("IDENT"), [xn], [ps_t])
        yield
        for q in range(4):
            dt = half * 4 + q
            if q % 2 == 0:
                kb.TS(hT[:, dt, col0:col0 + 128], ps_t[:, q * 128:(q + 1) * 128], GS[:, dt, which:which + 1],
                      SH[:, dt, which:which + 1], ALU.mult, ALU.add, [ps_t, GS, SH], [hT])
            else:
                kb.ACT(hT[:, dt, col0:col0 + 128], ps_t[:, q * 128:(q + 1) * 128], AF.Identity, [ps_t, GS, SH], [hT],
                       scale=GS[:, dt, which:which + 1], bias=SH[:, dt, which:which + 1])
        yield


def phase_a(kb, l, src):
    P = kb.P
    with P.scope():
        win = P.sbuf("win", [128, 8, 3600], BF16)
        for kt in range(8):
            kb.LD(win[:, kt, :], kb.prm["w_in"][l][kt * 128:(kt + 1) * 128, :], [win.s(kt)], q="pool")
        winb = [win.s(kt) for kt in range(8)]
        xbuf = [P.sbuf("xa%d" % i, [128, 1024]) for i in range(2)]
        hTb = [P.sbuf("hTa%d" % i, [128, 8, 512], BF16) for i in range(2)]
        nb = {"junk": P.sbuf("junk", [128, 1024]), "st": P.sbuf("st", [128, 4]), "xn": P.sbuf("xn", [128, 1024]),
              "ps_t": [P.psum("ps_t%d" % i, [128, 512]) for i in range(2)]}
        ps_f = [P.psum("ps_f%d" % i, [128, 512]) for i in range(3)]
        ps_a = [P.psum("ps_a%d" % i, [128, 512]) for i in range(2)]
        ps_b = P.psum("ps_b", [128, 16])
        stg = [P.sbuf("stg%d" % i, [128, 512]) for i in range(4)]
        stt = [P.sbuf("stt%d" % i, [128, NZT]) for i in range(2)]
        kx = kf = ks = ka = 0
        for gi, t0 in enumerate(range(0, S, 512)):
            n = min(512, S - t0)
            hT = hTb[gi % 2]
            for ti in range(n // 128):
                tt = t0 // 128 + ti
                which = 1 if tt < 2 else 0
                xt = xbuf[kx % 2]; kx += 1
                kb.LD(xt[:], src[tt * 128:(tt + 1) * 128, :], [xt])
                norm_to_fm(kb, xt, hT, ti * 128, kb.GS1, kb.SH1, which, nb, None)
            for (zr, wc) in FM_BLOCKS:
                ps = ps_f[kf % 3]; kf += 1
                for kt in range(8):
                    kb.MM(ps[:, :n], win[:, kt, wc:wc + 128], hT[:, kt, :n], kt == 0, kt == 7, [winb[kt], hT], [ps])
                sg = stg[ks % 4]; ks += 1
                kb.CP(sg[:, :n], ps[:, :n], [ps], [sg], eng=kb.evac_eng())
                kb.ST(kb.ZF[zr:zr + 128, t0:t0 + n], sg[:, :n], [sg])
            for ti in range(n // 128):
                tt = t0 // 128 + ti
                pa = ps_a[ka % 2]
                so = stt[ka % 2]; ka += 1
                for (c0, w0, wn) in ((0, C_HI, 256), (256, C_RV, 256)):
                    for kt in range(8):
                        kb.MM(pa[:, c0:c0 + wn], hT[:, kt, ti * 128:(ti + 1) * 128], win[:, kt, w0:w0 + wn],
                              kt == 0, kt == 7, [winb[kt], hT], [pa])
                for kt in range(8):
                    kb.MM(ps_b[:], hT[:, kt, ti * 128:(ti + 1) * 128], win[:, kt, C_GA:C_GA + 16],
                          kt == 0, kt == 7, [winb[kt], hT], [ps_b])
                kb.CP(so[:, 0:512], pa[:], [pa], [so], eng="act")
                kb.CP(so[:, 512:528], ps_b[:], [ps_b], [so], eng="dve")
                kb.ST(kb.ZT[tt * 128:(tt + 1) * 128, :], so[:], [so])


def build(cfg):
    kb = KB(cfg)
    P = kb.P
    declare(kb)
    P.barrier()
    stages = cfg.get("stages", "all")
    for l in cfg.get("layers", range(DEPTH)):
        src = kb.xin if l == 0 else kb.XS
        if stages == "all" or "M" in stages:
            phase_mod(kb, l)
        if stages == "all" or "A" in stages:
            phase_a(kb, l, src)
        if stages == "all" or "R" in stages:
            (mixer_ret if cfg.get("ret_old") else mixer_ret2)(kb, l)
        if stages == "all" or "H" in stages:
            (mixer_hgrn if cfg.get("hgrn_old") else mixer_hgrn2)(kb, l)
        if stages == "all" or "G" in stages:
            (mixer_gdn if cfg.get("gdn_old") else mixer_gdn2)(kb, l)
        if stages == "all" or "S" in stages:
            (mixer_s5 if cfg.get("s5_old") else mixer_s5_2)(kb, l)
        if stages == "all" or "C" in stages:
            phase_c(kb, l, src)
    P.emit()
    P.close()
    return kb


_CONSTS = None


def host_inputs(inputs, cores=range(8)):
    global _CONSTS
    if _CONSTS is None:
        rc, rs = _rope_tables()
        cl, cc = _conv_masks()
        _CONSTS = {"constp": CONST_NP, "ropec": rc, "ropes": rs, "cmlat": cl, "cmctx": cc, "cwin": _conv_win_masks()}
    maps = []
    for b in cores:
        m = {"xin": np.ascontiguousarray(np.concatenate([inputs["ctx"][b], inputs["x"][b]], axis=0), dtype=np.float32),
             "cvecT": np.ascontiguousarray(np.stack([inputs["c"][b], inputs["c_ctx"]], axis=1), dtype=np.float32)}
        for k in PARAM_SHAPES:
            m[k] = np.ascontiguousarray(inputs[k], dtype=np.float32)
        m.update(_CONSTS)
        maps.append(m)
    return maps


def kernel(**inputs):
    inputs = {k: np.asarray(v) for k, v in inputs.items()}
    kb = build({})
    maps = host_inputs(inputs)
    res = run_bass_kernel_spmd(kb.nc, maps, core_ids=list(range(8)))
    out = np.stack([np.asarray(r["y"]).reshape(4096, D) for r in res.results], axis=0)
    return out.astype(np.float32)


def phase_c(kb, l, src):
    P = kb.P
    last = (l == DEPTH - 1)
    t_start = 2 if last else 0
    with P.scope():
        wout = P.sbuf("wout", [128, 8, 1024], BF16)
        for ft in range(8):
            kb.LD(wout[:, ft, :], kb.prm["w_out"][l][ft * 128:(ft + 1) * 128, :], [wout.s(ft)], q="pool")
        wb = [wout.s(ft) for ft in range(8)]

        def make(si):
            pf = "c%d_" % si
            yc = P.sbuf(pf + "yc", [128, 8, 128], BF16)
            xt = P.sbuf(pf + "x", [128, 1024]); x1 = P.sbuf(pf + "x1", [128, 1024])
            tmpb = [P.sbuf(pf + "t%d" % i, [128, 512]) for i in range(2)]
            h2 = P.sbuf(pf + "h2", [128, 8, 128], BF16)
            nb = {"junk": P.sbuf(pf + "junk", [128, 1024]), "st": P.sbuf(pf + "st", [128, 4]), "xn": P.sbuf(pf + "xn", [128, 1024]),
                  "ps_t": [P.psum(pf + "pst%d" % i, [128, 512]) for i in range(2)]}
            ps_y = [P.psum(pf + "psy%d" % i, [128, 512]) for i in range(2)]

            def gen():
                for tt in range(t_start + si, NT, 2):
                    which = 1 if tt < 2 else 0
                    cols = slice(tt * 128, (tt + 1) * 128)
                    kb.LD(yc[:], kb.YC[:, cols].rearrange("(ft p) t -> p ft t", p=128), [yc], q="pool")
                    kb.LD(xt[:], src[cols, :], [xt])
                    yield
                    for half in range(2):
                        ps = ps_y[half]
                        for ft in range(8):
                            kb.MM(ps[:], yc[:, ft, :], wout[:, ft, half * 512:(half + 1) * 512], ft == 0, ft == 7,
                                  [yc, wb[ft]], [ps])
                        yield
                    for half in range(2):
                        ps = ps_y[half]
                        tm = tmpb[half]
                        kb.TT(tm[:], ps[:], kb.GATE1[:, which, half * 512:(half + 1) * 512], ALU.mult, [ps, kb.GATE1], [tm])
                        kb.TT(x1[:, half * 512:(half + 1) * 512], xt[:, half * 512:(half + 1) * 512], tm[:], ALU.add,
                              [xt, tm], [x1], eng="pool")
                        yield
                    kb.ST(kb.XS[cols, :], x1[:], [x1])
                    yield from norm_to_fm_g(kb, x1, h2, 0, kb.GS2, kb.SH2, which, nb)
                    kb.ST(kb.H2T[:, cols].rearrange("(dt p) t -> p dt t", p=128), h2[:], [h2])
                    yield
            return gen()
        run_interleaved([make(0), make(1)])
    with P.scope():
        w1 = P.sbuf("w1", [128, 8, 4096], BF16)
        w2 = P.sbuf("w2", [128, 32, 1024], BF16)
        for kt in range(8):
            kb.LD(w1[:, kt, :], kb.prm["mlp_w1"][l][kt * 128:(kt + 1) * 128, :], [w1.s(kt)], q="pool")
        for fb in range(32):
            kb.LD(w2[:, fb, :], kb.prm["mlp_w2"][l][fb * 128:(fb + 1) * 128, :], [w2.s(fb)], q="pool")
        h2b = [P.sbuf("h2d%d" % i, [128, 8, 256], BF16) for i in range(2)]
        uTb = [P.sbuf("uT%d" % i, [128, 16, 256], BF16) for i in range(1)]
        rb = [P.sbuf("relu%d" % i, [128, 256]) for i in range(3)]
        xb = [P.sbuf("xd%d" % i, [128, 1024]) for i in range(2)]
        tmpb = [P.sbuf("td%d" % i, [128, 512]) for i in range(2)]
        ps_u = [P.psum("ps_u%d" % i, [128, 256]) for i in range(3)]
        ps_y = [P.psum("ps_y2%d" % i, [128, 512]) for i in range(4)]
        if last:
            fg = P.sbuf("fg", [128, 1024])
            kb.LD(fg[:], kb.prm["final_norm_g"][:].partition_broadcast(128), [fg])
            stf = P.sbuf("stf", [128, 4])
            xnf = P.sbuf("xnf", [128, 1024])
        k = 0; ku = 0
        for g0 in range(t_start, NT, 2):
            h2 = h2b[k % 2]; uT = uTb[0]
            cols = slice(g0 * 128, (g0 + 2) * 128)
            kb.LD(h2[:], kb.H2T[:, cols].rearrange("(dt p) t -> p dt t", p=128), [h2])
            for hh in range(2):
                for fl in range(16):
                    fb = hh * 16 + fl
                    ps = ps_u[ku % 3]; r = rb[ku % 3]; ku += 1
                    for kt in range(8):
                        kb.MM(ps[:], w1[:, kt, fb * 128:(fb + 1) * 128], h2[:, kt, :], kt == 0, kt == 7, [w1.s(kt), h2], [ps])
                    kb.ACT(r[:], ps[:], AF.Relu, [ps], [r])
                    kb.TT(uT[:, fl, :], r[:], r[:], ALU.mult, [r], [uT.s(fl)], eng=("dve" if fb % 2 else "pool"))
                for ti in range(2):
                    for half in range(2):
                        ps = ps_y[2 * ti + half]
                        for fl in range(16):
                            fb = hh * 16 + fl
                            kb.MM(ps[:], uT[:, fl, ti * 128:(ti + 1) * 128], w2[:, fb, half * 512:(half + 1) * 512],
                                  fb == 0, fb == 31, [uT.s(fl), w2.s(fb)], [ps])
            for ti in range(2):
                tt = g0 + ti
                which = 1 if tt < 2 else 0
                xt = xb[ti]
                rows = slice(tt * 128, (tt + 1) * 128)
                kb.LD(xt[:], kb.XS[rows, :], [xt])
                for half in range(2):
                    ps = ps_y[2 * ti + half]
                    tm = tmpb[half]
                    kb.TT(tm[:], ps[:], kb.GATE2[:, which, half * 512:(half + 1) * 512], ALU.mult, [ps, kb.GATE2], [tm])
                    kb.TT(xt[:, half * 512:(half + 1) * 512], xt[:, half * 512:(half + 1) * 512], tm[:], ALU.add,
                          [xt, tm], [xt], eng="pool")
                if not last:
                    kb.ST(kb.XS[rows, :], xt[:], [xt])
                else:
                    kb.MS(stf[:, 0:1], 0.0, [stf])
                    kb.ACT(xnf[:], xt[:], AF.Square, [xt], [xnf, stf], accum_out=stf[:, 0:1])
                    kb.ACT(stf[:, 1:2], stf[:, 0:1], AF.Sqrt, [stf], [stf], scale=1.0 / D, bias=kb.C("CCOL")[:, 0:1])
                    kb.RECIP(stf[:, 2:3], stf[:, 1:2], [stf], [stf])
                    kb.ACT(xnf[:], xt[:], AF.Copy, [xt, stf], [xnf], scale=stf[:, 2:3])
                    kb.TT(xnf[:], xnf[:], fg[:], ALU.mult, [xnf, fg], [xnf])
                    kb.P.dma(kb.y[(tt - 2) * 128:(tt - 1) * 128, :], xnf[:], reads=[xnf.b], q="pool", is_output=True)
            k += 1


def finalize_gated(kb, OACC, gate_row0, gain, yc_row0, pfx):
    P = kb.P
    def two(nm):
        return [P.sbuf(pfx + nm + "%d" % i, [128, 2, 128]) for i in range(2)]
    gb, sq, rt, eg, ob = two("fg"), two("fsq"), two("frt"), two("feg"), two("fo")
    ps_m = [P.psum(pfx + "fps%d" % i, [128, 2, 128]) for i in range(2)]
    for n in range(NT):
        cols = slice(n * 128, (n + 1) * 128)
        i = n % 2
        g = gb[i]
        kb.LD(g[:], kb.ZF[gate_row0:gate_row0 + 256, cols].rearrange("(hp p) t -> p hp t", p=128), [g])
        o = OACC[:, :, cols]
        kb.TT(sq[i][:], o, o, ALU.mult, [OACC.s(n)], [sq[i]])
        kb.MM(ps_m[i][:].rearrange("p a b -> p (a b)"), kb.C("BLK64"), sq[i][:].rearrange("p a b -> p (a b)"), True, True,
              [sq[i]], [ps_m[i]])
        kb.ACT(rt[i][:], ps_m[i][:], AF.Ln, [ps_m[i]], [rt[i]], scale=1.0 / 64, bias=kb.C("CCOL")[:, 0:1])
        kb.ACT(rt[i][:], rt[i][:], AF.Exp, [rt[i]], [rt[i]], scale=-0.5)
        kb.ACT(eg[i][:], g[:], AF.Exp, [g], [eg[i]], scale=-1.0)
        kb.TS(eg[i][:], eg[i][:], 1.0, None, ALU.add, None, [eg[i]], [eg[i]])
        kb.RECIP(eg[i][:], eg[i][:], [eg[i]], [eg[i]])
        kb.TT(eg[i][:], eg[i][:], g[:], ALU.mult, [eg[i], g], [eg[i]], eng="pool")
        kb.TT(ob[i][:], o, rt[i][:], ALU.mult, [OACC.s(n), rt[i]], [ob[i]])
        if gain is not None:
            kb.STT(ob[i][:], ob[i][:], gain[:, 0:1], eg[i][:], ALU.mult, ALU.mult, [ob[i], gain, eg[i]], [ob[i]])
        else:
            kb.TT(ob[i][:], ob[i][:], eg[i][:], ALU.mult, [ob[i], eg[i]], [ob[i]])
        kb.ST(kb.YC[yc_row0:yc_row0 + 256, cols].rearrange("(hp p) t -> p hp t", p=128), ob[i][:], [ob[i]])


def oacc_write(kb, OACC, hp, n, ps, d):
    cols = slice(n * 128, (n + 1) * 128)
    if d == 0:
        kb.CP(OACC[:, hp, cols], ps[:], [ps], [OACC.s(n)], eng="act")
    else:
        kb.TT(OACC[:, hp, cols], OACC[:, hp, cols], ps[:], ALU.add, [ps], [OACC.s(n)])


def mixer_ret(kb, l):
    P = kb.P
    with P.scope():
        OACC = P.sbuf("r_oacc", [128, 2, S])
        with P.scope():
            lgt = P.sbuf("r_lgt", [128, 8])
            kb.LD(lgt[:], kb.prm["ret_decay_logit"][l].rearrange("d h -> (d h)").partition_broadcast(128), [lgt])
            LG = P.sbuf("r_LG", [128, 8])
            kb.ACT(LG[:], lgt[:], AF.Sigmoid, [lgt], [LG])
            kb.ACT(LG[:], LG[:], AF.Ln, [LG], [LG])
            LGP = P.sbuf("r_LGP", [128, 4])
            for d in range(2):
                for hp in range(2):
                    c = 2 * d + hp
                    kb.CP(LGP[0:64, c:c + 1], LG[0:64, 4 * d + 2 * hp:4 * d + 2 * hp + 1], [LG], [LGP])
                    kb.CP(LGP[64:128, c:c + 1], LG[64:128, 4 * d + 2 * hp + 1:4 * d + 2 * hp + 2], [LG], [LGP])
            MK = [P.sbuf("r_MK%d" % d, [128, 4, 128]) for d in range(2)]
            QDEC = [[P.sbuf("r_QD%d%d" % (d, hp), [128, 128]) for hp in range(2)] for d in range(2)]
            etmp = P.sbuf("r_etmp", [128, 128])
            for d in range(2):
                for h in range(4):
                    kb.ACT(etmp[:], kb.C("DIFF" if d == 0 else "NDIFF"), AF.Exp, [LG], [etmp],
                           scale=LG[:, 4 * d + h:4 * d + h + 1])
                    kb.STT(MK[d][:, h, :], etmp[:], 0.125, kb.C("TRIF" if d == 0 else "TRIB"), ALU.mult, ALU.mult,
                           [etmp], [MK[d]])
                for hp in range(2):
                    kb.ACT(QDEC[d][hp][:], kb.C("IOTAF1" if d == 0 else "RIOTAF"), AF.Exp, [LGP], [QDEC[d][hp]],
                           scale=LGP[:, 2 * d + hp:2 * d + hp + 1])
            KD = P.sbuf("r_KD", [128, 8])
            kb.ACT(KD[:, 0:4], LG[:, 0:4], AF.Exp, [LG], [KD], scale=kb.C("CCOL")[:, 3:4])
            kb.ACT(KD[:, 4:8], LG[:, 4:8], AF.Exp, [LG], [KD], scale=kb.C("CCOL")[:, 2:3])
            kb.TS(KD[:], KD[:], 0.125, None, ALU.mult, None, [KD], [KD])
            CV = P.sbuf("r_CV", [128, 4])
            kb.ACT(CV[:], LGP[:], AF.Exp, [LGP], [CV], scale=128.0)
            qTb = [P.sbuf("r_q%d" % i, [128, 2, 128]) for i in range(2)]
            kTb = [P.sbuf("r_k%d" % i, [128, 2, 128]) for i in range(2)]
            csb = [P.sbuf("r_cs%d" % i, [128, 2, 128]) for i in range(2)]
            Vp = [[P.sbuf("r_vp%d%d" % (i, h), [128, 128]) for h in range(4)] for i in range(2)]
            khp = [[P.sbuf("r_kh%d%d" % (i, h), [128, 128]) for h in range(4)] for i in range(2)]
            for i in range(2):
                for h in range(4):
                    kb.MS(Vp[i][h][:], 0.0, [Vp[i][h]], eng="pool")
                    kb.MS(khp[i][h][:], 0.0, [khp[i][h]], eng="pool")
            t1 = [P.sbuf("r_t1%d" % i, [128, 128]) for i in range(2)]
            t2 = [P.sbuf("r_t2%d" % i, [128, 128]) for i in range(2)]
            qr = [P.sbuf("r_qr%d" % i, [128, 2, 128]) for i in range(2)]
            kr = [P.sbuf("r_kr%d" % i, [128, 2, 128]) for i in range(2)]
            AT = [P.sbuf("r_AT%d" % i, [128, 2, 128]) for i in range(2)]
            qd = [P.sbuf("r_qd%d" % i, [128, 128]) for i in range(2)]
            Sb = [P.sbuf("r_S%d" % hp, [128, 128]) for hp in range(2)]
            ps_r = [P.psum("r_psr%d" % i, [128, 256]) for i in range(2)]
            ps_s = [P.psum("r_pss%d" % i, [128, 2, 128]) for i in range(2)]
            ps_o = [P.psum("r_pso%d" % i, [128, 128]) for i in range(2)]
            ps_k = P.psum("r_psk", [128, 128])
            ps_kv = P.psum("r_pskv", [128, 128])
            it = 0
            for d in range(2):
                for hp in range(2):
                    kb.MS(Sb[hp][:], 0.0, [Sb[hp]])
                for n in ORDER[d]:
                    cols = slice(n * 128, (n + 1) * 128)
                    b = it % 2; it += 1
                    qT, kT, cs = qTb[b], kTb[b], csb[b]
                    kb.LD(qT[:], kb.ZF[Z_RQ:Z_RQ + 256, cols].rearrange("(hp p) t -> p hp t", p=128), [qT])
                    kb.LD(kT[:], kb.ZF[Z_RK:Z_RK + 256, cols].rearrange("(hp p) t -> p hp t", p=128), [kT])
                    kb.LD(cs[:, 0, :], kb.ropec[:, cols], [cs])
                    kb.LD(cs[:, 1, :], kb.ropes[:, cols], [cs])
                    for h in range(4):
                        kb.LD(Vp[b][h][:, 64 * (h % 2):64 * (h % 2) + 64], kb.ZT[cols, 256 + 64 * h:256 + 64 * h + 64],
                              [Vp[b][h]])
                    for hp in range(2):
                        j = (it * 2 + hp) % 2
                        pr = ps_r[j]
                        kb.MM(pr[:, 0:128], kb.C("ROT"), qT[:, hp, :], True, True, [qT], [pr])
                        kb.MM(pr[:, 128:256], kb.C("ROT"), kT[:, hp, :], True, True, [kT], [pr])
                        for (src_, dst, off) in ((qT, qr[b], 0), (kT, kr[b], 128)):
                            kb.TT(t1[j][:], src_[:, hp, :], cs[:, 0, :], ALU.mult, [src_, cs], [t1[j]], eng="pool")
                            kb.TT(t2[j][:], pr[:, off:off + 128], cs[:, 1, :], ALU.mult, [pr, cs], [t2[j]])
                            kb.TT(dst[:, hp, :], t1[j][:], t2[j][:], ALU.add, [t1[j], t2[j]], [dst.s(hp)], eng="pool")
                        pss = ps_s[j]
                        for h2 in range(2):
                            kb.MM(pss[:, h2, :], kr[b][64 * h2:64 * h2 + 64, hp, :], qr[b][64 * h2:64 * h2 + 64, hp, :],
                                  True, True, [kr[b].s(hp), qr[b].s(hp)], [pss])
                        kb.TT(AT[j][:], pss[:], MK[d][:, 2 * hp:2 * hp + 2, :], ALU.mult, [pss, MK[d]], [AT[j]])
                        kb.TT(qd[j][:], qr[b][:, hp, :], QDEC[d][hp][:], ALU.mult, [qr[b].s(hp), QDEC[d][hp]], [qd[j]],
                              eng="pool")
                        po = ps_o[j]
                        kb.MM(po[:], Vp[b][2 * hp][:], AT[j][:, 0, :], True, False, [Vp[b][2 * hp], AT[j]], [po])
                        kb.MM(po[:], Vp[b][2 * hp + 1][:], AT[j][:, 1, :], False, False, [Vp[b][2 * hp + 1], AT[j]], [po])
                        kb.MM(po[:], Sb[hp][:], qd[j][:], False, True, [Sb[hp], qd[j]], [po])
                        oacc_write(kb, OACC, hp, n, po, d)
                        kb.TR(ps_k[:], kr[b][:, hp, :], kb.C("IDENT"), [kr[b].s(hp)], [ps_k])
                        for h2 in range(2):
                            h = 2 * hp + h2
                            kb.ACT(khp[b][h][:, 64 * h2:64 * h2 + 64], ps_k[:, 64 * h2:64 * h2 + 64], AF.Copy,
                                   [ps_k, KD], [khp[b][h]], scale=KD[:, 4 * d + h:4 * d + h + 1])
                        kb.MM(ps_kv[:], khp[b][2 * hp][:], Vp[b][2 * hp][:], True, False,
                              [khp[b][2 * hp], Vp[b][2 * hp]], [ps_kv])
                        kb.MM(ps_kv[:], khp[b][2 * hp + 1][:], Vp[b][2 * hp + 1][:], False, True,
                              [khp[b][2 * hp + 1], Vp[b][2 * hp + 1]], [ps_kv])
                        kb.STT(Sb[hp][:], Sb[hp][:], CV[:, 2 * d + hp:2 * d + hp + 1], ps_kv[:], ALU.mult, ALU.add,
                               [Sb[hp], CV, ps_kv], [Sb[hp]])
        with P.scope():
            finalize_gated(kb, OACC, Z_RG, None, 256, "r_")


def mixer_hgrn(kb, l):
    P = kb.P
    with P.scope():
        OACC = P.sbuf("h_oacc", [128, 2, S])
        with P.scope():
            LB = P.sbuf("h_LB", [128, 4]); OML = P.sbuf("h_OML", [128, 4])
            if l == 0:
                kb.MS(LB[:], 0.0, [LB]); kb.MS(OML[:], 1.0, [OML])
            else:
                lgt = P.sbuf("h_lgt", [128, 8])
                kb.LD(lgt[:], kb.prm["hgrn_lb_logits"][:].rearrange("l d (hp p) -> p (l d hp)", p=128), [lgt],
                      allow_slow_non_contiguous=True)
                kb.TT(LB[:], lgt[:, 4:8], lgt[:, 0:4], ALU.subtract, [lgt], [LB])
                kb.ACT(LB[:], LB[:], AF.Sigmoid, [LB], [LB])
                kb.TS(OML[:], LB[:], -1.0, 1.0, ALU.mult, ALU.add, [LB], [OML])
            G = P.sbuf("h_G", [128, 1])
            for hh in range(2):
                kb.LD(G[64 * hh:64 * hh + 64, :], kb.prm["hgrn_norm_g"][l].rearrange("(p o) -> p o", o=1), [G])
            kb.hgrn_gain = G
            hqb = [P.sbuf("h_q%d" % i, [128, 2, 128]) for i in range(2)]
            hfb = [P.sbuf("h_f%d" % i, [128, 2, 128]) for i in range(2)]
            Vp = [[P.sbuf("h_vp%d%d" % (i, h), [128, 128]) for h in range(4)] for i in range(2)]
            khp = [[P.sbuf("h_kh%d%d" % (i, h), [128, 128]) for h in range(4)] for i in range(2)]
            for i in range(2):
                for h in range(4):
                    kb.MS(Vp[i][h][:], 0.0, [Vp[i][h]], eng="pool")
                    kb.MS(khp[i][h][:], 0.0, [khp[i][h]], eng="pool")
            MREF = [[P.sbuf("h_mr%d%d" % (d, i), [128, 4]) for i in range(2)] for d in range(2)]
            for d in range(2):
                for i in range(2):
                    kb.MS(MREF[d][i][:], 0.0, [MREF[d][i]])

            def two(name, shape=(128, 128)):
                return [P.sbuf("h_%s%d" % (name, i), list(shape)) for i in range(2)]
            qs, sgm, ff, logf, kk, bb, pre = two("qs"), two("sg"), two("ff"), two("lf"), two("kk"), two("bb"), two("pre")
            e1, Ql, e2, Qd = two("e1"), two("Ql"), two("e2"), two("Qd")
            Kt = [two("Kt%d" % r) for r in range(4)]
            ex = two("ex")
            AT = two("AT", (128, 2, 128))
            KhT = two("KhT")
            bend = two("bend", (128, 2))
            Sb = [P.sbuf("h_S%d" % hp, [128, 128]) for hp in range(2)]
            ps_s = [P.psum("h_pss%d" % i, [128, 2, 128]) for i in range(2)]
            ps_o = [P.psum("h_pso%d" % i, [128, 128]) for i in range(2)]
            ps_k = [P.psum("h_psk%d" % i, [128, 128]) for i in range(2)]
            ps_kv = [P.psum("h_pskv%d" % i, [128, 128]) for i in range(2)]
            it = 0
            jj = 0
            for d in range(2):
                zf = Z_HFF if d == 0 else Z_HFB
                for hp in range(2):
                    kb.MS(Sb[hp][:], 0.0, [Sb[hp]])
                for n in ORDER[d]:
                    cols = slice(n * 128, (n + 1) * 128)
                    b = it % 2; it += 1
                    hq, hf = hqb[b], hfb[b]
                    kb.LD(hq[:], kb.ZF[Z_HQ:Z_HQ + 256, cols].rearrange("(hp p) t -> p hp t", p=128), [hq])
                    kb.LD(hf[:], kb.ZF[zf:zf + 256, cols].rearrange("(hp p) t -> p hp t", p=128), [hf])
                    for h in range(4):
                        kb.LD(Vp[b][h][:, 64 * (h % 2):64 * (h % 2) + 64], kb.ZT[cols, 64 * h:64 * h + 64], [Vp[b][h]])
                    for hp in range(2):
                        j = jj % 2; jj += 1
                        c = 2 * d + hp
                        mref = MREF[d][j]
                        kb.ACT(qs[j][:], hq[:, hp, :], AF.Silu, [hq], [qs[j]])
                        kb.ACT(sgm[j][:], hf[:, hp, :], AF.Sigmoid, [hf], [sgm[j]])
                        kb.TS(ff[j][:], sgm[j][:], OML[:, c:c + 1], LB[:, c:c + 1], ALU.mult, ALU.add, [sgm[j], OML, LB], [ff[j]])
                        kb.ACT(logf[j][:], ff[j][:], AF.Ln, [ff[j]], [logf[j]])
                        kb.TS(kk[j][:], ff[j][:], -1.0, 1.0, ALU.mult, ALU.add, [ff[j]], [kk[j]], eng="pool")
                        B = bb[j]
                        if d == 0:
                            kb.SCAN(B[:], kb.C("ONES"), logf[j][:], [logf[j]], [B])
                            kb.CP(mref[:, 1:4], B[:].rearrange("p (r c) -> p r c", c=32)[:, 0:3, 31], [B], [mref])
                            be = B[:, 127:128]
                        else:
                            kb.SCAN(pre[j][:], kb.C("ONES"), logf[j][:], [logf[j]], [pre[j]])
                            kb.STT(B[:], pre[j][:], -1.0, logf[j][:], ALU.mult, ALU.add, [pre[j], logf[j]], [B])
                            kb.TS(B[:], B[:], pre[j][:, 127:128], None, ALU.add, None, [B, pre[j]], [B])
                            kb.CP(mref[:, 0:3], B[:].rearrange("p (r c) -> p r c", c=32)[:, 1:4, 0], [B], [mref])
                            be = B[:, 0:1]
                        kb.TT(e1[j][:].rearrange("p (r c) -> p r c", c=32), B[:].rearrange("p (r c) -> p r c", c=32),
                              mref[:].unsqueeze(2).to_broadcast([128, 4, 32]), ALU.subtract, [B, mref], [e1[j]])
                        kb.ACT(e1[j][:], e1[j][:], AF.Exp, [e1[j]], [e1[j]])
                        kb.STT(Ql[j][:], qs[j][:], 0.125, e1[j][:], ALU.mult, ALU.mult, [qs[j], e1[j]], [Ql[j]], eng="pool")
                        kb.ACT(e2[j][:], B[:], AF.Exp, [B], [e2[j]])
                        kb.STT(Qd[j][:], qs[j][:], 0.125, e2[j][:], ALU.mult, ALU.mult, [qs[j], e2[j]], [Qd[j]], eng="pool")
                        pss = ps_s[j]
                        for r in range(4):
                            kb.ACT(ex[j][:], B[:], AF.Exp, [B, mref], [ex[j]], scale=-1.0, bias=mref[:, r:r + 1])
                            kb.STT(Kt[r][j][:], ex[j][:], 1e26, kk[j][:], ALU.min, ALU.mult, [ex[j], kk[j]], [Kt[r][j]])
                            for h2 in range(2):
                                kb.MM(pss[:, h2, 32 * r:32 * r + 32], Kt[r][j][64 * h2:64 * h2 + 64, :],
                                      Ql[j][64 * h2:64 * h2 + 64, 32 * r:32 * r + 32], True, True,
                                      [Kt[r][j], Ql[j]], [pss])
                        kb.TT(AT[j][:], pss[:], kb.C("TRIF" if d == 0 else "TRIB").unsqueeze(1).to_broadcast([128, 2, 128]),
                              ALU.mult, [pss], [AT[j]])
                        po = ps_o[j]
                        kb.MM(po[:], Vp[b][2 * hp][:], AT[j][:, 0, :], True, False, [Vp[b][2 * hp], AT[j]], [po])
                        kb.MM(po[:], Vp[b][2 * hp + 1][:], AT[j][:, 1, :], False, False, [Vp[b][2 * hp + 1], AT[j]], [po])
                        kb.MM(po[:], Sb[hp][:], Qd[j][:], False, True, [Sb[hp], Qd[j]], [po])
                        oacc_write(kb, OACC, hp, n, po, d)
                        kb.CP(bend[j][:, 0:1], be, [B], [bend[j]])
                        kb.ACT(KhT[j][:], B[:], AF.Exp, [B, bend[j]], [KhT[j]], scale=-1.0, bias=bend[j][:, 0:1])
                        kb.TT(KhT[j][:], KhT[j][:], kk[j][:], ALU.mult, [KhT[j], kk[j]], [KhT[j]], eng="pool")
                        kb.ACT(bend[j][:, 1:2], bend[j][:, 0:1], AF.Exp, [bend[j]], [bend[j]])
                        pk = ps_k[j]
                        kb.TR(pk[:], KhT[j][:], kb.C("IDENT"), [KhT[j]], [pk])
                        for h2 in range(2):
                            h = 2 * hp + h2
                            kb.CP(khp[b][h][:, 64 * h2:64 * h2 + 64], pk[:, 64 * h2:64 * h2 + 64], [pk], [khp[b][h]],
                                  eng=("act" if h2 else "dve"))
                        pkv = ps_kv[j]
                        kb.MM(pkv[:], khp[b][2 * hp][:], Vp[b][2 * hp][:], True, False, [khp[b][2 * hp], Vp[b][2 * hp]], [pkv])
                        kb.MM(pkv[:], khp[b][2 * hp + 1][:], Vp[b][2 * hp + 1][:], False, True,
                              [khp[b][2 * hp + 1], Vp[b][2 * hp + 1]], [pkv])
                        kb.STT(Sb[hp][:], Sb[hp][:], bend[j][:, 1:2], pkv[:], ALU.mult, ALU.add,
                               [Sb[hp], bend[j], pkv], [Sb[hp]])
        with P.scope():
            G = P.sbuf("h_G2", [128, 1])
            for hh in range(2):
                kb.LD(G[64 * hh:64 * hh + 64, :], kb.prm["hgrn_norm_g"][l].rearrange("(p o) -> p o", o=1), [G])
            finalize_gated(kb, OACC, Z_HG, G, 0, "h_")


PI = float(np.pi)


def _sincos(kb, ang, sin_out, cos_out, R, tmp, shape=None):
    P = kb.P
    shp = list(ang.shape)
    with P.scope():
        ki = P.sbuf("sc_ki", shp, mybir.dt.int32)
        kf = P.sbuf("sc_kf", shp)
        r = P.sbuf("sc_r", shp)
        m = P.sbuf("sc_m", shp)
        C1 = 6.28125
        C2 = 2 * PI - C1
        for (shift, out) in ((0.0, sin_out), (PI / 2, cos_out)):
            kb.TS(r[:], ang, shift, None, ALU.add, None, R, [r])
            kb.TS(kf[:], r[:], 1.0 / (2 * PI), None, ALU.mult, None, [r], [kf])
            kb.CP(ki[:], kf[:], [kf], [ki])
            kb.CP(kf[:], ki[:], [ki], [kf])
            kb.STT(r[:], kf[:], -C1, r[:], ALU.mult, ALU.add, [kf, r], [r])
            kb.STT(r[:], kf[:], -C2, r[:], ALU.mult, ALU.add, [kf, r], [r])
            kb.TS(m[:], r[:], PI, 2 * PI, ALU.is_gt, ALU.mult, [r], [m])
            kb.TT(r[:], r[:], m[:], ALU.subtract, [r, m], [r])
            kb.TS(m[:], r[:], -PI, 2 * PI, ALU.is_lt, ALU.mult, [r], [m])
            kb.TT(r[:], r[:], m[:], ALU.add, [r, m], [r])
            kb.ACT(out, r[:], AF.Sin, [r], R)


def mixer_s5(kb, l):
    P = kb.P
    prm = kb.prm
    with P.scope():
        OACC = P.sbuf("s_oacc", [128, 2, S])
        with P.scope():
            WX = P.sbuf("s_WX", [128, 2, 8, 2, 64])
            Cblk = P.sbuf("s_Cblk", [128, 16, 128])
            kb.MS(WX[:], 0.0, [WX], eng="pool")
            kb.MS(Cblk[:], 0.0, [Cblk], eng="pool")
            for g8 in range(8):
                for ri, nm in enumerate(("s5_b_re", "s5_b_im")):
                    for gg in range(2):
                        src = prm[nm][l][8 * gg + g8].rearrange("p c -> c p")
                        kb.LD(WX[16 * g8:16 * g8 + 16, gg, g8, ri, :], src, [WX], allow_slow_non_contiguous=True)
            for g in range(16):
                g8 = g % 8
                kb.LD(Cblk[0:64, g, 16 * g8:16 * g8 + 16], prm["s5_c_re"][l][g].rearrange("c p -> p c"), [Cblk],
                      allow_slow_non_contiguous=True)
                kb.LD(Cblk[64:128, g, 16 * g8:16 * g8 + 16], prm["s5_c_im"][l][g].rearrange("c p -> p c"), [Cblk],
                      allow_slow_non_contiguous=True)
            kb.TS(Cblk[64:128, :, :], Cblk[64:128, :, :], -1.0, None, ALU.mult, None, [Cblk], [Cblk])
            Cb16 = P.sbuf("s_Cb16", [128, 16, 128], BF16)
            kb.CP(Cb16[:], Cblk[:], [Cblk], [Cb16])
            VFr = P.sbuf("s_VFr", [128, 16, 64]); VFi = P.sbuf("s_VFi", [128, 16, 64])
            T1 = P.sbuf("s_T1", [128, 16, 128]); T2 = P.sbuf("s_T2", [128, 16, 128])
            AR = P.sbuf("s_AR", [128, 16]); NAI = P.sbuf("s_NAI", [128, 16])
            for d in range(2):
                with P.scope():
                    lr = P.sbuf("s_lr", [128, 16, 64]); li = P.sbuf("s_li", [128, 16, 64]); dtb = P.sbuf("s_dt", [128, 16])
                    kb.LD(lr[:], prm["s5_lam_re"][l][d].rearrange("g p -> (g p)").partition_broadcast(128), [lr])
                    kb.LD(li[:], prm["s5_lam_im"][l][d].rearrange("g p -> (g p)").partition_broadcast(128), [li])
                    kb.LD(dtb[:], prm["s5_log_dt"][l][d].partition_broadcast(128), [dtb])
                    kb.ACT(dtb[:], dtb[:], AF.Exp, [dtb], [dtb])
                    dt_bc = dtb[:].unsqueeze(2).to_broadcast([128, 16, 64])
                    lrdt = P.sbuf("s_lrdt", [128, 16, 64]); lidt = P.sbuf("s_lidt", [128, 16, 64])
                    kb.TT(lrdt[:], lr[:], dt_bc, ALU.mult, [lr, dtb], [lrdt])
                    kb.TT(lidt[:], li[:], dt_bc, ALU.mult, [li, dtb], [lidt])
                    a = [P.sbuf("s_a%d" % i, [128, 16, 64]) for i in range(8)]
                    mag, ang, sn, cs, tmp, ar, ai, t2 = a
                    kb.ACT(mag[:], lrdt[:], AF.Exp, [lrdt], [mag])
                    _sincos(kb, lidt[:], sn[:], cs[:], [lidt, sn, cs, tmp], tmp[:])
                    kb.TT(ar[:], mag[:], cs[:], ALU.mult, [mag, cs], [ar])
                    kb.TT(ai[:], mag[:], sn[:], ALU.mult, [mag, sn], [ai])
                    den = P.sbuf("s_den", [128, 16, 64]); fr = P.sbuf("s_fr", [128, 16, 64]); fi = P.sbuf("s_fi", [128, 16, 64])
                    kb.TT(den[:], lr[:], lr[:], ALU.mult, [lr], [den])
                    kb.TT(t2[:], li[:], li[:], ALU.mult, [li], [t2])
                    kb.TT(den[:], den[:], t2[:], ALU.add, [den, t2], [den])
                    kb.RECIP(den[:], den[:], [den], [den])
                    kb.TS(ar[:], ar[:], -1.0, None, ALU.add, None, [ar], [ar])
                    kb.TT(fr[:], ar[:], lr[:], ALU.mult, [ar, lr], [fr])
                    kb.TT(t2[:], ai[:], li[:], ALU.mult, [ai, li], [t2])
                    kb.TT(fr[:], fr[:], t2[:], ALU.add, [fr, t2], [fr])
                    kb.TT(fr[:], fr[:], den[:], ALU.mult, [fr, den], [fr])
                    kb.TT(fi[:], ai[:], lr[:], ALU.mult, [ai, lr], [fi])
                    kb.TT(t2[:], ar[:], li[:], ALU.mult, [ar, li], [t2])
                    kb.TT(fi[:], fi[:], t2[:], ALU.subtract, [fi, t2], [fi])
                    kb.TT(fi[:], fi[:], den[:], ALU.mult, [fi, den], [fi])
                    jcol = kb.C("CCOL")[:, 2:3] if d == 0 else kb.C("CCOL")[:, 3:4]
                    njcol = kb.C("CCOL")[:, 6:7] if d == 0 else kb.C("CCOL")[:, 7:8]
                    kb.ACT(mag[:], lrdt[:], AF.Exp, [lrdt], [mag], scale=njcol)
                    kb.TS(ang[:], lidt[:], jcol, None, ALU.mult, None, [lidt], [ang])
                    _sincos(kb, ang[:], sn[:], cs[:], [ang, sn, cs, tmp], tmp[:])
                    vr, vi = ar, ai
                    kb.TT(vr[:], mag[:], cs[:], ALU.mult, [mag, cs], [vr])
                    kb.TT(vi[:], mag[:], sn[:], ALU.mult, [mag, sn], [vi])
                    kb.TS(vi[:], vi[:], -1.0, None, ALU.mult, None, [vi], [vi])
                    kb.TT(VFr[:], vr[:], fr[:], ALU.mult, [vr, fr], [VFr])
                    kb.TT(t2[:], vi[:], fi[:], ALU.mult, [vi, fi], [t2])
                    kb.TT(VFr[:], VFr[:], t2[:], ALU.subtract, [VFr, t2], [VFr])
                    kb.TT(VFi[:], vr[:], fi[:], ALU.mult, [vr, fi], [VFi])
                    kb.TT(t2[:], vi[:], fr[:], ALU.mult, [vi, fr], [t2])
                    kb.TT(VFi[:], VFi[:], t2[:], ALU.add, [VFi, t2], [VFi])
                with P.scope():
                    dtb = P.sbuf("s_dt2", [128, 16])
                    kb.LD(dtb[:], prm["s5_log_dt"][l][d].partition_broadcast(128), [dtb])
                    kb.ACT(dtb[:], dtb[:], AF.Exp, [dtb], [dtb])
                    lrp = P.sbuf("s_lrp", [128, 16]); lip = P.sbuf("s_lip", [128, 16])
                    for hh in range(2):
                        kb.LD(lrp[64 * hh:64 * hh + 64, :], prm["s5_lam_re"][l][d].rearrange("g p -> p g"), [lrp],
                              allow_slow_non_contiguous=True)
                        kb.LD(lip[64 * hh:64 * hh + 64, :], prm["s5_lam_im"][l][d].rearrange("g p -> p g"), [lip],
                              allow_slow_non_contiguous=True)
                    kb.TT(lrp[:], lrp[:], dtb[:], ALU.mult, [lrp, dtb], [lrp])
                    kb.TT(lip[:], lip[:], dtb[:], ALU.mult, [lip, dtb], [lip])
                    b4 = [P.sbuf("s_b%d" % i, [128, 16, 128]) for i in range(4)]
                    arg, sn2, cs2, tmp2 = b4
                    mt = kb.C("IOTAF" if d == 0 else "R127F")
                    mt_bc = mt.unsqueeze(1).to_broadcast([128, 16, 128])
                    kb.TT(arg[:], lrp[:].unsqueeze(2).to_broadcast([128, 16, 128]), mt_bc, ALU.mult, [lrp], [arg])
                    kb.ACT(T1[:], arg[:], AF.Exp, [arg], [T1])
                    kb.TT(arg[:], lip[:].unsqueeze(2).to_broadcast([128, 16, 128]), mt_bc, ALU.mult, [lip, T1], [arg])
                    _sincos(kb, arg[:], sn2[:], cs2[:], [arg, sn2, cs2, tmp2], tmp2[:])
                    kb.TT(T2[:], T1[:], sn2[:], ALU.mult, [T1, sn2], [T2])
                    kb.TS(T2[:], T2[:], -1.0, None, ALU.mult, None, [T2], [T2])
                    kb.TT(T1[:], T1[:], cs2[:], ALU.mult, [T1, cs2], [T1])
                    c4 = [P.sbuf("s_c%d" % i, [128, 16]) for i in range(4)]
                    kb.ACT(c4[0][:], lrp[:], AF.Exp, [lrp], [c4[0]])
                    _sincos(kb, lip[:], c4[1][:], c4[2][:], [lip, c4[1], c4[2], c4[3]], c4[3][:])
                    kb.TT(AR[:], c4[0][:], c4[2][:], ALU.mult, [c4[0], c4[2]], [AR])
                    kb.TT(NAI[:], c4[0][:], c4[1][:], ALU.mult, [c4[0], c4[1]], [NAI])
                    kb.TS(NAI[:], NAI[:], -1.0, None, ALU.mult, None, [NAI], [NAI])
                sweep_scope = P.scope(); sweep_scope.__enter__()
                uTb = [P.sbuf("s_u%d" % i, [128, 2, 128]) for i in range(2)]
                mm_ = [P.sbuf("s_m%d" % i, [128, 8, 64]) for i in range(4)]
                W3 = [P.sbuf("s_W3%d" % i, [128, 8, 3, 64], BF16) for i in range(2)]
                Hb = [P.sbuf("s_Hb%d" % i, [128, 8, 128], BF16) for i in range(2)]
                tri16 = P.sbuf("s_tri16", [128, 128], BF16)
                kb.CP(tri16[:], kb.C("TRIF" if d == 0 else "TRIB"), [], [tri16])
                tP = [P.sbuf("s_tP%d" % i, [128, 8, 128]) for i in range(2)]
                tPs = [P.sbuf("s_tPs%d" % i, [128, 8, 128]) for i in range(2)]
                H1 = [P.sbuf("s_H1%d" % i, [128, 8, 128]) for i in range(2)]
                H2 = [P.sbuf("s_H2%d" % i, [128, 8, 128]) for i in range(2)]
                hend = P.sbuf("s_hend", [128, 16]); hsend = P.sbuf("s_hsend", [128, 16])
                hp_ = P.sbuf("s_hp", [128, 16]); hps_ = P.sbuf("s_hps", [128, 16])
                sm = [P.sbuf("s_sm%d" % i, [128, 16]) for i in range(4)]
                xps = P.psum("s_xps", [128, 1024])
                pps = P.psum("s_pps", [128, 8, 128])
                ppss = P.psum("s_ppss", [128, 8, 128])
                yps = [P.psum("s_yps%d" % i, [128, 128]) for i in range(2)]
                kb.MS(hp_[:], 0.0, [hp_]); kb.MS(hps_[:], 0.0, [hps_])
                te = 127 if d == 0 else 0
                tri = kb.C("TRIF" if d == 0 else "TRIB")
                it = 0
                for n in ORDER[d]:
                    cols = slice(n * 128, (n + 1) * 128)
                    uT = uTb[it % 2]; it += 1
                    kb.LD(uT[:], kb.ZF[Z_SU:Z_SU + 256, cols].rearrange("(gg p) t -> p gg t", p=128), [uT])
                    for gg in range(2):
                        j = gg
                        for half in range(2):
                            kb.MM(xps[:, half * 512:(half + 1) * 512], uT[:, gg, :],
                                  WX[:, gg, half * 4:(half + 1) * 4, :, :].rearrange("q a r p -> q (a r p)"),
                                  True, True, [uT, WX], [xps])
                        xv = xps[:].rearrange("t (g r p) -> t g r p", r=2, p=64)
                        gs = slice(gg * 8, gg * 8 + 8)
                        kb.TT(mm_[0][:], xv[:, :, 0, :], VFr[:, gs, :], ALU.mult, [xps, VFr], [mm_[0]])
                        kb.TT(mm_[1][:], xv[:, :, 1, :], VFi[:, gs, :], ALU.mult, [xps, VFi], [mm_[1]])
                        kb.TT(mm_[2][:], xv[:, :, 0, :], VFi[:, gs, :], ALU.mult, [xps, VFi], [mm_[2]])
                        kb.TT(mm_[3][:], xv[:, :, 1, :], VFr[:, gs, :], ALU.mult, [xps, VFr], [mm_[3]])
                        w3 = W3[j]
                        kb.TT(w3[:, :, 0, :], mm_[0][:], mm_[1][:], ALU.subtract, [mm_[0], mm_[1]], [w3], eng="pool")
                        kb.TT(w3[:, :, 1, :], mm_[2][:], mm_[3][:], ALU.add, [mm_[2], mm_[3]], [w3], eng="pool")
                        kb.TT(w3[:, :, 2, :], mm_[1][:], mm_[0][:], ALU.subtract, [mm_[0], mm_[1]], [w3], eng="pool")
                        for g8 in range(8):
                            kb.MM(pps[:, g8, :], w3[:, g8, 0:2, :].rearrange("q r p -> q (r p)"), tri16[:], True, True, [w3, tri16], [pps])
                            kb.MM(ppss[:, g8, :], w3[:, g8, 1:3, :].rearrange("q r p -> q (r p)"), tri16[:], True, True, [w3, tri16], [ppss])
                        kb.TT(tP[j][:], pps[:], hp_[:, gs].unsqueeze(2).to_broadcast([128, 8, 128]), ALU.add, [pps, hp_], [tP[j]])
                        kb.TT(tPs[j][:], ppss[:], hps_[:, gs].unsqueeze(2).to_broadcast([128, 8, 128]), ALU.add,
                              [ppss, hps_], [tPs[j]])
                        kb.TT(H1[j][:], tP[j][:], T1[:, gs, :], ALU.mult, [tP[j], T1], [H1[j]], eng="pool")
                        kb.TT(H2[j][:], tPs[j][:], T2[:, gs, :], ALU.mult, [tPs[j], T2], [H2[j]])
                        kb.TT(Hb[j][:], H1[j][:], H2[j][:], ALU.add, [H1[j], H2[j]], [Hb[j]], eng="pool")
                        yp = yps[gg]
                        for g8 in range(8):
                            kb.MM(yp[:], Cb16[:, gg * 8 + g8, :], Hb[j][:, g8, :], g8 == 0, g8 == 7, [Cb16, Hb[j]], [yp])
                        oacc_write(kb, OACC, gg, n, yp, d)
                        kb.TT(hend[:, gs], H1[j][:, :, te], H2[j][:, :, te], ALU.add, [H1[j], H2[j]], [hend])
                        kb.TT(sm[0][:, 0:8], tPs[j][:, :, te], T1[:, gs, te], ALU.mult, [tPs[j], T1], [sm[0]])
                        kb.TT(sm[1][:, 0:8], tP[j][:, :, te], T2[:, gs, te], ALU.mult, [tP[j], T2], [sm[1]])
                        kb.TT(hsend[:, gs], sm[0][:, 0:8], sm[1][:, 0:8], ALU.subtract, [sm[0], sm[1]], [hsend])
                    kb.TT(sm[0][:], hend[:], AR[:], ALU.mult, [hend, AR], [sm[0]])
                    kb.TT(sm[1][:], hsend[:], NAI[:], ALU.mult, [hsend, NAI], [sm[1]])
                    kb.TT(sm[2][:], hsend[:], AR[:], ALU.mult, [hsend, AR], [sm[2]])
                    kb.TT(sm[3][:], hend[:], NAI[:], ALU.mult, [hend, NAI], [sm[3]])
                    kb.TT(hp_[:], sm[0][:], sm[1][:], ALU.add, [sm[0], sm[1]], [hp_])
                    kb.TT(hps_[:], sm[2][:], sm[3][:], ALU.subtract, [sm[2], sm[3]], [hps_])
                sweep_scope.__exit__(None, None, None)
        with P.scope():
            dsk = P.sbuf("s_dsk", [128, 2]); glb = P.sbuf("s_glb", [128, 2])
            kb.LD(dsk[:], prm["s5_d"][l].rearrange("(gg p) -> p gg", p=128), [dsk], allow_slow_non_contiguous=True)
            kb.LD(glb[:], prm["s5_glu_b"][l].rearrange("(gg p) -> p gg", p=128), [glb], allow_slow_non_contiguous=True)
            gw = P.sbuf("s_gw", [128, 2, 256])
            kb.LD(gw[:], prm["s5_glu_w"][l].rearrange("(ct p) o -> p ct o", p=128), [gw])
            uTb = [P.sbuf("s_fu%d" % i, [128, 2, 128]) for i in range(2)]
            yy = [P.sbuf("s_yy%d" % i, [128, 2, 128]) for i in range(2)]
            x2 = [P.sbuf("s_x2%d" % i, [128, 2, 128]) for i in range(2)]
            th = [P.sbuf("s_th%d" % i, [128, 2, 128]) for i in range(2)]
            sgb = [P.sbuf("s_sg%d" % i, [128, 128]) for i in range(2)]
            ob = [P.sbuf("s_ob%d" % i, [128, 128]) for i in range(2)]
            psz = [P.psum("s_psz%d" % i, [128, 128]) for i in range(2)]
            k = 0
            for n in range(NT):
                cols = slice(n * 128, (n + 1) * 128)
                i = n % 2
                kb.LD(uTb[i][:], kb.ZF[Z_SU:Z_SU + 256, cols].rearrange("(gg p) t -> p gg t", p=128), [uTb[i]])
                for gg in range(2):
                    kb.STT(yy[i][:, gg, :], uTb[i][:, gg, :], dsk[:, gg:gg + 1], OACC[:, gg, cols], ALU.mult, ALU.add,
                           [uTb[i], dsk, OACC.s(n)], [yy[i]])
                kb.TT(x2[i][:], yy[i][:], yy[i][:], ALU.mult, [yy[i]], [x2[i]], eng="pool")
                kb.TS(x2[i][:], x2[i][:], 0.044715, 1.0, ALU.mult, ALU.add, [x2[i]], [x2[i]])
                kb.TT(x2[i][:], x2[i][:], yy[i][:], ALU.mult, [x2[i], yy[i]], [x2[i]], eng="pool")
                kb.ACT(th[i][:], x2[i][:], AF.Tanh, [x2[i]], [th[i]], scale=0.7978845608028654)
                kb.TS(th[i][:], th[i][:], 1.0, 0.5, ALU.add, ALU.mult, [th[i]], [th[i]])
                kb.TT(yy[i][:], yy[i][:], th[i][:], ALU.mult, [yy[i], th[i]], [yy[i]], eng="pool")
                for ot in range(2):
                    q = k % 2; k += 1
                    for ct in range(2):
                        kb.MM(psz[q][:], gw[:, ct, ot * 128:(ot + 1) * 128], yy[i][:, ct, :], ct == 0, ct == 1, [gw, yy[i]], [psz[q]])
                    kb.ACT(sgb[q][:], psz[q][:], AF.Sigmoid, [psz[q], glb], [sgb[q]], bias=glb[:, ot:ot + 1])
                    kb.TT(ob[q][:], yy[i][:, ot, :], sgb[q][:], ALU.mult, [yy[i], sgb[q]], [ob[q]])
                    kb.ST(kb.YC[768 + ot * 128:768 + (ot + 1) * 128, cols], ob[q][:], [ob[q]])


def gdn_conv(kb, l):
    P = kb.P
    with P.scope():
        CW = P.sbuf("g_cw", [128, 6, 9])
        for kh in range(3):
            for kw in range(3):
                kb.LD(CW[:, :, kh * 3 + kw], kb.prm["gdn_conv_w"][l][kh, kw].rearrange("(ct p) -> p ct", p=128), [CW],
                      allow_slow_non_contiguous=True)
        mlat = P.sbuf("g_mlat", [128, 2, 512]); mctx = P.sbuf("g_mctx", [128, 2, 256])
        kb.LD(mlat[:], kb.cmlat[:], [mlat]); kb.LD(mctx[:], kb.cmctx[:], [mctx])
        Wb = [P.sbuf("g_w%d" % i, [128, 642]) for i in range(2)]
        acc = [[P.sbuf("g_acc%d%d" % (i, j), [128, 512]) for j in range(3)] for i in range(2)]
        sl = [P.sbuf("g_sl%d" % i, [128, 512]) for i in range(2)]
        sq = [P.sbuf("g_sq%d" % i, [128, 512]) for i in range(2)]
        rt = [P.sbuf("g_rt%d" % i, [128, 512]) for i in range(2)]
        ps = [P.psum("g_psn%d" % i, [128, 512]) for i in range(2)]
        spans = [(0, 256, True)] + [(256 + 512 * k, 512, False) for k in range(8)]
        it = 0
        for (t0, L, is_ctx) in spans:
            lo = 0 if is_ctx else 256
            hi = 256 if is_ctx else S
            a = max(lo, t0 - 65); b = min(hi, t0 + L + 65)
            for ct in range(6):
                i = it % 2; it += 1
                W = Wb[i]
                kb.MS(W[:], 0.0, [W], eng="pool")
                kb.LD(W[:, 65 + (a - t0):65 + (b - t0)], kb.ZF[Z_GQKV + ct * 128:Z_GQKV + (ct + 1) * 128, a:b], [W])
                rows = (1,) if is_ctx else (0, 1, 2)
                masks = mctx if is_ctx else mlat
                for dwi, shift in enumerate((-1, 0, 1)):
                    A = acc[i][dwi]
                    eng = "dve"
                    for q, dh in enumerate(rows):
                        o0 = 65 + 64 * (dh - 1) + shift
                        src = W[:, o0:o0 + L]
                        wcol = CW[:, ct, dh * 3 + dwi:dh * 3 + dwi + 1]
                        if q == 0:
                            kb.TS(A[:, :L], src, wcol, None, ALU.mult, None, [W, CW], [A], eng=("pool" if dwi != 1 else "dve"))
                        else:
                            kb.STT(A[:, :L], src, wcol, A[:, :L], ALU.mult, ALU.add, [W, CW, A], [A])
                    if dwi != 1:
                        mi = 0 if dwi == 0 else 1
                        kb.TT(A[:, :L], A[:, :L], masks[:, mi, :L], ALU.mult, [A, masks], [A], eng="pool")
                A0, A1, A2 = acc[i]
                kb.TT(A1[:, :L], A1[:, :L], A0[:, :L], ALU.add, [A0, A1], [A1], eng="pool")
                kb.TT(A1[:, :L], A1[:, :L], A2[:, :L], ALU.add, [A1, A2], [A1], eng="pool")
                kb.ACT(sl[i][:, :L], A1[:, :L], AF.Silu, [A1], [sl[i]])
                if ct < 4:
                    kb.TT(sq[i][:, :L], sl[i][:, :L], sl[i][:, :L], ALU.mult, [sl[i]], [sq[i]], eng="pool")
                    kb.MM(ps[i][:, :L], kb.C("BLK64"), sq[i][:, :L], True, True, [sq[i]], [ps[i]])
                    kb.ACT(rt[i][:, :L], ps[i][:, :L], AF.Sqrt, [ps[i]], [rt[i]], bias=kb.C("CCOL")[:, 0:1])
                    kb.RECIP(rt[i][:, :L], rt[i][:, :L], [rt[i]], [rt[i]])
                    if ct < 2:
                        kb.STT(sl[i][:, :L], sl[i][:, :L], 0.125, rt[i][:, :L], ALU.mult, ALU.mult, [sl[i], rt[i]], [sl[i]])
                    else:
                        kb.TT(sl[i][:, :L], sl[i][:, :L], rt[i][:, :L], ALU.mult, [sl[i], rt[i]], [sl[i]])
                kb.ST(kb.QKVF[ct * 128:(ct + 1) * 128, t0:t0 + L], sl[i][:, :L], [sl[i]])


def mixer_gdn(kb, l):
    P = kb.P
    gdn_conv(kb, l)
    upto = kb.cfg.get("gdn_upto", 99)
    if upto < 1:
        return
    with P.scope():
        OACC = P.sbuf("g_oacc", [128, 2, S])
        with P.scope():
            DTB = P.sbuf("g_dtb", [128, 8]); NEGA = P.sbuf("g_nega", [128, 8])
            kb.LD(DTB[:], kb.prm["gdn_dt_bias"][l].rearrange("d h -> (d h)").partition_broadcast(128), [DTB])
            kb.LD(NEGA[:], kb.prm["gdn_a_log"][l].rearrange("d h -> (d h)").partition_broadcast(128), [NEGA])
            kb.ACT(NEGA[:], NEGA[:], AF.Exp, [NEGA], [NEGA])
            kb.TS(NEGA[:], NEGA[:], -1.0, None, ALU.mult, None, [NEGA], [NEGA])
            qnb = [P.sbuf("g_q%d" % i, [128, 2, 128]) for i in range(2)]
            knb = [P.sbuf("g_k%d" % i, [128, 2, 128]) for i in range(2)]
            vvb = [P.sbuf("g_v%d" % i, [128, 2, 128]) for i in range(2)]
            gabb = [P.sbuf("g_gab%d" % i, [128, 16]) for i in range(2)]

            def sm4(name, w=4):
                return P.sbuf("g_" + name, [128, w])
            xa, ea, loga, beta, lnb = sm4("xa"), sm4("ea"), sm4("loga"), sm4("beta"), sm4("lnb")
            gtm, ngt, ekr, cdec, eg, beg, gpl = sm4("gtm"), sm4("ngt"), sm4("ekr"), sm4("cdec"), sm4("eg"), sm4("beg"), sm4("gpl")
            ROWS = P.sbuf("g_rows", [4, 384])
            LI = P.sbuf("g_LI", [128, 4, 128]); LBT = P.sbuf("g_LBT", [128, 4, 128]); LBm = P.sbuf("g_LB", [128, 4, 128])
            NAT = P.sbuf("g_NAT", [128, 4, 128]); NA = P.sbuf("g_NA", [128, 4, 128]); QKm = P.sbuf("g_QKm", [128, 4, 128])
            Tm = P.sbuf("g_Tm", [128, 4, 128]); Wm = P.sbuf("g_Wm", [128, 4, 128])
            x1 = P.sbuf("g_x1", [128, 4, 128]); y1 = P.sbuf("g_y1", [128, 4, 128])
            tmx = P.sbuf("g_tmx", [128, 4, 128]); tmy = P.sbuf("g_tmy", [128, 4, 128])
            Rm = [P.sbuf("g_R%d" % h, [128, 128]) for h in range(4)]
            khp = [P.sbuf("g_kh%d" % h, [128, 128]) for h in range(4)]
            vnp = [P.sbuf("g_vn%d" % h, [128, 128]) for h in range(4)]
            for h in range(4):
                kb.MS(khp[h][:], 0.0, [khp[h]], eng="pool")
                kb.MS(vnp[h][:], 0.0, [vnp[h]], eng="pool")
            upair = [P.sbuf("g_up%d" % hp, [128, 128]) for hp in range(2)]
            wTp = [P.sbuf("g_wT%d" % hp, [128, 128]) for hp in range(2)]
            EG = [P.sbuf("g_EG%d" % hp, [128, 128]) for hp in range(2)]
            qd = [P.sbuf("g_qd%d" % hp, [128, 128]) for hp in range(2)]
            cdp = [P.sbuf("g_cdp%d" % hp, [128, 1]) for hp in range(2)]
            Sb = [P.sbuf("g_S%d" % hp, [128, 128]) for hp in range(2)]
            B = [P.psum("g_B%d" % i, [128, 512]) for i in range(8)]
            ident = kb.C("IDENT")
            it = 0
            for d in range(2):
                tri = kb.C("TRIF" if d == 0 else "TRIB")
                rem = kb.C("SUFF" if d == 0 else "PREB")
                n_incl = kb.C("NLE" if d == 0 else "NGE")
                n_strT = kb.C("NLT" if d == 0 else "NGT")
                n_str = kb.C("NGT" if d == 0 else "NLT")
                for hp in range(2):
                    kb.MS(Sb[hp][:], 0.0, [Sb[hp]])
                for n in ORDER[d][:kb.cfg.get("ntiles", NT)]:
                    cols = slice(n * 128, (n + 1) * 128)
                    b = it % 2; it += 1
                    qn, kn, vv, gab = qnb[b], knb[b], vvb[b], gabb[b]
                    kb.LD(qn[:], kb.QKVF[0:256, cols].rearrange("(hp p) t -> p hp t", p=128), [qn])
                    kb.LD(kn[:], kb.QKVF[256:512, cols].rearrange("(hp p) t -> p hp t", p=128), [kn])
                    kb.LD(vv[:], kb.QKVF[512:768, cols].rearrange("(hp p) t -> p hp t", p=128), [vv])
                    kb.LD(gab[:], kb.ZT[cols, 512:528], [gab])
                    kb.TT(xa[:], gab[:, 4 * d:4 * d + 4], DTB[:, 4 * d:4 * d + 4], ALU.add, [gab, DTB], [xa])
                    kb.ACT(ea[:], xa[:], AF.Exp, [xa], [ea])
                    kb.ACT(ea[:], ea[:], AF.Ln, [ea], [ea], bias=kb.C("CCOL")[:, 1:2])
                    kb.TT(loga[:], ea[:], NEGA[:, 4 * d:4 * d + 4], ALU.mult, [ea, NEGA], [loga])
                    kb.ACT(beta[:], gab[:, 8 + 4 * d:12 + 4 * d], AF.Sigmoid, [gab], [beta])
                    kb.ACT(lnb[:], beta[:], AF.Ln, [beta], [lnb])
                    kb.MM(B[0][:, 0:4], tri, loga[:], True, True, [loga], [B[0]])
                    kb.MM(B[0][:, 4:8], rem, loga[:], True, True, [loga], [B[0]])
                    kb.MM(B[0][:, 8:12], kb.C("ONES"), loga[:], True, True, [loga], [B[0]])
                    kb.CP(gtm[:], B[0][:, 0:4], [B[0]], [gtm])
                    kb.TS(ngt[:], B[0][:, 0:4], -1.0, None, ALU.mult, None, [B[0]], [ngt])
                    kb.ACT(ekr[:], B[0][:, 4:8], AF.Exp, [B[0]], [ekr])
                    kb.ACT(cdec[:], B[0][:, 8:12], AF.Exp, [B[0]], [cdec])
                    kb.ACT(eg[:], gtm[:], AF.Exp, [gtm], [eg])
                    kb.TT(beg[:], beta[:], eg[:], ALU.mult, [beta, eg], [beg])
                    kb.TT(gpl[:], gtm[:], lnb[:], ALU.add, [gtm, lnb], [gpl])
                    kb.MM(B[1][0:4, 0:128], loga[:], tri, True, True, [loga], [B[1]])
                    kb.MM(B[1][0:4, 128:256], loga[:], tri, True, False, [loga], [B[1]])
                    kb.MM(B[1][0:4, 128:256], lnb[:], ident, False, True, [lnb], [B[1]])
                    kb.CP(ROWS[:, 0:256], B[1][0:4, 0:256], [B[1]], [ROWS])
                    kb.TS(ROWS[:, 256:384], B[1][0:4, 0:128], -1.0, None, ALU.mult, None, [B[1]], [ROWS])
                    if upto < 2:
                        continue
                    for (dst, rsl, negm, bias_t, bank) in ((LI, slice(0, 128), n_incl, ngt, B[2]),
                                                           (LBT, slice(128, 256), n_strT, ngt, B[3]),
                                                           (LBm, slice(256, 384), n_str, gpl, B[2])):
                        for h in range(4):
                            kb.MM(bank[:, h * 128:(h + 1) * 128], kb.C("SELH%d" % h)[0:4, :], ROWS[:, rsl], True, False,
                                  [ROWS], [bank])
                            kb.MM(bank[:, h * 128:(h + 1) * 128], ident, negm, False, True, [], [bank])
                        for h in range(4):
                            kb.ACT(dst[:, h, :], bank[:, h * 128:(h + 1) * 128], AF.Exp, [bank, bias_t], [dst],
                                   bias=bias_t[:, h:h + 1])
                    if upto < 3:
                        continue
                    for h in range(4):
                        hp, h2 = divmod(h, 2)
                        ksl = kn[64 * h2:64 * h2 + 64, hp, :]
                        kb.MM(B[4][:, h * 128:(h + 1) * 128], ksl, ksl, True, True, [kn], [B[4]])
                        kb.MM(B[5][:, h * 128:(h + 1) * 128], ksl, qn[64 * h2:64 * h2 + 64, hp, :], True, True, [kn, qn], [B[5]])
                    b4v = B[4][:].rearrange("p (h t) -> p h t", h=4)
                    b5v = B[5][:].rearrange("p (h t) -> p h t", h=4)
                    kb.STT(NAT[:], b4v, -1.0, LBT[:], ALU.mult, ALU.mult, [B[4], LBT], [NAT])
                    kb.STT(NA[:], b4v, -1.0, LBm[:], ALU.mult, ALU.mult, [B[4], LBm], [NA])
                    kb.TT(QKm[:], b5v, LI[:], ALU.mult, [B[5], LI], [QKm])
                    if upto < 4:
                        continue
                    idb = ident.unsqueeze(1).to_broadcast([128, 4, 128])
                    kb.CP(Tm[:], idb, [], [Tm])
                    kb.CP(Wm[:], idb, [], [Wm], eng="pool")
                    for s_ in (1, 2, 4, 8, 16, 32, 64):
                        mT = kb.C(("MOFF%d" if d == 0 else "MOFFT%d") % s_).unsqueeze(1).to_broadcast([128, 4, 128])
                        mW = kb.C(("MOFFT%d" if d == 0 else "MOFF%d") % s_).unsqueeze(1).to_broadcast([128, 4, 128])
                        for h in range(4):
                            kb.MM(B[2][:, h * 128:(h + 1) * 128], NAT[:, h, :], Tm[:, h, :], True, True, [NAT, Tm], [B[2]])
                        for h in range(4):
                            kb.MM(B[3][:, h * 128:(h + 1) * 128], NA[:, h, :], Wm[:, h, :], True, True, [NA, Wm], [B[3]])
                        kb.CP(x1[:], B[2][:].rearrange("p (h t) -> p h t", h=4), [B[2]], [x1], eng="act")
                        kb.CP(y1[:], B[3][:].rearrange("p (h t) -> p h t", h=4), [B[3]], [y1], eng="dve")
                        for h in range(4):
                            kb.MM(B[4][:, h * 128:(h + 1) * 128], Wm[:, h, :], x1[:, h, :], True, True, [Wm, x1], [B[4]])
                        for h in range(4):
                            kb.MM(B[5][:, h * 128:(h + 1) * 128], Tm[:, h, :], y1[:, h, :], True, True, [Tm, y1], [B[5]])
                        kb.TT(tmx[:], B[4][:].rearrange("p (h t) -> p h t", h=4), mT, ALU.mult, [B[4]], [tmx])
                        kb.TT(tmy[:], B[5][:].rearrange("p (h t) -> p h t", h=4), mW, ALU.mult, [B[5]], [tmy])
                        kb.TT(Tm[:], Tm[:], tmx[:], ALU.add, [Tm, tmx], [Tm], eng="pool")
                        kb.TT(Wm[:], Wm[:], tmy[:], ALU.add, [Wm, tmy], [Wm], eng="pool")
                    if upto < 5:
                        continue
                    for hp in range(2):
                        kb.TR(B[0][:, 128:256], kn[:, hp, :], ident, [kn], [B[0]])
                        kb.TR(B[0][:, 256:384], vv[:, hp, :], ident, [vv], [B[0]])
                        for h2 in range(2):
                            h = 2 * hp + h2
                            kc = slice(64 * h2, 64 * h2 + 64)
                            vc = slice(64 * (1 - h2), 64 * (1 - h2) + 64)
                            kb.TS(Rm[h][:, kc], B[0][:, 128 + 64 * h2:128 + 64 * h2 + 64], beg[:, h:h + 1], None, ALU.mult, None,
                                  [B[0], beg], [Rm[h]])
                            kb.ACT(Rm[h][:, vc], B[0][:, 256 + 64 * h2:256 + 64 * h2 + 64], AF.Copy, [B[0], beta], [Rm[h]],
                                   scale=beta[:, h:h + 1])
                            kb.ACT(khp[h][:, kc], B[0][:, 128 + 64 * h2:128 + 64 * h2 + 64], AF.Copy, [B[0], ekr], [khp[h]],
                                   scale=ekr[:, h:h + 1])
                    if upto < 6:
                        continue
                    for h in range(4):
                        kb.MM(B[2][:, h * 128:(h + 1) * 128], Wm[:, h, :], Rm[h][:], True, True, [Wm, Rm[h]], [B[2]])
                        kb.MM(B[3][:, h * 128:(h + 1) * 128], Rm[h][:], Wm[:, h, :], True, True, [Wm, Rm[h]], [B[3]])
                    for h in range(4):
                        hp, h2 = divmod(h, 2)
                        vc0 = 64 * (1 - h2)
                        kb.CP(upair[hp][:, 64 * h2:64 * h2 + 64], B[2][:, h * 128 + vc0:h * 128 + vc0 + 64], [B[2]], [upair[hp]],
                              eng=("act" if h2 else "dve"))
                        kb.CP(wTp[hp][64 * h2:64 * h2 + 64, :], B[3][64 * h2:64 * h2 + 64, h * 128:(h + 1) * 128], [B[3]], [wTp[hp]],
                              eng=("dve" if h2 else "act"))
                    if upto < 7:
                        continue
                    for hp in range(2):
                        kb.MM(B[1][:, 256:384], kb.C("SELP%d" % hp)[0:4, :], ROWS[:, 0:128], True, True, [ROWS], [B[1]])
                        kb.ACT(EG[hp][:], B[1][:, 256:384], AF.Exp, [B[1]], [EG[hp]])
                        kb.TT(qd[hp][:], qn[:, hp, :], EG[hp][:], ALU.mult, [qn, EG[hp]], [qd[hp]], eng="pool")
                        pws = B[7][:, hp * 128:(hp + 1) * 128]
                        kb.MM(pws, wTp[hp][:], Sb[hp][:], True, True, [wTp[hp], Sb[hp]], [B[7]])
                        for h2 in range(2):
                            h = 2 * hp + h2
                            cs_ = slice(64 * h2, 64 * h2 + 64)
                            kb.TT(vnp[h][:, cs_], upair[hp][:, cs_], B[7][:, hp * 128 + 64 * h2:hp * 128 + 64 * h2 + 64],
                                  ALU.subtract, [upair[hp], B[7]], [vnp[h]])
                        po = B[6][:, hp * 256:hp * 256 + 128]
                        kb.MM(po, Sb[hp][:], qd[hp][:], True, False, [Sb[hp], qd[hp]], [B[6].s(hp)])
                        kb.MM(po, vnp[2 * hp][:], QKm[:, 2 * hp, :], False, False, [vnp[2 * hp], QKm], [B[6].s(hp)])
                        kb.MM(po, vnp[2 * hp + 1][:], QKm[:, 2 * hp + 1, :], False, True, [vnp[2 * hp + 1], QKm], [B[6].s(hp)])
                        cols_ = slice(n * 128, (n + 1) * 128)
                        if d == 0:
                            kb.CP(OACC[:, hp, cols_], po, [B[6].s(hp)], [OACC.s(n)], eng="act")
                        else:
                            kb.TT(OACC[:, hp, cols_], OACC[:, hp, cols_], po, ALU.add, [B[6].s(hp)], [OACC.s(n)])
                        pkv = B[6][:, hp * 256 + 128:hp * 256 + 256]
                        kb.MM(pkv, khp[2 * hp][:], vnp[2 * hp][:], True, False, [khp[2 * hp], vnp[2 * hp]], [B[6].s(2 + hp)])
                        kb.MM(pkv, khp[2 * hp + 1][:], vnp[2 * hp + 1][:], False, True, [khp[2 * hp + 1], vnp[2 * hp + 1]],
                              [B[6].s(2 + hp)])
                        kb.CP(cdp[hp][0:64, :], cdec[0:64, 2 * hp:2 * hp + 1], [cdec], [cdp[hp]])
                        kb.CP(cdp[hp][64:128, :], cdec[64:128, 2 * hp + 1:2 * hp + 2], [cdec], [cdp[hp]])
                        kb.STT(Sb[hp][:], Sb[hp][:], cdp[hp][:, 0:1], pkv, ALU.mult, ALU.add,
                               [Sb[hp], cdp[hp], B[6].s(2 + hp)], [Sb[hp]])
        with P.scope():
            G = P.sbuf("g_G2", [128, 1])
            for hh in range(2):
                kb.LD(G[64 * hh:64 * hh + 64, :], kb.prm["gdn_norm_g"][l].rearrange("(p o) -> p o", o=1), [G])
            finalize_gated(kb, OACC, Z_GG, G, 512, "g_")


class _Ctx:
    pass


def mixer_gdn2(kb, l):
    P = kb.P
    (gdn_conv if kb.cfg.get('conv_old') else gdn_conv2)(kb, l)
    with P.scope():
        OACC = P.sbuf("g_oacc", [128, 2, S])
        kb.MS(OACC[:, 0, :], 0.0, [OACC.s(n) for n in range(NT)], eng="pool")
        kb.MS(OACC[:, 1, :], 0.0, [OACC.s(n) for n in range(NT)], eng="pool")
        with P.scope():
            DTB = P.sbuf("g_dtb", [128, 8]); NEGA = P.sbuf("g_nega", [128, 8])
            kb.LD(DTB[:], kb.prm["gdn_dt_bias"][l].rearrange("d h -> (d h)").partition_broadcast(128), [DTB])
            kb.LD(NEGA[:], kb.prm["gdn_a_log"][l].rearrange("d h -> (d h)").partition_broadcast(128), [NEGA])
            kb.ACT(NEGA[:], NEGA[:], AF.Exp, [NEGA], [NEGA])
            kb.TS(NEGA[:], NEGA[:], -1.0, None, ALU.mult, None, [NEGA], [NEGA])
            ident = kb.C("IDENT")
            idb = ident.unsqueeze(1).to_broadcast([128, 4, 128])
            cxs = []
            for d in range(2):
                cx = _Ctx()
                cx.d = d
                pf = "g%d_" % d
                cx.qnb = [P.sbuf(pf + "q%d" % i, [128, 2, 128]) for i in range(2)]
                cx.knb = [P.sbuf(pf + "k%d" % i, [128, 2, 128]) for i in range(2)]
                cx.vvb = [P.sbuf(pf + "v%d" % i, [128, 2, 128]) for i in range(2)]
                cx.gabb = [P.sbuf(pf + "gab%d" % i, [128, 16]) for i in range(2)]
                for nm in ("xa", "ea", "loga", "beta", "lnb", "gtm", "ngt", "ekr", "cdec", "eg", "beg", "gpl"):
                    setattr(cx, nm, P.sbuf(pf + nm, [128, 4]))
                cx.ROWS = P.sbuf(pf + "rows", [4, 384])
                for nm in ("LI", "LBT", "LBm"):
                    setattr(cx, nm, P.sbuf(pf + nm, [128, 4, 128]))
                cx.QKm = P.sbuf(pf + "QKm", [128, 4, 128], BF16)
                cx.ROWSX = P.sbuf(pf + "rowsx", [4, 3, 4, 128])
                cx.knp = [[P.sbuf(pf + "knp%d%d" % (i, h), [128, 128], BF16) for h in range(4)] for i in range(2)]
                for i in range(2):
                    for h in range(4):
                        kb.MS(cx.knp[i][h][:], 0.0, [cx.knp[i][h]], eng="pool")
                cx.kq16 = [P.sbuf(pf + "kq16%d" % i, [128, 2, 2, 128], BF16) for i in range(2)]
                for nm in ("NAT", "NA", "Tm", "Wm", "x1", "y1", "tmx", "tmy"):
                    setattr(cx, nm, P.sbuf(pf + nm, [128, 4, 128], BF16))
                cx.Rm = [P.sbuf(pf + "R%d" % h, [128, 128], BF16) for h in range(4)]
                cx.khp = [P.sbuf(pf + "kh%d" % h, [128, 128], BF16) for h in range(4)]
                cx.vnp = [P.sbuf(pf + "vn%d" % h, [128, 128], BF16) for h in range(4)]
                for h in range(4):
                    kb.MS(cx.khp[h][:], 0.0, [cx.khp[h]], eng="pool")
                    kb.MS(cx.vnp[h][:], 0.0, [cx.vnp[h]], eng="pool")
                cx.upair = [P.sbuf(pf + "up%d" % hp, [128, 128]) for hp in range(2)]
                cx.wTp = [P.sbuf(pf + "wT%d" % hp, [128, 128]) for hp in range(2)]
                cx.EG = [P.sbuf(pf + "EG%d" % hp, [128, 128]) for hp in range(2)]
                cx.qd = [P.sbuf(pf + "qd%d" % hp, [128, 128]) for hp in range(2)]
                cx.cdp = [P.sbuf(pf + "cdp%d" % hp, [128, 1]) for hp in range(2)]
                cx.Sb = [P.sbuf(pf + "S%d" % hp, [128, 128]) for hp in range(2)]
                for hp in range(2):
                    kb.MS(cx.Sb[hp][:], 0.0, [cx.Sb[hp]])
                cx.B = [P.psum(pf + "B%d" % i, [128, 512]) for i in range(4)]
                cx.tri = kb.C("TRIF" if d == 0 else "TRIB")
                cx.rem = kb.C("SUFF" if d == 0 else "PREB")
                cx.n_incl = kb.C("NLE" if d == 0 else "NGE")
                cx.n_strT = kb.C("NLT" if d == 0 else "NGT")
                cx.n_str = kb.C("NGT" if d == 0 else "NLT")
                cx.it = 0
                cx.id16 = P.sbuf(pf + "id16", [128, 128], BF16)
                kb.CP(cx.id16[:], ident, [], [cx.id16])
                for nm_, cn in (("n_incl4", "NLE" if d == 0 else "NGE"), ("n_strT4", "NLT" if d == 0 else "NGT"),
                                ("n_str4", "NGT" if d == 0 else "NLT")):
                    t_ = P.sbuf(pf + nm_, [128, 4, 128], BF16)
                    kb.CP(t_[:], kb.C(cn).unsqueeze(1).to_broadcast([128, 4, 128]), [], [t_])
                    setattr(cx, nm_, t_[:].rearrange("p h i -> p (h i)"))
                bd = P.sbuf(pf + "bd4", [4, 4, 128])
                for h in range(4):
                    kb.CP(bd[:, h, :], kb.C("SELH%d" % h)[0:4, :], [], [bd])
                cx.bd4 = bd[:]
                cx.mT = {}; cx.mW = {}
                for s_ in (2, 4, 8, 16, 32, 64):
                    for nm_, dct, cn in (("mT", cx.mT, ("MOFF%d" if d == 0 else "MOFFT%d") % s_),
                                         ("mW", cx.mW, ("MOFFT%d" if d == 0 else "MOFF%d") % s_)):
                        mt_ = P.sbuf(pf + nm_ + str(s_), [128, 4, 128], mybir.dt.uint8)
                        kb.CP(mt_[:], kb.C(cn).unsqueeze(1).to_broadcast([128, 4, 128]), [], [mt_])
                        dct[s_] = mt_
                cxs.append(cx)

            def step(cx, n):
                d = cx.d
                Pa, Pb, Pc, Pd = cx.B
                cols = slice(n * 128, (n + 1) * 128)
                b = cx.it % 2; cx.it += 1
                qn, kn, vv, gab = cx.qnb[b], cx.knb[b], cx.vvb[b], cx.gabb[b]
                xa, ea, loga, beta, lnb = cx.xa, cx.ea, cx.loga, cx.beta, cx.lnb
                gtm, ngt, ekr, cdec, eg, beg, gpl = cx.gtm, cx.ngt, cx.ekr, cx.cdec, cx.eg, cx.beg, cx.gpl
                ROWS, LI, LBT, LBm, NAT, NA, QKm = cx.ROWS, cx.LI, cx.LBT, cx.LBm, cx.NAT, cx.NA, cx.QKm
                Tm, Wm, x1, y1, tmx, tmy = cx.Tm, cx.Wm, cx.x1, cx.y1, cx.tmx, cx.tmy
                Rm, khp, vnp, upair, wTp, EG, qd, cdp, Sb = cx.Rm, cx.khp, cx.vnp, cx.upair, cx.wTp, cx.EG, cx.qd, cx.cdp, cx.Sb
                tri = cx.tri
                kb.LD(qn[:], kb.QKVF[0:256, cols].rearrange("(hp p) t -> p hp t", p=128), [qn])
                kb.LD(kn[:], kb.QKVF[256:512, cols].rearrange("(hp p) t -> p hp t", p=128), [kn])
                kb.LD(vv[:], kb.QKVF[512:768, cols].rearrange("(hp p) t -> p hp t", p=128), [vv])
                kb.LD(gab[:], kb.ZT[cols, 512:528], [gab])
                knp = cx.knp[b]; kq16 = cx.kq16[b]
                for h in range(4):
                    kb.LD(knp[h][64 * (h % 2):64 * (h % 2) + 64, :], kb.QKVF[256 + 64 * h:256 + 64 * h + 64, cols], [knp[h]], q="pool")
                kb.LD(kq16[:, 0, :, :], kb.QKVF[256:512, cols].rearrange("(hp p) t -> p hp t", p=128), [kq16], q="pool")
                kb.LD(kq16[:, 1, :, :], kb.QKVF[0:256, cols].rearrange("(hp p) t -> p hp t", p=128), [kq16], q="pool")
                kb.TT(xa[:], gab[:, 4 * d:4 * d + 4], DTB[:, 4 * d:4 * d + 4], ALU.add, [gab, DTB], [xa])
                kb.ACT(ea[:], xa[:], AF.Exp, [xa], [ea])
                kb.ACT(ea[:], ea[:], AF.Ln, [ea], [ea], bias=kb.C("CCOL")[:, 1:2])
                kb.TT(loga[:], ea[:], NEGA[:, 4 * d:4 * d + 4], ALU.mult, [ea, NEGA], [loga])
                kb.ACT(beta[:], gab[:, 8 + 4 * d:12 + 4 * d], AF.Sigmoid, [gab], [beta])
                kb.ACT(lnb[:], beta[:], AF.Ln, [beta], [lnb])
                yield
                kb.MM(Pc[:, 0:4], tri, loga[:], True, True, [loga], [Pc])
                kb.MM(Pc[:, 4:8], cx.rem, loga[:], True, True, [loga], [Pc])
                kb.MM(Pc[:, 8:12], kb.C("ONES"), loga[:], True, True, [loga], [Pc])
                kb.CP(gtm[:], Pc[:, 0:4], [Pc], [gtm])
                kb.TS(ngt[:], Pc[:, 0:4], -1.0, None, ALU.mult, None, [Pc], [ngt])
                kb.ACT(ekr[:], Pc[:, 4:8], AF.Exp, [Pc], [ekr])
                kb.ACT(cdec[:], Pc[:, 8:12], AF.Exp, [Pc], [cdec])
                kb.ACT(eg[:], gtm[:], AF.Exp, [gtm], [eg])
                kb.TT(beg[:], beta[:], eg[:], ALU.mult, [beta, eg], [beg])
                kb.TT(gpl[:], gtm[:], lnb[:], ALU.add, [gtm, lnb], [gpl])
                yield
                kb.MM(Pd[0:4, 0:128], loga[:], tri, True, True, [loga], [Pd])
                kb.MM(Pd[0:4, 128:256], loga[:], tri, True, False, [loga], [Pd])
                kb.MM(Pd[0:4, 128:256], lnb[:], ident, False, True, [lnb], [Pd])
                kb.CP(ROWS[:, 0:256], Pd[0:4, 0:256], [Pd], [ROWS])
                kb.TS(ROWS[:, 256:384], Pd[0:4, 0:128], -1.0, None, ALU.mult, None, [Pd], [ROWS])
                yield
                kb.TT(cx.ROWSX[:], ROWS[:].rearrange("c (r i) -> c r i", r=3).unsqueeze(2).to_broadcast([4, 3, 4, 128]),
                      cx.bd4.unsqueeze(1).to_broadcast([4, 3, 4, 128]), ALU.mult, [ROWS], [cx.ROWSX])
                for (dst, ri, negm4, bias_t, bank) in ((LI, 0, cx.n_incl4, ngt, Pa), (LBT, 1, cx.n_strT4, ngt, Pb),
                                                       (LBm, 2, cx.n_str4, gpl, Pa)):
                    kb.MM(bank[:], kb.C("ONES")[0:4, :], cx.ROWSX[:, ri, :, :].rearrange("c h i -> c (h i)"), True, False,
                          [cx.ROWSX], [bank])
                    kb.MM(bank[:], cx.id16[:], negm4[:], False, True, [], [bank])
                    yield
                    for h in range(4):
                        kb.ACT(dst[:, h, :], bank[:, h * 128:(h + 1) * 128], AF.Exp, [bank, bias_t], [dst], bias=bias_t[:, h:h + 1])
                    yield
                for h in range(4):
                    hp, h2 = divmod(h, 2)
                    kb.MM(Pa[:, h * 128:(h + 1) * 128], knp[h][:], kq16[:, 0, hp, :], True, True, [knp[h], kq16], [Pa])
                    kb.MM(Pb[:, h * 128:(h + 1) * 128], knp[h][:], kq16[:, 1, hp, :], True, True, [knp[h], kq16], [Pb])
                yield
                pav = Pa[:].rearrange("p (h t) -> p h t", h=4)
                pbv = Pb[:].rearrange("p (h t) -> p h t", h=4)
                kb.STT(NAT[:], pav, -1.0, LBT[:], ALU.mult, ALU.mult, [Pa, LBT], [NAT])
                kb.STT(NA[:], pav, -1.0, LBm[:], ALU.mult, ALU.mult, [Pa, LBm], [NA])
                kb.TT(QKm[:], pbv, LI[:], ALU.mult, [Pb, LI], [QKm])
                yield
                mT = kb.C("MOFF1" if d == 0 else "MOFFT1").unsqueeze(1).to_broadcast([128, 4, 128])
                mW = kb.C("MOFFT1" if d == 0 else "MOFF1").unsqueeze(1).to_broadcast([128, 4, 128])
                kb.TT(tmx[:], NA[:], mT, ALU.mult, [NA], [tmx])
                kb.TT(tmy[:], NAT[:], mW, ALU.mult, [NAT], [tmy], eng="pool")
                kb.TT(Tm[:], tmx[:], idb, ALU.add, [tmx], [Tm])
                kb.TT(Wm[:], tmy[:], idb, ALU.add, [tmy], [Wm], eng="pool")
                yield
                for s_ in (2, 4, 8, 16, 32, 64):
                    for h in range(4):
                        kb.MM(Pa[:, h * 128:(h + 1) * 128], NAT[:, h, :], Tm[:, h, :], True, True, [NAT, Tm], [Pa])
                    for h in range(4):
                        kb.MM(Pb[:, h * 128:(h + 1) * 128], NA[:, h, :], Wm[:, h, :], True, True, [NA, Wm], [Pb])
                    yield
                    kb.CP(x1[:], pav, [Pa], [x1], eng="act")
                    kb.CP(y1[:], pbv, [Pb], [y1], eng="act")
                    yield
                    for h in range(4):
                        kb.MM(Pa[:, h * 128:(h + 1) * 128], Wm[:, h, :], x1[:, h, :], True, True, [Wm, x1], [Pa])
                    for h in range(4):
                        kb.MM(Pb[:, h * 128:(h + 1) * 128], Tm[:, h, :], y1[:, h, :], True, True, [Tm, y1], [Pb])
                    yield
                    kb.CPRED(Tm[:], cx.mT[s_][:], pav, [Pa, cx.mT[s_]], [Tm])
                    kb.CPRED(Wm[:], cx.mW[s_][:], pbv, [Pb, cx.mW[s_]], [Wm])
                    yield
                for hp in range(2):
                    kb.TR(Pc[:, 128:256], kn[:, hp, :], ident, [kn], [Pc])
                    kb.TR(Pc[:, 256:384], vv[:, hp, :], ident, [vv], [Pc])
                    for h2 in range(2):
                        h = 2 * hp + h2
                        kc = slice(64 * h2, 64 * h2 + 64)
                        vc = slice(64 * (1 - h2), 64 * (1 - h2) + 64)
                        kb.TS(Rm[h][:, kc], Pc[:, 128 + 64 * h2:128 + 64 * h2 + 64], beg[:, h:h + 1], None, ALU.mult, None,
                              [Pc, beg], [Rm[h]])
                        kb.ACT(Rm[h][:, vc], Pc[:, 256 + 64 * h2:256 + 64 * h2 + 64], AF.Copy, [Pc, beta], [Rm[h]],
                               scale=beta[:, h:h + 1])
                        kb.ACT(khp[h][:, kc], Pc[:, 128 + 64 * h2:128 + 64 * h2 + 64], AF.Copy, [Pc, ekr], [khp[h]],
                               scale=ekr[:, h:h + 1])
                    yield
                for h in range(4):
                    kb.MM(Pa[:, h * 128:(h + 1) * 128], Wm[:, h, :], Rm[h][:], True, True, [Wm, Rm[h]], [Pa])
                    kb.MM(Pb[:, h * 128:(h + 1) * 128], Rm[h][:], Wm[:, h, :], True, True, [Wm, Rm[h]], [Pb])
                yield
                for h in range(4):
                    hp, h2 = divmod(h, 2)
                    vc0 = 64 * (1 - h2)
                    kb.CP(upair[hp][:, 64 * h2:64 * h2 + 64], Pa[:, h * 128 + vc0:h * 128 + vc0 + 64], [Pa], [upair[hp]], eng="dve")
                    kb.CP(wTp[hp][64 * h2:64 * h2 + 64, :], Pb[64 * h2:64 * h2 + 64, h * 128:(h + 1) * 128], [Pb], [wTp[hp]], eng="act")
                yield
                for hp in range(2):
                    kb.MM(Pc[:, 384:512], kb.C("SELP%d" % hp)[0:4, :], ROWS[:, 0:128], True, True, [ROWS], [Pc])
                    kb.ACT(EG[hp][:], Pc[:, 384:512], AF.Exp, [Pc], [EG[hp]])
                    kb.TT(qd[hp][:], qn[:, hp, :], EG[hp][:], ALU.mult, [qn, EG[hp]], [qd[hp]], eng="pool")
                    yield
                    pws = Pc[:, hp * 128:(hp + 1) * 128]
                    kb.MM(pws, wTp[hp][:], Sb[hp][:], True, True, [wTp[hp], Sb[hp]], [Pc])
                    for h2 in range(2):
                        h = 2 * hp + h2
                        cs_ = slice(64 * h2, 64 * h2 + 64)
                        kb.TT(vnp[h][:, cs_], upair[hp][:, cs_], Pc[:, hp * 128 + 64 * h2:hp * 128 + 64 * h2 + 64],
                              ALU.subtract, [upair[hp], Pc], [vnp[h]])
                    po = Pd[:, hp * 256:hp * 256 + 128]
                    kb.MM(po, Sb[hp][:], qd[hp][:], True, False, [Sb[hp], qd[hp]], [Pd])
                    kb.MM(po, vnp[2 * hp][:], QKm[:, 2 * hp, :], False, False, [vnp[2 * hp], QKm], [Pd])
                    kb.MM(po, vnp[2 * hp + 1][:], QKm[:, 2 * hp + 1, :], False, True, [vnp[2 * hp + 1], QKm], [Pd])
                    kb.TT(OACC[:, hp, cols], OACC[:, hp, cols], po, ALU.add, [Pd], [OACC.s(n)])
                    yield
                    pkv = Pd[:, hp * 256 + 128:hp * 256 + 256]
                    kb.MM(pkv, khp[2 * hp][:], vnp[2 * hp][:], True, False, [khp[2 * hp], vnp[2 * hp]], [Pd])
                    kb.MM(pkv, khp[2 * hp + 1][:], vnp[2 * hp + 1][:], False, True, [khp[2 * hp + 1], vnp[2 * hp + 1]], [Pd])
                    kb.CP(cdp[hp][0:64, :], cdec[0:64, 2 * hp:2 * hp + 1], [cdec], [cdp[hp]])
                    kb.CP(cdp[hp][64:128, :], cdec[64:128, 2 * hp + 1:2 * hp + 2], [cdec], [cdp[hp]])
                    kb.STT(Sb[hp][:], Sb[hp][:], cdp[hp][:, 0:1], pkv, ALU.mult, ALU.add, [Sb[hp], cdp[hp], Pd], [Sb[hp]])
                    yield

            def stream(cx):
                for n in ORDER[cx.d][:kb.cfg.get("ntiles", NT)]:
                    yield from step(cx, n)
            active = [stream(cxs[0]), stream(cxs[1])]
            for _ in range(kb.cfg.get("g_off", 19)):
                next(active[0])
            while active:
                for g_ in list(active):
                    try:
                        next(g_)
                    except StopIteration:
                        active.remove(g_)
        with P.scope():
            G = P.sbuf("g_G2", [128, 1])
            for hh in range(2):
                kb.LD(G[64 * hh:64 * hh + 64, :], kb.prm["gdn_norm_g"][l].rearrange("(p o) -> p o", o=1), [G])
            finalize_gated(kb, OACC, Z_GG, G, 512, "g_")


def run_interleaved(gens, offset=0):
    active = list(gens)
    for _ in range(offset):
        try:
            next(active[0])
        except StopIteration:
            active.pop(0)
            break
    while active:
        for g_ in list(active):
            try:
                next(g_)
            except StopIteration:
                active.remove(g_)


def oacc_add(kb, OACC, hp, n, ps):
    cols = slice(n * 128, (n + 1) * 128)
    kb.TT(OACC[:, hp, cols], OACC[:, hp, cols], ps[:], ALU.add, [ps], [OACC.s(n)])


def oacc_zero(kb, OACC):
    for hp in range(2):
        kb.MS(OACC[:, hp, :], 0.0, [OACC.s(n) for n in range(NT)], eng="pool")


def mixer_hgrn2(kb, l):
    P = kb.P
    with P.scope():
        OACC = P.sbuf("h_oacc", [128, 2, S])
        oacc_zero(kb, OACC)
        with P.scope():
            LB = P.sbuf("h_LB", [128, 4]); OML = P.sbuf("h_OML", [128, 4])
            if l == 0:
                kb.MS(LB[:], 0.0, [LB]); kb.MS(OML[:], 1.0, [OML])
            else:
                lgt = P.sbuf("h_lgt", [128, 8])
                kb.LD(lgt[:], kb.prm["hgrn_lb_logits"][:].rearrange("l d (hp p) -> p (l d hp)", p=128), [lgt],
                      allow_slow_non_contiguous=True)
                kb.TT(LB[:], lgt[:, 4:8], lgt[:, 0:4], ALU.subtract, [lgt], [LB])
                kb.ACT(LB[:], LB[:], AF.Sigmoid, [LB], [LB])
                kb.TS(OML[:], LB[:], -1.0, 1.0, ALU.mult, ALU.add, [LB], [OML])

            def make(d):
                pf = "h%d_" % d
                hqb = [P.sbuf(pf + "q%d" % i, [128, 2, 128]) for i in range(2)]
                hfb = [P.sbuf(pf + "f%d" % i, [128, 2, 128]) for i in range(2)]
                Vp = [[P.sbuf(pf + "vp%d%d" % (i, h), [128, 128], BF16) for h in range(4)] for i in range(2)]
                khp = [[P.sbuf(pf + "kh%d%d" % (i, h), [128, 128], BF16) for h in range(4)] for i in range(2)]
                Qlp = [[P.sbuf(pf + "qlp%d%d" % (i, h2), [128, 128], BF16) for h2 in range(2)] for i in range(2)]
                for i in range(2):
                    for h2 in range(2):
                        kb.MS(Qlp[i][h2][:], 0.0, [Qlp[i][h2]], eng="pool")
                for i in range(2):
                    for h in range(4):
                        kb.MS(Vp[i][h][:], 0.0, [Vp[i][h]], eng="pool")
                        kb.MS(khp[i][h][:], 0.0, [khp[i][h]], eng="pool")
                MREF = [P.sbuf(pf + "mr%d" % i, [128, 4]) for i in range(2)]
                for i in range(2):
                    kb.MS(MREF[i][:], 0.0, [MREF[i]])

                def two(name, shape=(128, 128)):
                    return [P.sbuf(pf + "%s%d" % (name, i), list(shape)) for i in range(2)]
                qs, sgm, ff, logf, kk, bb, pre = two("qs"), two("sg"), two("ff"), two("lf"), two("kk"), two("bb"), two("pre")
                e1, Ql, e2, Qd = two("e1"), two("Ql"), two("e2"), two("Qd")
                Kt = [[P.sbuf(pf + "Kt%d%d" % (r, i), [128, 128], BF16) for i in range(2)] for r in range(4)]
                ex = two("ex")
                AT = [P.sbuf(pf + "AT%d" % i, [128, 2, 128], BF16) for i in range(2)]
                KhT = two("KhT")
                bend = two("bend", (128, 2))
                Sb = [P.sbuf(pf + "S%d" % hp, [128, 128]) for hp in range(2)]
                for hp in range(2):
                    kb.MS(Sb[hp][:], 0.0, [Sb[hp]])
                pss = P.psum(pf + "pss", [128, 2, 128])
                po = P.psum(pf + "pso", [128, 128])
                pk = P.psum(pf + "psk", [128, 128])
                pkv = P.psum(pf + "pskv", [128, 128])
                zf = Z_HFF if d == 0 else Z_HFB
                tri = kb.C("TRIF" if d == 0 else "TRIB").unsqueeze(1).to_broadcast([128, 2, 128])

                def gen():
                    it = 0
                    jj = 0
                    for n in ORDER[d]:
                        cols = slice(n * 128, (n + 1) * 128)
                        b = it % 2; it += 1
                        hq, hf = hqb[b], hfb[b]
                        kb.LD(hq[:], kb.ZF[Z_HQ:Z_HQ + 256, cols].rearrange("(hp p) t -> p hp t", p=128), [hq])
                        kb.LD(hf[:], kb.ZF[zf:zf + 256, cols].rearrange("(hp p) t -> p hp t", p=128), [hf])
                        for h in range(4):
                            kb.LD(Vp[b][h][:, 64 * (h % 2):64 * (h % 2) + 64], kb.ZT[cols, 64 * h:64 * h + 64], [Vp[b][h]], q="pool")
                        yield
                        for hp in range(2):
                            j = jj % 2; jj += 1
                            c = 2 * d + hp
                            mref = MREF[j]
                            kb.ACT(qs[j][:], hq[:, hp, :], AF.Exp, [hq], [qs[j]], scale=-1.0)
                            kb.TS(qs[j][:], qs[j][:], 1.0, None, ALU.add, None, [qs[j]], [qs[j]])
                            kb.RECIP(qs[j][:], qs[j][:], [qs[j]], [qs[j]])
                            kb.TT(qs[j][:], qs[j][:], hq[:, hp, :], ALU.mult, [qs[j], hq], [qs[j]], eng="pool")
                            kb.ACT(sgm[j][:], hf[:, hp, :], AF.Exp, [hf], [sgm[j]], scale=-1.0)
                            kb.TS(sgm[j][:], sgm[j][:], 1.0, None, ALU.add, None, [sgm[j]], [sgm[j]])
                            kb.RECIP(sgm[j][:], sgm[j][:], [sgm[j]], [sgm[j]])
                            kb.TS(ff[j][:], sgm[j][:], OML[:, c:c + 1], LB[:, c:c + 1], ALU.mult, ALU.add, [sgm[j], OML, LB], [ff[j]])
                            kb.ACT(logf[j][:], ff[j][:], AF.Ln, [ff[j]], [logf[j]])
                            kb.TS(kk[j][:], ff[j][:], -1.0, 1.0, ALU.mult, ALU.add, [ff[j]], [kk[j]], eng="pool")
                            yield
                            B = bb[j]
                            if d == 0:
                                kb.SCAN(B[:], kb.C("ONES"), logf[j][:], [logf[j]], [B])
                                kb.CP(mref[:, 1:4], B[:].rearrange("p (r c) -> p r c", c=32)[:, 0:3, 31], [B], [mref])
                                be = B[:, 127:128]
                            else:
                                kb.SCAN(pre[j][:], kb.C("ONES"), logf[j][:], [logf[j]], [pre[j]])
                                kb.STT(B[:], pre[j][:], -1.0, logf[j][:], ALU.mult, ALU.add, [pre[j], logf[j]], [B])
                                kb.TS(B[:], B[:], pre[j][:, 127:128], None, ALU.add, None, [B, pre[j]], [B])
                                kb.CP(mref[:, 0:3], B[:].rearrange("p (r c) -> p r c", c=32)[:, 1:4, 0], [B], [mref])
                                be = B[:, 0:1]
                            yield
                            kb.TT(e1[j][:].rearrange("p (r c) -> p r c", c=32), B[:].rearrange("p (r c) -> p r c", c=32),
                                  mref[:].unsqueeze(2).to_broadcast([128, 4, 32]), ALU.subtract, [B, mref], [e1[j]])
                            kb.ACT(e1[j][:], e1[j][:], AF.Exp, [e1[j]], [e1[j]])
                            for h2 in range(2):
                                rs_ = slice(64 * h2, 64 * h2 + 64)
                                kb.STT(Qlp[j][h2][rs_, :], qs[j][rs_, :], 0.125, e1[j][rs_, :], ALU.mult, ALU.mult,
                                       [qs[j], e1[j]], [Qlp[j][h2]])
                            kb.ACT(e2[j][:], B[:], AF.Exp, [B], [e2[j]])
                            kb.STT(Qd[j][:], qs[j][:], 0.125, e2[j][:], ALU.mult, ALU.mult, [qs[j], e2[j]], [Qd[j]])
                            yield
                            for r in range(4):
                                kb.ACT(ex[j][:], B[:], AF.Exp, [B, mref], [ex[j]], scale=-1.0, bias=mref[:, r:r + 1])
                                kb.STT(Kt[r][j][:], ex[j][:], 1e26, kk[j][:], ALU.min, ALU.mult, [ex[j], kk[j]], [Kt[r][j]])
                                for h2 in range(2):
                                    kb.MM(pss[:, h2, 32 * r:32 * r + 32], Kt[r][j][:],
                                          Qlp[j][h2][:, 32 * r:32 * r + 32], True, True,
                                          [Kt[r][j], Qlp[j][h2]], [pss])
                                yield
                            kb.TT(AT[j][:], pss[:], tri, ALU.mult, [pss], [AT[j]])
                            yield
                            kb.MM(po[:], Vp[b][2 * hp][:], AT[j][:, 0, :], True, False, [Vp[b][2 * hp], AT[j]], [po])
                            kb.MM(po[:], Vp[b][2 * hp + 1][:], AT[j][:, 1, :], False, False, [Vp[b][2 * hp + 1], AT[j]], [po])
                            kb.MM(po[:], Sb[hp][:], Qd[j][:], False, True, [Sb[hp], Qd[j]], [po])
                            oacc_add(kb, OACC, hp, n, po)
                            kb.CP(bend[j][:, 0:1], be, [B], [bend[j]])
                            kb.ACT(KhT[j][:], B[:], AF.Exp, [B, bend[j]], [KhT[j]], scale=-1.0, bias=bend[j][:, 0:1])
                            kb.TT(KhT[j][:], KhT[j][:], kk[j][:], ALU.mult, [KhT[j], kk[j]], [KhT[j]], eng="pool")
                            kb.ACT(bend[j][:, 1:2], bend[j][:, 0:1], AF.Exp, [bend[j]], [bend[j]])
                            yield
                            kb.TR(pk[:], KhT[j][:], kb.C("IDENT"), [KhT[j]], [pk])
                            for h2 in range(2):
                                h = 2 * hp + h2
                                kb.CP(khp[b][h][:, 64 * h2:64 * h2 + 64], pk[:, 64 * h2:64 * h2 + 64], [pk], [khp[b][h]],
                                      eng=("act" if h2 else "dve"))
                            yield
                            kb.MM(pkv[:], khp[b][2 * hp][:], Vp[b][2 * hp][:], True, False, [khp[b][2 * hp], Vp[b][2 * hp]], [pkv])
                            kb.MM(pkv[:], khp[b][2 * hp + 1][:], Vp[b][2 * hp + 1][:], False, True,
                                  [khp[b][2 * hp + 1], Vp[b][2 * hp + 1]], [pkv])
                            kb.STT(Sb[hp][:], Sb[hp][:], bend[j][:, 1:2], pkv[:], ALU.mult, ALU.add,
                                   [Sb[hp], bend[j], pkv], [Sb[hp]])
                            yield
                return gen()
            run_interleaved([make(0), make(1)], offset=kb.cfg.get("h_off", 11))
        with P.scope():
            G = P.sbuf("h_G2", [128, 1])
            for hh in range(2):
                kb.LD(G[64 * hh:64 * hh + 64, :], kb.prm["hgrn_norm_g"][l].rearrange("(p o) -> p o", o=1), [G])
            finalize_gated(kb, OACC, Z_HG, G, 0, "h_")


def mixer_ret2(kb, l):
    P = kb.P
    with P.scope():
        OACC = P.sbuf("r_oacc", [128, 2, S])
        oacc_zero(kb, OACC)
        with P.scope():
            lgt = P.sbuf("r_lgt", [128, 8])
            kb.LD(lgt[:], kb.prm["ret_decay_logit"][l].rearrange("d h -> (d h)").partition_broadcast(128), [lgt])
            LG = P.sbuf("r_LG", [128, 8])
            kb.ACT(LG[:], lgt[:], AF.Sigmoid, [lgt], [LG])
            kb.ACT(LG[:], LG[:], AF.Ln, [LG], [LG])
            LGP = P.sbuf("r_LGP", [128, 4])
            for d in range(2):
                for hp in range(2):
                    c = 2 * d + hp
                    kb.CP(LGP[0:64, c:c + 1], LG[0:64, 4 * d + 2 * hp:4 * d + 2 * hp + 1], [LG], [LGP])
                    kb.CP(LGP[64:128, c:c + 1], LG[64:128, 4 * d + 2 * hp + 1:4 * d + 2 * hp + 2], [LG], [LGP])
            MK = [P.sbuf("r_MK%d" % d, [128, 4, 128]) for d in range(2)]
            QDEC = [[P.sbuf("r_QD%d%d" % (d, hp), [128, 128]) for hp in range(2)] for d in range(2)]
            etmp = P.sbuf("r_etmp", [128, 128])
            for d in range(2):
                for h in range(4):
                    kb.ACT(etmp[:], kb.C("DIFF" if d == 0 else "NDIFF"), AF.Exp, [LG], [etmp],
                           scale=LG[:, 4 * d + h:4 * d + h + 1])
                    kb.STT(MK[d][:, h, :], etmp[:], 0.125, kb.C("TRIF" if d == 0 else "TRIB"), ALU.mult, ALU.mult,
                           [etmp], [MK[d]])
                for hp in range(2):
                    kb.ACT(QDEC[d][hp][:], kb.C("IOTAF1" if d == 0 else "RIOTAF"), AF.Exp, [LGP], [QDEC[d][hp]],
                           scale=LGP[:, 2 * d + hp:2 * d + hp + 1])
            KD = P.sbuf("r_KD", [128, 8])
            kb.ACT(KD[:, 0:4], LG[:, 0:4], AF.Exp, [LG], [KD], scale=kb.C("CCOL")[:, 3:4])
            kb.ACT(KD[:, 4:8], LG[:, 4:8], AF.Exp, [LG], [KD], scale=kb.C("CCOL")[:, 2:3])
            kb.TS(KD[:], KD[:], 0.125, None, ALU.mult, None, [KD], [KD])
            CV = P.sbuf("r_CV", [128, 4])
            kb.ACT(CV[:], LGP[:], AF.Exp, [LGP], [CV], scale=128.0)

            def make(d):
                pf = "r%d_" % d
                qTb = [P.sbuf(pf + "q%d" % i, [128, 2, 128]) for i in range(2)]
                kTb = [P.sbuf(pf + "k%d" % i, [128, 2, 128]) for i in range(2)]
                csb = [P.sbuf(pf + "cs%d" % i, [128, 2, 128]) for i in range(2)]
                Vp = [[P.sbuf(pf + "vp%d%d" % (i, h), [128, 128], BF16) for h in range(4)] for i in range(2)]
                khp = [[P.sbuf(pf + "kh%d%d" % (i, h), [128, 128], BF16) for h in range(4)] for i in range(2)]
                qrp = [[P.sbuf(pf + "qrp%d%d" % (i, h2), [128, 128], BF16) for h2 in range(2)] for i in range(2)]
                for i in range(2):
                    for h2 in range(2):
                        kb.MS(qrp[i][h2][:], 0.0, [qrp[i][h2]], eng="pool")
                kr16 = [P.sbuf(pf + "kr16%d" % i, [128, 128], BF16) for i in range(2)]
                for i in range(2):
                    for h in range(4):
                        kb.MS(Vp[i][h][:], 0.0, [Vp[i][h]], eng="pool")
                        kb.MS(khp[i][h][:], 0.0, [khp[i][h]], eng="pool")
                t1 = [P.sbuf(pf + "t1%d" % i, [128, 128]) for i in range(2)]
                t2 = [P.sbuf(pf + "t2%d" % i, [128, 128]) for i in range(2)]
                qr = [P.sbuf(pf + "qr%d" % i, [128, 2, 128]) for i in range(2)]
                kr = [P.sbuf(pf + "kr%d" % i, [128, 2, 128]) for i in range(2)]
                AT = [P.sbuf(pf + "AT%d" % i, [128, 2, 128], BF16) for i in range(2)]
                qd = [P.sbuf(pf + "qd%d" % i, [128, 128]) for i in range(2)]
                Sb = [P.sbuf(pf + "S%d" % hp, [128, 128]) for hp in range(2)]
                for hp in range(2):
                    kb.MS(Sb[hp][:], 0.0, [Sb[hp]])
                pr = P.psum(pf + "psr", [128, 256])
                pss = P.psum(pf + "pss", [128, 2, 128])
                po = P.psum(pf + "pso", [128, 128])
                pkk = P.psum(pf + "pskk", [128, 256])

                def gen():
                    it = 0
                    jj = 0
                    for n in ORDER[d]:
                        cols = slice(n * 128, (n + 1) * 128)
                        b = it % 2; it += 1
                        qT, kT, cs = qTb[b], kTb[b], csb[b]
                        kb.LD(qT[:], kb.ZF[Z_RQ:Z_RQ + 256, cols].rearrange("(hp p) t -> p hp t", p=128), [qT])
                        kb.LD(kT[:], kb.ZF[Z_RK:Z_RK + 256, cols].rearrange("(hp p) t -> p hp t", p=128), [kT])
                        kb.LD(cs[:, 0, :], kb.ropec[:, cols], [cs])
                        kb.LD(cs[:, 1, :], kb.ropes[:, cols], [cs])
                        for h in range(4):
                            kb.LD(Vp[b][h][:, 64 * (h % 2):64 * (h % 2) + 64], kb.ZT[cols, 256 + 64 * h:256 + 64 * h + 64],
                                  [Vp[b][h]], q="pool")
                        yield
                        for hp in range(2):
                            j = jj % 2; jj += 1
                            kb.MM(pr[:, 0:128], kb.C("ROT"), qT[:, hp, :], True, True, [qT], [pr])
                            kb.MM(pr[:, 128:256], kb.C("ROT"), kT[:, hp, :], True, True, [kT], [pr])
                            yield
                            for (src_, dst, off) in ((qT, qr[b], 0), (kT, kr[b], 128)):
                                kb.TT(t1[j][:], src_[:, hp, :], cs[:, 0, :], ALU.mult, [src_, cs], [t1[j]])
                                kb.TT(t2[j][:], pr[:, off:off + 128], cs[:, 1, :], ALU.mult, [pr, cs], [t2[j]])
                                kb.TT(dst[:, hp, :], t1[j][:], t2[j][:], ALU.add, [t1[j], t2[j]], [dst.s(hp)], eng="pool")
                                yield
                            kb.CP(kr16[j][:], kr[b][:, hp, :], [kr[b].s(hp)], [kr16[j]], eng="act")
                            for h2 in range(2):
                                rs_ = slice(64 * h2, 64 * h2 + 64)
                                kb.CP(qrp[j][h2][rs_, :], qr[b][rs_, hp, :], [qr[b].s(hp)], [qrp[j][h2]], eng="act")
                            yield
                            for h2 in range(2):
                                kb.MM(pss[:, h2, :], kr16[j][:], qrp[j][h2][:], True, True, [kr16[j], qrp[j][h2]], [pss])
                            yield
                            kb.TT(AT[j][:], pss[:], MK[d][:, 2 * hp:2 * hp + 2, :], ALU.mult, [pss, MK[d]], [AT[j]])
                            kb.TT(qd[j][:], qr[b][:, hp, :], QDEC[d][hp][:], ALU.mult, [qr[b].s(hp), QDEC[d][hp]], [qd[j]],
                                  eng="pool")
                            yield
                            kb.MM(po[:], Vp[b][2 * hp][:], AT[j][:, 0, :], True, False, [Vp[b][2 * hp], AT[j]], [po])
                            kb.MM(po[:], Vp[b][2 * hp + 1][:], AT[j][:, 1, :], False, False, [Vp[b][2 * hp + 1], AT[j]], [po])
                            kb.MM(po[:], Sb[hp][:], qd[j][:], False, True, [Sb[hp], qd[j]], [po])
                            kb.TR(pkk[:, 0:128], kr[b][:, hp, :], kb.C("IDENT"), [kr[b].s(hp)], [pkk])
                            yield
                            oacc_add(kb, OACC, hp, n, po)
                            for h2 in range(2):
                                h = 2 * hp + h2
                                kb.ACT(khp[b][h][:, 64 * h2:64 * h2 + 64], pkk[:, 64 * h2:64 * h2 + 64], AF.Copy,
                                       [pkk, KD], [khp[b][h]], scale=KD[:, 4 * d + h:4 * d + h + 1])
                            yield
                            kb.MM(pkk[:, 128:256], khp[b][2 * hp][:], Vp[b][2 * hp][:], True, False,
                                  [khp[b][2 * hp], Vp[b][2 * hp]], [pkk])
                            kb.MM(pkk[:, 128:256], khp[b][2 * hp + 1][:], Vp[b][2 * hp + 1][:], False, True,
                                  [khp[b][2 * hp + 1], Vp[b][2 * hp + 1]], [pkk])
                            kb.STT(Sb[hp][:], Sb[hp][:], CV[:, 2 * d + hp:2 * d + hp + 1], pkk[:, 128:256], ALU.mult, ALU.add,
                                   [Sb[hp], CV, pkk], [Sb[hp]])
                            yield
                return gen()
            run_interleaved([make(0), make(1)], offset=kb.cfg.get("r_off", 8))
        with P.scope():
            finalize_gated(kb, OACC, Z_RG, None, 256, "r_")


def s5_tables(kb, l, d, VFr, VFi, T1, T2, AR, NAI):
    P = kb.P
    prm = kb.prm
    with P.scope():
        lr = P.sbuf("s_lr", [128, 16, 64]); li = P.sbuf("s_li", [128, 16, 64]); dtb = P.sbuf("s_dt", [128, 16])
        kb.LD(lr[:], prm["s5_lam_re"][l][d].rearrange("g p -> (g p)").partition_broadcast(128), [lr])
        kb.LD(li[:], prm["s5_lam_im"][l][d].rearrange("g p -> (g p)").partition_broadcast(128), [li])
        kb.LD(dtb[:], prm["s5_log_dt"][l][d].partition_broadcast(128), [dtb])
        kb.ACT(dtb[:], dtb[:], AF.Exp, [dtb], [dtb])
        dt_bc = dtb[:].unsqueeze(2).to_broadcast([128, 16, 64])
        lrdt = P.sbuf("s_lrdt", [128, 16, 64]); lidt = P.sbuf("s_lidt", [128, 16, 64])
        kb.TT(lrdt[:], lr[:], dt_bc, ALU.mult, [lr, dtb], [lrdt])
        kb.TT(lidt[:], li[:], dt_bc, ALU.mult, [li, dtb], [lidt])
        a = [P.sbuf("s_a%d" % i, [128, 16, 64]) for i in range(8)]
        mag, ang, sn, cs, tmp, ar, ai, t2 = a
        kb.ACT(mag[:], lrdt[:], AF.Exp, [lrdt], [mag])
        _sincos(kb, lidt[:], sn[:], cs[:], [lidt, sn, cs, tmp], tmp[:])
        kb.TT(ar[:], mag[:], cs[:], ALU.mult, [mag, cs], [ar])
        kb.TT(ai[:], mag[:], sn[:], ALU.mult, [mag, sn], [ai])
        den = P.sbuf("s_den", [128, 16, 64]); fr = P.sbuf("s_fr", [128, 16, 64]); fi = P.sbuf("s_fi", [128, 16, 64])
        kb.TT(den[:], lr[:], lr[:], ALU.mult, [lr], [den])
        kb.TT(t2[:], li[:], li[:], ALU.mult, [li], [t2])
        kb.TT(den[:], den[:], t2[:], ALU.add, [den, t2], [den])
        kb.RECIP(den[:], den[:], [den], [den])
        kb.TS(ar[:], ar[:], -1.0, None, ALU.add, None, [ar], [ar])
        kb.TT(fr[:], ar[:], lr[:], ALU.mult, [ar, lr], [fr])
        kb.TT(t2[:], ai[:], li[:], ALU.mult, [ai, li], [t2])
        kb.TT(fr[:], fr[:], t2[:], ALU.add, [fr, t2], [fr])
        kb.TT(fr[:], fr[:], den[:], ALU.mult, [fr, den], [fr])
        kb.TT(fi[:], ai[:], lr[:], ALU.mult, [ai, lr], [fi])
        kb.TT(t2[:], ar[:], li[:], ALU.mult, [ar, li], [t2])
        kb.TT(fi[:], fi[:], t2[:], ALU.subtract, [fi, t2], [fi])
        kb.TT(fi[:], fi[:], den[:], ALU.mult, [fi, den], [fi])
        jcol = kb.C("CCOL")[:, 2:3] if d == 0 else kb.C("CCOL")[:, 3:4]
        njcol = kb.C("CCOL")[:, 6:7] if d == 0 else kb.C("CCOL")[:, 7:8]
        kb.ACT(mag[:], lrdt[:], AF.Exp, [lrdt], [mag], scale=njcol)
        kb.TS(ang[:], lidt[:], jcol, None, ALU.mult, None, [lidt], [ang])
        _sincos(kb, ang[:], sn[:], cs[:], [ang, sn, cs, tmp], tmp[:])
        vr, vi = ar, ai
        kb.TT(vr[:], mag[:], cs[:], ALU.mult, [mag, cs], [vr])
        kb.TT(vi[:], mag[:], sn[:], ALU.mult, [mag, sn], [vi])
        kb.TS(vi[:], vi[:], -1.0, None, ALU.mult, None, [vi], [vi])
        kb.TT(VFr[:], vr[:], fr[:], ALU.mult, [vr, fr], [VFr])
        kb.TT(t2[:], vi[:], fi[:], ALU.mult, [vi, fi], [t2])
        kb.TT(VFr[:], VFr[:], t2[:], ALU.subtract, [VFr, t2], [VFr])
        kb.TT(VFi[:], vr[:], fi[:], ALU.mult, [vr, fi], [VFi])
        kb.TT(t2[:], vi[:], fr[:], ALU.mult, [vi, fr], [t2])
        kb.TT(VFi[:], VFi[:], t2[:], ALU.add, [VFi, t2], [VFi])
    with P.scope():
        dtb = P.sbuf("s_dt2", [128, 16])
        kb.LD(dtb[:], prm["s5_log_dt"][l][d].partition_broadcast(128), [dtb])
        kb.ACT(dtb[:], dtb[:], AF.Exp, [dtb], [dtb])
        lrp = P.sbuf("s_lrp", [128, 16]); lip = P.sbuf("s_lip", [128, 16])
        for hh in range(2):
            kb.LD(lrp[64 * hh:64 * hh + 64, :], prm["s5_lam_re"][l][d].rearrange("g p -> p g"), [lrp],
                  allow_slow_non_contiguous=True)
            kb.LD(lip[64 * hh:64 * hh + 64, :], prm["s5_lam_im"][l][d].rearrange("g p -> p g"), [lip],
                  allow_slow_non_contiguous=True)
        kb.TT(lrp[:], lrp[:], dtb[:], ALU.mult, [lrp, dtb], [lrp])
        kb.TT(lip[:], lip[:], dtb[:], ALU.mult, [lip, dtb], [lip])
        b4 = [P.sbuf("s_b%d" % i, [128, 16, 128]) for i in range(4)]
        arg, sn2, cs2, tmp2 = b4
        mt = kb.C("IOTAF" if d == 0 else "R127F")
        mt_bc = mt.unsqueeze(1).to_broadcast([128, 16, 128])
        kb.TT(arg[:], lrp[:].unsqueeze(2).to_broadcast([128, 16, 128]), mt_bc, ALU.mult, [lrp], [arg])
        kb.ACT(T1[:], arg[:], AF.Exp, [arg], [T1])
        kb.TT(arg[:], lip[:].unsqueeze(2).to_broadcast([128, 16, 128]), mt_bc, ALU.mult, [lip, T1], [arg])
        _sincos(kb, arg[:], sn2[:], cs2[:], [arg, sn2, cs2, tmp2], tmp2[:])
        kb.TT(T2[:], T1[:], sn2[:], ALU.mult, [T1, sn2], [T2])
        kb.TS(T2[:], T2[:], -1.0, None, ALU.mult, None, [T2], [T2])
        kb.TT(T1[:], T1[:], cs2[:], ALU.mult, [T1, cs2], [T1])
        c4 = [P.sbuf("s_c%d" % i, [128, 16]) for i in range(4)]
        kb.ACT(c4[0][:], lrp[:], AF.Exp, [lrp], [c4[0]])
        _sincos(kb, lip[:], c4[1][:], c4[2][:], [lip, c4[1], c4[2], c4[3]], c4[3][:])
        kb.TT(AR[:], c4[0][:], c4[2][:], ALU.mult, [c4[0], c4[2]], [AR])
        kb.TT(NAI[:], c4[0][:], c4[1][:], ALU.mult, [c4[0], c4[1]], [NAI])
        kb.TS(NAI[:], NAI[:], -1.0, None, ALU.mult, None, [NAI], [NAI])


def mixer_s5_2(kb, l):
    P = kb.P
    prm = kb.prm
    with P.scope():
        WX = P.sbuf("s_WX", [128, 2, 8, 2, 64])
        Cblk = P.sbuf("s_Cblk", [128, 16, 128])
        kb.MS(WX[:], 0.0, [WX], eng="pool")
        kb.MS(Cblk[:], 0.0, [Cblk], eng="pool")
        for g8 in range(8):
            for ri, nm in enumerate(("s5_b_re", "s5_b_im")):
                for gg in range(2):
                    src = prm[nm][l][8 * gg + g8].rearrange("p c -> c p")
                    kb.LD(WX[16 * g8:16 * g8 + 16, gg, g8, ri, :], src, [WX], allow_slow_non_contiguous=True)
        for g in range(16):
            g8 = g % 8
            kb.LD(Cblk[0:64, g, 16 * g8:16 * g8 + 16], prm["s5_c_re"][l][g].rearrange("c p -> p c"), [Cblk],
                  allow_slow_non_contiguous=True)
            kb.LD(Cblk[64:128, g, 16 * g8:16 * g8 + 16], prm["s5_c_im"][l][g].rearrange("c p -> p c"), [Cblk],
                  allow_slow_non_contiguous=True)
        kb.TS(Cblk[64:128, :, :], Cblk[64:128, :, :], -1.0, None, ALU.mult, None, [Cblk], [Cblk])
        Cb16 = P.sbuf("s_Cb16", [128, 16, 128], BF16)
        kb.CP(Cb16[:], Cblk[:], [Cblk], [Cb16])

        tabs = []
        for d in range(2):
            VFr = P.sbuf("s_VFr%d" % d, [128, 16, 64]); VFi = P.sbuf("s_VFi%d" % d, [128, 16, 64])
            T1 = P.sbuf("s_T1%d" % d, [128, 16, 128]); T2 = P.sbuf("s_T2%d" % d, [128, 16, 128])
            AR = P.sbuf("s_AR%d" % d, [128, 16]); NAI = P.sbuf("s_NAI%d" % d, [128, 16])
            s5_tables(kb, l, d, VFr, VFi, T1, T2, AR, NAI)
            tabs.append((VFr, VFi, T1, T2, AR, NAI))
        OACC = P.sbuf("s_oacc", [128, 2, S])
        oacc_zero(kb, OACC)
        with P.scope():
            def make(d):
                pf = "s%d_" % d
                VFr, VFi, T1, T2, AR, NAI = tabs[d]
                uTb = [P.sbuf(pf + "u%d" % i, [128, 2, 128]) for i in range(2)]
                mm_ = [P.sbuf(pf + "m%d" % i, [128, 4, 64]) for i in range(4)]
                W3 = [P.sbuf(pf + "W3%d" % i, [128, 4, 3, 64], BF16) for i in range(2)]
                tP = P.sbuf(pf + "tP", [128, 4, 128]); tPs = P.sbuf(pf + "tPs", [128, 4, 128])
                H1 = P.sbuf(pf + "H1", [128, 4, 128]); H2 = P.sbuf(pf + "H2", [128, 4, 128])
                Hb = [P.sbuf(pf + "Hb%d" % i, [128, 4, 128], BF16) for i in range(2)]
                tri16 = P.sbuf(pf + "tri16", [128, 128], BF16)
                kb.CP(tri16[:], kb.C("TRIF" if d == 0 else "TRIB"), [], [tri16])
                hend = P.sbuf(pf + "hend", [128, 16]); hsend = P.sbuf(pf + "hsend", [128, 16])
                hp_ = P.sbuf(pf + "hp", [128, 16]); hps_ = P.sbuf(pf + "hps", [128, 16])
                sm = [P.sbuf(pf + "sm%d" % i, [128, 16]) for i in range(4)]
                kb.MS(hp_[:], 0.0, [hp_]); kb.MS(hps_[:], 0.0, [hps_])
                xps = P.psum(pf + "xps", [128, 512])
                pps = P.psum(pf + "pps", [128, 4, 128])
                ppss = P.psum(pf + "ppss", [128, 4, 128])
                yps = P.psum(pf + "yps", [128, 128])
                te = 127 if d == 0 else 0

                def gen():
                    it = 0
                    kq = 0
                    for n in ORDER[d]:
                        uT = uTb[it % 2]; it += 1
                        cols = slice(n * 128, (n + 1) * 128)
                        kb.LD(uT[:], kb.ZF[Z_SU:Z_SU + 256, cols].rearrange("(gg p) t -> p gg t", p=128), [uT])
                        yield
                        for q in range(4):
                            gg, qq = divmod(q, 2)
                            gs = slice(4 * q, 4 * q + 4)
                            w3 = W3[kq % 2]; hb = Hb[kq % 2]; kq += 1
                            kb.MM(xps[:], uT[:, gg, :], WX[:, gg, 4 * qq:4 * qq + 4, :, :].rearrange("q a r p -> q (a r p)"),
                                  True, True, [uT, WX], [xps])
                            xv = xps[:].rearrange("t (g r p) -> t g r p", r=2, p=64)
                            kb.TT(mm_[0][:], xv[:, :, 0, :], VFr[:, gs, :], ALU.mult, [xps, VFr], [mm_[0]])
                            kb.TT(mm_[1][:], xv[:, :, 1, :], VFi[:, gs, :], ALU.mult, [xps, VFi], [mm_[1]])
                            kb.TT(mm_[2][:], xv[:, :, 0, :], VFi[:, gs, :], ALU.mult, [xps, VFi], [mm_[2]])
                            kb.TT(mm_[3][:], xv[:, :, 1, :], VFr[:, gs, :], ALU.mult, [xps, VFr], [mm_[3]])
                            yield
                            kb.TT(w3[:, :, 0, :], mm_[0][:], mm_[1][:], ALU.subtract, [mm_[0], mm_[1]], [w3])
                            kb.TT(w3[:, :, 1, :], mm_[2][:], mm_[3][:], ALU.add, [mm_[2], mm_[3]], [w3], eng="pool")
                            kb.TT(w3[:, :, 2, :], mm_[1][:], mm_[0][:], ALU.subtract, [mm_[0], mm_[1]], [w3])
                            yield
                            for i in range(4):
                                kb.MM(pps[:, i, :], w3[:, i, 0:2, :].rearrange("q r p -> q (r p)"), tri16[:], True, True, [w3, tri16], [pps])
                            for i in range(4):
                                kb.MM(ppss[:, i, :], w3[:, i, 1:3, :].rearrange("q r p -> q (r p)"), tri16[:], True, True, [w3, tri16], [ppss])
                            yield
                            kb.TT(tP[:], pps[:], hp_[:, gs].unsqueeze(2).to_broadcast([128, 4, 128]), ALU.add, [pps, hp_], [tP])
                            for i in range(4):
                                g = 4 * q + i
                                kb.ACT(tPs[:, i, :], ppss[:, i, :], AF.Identity, [ppss, hps_], [tPs], bias=hps_[:, g:g + 1])
                            yield
                            kb.TT(sm[0][:, 0:4], tPs[:, :, te], T1[:, gs, te], ALU.mult, [tPs, T1], [sm[0]])
                            kb.TT(sm[1][:, 0:4], tP[:, :, te], T2[:, gs, te], ALU.mult, [tP, T2], [sm[1]])
                            kb.TT(hsend[:, gs], sm[0][:, 0:4], sm[1][:, 0:4], ALU.subtract, [sm[0], sm[1]], [hsend])
                            kb.TT(sm[2][:, 0:4], tP[:, :, te], T1[:, gs, te], ALU.mult, [tP, T1], [sm[2]])
                            kb.TT(sm[3][:, 0:4], tPs[:, :, te], T2[:, gs, te], ALU.mult, [tPs, T2], [sm[3]])
                            kb.TT(hend[:, gs], sm[2][:, 0:4], sm[3][:, 0:4], ALU.add, [sm[2], sm[3]], [hend])
                            yield
                            kb.TT(H1[:], tP[:], T1[:, gs, :], ALU.mult, [tP, T1], [H1], eng="pool")
                            kb.TT(H2[:], tPs[:], T2[:, gs, :], ALU.mult, [tPs, T2], [H2])
                            yield
                            kb.TT(hb[:], H1[:], H2[:], ALU.add, [H1, H2], [hb])
                            yield
                            for i in range(4):
                                g = 4 * q + i
                                kb.MM(yps[:], Cb16[:, g, :], hb[:, i, :], (g % 8) == 0, (g % 8) == 7, [Cb16, hb], [yps])
                            if qq == 1:
                                oacc_add(kb, OACC, gg, n, yps)
                            yield
                        kb.TT(sm[0][:], hend[:], AR[:], ALU.mult, [hend, AR], [sm[0]])
                        kb.TT(sm[1][:], hsend[:], NAI[:], ALU.mult, [hsend, NAI], [sm[1]])
                        kb.TT(sm[2][:], hsend[:], AR[:], ALU.mult, [hsend, AR], [sm[2]])
                        kb.TT(sm[3][:], hend[:], NAI[:], ALU.mult, [hend, NAI], [sm[3]])
                        kb.TT(hp_[:], sm[0][:], sm[1][:], ALU.add, [sm[0], sm[1]], [hp_])
                        kb.TT(hps_[:], sm[2][:], sm[3][:], ALU.subtract, [sm[2], sm[3]], [hps_])
                        yield
                return gen()
            run_interleaved([make(0), make(1)], offset=kb.cfg.get("s_off", 17))
        with P.scope():
            dsk = P.sbuf("s_dsk", [128, 2]); glb = P.sbuf("s_glb", [128, 2])
            kb.LD(dsk[:], prm["s5_d"][l].rearrange("(gg p) -> p gg", p=128), [dsk], allow_slow_non_contiguous=True)
            kb.LD(glb[:], prm["s5_glu_b"][l].rearrange("(gg p) -> p gg", p=128), [glb], allow_slow_non_contiguous=True)
            gw = P.sbuf("s_gw", [128, 2, 256])
            kb.LD(gw[:], prm["s5_glu_w"][l].rearrange("(ct p) o -> p ct o", p=128), [gw])
            uTb = [P.sbuf("s_fu%d" % i, [128, 2, 128]) for i in range(2)]
            yy = [P.sbuf("s_yy%d" % i, [128, 2, 128]) for i in range(2)]
            x2 = [P.sbuf("s_x2%d" % i, [128, 2, 128]) for i in range(2)]
            th = [P.sbuf("s_th%d" % i, [128, 2, 128]) for i in range(2)]
            sgb = [P.sbuf("s_sg%d" % i, [128, 128]) for i in range(2)]
            ob = [P.sbuf("s_ob%d" % i, [128, 128]) for i in range(2)]
            psz = [P.psum("s_psz%d" % i, [128, 128]) for i in range(2)]
            k = 0
            for n in range(NT):
                cols = slice(n * 128, (n + 1) * 128)
                i = n % 2
                kb.LD(uTb[i][:], kb.ZF[Z_SU:Z_SU + 256, cols].rearrange("(gg p) t -> p gg t", p=128), [uTb[i]])
                for gg in range(2):
                    kb.STT(yy[i][:, gg, :], uTb[i][:, gg, :], dsk[:, gg:gg + 1], OACC[:, gg, cols], ALU.mult, ALU.add,
                           [uTb[i], dsk, OACC.s(n)], [yy[i]])
                kb.TT(x2[i][:], yy[i][:], yy[i][:], ALU.mult, [yy[i]], [x2[i]], eng="pool")
                kb.TS(x2[i][:], x2[i][:], 0.044715, 1.0, ALU.mult, ALU.add, [x2[i]], [x2[i]])
                kb.TT(x2[i][:], x2[i][:], yy[i][:], ALU.mult, [x2[i], yy[i]], [x2[i]], eng="pool")
                kb.ACT(th[i][:], x2[i][:], AF.Tanh, [x2[i]], [th[i]], scale=0.7978845608028654)
                kb.TS(th[i][:], th[i][:], 1.0, 0.5, ALU.add, ALU.mult, [th[i]], [th[i]])
                kb.TT(yy[i][:], yy[i][:], th[i][:], ALU.mult, [yy[i], th[i]], [yy[i]], eng="pool")
                for ot in range(2):
                    q = k % 2; k += 1
                    for ct in range(2):
                        kb.MM(psz[q][:], gw[:, ct, ot * 128:(ot + 1) * 128], yy[i][:, ct, :], ct == 0, ct == 1, [gw, yy[i]], [psz[q]])
                    kb.ACT(sgb[q][:], psz[q][:], AF.Sigmoid, [psz[q], glb], [sgb[q]], bias=glb[:, ot:ot + 1])
                    kb.TT(ob[q][:], yy[i][:, ot, :], sgb[q][:], ALU.mult, [yy[i], sgb[q]], [ob[q]])
                    kb.ST(kb.YC[768 + ot * 128:768 + (ot + 1) * 128, cols], ob[q][:], [ob[q]])


def _conv_win_masks():
    tp = np.arange(642) - 65
    w = np.mod(tp, 64)
    m = np.ones((2, 642), np.float32)
    m[0, w == 63] = 0.0
    m[1, w == 0] = 0.0
    return np.broadcast_to(m[None], (128, 2, 642)).copy()


def gdn_conv2(kb, l):
    P = kb.P
    with P.scope():
        CW = P.sbuf("g_cw", [128, 6, 9])
        for kh in range(3):
            for kw in range(3):
                kb.LD(CW[:, :, kh * 3 + kw], kb.prm["gdn_conv_w"][l][kh, kw].rearrange("(ct p) -> p ct", p=128), [CW],
                      allow_slow_non_contiguous=True)
        DW = P.sbuf("g_dw", [128, 6, 9, 128], BF16)
        for ct in range(6):
            for tp_ in range(9):
                kb.TS(DW[:, ct, tp_, :], kb.C("IDENT"), CW[:, ct, tp_:tp_ + 1], None, ALU.mult, None, [CW], [DW],
                      eng=("pool" if tp_ % 2 else "dve"))
        wm = P.sbuf("g_wm", [128, 2, 642])
        kb.LD(wm[:], kb.cwin[:], [wm])
        Wb = [P.sbuf("g_w%d" % i, [128, 642]) for i in range(2)]
        WLb = [P.sbuf("g_wl%d" % i, [128, 642], BF16) for i in range(2)]
        WRb = [P.sbuf("g_wr%d" % i, [128, 642], BF16) for i in range(2)]
        WCb = [P.sbuf("g_wc%d" % i, [128, 642], BF16) for i in range(2)]
        sl = [P.sbuf("g_sl%d" % i, [128, 512]) for i in range(2)]
        sq = [P.sbuf("g_sq%d" % i, [128, 512]) for i in range(2)]
        rt = [P.sbuf("g_rt%d" % i, [128, 512]) for i in range(2)]
        psc = [P.psum("g_psc%d" % i, [128, 512]) for i in range(2)]
        ps = [P.psum("g_psn%d" % i, [128, 512]) for i in range(2)]
        spans = [(0, 256, True)] + [(256 + 512 * k, 512, False) for k in range(8)]
        it = 0
        for (t0, L, is_ctx) in spans:
            lo = 0 if is_ctx else 256
            hi = 256 if is_ctx else S
            a = max(lo, t0 - 65); b = min(hi, t0 + L + 65)
            for ct in range(6):
                i = it % 2; it += 1
                W = Wb[i]
                full = (a == t0 - 65) and (b == t0 + L + 65) and L == 512
                if not full:
                    kb.MS(W[:], 0.0, [W], eng="pool")
                kb.LD(W[:, 65 + (a - t0):65 + (b - t0)], kb.ZF[Z_GQKV + ct * 128:Z_GQKV + (ct + 1) * 128, a:b], [W])
                WC = WCb[i]
                kb.CP(WC[:], W[:], [W], [WC], eng="act")
                if is_ctx:
                    WL = WR = WC
                    rows = (1,)
                else:
                    WL, WR = WLb[i], WRb[i]
                    kb.TT(WL[:], W[:], wm[:, 0, :], ALU.mult, [W, wm], [WL])
                    kb.TT(WR[:], W[:], wm[:, 1, :], ALU.mult, [W, wm], [WR])
                    rows = (0, 1, 2)
                pc = psc[i]
                taps = [(dh, dwi) for dh in rows for dwi in range(3)]
                for q, (dh, dwi) in enumerate(taps):
                    srcT = (WL, WC, WR)[dwi]
                    o0 = 65 + 64 * (dh - 1) + (dwi - 1)
                    kb.MM(pc[:, :L], DW[:, ct, dh * 3 + dwi, :], srcT[:, o0:o0 + L], q == 0, q == len(taps) - 1, [DW, srcT], [pc])
                kb.ACT(sl[i][:, :L], pc[:, :L], AF.Silu, [pc], [sl[i]])
                if ct < 4:
                    kb.TT(sq[i][:, :L], sl[i][:, :L], sl[i][:, :L], ALU.mult, [sl[i]], [sq[i]])
                    kb.MM(ps[i][:, :L], kb.C("BLK64"), sq[i][:, :L], True, True, [sq[i]], [ps[i]])
                    kb.ACT(rt[i][:, :L], ps[i][:, :L], AF.Sqrt, [ps[i]], [rt[i]], bias=kb.C("CCOL")[:, 0:1])
                    kb.RECIP(rt[i][:, :L], rt[i][:, :L], [rt[i]], [rt[i]])
                    if ct < 2:
                        kb.STT(sl[i][:, :L], sl[i][:, :L], 0.125, rt[i][:, :L], ALU.mult, ALU.mult, [sl[i], rt[i]], [sl[i]])
                    else:
                        kb.TT(sl[i][:, :L], sl[i][:, :L], rt[i][:, :L], ALU.mult, [sl[i], rt[i]], [sl[i]], eng="pool")
                kb.ST(kb.QKVF[ct * 128:(ct + 1) * 128, t0:t0 + L], sl[i][:, :L], [sl[i]])
```
